# Optimizing a Trainium2 kernel written in Bass

```python
import math
import jax, jax.numpy as jnp
from jax import lax
import numpy as np

D_MODEL = 1024
BATCH = 4
SEQ = 8192
DEPTH = 1

M_HEADS = 4
M_HEAD_DIM = 128
M_WIDTH = M_HEADS * M_HEAD_DIM
M_CONV = 4
M_CHUNK = 64
A_HEADS = 4
A_QK_DIM = 64
A_V_DIM = 2 * A_QK_DIM
A_WIDTH = A_HEADS * A_V_DIM
A_QBLOCK = 128
N_BUCKETS = 32
MAX_DISTANCE = 128
D_FF = 2816
FFN_CONV = 3
RMS_EPS = 1e-6
IN_SIZES = (2 * M_WIDTH, M_WIDTH, M_WIDTH, M_HEADS, M_HEADS, 2 * A_HEADS * A_QK_DIM, 2 * A_HEADS * A_QK_DIM, A_WIDTH, D_MODEL, D_MODEL)
IN_COLS = sum(IN_SIZES)
IN_SPLITS = tuple(int(v) for v in np.cumsum(IN_SIZES)[:-1])

kernel_name = 'hybrid_mlstm_diffattn_convffn_block'


def rms_norm(x, g):
    xf = x.astype(jnp.float32)
    y = xf * lax.rsqrt(jnp.mean(xf * xf, axis=-1, keepdims=True) + RMS_EPS)
    return (y * g.astype(jnp.float32)).astype(x.dtype)


def causal_dwconv(x, w, b):
    k_w = w.shape[0]
    y = lax.conv_general_dilated(x, w[:, None, :].astype(x.dtype), window_strides=(1,), padding=[(k_w - 1, 0)],
                                 dimension_numbers=('NWC', 'WIO', 'NWC'), feature_group_count=x.shape[-1])
    return y + b.astype(x.dtype)


def t5_bucket(dist):
    n = jnp.maximum(dist, 0)
    max_exact = N_BUCKETS // 2
    nf = jnp.maximum(n, 1).astype(jnp.float32)
    large = max_exact + (jnp.log(nf / max_exact) / math.log(MAX_DISTANCE / max_exact) * (N_BUCKETS - max_exact)).astype(jnp.int32)
    large = jnp.minimum(large, N_BUCKETS - 1)
    return jnp.where(n < max_exact, n, large)


def mlstm_chunkwise(q, k, v, i_pre, f_pre):
    b_, h_, s_, d_ = q.shape
    L = M_CHUNK
    nc = s_ // L
    q = q.reshape(b_, h_, nc, L, d_)
    k = k.reshape(b_, h_, nc, L, d_)
    v = v.reshape(b_, h_, nc, L, d_)
    logf = jax.nn.log_sigmoid(f_pre).reshape(b_, h_, nc, L)
    logi = i_pre.reshape(b_, h_, nc, L)
    bcum = jnp.cumsum(logf, axis=-1)
    b_last = bcum[..., -1]
    g = b_last[..., None] - bcum + logi
    m_loc = jnp.max(g, axis=-1)
    w = jnp.exp(g - m_loc[..., None])
    c_loc = jnp.einsum('bhcs,bhcsd,bhcse->bhcde', w, v, k)
    n_loc = jnp.einsum('bhcs,bhcse->bhce', w, k)

    def step(carry, inp):
        c_st, n_st, m_st = carry
        cl, nl, ml, f_tot = inp
        m_new = jnp.maximum(f_tot + m_st, ml)
        a = jnp.exp(f_tot + m_st - m_new)
        bb = jnp.exp(ml - m_new)
        c_new = a[..., None, None] * c_st + bb[..., None, None] * cl
        n_new = a[..., None] * n_st + bb[..., None] * nl
        return (c_new, n_new, m_new), (c_st, n_st, m_st)

    init = (jnp.zeros((b_, h_, d_, d_), jnp.float32), jnp.zeros((b_, h_, d_), jnp.float32), jnp.zeros((b_, h_), jnp.float32))
    xs = (jnp.moveaxis(c_loc, 2, 0), jnp.moveaxis(n_loc, 2, 0), jnp.moveaxis(m_loc, 2, 0), jnp.moveaxis(b_last, 2, 0))
    _, (c_in, n_in, m_in) = lax.scan(step, init, xs)
    c_in = jnp.moveaxis(c_in, 0, 2)
    n_in = jnp.moveaxis(n_in, 0, 2)
    m_in = jnp.moveaxis(m_in, 0, 2)

    a_t = bcum + m_in[..., None]
    dmat = bcum[..., :, None] - bcum[..., None, :] + logi[..., None, :]
    causal = jnp.tril(jnp.ones((L, L), dtype=bool))
    dmat = jnp.where(causal, dmat, -jnp.inf)
    m_t = jnp.maximum(a_t, jnp.max(dmat, axis=-1))
    inter_w = jnp.exp(a_t - m_t)
    s = jnp.einsum('bhctd,bhcsd->bhcts', q, k) * jnp.exp(dmat - m_t[..., None])
    num = jnp.einsum('bhcts,bhcsd->bhctd', s, v) + inter_w[..., None] * jnp.einsum('bhcde,bhcte->bhctd', c_in, q)
    den_dot = jnp.sum(s, axis=-1) + inter_w * jnp.einsum('bhce,bhcte->bhct', n_in, q)
    den = jnp.maximum(jnp.abs(den_dot), jnp.exp(-m_t))
    return (num / den[..., None]).reshape(b_, h_, s_, d_)


def diff_attention(q, k, v, rel_bias, lam):
    b_, h_, _, s_, _ = q.shape
    dv = v.shape[-1]
    nqb = s_ // A_QBLOCK
    k_pos = jnp.arange(s_, dtype=jnp.int32)
    bias_tab = rel_bias.astype(jnp.float32).T

    def one_block(j):
        start = j * A_QBLOCK
        qb = lax.dynamic_slice_in_dim(q, start, A_QBLOCK, axis=3)
        logits = jnp.einsum('bhmqd,bhmkd->bhmqk', qb, k)
        dist = (start + jnp.arange(A_QBLOCK, dtype=jnp.int32))[:, None] - k_pos[None, :]
        bias = bias_tab[:, t5_bucket(dist)]
        logits = jnp.where(dist >= 0, logits + bias[None, :, None], -jnp.inf)
        p = jax.nn.softmax(logits, axis=-1)
        wts = p[:, :, 0] - lam * p[:, :, 1]
        return jnp.einsum('bhqk,bhkd->bhqd', wts, v)

    out = lax.map(one_block, jnp.arange(nqb, dtype=jnp.int32))
    return out.transpose(1, 0, 3, 2, 4).reshape(b_, s_, h_, dv)


def setup_inputs(seed: int = 0) -> dict:
    key = jax.random.key(seed)
    ks = jax.random.split(key, 24)
    f32 = jnp.float32

    def nrm(k, shape, s):
        return jax.random.normal(k, shape, f32) * s

    return {
        'x': nrm(ks[0], (BATCH, SEQ, D_MODEL), 1.0),
        'c': nrm(ks[1], (BATCH, D_MODEL), 1.0),
        'w_ada': nrm(ks[2], (DEPTH, D_MODEL, 6 * D_MODEL), D_MODEL ** -0.5),
        'b_ada': nrm(ks[3], (DEPTH, 6 * D_MODEL), 0.02),
        'norm1_g': 1.0 + nrm(ks[4], (DEPTH, D_MODEL), 0.1),
        'w_in': nrm(ks[5], (DEPTH, D_MODEL, IN_COLS), D_MODEL ** -0.5),
        'm_conv_w': nrm(ks[6], (DEPTH, M_CONV, 2 * M_WIDTH), M_CONV ** -0.5),
        'm_conv_b': nrm(ks[7], (DEPTH, 2 * M_WIDTH), 0.02),
        'm_igate_b': nrm(ks[8], (DEPTH, M_HEADS), 0.1),
        'm_fgate_b': jnp.linspace(3.0, 6.0, M_HEADS, dtype=f32)[None, :] + nrm(ks[9], (DEPTH, M_HEADS), 0.1),
        'm_norm_g': 1.0 + nrm(ks[10], (DEPTH, M_WIDTH), 0.1),
        'a_qnorm_g': 1.0 + nrm(ks[11], (DEPTH, A_QK_DIM), 0.1),
        'a_knorm_g': 1.0 + nrm(ks[12], (DEPTH, A_QK_DIM), 0.1),
        'a_lambda': nrm(ks[13], (DEPTH, 4, A_QK_DIM), 0.1),
        'a_norm_g': 1.0 + nrm(ks[14], (DEPTH, A_WIDTH), 0.1),
        'rel_bias': nrm(ks[15], (N_BUCKETS, A_HEADS), 0.5),
        'w_branch_m': nrm(ks[16], (DEPTH, M_WIDTH, D_MODEL), M_WIDTH ** -0.5),
        'w_branch_a': nrm(ks[17], (DEPTH, A_WIDTH, D_MODEL), A_WIDTH ** -0.5),
        'w_out': nrm(ks[18], (DEPTH, D_MODEL, D_MODEL), D_MODEL ** -0.5),
        'norm2_g': 1.0 + nrm(ks[19], (DEPTH, D_MODEL), 0.1),
        'w_up': nrm(ks[20], (DEPTH, D_MODEL, 2 * D_FF), D_MODEL ** -0.5),
        'ffn_conv_w': nrm(ks[21], (DEPTH, FFN_CONV, 2 * D_FF), FFN_CONV ** -0.5),
        'ffn_conv_b': nrm(ks[22], (DEPTH, 2 * D_FF), 0.02),
        'w_down': nrm(ks[23], (DEPTH, D_FF, D_MODEL), D_FF ** -0.5),
    }


def reference(x, c, w_ada, b_ada, norm1_g, w_in, m_conv_w, m_conv_b, m_igate_b, m_fgate_b, m_norm_g, a_qnorm_g, a_knorm_g, a_lambda, a_norm_g, rel_bias, w_branch_m, w_branch_a, w_out, norm2_g, w_up, ffn_conv_w, ffn_conv_b, w_down):
    f32 = jnp.float32
    b_, s_, _ = x.shape
    for l in range(DEPTH):
        mod = jax.nn.silu(c) @ w_ada[l] + b_ada[l]
        shift1, scale1, gate1, shift2, scale2, gate2 = jnp.split(mod[:, None, :], 6, axis=-1)

        h = rms_norm(x, norm1_g[l]) * (1.0 + scale1) + shift1
        proj = h @ w_in[l]
        mqk, mv, mo, mi, mf, aq, ak, av, g_m, g_a = jnp.split(proj, IN_SPLITS, axis=-1)

        mqk = jax.nn.silu(causal_dwconv(mqk, m_conv_w[l], m_conv_b[l]))
        mq, mk = jnp.split(mqk, 2, axis=-1)

        def to_heads(t):
            return t.astype(f32).reshape(b_, s_, M_HEADS, M_HEAD_DIM).transpose(0, 2, 1, 3)

        i_pre = (mi + m_igate_b[l]).astype(f32).transpose(0, 2, 1)
        f_pre = (mf + m_fgate_b[l]).astype(f32).transpose(0, 2, 1)
        hm = mlstm_chunkwise(to_heads(mq), to_heads(mk) * (M_HEAD_DIM ** -0.5), to_heads(mv), i_pre, f_pre)
        hm = rms_norm(hm.transpose(0, 2, 1, 3), m_norm_g[l].reshape(M_HEADS, M_HEAD_DIM)).reshape(b_, s_, M_WIDTH)
        hm = (jax.nn.sigmoid(mo.astype(f32)) * hm).astype(x.dtype)

        lam_init = 0.8 - 0.6 * math.exp(-0.3 * l)
        lamv = a_lambda[l].astype(f32)
        lam = jnp.exp(jnp.sum(lamv[0] * lamv[1])) - jnp.exp(jnp.sum(lamv[2] * lamv[3])) + lam_init
        qa = rms_norm(aq.reshape(b_, s_, 2, A_HEADS, A_QK_DIM), a_qnorm_g[l]).astype(f32).transpose(0, 3, 2, 1, 4) * (A_QK_DIM ** -0.5)
        ka = rms_norm(ak.reshape(b_, s_, 2, A_HEADS, A_QK_DIM), a_knorm_g[l]).astype(f32).transpose(0, 3, 2, 1, 4)
        va = av.astype(f32).reshape(b_, s_, A_HEADS, A_V_DIM).transpose(0, 2, 1, 3)
        ha = diff_attention(qa, ka, va, rel_bias, lam)
        ha = (rms_norm(ha, a_norm_g[l].reshape(A_HEADS, A_V_DIM)) * (1.0 - lam_init)).reshape(b_, s_, A_WIDTH).astype(x.dtype)

        y = jax.nn.sigmoid(g_m) * (hm @ w_branch_m[l]) + jax.nn.sigmoid(g_a) * (ha @ w_branch_a[l])
        x = x + gate1 * (y @ w_out[l])

        h2 = rms_norm(x, norm2_g[l]) * (1.0 + scale2) + shift2
        u = causal_dwconv(h2 @ w_up[l], ffn_conv_w[l], ffn_conv_b[l])
        val, gate = jnp.split(u, 2, axis=-1)
        x = x + gate2 * ((jax.nn.silu(gate) * val) @ w_down[l])
    return x
```

```python
import math
from contextlib import ExitStack

import numpy as np

import concourse.bass as bass
import concourse.mybir as mybir
from concourse.bass_utils import run_bass_kernel_spmd

F32 = mybir.dt.float32
BF16 = mybir.dt.bfloat16
AF = mybir.ActivationFunctionType
ALU = mybir.AluOpType
AX = mybir.AxisListType

D = 1024
DC = 8
DFF = 2816
NF = 22
EPS = 1e-6
LAM_INIT = 0.8 - 0.6 * math.exp(-0.3 * 0)
NEG = -30000.0

PIECES = {}
_off = 0
for _nm, _n in (("mq", 4096), ("mk", 4096), ("mv", 4096), ("mo", 4096), ("aq", 4096), ("ak", 4096),
                ("av", 4096), ("gm0", 4096), ("gm1", 4096), ("ga0", 4096), ("ga1", 4096), ("gif", 64),
                ("bm", 4096), ("ba", 4096), ("out0", 4096), ("out1", 4096)):
    PIECES[_nm] = (_off, _n)
    _off += _n
for _j in range(11):
    PIECES["up%d" % _j] = (_off, 4096)
    _off += 4096
PIECES["down"] = (_off, NF * 1024)
_off += NF * 1024
WTOT = _off
WPAD = ((WTOT + 4095) // 4096) * 4096


class Buf:
    __slots__ = ("name", "w", "rs", "sem", "semv", "grp")

    def __init__(self, name, grp=None):
        self.name = name
        self.w = None
        self.rs = []
        self.sem = None
        self.semv = 0
        self.grp = grp


class Sched:
    def __init__(self, nc, stack):
        self.nc = nc
        self.stack = stack
        self.engs = {}
        for nm, h in (("pe", nc.tensor), ("act", nc.scalar), ("dve", nc.vector), ("pool", nc.gpsimd), ("sp", nc.sync)):
            sem = stack.enter_context(nc.semaphore("s_" + nm))
            self.engs[nm] = dict(h=h, sem=sem, cnt=0, seen={})
        self.groups = {}
        self.dma_ev = {}
        self.free_sems = {}
        self.live = []
        self.nsem = 0

    def buf(self, name, grp=None):
        return Buf(name, grp)

    def _need(self, e, deps):
        E = self.engs[e]
        best = {}
        for d in deps:
            if d is None:
                continue
            sem, val, en = d
            if en == e and e == "pe":
                continue
            k = id(sem)
            if k not in best or best[k][1] < val:
                best[k] = (sem, val)
        for k, (sem, val) in best.items():
            if E["seen"].get(k, 0) >= val:
                continue
            E["h"].wait_ge(sem, val)
            E["seen"][k] = val

    def op(self, e, fn, reads=(), writes=()):
        E = self.engs[e]
        deps = []
        for b in reads:
            deps.append(b.w)
        for b in writes:
            if b.w is not None and b.w[2] != e:
                deps.append(b.w)
            for r in b.rs:
                if r[2] != e:
                    deps.append(r)
        self._need(e, deps)
        ins = fn(E["h"])
        E["cnt"] += 1
        ins.then_inc(E["sem"], 1)
        ev = (E["sem"], E["cnt"], e)
        for b in reads:
            b.rs.append(ev)
        for b in writes:
            b.w = ev
            b.rs = []
        return ins

    def dma(self, q, out, in_, reads=(), writes=()):
        E = self.engs[q]
        deps = []
        for b in reads:
            deps.append(b.w)
        for b in writes:
            if b.grp in ("wbw", "kvw", "outw"):
                continue
            deps.append(b.w)
            deps.extend(b.rs)
        self._need(q, deps)
        ins = E["h"].dma_start(out=out, in_=in_)
        cands = list(writes) + list(reads)
        tgt = ([b for b in cands if b.grp is None] + cands)[0]
        if tgt.grp is not None:
            gk = tgt.grp
            if gk not in self.groups:
                self.groups[gk] = [self.stack.enter_context(self.nc.semaphore("g_" + gk)), 0]
            g = self.groups[gk]
            g[1] += 16
            sem, val = g[0], g[1]
        else:
            if tgt.sem is None:
                tgt.sem = {}
                self.live.append(tgt)
            if q not in tgt.sem:
                fs = self.free_sems.setdefault(q, [])
                if not fs:
                    self.nsem += 1
                    fs.append([self.stack.enter_context(self.nc.semaphore("dp%d" % self.nsem)), 0])
                tgt.sem[q] = fs.pop()
            ent = tgt.sem[q]
            ent[1] += 16
            sem, val = ent[0], ent[1]
        ins.then_inc(sem, 16)
        self.dma_ev[id(sem)] = (sem, val)
        ev = (sem, val, "dma")
        for b in reads:
            b.rs.append(ev)
        for b in writes:
            b.w = ev
            b.rs = []
        return ins

    def barrier(self, engines=("pe", "act", "dve", "pool", "sp")):
        deps = [(E["sem"], E["cnt"], "x") for E in self.engs.values() if E["cnt"] > 0]
        deps += [(s, v, "dma") for (s, v) in self.dma_ev.values()]
        for e in engines:
            self._need(e, deps)
        self.dma_ev = {}
        for b in self.live:
            for qq, ent in b.sem.items():
                self.free_sems[qq].append(ent)
            b.sem = None
        self.live = []

    def seal(self, bufs, grp):
        g = self.groups[grp]
        for b in bufs:
            b.w = (g[0], g[1], "dma")


def build_program(NCTX, NFULL, debug=False):
    NU = NCTX + NFULL
    NFLAG = NU // 2
    NBLK = NU * 4
    NTOK = NU * 512
    NOWN = (NFULL - 1) * 512
    assert NU == 2 * (NFULL - 1)

    nc = bass.Bass("TRN2", target_bir_lowering=False)

    def din(name, shape, dt=F32):
        return nc.dram_tensor(name, list(shape), dt, kind="ExternalInput").ap()

    x_d = din("x", [NTOK, D])
    cfm_d = din("cfm", [128, 8])
    wada_d = din("wada", [12, 128, 8 * 512])
    bada_d = din("bada", [6144])
    badafm_d = din("badafm", [128, 48])
    g1fm_d = din("g1fm", [128, 8])
    g2fm_d = din("g2fm", [128, 8])
    wall_d = din("wall", [128, WPAD])
    mcw_d = din("mcw", [128, 8 * 4])
    mcb_d = din("mcb", [128, 8])
    fcw_d = din("fcw", [128, 44 * 3])
    fcb_d = din("fcb", [128, 44])
    gifb_d = din("gifb", [8])
    mng_d = din("mng", [512])
    ang_d = din("ang", [512])
    gq_d = din("gq", [512])
    gk_d = din("gk", [512])
    alam_d = din("alam", [256])
    biasg_d = din("biasg", [128, 4 * 2 * 2 * 128])
    maskn_d = din("maskn", [128, 4 * 2 * 2 * 128])
    farb_d = din("farb", [4])
    flag_d = din("flag", [128, 1])
    identb_d = din("identb", [128, 128], BF16)
    identf_d = din("identf", [128, 128])
    umask_d = din("umask", [128, 128])
    negm4_d = din("negm4", [128, 512])
    out_d = nc.dram_tensor("out", [NOWN, D], F32, kind="ExternalOutput").ap()
    wb_d = nc.dram_tensor("wb_scr", [128, WPAD], BF16, kind="Internal").ap()
    kt_d = nc.dram_tensor("kt_scr", [4, 128, NBLK * 128], BF16, kind="Internal").ap()
    va_d = nc.dram_tensor("va_scr", [4, 128, NBLK * 130], BF16, kind="Internal").ap()

    with ExitStack() as top:
        S = Sched(nc, top)

        uid = [0]

        def sbt(st, name, shape, dt=F32):
            uid[0] += 1
            return st.enter_context(nc.sbuf_tensor("s%d_%s" % (uid[0], name), list(shape), dt))

        banks = [top.enter_context(nc.psum_tensor("bank%d" % i, [128, 512], F32)) for i in range(8)]
        bbufs = [S.buf("bank%d" % i) for i in range(8)]
        bank_rr = [0]

        def nbank(lo=0, hi=8):
            i = lo + (bank_rr[0] % (hi - lo))
            bank_rr[0] += 1
            return banks[i], bbufs[i]

        def bfview(bank):
            return bank[:].bitcast(BF16)

        identb = sbt(top, "identb", [128, 128], BF16)
        identf = sbt(top, "identf", [128, 128])
        umask = sbt(top, "umask", [128, 128])
        negm4 = sbt(top, "negm4", [128, 512])
        ones_f = sbt(top, "ones_f", [128, 128])
        flag = sbt(top, "flag", [128, 1])
        epst = sbt(top, "epst", [128, 1])
        onet = sbt(top, "onet", [128, 1])
        mult1 = sbt(top, "mult1", [128, 8])
        shift1 = sbt(top, "shift1", [128, 8])
        mult2 = sbt(top, "mult2", [128, 8])
        shift2 = sbt(top, "shift2", [128, 8])
        gate1 = sbt(top, "gate1", [128, D])
        gate2 = sbt(top, "gate2", [128, D])
        mcw = sbt(top, "mcw", [128, 8, 4])
        mcb = sbt(top, "mcb", [128, 8])
        mcnb = sbt(top, "mcnb", [128, 8])
        fcw = sbt(top, "fcw", [128, 44, 3])
        fcb = sbt(top, "fcb", [128, 44])
        fcnb = sbt(top, "fcnb", [128, 44])
        gifb = sbt(top, "gifb", [128, 8])
        mng = sbt(top, "mng", [128, 512])
        ang = sbt(top, "ang", [128, 512])
        gq = sbt(top, "gq", [128, 512])
        gk = sbt(top, "gk", [128, 512])
        biasb = sbt(top, "biasb", [128, 4, 2, 256], BF16)
        farb = sbt(top, "farb", [128, 4])
        neglam = sbt(top, "neglam", [128, 1])
        Sst = sbt(top, "Sst", [128, 4, 130])
        Sstb = sbt(top, "Sstb", [128, 4, 130], BF16)
        mhalo = sbt(top, "mhalo", [128, 8, 3])
        fhalo = sbt(top, "fhalo", [128, 44, 2])
        CONST = S.buf("const")
        b_S = S.buf("Sst")
        b_Sb = S.buf("Sstb")
        b_mhalo = S.buf("mhalo")
        b_fhalo = S.buf("fhalo")
        b_wb = S.buf("wb_scr", grp="wbw")
        b_ktd = S.buf("kt_scr", grp="kvw")
        b_vad = S.buf("va_scr", grp="kvw")
        b_out = S.buf("outd", grp="outw")

        def ld_const(t, src, grp="cst"):
            b = S.buf("c_" + t.name, grp=grp)
            S.dma("sp", t[:], src, writes=[b])
            return b

        cb = []
        cb.append(ld_const(identb, identb_d[:, :]))
        cb.append(ld_const(identf, identf_d[:, :]))
        cb.append(ld_const(umask, umask_d[:, :]))
        cb.append(ld_const(negm4, negm4_d[:, :]))
        cb.append(ld_const(flag, flag_d[:, :]))
        cb.append(ld_const(mcw, mcw_d.rearrange("p (c k) -> p c k", k=4)))
        cb.append(ld_const(mcb, mcb_d[:, :]))
        cb.append(ld_const(fcw, fcw_d.rearrange("p (c k) -> p c k", k=3)))
        cb.append(ld_const(fcb, fcb_d[:, :]))
        cb.append(ld_const(gifb, gifb_d.partition_broadcast(128)))
        cb.append(ld_const(mng, mng_d.partition_broadcast(128)))
        cb.append(ld_const(ang, ang_d.partition_broadcast(128)))
        cb.append(ld_const(gq, gq_d.partition_broadcast(128)))
        cb.append(ld_const(gk, gk_d.partition_broadcast(128)))
        cb.append(ld_const(farb, farb_d.partition_broadcast(128)))
        S.seal(cb, "cst")
        S.op("dve", lambda e: e.memset(ones_f[:], 1.0), writes=[CONST])
        S.op("dve", lambda e: e.memset(epst[:], EPS), writes=[CONST])
        S.op("dve", lambda e: e.memset(onet[:], 1.0), writes=[CONST])
        S.op("dve", lambda e: e.memset(Sst[:], 0.0), writes=[b_S])
        S.op("dve", lambda e: e.memset(Sstb[:], 0.0), writes=[b_Sb])
        S.op("dve", lambda e: e.memset(mhalo[:], 0.0), writes=[b_mhalo])
        S.op("dve", lambda e: e.memset(fhalo[:], 0.0), writes=[b_fhalo])
        S.op("dve", lambda e: e.tensor_scalar(out=mcnb[:], in0=mcb[:], scalar1=-1.0, scalar2=None, op0=ALU.mult), reads=cb, writes=[CONST])
        S.op("dve", lambda e: e.tensor_scalar(out=fcnb[:], in0=fcb[:], scalar1=-1.0, scalar2=None, op0=ALU.mult), reads=cb, writes=[CONST])
        S.op("dve", lambda e: e.tensor_scalar(out=gq[:], in0=gq[:], scalar1=0.125, scalar2=None, op0=ALU.mult), reads=cb, writes=[CONST])
        S.op("dve", lambda e: e.tensor_scalar(out=ang[:], in0=ang[:], scalar1=1.0 - LAM_INIT, scalar2=None, op0=ALU.mult), reads=cb, writes=[CONST])

        with ExitStack() as st:
            cfm = sbt(st, "cfm", [128, 8])
            sc = sbt(st, "sc", [128, 8])
            sct = sbt(st, "sct", [128, 8])
            scbc = sbt(st, "scbc", [128, 8, 128])
            badafm = sbt(st, "badafm", [128, 48])
            g1fm = sbt(st, "g1fm", [128, 8])
            g2fm = sbt(st, "g2fm", [128, 8])
            modfm = sbt(st, "modfm", [128, 48])
            alam = sbt(st, "alam", [128, 256])
            lt = sbt(st, "lt", [128, 128])
            ls = sbt(st, "ls", [128, 2])
            biasg = sbt(st, "biasg", [128, 2048])
            maskn = sbt(st, "maskn", [128, 2048])
            wst = [sbt(st, "wst%d" % i, [128, 4096]) for i in range(2)]
            wbo = [sbt(st, "wbo%d" % i, [128, 4096], BF16) for i in range(2)]
            bbt = [sbt(st, "bbt%d" % i, [128, 512]) for i in range(2)]
            b_wst = [S.buf("wst%d" % i) for i in range(2)]
            b_wbo = [S.buf("wbo%d" % i) for i in range(2)]
            b_bbt = [S.buf("bbt%d" % i) for i in range(2)]
            L = S.buf("prel")
            b_l = [ld_const(cfm, cfm_d[:, :], "cst2"), ld_const(badafm, badafm_d[:, :], "cst2"), ld_const(g1fm, g1fm_d[:, :], "cst2"),
                   ld_const(g2fm, g2fm_d[:, :], "cst2"), ld_const(alam, alam_d.partition_broadcast(128), "cst2"),
                   ld_const(biasg, biasg_d[:, :], "cst2"), ld_const(maskn, maskn_d[:, :], "cst2")]
            S.seal(b_l, "cst2")
            S.op("act", lambda e: e.activation(out=sct[:], in_=cfm[:], func=AF.Exp, scale=-1.0), reads=b_l, writes=[L])
            S.op("dve", lambda e: e.tensor_scalar(out=sct[:], in0=sct[:], scalar1=1.0, scalar2=None, op0=ALU.add), reads=[L], writes=[L])
            S.op("dve", lambda e: e.reciprocal(out=sct[:], in_=sct[:]), reads=[L], writes=[L])
            S.op("dve", lambda e: e.tensor_tensor(out=sc[:], in0=cfm[:], in1=sct[:], op=ALU.mult), reads=[L], writes=[L])
            S.op("dve", lambda e: e.tensor_copy(out=scbc[:], in_=sc[:].unsqueeze(2).to_broadcast([128, 8, 128])), reads=[L], writes=[L])
            S.op("dve", lambda e: e.tensor_tensor(out=lt[:, 0:64], in0=alam[:, 0:64], in1=alam[:, 64:128], op=ALU.mult), reads=b_l, writes=[L])
            S.op("dve", lambda e: e.tensor_tensor(out=lt[:, 64:128], in0=alam[:, 128:192], in1=alam[:, 192:256], op=ALU.mult), reads=[L], writes=[L])
            S.op("dve", lambda e: e.tensor_reduce(out=ls[:], in_=lt[:].rearrange("p (a b) -> p a b", a=2), axis=AX.X, op=ALU.add), reads=[L], writes=[L])
            S.op("act", lambda e: e.activation(out=ls[:], in_=ls[:], func=AF.Exp), reads=[L], writes=[L])
            S.op("dve", lambda e: e.tensor_tensor(out=neglam[:], in0=ls[:, 1:2], in1=ls[:, 0:1], op=ALU.subtract), reads=[L], writes=[CONST])
            S.op("dve", lambda e: e.tensor_scalar(out=neglam[:], in0=neglam[:], scalar1=-LAM_INIT, scalar2=None, op0=ALU.add), reads=[CONST], writes=[CONST])
            S.op("dve", lambda e: e.tensor_tensor(out=biasb[:].rearrange("p a b c -> p (a b c)"), in0=biasg[:], in1=maskn[:], op=ALU.add), reads=b_l, writes=[CONST])

            fm_ps, fm_b = banks[7], bbufs[7]
            fm_cols = {0: 0, 1: 4, 2: 8, 3: 12, 6: 24, 7: 28, 8: 32, 9: 36}
            for pc in range(12):
                i = pc % 2
                S.dma("sp", wst[i][:], wada_d[pc, :, :], writes=[b_wst[i]])
                wv = wst[i][:].rearrange("p (c n) -> p c n", c=8)
                if pc in (4, 5, 10, 11):
                    S.dma("sp", bbt[i][:], bada_d[pc * 512:(pc + 1) * 512].partition_broadcast(128), writes=[b_bbt[i]])
                    pb, pbb = nbank(0, 6)
                    for dc in range(8):
                        S.op("pe", lambda e: e.matmul(pb[:], lhsT=scbc[:, dc, :], rhs=wv[:, dc, :], start=(dc == 0), stop=(dc == 7)),
                             reads=[L, b_wst[i]], writes=[pbb])
                    gt = gate1 if pc < 6 else gate2
                    off = (pc % 2) * 512
                    S.op("dve", lambda e: e.tensor_tensor(out=gt[:, off:off + 512], in0=pb[:], in1=bbt[i][:], op=ALU.add),
                         reads=[pbb, b_bbt[i]], writes=[CONST])
                else:
                    for k in range(4):
                        col = fm_cols[pc] + k
                        for dc in range(8):
                            S.op("pe", lambda e: e.matmul(fm_ps[:, col:col + 1], lhsT=wv[:, dc, k * 128:(k + 1) * 128], rhs=sc[:, dc:dc + 1],
                                                         start=(dc == 0), stop=(dc == 7)), reads=[L, b_wst[i]], writes=[fm_b])
            S.op("dve", lambda e: e.tensor_tensor(out=modfm[:, 0:16], in0=fm_ps[:, 0:16], in1=badafm[:, 0:16], op=ALU.add), reads=[fm_b] + b_l, writes=[L])
            S.op("dve", lambda e: e.tensor_tensor(out=modfm[:, 24:40], in0=fm_ps[:, 24:40], in1=badafm[:, 24:40], op=ALU.add), reads=[fm_b] + b_l, writes=[L])
            S.op("dve", lambda e: e.scalar_tensor_tensor(out=mult1[:], in0=modfm[:, 8:16], scalar=1.0, in1=g1fm[:], op0=ALU.add, op1=ALU.mult), reads=[L], writes=[CONST])
            S.op("dve", lambda e: e.tensor_copy(out=shift1[:], in_=modfm[:, 0:8]), reads=[L], writes=[CONST])
            S.op("dve", lambda e: e.scalar_tensor_tensor(out=mult2[:], in0=modfm[:, 32:40], scalar=1.0, in1=g2fm[:], op0=ALU.add, op1=ALU.mult), reads=[L], writes=[CONST])
            S.op("dve", lambda e: e.tensor_copy(out=shift2[:], in_=modfm[:, 24:32]), reads=[L], writes=[CONST])

            cast_eng = ["pool", "dve", "act"]
            for ch in range(WPAD // 4096):
                i = ch % 2
                S.dma("sp", wst[i][:], wall_d[:, ch * 4096:(ch + 1) * 4096], writes=[b_wst[i]])
                ce = cast_eng[ch % 3]
                if ce == "act":
                    S.op("act", lambda e: e.copy(out=wbo[i][:], in_=wst[i][:]), reads=[b_wst[i]], writes=[b_wbo[i]])
                else:
                    S.op(ce, lambda e: e.tensor_copy(out=wbo[i][:], in_=wst[i][:]), reads=[b_wst[i]], writes=[b_wbo[i]])
                S.dma("pool", wb_d[:, ch * 4096:(ch + 1) * 4096], wbo[i][:], reads=[b_wbo[i]], writes=[b_wb])
            S.barrier()

        def wload(st, name, piece, shape3=None):
            off, n = PIECES[piece]
            t = sbt(st, name, [128, n], BF16)
            b = S.buf(name)
            S.dma("sp", t[:], wb_d[:, off:off + n], reads=[b_wb], writes=[b])
            return t, b

        for u in range(NU):
            full = u >= NCTX
            flagged = u < NFLAG
            own = u >= NFLAG
            with ExitStack() as su:
                xs = sbt(su, "xs", [128, 4, D])
                hT = sbt(su, "hT", [128, 8, 512], BF16)
                mqT = sbt(su, "mqT", [128, 8, 512], BF16)
                mVA = sbt(su, "mVA", [128, 4, 4, 130], BF16)
                sigmo = sbt(su, "sigmo", [128, 4, 512])
                gif = sbt(su, "gif", [128, 4, 8])
                Qbd = sbt(su, "Qbd", [128, 4, 4, 256], BF16)
                hmT = sbt(su, "hmT", [128, 4, 512], BF16)
                haT = sbt(su, "haT", [128, 4, 512], BF16)
                b_xs = [S.buf("xs%d" % j) for j in range(4)]
                b_hT = S.buf("hT")
                b_mqT = S.buf("mqT")
                b_mVA = S.buf("mVA")
                b_sig = S.buf("sigmo")
                b_gif = S.buf("gif")
                b_Qbd = S.buf("Qbd")
                b_hmT = S.buf("hmT")
                b_haT = S.buf("haT")
                if full:
                    S.op("pool", lambda e: e.memset(Qbd[:], 0.0), writes=[b_Qbd])

                def norm_block(st, j, mult, shift, hdst, b_hdst, tagp):
                    junk = sbt(st, tagp + "junk%d" % j, [128, D], BF16)
                    xn = sbt(st, tagp + "xn%d" % j, [128, D], BF16)
                    ss = sbt(st, tagp + "ss%d" % j, [128, 2])
                    tmp = sbt(st, tagp + "tmp%d" % j, [128, 8, 128])
                    bl = S.buf("nb")
                    S.op("act", lambda e: e.activation(out=junk[:], in_=xs[:, j, :], func=AF.Square, scale=1.0 / 32.0, accum_out=ss[:, 0:1]),
                         reads=[b_xs[j]], writes=[bl])
                    S.op("act", lambda e: e.activation(out=ss[:, 1:2], in_=ss[:, 0:1], func=AF.Ln, bias=epst[:, 0:1], scale=1.0), reads=[bl, CONST], writes=[bl])
                    S.op("act", lambda e: e.activation(out=ss[:, 1:2], in_=ss[:, 1:2], func=AF.Exp, scale=-0.5), reads=[bl], writes=[bl])
                    S.op("dve", lambda e: e.tensor_scalar(out=xn[:], in0=xs[:, j, :], scalar1=ss[:, 1:2], scalar2=None, op0=ALU.mult),
                         reads=[bl, b_xs[j]], writes=[bl])
                    pb, pbb = nbank()
                    pv = bfview(pb).rearrange("p (c t) -> p c t", c=8)
                    for c in range(8):
                        S.op("pe", lambda e: e.transpose(out=pv[:, c, :], in_=xn[:, c * 128:(c + 1) * 128], identity=identb[:]),
                             reads=[bl, cb[0]], writes=[pbb])
                    S.op("dve", lambda e: e.tensor_tensor(out=tmp[:], in0=pv, in1=mult[:].unsqueeze(2).to_broadcast([128, 8, 128]), op=ALU.mult),
                         reads=[pbb, CONST], writes=[bl])
                    S.op("pool", lambda e: e.tensor_tensor(out=hdst[:, :, j * 128:(j + 1) * 128], in0=tmp[:],
                                                          in1=shift[:].unsqueeze(2).to_broadcast([128, 8, 128]), op=ALU.add),
                         reads=[bl, CONST], writes=[b_hdst])

                with ExitStack() as st:
                    names = ["mk", "mv", "gif", "ak", "av"] + (["mq", "mo", "aq"] if full else [])
                    W = {}
                    for nm in names:
                        W[nm] = wload(st, "w_" + nm, nm)
                    for j in range(4):
                        blk = u * 4 + j
                        S.dma("sp", xs[:, j, :], x_d[blk * 128:(blk + 1) * 128, :], writes=[b_xs[j]])
                    for j in range(4):
                        norm_block(st, j, mult1, shift1, hT, b_hT, "n1")
                    if flagged:
                        S.op("dve", lambda e: e.tensor_copy(out=mVA[:, :, :, 128:130], in_=flag[:, 0:1].unsqueeze(1).unsqueeze(1).to_broadcast([128, 4, 4, 2])),
                             reads=[cb[4]], writes=[b_mVA])
                    else:
                        S.op("dve", lambda e: e.memset(mVA[:, :, :, 128:130], 1.0), writes=[b_mVA])
                    pre = [sbt(st, "pre%d" % i, [128, 515]) for i in range(2)]
                    acc = [sbt(st, "acc%d" % i, [128, 512]) for i in range(2)]
                    et = [sbt(st, "et%d" % i, [128, 512]) for i in range(2)]
                    b_pre = [S.buf("pre%d" % i) for i in range(2)]
                    b_acc = [S.buf("acc%d" % i) for i in range(2)]
                    b_et = [S.buf("et%d" % i) for i in range(2)]
                    chunks = list(range(8)) if full else list(range(4, 8))
                    for ci, c in enumerate(chunks):
                        i = ci % 2
                        wt, wbf = W["mq" if c < 4 else "mk"]
                        wv = wt[:].rearrange("p (c n) -> p c n", c=8)
                        k = c % 4
                        pb, pbb = nbank()
                        for dc in range(8):
                            S.op("pe", lambda e: e.matmul(pb[:], lhsT=wv[:, dc, k * 128:(k + 1) * 128], rhs=hT[:, dc, :], start=(dc == 0), stop=(dc == 7)),
                                 reads=[wbf, b_hT], writes=[pbb])
                        S.op("pool", lambda e: e.tensor_copy(out=pre[i][:, 0:3], in_=mhalo[:, c, :]), reads=[b_mhalo], writes=[b_pre[i]])
                        if flagged:
                            S.op("act", lambda e: e.activation(out=pre[i][:, 3:515], in_=pb[:], func=AF.Copy, scale=flag[:, 0:1]), reads=[pbb, cb[4]], writes=[b_pre[i]])
                        else:
                            S.op("act", lambda e: e.copy(out=pre[i][:, 3:515], in_=pb[:]), reads=[pbb], writes=[b_pre[i]])
                        S.op("pool", lambda e: e.tensor_copy(out=mhalo[:, c, :], in_=pre[i][:, 512:515]), reads=[b_pre[i]], writes=[b_mhalo])
                        S.op("pool", lambda e: e.tensor_scalar(out=acc[i][:], in0=pre[i][:, 0:512], scalar1=mcw[:, c, 0:1], scalar2=None, op0=ALU.mult),
                             reads=[b_pre[i], cb[5]], writes=[b_acc[i]])
                        S.op("pool", lambda e: e.tensor_scalar(out=et[i][:], in0=pre[i][:, 1:513], scalar1=mcw[:, c, 1:2], scalar2=None, op0=ALU.mult),
                             reads=[b_pre[i], cb[5]], writes=[b_et[i]])
                        S.op("pool", lambda e: e.tensor_tensor(out=acc[i][:], in0=acc[i][:], in1=et[i][:], op=ALU.add), reads=[b_acc[i], b_et[i]], writes=[b_acc[i]])
                        for tp in range(2, 4):
                            S.op("dve", lambda e: e.scalar_tensor_tensor(out=acc[i][:], in0=pre[i][:, tp:tp + 512], scalar=mcw[:, c, tp:tp + 1], in1=acc[i][:],
                                                                        op0=ALU.mult, op1=ALU.add), reads=[b_pre[i], b_acc[i]], writes=[b_acc[i]])
                        S.op("act", lambda e: e.activation(out=et[i][:], in_=acc[i][:], func=AF.Exp, scale=-1.0, bias=mcnb[:, c:c + 1]),
                             reads=[b_acc[i], CONST], writes=[b_et[i]])
                        sk = (128.0 ** 0.5) if c < 4 else 1.0
                        S.op("dve", lambda e: e.tensor_scalar(out=et[i][:], in0=et[i][:], scalar1=1.0, scalar2=sk, op0=ALU.add, op1=ALU.mult),
                             reads=[b_et[i]], writes=[b_et[i]])
                        S.op("dve", lambda e: e.reciprocal(out=et[i][:], in_=et[i][:]), reads=[b_et[i]], writes=[b_et[i]])
                        S.op("dve", lambda e: e.scalar_tensor_tensor(out=mqT[:, c, :], in0=acc[i][:], scalar=mcb[:, c:c + 1], in1=et[i][:], op0=ALU.add, op1=ALU.mult),
                             reads=[b_acc[i], b_et[i], cb[6]], writes=[b_mqT])
                    sq = [sbt(st, "sq%d" % i, [128, 512]) for i in range(2)]
                    qn = [sbt(st, "qn%d" % i, [128, 512]) for i in range(2)]
                    qb = [sbt(st, "qb%d" % i, [128, 512], BF16) for i in range(2)]
                    rs = [sbt(st, "rs%d" % i, [128, 16]) for i in range(2)]
                    KTb = [sbt(st, "KTb%d" % i, [128, 4, 128], BF16) for i in range(2)]
                    VAb = [sbt(st, "VAb%d" % i, [128, 4, 130], BF16) for i in range(2)]
                    b_t = [S.buf("tmj%d" % i) for i in range(2)]
                    b_KTb = [S.buf("KTb%d" % i) for i in range(2)]
                    b_VAb = [S.buf("VAb%d" % i) for i in range(2)]
                    for i in range(2):
                        if flagged:
                            S.op("dve", lambda e: e.tensor_copy(out=VAb[i][:, :, 128:130], in_=flag[:, 0:1].unsqueeze(1).to_broadcast([128, 4, 2])),
                                 reads=[cb[4]], writes=[b_VAb[i]])
                        else:
                            S.op("dve", lambda e: e.memset(VAb[i][:, :, 128:130], 1.0), writes=[b_VAb[i]])
                    for j in range(4):
                        blk = u * 4 + j
                        i = j % 2
                        tsl = slice(j * 128, (j + 1) * 128)

                        def proj(nm):
                            wt, wbf = W[nm]
                            n = PIECES[nm][1] // 8
                            wv = wt[:].rearrange("p (c n) -> p c n", c=8)
                            pb, pbb = nbank()
                            for dc in range(8):
                                S.op("pe", lambda e: e.matmul(pb[:, 0:n], lhsT=hT[:, dc, tsl], rhs=wv[:, dc, :], start=(dc == 0), stop=(dc == 7)),
                                     reads=[wbf, b_hT], writes=[pbb])
                            return pb, pbb

                        def evac_scaled(out_ap, in_ap, rd, wr):
                            if flagged:
                                S.op("act", lambda e: e.activation(out=out_ap, in_=in_ap, func=AF.Copy, scale=flag[:, 0:1]), reads=rd + [cb[4]], writes=wr)
                            else:
                                S.op("act", lambda e: e.copy(out=out_ap, in_=in_ap), reads=rd, writes=wr)

                        pb, pbb = proj("mv")
                        evac_scaled(mVA[:, j, :, 0:128], pb[:].rearrange("p (h d) -> p h d", h=4), [pbb], [b_mVA])
                        pb, pbb = proj("gif")
                        S.op("dve", lambda e: e.tensor_tensor(out=gif[:, j, :], in0=pb[:, 0:8], in1=gifb[:], op=ALU.add), reads=[pbb, cb[9]], writes=[b_gif])
                        pb, pbb = proj("av")
                        evac_scaled(VAb[i][:, :, 0:128], pb[:].rearrange("p (h d) -> p h d", h=4), [pbb], [b_VAb[i]])
                        S.dma("pool", va_d.rearrange("h p (b c) -> p h b c", c=130)[:, :, blk, :], VAb[i][:], reads=[b_VAb[i]], writes=[b_vad])
                        if full:
                            pb, pbb = proj("mo")
                            S.op("act", lambda e: e.activation(out=sigmo[:, j, :], in_=pb[:], func=AF.Exp, scale=-1.0), reads=[pbb], writes=[b_sig])
                            S.op("dve", lambda e: e.tensor_scalar(out=sigmo[:, j, :], in0=sigmo[:, j, :], scalar1=1.0, scalar2=None, op0=ALU.add), reads=[b_sig], writes=[b_sig])
                            S.op("dve", lambda e: e.reciprocal(out=sigmo[:, j, :], in_=sigmo[:, j, :]), reads=[b_sig], writes=[b_sig])
                        for nm in (["aq", "ak"] if full else ["ak"]):
                            pb, pbb = proj(nm)
                            gt = gq if nm == "aq" else gk
                            S.op("act", lambda e: e.activation(out=sq[i][:], in_=pb[:], func=AF.Square, scale=0.125), reads=[pbb], writes=[b_t[i]])
                            S.op("dve", lambda e: e.tensor_reduce(out=rs[i][:, 0:8], in_=sq[i][:].rearrange("p (g d) -> p g d", d=64), axis=AX.X, op=ALU.add),
                                 reads=[b_t[i]], writes=[b_t[i]])
                            S.op("act", lambda e: e.activation(out=rs[i][:, 8:16], in_=rs[i][:, 0:8], func=AF.Ln, bias=epst[:, 0:1], scale=1.0), reads=[b_t[i], CONST], writes=[b_t[i]])
                            S.op("act", lambda e: e.activation(out=rs[i][:, 8:16], in_=rs[i][:, 8:16], func=AF.Exp, scale=-0.5), reads=[b_t[i]], writes=[b_t[i]])
                            S.op("dve", lambda e: e.tensor_tensor(out=qn[i][:].rearrange("p (g d) -> p g d", d=64), in0=pb[:].rearrange("p (g d) -> p g d", d=64),
                                                                 in1=rs[i][:, 8:16].unsqueeze(2).to_broadcast([128, 8, 64]), op=ALU.mult), reads=[pbb, b_t[i]], writes=[b_t[i]])
                            S.op("pool", lambda e: e.tensor_tensor(out=qb[i][:], in0=qn[i][:], in1=gt[:], op=ALU.mult), reads=[b_t[i], CONST] + cb, writes=[b_t[i]])
                            tb, tbb = nbank()
                            tv = bfview(tb)[:, 0:512].rearrange("p (h t) -> p h t", h=4)
                            for h in range(4):
                                S.op("pe", lambda e: e.transpose(out=tv[:, h, :], in_=qb[i][:, h * 128:(h + 1) * 128], identity=identb[:]), reads=[b_t[i], cb[0]], writes=[tbb])
                            if nm == "aq":
                                S.op("dve", lambda e: e.tensor_copy(out=Qbd[0:64, :, j, 0:128], in_=tv[0:64, :, :]), reads=[tbb], writes=[b_Qbd])
                                S.op("dve", lambda e: e.tensor_copy(out=Qbd[64:128, :, j, 128:256], in_=tv[64:128, :, :]), reads=[tbb], writes=[b_Qbd])
                            else:
                                S.op("act", lambda e: e.copy(out=KTb[i][:], in_=tv), reads=[tbb], writes=[b_KTb[i]])
                                S.dma("pool", kt_d.rearrange("h p k -> p h k")[:, :, blk * 128:(blk + 1) * 128], KTb[i][:], reads=[b_KTb[i]], writes=[b_ktd])
                    S.barrier()

                with ExitStack() as st:
                    for j in range(4):
                        i = j % 2
                        tsl = slice(j * 128, (j + 1) * 128)
                        lf = sbt(st, "lf%d" % j, [128, 4])
                        e1 = sbt(st, "e1%d" % j, [128, 4])
                        e2 = sbt(st, "e2%d" % j, [128, 4])
                        wsx = sbt(st, "wsx%d" % j, [128, 4])
                        RL = sbt(st, "RL%d" % j, [128, 4, 128])
                        Et = sbt(st, "Et%d" % j, [128, 512])
                        Kw = sbt(st, "Kw%d" % j, [128, 4, 128], BF16)
                        bm = S.buf("ml%d" % j)
                        b_RL = S.buf("RL")
                        b_Et = S.buf("Et")
                        b_Kw = S.buf("Kw")
                        S.op("act", lambda e: e.activation(out=lf[:], in_=gif[:, j, 4:8], func=AF.Exp, scale=-1.0), reads=[b_gif], writes=[bm])
                        S.op("act", lambda e: e.activation(out=lf[:], in_=lf[:], func=AF.Ln, bias=onet[:, 0:1], scale=1.0), reads=[bm, CONST], writes=[bm])
                        S.op("dve", lambda e: e.tensor_scalar(out=lf[:], in0=lf[:], scalar1=-1.0, scalar2=None, op0=ALU.mult), reads=[bm], writes=[bm])
                        p1, p1b = nbank()
                        S.op("pe", lambda e: e.matmul(p1[:, 0:4], lhsT=umask[:], rhs=lf[:], start=True, stop=True), reads=[bm, cb[2]], writes=[p1b])
                        S.op("dve", lambda e: e.tensor_tensor(out=RL[:], in0=umask[:].unsqueeze(1).to_broadcast([128, 4, 128]),
                                                             in1=lf[:].unsqueeze(2).to_broadcast([128, 4, 128]), op=ALU.mult), reads=[bm, cb[2]], writes=[b_RL])
                        RLf = RL[:].rearrange("p h t -> p (h t)")
                        pBc, pBcb = nbank()
                        S.op("pe", lambda e: e.matmul(pBc[:], lhsT=ones_f[:], rhs=RLf, start=True, stop=True), reads=[b_RL, CONST], writes=[pBcb])
                        S.op("dve", lambda e: e.tensor_tensor(out=e1[:], in0=gif[:, j, 0:4], in1=p1[:, 0:4], op=ALU.subtract), reads=[b_gif, p1b], writes=[bm])
                        blast = pBc[:].rearrange("p (h t) -> p h t", h=4)[:, :, 127]
                        S.op("dve", lambda e: e.tensor_tensor(out=e2[:], in0=e1[:], in1=blast, op=ALU.add), reads=[bm, pBcb], writes=[bm])
                        S.op("act", lambda e: e.activation(out=wsx[:], in_=e2[:], func=AF.Exp), reads=[bm], writes=[bm])
                        S.op("act", lambda e: e.activation(out=Et[:], in_=pBc[:], func=AF.Exp), reads=[pBcb], writes=[b_Et])
                        tb, tbb = nbank()
                        tv = bfview(tb)[:, 0:512].rearrange("p (h t) -> p h t", h=4)
                        for h in range(4):
                            S.op("pe", lambda e: e.transpose(out=tv[:, h, :], in_=mqT[:, 4 + h, tsl], identity=identb[:]), reads=[b_mqT, cb[0]], writes=[tbb])
                        S.op("dve", lambda e: e.tensor_tensor(out=Kw[:], in0=tv, in1=wsx[:].unsqueeze(2).to_broadcast([128, 4, 128]), op=ALU.mult),
                             reads=[tbb, bm], writes=[b_Kw])
                        if full:
                            DT = sbt(st, "DT%d" % j, [128, 4, 128])
                            PT = sbt(st, "PT%d" % j, [128, 4, 128], BF16)
                            qpT = sbt(st, "qpT%d" % j, [128, 4, 128], BF16)
                            rden = sbt(st, "rden%d" % j, [128, 4])
                            hmr = sbt(st, "hmr%d" % j, [128, 4, 128])
                            hsq = sbt(st, "hsq%d" % j, [128, 4, 128])
                            hss = sbt(st, "hss%d" % j, [128, 8])
                            gs = sbt(st, "gs%d" % j, [128, 512])
                            hmf = sbt(st, "hmf%d" % j, [128, 4, 128], BF16)
                            b_o = S.buf("mo%d" % j)
                            b_DT = S.buf("DT")
                            b_PT = S.buf("PT")
                            b_qp = S.buf("qpT")
                            pBm, pBmb = nbank()
                            S.op("pe", lambda e: e.matmul(pBm[:], lhsT=ones_f[:], rhs=RLf, start=True, stop=False), reads=[b_RL, CONST], writes=[pBmb])
                            S.op("pe", lambda e: e.matmul(pBm[:], lhsT=identf[:], rhs=negm4[:], start=False, stop=True), reads=[cb[1], cb[3]], writes=[pBmb])
                            for h in range(4):
                                S.op("act", lambda e: e.activation(out=DT[:, h, :], in_=pBm[:, h * 128:(h + 1) * 128], func=AF.Exp, bias=e1[:, h:h + 1], scale=1.0),
                                     reads=[pBmb, bm], writes=[b_DT])
                            S.op("dve", lambda e: e.tensor_tensor(out=qpT[:], in0=mqT[:, 0:4, tsl], in1=Et[:].rearrange("p (h t) -> p h t", h=4), op=ALU.mult),
                                 reads=[b_mqT, b_Et], writes=[b_qp])
                            pA, pAb = nbank()
                            for h in range(4):
                                S.op("pe", lambda e: e.matmul(pA[:, h * 128:(h + 1) * 128], lhsT=mqT[:, 4 + h, tsl], rhs=mqT[:, h, tsl], start=True, stop=True),
                                     reads=[b_mqT], writes=[pAb])
                            S.op("dve", lambda e: e.tensor_tensor(out=PT[:], in0=pA[:].rearrange("p (h t) -> p h t", h=4), in1=DT[:], op=ALU.mult),
                                 reads=[pAb, b_DT], writes=[b_PT])
                            pN = [nbank(), nbank()]
                            for h in range(4):
                                pn, pnb = pN[h // 2]
                                o = (h % 2) * 130
                                S.op("pe", lambda e: e.matmul(pn[:, o:o + 130], lhsT=PT[:, h, :], rhs=mVA[:, j, h, :], start=True, stop=False), reads=[b_PT, b_mVA], writes=[pnb])
                                S.op("pe", lambda e: e.matmul(pn[:, o:o + 130], lhsT=qpT[:, h, :], rhs=Sstb[:, h, :], start=False, stop=True), reads=[b_qp, b_Sb], writes=[pnb])
                            for hp in range(2):
                                pn, pnb = pN[hp]
                                pv = pn[:, 0:260].rearrange("p (h c) -> p h c", h=2)
                                S.op("act", lambda e: e.activation(out=rden[:, 2 * hp:2 * hp + 2], in_=pv[:, :, 128], func=AF.Abs), reads=[pnb], writes=[b_o])
                            S.op("dve", lambda e: e.tensor_scalar(out=rden[:], in0=rden[:], scalar1=1.0, scalar2=None, op0=ALU.max), reads=[b_o], writes=[b_o])
                            S.op("dve", lambda e: e.reciprocal(out=rden[:], in_=rden[:]), reads=[b_o], writes=[b_o])
                            for hp in range(2):
                                pn, pnb = pN[hp]
                                pv = pn[:, 0:260].rearrange("p (h c) -> p h c", h=2)
                                S.op("dve", lambda e: e.tensor_tensor(out=hmr[:, 2 * hp:2 * hp + 2, :], in0=pv[:, :, 0:128],
                                                                     in1=rden[:, 2 * hp:2 * hp + 2].unsqueeze(2).to_broadcast([128, 2, 128]), op=ALU.mult),
                                     reads=[pnb, b_o], writes=[b_o])
                            S.op("act", lambda e: e.activation(out=hsq[:], in_=hmr[:], func=AF.Square, scale=1.0 / math.sqrt(128.0)), reads=[b_o], writes=[b_o])
                            S.op("dve", lambda e: e.tensor_reduce(out=hss[:, 0:4], in_=hsq[:], axis=AX.X, op=ALU.add), reads=[b_o], writes=[b_o])
                            S.op("act", lambda e: e.activation(out=hss[:, 4:8], in_=hss[:, 0:4], func=AF.Ln, bias=epst[:, 0:1], scale=1.0), reads=[b_o, CONST], writes=[b_o])
                            S.op("act", lambda e: e.activation(out=hss[:, 4:8], in_=hss[:, 4:8], func=AF.Exp, scale=-0.5), reads=[b_o], writes=[b_o])
                            S.op("pool", lambda e: e.tensor_tensor(out=gs[:], in0=mng[:], in1=sigmo[:, j, :], op=ALU.mult), reads=[b_sig] + cb, writes=[b_o])
                            for h in range(4):
                                S.op("dve", lambda e: e.scalar_tensor_tensor(out=hmf[:, h, :], in0=hmr[:, h, :], scalar=hss[:, 4 + h:5 + h], in1=gs[:, h * 128:(h + 1) * 128],
                                                                            op0=ALU.mult, op1=ALU.mult), reads=[b_o], writes=[b_o])
                            tb2, tb2b = nbank()
                            tv2 = bfview(tb2)[:, 0:512].rearrange("p (h t) -> p h t", h=4)
                            for h in range(4):
                                S.op("pe", lambda e: e.transpose(out=tv2[:, h, :], in_=hmf[:, h, :], identity=identb[:]), reads=[b_o, cb[0]], writes=[tb2b])
                            S.op("act", lambda e: e.copy(out=hmT[:, :, tsl], in_=tv2), reads=[tb2b], writes=[b_hmT])
                        pC = [nbank(), nbank()]
                        for h in range(4):
                            pc_, pcb = pC[h // 2]
                            o = (h % 2) * 130
                            S.op("pe", lambda e: e.matmul(pc_[:, o:o + 130], lhsT=Kw[:, h, :], rhs=mVA[:, j, h, :], start=True, stop=True), reads=[b_Kw, b_mVA], writes=[pcb])
                        Ev = Et[:].rearrange("p (h t) -> p h t", h=4)
                        for h in range(4):
                            pc_, pcb = pC[h // 2]
                            o = (h % 2) * 130
                            S.op("dve", lambda e: e.scalar_tensor_tensor(out=Sst[:, h, :], in0=Sst[:, h, :], scalar=Ev[:, h, 127:128], in1=pc_[:, o:o + 130],
                                                                        op0=ALU.mult, op1=ALU.add), reads=[b_S, b_Et, pcb], writes=[b_S])
                        S.op("act", lambda e: e.copy(out=Sstb[:], in_=Sst[:]), reads=[b_S], writes=[b_Sb])
                    S.barrier()

                if full:
                    with ExitStack() as st:
                        NKC = 8
                        kch = [sbt(st, "kch%d" % i, [128, NKC * 128], BF16) for i in range(3)]
                        vch = [sbt(st, "vch%d" % i, [128, NKC, 130], BF16) for i in range(3)]
                        b_kch = [S.buf("kch%d" % i) for i in range(3)]
                        b_vch = [S.buf("vch%d" % i) for i in range(3)]
                        pex = [sbt(st, "pex%d" % i, [128, 256], BF16) for i in range(4)]
                        b_pex = [S.buf("pex%d" % i) for i in range(4)]
                        rr = sbt(st, "rr", [128, 8])
                        har = sbt(st, "har", [128, 4, 128])
                        hsq = sbt(st, "ahsq", [128, 4, 128])
                        hss = sbt(st, "ahss", [128, 8])
                        haf = sbt(st, "haf", [128, 4, 128], BF16)
                        b_a = S.buf("attn_o")
                        nkb_tot = u * 4 + 4
                        chunk_list = [(s0, min(NKC, nkb_tot - s0)) for s0 in range(0, nkb_tot, NKC)]
                        ring = 0
                        sring = 0
                        for h in range(4):
                            accs = [(banks[jq], bbufs[jq]) for jq in range(4)]
                            for jq in range(4):
                                S.op("dve", lambda e: e.memset(accs[jq][0][:, 0:260], 0.0), writes=[accs[jq][1]])
                            for (s0, nk) in chunk_list:
                                ri = ring % 3
                                ring += 1
                                S.dma("sp", kch[ri][:, 0:nk * 128], kt_d[h, :, s0 * 128:(s0 + nk) * 128], reads=[b_ktd], writes=[b_kch[ri]])
                                S.dma("sp", vch[ri][:, 0:nk, :], va_d[h, :, s0 * 130:(s0 + nk) * 130].rearrange("p (b c) -> p b c", c=130), reads=[b_vad], writes=[b_vch[ri]])
                                for kk in range(nk):
                                    kb = s0 + kk
                                    for jq in range(4):
                                        qbk = u * 4 + jq
                                        if kb > qbk:
                                            continue
                                        delta = qbk - kb
                                        sb_i = 4 + (sring // 2) % 2
                                        half = sring % 2
                                        sring += 1
                                        sp_, spb = banks[sb_i], bbufs[sb_i]
                                        sv = sp_[:, half * 256:(half + 1) * 256]
                                        near = delta <= 1
                                        S.op("pe", lambda e: e.matmul(sv, lhsT=kch[ri][:, kk * 128:(kk + 1) * 128], rhs=Qbd[:, h, jq, :], start=True, stop=not near),
                                             reads=[b_kch[ri], b_Qbd], writes=[spb])
                                        pi = sring % 4
                                        if near:
                                            S.op("pe", lambda e: e.matmul(sv, lhsT=identb[:], rhs=biasb[:, h, delta, :], start=False, stop=True), reads=[CONST, cb[0]], writes=[spb])
                                            S.op("act", lambda e: e.activation(out=pex[pi][:], in_=sv, func=AF.Exp), reads=[spb], writes=[b_pex[pi]])
                                        else:
                                            S.op("act", lambda e: e.activation(out=pex[pi][:], in_=sv, func=AF.Exp, bias=farb[:, h:h + 1], scale=1.0),
                                                 reads=[spb, cb[14]], writes=[b_pex[pi]])
                                        ab, abb = accs[jq]
                                        for m in range(2):
                                            S.op("pe", lambda e: e.matmul(ab[:, m * 130:(m + 1) * 130], lhsT=pex[pi][:, m * 128:(m + 1) * 128], rhs=vch[ri][:, kk, :],
                                                                         start=False, stop=False, skip_group_check=True), reads=[b_pex[pi], b_vch[ri]], writes=[abb])
                            for jq in range(4):
                                ab, abb = accs[jq]
                                av = ab[:, 0:260].rearrange("p (m c) -> p m c", m=2)
                                S.op("dve", lambda e: e.tensor_scalar(out=rr[:, 2 * jq:2 * jq + 2], in0=av[:, :, 128], scalar1=1e-30, scalar2=None, op0=ALU.max), reads=[abb], writes=[b_a])
                                S.op("dve", lambda e: e.reciprocal(out=rr[:, 2 * jq:2 * jq + 2], in_=rr[:, 2 * jq:2 * jq + 2]), reads=[b_a], writes=[b_a])
                                S.op("dve", lambda e: e.tensor_tensor(out=rr[:, 2 * jq + 1:2 * jq + 2], in0=rr[:, 2 * jq + 1:2 * jq + 2], in1=neglam[:], op=ALU.mult),
                                     reads=[b_a, CONST], writes=[b_a])
                                S.op("dve", lambda e: e.tensor_scalar(out=har[:, jq, :], in0=av[:, 0, 0:128], scalar1=rr[:, 2 * jq:2 * jq + 1], scalar2=None, op0=ALU.mult),
                                     reads=[abb, b_a], writes=[b_a])
                                S.op("dve", lambda e: e.scalar_tensor_tensor(out=har[:, jq, :], in0=av[:, 1, 0:128], scalar=rr[:, 2 * jq + 1:2 * jq + 2], in1=har[:, jq, :],
                                                                            op0=ALU.mult, op1=ALU.add), reads=[abb, b_a], writes=[b_a])
                            S.op("act", lambda e: e.activation(out=hsq[:], in_=har[:], func=AF.Square, scale=1.0 / math.sqrt(128.0)), reads=[b_a], writes=[b_a])
                            S.op("dve", lambda e: e.tensor_reduce(out=hss[:, 0:4], in_=hsq[:], axis=AX.X, op=ALU.add), reads=[b_a], writes=[b_a])
                            S.op("act", lambda e: e.activation(out=hss[:, 4:8], in_=hss[:, 0:4], func=AF.Ln, bias=epst[:, 0:1], scale=1.0), reads=[b_a, CONST], writes=[b_a])
                            S.op("act", lambda e: e.activation(out=hss[:, 4:8], in_=hss[:, 4:8], func=AF.Exp, scale=-0.5), reads=[b_a], writes=[b_a])
                            for jq in range(4):
                                S.op("dve", lambda e: e.scalar_tensor_tensor(out=haf[:, jq, :], in0=har[:, jq, :], scalar=hss[:, 4 + jq:5 + jq], in1=ang[:, h * 128:(h + 1) * 128],
                                                                            op0=ALU.mult, op1=ALU.mult), reads=[b_a, CONST] + cb, writes=[b_a])
                            tb, tbb = banks[6 + h % 2], bbufs[6 + h % 2]
                            tv = bfview(tb)[:, 0:512].rearrange("p (j t) -> p j t", j=4)
                            for jq in range(4):
                                S.op("pe", lambda e: e.transpose(out=tv[:, jq, :], in_=haf[:, jq, :], identity=identb[:]), reads=[b_a, cb[0]], writes=[tbb])
                            S.op("act", lambda e: e.copy(out=haT[:, h, :], in_=tv.rearrange("p j t -> p (j t)")), reads=[tbb], writes=[b_haT])
                        S.barrier()

                    h2T = hT
                    b_h2T = b_hT
                    with ExitStack() as st:
                        W = {}
                        for nm in ("bm", "ba", "gm0", "gm1", "ga0", "ga1", "out0", "out1"):
                            W[nm] = wload(st, "w_" + nm, nm)
                        yT = sbt(st, "yT", [128, 8, 512], BF16)
                        b_yT = S.buf("yT")
                        sg = [[sbt(st, "sg%d_%d" % (i, a), [128, 512]) for a in range(2)] for i in range(2)]
                        ty = [[sbt(st, "ty%d_%d" % (i, a), [128, 512]) for a in range(2)] for i in range(2)]
                        b_sg = [S.buf("sg%d" % i) for i in range(2)]
                        for c in range(8):
                            i = c % 2
                            res = []
                            for a, (bw, gw, srcT, b_src) in enumerate((("bm", "gm", hmT, b_hmT), ("ba", "ga", haT, b_haT))):
                                wt, wbf = W[bw]
                                wv = wt[:].rearrange("p (k n) -> p k n", k=4)
                                py, pyb = nbank()
                                for k in range(4):
                                    S.op("pe", lambda e: e.matmul(py[:], lhsT=wv[:, k, c * 128:(c + 1) * 128], rhs=srcT[:, k, :], start=(k == 0), stop=(k == 3)),
                                         reads=[wbf, b_src], writes=[pyb])
                                gt_, gbf = W[gw + str(c // 4)]
                                gv = gt_[:].rearrange("p (c n) -> p c n", c=8)
                                pg, pgb = nbank()
                                for dc in range(8):
                                    S.op("pe", lambda e: e.matmul(pg[:], lhsT=gv[:, dc, (c % 4) * 128:(c % 4 + 1) * 128], rhs=hT[:, dc, :], start=(dc == 0), stop=(dc == 7)),
                                         reads=[gbf, b_hT], writes=[pgb])
                                S.op("act", lambda e: e.activation(out=sg[i][a][:], in_=pg[:], func=AF.Exp, scale=-1.0), reads=[pgb], writes=[b_sg[i]])
                                S.op("pool", lambda e: e.tensor_scalar(out=sg[i][a][:], in0=sg[i][a][:], scalar1=1.0, scalar2=None, op0=ALU.add), reads=[b_sg[i]], writes=[b_sg[i]])
                                S.op("dve", lambda e: e.reciprocal(out=sg[i][a][:], in_=sg[i][a][:]), reads=[b_sg[i]], writes=[b_sg[i]])
                                S.op("dve", lambda e: e.tensor_tensor(out=ty[i][a][:], in0=py[:], in1=sg[i][a][:], op=ALU.mult), reads=[pyb, b_sg[i]], writes=[b_sg[i]])
                            S.op("pool", lambda e: e.tensor_tensor(out=yT[:, c, :], in0=ty[i][0][:], in1=ty[i][1][:], op=ALU.add), reads=[b_sg[i]], writes=[b_yT])
                        tx = [sbt(st, "tx%d" % i, [128, 512]) for i in range(2)]
                        b_tx = [S.buf("tx%d" % i) for i in range(2)]
                        for j in range(4):
                            tsl = slice(j * 128, (j + 1) * 128)
                            for n in range(2):
                                wt, wbf = W["out%d" % n]
                                wv = wt[:].rearrange("p (c n) -> p c n", c=8)
                                po, pob = nbank()
                                for c in range(8):
                                    S.op("pe", lambda e: e.matmul(po[:], lhsT=yT[:, c, tsl], rhs=wv[:, c, :], start=(c == 0), stop=(c == 7)), reads=[wbf, b_yT], writes=[pob])
                                i = (j * 2 + n) % 2
                                S.op("dve", lambda e: e.tensor_tensor(out=tx[i][:], in0=po[:], in1=gate1[:, n * 512:(n + 1) * 512], op=ALU.mult), reads=[pob, CONST], writes=[b_tx[i]])
                                S.op("pool", lambda e: e.tensor_tensor(out=xs[:, j, n * 512:(n + 1) * 512], in0=xs[:, j, n * 512:(n + 1) * 512], in1=tx[i][:], op=ALU.add),
                                     reads=[b_tx[i], b_xs[j]], writes=[b_xs[j]])
                        S.barrier()
                    with ExitStack() as st:
                        for j in range(4):
                            norm_block(st, j, mult2, shift2, h2T, b_h2T, "n2")
                        S.barrier()

                    with ExitStack() as st:
                        wd, wdb = wload(st, "w_down", "down")
                        wdv = wd[:].rearrange("p (f n) -> p f n", f=NF)
                        actT = sbt(st, "actT", [128, NF, 512], BF16)
                        b_actT = S.buf("actT")
                        wu = [None, None, None]
                        pre = [sbt(st, "fpre%d" % i, [128, 514]) for i in range(4)]
                        uu = [sbt(st, "fu%d" % i, [128, 512]) for i in range(4)]
                        b_pre = [S.buf("fpre%d" % i) for i in range(4)]
                        b_uu = [S.buf("fu%d" % i) for i in range(4)]
                        ftm = [sbt(st, "ftm%d" % i, [128, 512]) for i in range(4)]
                        b_ftm = [S.buf("ftm%d" % i) for i in range(4)]
                        fe = [sbt(st, "fe%d" % i, [128, 512]) for i in range(2)]
                        b_fe = [S.buf("fe%d" % i) for i in range(2)]
                        wus = [sbt(st, "wup%d" % i, [128, 4096], BF16) for i in range(3)]
                        b_wus = [S.buf("wup%d" % i) for i in range(3)]

                        def ldup(jj):
                            off, n = PIECES["up%d" % jj]
                            S.dma("sp", wus[jj % 3][:], wb_d[:, off:off + n], reads=[b_wb], writes=[b_wus[jj % 3]])

                        ldup(0)
                        ldup(1)
                        for jj in range(11):
                            if jj + 2 < 11:
                                ldup(jj + 2)
                            wv = wus[jj % 3][:].rearrange("p (c n) -> p c n", c=8)
                            wbf = b_wus[jj % 3]
                            for k in range(4):
                                q = jj * 4 + k
                                pb, pbb = nbank()
                                for dc in range(8):
                                    S.op("pe", lambda e: e.matmul(pb[:], lhsT=wv[:, dc, k * 128:(k + 1) * 128], rhs=h2T[:, dc, :], start=(dc == 0), stop=(dc == 7)),
                                         reads=[wbf, b_h2T], writes=[pbb])
                                S.op("pool", lambda e: e.tensor_copy(out=pre[k][:, 0:2], in_=fhalo[:, q, :]), reads=[b_fhalo], writes=[b_pre[k]])
                                if flagged:
                                    S.op("act", lambda e: e.activation(out=pre[k][:, 2:514], in_=pb[:], func=AF.Copy, scale=flag[:, 0:1]), reads=[pbb, cb[4]], writes=[b_pre[k]])
                                else:
                                    S.op("act", lambda e: e.copy(out=pre[k][:, 2:514], in_=pb[:]), reads=[pbb], writes=[b_pre[k]])
                                S.op("pool", lambda e: e.tensor_copy(out=fhalo[:, q, :], in_=pre[k][:, 512:514]), reads=[b_pre[k]], writes=[b_fhalo])
                                S.op("pool", lambda e: e.tensor_scalar(out=uu[k][:], in0=pre[k][:, 0:512], scalar1=fcw[:, q, 0:1], scalar2=None, op0=ALU.mult),
                                     reads=[b_pre[k], cb[7]], writes=[b_uu[k]])
                                S.op("pool", lambda e: e.tensor_scalar(out=ftm[k][:], in0=pre[k][:, 1:513], scalar1=fcw[:, q, 1:2], scalar2=None, op0=ALU.mult),
                                     reads=[b_pre[k], cb[7]], writes=[b_ftm[k]])
                                S.op("pool", lambda e: e.tensor_tensor(out=uu[k][:], in0=uu[k][:], in1=ftm[k][:], op=ALU.add), reads=[b_uu[k], b_ftm[k]], writes=[b_uu[k]])
                                S.op("dve", lambda e: e.scalar_tensor_tensor(out=uu[k][:], in0=pre[k][:, 2:514], scalar=fcw[:, q, 2:3], in1=uu[k][:],
                                                                            op0=ALU.mult, op1=ALU.add), reads=[b_pre[k], b_uu[k]], writes=[b_uu[k]])
                            for i in range(2):
                                f = 2 * jj + i
                                qv = jj * 4 + i
                                qg = jj * 4 + 2 + i
                                S.op("act", lambda e: e.activation(out=fe[i][:], in_=uu[2 + i][:], func=AF.Exp, scale=-1.0, bias=fcnb[:, qg:qg + 1]),
                                     reads=[b_uu[2 + i], CONST], writes=[b_fe[i]])
                                S.op("dve", lambda e: e.tensor_scalar(out=fe[i][:], in0=fe[i][:], scalar1=1.0, scalar2=None, op0=ALU.add), reads=[b_fe[i]], writes=[b_fe[i]])
                                S.op("dve", lambda e: e.reciprocal(out=fe[i][:], in_=fe[i][:]), reads=[b_fe[i]], writes=[b_fe[i]])
                                S.op("dve", lambda e: e.scalar_tensor_tensor(out=fe[i][:], in0=uu[2 + i][:], scalar=fcb[:, qg:qg + 1], in1=fe[i][:], op0=ALU.add, op1=ALU.mult),
                                     reads=[b_uu[2 + i], b_fe[i], cb[8]], writes=[b_fe[i]])
                                S.op("dve", lambda e: e.scalar_tensor_tensor(out=actT[:, f, :], in0=uu[i][:], scalar=fcb[:, qv:qv + 1], in1=fe[i][:], op0=ALU.add, op1=ALU.mult),
                                     reads=[b_uu[i], b_fe[i], cb[8]], writes=[b_actT])
                        tx = fe
                        b_tx = b_fe
                        for j in range(4):
                            tsl = slice(j * 128, (j + 1) * 128)
                            blk = u * 4 + j
                            for n in range(2):
                                po, pob = nbank()
                                for f in range(NF):
                                    S.op("pe", lambda e: e.matmul(po[:], lhsT=actT[:, f, tsl], rhs=wdv[:, f, n * 512:(n + 1) * 512], start=(f == 0), stop=(f == NF - 1)),
                                         reads=[wdb, b_actT], writes=[pob])
                                i = (j * 2 + n) % 2
                                S.op("dve", lambda e: e.tensor_tensor(out=tx[i][:], in0=po[:], in1=gate2[:, n * 512:(n + 1) * 512], op=ALU.mult), reads=[pob, CONST], writes=[b_tx[i]])
                                S.op("pool", lambda e: e.tensor_tensor(out=xs[:, j, n * 512:(n + 1) * 512], in0=xs[:, j, n * 512:(n + 1) * 512], in1=tx[i][:], op=ALU.add),
                                     reads=[b_tx[i], b_xs[j]], writes=[b_xs[j]])
                            if own:
                                ob = blk - NFLAG * 4
                                S.dma("pool", out_d[ob * 128:(ob + 1) * 128, :], xs[:, j, :], reads=[b_xs[j]], writes=[b_out])
                        S.barrier()
        S.barrier()
    return nc


def _t5_bucket(n):
    n = np.maximum(n, 0)
    max_exact = 16
    nf = np.maximum(n, 1).astype(np.float32)
    large = max_exact + (np.log(nf / np.float32(max_exact)) / np.float32(math.log(128 / max_exact)) * np.float32(32 - max_exact)).astype(np.int32)
    large = np.minimum(large, 31)
    return np.where(n < max_exact, n, large)


def _fm(v, nch):
    return np.ascontiguousarray(np.asarray(v, np.float32).reshape(nch, 128).T)


def _piece_fm(w):
    n = w.shape[1]
    return w.reshape(8, 128, n).transpose(1, 0, 2).reshape(128, 8 * n)


def prepare_inputs(NCTX, NFULL, inputs):
    f32 = np.float32
    g = {k: np.asarray(v) for k, v in inputs.items()}
    NU = NCTX + NFULL
    half_tok = (NU // 2) * 512
    x = g["x"].astype(f32, copy=False)
    B = x.shape[0]
    assert x.shape[1] == 2 * half_tok
    w_in = g["w_in"][0]
    cols = {}
    o = 0
    for nm, n in (("mqk", 1024), ("mv", 512), ("mo", 512), ("mi", 4), ("mf", 4), ("aq", 512), ("ak", 512), ("av", 512), ("gm", 1024), ("ga", 1024)):
        cols[nm] = w_in[:, o:o + n]
        o += n

    def perm_qk(w):
        return w.reshape(1024, 2, 4, 64).transpose(0, 2, 1, 3).reshape(1024, 512)

    wall = np.zeros((128, WPAD), f32)

    def put(nm, arr):
        off, n = PIECES[nm]
        assert arr.shape == (128, n), (nm, arr.shape, n)
        wall[:, off:off + n] = arr

    put("mq", _piece_fm(cols["mqk"][:, 0:512]))
    put("mk", _piece_fm(cols["mqk"][:, 512:1024]))
    put("mv", _piece_fm(cols["mv"]))
    put("mo", _piece_fm(cols["mo"]))
    put("aq", _piece_fm(perm_qk(cols["aq"])))
    put("ak", _piece_fm(perm_qk(cols["ak"])))
    put("av", _piece_fm(cols["av"]))
    put("gm0", _piece_fm(cols["gm"][:, 0:512]))
    put("gm1", _piece_fm(cols["gm"][:, 512:1024]))
    put("ga0", _piece_fm(cols["ga"][:, 0:512]))
    put("ga1", _piece_fm(cols["ga"][:, 512:1024]))
    put("gif", _piece_fm(np.concatenate([cols["mi"], cols["mf"]], axis=1)))
    put("bm", g["w_branch_m"][0].reshape(4, 128, 1024).transpose(1, 0, 2).reshape(128, 4096))
    put("ba", g["w_branch_a"][0].reshape(4, 128, 1024).transpose(1, 0, 2).reshape(128, 4096))
    put("out0", _piece_fm(g["w_out"][0][:, 0:512]))
    put("out1", _piece_fm(g["w_out"][0][:, 512:1024]))
    w_up = g["w_up"][0]
    fcw_full = g["ffn_conv_w"][0]
    fcb_full = g["ffn_conv_b"][0]
    chunk_cols = []
    for jj in range(11):
        cc = [np.arange((2 * jj + i) * 128, (2 * jj + i + 1) * 128) for i in range(2)]
        cc += [DFF + np.arange((2 * jj + i) * 128, (2 * jj + i + 1) * 128) for i in range(2)]
        idx = np.concatenate(cc)
        chunk_cols.append(idx)
        put("up%d" % jj, _piece_fm(w_up[:, idx]))
    allidx = np.concatenate(chunk_cols)
    fcw = fcw_full[:, allidx].reshape(3, 44, 128).transpose(2, 1, 0).reshape(128, 44 * 3)
    fcb = fcb_full[allidx].reshape(44, 128).T
    put("down", g["w_down"][0].reshape(NF, 128, 1024).transpose(1, 0, 2).reshape(128, NF * 1024))

    w_ada = g["w_ada"][0]
    wada = np.stack([_piece_fm(w_ada[:, p * 512:(p + 1) * 512]) for p in range(12)]).astype(f32)
    bada = g["b_ada"][0].astype(f32)
    mcw = g["m_conv_w"][0].reshape(4, 8, 128).transpose(2, 1, 0).reshape(128, 32)
    mcb = _fm(g["m_conv_b"][0], 8)
    gifb = np.concatenate([g["m_igate_b"][0], g["m_fgate_b"][0]]).astype(f32)
    gq = np.tile(g["a_qnorm_g"][0], 8).astype(f32)
    gk = np.tile(g["a_knorm_g"][0], 8).astype(f32)
    rel = g["rel_bias"].astype(f32)
    kk = np.arange(128)[:, None]
    qq = np.arange(128)[None, :]
    biasg = np.zeros((128, 4, 2, 2, 128), f32)
    maskn = np.zeros((128, 4, 2, 2, 128), f32)
    for dl in range(2):
        dist = qq - kk + 128 * dl
        bidx = _t5_bucket(dist)
        for h in range(4):
            t = rel[bidx, h]
            biasg[:, h, dl, 0, :] = t
            biasg[:, h, dl, 1, :] = t
        if dl == 0:
            mk = np.where(dist < 0, NEG, 0.0).astype(f32)
            maskn[:, :, 0, :, :] = mk[:, None, None, :]
    farb = rel[31, :].astype(f32)
    umask = (kk <= qq).astype(f32)
    negm4 = np.tile(np.where(kk <= qq, 0.0, NEG).astype(f32), (1, 4))
    import ml_dtypes
    common = dict(
        wada=wada, bada=bada, badafm=_fm(bada, 48), g1fm=_fm(g["norm1_g"][0], 8), g2fm=_fm(g["norm2_g"][0], 8),
        wall=wall, mcw=np.ascontiguousarray(mcw, f32), mcb=mcb, fcw=np.ascontiguousarray(fcw, f32), fcb=np.ascontiguousarray(fcb, f32),
        gifb=gifb, mng=g["m_norm_g"][0].astype(f32), ang=g["a_norm_g"][0].astype(f32), gq=gq, gk=gk,
        alam=g["a_lambda"][0].reshape(256).astype(f32), biasg=biasg.reshape(128, -1), maskn=maskn.reshape(128, -1), farb=farb,
        identb=np.eye(128).astype(ml_dtypes.bfloat16), identf=np.eye(128, dtype=f32), umask=umask, negm4=negm4,
    )
    in_maps = []
    for b in range(B):
        cfm = _fm(g["c"][b], 8)
        for hf in range(2):
            if hf == 0:
                xl = np.concatenate([np.zeros((half_tok, D), f32), x[b, 0:half_tok]], axis=0)
                fl = np.zeros((128, 1), f32)
            else:
                xl = x[b]
                fl = np.ones((128, 1), f32)
            m = dict(common)
            m["x"] = np.ascontiguousarray(xl)
            m["cfm"] = cfm
            m["flag"] = fl
            in_maps.append(m)
    return in_maps


_NC_CACHE = {}


def run(NCTX, NFULL, inputs):
    key = (NCTX, NFULL)
    if key not in _NC_CACHE:
        _NC_CACHE[key] = build_program(NCTX, NFULL)
    nc = _NC_CACHE[key]
    in_maps = prepare_inputs(NCTX, NFULL, inputs)
    res = run_bass_kernel_spmd(nc, in_maps, core_ids=list(range(len(in_maps))))
    B = len(in_maps) // 2
    half_tok = ((NCTX + NFULL) // 2) * 512
    out = np.empty((B, 2 * half_tok, D), np.float32)
    for b in range(B):
        for hf in range(2):
            out[b, hf * half_tok:(hf + 1) * half_tok] = res.results[b * 2 + hf]["out"]
    return out


def kernel(**inputs):
    return run(7, 9, inputs)
```

```python
import math
from contextlib import ExitStack

import numpy as np

import concourse.bass as bass
import concourse.mybir as mybir
from concourse.bass_utils import run_bass_kernel_spmd

F32 = mybir.dt.float32
BF16 = mybir.dt.bfloat16
AF = mybir.ActivationFunctionType
ALU = mybir.AluOpType
AX = mybir.AxisListType

D = 1024
DC = 8
DFF = 2816
NF = 22
EPS = 1e-6
LAM_INIT = 0.8 - 0.6 * math.exp(-0.3 * 0)
NEG = -30000.0

PIECES = {}
_off = 0
for _nm, _n in (("mq", 4096), ("mk", 4096), ("mv", 4096), ("mo", 4096), ("aq", 4096), ("ak", 4096),
                ("av", 4096), ("gm0", 4096), ("gm1", 4096), ("ga0", 4096), ("ga1", 4096), ("gif", 64),
                ("bm", 4096), ("ba", 4096), ("out0", 4096), ("out1", 4096)):
    PIECES[_nm] = (_off, _n)
    _off += _n
for _j in range(11):
    PIECES["up%d" % _j] = (_off, 4096)
    _off += 4096
PIECES["down"] = (_off, NF * 1024)
_off += NF * 1024
WTOT = _off
WPAD = ((WTOT + 4095) // 4096) * 4096


class Buf:
    __slots__ = ("name", "w", "rs", "sem", "semv", "grp")

    def __init__(self, name, grp=None):
        self.name = name
        self.w = None
        self.rs = []
        self.sem = None
        self.semv = 0
        self.grp = grp


class Sched:
    def __init__(self, nc, stack):
        self.nc = nc
        self.stack = stack
        self.engs = {}
        for nm, h in (("pe", nc.tensor), ("act", nc.scalar), ("dve", nc.vector), ("pool", nc.gpsimd), ("sp", nc.sync)):
            sem = stack.enter_context(nc.semaphore("s_" + nm))
            self.engs[nm] = dict(h=h, sem=sem, cnt=0, seen={})
        self.groups = {}
        self.dma_ev = {}
        self.free_sems = {}
        self.live = []
        self.nsem = 0

    def buf(self, name, grp=None):
        return Buf(name, grp)

    def _need(self, e, deps):
        E = self.engs[e]
        best = {}
        for d in deps:
            if d is None:
                continue
            sem, val, en = d
            if en == e and e == "pe":
                continue
            k = id(sem)
            if k not in best or best[k][1] < val:
                best[k] = (sem, val)
        for k, (sem, val) in best.items():
            if E["seen"].get(k, 0) >= val:
                continue
            E["h"].wait_ge(sem, val)
            E["seen"][k] = val

    def op(self, e, fn, reads=(), writes=()):
        E = self.engs[e]
        deps = []
        for b in reads:
            deps.append(b.w)
        for b in writes:
            if b.w is not None and b.w[2] != e:
                deps.append(b.w)
            for r in b.rs:
                if r[2] != e:
                    deps.append(r)
        self._need(e, deps)
        ins = fn(E["h"])
        E["cnt"] += 1
        ins.then_inc(E["sem"], 1)
        ev = (E["sem"], E["cnt"], e)
        for b in reads:
            b.rs.append(ev)
        for b in writes:
            b.w = ev
            b.rs = []
        return ins

    def dma(self, q, out, in_, reads=(), writes=()):
        E = self.engs[q]
        deps = []
        for b in reads:
            deps.append(b.w)
        for b in writes:
            if b.grp in ("wbw", "kvw", "outw"):
                continue
            deps.append(b.w)
            deps.extend(b.rs)
        self._need(q, deps)
        ins = E["h"].dma_start(out=out, in_=in_)
        cands = list(writes) + list(reads)
        tgt = ([b for b in cands if b.grp is None] + cands)[0]
        if tgt.grp is not None:
            gk = tgt.grp
            if gk not in self.groups:
                self.groups[gk] = [self.stack.enter_context(self.nc.semaphore("g_" + gk)), 0]
            g = self.groups[gk]
            g[1] += 16
            sem, val = g[0], g[1]
        else:
            if tgt.sem is None:
                tgt.sem = {}
                self.live.append(tgt)
            if q not in tgt.sem:
                fs = self.free_sems.setdefault(q, [])
                if not fs:
                    self.nsem += 1
                    fs.append([self.stack.enter_context(self.nc.semaphore("dp%d" % self.nsem)), 0])
                tgt.sem[q] = fs.pop()
            ent = tgt.sem[q]
            ent[1] += 16
            sem, val = ent[0], ent[1]
        ins.then_inc(sem, 16)
        self.dma_ev[id(sem)] = (sem, val)
        ev = (sem, val, "dma")
        for b in reads:
            b.rs.append(ev)
        for b in writes:
            b.w = ev
            b.rs = []
        return ins

    def barrier(self, engines=("pe", "act", "dve", "pool", "sp")):
        deps = [(E["sem"], E["cnt"], "x") for E in self.engs.values() if E["cnt"] > 0]
        deps += [(s, v, "dma") for (s, v) in self.dma_ev.values()]
        for e in engines:
            self._need(e, deps)
        self.dma_ev = {}
        for b in self.live:
            for qq, ent in b.sem.items():
                self.free_sems[qq].append(ent)
            b.sem = None
        self.live = []

    def seal(self, bufs, grp):
        g = self.groups[grp]
        for b in bufs:
            b.w = (g[0], g[1], "dma")


def build_program(NCTX, NFULL, debug=False):
    NU = NCTX + NFULL
    NFLAG = NU // 2
    NBLK = NU * 4
    NTOK = NU * 512
    NOWN = (NFULL - 1) * 512
    assert NU == 2 * (NFULL - 1)

    nc = bass.Bass("TRN2", target_bir_lowering=False)

    def din(name, shape, dt=F32):
        return nc.dram_tensor(name, list(shape), dt, kind="ExternalInput").ap()

    x_d = din("x", [NTOK, D])
    cfm_d = din("cfm", [128, 8])
    wada_d = din("wada", [12, 128, 8 * 512])
    bada_d = din("bada", [6144])
    badafm_d = din("badafm", [128, 48])
    g1fm_d = din("g1fm", [128, 8])
    g2fm_d = din("g2fm", [128, 8])
    wall_d = din("wall", [128, WPAD])
    mcw_d = din("mcw", [128, 8 * 4])
    mcb_d = din("mcb", [128, 8])
    fcw_d = din("fcw", [128, 44 * 3])
    fcb_d = din("fcb", [128, 44])
    gifb_d = din("gifb", [8])
    mng_d = din("mng", [512])
    ang_d = din("ang", [512])
    gq_d = din("gq", [512])
    gk_d = din("gk", [512])
    alam_d = din("alam", [256])
    biasg_d = din("biasg", [128, 4 * 2 * 2 * 128])
    maskn_d = din("maskn", [128, 4 * 2 * 2 * 128])
    farb_d = din("farb", [4])
    flag_d = din("flag", [128, 1])
    identb_d = din("identb", [128, 128], BF16)
    identf_d = din("identf", [128, 128])
    umask_d = din("umask", [128, 128])
    negm4_d = din("negm4", [128, 512])
    out_d = nc.dram_tensor("out", [NOWN, D], F32, kind="ExternalOutput").ap()
    wb_d = nc.dram_tensor("wb_scr", [128, WPAD], BF16, kind="Internal").ap()
    kt_d = nc.dram_tensor("kt_scr", [4, 128, NBLK * 128], BF16, kind="Internal").ap()
    va_d = nc.dram_tensor("va_scr", [4, 128, NBLK * 130], BF16, kind="Internal").ap()

    with ExitStack() as top:
        S = Sched(nc, top)

        uid = [0]

        def sbt(st, name, shape, dt=F32):
            uid[0] += 1
            return st.enter_context(nc.sbuf_tensor("s%d_%s" % (uid[0], name), list(shape), dt))

        banks = [top.enter_context(nc.psum_tensor("bank%d" % i, [128, 512], F32)) for i in range(8)]
        bbufs = [S.buf("bank%d" % i) for i in range(8)]
        bank_rr = [0]

        def nbank(lo=0, hi=8):
            i = lo + (bank_rr[0] % (hi - lo))
            bank_rr[0] += 1
            return banks[i], bbufs[i]

        def bfview(bank):
            return bank[:].bitcast(BF16)

        identb = sbt(top, "identb", [128, 128], BF16)
        identf = sbt(top, "identf", [128, 128])
        umask = sbt(top, "umask", [128, 128])
        negm4 = sbt(top, "negm4", [128, 512])
        ones_f = sbt(top, "ones_f", [128, 128])
        flag = sbt(top, "flag", [128, 1])
        epst = sbt(top, "epst", [128, 1])
        onet = sbt(top, "onet", [128, 1])
        mult1 = sbt(top, "mult1", [128, 8])
        shift1 = sbt(top, "shift1", [128, 8])
        mult2 = sbt(top, "mult2", [128, 8])
        shift2 = sbt(top, "shift2", [128, 8])
        gate1 = sbt(top, "gate1", [128, D])
        gate2 = sbt(top, "gate2", [128, D])
        mcw = sbt(top, "mcw", [128, 8, 4])
        mcb = sbt(top, "mcb", [128, 8])
        mcwf = sbt(top, "mcwf", [128, 8, 4])
        fcwf = sbt(top, "fcwf", [128, 44, 3])
        mcnb = sbt(top, "mcnb", [128, 8])
        fcw = sbt(top, "fcw", [128, 44, 3])
        fcb = sbt(top, "fcb", [128, 44])
        fcnb = sbt(top, "fcnb", [128, 44])
        gifb = sbt(top, "gifb", [128, 8])
        mng = sbt(top, "mng", [128, 512])
        ang = sbt(top, "ang", [128, 512])
        gq = sbt(top, "gq", [128, 512])
        gk = sbt(top, "gk", [128, 512])
        biasb = sbt(top, "biasb", [128, 4, 2, 256], BF16)
        farb = sbt(top, "farb", [128, 4])
        neglam = sbt(top, "neglam", [128, 1])
        Sst = sbt(top, "Sst", [128, 4, 130])
        Sstb = sbt(top, "Sstb", [128, 4, 130], BF16)
        mhalo = sbt(top, "mhalo", [128, 8, 3])
        fhalo = sbt(top, "fhalo", [128, 44, 2])
        CONST = S.buf("const")
        b_S = S.buf("Sst")
        b_Sb = S.buf("Sstb")
        b_mhalo = S.buf("mhalo")
        b_fhalo = S.buf("fhalo")
        b_wb = S.buf("wb_scr", grp="wbw")
        b_ktd = S.buf("kt_scr", grp="kvw")
        b_vad = S.buf("va_scr", grp="kvw")
        b_out = S.buf("outd", grp="outw")

        def ld_const(t, src, grp="cst"):
            b = S.buf("c_" + t.name, grp=grp)
            S.dma("sp", t[:], src, writes=[b])
            return b

        cb = []
        cb.append(ld_const(identb, identb_d[:, :]))
        cb.append(ld_const(identf, identf_d[:, :]))
        cb.append(ld_const(umask, umask_d[:, :]))
        cb.append(ld_const(negm4, negm4_d[:, :]))
        cb.append(ld_const(flag, flag_d[:, :]))
        cb.append(ld_const(mcw, mcw_d.rearrange("p (c k) -> p c k", k=4)))
        cb.append(ld_const(mcb, mcb_d[:, :]))
        cb.append(ld_const(fcw, fcw_d.rearrange("p (c k) -> p c k", k=3)))
        cb.append(ld_const(fcb, fcb_d[:, :]))
        cb.append(ld_const(gifb, gifb_d.partition_broadcast(128)))
        cb.append(ld_const(mng, mng_d.partition_broadcast(128)))
        cb.append(ld_const(ang, ang_d.partition_broadcast(128)))
        cb.append(ld_const(gq, gq_d.partition_broadcast(128)))
        cb.append(ld_const(gk, gk_d.partition_broadcast(128)))
        cb.append(ld_const(farb, farb_d.partition_broadcast(128)))
        S.seal(cb, "cst")
        S.op("dve", lambda e: e.memset(ones_f[:], 1.0), writes=[CONST])
        S.op("dve", lambda e: e.memset(epst[:], EPS), writes=[CONST])
        S.op("dve", lambda e: e.memset(onet[:], 1.0), writes=[CONST])
        S.op("dve", lambda e: e.memset(Sst[:], 0.0), writes=[b_S])
        S.op("dve", lambda e: e.memset(Sstb[:], 0.0), writes=[b_Sb])
        S.op("dve", lambda e: e.memset(mhalo[:], 0.0), writes=[b_mhalo])
        S.op("dve", lambda e: e.memset(fhalo[:], 0.0), writes=[b_fhalo])
        S.op("dve", lambda e: e.tensor_scalar(out=mcnb[:], in0=mcb[:], scalar1=-1.0, scalar2=None, op0=ALU.mult), reads=cb, writes=[CONST])
        S.op("dve", lambda e: e.tensor_scalar(out=fcnb[:], in0=fcb[:], scalar1=-1.0, scalar2=None, op0=ALU.mult), reads=cb, writes=[CONST])
        S.op("dve", lambda e: e.tensor_scalar(out=gq[:], in0=gq[:], scalar1=0.125, scalar2=None, op0=ALU.mult), reads=cb, writes=[CONST])
        S.op("dve", lambda e: e.tensor_scalar(out=mcwf[:], in0=mcw[:], scalar1=flag[:, 0:1], scalar2=None, op0=ALU.mult), reads=cb, writes=[CONST])
        S.op("dve", lambda e: e.tensor_scalar(out=fcwf[:], in0=fcw[:], scalar1=flag[:, 0:1], scalar2=None, op0=ALU.mult), reads=cb, writes=[CONST])
        S.op("dve", lambda e: e.tensor_scalar(out=ang[:], in0=ang[:], scalar1=1.0 - LAM_INIT, scalar2=None, op0=ALU.mult), reads=cb, writes=[CONST])

        with ExitStack() as st:
            cfm = sbt(st, "cfm", [128, 8])
            sc = sbt(st, "sc", [128, 8])
            sct = sbt(st, "sct", [128, 8])
            scbc = sbt(st, "scbc", [128, 8, 128])
            badafm = sbt(st, "badafm", [128, 48])
            g1fm = sbt(st, "g1fm", [128, 8])
            g2fm = sbt(st, "g2fm", [128, 8])
            modfm = sbt(st, "modfm", [128, 48])
            alam = sbt(st, "alam", [128, 256])
            lt = sbt(st, "lt", [128, 128])
            ls = sbt(st, "ls", [128, 2])
            biasg = sbt(st, "biasg", [128, 2048])
            maskn = sbt(st, "maskn", [128, 2048])
            wst = [sbt(st, "wst%d" % i, [128, 4096]) for i in range(2)]
            wbo = [sbt(st, "wbo%d" % i, [128, 4096], BF16) for i in range(2)]
            bbt = [sbt(st, "bbt%d" % i, [128, 512]) for i in range(2)]
            b_wst = [S.buf("wst%d" % i) for i in range(2)]
            b_wbo = [S.buf("wbo%d" % i) for i in range(2)]
            b_bbt = [S.buf("bbt%d" % i) for i in range(2)]
            L = S.buf("prel")
            b_l = [ld_const(cfm, cfm_d[:, :], "cst2"), ld_const(badafm, badafm_d[:, :], "cst2"), ld_const(g1fm, g1fm_d[:, :], "cst2"),
                   ld_const(g2fm, g2fm_d[:, :], "cst2"), ld_const(alam, alam_d.partition_broadcast(128), "cst2"),
                   ld_const(biasg, biasg_d[:, :], "cst2"), ld_const(maskn, maskn_d[:, :], "cst2")]
            S.seal(b_l, "cst2")
            S.op("act", lambda e: e.activation(out=sct[:], in_=cfm[:], func=AF.Exp, scale=-1.0), reads=b_l, writes=[L])
            S.op("dve", lambda e: e.tensor_scalar(out=sct[:], in0=sct[:], scalar1=1.0, scalar2=None, op0=ALU.add), reads=[L], writes=[L])
            S.op("dve", lambda e: e.reciprocal(out=sct[:], in_=sct[:]), reads=[L], writes=[L])
            S.op("dve", lambda e: e.tensor_tensor(out=sc[:], in0=cfm[:], in1=sct[:], op=ALU.mult), reads=[L], writes=[L])
            S.op("dve", lambda e: e.tensor_copy(out=scbc[:], in_=sc[:].unsqueeze(2).to_broadcast([128, 8, 128])), reads=[L], writes=[L])
            S.op("dve", lambda e: e.tensor_tensor(out=lt[:, 0:64], in0=alam[:, 0:64], in1=alam[:, 64:128], op=ALU.mult), reads=b_l, writes=[L])
            S.op("dve", lambda e: e.tensor_tensor(out=lt[:, 64:128], in0=alam[:, 128:192], in1=alam[:, 192:256], op=ALU.mult), reads=[L], writes=[L])
            S.op("dve", lambda e: e.tensor_reduce(out=ls[:], in_=lt[:].rearrange("p (a b) -> p a b", a=2), axis=AX.X, op=ALU.add), reads=[L], writes=[L])
            S.op("act", lambda e: e.activation(out=ls[:], in_=ls[:], func=AF.Exp), reads=[L], writes=[L])
            S.op("dve", lambda e: e.tensor_tensor(out=neglam[:], in0=ls[:, 1:2], in1=ls[:, 0:1], op=ALU.subtract), reads=[L], writes=[CONST])
            S.op("dve", lambda e: e.tensor_scalar(out=neglam[:], in0=neglam[:], scalar1=-LAM_INIT, scalar2=None, op0=ALU.add), reads=[CONST], writes=[CONST])
            S.op("dve", lambda e: e.tensor_tensor(out=biasb[:].rearrange("p a b c -> p (a b c)"), in0=biasg[:], in1=maskn[:], op=ALU.add), reads=b_l, writes=[CONST])

            fm_ps, fm_b = banks[7], bbufs[7]
            fm_cols = {0: 0, 1: 4, 2: 8, 3: 12, 6: 24, 7: 28, 8: 32, 9: 36}
            for pc in range(12):
                i = pc % 2
                S.dma("sp", wst[i][:], wada_d[pc, :, :], writes=[b_wst[i]])
                wv = wst[i][:].rearrange("p (c n) -> p c n", c=8)
                if pc in (4, 5, 10, 11):
                    S.dma("sp", bbt[i][:], bada_d[pc * 512:(pc + 1) * 512].partition_broadcast(128), writes=[b_bbt[i]])
                    pb, pbb = nbank(0, 6)
                    for dc in range(8):
                        S.op("pe", lambda e: e.matmul(pb[:], lhsT=scbc[:, dc, :], rhs=wv[:, dc, :], start=(dc == 0), stop=(dc == 7)),
                             reads=[L, b_wst[i]], writes=[pbb])
                    gt = gate1 if pc < 6 else gate2
                    off = (pc % 2) * 512
                    S.op("dve", lambda e: e.tensor_tensor(out=gt[:, off:off + 512], in0=pb[:], in1=bbt[i][:], op=ALU.add),
                         reads=[pbb, b_bbt[i]], writes=[CONST])
                else:
                    for k in range(4):
                        col = fm_cols[pc] + k
                        for dc in range(8):
                            S.op("pe", lambda e: e.matmul(fm_ps[:, col:col + 1], lhsT=wv[:, dc, k * 128:(k + 1) * 128], rhs=sc[:, dc:dc + 1],
                                                         start=(dc == 0), stop=(dc == 7)), reads=[L, b_wst[i]], writes=[fm_b])
            S.op("dve", lambda e: e.tensor_tensor(out=modfm[:, 0:16], in0=fm_ps[:, 0:16], in1=badafm[:, 0:16], op=ALU.add), reads=[fm_b] + b_l, writes=[L])
            S.op("dve", lambda e: e.tensor_tensor(out=modfm[:, 24:40], in0=fm_ps[:, 24:40], in1=badafm[:, 24:40], op=ALU.add), reads=[fm_b] + b_l, writes=[L])
            S.op("dve", lambda e: e.scalar_tensor_tensor(out=mult1[:], in0=modfm[:, 8:16], scalar=1.0, in1=g1fm[:], op0=ALU.add, op1=ALU.mult), reads=[L], writes=[CONST])
            S.op("dve", lambda e: e.tensor_copy(out=shift1[:], in_=modfm[:, 0:8]), reads=[L], writes=[CONST])
            S.op("dve", lambda e: e.scalar_tensor_tensor(out=mult2[:], in0=modfm[:, 32:40], scalar=1.0, in1=g2fm[:], op0=ALU.add, op1=ALU.mult), reads=[L], writes=[CONST])
            S.op("dve", lambda e: e.tensor_copy(out=shift2[:], in_=modfm[:, 24:32]), reads=[L], writes=[CONST])

            cast_eng = ["dve", "act"]
            for ch in range(WPAD // 4096):
                i = ch % 2
                S.dma("sp", wst[i][:], wall_d[:, ch * 4096:(ch + 1) * 4096], writes=[b_wst[i]])
                ce = cast_eng[ch % 2]
                if ce == "act":
                    S.op("act", lambda e: e.copy(out=wbo[i][:], in_=wst[i][:]), reads=[b_wst[i]], writes=[b_wbo[i]])
                else:
                    S.op(ce, lambda e: e.tensor_copy(out=wbo[i][:], in_=wst[i][:]), reads=[b_wst[i]], writes=[b_wbo[i]])
                S.dma("pool", wb_d[:, ch * 4096:(ch + 1) * 4096], wbo[i][:], reads=[b_wbo[i]], writes=[b_wb])
            S.barrier()

        def wload(st, name, piece, shape3=None):
            off, n = PIECES[piece]
            t = sbt(st, name, [128, n], BF16)
            b = S.buf(name)
            S.dma("sp", t[:], wb_d[:, off:off + n], reads=[b_wb], writes=[b])
            return t, b

        for u in range(NU):
            full = u >= NCTX
            flagged = u < NFLAG
            own = u >= NFLAG
            with ExitStack() as su:
                xs = sbt(su, "xs", [128, 4, D])
                hT = sbt(su, "hT", [128, 8, 512], BF16)
                mqT = sbt(su, "mqT", [128, 8, 512], BF16)
                mVA = sbt(su, "mVA", [128, 4, 4, 130], BF16)
                sigmo = sbt(su, "sigmo", [128, 4, 512])
                gif = sbt(su, "gif", [128, 4, 8])
                Qbd = sbt(su, "Qbd", [128, 4, 4, 256], BF16)
                hmT = sbt(su, "hmT", [128, 4, 512], BF16)
                haT = sbt(su, "haT", [128, 4, 512], BF16)
                b_xs = [S.buf("xs%d" % j) for j in range(4)]
                b_hT = S.buf("hT")
                b_mqT = S.buf("mqT")
                b_mVA = S.buf("mVA")
                b_sig = S.buf("sigmo")
                b_gif = S.buf("gif")
                b_Qbd = S.buf("Qbd")
                b_hmT = S.buf("hmT")
                b_haT = S.buf("haT")
                if full:
                    S.op("dve", lambda e: e.memset(Qbd[:], 0.0), writes=[b_Qbd])

                def norm_block(st, j, mult, shift, hdst, b_hdst, tagp):
                    junk = sbt(st, tagp + "junk%d" % j, [128, D], BF16)
                    xn = sbt(st, tagp + "xn%d" % j, [128, D], BF16)
                    ss = sbt(st, tagp + "ss%d" % j, [128, 2])
                    bl = S.buf("nb")
                    S.op("act", lambda e: e.activation(out=junk[:], in_=xs[:, j, :], func=AF.Square, scale=1.0 / 32.0, accum_out=ss[:, 0:1]),
                         reads=[b_xs[j]], writes=[bl])
                    S.op("act", lambda e: e.activation(out=ss[:, 1:2], in_=ss[:, 0:1], func=AF.Ln, bias=epst[:, 0:1], scale=1.0), reads=[bl, CONST], writes=[bl])
                    S.op("act", lambda e: e.activation(out=ss[:, 1:2], in_=ss[:, 1:2], func=AF.Exp, scale=-0.5), reads=[bl], writes=[bl])
                    S.op("dve", lambda e: e.tensor_scalar(out=xn[:], in0=xs[:, j, :], scalar1=ss[:, 1:2], scalar2=None, op0=ALU.mult),
                         reads=[bl, b_xs[j]], writes=[bl])
                    pb, pbb = nbank()
                    pv = bfview(pb).rearrange("p (c t) -> p c t", c=8)
                    for c in range(8):
                        S.op("pe", lambda e: e.transpose(out=pv[:, c, :], in_=xn[:, c * 128:(c + 1) * 128], identity=identb[:]),
                             reads=[bl, cb[0]], writes=[pbb])
                    for c in range(8):
                        S.op("act", lambda e: e.activation(out=hdst[:, c, j * 128:(j + 1) * 128], in_=pv[:, c, :], func=AF.Identity,
                                                          scale=mult[:, c:c + 1], bias=shift[:, c:c + 1]), reads=[pbb, CONST], writes=[b_hdst])

                with ExitStack() as st:
                    names = ["mk", "mv", "gif", "ak", "av"] + (["mq", "mo", "aq"] if full else [])
                    W = {}
                    for nm in names:
                        W[nm] = wload(st, "w_" + nm, nm)
                    for j in range(4):
                        blk = u * 4 + j
                        S.dma("sp", xs[:, j, :], x_d[blk * 128:(blk + 1) * 128, :], writes=[b_xs[j]])
                    for j in range(4):
                        norm_block(st, j, mult1, shift1, hT, b_hT, "n1")
                    if flagged:
                        S.op("dve", lambda e: e.tensor_copy(out=mVA[:, :, :, 128:130], in_=flag[:, 0:1].unsqueeze(1).unsqueeze(1).to_broadcast([128, 4, 4, 2])),
                             reads=[cb[4]], writes=[b_mVA])
                    else:
                        S.op("dve", lambda e: e.memset(mVA[:, :, :, 128:130], 1.0), writes=[b_mVA])
                    acc = [sbt(st, "acc%d" % i, [128, 512]) for i in range(2)]
                    et = [sbt(st, "et%d" % i, [128, 512]) for i in range(2)]
                    b_acc = [S.buf("acc%d" % i) for i in range(2)]
                    b_et = [S.buf("et%d" % i) for i in range(2)]
                    chunks = list(range(8)) if full else list(range(4, 8))
                    for ci, c in enumerate(chunks):
                        i = ci % 2
                        wt, wbf = W["mq" if c < 4 else "mk"]
                        wv = wt[:].rearrange("p (c n) -> p c n", c=8)
                        k = c % 4
                        pb, pbb = nbank()
                        for dc in range(8):
                            S.op("pe", lambda e: e.matmul(pb[:], lhsT=wv[:, dc, k * 128:(k + 1) * 128], rhs=hT[:, dc, :], start=(dc == 0), stop=(dc == 7)),
                                 reads=[wbf, b_hT], writes=[pbb])
                        wsel = mcwf if flagged else mcw
                        S.op("act", lambda e: e.activation(out=acc[i][:], in_=pb[:], func=AF.Identity, scale=wsel[:, c, 3:4], bias=mcb[:, c:c + 1]),
                             reads=[pbb, CONST] + cb, writes=[b_acc[i]])
                        for tp in range(3):
                            sh = 3 - tp
                            S.op("dve", lambda e: e.scalar_tensor_tensor(out=acc[i][:, sh:512], in0=pb[:, 0:512 - sh], scalar=wsel[:, c, tp:tp + 1], in1=acc[i][:, sh:512],
                                                                        op0=ALU.mult, op1=ALU.add), reads=[pbb, b_acc[i], CONST], writes=[b_acc[i]])
                        for tp in range(3):
                            sh = 3 - tp
                            S.op("dve", lambda e: e.scalar_tensor_tensor(out=acc[i][:, 0:sh], in0=mhalo[:, c, 3 - sh:3], scalar=mcw[:, c, tp:tp + 1], in1=acc[i][:, 0:sh],
                                                                        op0=ALU.mult, op1=ALU.add), reads=[b_mhalo, b_acc[i]] + cb, writes=[b_acc[i]])
                        if flagged:
                            S.op("act", lambda e: e.activation(out=mhalo[:, c, :], in_=pb[:, 509:512], func=AF.Copy, scale=flag[:, 0:1]), reads=[pbb, cb[4]], writes=[b_mhalo])
                        else:
                            S.op("act", lambda e: e.copy(out=mhalo[:, c, :], in_=pb[:, 509:512]), reads=[pbb], writes=[b_mhalo])
                        S.op("act", lambda e: e.activation(out=et[i][:], in_=acc[i][:], func=AF.Exp, scale=-1.0), reads=[b_acc[i]], writes=[b_et[i]])
                        sk = (128.0 ** 0.5) if c < 4 else 1.0
                        S.op("dve", lambda e: e.tensor_scalar(out=et[i][:], in0=et[i][:], scalar1=1.0, scalar2=sk, op0=ALU.add, op1=ALU.mult),
                             reads=[b_et[i]], writes=[b_et[i]])
                        S.op("dve", lambda e: e.reciprocal(out=et[i][:], in_=et[i][:]), reads=[b_et[i]], writes=[b_et[i]])
                        S.op("dve", lambda e: e.tensor_tensor(out=mqT[:, c, :], in0=acc[i][:], in1=et[i][:], op=ALU.mult),
                             reads=[b_acc[i], b_et[i]], writes=[b_mqT])
                    sq = [sbt(st, "sq%d" % i, [128, 512]) for i in range(2)]
                    qn = [sbt(st, "qn%d" % i, [128, 512]) for i in range(2)]
                    qb = [sbt(st, "qb%d" % i, [128, 512], BF16) for i in range(2)]
                    rs = [sbt(st, "rs%d" % i, [128, 16]) for i in range(2)]
                    KTb = [sbt(st, "KTb%d" % i, [128, 4, 128], BF16) for i in range(2)]
                    VAb = [sbt(st, "VAb%d" % i, [128, 4, 130], BF16) for i in range(2)]
                    b_t = [S.buf("tmj%d" % i) for i in range(2)]
                    b_KTb = [S.buf("KTb%d" % i) for i in range(2)]
                    b_VAb = [S.buf("VAb%d" % i) for i in range(2)]
                    for i in range(2):
                        if flagged:
                            S.op("dve", lambda e: e.tensor_copy(out=VAb[i][:, :, 128:130], in_=flag[:, 0:1].unsqueeze(1).to_broadcast([128, 4, 2])),
                                 reads=[cb[4]], writes=[b_VAb[i]])
                        else:
                            S.op("dve", lambda e: e.memset(VAb[i][:, :, 128:130], 1.0), writes=[b_VAb[i]])
                    for j in range(4):
                        blk = u * 4 + j
                        i = j % 2
                        tsl = slice(j * 128, (j + 1) * 128)

                        def proj(nm):
                            wt, wbf = W[nm]
                            n = PIECES[nm][1] // 8
                            wv = wt[:].rearrange("p (c n) -> p c n", c=8)
                            pb, pbb = nbank()
                            for dc in range(8):
                                S.op("pe", lambda e: e.matmul(pb[:, 0:n], lhsT=hT[:, dc, tsl], rhs=wv[:, dc, :], start=(dc == 0), stop=(dc == 7)),
                                     reads=[wbf, b_hT], writes=[pbb])
                            return pb, pbb

                        def evac_scaled(out_ap, in_ap, rd, wr):
                            if flagged:
                                S.op("act", lambda e: e.activation(out=out_ap, in_=in_ap, func=AF.Copy, scale=flag[:, 0:1]), reads=rd + [cb[4]], writes=wr)
                            else:
                                S.op("act", lambda e: e.copy(out=out_ap, in_=in_ap), reads=rd, writes=wr)

                        pb, pbb = proj("mv")
                        evac_scaled(mVA[:, j, :, 0:128], pb[:].rearrange("p (h d) -> p h d", h=4), [pbb], [b_mVA])
                        pb, pbb = proj("gif")
                        S.op("dve", lambda e: e.tensor_tensor(out=gif[:, j, :], in0=pb[:, 0:8], in1=gifb[:], op=ALU.add), reads=[pbb, cb[9]], writes=[b_gif])
                        pb, pbb = proj("av")
                        evac_scaled(VAb[i][:, :, 0:128], pb[:].rearrange("p (h d) -> p h d", h=4), [pbb], [b_VAb[i]])
                        S.dma("pool", va_d.rearrange("h p (b c) -> p h b c", c=130)[:, :, blk, :], VAb[i][:], reads=[b_VAb[i]], writes=[b_vad])
                        if full:
                            pb, pbb = proj("mo")
                            S.op("act", lambda e: e.activation(out=sigmo[:, j, :], in_=pb[:], func=AF.Exp, scale=-1.0), reads=[pbb], writes=[b_sig])
                            S.op("dve", lambda e: e.tensor_scalar(out=sigmo[:, j, :], in0=sigmo[:, j, :], scalar1=1.0, scalar2=None, op0=ALU.add), reads=[b_sig], writes=[b_sig])
                            S.op("dve", lambda e: e.reciprocal(out=sigmo[:, j, :], in_=sigmo[:, j, :]), reads=[b_sig], writes=[b_sig])
                        for nm in (["aq", "ak"] if full else ["ak"]):
                            pb, pbb = proj(nm)
                            gt = gq if nm == "aq" else gk
                            S.op("act", lambda e: e.activation(out=sq[i][:], in_=pb[:], func=AF.Square, scale=0.125), reads=[pbb], writes=[b_t[i]])
                            S.op("dve", lambda e: e.tensor_reduce(out=rs[i][:, 0:8], in_=sq[i][:].rearrange("p (g d) -> p g d", d=64), axis=AX.X, op=ALU.add),
                                 reads=[b_t[i]], writes=[b_t[i]])
                            S.op("act", lambda e: e.activation(out=rs[i][:, 8:16], in_=rs[i][:, 0:8], func=AF.Ln, bias=epst[:, 0:1], scale=1.0), reads=[b_t[i], CONST], writes=[b_t[i]])
                            S.op("act", lambda e: e.activation(out=rs[i][:, 8:16], in_=rs[i][:, 8:16], func=AF.Exp, scale=-0.5), reads=[b_t[i]], writes=[b_t[i]])
                            S.op("dve", lambda e: e.tensor_tensor(out=qn[i][:].rearrange("p (g d) -> p g d", d=64), in0=pb[:].rearrange("p (g d) -> p g d", d=64),
                                                                 in1=rs[i][:, 8:16].unsqueeze(2).to_broadcast([128, 8, 64]), op=ALU.mult), reads=[pbb, b_t[i]], writes=[b_t[i]])
                            S.op("dve", lambda e: e.tensor_tensor(out=qb[i][:], in0=qn[i][:], in1=gt[:], op=ALU.mult), reads=[b_t[i], CONST] + cb, writes=[b_t[i]])
                            tb, tbb = nbank()
                            tv = bfview(tb)[:, 0:512].rearrange("p (h t) -> p h t", h=4)
                            for h in range(4):
                                S.op("pe", lambda e: e.transpose(out=tv[:, h, :], in_=qb[i][:, h * 128:(h + 1) * 128], identity=identb[:]), reads=[b_t[i], cb[0]], writes=[tbb])
                            if nm == "aq":
                                S.op("dve", lambda e: e.tensor_copy(out=Qbd[0:64, :, j, 0:128], in_=tv[0:64, :, :]), reads=[tbb], writes=[b_Qbd])
                                S.op("dve", lambda e: e.tensor_copy(out=Qbd[64:128, :, j, 128:256], in_=tv[64:128, :, :]), reads=[tbb], writes=[b_Qbd])
                            else:
                                S.op("act", lambda e: e.copy(out=KTb[i][:], in_=tv), reads=[tbb], writes=[b_KTb[i]])
                                S.dma("pool", kt_d.rearrange("h p k -> p h k")[:, :, blk * 128:(blk + 1) * 128], KTb[i][:], reads=[b_KTb[i]], writes=[b_ktd])
                    S.barrier()

                with ExitStack() as st:
                    for j in range(4):
                        i = j % 2
                        tsl = slice(j * 128, (j + 1) * 128)
                        lf = sbt(st, "lf%d" % j, [128, 4])
                        e1 = sbt(st, "e1%d" % j, [128, 4])
                        e2 = sbt(st, "e2%d" % j, [128, 4])
                        wsx = sbt(st, "wsx%d" % j, [128, 4])
                        RL = sbt(st, "RL%d" % j, [128, 4, 128])
                        Et = sbt(st, "Et%d" % j, [128, 512])
                        Kw = sbt(st, "Kw%d" % j, [128, 4, 128], BF16)
                        bm = S.buf("ml%d" % j)
                        b_RL = S.buf("RL")
                        b_Et = S.buf("Et")
                        b_Kw = S.buf("Kw")
                        S.op("act", lambda e: e.activation(out=lf[:], in_=gif[:, j, 4:8], func=AF.Exp, scale=-1.0), reads=[b_gif], writes=[bm])
                        S.op("act", lambda e: e.activation(out=lf[:], in_=lf[:], func=AF.Ln, bias=onet[:, 0:1], scale=1.0), reads=[bm, CONST], writes=[bm])
                        S.op("dve", lambda e: e.tensor_scalar(out=lf[:], in0=lf[:], scalar1=-1.0, scalar2=None, op0=ALU.mult), reads=[bm], writes=[bm])
                        p1, p1b = nbank()
                        S.op("pe", lambda e: e.matmul(p1[:, 0:4], lhsT=umask[:], rhs=lf[:], start=True, stop=True), reads=[bm, cb[2]], writes=[p1b])
                        S.op("dve", lambda e: e.tensor_tensor(out=RL[:], in0=umask[:].unsqueeze(1).to_broadcast([128, 4, 128]),
                                                             in1=lf[:].unsqueeze(2).to_broadcast([128, 4, 128]), op=ALU.mult), reads=[bm, cb[2]], writes=[b_RL])
                        RLf = RL[:].rearrange("p h t -> p (h t)")
                        pBc, pBcb = nbank()
                        S.op("pe", lambda e: e.matmul(pBc[:], lhsT=ones_f[:], rhs=RLf, start=True, stop=True), reads=[b_RL, CONST], writes=[pBcb])
                        S.op("dve", lambda e: e.tensor_tensor(out=e1[:], in0=gif[:, j, 0:4], in1=p1[:, 0:4], op=ALU.subtract), reads=[b_gif, p1b], writes=[bm])
                        blast = pBc[:].rearrange("p (h t) -> p h t", h=4)[:, :, 127]
                        S.op("dve", lambda e: e.tensor_tensor(out=e2[:], in0=e1[:], in1=blast, op=ALU.add), reads=[bm, pBcb], writes=[bm])
                        S.op("act", lambda e: e.activation(out=wsx[:], in_=e2[:], func=AF.Exp), reads=[bm], writes=[bm])
                        S.op("act", lambda e: e.activation(out=Et[:], in_=pBc[:], func=AF.Exp), reads=[pBcb], writes=[b_Et])
                        tb, tbb = nbank()
                        tv = bfview(tb)[:, 0:512].rearrange("p (h t) -> p h t", h=4)
                        for h in range(4):
                            S.op("pe", lambda e: e.transpose(out=tv[:, h, :], in_=mqT[:, 4 + h, tsl], identity=identb[:]), reads=[b_mqT, cb[0]], writes=[tbb])
                        S.op("dve", lambda e: e.tensor_tensor(out=Kw[:], in0=tv, in1=wsx[:].unsqueeze(2).to_broadcast([128, 4, 128]), op=ALU.mult),
                             reads=[tbb, bm], writes=[b_Kw])
                        if full:
                            DT = sbt(st, "DT%d" % j, [128, 4, 128])
                            PT = sbt(st, "PT%d" % j, [128, 4, 128], BF16)
                            qpT = sbt(st, "qpT%d" % j, [128, 4, 128], BF16)
                            rden = sbt(st, "rden%d" % j, [128, 4])
                            hmr = sbt(st, "hmr%d" % j, [128, 4, 128])
                            hsq = sbt(st, "hsq%d" % j, [128, 4, 128])
                            hss = sbt(st, "hss%d" % j, [128, 8])
                            gs = sbt(st, "gs%d" % j, [128, 512])
                            hmf = sbt(st, "hmf%d" % j, [128, 4, 128], BF16)
                            b_o = S.buf("mo%d" % j)
                            b_DT = S.buf("DT")
                            b_PT = S.buf("PT")
                            b_qp = S.buf("qpT")
                            pBm, pBmb = nbank()
                            S.op("pe", lambda e: e.matmul(pBm[:], lhsT=ones_f[:], rhs=RLf, start=True, stop=False), reads=[b_RL, CONST], writes=[pBmb])
                            S.op("pe", lambda e: e.matmul(pBm[:], lhsT=identf[:], rhs=negm4[:], start=False, stop=True), reads=[cb[1], cb[3]], writes=[pBmb])
                            for h in range(4):
                                S.op("act", lambda e: e.activation(out=DT[:, h, :], in_=pBm[:, h * 128:(h + 1) * 128], func=AF.Exp, bias=e1[:, h:h + 1], scale=1.0),
                                     reads=[pBmb, bm], writes=[b_DT])
                            S.op("dve", lambda e: e.tensor_tensor(out=qpT[:], in0=mqT[:, 0:4, tsl], in1=Et[:].rearrange("p (h t) -> p h t", h=4), op=ALU.mult),
                                 reads=[b_mqT, b_Et], writes=[b_qp])
                            pA, pAb = nbank()
                            for h in range(4):
                                S.op("pe", lambda e: e.matmul(pA[:, h * 128:(h + 1) * 128], lhsT=mqT[:, 4 + h, tsl], rhs=mqT[:, h, tsl], start=True, stop=True),
                                     reads=[b_mqT], writes=[pAb])
                            S.op("dve", lambda e: e.tensor_tensor(out=PT[:], in0=pA[:].rearrange("p (h t) -> p h t", h=4), in1=DT[:], op=ALU.mult),
                                 reads=[pAb, b_DT], writes=[b_PT])
                            pN = [nbank(), nbank()]
                            for h in range(4):
                                pn, pnb = pN[h // 2]
                                o = (h % 2) * 130
                                S.op("pe", lambda e: e.matmul(pn[:, o:o + 130], lhsT=PT[:, h, :], rhs=mVA[:, j, h, :], start=True, stop=False), reads=[b_PT, b_mVA], writes=[pnb])
                                S.op("pe", lambda e: e.matmul(pn[:, o:o + 130], lhsT=qpT[:, h, :], rhs=Sstb[:, h, :], start=False, stop=True), reads=[b_qp, b_Sb], writes=[pnb])
                            for hp in range(2):
                                pn, pnb = pN[hp]
                                pv = pn[:, 0:260].rearrange("p (h c) -> p h c", h=2)
                                S.op("act", lambda e: e.activation(out=rden[:, 2 * hp:2 * hp + 2], in_=pv[:, :, 128], func=AF.Abs), reads=[pnb], writes=[b_o])
                            S.op("dve", lambda e: e.tensor_scalar(out=rden[:], in0=rden[:], scalar1=1.0, scalar2=None, op0=ALU.max), reads=[b_o], writes=[b_o])
                            S.op("dve", lambda e: e.reciprocal(out=rden[:], in_=rden[:]), reads=[b_o], writes=[b_o])
                            for hp in range(2):
                                pn, pnb = pN[hp]
                                pv = pn[:, 0:260].rearrange("p (h c) -> p h c", h=2)
                                S.op("dve", lambda e: e.tensor_tensor(out=hmr[:, 2 * hp:2 * hp + 2, :], in0=pv[:, :, 0:128],
                                                                     in1=rden[:, 2 * hp:2 * hp + 2].unsqueeze(2).to_broadcast([128, 2, 128]), op=ALU.mult),
                                     reads=[pnb, b_o], writes=[b_o])
                            S.op("act", lambda e: e.activation(out=hsq[:], in_=hmr[:], func=AF.Square, scale=1.0 / math.sqrt(128.0)), reads=[b_o], writes=[b_o])
                            S.op("dve", lambda e: e.tensor_reduce(out=hss[:, 0:4], in_=hsq[:], axis=AX.X, op=ALU.add), reads=[b_o], writes=[b_o])
                            S.op("act", lambda e: e.activation(out=hss[:, 4:8], in_=hss[:, 0:4], func=AF.Ln, bias=epst[:, 0:1], scale=1.0), reads=[b_o, CONST], writes=[b_o])
                            S.op("act", lambda e: e.activation(out=hss[:, 4:8], in_=hss[:, 4:8], func=AF.Exp, scale=-0.5), reads=[b_o], writes=[b_o])
                            S.op("dve", lambda e: e.tensor_tensor(out=gs[:], in0=mng[:], in1=sigmo[:, j, :], op=ALU.mult), reads=[b_sig] + cb, writes=[b_o])
                            for h in range(4):
                                S.op("dve", lambda e: e.scalar_tensor_tensor(out=hmf[:, h, :], in0=hmr[:, h, :], scalar=hss[:, 4 + h:5 + h], in1=gs[:, h * 128:(h + 1) * 128],
                                                                            op0=ALU.mult, op1=ALU.mult), reads=[b_o], writes=[b_o])
                            tb2, tb2b = nbank()
                            tv2 = bfview(tb2)[:, 0:512].rearrange("p (h t) -> p h t", h=4)
                            for h in range(4):
                                S.op("pe", lambda e: e.transpose(out=tv2[:, h, :], in_=hmf[:, h, :], identity=identb[:]), reads=[b_o, cb[0]], writes=[tb2b])
                            S.op("act", lambda e: e.copy(out=hmT[:, :, tsl], in_=tv2), reads=[tb2b], writes=[b_hmT])
                        pC = [nbank(), nbank()]
                        for h in range(4):
                            pc_, pcb = pC[h // 2]
                            o = (h % 2) * 130
                            S.op("pe", lambda e: e.matmul(pc_[:, o:o + 130], lhsT=Kw[:, h, :], rhs=mVA[:, j, h, :], start=True, stop=True), reads=[b_Kw, b_mVA], writes=[pcb])
                        Ev = Et[:].rearrange("p (h t) -> p h t", h=4)
                        for h in range(4):
                            pc_, pcb = pC[h // 2]
                            o = (h % 2) * 130
                            S.op("dve", lambda e: e.scalar_tensor_tensor(out=Sst[:, h, :], in0=Sst[:, h, :], scalar=Ev[:, h, 127:128], in1=pc_[:, o:o + 130],
                                                                        op0=ALU.mult, op1=ALU.add), reads=[b_S, b_Et, pcb], writes=[b_S])
                        S.op("act", lambda e: e.copy(out=Sstb[:], in_=Sst[:]), reads=[b_S], writes=[b_Sb])
                    S.barrier()

                if full:
                    with ExitStack() as st:
                        NKC = 8
                        kch = [sbt(st, "kch%d" % i, [128, NKC * 128], BF16) for i in range(3)]
                        vch = [sbt(st, "vch%d" % i, [128, NKC, 130], BF16) for i in range(3)]
                        b_kch = [S.buf("kch%d" % i) for i in range(3)]
                        b_vch = [S.buf("vch%d" % i) for i in range(3)]
                        pex = [sbt(st, "pex%d" % i, [128, 256], BF16) for i in range(4)]
                        b_pex = [S.buf("pex%d" % i) for i in range(4)]
                        rr = sbt(st, "rr", [128, 8])
                        har = sbt(st, "har", [128, 4, 128])
                        hsq = sbt(st, "ahsq", [128, 4, 128])
                        hss = sbt(st, "ahss", [128, 8])
                        haf = sbt(st, "haf", [128, 4, 128], BF16)
                        b_a = S.buf("attn_o")
                        nkb_tot = u * 4 + 4
                        chunk_list = [(s0, min(NKC, nkb_tot - s0)) for s0 in range(0, nkb_tot, NKC)]
                        ring = 0
                        for h in range(4):
                            accs = [(banks[jq], bbufs[jq]) for jq in range(4)]
                            for jq in range(4):
                                S.op("dve", lambda e: e.memset(accs[jq][0][:, 0:260], 0.0), writes=[accs[jq][1]])
                            items = []
                            ring0 = ring
                            for ci, (s0, nk) in enumerate(chunk_list):
                                ri = (ring0 + ci) % 3
                                for kk in range(nk):
                                    kb = s0 + kk
                                    for jq in range(4):
                                        if kb <= u * 4 + jq:
                                            items.append((ri, kk, kb, jq, ci))
                            ring += len(chunk_list)
                            loaded = set()

                            def ensure_chunk(ci):
                                if ci >= len(chunk_list) or ci in loaded:
                                    return
                                loaded.add(ci)
                                s0, nk = chunk_list[ci]
                                ri = (ring0 + ci) % 3
                                S.dma("sp", kch[ri][:, 0:nk * 128], kt_d[h, :, s0 * 128:(s0 + nk) * 128], reads=[b_ktd], writes=[b_kch[ri]])
                                S.dma("sp", vch[ri][:, 0:nk, :], va_d[h, :, s0 * 130:(s0 + nk) * 130].rearrange("p (b c) -> p b c", c=130), reads=[b_vad], writes=[b_vch[ri]])

                            LAG = 2

                            def emit_scores(it, idx):
                                ri, kk, kb, jq, ci = it
                                ensure_chunk(ci)
                                ensure_chunk(ci + 1)
                                delta = u * 4 + jq - kb
                                sl = idx % 4
                                sp_, spb = banks[4 + sl // 2], bbufs[4 + sl // 2]
                                sv = sp_[:, (sl % 2) * 256:(sl % 2 + 1) * 256]
                                near = delta <= 1
                                S.op("pe", lambda e: e.matmul(sv, lhsT=kch[ri][:, kk * 128:(kk + 1) * 128], rhs=Qbd[:, h, jq, :], start=True, stop=not near),
                                     reads=[b_kch[ri], b_Qbd], writes=[spb])
                                if near:
                                    S.op("pe", lambda e: e.matmul(sv, lhsT=identb[:], rhs=biasb[:, h, delta, :], start=False, stop=True), reads=[CONST, cb[0]], writes=[spb])
                                    S.op("act", lambda e: e.activation(out=pex[sl][:], in_=sv, func=AF.Exp), reads=[spb], writes=[b_pex[sl]])
                                else:
                                    S.op("act", lambda e: e.activation(out=pex[sl][:], in_=sv, func=AF.Exp, bias=farb[:, h:h + 1], scale=1.0),
                                         reads=[spb, cb[14]], writes=[b_pex[sl]])

                            def emit_pv(it, idx):
                                ri, kk, kb, jq, ci = it
                                sl = idx % 4
                                ab, abb = accs[jq]
                                for m in range(2):
                                    S.op("pe", lambda e: e.matmul(ab[:, m * 130:(m + 1) * 130], lhsT=pex[sl][:, m * 128:(m + 1) * 128], rhs=vch[ri][:, kk, :],
                                                                 start=False, stop=False, skip_group_check=True), reads=[b_pex[sl], b_vch[ri]], writes=[abb])

                            for idx in range(len(items) + LAG):
                                if idx < len(items):
                                    emit_scores(items[idx], idx)
                                if idx >= LAG:
                                    emit_pv(items[idx - LAG], idx - LAG)
                            for jq in range(4):
                                ab, abb = accs[jq]
                                av = ab[:, 0:260].rearrange("p (m c) -> p m c", m=2)
                                S.op("dve", lambda e: e.tensor_scalar(out=rr[:, 2 * jq:2 * jq + 2], in0=av[:, :, 128], scalar1=1e-30, scalar2=None, op0=ALU.max), reads=[abb], writes=[b_a])
                                S.op("dve", lambda e: e.reciprocal(out=rr[:, 2 * jq:2 * jq + 2], in_=rr[:, 2 * jq:2 * jq + 2]), reads=[b_a], writes=[b_a])
                                S.op("dve", lambda e: e.tensor_tensor(out=rr[:, 2 * jq + 1:2 * jq + 2], in0=rr[:, 2 * jq + 1:2 * jq + 2], in1=neglam[:], op=ALU.mult),
                                     reads=[b_a, CONST], writes=[b_a])
                                S.op("dve", lambda e: e.tensor_scalar(out=har[:, jq, :], in0=av[:, 0, 0:128], scalar1=rr[:, 2 * jq:2 * jq + 1], scalar2=None, op0=ALU.mult),
                                     reads=[abb, b_a], writes=[b_a])
                                S.op("dve", lambda e: e.scalar_tensor_tensor(out=har[:, jq, :], in0=av[:, 1, 0:128], scalar=rr[:, 2 * jq + 1:2 * jq + 2], in1=har[:, jq, :],
                                                                            op0=ALU.mult, op1=ALU.add), reads=[abb, b_a], writes=[b_a])
                            S.op("act", lambda e: e.activation(out=hsq[:], in_=har[:], func=AF.Square, scale=1.0 / math.sqrt(128.0)), reads=[b_a], writes=[b_a])
                            S.op("dve", lambda e: e.tensor_reduce(out=hss[:, 0:4], in_=hsq[:], axis=AX.X, op=ALU.add), reads=[b_a], writes=[b_a])
                            S.op("act", lambda e: e.activation(out=hss[:, 4:8], in_=hss[:, 0:4], func=AF.Ln, bias=epst[:, 0:1], scale=1.0), reads=[b_a, CONST], writes=[b_a])
                            S.op("act", lambda e: e.activation(out=hss[:, 4:8], in_=hss[:, 4:8], func=AF.Exp, scale=-0.5), reads=[b_a], writes=[b_a])
                            for jq in range(4):
                                S.op("dve", lambda e: e.scalar_tensor_tensor(out=haf[:, jq, :], in0=har[:, jq, :], scalar=hss[:, 4 + jq:5 + jq], in1=ang[:, h * 128:(h + 1) * 128],
                                                                            op0=ALU.mult, op1=ALU.mult), reads=[b_a, CONST] + cb, writes=[b_a])
                            tb, tbb = banks[6 + h % 2], bbufs[6 + h % 2]
                            tv = bfview(tb)[:, 0:512].rearrange("p (j t) -> p j t", j=4)
                            for jq in range(4):
                                S.op("pe", lambda e: e.transpose(out=tv[:, jq, :], in_=haf[:, jq, :], identity=identb[:]), reads=[b_a, cb[0]], writes=[tbb])
                            S.op("act", lambda e: e.copy(out=haT[:, h, :], in_=tv.rearrange("p j t -> p (j t)")), reads=[tbb], writes=[b_haT])
                        S.barrier()

                    h2T = hT
                    b_h2T = b_hT
                    with ExitStack() as st:
                        W = {}
                        for nm in ("bm", "ba", "gm0", "gm1", "ga0", "ga1", "out0", "out1"):
                            W[nm] = wload(st, "w_" + nm, nm)
                        yT = sbt(st, "yT", [128, 8, 512], BF16)
                        b_yT = S.buf("yT")
                        sg = [[sbt(st, "sg%d_%d" % (i, a), [128, 512]) for a in range(2)] for i in range(2)]
                        ty = [[sbt(st, "ty%d_%d" % (i, a), [128, 512]) for a in range(2)] for i in range(2)]
                        b_sg = [S.buf("sg%d" % i) for i in range(2)]
                        for c in range(8):
                            i = c % 2
                            res = []
                            for a, (bw, gw, srcT, b_src) in enumerate((("bm", "gm", hmT, b_hmT), ("ba", "ga", haT, b_haT))):
                                wt, wbf = W[bw]
                                wv = wt[:].rearrange("p (k n) -> p k n", k=4)
                                py, pyb = nbank()
                                for k in range(4):
                                    S.op("pe", lambda e: e.matmul(py[:], lhsT=wv[:, k, c * 128:(c + 1) * 128], rhs=srcT[:, k, :], start=(k == 0), stop=(k == 3)),
                                         reads=[wbf, b_src], writes=[pyb])
                                gt_, gbf = W[gw + str(c // 4)]
                                gv = gt_[:].rearrange("p (c n) -> p c n", c=8)
                                pg, pgb = nbank()
                                for dc in range(8):
                                    S.op("pe", lambda e: e.matmul(pg[:], lhsT=gv[:, dc, (c % 4) * 128:(c % 4 + 1) * 128], rhs=hT[:, dc, :], start=(dc == 0), stop=(dc == 7)),
                                         reads=[gbf, b_hT], writes=[pgb])
                                S.op("act", lambda e: e.activation(out=sg[i][a][:], in_=pg[:], func=AF.Exp, scale=-1.0), reads=[pgb], writes=[b_sg[i]])
                                S.op("dve", lambda e: e.tensor_scalar(out=sg[i][a][:], in0=sg[i][a][:], scalar1=1.0, scalar2=None, op0=ALU.add), reads=[b_sg[i]], writes=[b_sg[i]])
                                S.op("dve", lambda e: e.reciprocal(out=sg[i][a][:], in_=sg[i][a][:]), reads=[b_sg[i]], writes=[b_sg[i]])
                                S.op("dve", lambda e: e.tensor_tensor(out=ty[i][a][:], in0=py[:], in1=sg[i][a][:], op=ALU.mult), reads=[pyb, b_sg[i]], writes=[b_sg[i]])
                            S.op("dve", lambda e: e.tensor_tensor(out=yT[:, c, :], in0=ty[i][0][:], in1=ty[i][1][:], op=ALU.add), reads=[b_sg[i]], writes=[b_yT])
                        tx = [sbt(st, "tx%d" % i, [128, 512]) for i in range(2)]
                        b_tx = [S.buf("tx%d" % i) for i in range(2)]
                        for j in range(4):
                            tsl = slice(j * 128, (j + 1) * 128)
                            for n in range(2):
                                wt, wbf = W["out%d" % n]
                                wv = wt[:].rearrange("p (c n) -> p c n", c=8)
                                po, pob = nbank()
                                for c in range(8):
                                    S.op("pe", lambda e: e.matmul(po[:], lhsT=yT[:, c, tsl], rhs=wv[:, c, :], start=(c == 0), stop=(c == 7)), reads=[wbf, b_yT], writes=[pob])
                                i = (j * 2 + n) % 2
                                S.op("dve", lambda e: e.tensor_tensor(out=tx[i][:], in0=po[:], in1=gate1[:, n * 512:(n + 1) * 512], op=ALU.mult), reads=[pob, CONST], writes=[b_tx[i]])
                                S.op("dve", lambda e: e.tensor_tensor(out=xs[:, j, n * 512:(n + 1) * 512], in0=xs[:, j, n * 512:(n + 1) * 512], in1=tx[i][:], op=ALU.add),
                                     reads=[b_tx[i], b_xs[j]], writes=[b_xs[j]])
                        S.barrier()
                    with ExitStack() as st:
                        for j in range(4):
                            norm_block(st, j, mult2, shift2, h2T, b_h2T, "n2")
                        S.barrier()

                    with ExitStack() as st:
                        wd, wdb = wload(st, "w_down", "down")
                        wdv = wd[:].rearrange("p (f n) -> p f n", f=NF)
                        actT = sbt(st, "actT", [128, NF, 512], BF16)
                        b_actT = S.buf("actT")
                        wu = [None, None, None]
                        uu = [sbt(st, "fu%d" % i, [128, 512]) for i in range(8)]
                        b_uu = [S.buf("fu%d" % i) for i in range(8)]
                        fe = [sbt(st, "fe%d" % i, [128, 512]) for i in range(2)]
                        b_fe = [S.buf("fe%d" % i) for i in range(2)]
                        wus = [sbt(st, "wup%d" % i, [128, 4096], BF16) for i in range(3)]
                        b_wus = [S.buf("wup%d" % i) for i in range(3)]

                        def ldup(jj):
                            off, n = PIECES["up%d" % jj]
                            S.dma("sp", wus[jj % 3][:], wb_d[:, off:off + n], reads=[b_wb], writes=[b_wus[jj % 3]])

                        ldup(0)
                        ldup(1)
                        for jj in range(11):
                            if jj + 2 < 11:
                                ldup(jj + 2)
                            wv = wus[jj % 3][:].rearrange("p (c n) -> p c n", c=8)
                            wbf = b_wus[jj % 3]
                            for k in range(4):
                                q = jj * 4 + k
                                pb, pbb = nbank()
                                for dc in range(8):
                                    S.op("pe", lambda e: e.matmul(pb[:], lhsT=wv[:, dc, k * 128:(k + 1) * 128], rhs=h2T[:, dc, :], start=(dc == 0), stop=(dc == 7)),
                                         reads=[wbf, b_h2T], writes=[pbb])
                                kk_ = (jj % 2) * 4 + k
                                wsel = fcwf if flagged else fcw
                                S.op("act", lambda e: e.activation(out=uu[kk_][:], in_=pb[:], func=AF.Identity, scale=wsel[:, q, 2:3], bias=fcb[:, q:q + 1]),
                                     reads=[pbb, CONST] + cb, writes=[b_uu[kk_]])
                                for tp in range(2):
                                    sh = 2 - tp
                                    S.op("dve", lambda e: e.scalar_tensor_tensor(out=uu[kk_][:, sh:512], in0=pb[:, 0:512 - sh], scalar=wsel[:, q, tp:tp + 1], in1=uu[kk_][:, sh:512],
                                                                                op0=ALU.mult, op1=ALU.add), reads=[pbb, b_uu[kk_], CONST], writes=[b_uu[kk_]])
                                for tp in range(2):
                                    sh = 2 - tp
                                    S.op("dve", lambda e: e.scalar_tensor_tensor(out=uu[kk_][:, 0:sh], in0=fhalo[:, q, 2 - sh:2], scalar=fcw[:, q, tp:tp + 1], in1=uu[kk_][:, 0:sh],
                                                                                op0=ALU.mult, op1=ALU.add), reads=[b_fhalo, b_uu[kk_]] + cb, writes=[b_uu[kk_]])
                                if flagged:
                                    S.op("act", lambda e: e.activation(out=fhalo[:, q, :], in_=pb[:, 510:512], func=AF.Copy, scale=flag[:, 0:1]), reads=[pbb, cb[4]], writes=[b_fhalo])
                                else:
                                    S.op("act", lambda e: e.copy(out=fhalo[:, q, :], in_=pb[:, 510:512]), reads=[pbb], writes=[b_fhalo])
                            for i in range(2):
                                f = 2 * jj + i
                                uv = uu[(jj % 2) * 4 + i]
                                ug = uu[(jj % 2) * 4 + 2 + i]
                                b_uv = b_uu[(jj % 2) * 4 + i]
                                b_ug = b_uu[(jj % 2) * 4 + 2 + i]
                                S.op("act", lambda e: e.activation(out=fe[i][:], in_=ug[:], func=AF.Exp, scale=-1.0), reads=[b_ug], writes=[b_fe[i]])
                                S.op("dve", lambda e: e.tensor_scalar(out=fe[i][:], in0=fe[i][:], scalar1=1.0, scalar2=None, op0=ALU.add), reads=[b_fe[i]], writes=[b_fe[i]])
                                S.op("dve", lambda e: e.reciprocal(out=fe[i][:], in_=fe[i][:]), reads=[b_fe[i]], writes=[b_fe[i]])
                                S.op("dve", lambda e: e.tensor_tensor(out=fe[i][:], in0=ug[:], in1=fe[i][:], op=ALU.mult), reads=[b_ug, b_fe[i]], writes=[b_fe[i]])
                                S.op("dve", lambda e: e.tensor_tensor(out=actT[:, f, :], in0=uv[:], in1=fe[i][:], op=ALU.mult), reads=[b_uv, b_fe[i]], writes=[b_actT])
                        tx = fe
                        b_tx = b_fe
                        for j in range(4):
                            tsl = slice(j * 128, (j + 1) * 128)
                            blk = u * 4 + j
                            for n in range(2):
                                po, pob = nbank()
                                for f in range(NF):
                                    S.op("pe", lambda e: e.matmul(po[:], lhsT=actT[:, f, tsl], rhs=wdv[:, f, n * 512:(n + 1) * 512], start=(f == 0), stop=(f == NF - 1)),
                                         reads=[wdb, b_actT], writes=[pob])
                                i = (j * 2 + n) % 2
                                S.op("dve", lambda e: e.tensor_tensor(out=tx[i][:], in0=po[:], in1=gate2[:, n * 512:(n + 1) * 512], op=ALU.mult), reads=[pob, CONST], writes=[b_tx[i]])
                                S.op("dve", lambda e: e.tensor_tensor(out=xs[:, j, n * 512:(n + 1) * 512], in0=xs[:, j, n * 512:(n + 1) * 512], in1=tx[i][:], op=ALU.add),
                                     reads=[b_tx[i], b_xs[j]], writes=[b_xs[j]])
                            if own:
                                ob = blk - NFLAG * 4
                                S.dma("pool", out_d[ob * 128:(ob + 1) * 128, :], xs[:, j, :], reads=[b_xs[j]], writes=[b_out])
                        S.barrier()
        S.barrier()
    return nc


def _t5_bucket(n):
    n = np.maximum(n, 0)
    max_exact = 16
    nf = np.maximum(n, 1).astype(np.float32)
    large = max_exact + (np.log(nf / np.float32(max_exact)) / np.float32(math.log(128 / max_exact)) * np.float32(32 - max_exact)).astype(np.int32)
    large = np.minimum(large, 31)
    return np.where(n < max_exact, n, large)


def _fm(v, nch):
    return np.ascontiguousarray(np.asarray(v, np.float32).reshape(nch, 128).T)


def _piece_fm(w):
    n = w.shape[1]
    return w.reshape(8, 128, n).transpose(1, 0, 2).reshape(128, 8 * n)


def prepare_inputs(NCTX, NFULL, inputs):
    f32 = np.float32
    g = {k: np.asarray(v) for k, v in inputs.items()}
    NU = NCTX + NFULL
    half_tok = (NU // 2) * 512
    x = g["x"].astype(f32, copy=False)
    B = x.shape[0]
    assert x.shape[1] == 2 * half_tok
    w_in = g["w_in"][0]
    cols = {}
    o = 0
    for nm, n in (("mqk", 1024), ("mv", 512), ("mo", 512), ("mi", 4), ("mf", 4), ("aq", 512), ("ak", 512), ("av", 512), ("gm", 1024), ("ga", 1024)):
        cols[nm] = w_in[:, o:o + n]
        o += n

    def perm_qk(w):
        return w.reshape(1024, 2, 4, 64).transpose(0, 2, 1, 3).reshape(1024, 512)

    wall = np.zeros((128, WPAD), f32)

    def put(nm, arr):
        off, n = PIECES[nm]
        assert arr.shape == (128, n), (nm, arr.shape, n)
        wall[:, off:off + n] = arr

    put("mq", _piece_fm(cols["mqk"][:, 0:512]))
    put("mk", _piece_fm(cols["mqk"][:, 512:1024]))
    put("mv", _piece_fm(cols["mv"]))
    put("mo", _piece_fm(cols["mo"]))
    put("aq", _piece_fm(perm_qk(cols["aq"])))
    put("ak", _piece_fm(perm_qk(cols["ak"])))
    put("av", _piece_fm(cols["av"]))
    put("gm0", _piece_fm(cols["gm"][:, 0:512]))
    put("gm1", _piece_fm(cols["gm"][:, 512:1024]))
    put("ga0", _piece_fm(cols["ga"][:, 0:512]))
    put("ga1", _piece_fm(cols["ga"][:, 512:1024]))
    put("gif", _piece_fm(np.concatenate([cols["mi"], cols["mf"]], axis=1)))
    put("bm", g["w_branch_m"][0].reshape(4, 128, 1024).transpose(1, 0, 2).reshape(128, 4096))
    put("ba", g["w_branch_a"][0].reshape(4, 128, 1024).transpose(1, 0, 2).reshape(128, 4096))
    put("out0", _piece_fm(g["w_out"][0][:, 0:512]))
    put("out1", _piece_fm(g["w_out"][0][:, 512:1024]))
    w_up = g["w_up"][0]
    fcw_full = g["ffn_conv_w"][0]
    fcb_full = g["ffn_conv_b"][0]
    chunk_cols = []
    for jj in range(11):
        cc = [np.arange((2 * jj + i) * 128, (2 * jj + i + 1) * 128) for i in range(2)]
        cc += [DFF + np.arange((2 * jj + i) * 128, (2 * jj + i + 1) * 128) for i in range(2)]
        idx = np.concatenate(cc)
        chunk_cols.append(idx)
        put("up%d" % jj, _piece_fm(w_up[:, idx]))
    allidx = np.concatenate(chunk_cols)
    fcw = fcw_full[:, allidx].reshape(3, 44, 128).transpose(2, 1, 0).reshape(128, 44 * 3)
    fcb = fcb_full[allidx].reshape(44, 128).T
    put("down", g["w_down"][0].reshape(NF, 128, 1024).transpose(1, 0, 2).reshape(128, NF * 1024))

    w_ada = g["w_ada"][0]
    wada = np.stack([_piece_fm(w_ada[:, p * 512:(p + 1) * 512]) for p in range(12)]).astype(f32)
    bada = g["b_ada"][0].astype(f32)
    mcw = g["m_conv_w"][0].reshape(4, 8, 128).transpose(2, 1, 0).reshape(128, 32)
    mcb = _fm(g["m_conv_b"][0], 8)
    gifb = np.concatenate([g["m_igate_b"][0], g["m_fgate_b"][0]]).astype(f32)
    gq = np.tile(g["a_qnorm_g"][0], 8).astype(f32)
    gk = np.tile(g["a_knorm_g"][0], 8).astype(f32)
    rel = g["rel_bias"].astype(f32)
    kk = np.arange(128)[:, None]
    qq = np.arange(128)[None, :]
    biasg = np.zeros((128, 4, 2, 2, 128), f32)
    maskn = np.zeros((128, 4, 2, 2, 128), f32)
    for dl in range(2):
        dist = qq - kk + 128 * dl
        bidx = _t5_bucket(dist)
        for h in range(4):
            t = rel[bidx, h]
            biasg[:, h, dl, 0, :] = t
            biasg[:, h, dl, 1, :] = t
        if dl == 0:
            mk = np.where(dist < 0, NEG, 0.0).astype(f32)
            maskn[:, :, 0, :, :] = mk[:, None, None, :]
    farb = rel[31, :].astype(f32)
    umask = (kk <= qq).astype(f32)
    negm4 = np.tile(np.where(kk <= qq, 0.0, NEG).astype(f32), (1, 4))
    import ml_dtypes
    common = dict(
        wada=wada, bada=bada, badafm=_fm(bada, 48), g1fm=_fm(g["norm1_g"][0], 8), g2fm=_fm(g["norm2_g"][0], 8),
        wall=wall, mcw=np.ascontiguousarray(mcw, f32), mcb=mcb, fcw=np.ascontiguousarray(fcw, f32), fcb=np.ascontiguousarray(fcb, f32),
        gifb=gifb, mng=g["m_norm_g"][0].astype(f32), ang=g["a_norm_g"][0].astype(f32), gq=gq, gk=gk,
        alam=g["a_lambda"][0].reshape(256).astype(f32), biasg=biasg.reshape(128, -1), maskn=maskn.reshape(128, -1), farb=farb,
        identb=np.eye(128).astype(ml_dtypes.bfloat16), identf=np.eye(128, dtype=f32), umask=umask, negm4=negm4,
    )
    in_maps = []
    for b in range(B):
        cfm = _fm(g["c"][b], 8)
        for hf in range(2):
            if hf == 0:
                xl = np.concatenate([np.zeros((half_tok, D), f32), x[b, 0:half_tok]], axis=0)
                fl = np.zeros((128, 1), f32)
            else:
                xl = x[b]
                fl = np.ones((128, 1), f32)
            m = dict(common)
            m["x"] = np.ascontiguousarray(xl)
            m["cfm"] = cfm
            m["flag"] = fl
            in_maps.append(m)
    return in_maps


_NC_CACHE = {}


def run(NCTX, NFULL, inputs):
    key = (NCTX, NFULL)
    if key not in _NC_CACHE:
        _NC_CACHE[key] = build_program(NCTX, NFULL)
    nc = _NC_CACHE[key]
    in_maps = prepare_inputs(NCTX, NFULL, inputs)
    res = run_bass_kernel_spmd(nc, in_maps, core_ids=list(range(len(in_maps))))
    B = len(in_maps) // 2
    half_tok = ((NCTX + NFULL) // 2) * 512
    out = np.empty((B, 2 * half_tok, D), np.float32)
    for b in range(B):
        for hf in range(2):
            out[b, hf * half_tok:(hf + 1) * half_tok] = res.results[b * 2 + hf]["out"]
    return out


def kernel(**inputs):
    return run(7, 9, inputs)
```

```python
import math
from contextlib import ExitStack

import numpy as np

import concourse.bass as bass
import concourse.mybir as mybir
from concourse.bass_utils import run_bass_kernel_spmd

F32 = mybir.dt.float32
BF16 = mybir.dt.bfloat16
AF = mybir.ActivationFunctionType
ALU = mybir.AluOpType
AX = mybir.AxisListType

D = 1024
DC = 8
DFF = 2816
NF = 22
EPS = 1e-6
LAM_INIT = 0.8 - 0.6 * math.exp(-0.3 * 0)
NEG = -30000.0

PIECES = {}
_off = 0
for _nm, _n in (("mq", 4096), ("mk", 4096), ("mv", 4096), ("mo", 4096), ("aq", 4096), ("ak", 4096),
                ("av", 4096), ("gm0", 4096), ("gm1", 4096), ("ga0", 4096), ("ga1", 4096), ("gif", 64),
                ("bm", 4096), ("ba", 4096), ("out0", 4096), ("out1", 4096)):
    PIECES[_nm] = (_off, _n)
    _off += _n
for _j in range(11):
    PIECES["up%d" % _j] = (_off, 4096)
    _off += 4096
PIECES["down"] = (_off, NF * 1024)
_off += NF * 1024
WTOT = _off
WPAD = ((WTOT + 4095) // 4096) * 4096


class Buf:
    __slots__ = ("name", "w", "rs", "sem", "semv", "grp")

    def __init__(self, name, grp=None):
        self.name = name
        self.w = None
        self.rs = []
        self.sem = None
        self.semv = 0
        self.grp = grp


class Sched:
    def __init__(self, nc, stack):
        self.nc = nc
        self.stack = stack
        self.engs = {}
        for nm, h in (("pe", nc.tensor), ("act", nc.scalar), ("dve", nc.vector), ("pool", nc.gpsimd), ("sp", nc.sync)):
            sem = stack.enter_context(nc.semaphore("s_" + nm))
            self.engs[nm] = dict(h=h, sem=sem, cnt=0, seen={})
        self.groups = {}
        self.dma_ev = {}
        self.free_sems = {}
        self.live = []
        self.nsem = 0

    def buf(self, name, grp=None):
        return Buf(name, grp)

    def _need(self, e, deps):
        E = self.engs[e]
        best = {}
        for d in deps:
            if d is None:
                continue
            sem, val, en = d
            if en == e and e == "pe":
                continue
            k = id(sem)
            if k not in best or best[k][1] < val:
                best[k] = (sem, val)
        for k, (sem, val) in best.items():
            if E["seen"].get(k, 0) >= val:
                continue
            E["h"].wait_ge(sem, val)
            E["seen"][k] = val

    def op(self, e, fn, reads=(), writes=()):
        E = self.engs[e]
        deps = []
        for b in reads:
            deps.append(b.w)
        for b in writes:
            if b.w is not None and b.w[2] != e:
                deps.append(b.w)
            for r in b.rs:
                if r[2] != e:
                    deps.append(r)
        self._need(e, deps)
        ins = fn(E["h"])
        E["cnt"] += 1
        ins.then_inc(E["sem"], 1)
        ev = (E["sem"], E["cnt"], e)
        for b in reads:
            b.rs.append(ev)
        for b in writes:
            b.w = ev
            b.rs = []
        return ins

    def dma(self, q, out, in_, reads=(), writes=()):
        E = self.engs[q]
        deps = []
        for b in reads:
            deps.append(b.w)
        for b in writes:
            if b.grp in ("wbw", "kvw", "outw"):
                continue
            deps.append(b.w)
            deps.extend(b.rs)
        self._need(q, deps)
        ins = E["h"].dma_start(out=out, in_=in_)
        cands = list(writes) + list(reads)
        tgt = ([b for b in cands if b.grp is None] + cands)[0]
        if tgt.grp is not None:
            gk = tgt.grp
            if gk not in self.groups:
                self.groups[gk] = [self.stack.enter_context(self.nc.semaphore("g_" + gk)), 0]
            g = self.groups[gk]
            g[1] += 16
            sem, val = g[0], g[1]
        else:
            if tgt.sem is None:
                tgt.sem = {}
                self.live.append(tgt)
            if q not in tgt.sem:
                fs = self.free_sems.setdefault(q, [])
                if not fs:
                    self.nsem += 1
                    fs.append([self.stack.enter_context(self.nc.semaphore("dp%d" % self.nsem)), 0])
                tgt.sem[q] = fs.pop()
            ent = tgt.sem[q]
            ent[1] += 16
            sem, val = ent[0], ent[1]
        ins.then_inc(sem, 16)
        self.dma_ev[id(sem)] = (sem, val)
        ev = (sem, val, "dma")
        for b in reads:
            b.rs.append(ev)
        for b in writes:
            b.w = ev
            b.rs = []
        return ins

    def barrier(self, engines=("pe", "act", "dve", "pool", "sp")):
        deps = [(E["sem"], E["cnt"], "x") for E in self.engs.values() if E["cnt"] > 0]
        deps += [(s, v, "dma") for (s, v) in self.dma_ev.values()]
        for e in engines:
            self._need(e, deps)
        self.dma_ev = {}
        for b in self.live:
            for qq, ent in b.sem.items():
                self.free_sems[qq].append(ent)
            b.sem = None
        self.live = []

    def seal(self, bufs, grp):
        g = self.groups[grp]
        for b in bufs:
            b.w = (g[0], g[1], "dma")


def build_program(NCTX, NFULL, debug=False):
    NU = NCTX + NFULL
    NFLAG = NU // 2
    NBLK = NU * 4
    NTOK = NU * 512
    NOWN = (NFULL - 1) * 512
    assert NU == 2 * (NFULL - 1)

    nc = bass.Bass("TRN2", target_bir_lowering=False)

    def din(name, shape, dt=F32):
        return nc.dram_tensor(name, list(shape), dt, kind="ExternalInput").ap()

    x_d = din("x", [NTOK, D])
    cfm_d = din("cfm", [128, 8])
    wada_d = din("wada", [12, 128, 8 * 512])
    bada_d = din("bada", [6144])
    badafm_d = din("badafm", [128, 48])
    g1fm_d = din("g1fm", [128, 8])
    g2fm_d = din("g2fm", [128, 8])
    wall_d = din("wall", [128, WPAD])
    mcw_d = din("mcw", [128, 8 * 4])
    mcb_d = din("mcb", [128, 8])
    fcw_d = din("fcw", [128, 44 * 3])
    fcb_d = din("fcb", [128, 44])
    gifb_d = din("gifb", [8])
    mng_d = din("mng", [512])
    ang_d = din("ang", [512])
    gq_d = din("gq", [512])
    gk_d = din("gk", [512])
    alam_d = din("alam", [256])
    biasg_d = din("biasg", [128, 4 * 2 * 2 * 128])
    maskn_d = din("maskn", [128, 4 * 2 * 2 * 128])
    farb_d = din("farb", [4])
    flag_d = din("flag", [128, 1])
    identb_d = din("identb", [128, 128], BF16)
    identf_d = din("identf", [128, 128])
    umask_d = din("umask", [128, 128])
    negm4_d = din("negm4", [128, 512])
    out_d = nc.dram_tensor("out", [NOWN, D], F32, kind="ExternalOutput").ap()
    wb_d = nc.dram_tensor("wb_scr", [128, WPAD], BF16, kind="Internal").ap()
    kt_d = nc.dram_tensor("kt_scr", [4, 128, NBLK * 128], BF16, kind="Internal").ap()
    va_d = nc.dram_tensor("va_scr", [4, 128, NBLK * 130], BF16, kind="Internal").ap()

    with ExitStack() as top:
        S = Sched(nc, top)

        uid = [0]

        def sbt(st, name, shape, dt=F32):
            uid[0] += 1
            return st.enter_context(nc.sbuf_tensor("s%d_%s" % (uid[0], name), list(shape), dt))

        banks = [top.enter_context(nc.psum_tensor("bank%d" % i, [128, 512], F32)) for i in range(8)]
        bbufs = [S.buf("bank%d" % i) for i in range(8)]
        bank_rr = [0]

        def nbank(lo=0, hi=8):
            i = lo + (bank_rr[0] % (hi - lo))
            bank_rr[0] += 1
            return banks[i], bbufs[i]

        def bfview(bank):
            return bank[:].bitcast(BF16)

        identb = sbt(top, "identb", [128, 128], BF16)
        identf = sbt(top, "identf", [128, 128])
        umask = sbt(top, "umask", [128, 128])
        negm4 = sbt(top, "negm4", [128, 512])
        ones_f = sbt(top, "ones_f", [128, 128])
        flag = sbt(top, "flag", [128, 1])
        epst = sbt(top, "epst", [128, 1])
        onet = sbt(top, "onet", [128, 1])
        mult1 = sbt(top, "mult1", [128, 8])
        shift1 = sbt(top, "shift1", [128, 8])
        mult2 = sbt(top, "mult2", [128, 8])
        shift2 = sbt(top, "shift2", [128, 8])
        gate1 = sbt(top, "gate1", [128, D])
        gate2 = sbt(top, "gate2", [128, D])
        mcw = sbt(top, "mcw", [128, 8, 4])
        mcb = sbt(top, "mcb", [128, 8])
        mcwf = sbt(top, "mcwf", [128, 8, 4])
        fcwf = sbt(top, "fcwf", [128, 44, 3])
        mcnb = sbt(top, "mcnb", [128, 8])
        fcw = sbt(top, "fcw", [128, 44, 3])
        fcb = sbt(top, "fcb", [128, 44])
        fcnb = sbt(top, "fcnb", [128, 44])
        gifb = sbt(top, "gifb", [128, 8])
        mng = sbt(top, "mng", [128, 512])
        ang = sbt(top, "ang", [128, 512])
        gq = sbt(top, "gq", [128, 512])
        gk = sbt(top, "gk", [128, 512])
        biasb = sbt(top, "biasb", [128, 4, 2, 256], BF16)
        farb = sbt(top, "farb", [128, 4])
        neglam = sbt(top, "neglam", [128, 1])
        Sst = sbt(top, "Sst", [128, 4, 130])
        Sstb = sbt(top, "Sstb", [128, 4, 130], BF16)
        mhalo = sbt(top, "mhalo", [128, 8, 3])
        fhalo = sbt(top, "fhalo", [128, 44, 2])
        CONST = S.buf("const")
        b_S = S.buf("Sst")
        b_Sb = S.buf("Sstb")
        b_mhalo = S.buf("mhalo")
        b_fhalo = S.buf("fhalo")
        b_wb = S.buf("wb_scr", grp="wbw")
        b_ktd = S.buf("kt_scr", grp="kvw")
        b_vad = S.buf("va_scr", grp="kvw")
        b_out = S.buf("outd", grp="outw")

        def ld_const(t, src, grp="cst"):
            b = S.buf("c_" + t.name, grp=grp)
            S.dma("sp", t[:], src, writes=[b])
            return b

        cb = []
        cb.append(ld_const(identb, identb_d[:, :]))
        cb.append(ld_const(identf, identf_d[:, :]))
        cb.append(ld_const(umask, umask_d[:, :]))
        cb.append(ld_const(negm4, negm4_d[:, :]))
        cb.append(ld_const(flag, flag_d[:, :]))
        cb.append(ld_const(mcw, mcw_d.rearrange("p (c k) -> p c k", k=4)))
        cb.append(ld_const(mcb, mcb_d[:, :]))
        cb.append(ld_const(fcw, fcw_d.rearrange("p (c k) -> p c k", k=3)))
        cb.append(ld_const(fcb, fcb_d[:, :]))
        cb.append(ld_const(gifb, gifb_d.partition_broadcast(128)))
        cb.append(ld_const(mng, mng_d.partition_broadcast(128)))
        cb.append(ld_const(ang, ang_d.partition_broadcast(128)))
        cb.append(ld_const(gq, gq_d.partition_broadcast(128)))
        cb.append(ld_const(gk, gk_d.partition_broadcast(128)))
        cb.append(ld_const(farb, farb_d.partition_broadcast(128)))
        S.seal(cb, "cst")
        S.op("dve", lambda e: e.memset(ones_f[:], 1.0), writes=[CONST])
        S.op("dve", lambda e: e.memset(epst[:], EPS), writes=[CONST])
        S.op("dve", lambda e: e.memset(onet[:], 1.0), writes=[CONST])
        S.op("dve", lambda e: e.memset(Sst[:], 0.0), writes=[b_S])
        S.op("dve", lambda e: e.memset(Sstb[:], 0.0), writes=[b_Sb])
        S.op("dve", lambda e: e.memset(mhalo[:], 0.0), writes=[b_mhalo])
        S.op("dve", lambda e: e.memset(fhalo[:], 0.0), writes=[b_fhalo])
        S.op("dve", lambda e: e.tensor_scalar(out=mcnb[:], in0=mcb[:], scalar1=-1.0, scalar2=None, op0=ALU.mult), reads=cb, writes=[CONST])
        S.op("dve", lambda e: e.tensor_scalar(out=fcnb[:], in0=fcb[:], scalar1=-1.0, scalar2=None, op0=ALU.mult), reads=cb, writes=[CONST])
        S.op("dve", lambda e: e.tensor_scalar(out=gq[:], in0=gq[:], scalar1=0.125, scalar2=None, op0=ALU.mult), reads=cb, writes=[CONST])
        S.op("dve", lambda e: e.tensor_scalar(out=mcwf[:], in0=mcw[:], scalar1=flag[:, 0:1], scalar2=None, op0=ALU.mult), reads=cb, writes=[CONST])
        S.op("dve", lambda e: e.tensor_scalar(out=fcwf[:], in0=fcw[:], scalar1=flag[:, 0:1], scalar2=None, op0=ALU.mult), reads=cb, writes=[CONST])
        S.op("dve", lambda e: e.tensor_scalar(out=ang[:], in0=ang[:], scalar1=1.0 - LAM_INIT, scalar2=None, op0=ALU.mult), reads=cb, writes=[CONST])

        with ExitStack() as st:
            cfm = sbt(st, "cfm", [128, 8])
            sc = sbt(st, "sc", [128, 8])
            sct = sbt(st, "sct", [128, 8])
            scbc = sbt(st, "scbc", [128, 8, 128])
            badafm = sbt(st, "badafm", [128, 48])
            g1fm = sbt(st, "g1fm", [128, 8])
            g2fm = sbt(st, "g2fm", [128, 8])
            modfm = sbt(st, "modfm", [128, 48])
            alam = sbt(st, "alam", [128, 256])
            lt = sbt(st, "lt", [128, 128])
            ls = sbt(st, "ls", [128, 2])
            biasg = sbt(st, "biasg", [128, 2048])
            maskn = sbt(st, "maskn", [128, 2048])
            wst = [sbt(st, "wst%d" % i, [128, 4096]) for i in range(2)]
            wbo = [sbt(st, "wbo%d" % i, [128, 4096], BF16) for i in range(2)]
            bbt = [sbt(st, "bbt%d" % i, [128, 512]) for i in range(2)]
            b_wst = [S.buf("wst%d" % i) for i in range(2)]
            b_wbo = [S.buf("wbo%d" % i) for i in range(2)]
            b_bbt = [S.buf("bbt%d" % i) for i in range(2)]
            L = S.buf("prel")
            b_l = [ld_const(cfm, cfm_d[:, :], "cst2"), ld_const(badafm, badafm_d[:, :], "cst2"), ld_const(g1fm, g1fm_d[:, :], "cst2"),
                   ld_const(g2fm, g2fm_d[:, :], "cst2"), ld_const(alam, alam_d.partition_broadcast(128), "cst2"),
                   ld_const(biasg, biasg_d[:, :], "cst2"), ld_const(maskn, maskn_d[:, :], "cst2")]
            S.seal(b_l, "cst2")
            S.op("act", lambda e: e.activation(out=sct[:], in_=cfm[:], func=AF.Exp, scale=-1.0), reads=b_l, writes=[L])
            S.op("dve", lambda e: e.tensor_scalar(out=sct[:], in0=sct[:], scalar1=1.0, scalar2=None, op0=ALU.add), reads=[L], writes=[L])
            S.op("dve", lambda e: e.reciprocal(out=sct[:], in_=sct[:]), reads=[L], writes=[L])
            S.op("dve", lambda e: e.tensor_tensor(out=sc[:], in0=cfm[:], in1=sct[:], op=ALU.mult), reads=[L], writes=[L])
            S.op("dve", lambda e: e.tensor_copy(out=scbc[:], in_=sc[:].unsqueeze(2).to_broadcast([128, 8, 128])), reads=[L], writes=[L])
            S.op("dve", lambda e: e.tensor_tensor(out=lt[:, 0:64], in0=alam[:, 0:64], in1=alam[:, 64:128], op=ALU.mult), reads=b_l, writes=[L])
            S.op("dve", lambda e: e.tensor_tensor(out=lt[:, 64:128], in0=alam[:, 128:192], in1=alam[:, 192:256], op=ALU.mult), reads=[L], writes=[L])
            S.op("dve", lambda e: e.tensor_reduce(out=ls[:], in_=lt[:].rearrange("p (a b) -> p a b", a=2), axis=AX.X, op=ALU.add), reads=[L], writes=[L])
            S.op("act", lambda e: e.activation(out=ls[:], in_=ls[:], func=AF.Exp), reads=[L], writes=[L])
            S.op("dve", lambda e: e.tensor_tensor(out=neglam[:], in0=ls[:, 1:2], in1=ls[:, 0:1], op=ALU.subtract), reads=[L], writes=[CONST])
            S.op("dve", lambda e: e.tensor_scalar(out=neglam[:], in0=neglam[:], scalar1=-LAM_INIT, scalar2=None, op0=ALU.add), reads=[CONST], writes=[CONST])
            S.op("dve", lambda e: e.tensor_tensor(out=biasb[:].rearrange("p a b c -> p (a b c)"), in0=biasg[:], in1=maskn[:], op=ALU.add), reads=b_l, writes=[CONST])

            fm_ps, fm_b = banks[7], bbufs[7]
            fm_cols = {0: 0, 1: 4, 2: 8, 3: 12, 6: 24, 7: 28, 8: 32, 9: 36}
            for pc in range(12):
                i = pc % 2
                S.dma("sp", wst[i][:], wada_d[pc, :, :], writes=[b_wst[i]])
                wv = wst[i][:].rearrange("p (c n) -> p c n", c=8)
                if pc in (4, 5, 10, 11):
                    S.dma("sp", bbt[i][:], bada_d[pc * 512:(pc + 1) * 512].partition_broadcast(128), writes=[b_bbt[i]])
                    pb, pbb = nbank(0, 6)
                    for dc in range(8):
                        S.op("pe", lambda e: e.matmul(pb[:], lhsT=scbc[:, dc, :], rhs=wv[:, dc, :], start=(dc == 0), stop=(dc == 7)),
                             reads=[L, b_wst[i]], writes=[pbb])
                    gt = gate1 if pc < 6 else gate2
                    off = (pc % 2) * 512
                    S.op("dve", lambda e: e.tensor_tensor(out=gt[:, off:off + 512], in0=pb[:], in1=bbt[i][:], op=ALU.add),
                         reads=[pbb, b_bbt[i]], writes=[CONST])
                else:
                    for k in range(4):
                        col = fm_cols[pc] + k
                        for dc in range(8):
                            S.op("pe", lambda e: e.matmul(fm_ps[:, col:col + 1], lhsT=wv[:, dc, k * 128:(k + 1) * 128], rhs=sc[:, dc:dc + 1],
                                                         start=(dc == 0), stop=(dc == 7)), reads=[L, b_wst[i]], writes=[fm_b])
            S.op("dve", lambda e: e.tensor_tensor(out=modfm[:, 0:16], in0=fm_ps[:, 0:16], in1=badafm[:, 0:16], op=ALU.add), reads=[fm_b] + b_l, writes=[L])
            S.op("dve", lambda e: e.tensor_tensor(out=modfm[:, 24:40], in0=fm_ps[:, 24:40], in1=badafm[:, 24:40], op=ALU.add), reads=[fm_b] + b_l, writes=[L])
            S.op("dve", lambda e: e.scalar_tensor_tensor(out=mult1[:], in0=modfm[:, 8:16], scalar=1.0, in1=g1fm[:], op0=ALU.add, op1=ALU.mult), reads=[L], writes=[CONST])
            S.op("dve", lambda e: e.tensor_copy(out=shift1[:], in_=modfm[:, 0:8]), reads=[L], writes=[CONST])
            S.op("dve", lambda e: e.scalar_tensor_tensor(out=mult2[:], in0=modfm[:, 32:40], scalar=1.0, in1=g2fm[:], op0=ALU.add, op1=ALU.mult), reads=[L], writes=[CONST])
            S.op("dve", lambda e: e.tensor_copy(out=shift2[:], in_=modfm[:, 24:32]), reads=[L], writes=[CONST])

            cast_eng = ["dve", "act"]
            for ch in range(WPAD // 4096):
                i = ch % 2
                S.dma("sp", wst[i][:], wall_d[:, ch * 4096:(ch + 1) * 4096], writes=[b_wst[i]])
                ce = cast_eng[ch % 2]
                if ce == "act":
                    S.op("act", lambda e: e.copy(out=wbo[i][:], in_=wst[i][:]), reads=[b_wst[i]], writes=[b_wbo[i]])
                else:
                    S.op(ce, lambda e: e.tensor_copy(out=wbo[i][:], in_=wst[i][:]), reads=[b_wst[i]], writes=[b_wbo[i]])
                S.dma("pool", wb_d[:, ch * 4096:(ch + 1) * 4096], wbo[i][:], reads=[b_wbo[i]], writes=[b_wb])
            S.barrier()

        def wload(st, name, piece, shape3=None):
            off, n = PIECES[piece]
            t = sbt(st, name, [128, n], BF16)
            b = S.buf(name)
            S.dma("sp", t[:], wb_d[:, off:off + n], reads=[b_wb], writes=[b])
            return t, b

        for u in range(NU):
            full = u >= NCTX
            flagged = u < NFLAG
            own = u >= NFLAG
            with ExitStack() as su:
                xs = sbt(su, "xs", [128, 4, D])
                hT = sbt(su, "hT", [128, 8, 512], BF16)
                mqT = sbt(su, "mqT", [128, 8, 512], BF16)
                mVA = sbt(su, "mVA", [128, 4, 4, 130], BF16)
                sigmo = sbt(su, "sigmo", [128, 4, 512])
                gif = sbt(su, "gif", [128, 4, 8])
                Qbd = sbt(su, "Qbd", [128, 4, 4, 256], BF16)
                hmT = sbt(su, "hmT", [128, 4, 512], BF16)
                haT = sbt(su, "haT", [128, 4, 512], BF16)
                b_xs = [S.buf("xs%d" % j) for j in range(4)]
                b_hT = S.buf("hT")
                b_mqT = S.buf("mqT")
                b_mVA = S.buf("mVA")
                b_sig = S.buf("sigmo")
                b_gif = S.buf("gif")
                b_Qbd = S.buf("Qbd")
                b_hmT = S.buf("hmT")
                b_haT = S.buf("haT")
                if full:
                    S.op("dve", lambda e: e.memset(Qbd[:], 0.0), writes=[b_Qbd])

                def norm_block(st, j, mult, shift, hdst, b_hdst, tagp):
                    junk = sbt(st, tagp + "junk%d" % j, [128, D], BF16)
                    xn = sbt(st, tagp + "xn%d" % j, [128, D], BF16)
                    ss = sbt(st, tagp + "ss%d" % j, [128, 2])
                    bl = S.buf("nb")
                    S.op("act", lambda e: e.activation(out=junk[:], in_=xs[:, j, :], func=AF.Square, scale=1.0 / 32.0, accum_out=ss[:, 0:1]),
                         reads=[b_xs[j]], writes=[bl])
                    S.op("act", lambda e: e.activation(out=ss[:, 1:2], in_=ss[:, 0:1], func=AF.Ln, bias=epst[:, 0:1], scale=1.0), reads=[bl, CONST], writes=[bl])
                    S.op("act", lambda e: e.activation(out=ss[:, 1:2], in_=ss[:, 1:2], func=AF.Exp, scale=-0.5), reads=[bl], writes=[bl])
                    S.op("dve", lambda e: e.tensor_scalar(out=xn[:], in0=xs[:, j, :], scalar1=ss[:, 1:2], scalar2=None, op0=ALU.mult),
                         reads=[bl, b_xs[j]], writes=[bl])
                    pb, pbb = nbank()
                    pv = bfview(pb).rearrange("p (c t) -> p c t", c=8)
                    for c in range(8):
                        S.op("pe", lambda e: e.transpose(out=pv[:, c, :], in_=xn[:, c * 128:(c + 1) * 128], identity=identb[:]),
                             reads=[bl, cb[0]], writes=[pbb])
                    for c in range(8):
                        S.op("act", lambda e: e.activation(out=hdst[:, c, j * 128:(j + 1) * 128], in_=pv[:, c, :], func=AF.Identity,
                                                          scale=mult[:, c:c + 1], bias=shift[:, c:c + 1]), reads=[pbb, CONST], writes=[b_hdst])

                with ExitStack() as st:
                    names = ["mk", "mv", "gif", "ak", "av"] + (["mq", "mo", "aq"] if full else [])
                    W = {}
                    for j in range(4):
                        blk = u * 4 + j
                        S.dma("sp", xs[:, j, :], x_d[blk * 128:(blk + 1) * 128, :], writes=[b_xs[j]])
                    for nm in names:
                        W[nm] = wload(st, "w_" + nm, nm)
                    for j in range(4):
                        norm_block(st, j, mult1, shift1, hT, b_hT, "n1")
                    if flagged:
                        S.op("dve", lambda e: e.tensor_copy(out=mVA[:, :, :, 128:130], in_=flag[:, 0:1].unsqueeze(1).unsqueeze(1).to_broadcast([128, 4, 4, 2])),
                             reads=[cb[4]], writes=[b_mVA])
                    else:
                        S.op("dve", lambda e: e.memset(mVA[:, :, :, 128:130], 1.0), writes=[b_mVA])
                    acc = [sbt(st, "acc%d" % i, [128, 512]) for i in range(2)]
                    et = [sbt(st, "et%d" % i, [128, 512]) for i in range(2)]
                    b_acc = [S.buf("acc%d" % i) for i in range(2)]
                    b_et = [S.buf("et%d" % i) for i in range(2)]
                    chunks = list(range(8)) if full else list(range(4, 8))
                    for ci, c in enumerate(chunks):
                        i = ci % 2
                        wt, wbf = W["mq" if c < 4 else "mk"]
                        wv = wt[:].rearrange("p (c n) -> p c n", c=8)
                        k = c % 4
                        pb, pbb = nbank()
                        for dc in range(8):
                            S.op("pe", lambda e: e.matmul(pb[:], lhsT=wv[:, dc, k * 128:(k + 1) * 128], rhs=hT[:, dc, :], start=(dc == 0), stop=(dc == 7)),
                                 reads=[wbf, b_hT], writes=[pbb])
                        wsel = mcwf if flagged else mcw
                        S.op("act", lambda e: e.activation(out=acc[i][:], in_=pb[:], func=AF.Identity, scale=wsel[:, c, 3:4], bias=mcb[:, c:c + 1]),
                             reads=[pbb, CONST] + cb, writes=[b_acc[i]])
                        for tp in range(3):
                            sh = 3 - tp
                            S.op("dve", lambda e: e.scalar_tensor_tensor(out=acc[i][:, sh:512], in0=pb[:, 0:512 - sh], scalar=wsel[:, c, tp:tp + 1], in1=acc[i][:, sh:512],
                                                                        op0=ALU.mult, op1=ALU.add), reads=[pbb, b_acc[i], CONST], writes=[b_acc[i]])
                        for tp in range(3):
                            sh = 3 - tp
                            S.op("dve", lambda e: e.scalar_tensor_tensor(out=acc[i][:, 0:sh], in0=mhalo[:, c, 3 - sh:3], scalar=mcw[:, c, tp:tp + 1], in1=acc[i][:, 0:sh],
                                                                        op0=ALU.mult, op1=ALU.add), reads=[b_mhalo, b_acc[i]] + cb, writes=[b_acc[i]])
                        if flagged:
                            S.op("act", lambda e: e.activation(out=mhalo[:, c, :], in_=pb[:, 509:512], func=AF.Copy, scale=flag[:, 0:1]), reads=[pbb, cb[4]], writes=[b_mhalo])
                        else:
                            S.op("act", lambda e: e.copy(out=mhalo[:, c, :], in_=pb[:, 509:512]), reads=[pbb], writes=[b_mhalo])
                        S.op("act", lambda e: e.activation(out=et[i][:], in_=acc[i][:], func=AF.Exp, scale=-1.0), reads=[b_acc[i]], writes=[b_et[i]])
                        sk = (128.0 ** 0.5) if c < 4 else 1.0
                        S.op("dve", lambda e: e.tensor_scalar(out=et[i][:], in0=et[i][:], scalar1=1.0, scalar2=sk, op0=ALU.add, op1=ALU.mult),
                             reads=[b_et[i]], writes=[b_et[i]])
                        S.op("dve", lambda e: e.reciprocal(out=et[i][:], in_=et[i][:]), reads=[b_et[i]], writes=[b_et[i]])
                        S.op("dve", lambda e: e.tensor_tensor(out=mqT[:, c, :], in0=acc[i][:], in1=et[i][:], op=ALU.mult),
                             reads=[b_acc[i], b_et[i]], writes=[b_mqT])
                    sq = [sbt(st, "sq%d" % i, [128, 512]) for i in range(2)]
                    qn = [sbt(st, "qn%d" % i, [128, 512]) for i in range(2)]
                    qb = [sbt(st, "qb%d" % i, [128, 512], BF16) for i in range(2)]
                    rs = [sbt(st, "rs%d" % i, [128, 16]) for i in range(2)]
                    KTb = [sbt(st, "KTb%d" % i, [128, 4, 128], BF16) for i in range(2)]
                    VAb = [sbt(st, "VAb%d" % i, [128, 4, 130], BF16) for i in range(2)]
                    b_t = [S.buf("tmj%d" % i) for i in range(2)]
                    b_KTb = [S.buf("KTb%d" % i) for i in range(2)]
                    b_VAb = [S.buf("VAb%d" % i) for i in range(2)]
                    for i in range(2):
                        if flagged:
                            S.op("dve", lambda e: e.tensor_copy(out=VAb[i][:, :, 128:130], in_=flag[:, 0:1].unsqueeze(1).to_broadcast([128, 4, 2])),
                                 reads=[cb[4]], writes=[b_VAb[i]])
                        else:
                            S.op("dve", lambda e: e.memset(VAb[i][:, :, 128:130], 1.0), writes=[b_VAb[i]])
                    for j in range(4):
                        blk = u * 4 + j
                        i = j % 2
                        tsl = slice(j * 128, (j + 1) * 128)

                        def proj(nm):
                            wt, wbf = W[nm]
                            n = PIECES[nm][1] // 8
                            wv = wt[:].rearrange("p (c n) -> p c n", c=8)
                            pb, pbb = nbank()
                            for dc in range(8):
                                S.op("pe", lambda e: e.matmul(pb[:, 0:n], lhsT=hT[:, dc, tsl], rhs=wv[:, dc, :], start=(dc == 0), stop=(dc == 7)),
                                     reads=[wbf, b_hT], writes=[pbb])
                            return pb, pbb

                        def evac_scaled(out_ap, in_ap, rd, wr):
                            if flagged:
                                S.op("act", lambda e: e.activation(out=out_ap, in_=in_ap, func=AF.Copy, scale=flag[:, 0:1]), reads=rd + [cb[4]], writes=wr)
                            else:
                                S.op("act", lambda e: e.copy(out=out_ap, in_=in_ap), reads=rd, writes=wr)

                        pb, pbb = proj("mv")
                        evac_scaled(mVA[:, j, :, 0:128], pb[:].rearrange("p (h d) -> p h d", h=4), [pbb], [b_mVA])
                        pb, pbb = proj("gif")
                        S.op("dve", lambda e: e.tensor_tensor(out=gif[:, j, :], in0=pb[:, 0:8], in1=gifb[:], op=ALU.add), reads=[pbb, cb[9]], writes=[b_gif])
                        pb, pbb = proj("av")
                        evac_scaled(VAb[i][:, :, 0:128], pb[:].rearrange("p (h d) -> p h d", h=4), [pbb], [b_VAb[i]])
                        S.dma("pool", va_d.rearrange("h p (b c) -> p h b c", c=130)[:, :, blk, :], VAb[i][:], reads=[b_VAb[i]], writes=[b_vad])
                        if full:
                            pb, pbb = proj("mo")
                            S.op("act", lambda e: e.activation(out=sigmo[:, j, :], in_=pb[:], func=AF.Exp, scale=-1.0), reads=[pbb], writes=[b_sig])
                            S.op("dve", lambda e: e.tensor_scalar(out=sigmo[:, j, :], in0=sigmo[:, j, :], scalar1=1.0, scalar2=None, op0=ALU.add), reads=[b_sig], writes=[b_sig])
                            S.op("dve", lambda e: e.reciprocal(out=sigmo[:, j, :], in_=sigmo[:, j, :]), reads=[b_sig], writes=[b_sig])
                        for nm in (["aq", "ak"] if full else ["ak"]):
                            pb, pbb = proj(nm)
                            gt = gq if nm == "aq" else gk
                            S.op("act", lambda e: e.activation(out=sq[i][:], in_=pb[:], func=AF.Square, scale=0.125), reads=[pbb], writes=[b_t[i]])
                            S.op("dve", lambda e: e.tensor_reduce(out=rs[i][:, 0:8], in_=sq[i][:].rearrange("p (g d) -> p g d", d=64), axis=AX.X, op=ALU.add),
                                 reads=[b_t[i]], writes=[b_t[i]])
                            S.op("act", lambda e: e.activation(out=rs[i][:, 8:16], in_=rs[i][:, 0:8], func=AF.Ln, bias=epst[:, 0:1], scale=1.0), reads=[b_t[i], CONST], writes=[b_t[i]])
                            S.op("act", lambda e: e.activation(out=rs[i][:, 8:16], in_=rs[i][:, 8:16], func=AF.Exp, scale=-0.5), reads=[b_t[i]], writes=[b_t[i]])
                            S.op("dve", lambda e: e.tensor_tensor(out=qn[i][:].rearrange("p (g d) -> p g d", d=64), in0=pb[:].rearrange("p (g d) -> p g d", d=64),
                                                                 in1=rs[i][:, 8:16].unsqueeze(2).to_broadcast([128, 8, 64]), op=ALU.mult), reads=[pbb, b_t[i]], writes=[b_t[i]])
                            S.op("dve", lambda e: e.tensor_tensor(out=qb[i][:], in0=qn[i][:], in1=gt[:], op=ALU.mult), reads=[b_t[i], CONST] + cb, writes=[b_t[i]])
                            tb, tbb = nbank()
                            tv = bfview(tb)[:, 0:512].rearrange("p (h t) -> p h t", h=4)
                            for h in range(4):
                                S.op("pe", lambda e: e.transpose(out=tv[:, h, :], in_=qb[i][:, h * 128:(h + 1) * 128], identity=identb[:]), reads=[b_t[i], cb[0]], writes=[tbb])
                            if nm == "aq":
                                S.op("dve", lambda e: e.tensor_copy(out=Qbd[0:64, :, j, 0:128], in_=tv[0:64, :, :]), reads=[tbb], writes=[b_Qbd])
                                S.op("dve", lambda e: e.tensor_copy(out=Qbd[64:128, :, j, 128:256], in_=tv[64:128, :, :]), reads=[tbb], writes=[b_Qbd])
                            else:
                                S.op("act", lambda e: e.copy(out=KTb[i][:], in_=tv), reads=[tbb], writes=[b_KTb[i]])
                                S.dma("pool", kt_d.rearrange("h p k -> p h k")[:, :, blk * 128:(blk + 1) * 128], KTb[i][:], reads=[b_KTb[i]], writes=[b_ktd])
                    S.barrier()

                with ExitStack() as st:
                    def mlstm_block(j):
                        i = j % 2
                        tsl = slice(j * 128, (j + 1) * 128)
                        lf = sbt(st, "lf%d" % j, [128, 4])
                        e1 = sbt(st, "e1%d" % j, [128, 4])
                        e2 = sbt(st, "e2%d" % j, [128, 4])
                        wsx = sbt(st, "wsx%d" % j, [128, 4])
                        RL = sbt(st, "RL%d" % j, [128, 4, 128])
                        Et = sbt(st, "Et%d" % j, [128, 512])
                        Kw = sbt(st, "Kw%d" % j, [128, 4, 128], BF16)
                        bm = S.buf("ml%d" % j)
                        b_RL = S.buf("RL")
                        b_Et = S.buf("Et")
                        b_Kw = S.buf("Kw")
                        S.op("act", lambda e: e.activation(out=lf[:], in_=gif[:, j, 4:8], func=AF.Exp, scale=-1.0), reads=[b_gif], writes=[bm])
                        S.op("act", lambda e: e.activation(out=lf[:], in_=lf[:], func=AF.Ln, bias=onet[:, 0:1], scale=1.0), reads=[bm, CONST], writes=[bm])
                        S.op("dve", lambda e: e.tensor_scalar(out=lf[:], in0=lf[:], scalar1=-1.0, scalar2=None, op0=ALU.mult), reads=[bm], writes=[bm])
                        p1, p1b = nbank()
                        S.op("pe", lambda e: e.matmul(p1[:, 0:4], lhsT=umask[:], rhs=lf[:], start=True, stop=True), reads=[bm, cb[2]], writes=[p1b])
                        S.op("dve", lambda e: e.tensor_tensor(out=RL[:], in0=umask[:].unsqueeze(1).to_broadcast([128, 4, 128]),
                                                             in1=lf[:].unsqueeze(2).to_broadcast([128, 4, 128]), op=ALU.mult), reads=[bm, cb[2]], writes=[b_RL])
                        RLf = RL[:].rearrange("p h t -> p (h t)")
                        pBc, pBcb = nbank()
                        S.op("pe", lambda e: e.matmul(pBc[:], lhsT=ones_f[:], rhs=RLf, start=True, stop=True), reads=[b_RL, CONST], writes=[pBcb])
                        S.op("dve", lambda e: e.tensor_tensor(out=e1[:], in0=gif[:, j, 0:4], in1=p1[:, 0:4], op=ALU.subtract), reads=[b_gif, p1b], writes=[bm])
                        blast = pBc[:].rearrange("p (h t) -> p h t", h=4)[:, :, 127]
                        S.op("dve", lambda e: e.tensor_tensor(out=e2[:], in0=e1[:], in1=blast, op=ALU.add), reads=[bm, pBcb], writes=[bm])
                        S.op("act", lambda e: e.activation(out=wsx[:], in_=e2[:], func=AF.Exp), reads=[bm], writes=[bm])
                        S.op("act", lambda e: e.activation(out=Et[:], in_=pBc[:], func=AF.Exp), reads=[pBcb], writes=[b_Et])
                        tb, tbb = nbank()
                        tv = bfview(tb)[:, 0:512].rearrange("p (h t) -> p h t", h=4)
                        for h in range(4):
                            S.op("pe", lambda e: e.transpose(out=tv[:, h, :], in_=mqT[:, 4 + h, tsl], identity=identb[:]), reads=[b_mqT, cb[0]], writes=[tbb])
                        S.op("dve", lambda e: e.tensor_tensor(out=Kw[:], in0=tv, in1=wsx[:].unsqueeze(2).to_broadcast([128, 4, 128]), op=ALU.mult),
                             reads=[tbb, bm], writes=[b_Kw])
                        yield
                        if full:
                            DT = sbt(st, "DT%d" % j, [128, 4, 128])
                            PT = sbt(st, "PT%d" % j, [128, 4, 128], BF16)
                            qpT = sbt(st, "qpT%d" % j, [128, 4, 128], BF16)
                            rden = sbt(st, "rden%d" % j, [128, 4])
                            hmr = sbt(st, "hmr%d" % j, [128, 4, 128])
                            hsq = sbt(st, "hsq%d" % j, [128, 4, 128])
                            hss = sbt(st, "hss%d" % j, [128, 8])
                            gs = sbt(st, "gs%d" % j, [128, 512])
                            hmf = sbt(st, "hmf%d" % j, [128, 4, 128], BF16)
                            b_o = S.buf("mo%d" % j)
                            b_DT = S.buf("DT")
                            b_PT = S.buf("PT")
                            b_qp = S.buf("qpT")
                            pBm, pBmb = nbank()
                            S.op("pe", lambda e: e.matmul(pBm[:], lhsT=ones_f[:], rhs=RLf, start=True, stop=False), reads=[b_RL, CONST], writes=[pBmb])
                            S.op("pe", lambda e: e.matmul(pBm[:], lhsT=identf[:], rhs=negm4[:], start=False, stop=True), reads=[cb[1], cb[3]], writes=[pBmb])
                            for h in range(4):
                                S.op("act", lambda e: e.activation(out=DT[:, h, :], in_=pBm[:, h * 128:(h + 1) * 128], func=AF.Exp, bias=e1[:, h:h + 1], scale=1.0),
                                     reads=[pBmb, bm], writes=[b_DT])
                            S.op("dve", lambda e: e.tensor_tensor(out=qpT[:], in0=mqT[:, 0:4, tsl], in1=Et[:].rearrange("p (h t) -> p h t", h=4), op=ALU.mult),
                                 reads=[b_mqT, b_Et], writes=[b_qp])
                            pA, pAb = nbank()
                            for h in range(4):
                                S.op("pe", lambda e: e.matmul(pA[:, h * 128:(h + 1) * 128], lhsT=mqT[:, 4 + h, tsl], rhs=mqT[:, h, tsl], start=True, stop=True),
                                     reads=[b_mqT], writes=[pAb])
                            S.op("dve", lambda e: e.tensor_tensor(out=PT[:], in0=pA[:].rearrange("p (h t) -> p h t", h=4), in1=DT[:], op=ALU.mult),
                                 reads=[pAb, b_DT], writes=[b_PT])
                        yield
                        if full:
                            pN = [nbank(), nbank()]
                            for h in range(4):
                                pn, pnb = pN[h // 2]
                                o = (h % 2) * 130
                                S.op("pe", lambda e: e.matmul(pn[:, o:o + 130], lhsT=PT[:, h, :], rhs=mVA[:, j, h, :], start=True, stop=False), reads=[b_PT, b_mVA], writes=[pnb])
                                S.op("pe", lambda e: e.matmul(pn[:, o:o + 130], lhsT=qpT[:, h, :], rhs=Sstb[:, h, :], start=False, stop=True), reads=[b_qp, b_Sb], writes=[pnb])
                            for hp in range(2):
                                pn, pnb = pN[hp]
                                pv = pn[:, 0:260].rearrange("p (h c) -> p h c", h=2)
                                S.op("act", lambda e: e.activation(out=rden[:, 2 * hp:2 * hp + 2], in_=pv[:, :, 128], func=AF.Abs), reads=[pnb], writes=[b_o])
                            S.op("dve", lambda e: e.tensor_scalar(out=rden[:], in0=rden[:], scalar1=1.0, scalar2=None, op0=ALU.max), reads=[b_o], writes=[b_o])
                            S.op("dve", lambda e: e.reciprocal(out=rden[:], in_=rden[:]), reads=[b_o], writes=[b_o])
                            for hp in range(2):
                                pn, pnb = pN[hp]
                                pv = pn[:, 0:260].rearrange("p (h c) -> p h c", h=2)
                                S.op("dve", lambda e: e.tensor_tensor(out=hmr[:, 2 * hp:2 * hp + 2, :], in0=pv[:, :, 0:128],
                                                                     in1=rden[:, 2 * hp:2 * hp + 2].unsqueeze(2).to_broadcast([128, 2, 128]), op=ALU.mult),
                                     reads=[pnb, b_o], writes=[b_o])
                            S.op("act", lambda e: e.activation(out=hsq[:], in_=hmr[:], func=AF.Square, scale=1.0 / math.sqrt(128.0)), reads=[b_o], writes=[b_o])
                            S.op("dve", lambda e: e.tensor_reduce(out=hss[:, 0:4], in_=hsq[:], axis=AX.X, op=ALU.add), reads=[b_o], writes=[b_o])
                            S.op("act", lambda e: e.activation(out=hss[:, 4:8], in_=hss[:, 0:4], func=AF.Ln, bias=epst[:, 0:1], scale=1.0), reads=[b_o, CONST], writes=[b_o])
                            S.op("act", lambda e: e.activation(out=hss[:, 4:8], in_=hss[:, 4:8], func=AF.Exp, scale=-0.5), reads=[b_o], writes=[b_o])
                            S.op("dve", lambda e: e.tensor_tensor(out=gs[:], in0=mng[:], in1=sigmo[:, j, :], op=ALU.mult), reads=[b_sig] + cb, writes=[b_o])
                            for h in range(4):
                                S.op("dve", lambda e: e.scalar_tensor_tensor(out=hmf[:, h, :], in0=hmr[:, h, :], scalar=hss[:, 4 + h:5 + h], in1=gs[:, h * 128:(h + 1) * 128],
                                                                            op0=ALU.mult, op1=ALU.mult), reads=[b_o], writes=[b_o])
                            tb2, tb2b = nbank()
                            tv2 = bfview(tb2)[:, 0:512].rearrange("p (h t) -> p h t", h=4)
                            for h in range(4):
                                S.op("pe", lambda e: e.transpose(out=tv2[:, h, :], in_=hmf[:, h, :], identity=identb[:]), reads=[b_o, cb[0]], writes=[tb2b])
                            S.op("act", lambda e: e.copy(out=hmT[:, :, tsl], in_=tv2), reads=[tb2b], writes=[b_hmT])
                        pC = [nbank(), nbank()]
                        for h in range(4):
                            pc_, pcb = pC[h // 2]
                            o = (h % 2) * 130
                            S.op("pe", lambda e: e.matmul(pc_[:, o:o + 130], lhsT=Kw[:, h, :], rhs=mVA[:, j, h, :], start=True, stop=True), reads=[b_Kw, b_mVA], writes=[pcb])
                        Ev = Et[:].rearrange("p (h t) -> p h t", h=4)
                        for h in range(4):
                            pc_, pcb = pC[h // 2]
                            o = (h % 2) * 130
                            S.op("dve", lambda e: e.scalar_tensor_tensor(out=Sst[:, h, :], in0=Sst[:, h, :], scalar=Ev[:, h, 127:128], in1=pc_[:, o:o + 130],
                                                                        op0=ALU.mult, op1=ALU.add), reads=[b_S, b_Et, pcb], writes=[b_S])
                        S.op("act", lambda e: e.copy(out=Sstb[:], in_=Sst[:]), reads=[b_S], writes=[b_Sb])
                    gens = [mlstm_block(j) for j in range(4)]
                    for ph in range(3):
                        for g_ in gens:
                            next(g_, None)
                    S.barrier()

                if full:
                    with ExitStack() as st:
                        NKC = 8
                        kch = [sbt(st, "kch%d" % i, [128, NKC * 128], BF16) for i in range(3)]
                        vch = [sbt(st, "vch%d" % i, [128, NKC, 130], BF16) for i in range(3)]
                        b_kch = [S.buf("kch%d" % i) for i in range(3)]
                        b_vch = [S.buf("vch%d" % i) for i in range(3)]
                        pex = [sbt(st, "pex%d" % i, [128, 512], BF16) for i in range(6)]
                        b_pex = [S.buf("pex%d" % i) for i in range(6)]
                        rr = sbt(st, "rr", [128, 8])
                        har = sbt(st, "har", [128, 4, 128])
                        hsq = sbt(st, "ahsq", [128, 4, 128])
                        hss = sbt(st, "ahss", [128, 8])
                        haf = sbt(st, "haf", [128, 4, 128], BF16)
                        b_a = S.buf("attn_o")
                        nkb_tot = u * 4 + 4
                        chunk_list = [(s0, min(NKC, nkb_tot - s0)) for s0 in range(0, nkb_tot, NKC)]
                        ring = 0
                        for h in range(4):
                            accs = [(banks[jq], bbufs[jq]) for jq in range(4)]
                            for jq in range(4):
                                S.op("dve", lambda e: e.memset(accs[jq][0][:, 0:260], 0.0), writes=[accs[jq][1]])
                            items = []
                            ring0 = ring
                            for ci, (s0, nk) in enumerate(chunk_list):
                                ri = (ring0 + ci) % 3
                                for kk in range(nk):
                                    kb = s0 + kk
                                    for p2 in range(2):
                                        j0 = 2 * p2
                                        if kb <= u * 4 + j0 - 2:
                                            items.append((ri, kk, kb, (j0, j0 + 1), ci))
                                        else:
                                            for jq in (j0, j0 + 1):
                                                if kb <= u * 4 + jq:
                                                    items.append((ri, kk, kb, (jq,), ci))
                            ring += len(chunk_list)
                            loaded = set()

                            def ensure_chunk(ci):
                                if ci >= len(chunk_list) or ci in loaded:
                                    return
                                loaded.add(ci)
                                s0, nk = chunk_list[ci]
                                ri = (ring0 + ci) % 3
                                S.dma("sp", kch[ri][:, 0:nk * 128], kt_d[h, :, s0 * 128:(s0 + nk) * 128], reads=[b_ktd], writes=[b_kch[ri]])
                                S.dma("sp", vch[ri][:, 0:nk, :], va_d[h, :, s0 * 130:(s0 + nk) * 130].rearrange("p (b c) -> p b c", c=130), reads=[b_vad], writes=[b_vch[ri]])

                            LAG = 3
                            NS = 4
                            NPX = len(pex)

                            def emit_scores(it, idx):
                                ri, kk, kb, jqs, ci = it
                                ensure_chunk(ci)
                                ensure_chunk(ci + 1)
                                nq = len(jqs)
                                sl = idx % NS
                                sp_, spb = banks[4 + sl], bbufs[4 + sl]
                                sv = sp_[:, 0:256 * nq]
                                px = idx % NPX
                                delta = u * 4 + jqs[0] - kb
                                near = (nq == 1) and delta <= 1
                                rhs = Qbd[:, h, jqs[0]:jqs[0] + nq, :].rearrange("p j c -> p (j c)")
                                S.op("pe", lambda e: e.matmul(sv, lhsT=kch[ri][:, kk * 128:(kk + 1) * 128], rhs=rhs, start=True, stop=not near),
                                     reads=[b_kch[ri], b_Qbd], writes=[spb])
                                if near:
                                    S.op("pe", lambda e: e.matmul(sv, lhsT=identb[:], rhs=biasb[:, h, delta, :], start=False, stop=True), reads=[CONST, cb[0]], writes=[spb])
                                    S.op("act", lambda e: e.activation(out=pex[px][:, 0:256 * nq], in_=sv, func=AF.Exp), reads=[spb], writes=[b_pex[px]])
                                else:
                                    S.op("act", lambda e: e.activation(out=pex[px][:, 0:256 * nq], in_=sv, func=AF.Exp, bias=farb[:, h:h + 1], scale=1.0),
                                         reads=[spb, cb[14]], writes=[b_pex[px]])

                            def emit_pv(it, idx):
                                ri, kk, kb, jqs, ci = it
                                px = idx % NPX
                                for a_, jq in enumerate(jqs):
                                    ab, abb = accs[jq]
                                    for m in range(2):
                                        c0 = a_ * 256 + m * 128
                                        S.op("pe", lambda e: e.matmul(ab[:, m * 130:(m + 1) * 130], lhsT=pex[px][:, c0:c0 + 128], rhs=vch[ri][:, kk, :],
                                                                     start=False, stop=False, skip_group_check=True), reads=[b_pex[px], b_vch[ri]], writes=[abb])

                            for idx in range(len(items) + LAG):
                                if idx < len(items):
                                    emit_scores(items[idx], idx)
                                if idx >= LAG:
                                    emit_pv(items[idx - LAG], idx - LAG)
                            for jq in range(4):
                                ab, abb = accs[jq]
                                av = ab[:, 0:260].rearrange("p (m c) -> p m c", m=2)
                                S.op("dve", lambda e: e.tensor_scalar(out=rr[:, 2 * jq:2 * jq + 2], in0=av[:, :, 128], scalar1=1e-30, scalar2=None, op0=ALU.max), reads=[abb], writes=[b_a])
                                S.op("dve", lambda e: e.reciprocal(out=rr[:, 2 * jq:2 * jq + 2], in_=rr[:, 2 * jq:2 * jq + 2]), reads=[b_a], writes=[b_a])
                                S.op("dve", lambda e: e.tensor_tensor(out=rr[:, 2 * jq + 1:2 * jq + 2], in0=rr[:, 2 * jq + 1:2 * jq + 2], in1=neglam[:], op=ALU.mult),
                                     reads=[b_a, CONST], writes=[b_a])
                                S.op("dve", lambda e: e.tensor_scalar(out=har[:, jq, :], in0=av[:, 0, 0:128], scalar1=rr[:, 2 * jq:2 * jq + 1], scalar2=None, op0=ALU.mult),
                                     reads=[abb, b_a], writes=[b_a])
                                S.op("dve", lambda e: e.scalar_tensor_tensor(out=har[:, jq, :], in0=av[:, 1, 0:128], scalar=rr[:, 2 * jq + 1:2 * jq + 2], in1=har[:, jq, :],
                                                                            op0=ALU.mult, op1=ALU.add), reads=[abb, b_a], writes=[b_a])
                            S.op("act", lambda e: e.activation(out=hsq[:], in_=har[:], func=AF.Square, scale=1.0 / math.sqrt(128.0)), reads=[b_a], writes=[b_a])
                            S.op("dve", lambda e: e.tensor_reduce(out=hss[:, 0:4], in_=hsq[:], axis=AX.X, op=ALU.add), reads=[b_a], writes=[b_a])
                            S.op("act", lambda e: e.activation(out=hss[:, 4:8], in_=hss[:, 0:4], func=AF.Ln, bias=epst[:, 0:1], scale=1.0), reads=[b_a, CONST], writes=[b_a])
                            S.op("act", lambda e: e.activation(out=hss[:, 4:8], in_=hss[:, 4:8], func=AF.Exp, scale=-0.5), reads=[b_a], writes=[b_a])
                            for jq in range(4):
                                S.op("dve", lambda e: e.scalar_tensor_tensor(out=haf[:, jq, :], in0=har[:, jq, :], scalar=hss[:, 4 + jq:5 + jq], in1=ang[:, h * 128:(h + 1) * 128],
                                                                            op0=ALU.mult, op1=ALU.mult), reads=[b_a, CONST] + cb, writes=[b_a])
                            tb, tbb = banks[6 + h % 2], bbufs[6 + h % 2]
                            tv = bfview(tb)[:, 0:512].rearrange("p (j t) -> p j t", j=4)
                            for jq in range(4):
                                S.op("pe", lambda e: e.transpose(out=tv[:, jq, :], in_=haf[:, jq, :], identity=identb[:]), reads=[b_a, cb[0]], writes=[tbb])
                            S.op("act", lambda e: e.copy(out=haT[:, h, :], in_=tv.rearrange("p j t -> p (j t)")), reads=[tbb], writes=[b_haT])
                        S.barrier()

                    h2T = hT
                    b_h2T = b_hT
                    with ExitStack() as st:
                        W = {}
                        for nm in ("bm", "ba", "gm0", "gm1", "ga0", "ga1", "out0", "out1"):
                            W[nm] = wload(st, "w_" + nm, nm)
                        yT = sbt(st, "yT", [128, 8, 512], BF16)
                        b_yT = S.buf("yT")
                        sg = [[sbt(st, "sg%d_%d" % (i, a), [128, 512]) for a in range(2)] for i in range(2)]
                        ty = [[sbt(st, "ty%d_%d" % (i, a), [128, 512]) for a in range(2)] for i in range(2)]
                        b_sg = [S.buf("sg%d" % i) for i in range(2)]
                        for c in range(8):
                            i = c % 2
                            res = []
                            for a, (bw, gw, srcT, b_src) in enumerate((("bm", "gm", hmT, b_hmT), ("ba", "ga", haT, b_haT))):
                                wt, wbf = W[bw]
                                wv = wt[:].rearrange("p (k n) -> p k n", k=4)
                                py, pyb = nbank()
                                for k in range(4):
                                    S.op("pe", lambda e: e.matmul(py[:], lhsT=wv[:, k, c * 128:(c + 1) * 128], rhs=srcT[:, k, :], start=(k == 0), stop=(k == 3)),
                                         reads=[wbf, b_src], writes=[pyb])
                                gt_, gbf = W[gw + str(c // 4)]
                                gv = gt_[:].rearrange("p (c n) -> p c n", c=8)
                                pg, pgb = nbank()
                                for dc in range(8):
                                    S.op("pe", lambda e: e.matmul(pg[:], lhsT=gv[:, dc, (c % 4) * 128:(c % 4 + 1) * 128], rhs=hT[:, dc, :], start=(dc == 0), stop=(dc == 7)),
                                         reads=[gbf, b_hT], writes=[pgb])
                                S.op("act", lambda e: e.activation(out=sg[i][a][:], in_=pg[:], func=AF.Sigmoid), reads=[pgb], writes=[b_sg[i]])
                                S.op("dve", lambda e: e.tensor_tensor(out=ty[i][a][:], in0=py[:], in1=sg[i][a][:], op=ALU.mult), reads=[pyb, b_sg[i]], writes=[b_sg[i]])
                            S.op("dve", lambda e: e.tensor_tensor(out=yT[:, c, :], in0=ty[i][0][:], in1=ty[i][1][:], op=ALU.add), reads=[b_sg[i]], writes=[b_yT])
                        tx = [sbt(st, "tx%d" % i, [128, 512]) for i in range(2)]
                        b_tx = [S.buf("tx%d" % i) for i in range(2)]
                        for j in range(4):
                            tsl = slice(j * 128, (j + 1) * 128)
                            for n in range(2):
                                wt, wbf = W["out%d" % n]
                                wv = wt[:].rearrange("p (c n) -> p c n", c=8)
                                po, pob = nbank()
                                for c in range(8):
                                    S.op("pe", lambda e: e.matmul(po[:], lhsT=yT[:, c, tsl], rhs=wv[:, c, :], start=(c == 0), stop=(c == 7)), reads=[wbf, b_yT], writes=[pob])
                                i = (j * 2 + n) % 2
                                S.op("dve", lambda e: e.tensor_tensor(out=tx[i][:], in0=po[:], in1=gate1[:, n * 512:(n + 1) * 512], op=ALU.mult), reads=[pob, CONST], writes=[b_tx[i]])
                                S.op("dve", lambda e: e.tensor_tensor(out=xs[:, j, n * 512:(n + 1) * 512], in0=xs[:, j, n * 512:(n + 1) * 512], in1=tx[i][:], op=ALU.add),
                                     reads=[b_tx[i], b_xs[j]], writes=[b_xs[j]])
                        for j in range(4):
                            norm_block(st, j, mult2, shift2, h2T, b_h2T, "n2")
                        S.barrier()

                    with ExitStack() as st:
                        wd, wdb = wload(st, "w_down", "down")
                        wdv = wd[:].rearrange("p (f n) -> p f n", f=NF)
                        actT = sbt(st, "actT", [128, NF, 512], BF16)
                        b_actT = S.buf("actT")
                        wu = [None, None, None]
                        uu = [sbt(st, "fu%d" % i, [128, 512]) for i in range(8)]
                        b_uu = [S.buf("fu%d" % i) for i in range(8)]
                        fe = [sbt(st, "fe%d" % i, [128, 512]) for i in range(2)]
                        b_fe = [S.buf("fe%d" % i) for i in range(2)]
                        wus = [sbt(st, "wup%d" % i, [128, 4096], BF16) for i in range(3)]
                        b_wus = [S.buf("wup%d" % i) for i in range(3)]

                        def ldup(jj):
                            off, n = PIECES["up%d" % jj]
                            S.dma("sp", wus[jj % 3][:], wb_d[:, off:off + n], reads=[b_wb], writes=[b_wus[jj % 3]])

                        ldup(0)
                        ldup(1)
                        for jj in range(11):
                            if jj + 2 < 11:
                                ldup(jj + 2)
                            wv = wus[jj % 3][:].rearrange("p (c n) -> p c n", c=8)
                            wbf = b_wus[jj % 3]
                            for k in range(4):
                                q = jj * 4 + k
                                pb, pbb = nbank()
                                for dc in range(8):
                                    S.op("pe", lambda e: e.matmul(pb[:], lhsT=wv[:, dc, k * 128:(k + 1) * 128], rhs=h2T[:, dc, :], start=(dc == 0), stop=(dc == 7)),
                                         reads=[wbf, b_h2T], writes=[pbb])
                                kk_ = (jj % 2) * 4 + k
                                wsel = fcwf if flagged else fcw
                                S.op("act", lambda e: e.activation(out=uu[kk_][:], in_=pb[:], func=AF.Identity, scale=wsel[:, q, 2:3], bias=fcb[:, q:q + 1]),
                                     reads=[pbb, CONST] + cb, writes=[b_uu[kk_]])
                                for tp in range(2):
                                    sh = 2 - tp
                                    S.op("dve", lambda e: e.scalar_tensor_tensor(out=uu[kk_][:, sh:512], in0=pb[:, 0:512 - sh], scalar=wsel[:, q, tp:tp + 1], in1=uu[kk_][:, sh:512],
                                                                                op0=ALU.mult, op1=ALU.add), reads=[pbb, b_uu[kk_], CONST], writes=[b_uu[kk_]])
                                for tp in range(2):
                                    sh = 2 - tp
                                    S.op("dve", lambda e: e.scalar_tensor_tensor(out=uu[kk_][:, 0:sh], in0=fhalo[:, q, 2 - sh:2], scalar=fcw[:, q, tp:tp + 1], in1=uu[kk_][:, 0:sh],
                                                                                op0=ALU.mult, op1=ALU.add), reads=[b_fhalo, b_uu[kk_]] + cb, writes=[b_uu[kk_]])
                                if flagged:
                                    S.op("act", lambda e: e.activation(out=fhalo[:, q, :], in_=pb[:, 510:512], func=AF.Copy, scale=flag[:, 0:1]), reads=[pbb, cb[4]], writes=[b_fhalo])
                                else:
                                    S.op("act", lambda e: e.copy(out=fhalo[:, q, :], in_=pb[:, 510:512]), reads=[pbb], writes=[b_fhalo])
                            for i in range(2):
                                f = 2 * jj + i
                                uv = uu[(jj % 2) * 4 + i]
                                ug = uu[(jj % 2) * 4 + 2 + i]
                                b_uv = b_uu[(jj % 2) * 4 + i]
                                b_ug = b_uu[(jj % 2) * 4 + 2 + i]
                                S.op("act", lambda e: e.activation(out=fe[i][:], in_=ug[:], func=AF.Silu), reads=[b_ug], writes=[b_fe[i]])
                                S.op("dve", lambda e: e.tensor_tensor(out=actT[:, f, :], in0=uv[:], in1=fe[i][:], op=ALU.mult), reads=[b_uv, b_fe[i]], writes=[b_actT])
                        tx = fe
                        b_tx = b_fe
                        for j in range(4):
                            tsl = slice(j * 128, (j + 1) * 128)
                            blk = u * 4 + j
                            for n in range(2):
                                po, pob = nbank()
                                for f in range(NF):
                                    S.op("pe", lambda e: e.matmul(po[:], lhsT=actT[:, f, tsl], rhs=wdv[:, f, n * 512:(n + 1) * 512], start=(f == 0), stop=(f == NF - 1)),
                                         reads=[wdb, b_actT], writes=[pob])
                                i = (j * 2 + n) % 2
                                S.op("dve", lambda e: e.tensor_tensor(out=tx[i][:], in0=po[:], in1=gate2[:, n * 512:(n + 1) * 512], op=ALU.mult), reads=[pob, CONST], writes=[b_tx[i]])
                                S.op("dve", lambda e: e.tensor_tensor(out=xs[:, j, n * 512:(n + 1) * 512], in0=xs[:, j, n * 512:(n + 1) * 512], in1=tx[i][:], op=ALU.add),
                                     reads=[b_tx[i], b_xs[j]], writes=[b_xs[j]])
                            if own:
                                ob = blk - NFLAG * 4
                                S.dma("pool", out_d[ob * 128:(ob + 1) * 128, :], xs[:, j, :], reads=[b_xs[j]], writes=[b_out])
                        S.barrier()
        S.barrier()
    return nc


def _t5_bucket(n):
    n = np.maximum(n, 0)
    max_exact = 16
    nf = np.maximum(n, 1).astype(np.float32)
    large = max_exact + (np.log(nf / np.float32(max_exact)) / np.float32(math.log(128 / max_exact)) * np.float32(32 - max_exact)).astype(np.int32)
    large = np.minimum(large, 31)
    return np.where(n < max_exact, n, large)


def _fm(v, nch):
    return np.ascontiguousarray(np.asarray(v, np.float32).reshape(nch, 128).T)


def _piece_fm(w):
    n = w.shape[1]
    return w.reshape(8, 128, n).transpose(1, 0, 2).reshape(128, 8 * n)


def prepare_inputs(NCTX, NFULL, inputs):
    f32 = np.float32
    g = {k: np.asarray(v) for k, v in inputs.items()}
    NU = NCTX + NFULL
    half_tok = (NU // 2) * 512
    x = g["x"].astype(f32, copy=False)
    B = x.shape[0]
    assert x.shape[1] == 2 * half_tok
    w_in = g["w_in"][0]
    cols = {}
    o = 0
    for nm, n in (("mqk", 1024), ("mv", 512), ("mo", 512), ("mi", 4), ("mf", 4), ("aq", 512), ("ak", 512), ("av", 512), ("gm", 1024), ("ga", 1024)):
        cols[nm] = w_in[:, o:o + n]
        o += n

    def perm_qk(w):
        return w.reshape(1024, 2, 4, 64).transpose(0, 2, 1, 3).reshape(1024, 512)

    wall = np.zeros((128, WPAD), f32)

    def put(nm, arr):
        off, n = PIECES[nm]
        assert arr.shape == (128, n), (nm, arr.shape, n)
        wall[:, off:off + n] = arr

    put("mq", _piece_fm(cols["mqk"][:, 0:512]))
    put("mk", _piece_fm(cols["mqk"][:, 512:1024]))
    put("mv", _piece_fm(cols["mv"]))
    put("mo", _piece_fm(cols["mo"]))
    put("aq", _piece_fm(perm_qk(cols["aq"])))
    put("ak", _piece_fm(perm_qk(cols["ak"])))
    put("av", _piece_fm(cols["av"]))
    put("gm0", _piece_fm(cols["gm"][:, 0:512]))
    put("gm1", _piece_fm(cols["gm"][:, 512:1024]))
    put("ga0", _piece_fm(cols["ga"][:, 0:512]))
    put("ga1", _piece_fm(cols["ga"][:, 512:1024]))
    put("gif", _piece_fm(np.concatenate([cols["mi"], cols["mf"]], axis=1)))
    put("bm", g["w_branch_m"][0].reshape(4, 128, 1024).transpose(1, 0, 2).reshape(128, 4096))
    put("ba", g["w_branch_a"][0].reshape(4, 128, 1024).transpose(1, 0, 2).reshape(128, 4096))
    put("out0", _piece_fm(g["w_out"][0][:, 0:512]))
    put("out1", _piece_fm(g["w_out"][0][:, 512:1024]))
    w_up = g["w_up"][0]
    fcw_full = g["ffn_conv_w"][0]
    fcb_full = g["ffn_conv_b"][0]
    chunk_cols = []
    for jj in range(11):
        cc = [np.arange((2 * jj + i) * 128, (2 * jj + i + 1) * 128) for i in range(2)]
        cc += [DFF + np.arange((2 * jj + i) * 128, (2 * jj + i + 1) * 128) for i in range(2)]
        idx = np.concatenate(cc)
        chunk_cols.append(idx)
        put("up%d" % jj, _piece_fm(w_up[:, idx]))
    allidx = np.concatenate(chunk_cols)
    fcw = fcw_full[:, allidx].reshape(3, 44, 128).transpose(2, 1, 0).reshape(128, 44 * 3)
    fcb = fcb_full[allidx].reshape(44, 128).T
    put("down", g["w_down"][0].reshape(NF, 128, 1024).transpose(1, 0, 2).reshape(128, NF * 1024))

    w_ada = g["w_ada"][0]
    wada = np.stack([_piece_fm(w_ada[:, p * 512:(p + 1) * 512]) for p in range(12)]).astype(f32)
    bada = g["b_ada"][0].astype(f32)
    mcw = g["m_conv_w"][0].reshape(4, 8, 128).transpose(2, 1, 0).reshape(128, 32)
    mcb = _fm(g["m_conv_b"][0], 8)
    gifb = np.concatenate([g["m_igate_b"][0], g["m_fgate_b"][0]]).astype(f32)
    gq = np.tile(g["a_qnorm_g"][0], 8).astype(f32)
    gk = np.tile(g["a_knorm_g"][0], 8).astype(f32)
    rel = g["rel_bias"].astype(f32)
    kk = np.arange(128)[:, None]
    qq = np.arange(128)[None, :]
    biasg = np.zeros((128, 4, 2, 2, 128), f32)
    maskn = np.zeros((128, 4, 2, 2, 128), f32)
    for dl in range(2):
        dist = qq - kk + 128 * dl
        bidx = _t5_bucket(dist)
        for h in range(4):
            t = rel[bidx, h]
            biasg[:, h, dl, 0, :] = t
            biasg[:, h, dl, 1, :] = t
        if dl == 0:
            mk = np.where(dist < 0, NEG, 0.0).astype(f32)
            maskn[:, :, 0, :, :] = mk[:, None, None, :]
    farb = rel[31, :].astype(f32)
    umask = (kk <= qq).astype(f32)
    negm4 = np.tile(np.where(kk <= qq, 0.0, NEG).astype(f32), (1, 4))
    import ml_dtypes
    common = dict(
        wada=wada, bada=bada, badafm=_fm(bada, 48), g1fm=_fm(g["norm1_g"][0], 8), g2fm=_fm(g["norm2_g"][0], 8),
        wall=wall, mcw=np.ascontiguousarray(mcw, f32), mcb=mcb, fcw=np.ascontiguousarray(fcw, f32), fcb=np.ascontiguousarray(fcb, f32),
        gifb=gifb, mng=g["m_norm_g"][0].astype(f32), ang=g["a_norm_g"][0].astype(f32), gq=gq, gk=gk,
        alam=g["a_lambda"][0].reshape(256).astype(f32), biasg=biasg.reshape(128, -1), maskn=maskn.reshape(128, -1), farb=farb,
        identb=np.eye(128).astype(ml_dtypes.bfloat16), identf=np.eye(128, dtype=f32), umask=umask, negm4=negm4,
    )
    in_maps = []
    for b in range(B):
        cfm = _fm(g["c"][b], 8)
        for hf in range(2):
            if hf == 0:
                xl = np.concatenate([np.zeros((half_tok, D), f32), x[b, 0:half_tok]], axis=0)
                fl = np.zeros((128, 1), f32)
            else:
                xl = x[b]
                fl = np.ones((128, 1), f32)
            m = dict(common)
            m["x"] = np.ascontiguousarray(xl)
            m["cfm"] = cfm
            m["flag"] = fl
            in_maps.append(m)
    return in_maps


_NC_CACHE = {}


def run(NCTX, NFULL, inputs):
    key = (NCTX, NFULL)
    if key not in _NC_CACHE:
        _NC_CACHE[key] = build_program(NCTX, NFULL)
    nc = _NC_CACHE[key]
    in_maps = prepare_inputs(NCTX, NFULL, inputs)
    res = run_bass_kernel_spmd(nc, in_maps, core_ids=list(range(len(in_maps))))
    B = len(in_maps) // 2
    half_tok = ((NCTX + NFULL) // 2) * 512
    out = np.empty((B, 2 * half_tok, D), np.float32)
    for b in range(B):
        for hf in range(2):
            out[b, hf * half_tok:(hf + 1) * half_tok] = res.results[b * 2 + hf]["out"]
    return out


def kernel(**inputs):
    return run(7, 9, inputs)
```

```python
import math
from contextlib import ExitStack

import numpy as np

import concourse.bass as bass
import concourse.mybir as mybir
from concourse.bass_utils import run_bass_kernel_spmd

F32 = mybir.dt.float32
BF16 = mybir.dt.bfloat16
AF = mybir.ActivationFunctionType
ALU = mybir.AluOpType
AX = mybir.AxisListType

D = 1024
DC = 8
DFF = 2816
NF = 22
EPS = 1e-6
LAM_INIT = 0.8 - 0.6 * math.exp(-0.3 * 0)
NEG = -30000.0

PIECES = {}
_off = 0
for _nm, _n in (("mq", 4096), ("mk", 4096), ("mv", 4096), ("mo", 4096), ("aq", 4096), ("ak", 4096),
                ("av", 4096), ("gm0", 4096), ("gm1", 4096), ("ga0", 4096), ("ga1", 4096), ("gif", 64),
                ("bm", 4096), ("ba", 4096), ("out0", 4096), ("out1", 4096)):
    PIECES[_nm] = (_off, _n)
    _off += _n
for _j in range(11):
    PIECES["up%d" % _j] = (_off, 4096)
    _off += 4096
PIECES["down"] = (_off, NF * 1024)
_off += NF * 1024
WTOT = _off
WPAD = ((WTOT + 4095) // 4096) * 4096


class Buf:
    __slots__ = ("name", "w", "rs", "sem", "semv", "grp")

    def __init__(self, name, grp=None):
        self.name = name
        self.w = None
        self.rs = []
        self.sem = None
        self.semv = 0
        self.grp = grp


class Sched:
    def __init__(self, nc, stack):
        self.nc = nc
        self.stack = stack
        self.engs = {}
        for nm, h in (("pe", nc.tensor), ("act", nc.scalar), ("dve", nc.vector), ("pool", nc.gpsimd), ("sp", nc.sync)):
            sem = stack.enter_context(nc.semaphore("s_" + nm))
            self.engs[nm] = dict(h=h, sem=sem, cnt=0, seen={})
        self.groups = {}
        self.dma_ev = {}
        self.free_sems = {}
        self.live = []
        self.nsem = 0

    def buf(self, name, grp=None):
        return Buf(name, grp)

    def _need(self, e, deps):
        E = self.engs[e]
        best = {}
        for d in deps:
            if d is None:
                continue
            sem, val, en = d
            if en == e and e == "pe":
                continue
            k = id(sem)
            if k not in best or best[k][1] < val:
                best[k] = (sem, val)
        for k, (sem, val) in best.items():
            if E["seen"].get(k, 0) >= val:
                continue
            E["h"].wait_ge(sem, val)
            E["seen"][k] = val

    def op(self, e, fn, reads=(), writes=()):
        E = self.engs[e]
        deps = []
        for b in reads:
            deps.append(b.w)
        for b in writes:
            if b.w is not None and b.w[2] != e:
                deps.append(b.w)
            for r in b.rs:
                if r[2] != e:
                    deps.append(r)
        self._need(e, deps)
        ins = fn(E["h"])
        E["cnt"] += 1
        ins.then_inc(E["sem"], 1)
        ev = (E["sem"], E["cnt"], e)
        for b in reads:
            b.rs.append(ev)
        for b in writes:
            b.w = ev
            b.rs = []
        return ins

    def dma(self, q, out, in_, reads=(), writes=()):
        E = self.engs[q]
        deps = []
        for b in reads:
            deps.append(b.w)
        for b in writes:
            if b.grp in ("wbw", "kvw", "outw"):
                continue
            deps.append(b.w)
            deps.extend(b.rs)
        self._need(q, deps)
        ins = E["h"].dma_start(out=out, in_=in_)
        cands = list(writes) + list(reads)
        tgt = ([b for b in cands if b.grp is None] + cands)[0]
        if tgt.grp is not None:
            gk = tgt.grp
            if gk not in self.groups:
                self.groups[gk] = [self.stack.enter_context(self.nc.semaphore("g_" + gk)), 0]
            g = self.groups[gk]
            g[1] += 16
            sem, val = g[0], g[1]
        else:
            if tgt.sem is None:
                tgt.sem = {}
                self.live.append(tgt)
            if q not in tgt.sem:
                fs = self.free_sems.setdefault(q, [])
                if not fs:
                    self.nsem += 1
                    fs.append([self.stack.enter_context(self.nc.semaphore("dp%d" % self.nsem)), 0])
                tgt.sem[q] = fs.pop()
            ent = tgt.sem[q]
            ent[1] += 16
            sem, val = ent[0], ent[1]
        ins.then_inc(sem, 16)
        self.dma_ev[id(sem)] = (sem, val)
        ev = (sem, val, "dma")
        for b in reads:
            b.rs.append(ev)
        for b in writes:
            b.w = ev
            b.rs = []
        return ins

    def barrier(self, engines=("pe", "act", "dve", "pool", "sp")):
        deps = [(E["sem"], E["cnt"], "x") for E in self.engs.values() if E["cnt"] > 0]
        deps += [(s, v, "dma") for (s, v) in self.dma_ev.values()]
        for e in engines:
            self._need(e, deps)
        self.dma_ev = {}
        for b in self.live:
            for qq, ent in b.sem.items():
                self.free_sems[qq].append(ent)
            b.sem = None
        self.live = []

    def seal(self, bufs, grp):
        g = self.groups[grp]
        for b in bufs:
            b.w = (g[0], g[1], "dma")


def build_program(NCTX, NFULL, debug=False):
    NU = NCTX + NFULL
    NFLAG = NU // 2
    NBLK = NU * 4
    NTOK = NU * 512
    NOWN = (NFULL - 1) * 512
    assert NU == 2 * (NFULL - 1)

    nc = bass.Bass("TRN2", target_bir_lowering=False)

    def din(name, shape, dt=F32):
        return nc.dram_tensor(name, list(shape), dt, kind="ExternalInput").ap()

    x_d = din("x", [NTOK, D])
    cfm_d = din("cfm", [128, 8])
    wada_d = din("wada", [12, 128, 8 * 512])
    bada_d = din("bada", [6144])
    badafm_d = din("badafm", [128, 48])
    g1fm_d = din("g1fm", [128, 8])
    g2fm_d = din("g2fm", [128, 8])
    wall_d = din("wall", [128, WPAD])
    mcw_d = din("mcw", [128, 8 * 4])
    mcb_d = din("mcb", [128, 8])
    fcw_d = din("fcw", [128, 44 * 3])
    fcb_d = din("fcb", [128, 44])
    gifb_d = din("gifb", [8])
    mng_d = din("mng", [512])
    ang_d = din("ang", [512])
    gq_d = din("gq", [512])
    gk_d = din("gk", [512])
    alam_d = din("alam", [256])
    biasg_d = din("biasg", [128, 4 * 2 * 2 * 128])
    maskn_d = din("maskn", [128, 4 * 2 * 2 * 128])
    farb_d = din("farb", [4])
    flag_d = din("flag", [128, 1])
    identb_d = din("identb", [128, 128], BF16)
    identf_d = din("identf", [128, 128])
    umask_d = din("umask", [128, 128])
    negm4_d = din("negm4", [128, 512])
    out_d = nc.dram_tensor("out", [NOWN, D], F32, kind="ExternalOutput").ap()
    wb_d = nc.dram_tensor("wb_scr", [128, WPAD], BF16, kind="Internal").ap()
    kt_d = nc.dram_tensor("kt_scr", [4, 128, NBLK * 128], BF16, kind="Internal").ap()
    va_d = nc.dram_tensor("va_scr", [4, 128, NBLK * 130], BF16, kind="Internal").ap()

    with ExitStack() as top:
        S = Sched(nc, top)

        uid = [0]

        def sbt(st, name, shape, dt=F32):
            uid[0] += 1
            return st.enter_context(nc.sbuf_tensor("s%d_%s" % (uid[0], name), list(shape), dt))

        banks = [top.enter_context(nc.psum_tensor("bank%d" % i, [128, 512], F32)) for i in range(8)]
        bbufs = [S.buf("bank%d" % i) for i in range(8)]
        bank_rr = [0]

        def nbank(lo=0, hi=8):
            i = lo + (bank_rr[0] % (hi - lo))
            bank_rr[0] += 1
            return banks[i], bbufs[i]

        def bfview(bank):
            return bank[:].bitcast(BF16)

        identb = sbt(top, "identb", [128, 128], BF16)
        identf = sbt(top, "identf", [128, 128])
        umask = sbt(top, "umask", [128, 128])
        negm4 = sbt(top, "negm4", [128, 512])
        ones_f = sbt(top, "ones_f", [128, 128])
        flag = sbt(top, "flag", [128, 1])
        epst = sbt(top, "epst", [128, 1])
        onet = sbt(top, "onet", [128, 1])
        mult1 = sbt(top, "mult1", [128, 8])
        shift1 = sbt(top, "shift1", [128, 8])
        mult2 = sbt(top, "mult2", [128, 8])
        shift2 = sbt(top, "shift2", [128, 8])
        gate1 = sbt(top, "gate1", [128, D])
        gate2 = sbt(top, "gate2", [128, D])
        mcw = sbt(top, "mcw", [128, 8, 4])
        mcb = sbt(top, "mcb", [128, 8])
        mcwf = sbt(top, "mcwf", [128, 8, 4])
        fcwf = sbt(top, "fcwf", [128, 44, 3])
        mcnb = sbt(top, "mcnb", [128, 8])
        fcw = sbt(top, "fcw", [128, 44, 3])
        fcb = sbt(top, "fcb", [128, 44])
        fcnb = sbt(top, "fcnb", [128, 44])
        gifb = sbt(top, "gifb", [128, 8])
        mng = sbt(top, "mng", [128, 512])
        ang = sbt(top, "ang", [128, 512])
        gq = sbt(top, "gq", [128, 512])
        gk = sbt(top, "gk", [128, 512])
        biasb = sbt(top, "biasb", [128, 4, 2, 256], BF16)
        farb = sbt(top, "farb", [128, 4])
        neglam = sbt(top, "neglam", [128, 1])
        Sst = sbt(top, "Sst", [128, 4, 130])
        Sstb = sbt(top, "Sstb", [128, 4, 130], BF16)
        mhalo = sbt(top, "mhalo", [128, 8, 3])
        fhalo = sbt(top, "fhalo", [128, 44, 2])
        CONST = S.buf("const")
        b_S = S.buf("Sst")
        b_Sb = S.buf("Sstb")
        b_mhalo = S.buf("mhalo")
        b_fhalo = S.buf("fhalo")
        b_wb = S.buf("wb_scr", grp="wbw")
        b_ktd = S.buf("kt_scr", grp="kvw")
        b_vad = S.buf("va_scr", grp="kvw")
        b_out = S.buf("outd", grp="outw")

        def ld_const(t, src, grp="cst"):
            b = S.buf("c_" + t.name, grp=grp)
            S.dma("sp", t[:], src, writes=[b])
            return b

        cb = []
        cb.append(ld_const(identb, identb_d[:, :]))
        cb.append(ld_const(identf, identf_d[:, :]))
        cb.append(ld_const(umask, umask_d[:, :]))
        cb.append(ld_const(negm4, negm4_d[:, :]))
        cb.append(ld_const(flag, flag_d[:, :]))
        cb.append(ld_const(mcw, mcw_d.rearrange("p (c k) -> p c k", k=4)))
        cb.append(ld_const(mcb, mcb_d[:, :]))
        cb.append(ld_const(fcw, fcw_d.rearrange("p (c k) -> p c k", k=3)))
        cb.append(ld_const(fcb, fcb_d[:, :]))
        cb.append(ld_const(gifb, gifb_d.partition_broadcast(128)))
        cb.append(ld_const(mng, mng_d.partition_broadcast(128)))
        cb.append(ld_const(ang, ang_d.partition_broadcast(128)))
        cb.append(ld_const(gq, gq_d.partition_broadcast(128)))
        cb.append(ld_const(gk, gk_d.partition_broadcast(128)))
        cb.append(ld_const(farb, farb_d.partition_broadcast(128)))
        S.seal(cb, "cst")
        S.op("dve", lambda e: e.memset(ones_f[:], 1.0), writes=[CONST])
        S.op("dve", lambda e: e.memset(epst[:], EPS), writes=[CONST])
        S.op("dve", lambda e: e.memset(onet[:], 1.0), writes=[CONST])
        S.op("dve", lambda e: e.memset(Sst[:], 0.0), writes=[b_S])
        S.op("dve", lambda e: e.memset(Sstb[:], 0.0), writes=[b_Sb])
        S.op("dve", lambda e: e.memset(mhalo[:], 0.0), writes=[b_mhalo])
        S.op("dve", lambda e: e.memset(fhalo[:], 0.0), writes=[b_fhalo])
        S.op("dve", lambda e: e.tensor_scalar(out=mcnb[:], in0=mcb[:], scalar1=-1.0, scalar2=None, op0=ALU.mult), reads=cb, writes=[CONST])
        S.op("dve", lambda e: e.tensor_scalar(out=fcnb[:], in0=fcb[:], scalar1=-1.0, scalar2=None, op0=ALU.mult), reads=cb, writes=[CONST])
        S.op("dve", lambda e: e.tensor_scalar(out=gq[:], in0=gq[:], scalar1=0.125, scalar2=None, op0=ALU.mult), reads=cb, writes=[CONST])
        S.op("dve", lambda e: e.tensor_scalar(out=mcwf[:], in0=mcw[:], scalar1=flag[:, 0:1], scalar2=None, op0=ALU.mult), reads=cb, writes=[CONST])
        S.op("dve", lambda e: e.tensor_scalar(out=fcwf[:], in0=fcw[:], scalar1=flag[:, 0:1], scalar2=None, op0=ALU.mult), reads=cb, writes=[CONST])
        S.op("dve", lambda e: e.tensor_scalar(out=ang[:], in0=ang[:], scalar1=1.0 - LAM_INIT, scalar2=None, op0=ALU.mult), reads=cb, writes=[CONST])

        with ExitStack() as st:
            cfm = sbt(st, "cfm", [128, 8])
            sc = sbt(st, "sc", [128, 8])
            sct = sbt(st, "sct", [128, 8])
            scbc = sbt(st, "scbc", [128, 8, 128])
            badafm = sbt(st, "badafm", [128, 48])
            g1fm = sbt(st, "g1fm", [128, 8])
            g2fm = sbt(st, "g2fm", [128, 8])
            modfm = sbt(st, "modfm", [128, 48])
            alam = sbt(st, "alam", [128, 256])
            lt = sbt(st, "lt", [128, 128])
            ls = sbt(st, "ls", [128, 2])
            biasg = sbt(st, "biasg", [128, 2048])
            maskn = sbt(st, "maskn", [128, 2048])
            wst = [sbt(st, "wst%d" % i, [128, 4096]) for i in range(2)]
            wbo = [sbt(st, "wbo%d" % i, [128, 4096], BF16) for i in range(2)]
            bbt = [sbt(st, "bbt%d" % i, [128, 512]) for i in range(2)]
            b_wst = [S.buf("wst%d" % i) for i in range(2)]
            b_wbo = [S.buf("wbo%d" % i) for i in range(2)]
            b_bbt = [S.buf("bbt%d" % i) for i in range(2)]
            L = S.buf("prel")
            b_l = [ld_const(cfm, cfm_d[:, :], "cst2"), ld_const(badafm, badafm_d[:, :], "cst2"), ld_const(g1fm, g1fm_d[:, :], "cst2"),
                   ld_const(g2fm, g2fm_d[:, :], "cst2"), ld_const(alam, alam_d.partition_broadcast(128), "cst2"),
                   ld_const(biasg, biasg_d[:, :], "cst2"), ld_const(maskn, maskn_d[:, :], "cst2")]
            S.seal(b_l, "cst2")
            S.op("act", lambda e: e.activation(out=sct[:], in_=cfm[:], func=AF.Exp, scale=-1.0), reads=b_l, writes=[L])
            S.op("dve", lambda e: e.tensor_scalar(out=sct[:], in0=sct[:], scalar1=1.0, scalar2=None, op0=ALU.add), reads=[L], writes=[L])
            S.op("dve", lambda e: e.reciprocal(out=sct[:], in_=sct[:]), reads=[L], writes=[L])
            S.op("dve", lambda e: e.tensor_tensor(out=sc[:], in0=cfm[:], in1=sct[:], op=ALU.mult), reads=[L], writes=[L])
            S.op("dve", lambda e: e.tensor_copy(out=scbc[:], in_=sc[:].unsqueeze(2).to_broadcast([128, 8, 128])), reads=[L], writes=[L])
            S.op("dve", lambda e: e.tensor_tensor(out=lt[:, 0:64], in0=alam[:, 0:64], in1=alam[:, 64:128], op=ALU.mult), reads=b_l, writes=[L])
            S.op("dve", lambda e: e.tensor_tensor(out=lt[:, 64:128], in0=alam[:, 128:192], in1=alam[:, 192:256], op=ALU.mult), reads=[L], writes=[L])
            S.op("dve", lambda e: e.tensor_reduce(out=ls[:], in_=lt[:].rearrange("p (a b) -> p a b", a=2), axis=AX.X, op=ALU.add), reads=[L], writes=[L])
            S.op("act", lambda e: e.activation(out=ls[:], in_=ls[:], func=AF.Exp), reads=[L], writes=[L])
            S.op("dve", lambda e: e.tensor_tensor(out=neglam[:], in0=ls[:, 1:2], in1=ls[:, 0:1], op=ALU.subtract), reads=[L], writes=[CONST])
            S.op("dve", lambda e: e.tensor_scalar(out=neglam[:], in0=neglam[:], scalar1=-LAM_INIT, scalar2=None, op0=ALU.add), reads=[CONST], writes=[CONST])
            S.op("dve", lambda e: e.tensor_tensor(out=biasb[:].rearrange("p a b c -> p (a b c)"), in0=biasg[:], in1=maskn[:], op=ALU.add), reads=b_l, writes=[CONST])

            fm_ps, fm_b = banks[7], bbufs[7]
            fm_cols = {0: 0, 1: 4, 2: 8, 3: 12, 6: 24, 7: 28, 8: 32, 9: 36}
            for pc in range(12):
                i = pc % 2
                S.dma("sp", wst[i][:], wada_d[pc, :, :], writes=[b_wst[i]])
                wv = wst[i][:].rearrange("p (c n) -> p c n", c=8)
                if pc in (4, 5, 10, 11):
                    S.dma("sp", bbt[i][:], bada_d[pc * 512:(pc + 1) * 512].partition_broadcast(128), writes=[b_bbt[i]])
                    pb, pbb = nbank(0, 6)
                    for dc in range(8):
                        S.op("pe", lambda e: e.matmul(pb[:], lhsT=scbc[:, dc, :], rhs=wv[:, dc, :], start=(dc == 0), stop=(dc == 7)),
                             reads=[L, b_wst[i]], writes=[pbb])
                    gt = gate1 if pc < 6 else gate2
                    off = (pc % 2) * 512
                    S.op("dve", lambda e: e.tensor_tensor(out=gt[:, off:off + 512], in0=pb[:], in1=bbt[i][:], op=ALU.add),
                         reads=[pbb, b_bbt[i]], writes=[CONST])
                else:
                    for k in range(4):
                        col = fm_cols[pc] + k
                        for dc in range(8):
                            S.op("pe", lambda e: e.matmul(fm_ps[:, col:col + 1], lhsT=wv[:, dc, k * 128:(k + 1) * 128], rhs=sc[:, dc:dc + 1],
                                                         start=(dc == 0), stop=(dc == 7)), reads=[L, b_wst[i]], writes=[fm_b])
            S.op("dve", lambda e: e.tensor_tensor(out=modfm[:, 0:16], in0=fm_ps[:, 0:16], in1=badafm[:, 0:16], op=ALU.add), reads=[fm_b] + b_l, writes=[L])
            S.op("dve", lambda e: e.tensor_tensor(out=modfm[:, 24:40], in0=fm_ps[:, 24:40], in1=badafm[:, 24:40], op=ALU.add), reads=[fm_b] + b_l, writes=[L])
            S.op("dve", lambda e: e.scalar_tensor_tensor(out=mult1[:], in0=modfm[:, 8:16], scalar=1.0, in1=g1fm[:], op0=ALU.add, op1=ALU.mult), reads=[L], writes=[CONST])
            S.op("dve", lambda e: e.tensor_copy(out=shift1[:], in_=modfm[:, 0:8]), reads=[L], writes=[CONST])
            S.op("dve", lambda e: e.scalar_tensor_tensor(out=mult2[:], in0=modfm[:, 32:40], scalar=1.0, in1=g2fm[:], op0=ALU.add, op1=ALU.mult), reads=[L], writes=[CONST])
            S.op("dve", lambda e: e.tensor_copy(out=shift2[:], in_=modfm[:, 24:32]), reads=[L], writes=[CONST])

            cast_eng = ["dve", "act"]
            for ch in range(WPAD // 4096):
                i = ch % 2
                S.dma("sp", wst[i][:], wall_d[:, ch * 4096:(ch + 1) * 4096], writes=[b_wst[i]])
                ce = cast_eng[ch % 2]
                if ce == "act":
                    S.op("act", lambda e: e.copy(out=wbo[i][:], in_=wst[i][:]), reads=[b_wst[i]], writes=[b_wbo[i]])
                else:
                    S.op(ce, lambda e: e.tensor_copy(out=wbo[i][:], in_=wst[i][:]), reads=[b_wst[i]], writes=[b_wbo[i]])
                S.dma("pool", wb_d[:, ch * 4096:(ch + 1) * 4096], wbo[i][:], reads=[b_wbo[i]], writes=[b_wb])
            S.barrier()

        def wload(st, name, piece, shape3=None):
            off, n = PIECES[piece]
            t = sbt(st, name, [128, n], BF16)
            b = S.buf(name)
            S.dma("sp", t[:], wb_d[:, off:off + n], reads=[b_wb], writes=[b])
            return t, b

        for u in range(NU):
            full = u >= NCTX
            flagged = u < NFLAG
            own = u >= NFLAG
            with ExitStack() as su:
                xs = sbt(su, "xs", [128, 4, D])
                hT = sbt(su, "hT", [128, 8, 512], BF16)
                mqT = sbt(su, "mqT", [128, 8, 512], BF16)
                mVA = sbt(su, "mVA", [128, 4, 4, 130], BF16)
                sigmo = sbt(su, "sigmo", [128, 4, 512])
                gif = sbt(su, "gif", [128, 4, 8])
                Qbd = sbt(su, "Qbd", [128, 4, 4, 256], BF16)
                hmT = sbt(su, "hmT", [128, 4, 512], BF16)
                haT = sbt(su, "haT", [128, 4, 512], BF16)
                b_xs = [S.buf("xs%d" % j) for j in range(4)]
                b_hT = S.buf("hT")
                b_mqT = S.buf("mqT")
                b_mVA = S.buf("mVA")
                b_sig = S.buf("sigmo")
                b_gif = S.buf("gif")
                b_Qbd = S.buf("Qbd")
                b_hmT = S.buf("hmT")
                b_haT = S.buf("haT")
                if full:
                    S.op("dve", lambda e: e.memset(Qbd[:], 0.0), writes=[b_Qbd])

                def norm_block(st, j, mult, shift, hdst, b_hdst, tagp):
                    junk = sbt(st, tagp + "junk%d" % j, [128, D], BF16)
                    xn = sbt(st, tagp + "xn%d" % j, [128, D], BF16)
                    ss = sbt(st, tagp + "ss%d" % j, [128, 2])
                    bl = S.buf("nb")
                    S.op("act", lambda e: e.activation(out=junk[:], in_=xs[:, j, :], func=AF.Square, scale=1.0 / 32.0, accum_out=ss[:, 0:1]),
                         reads=[b_xs[j]], writes=[bl])
                    S.op("act", lambda e: e.activation(out=ss[:, 1:2], in_=ss[:, 0:1], func=AF.Ln, bias=epst[:, 0:1], scale=1.0), reads=[bl, CONST], writes=[bl])
                    S.op("act", lambda e: e.activation(out=ss[:, 1:2], in_=ss[:, 1:2], func=AF.Exp, scale=-0.5), reads=[bl], writes=[bl])
                    S.op("dve", lambda e: e.tensor_scalar(out=xn[:], in0=xs[:, j, :], scalar1=ss[:, 1:2], scalar2=None, op0=ALU.mult),
                         reads=[bl, b_xs[j]], writes=[bl])
                    pb, pbb = nbank()
                    pv = bfview(pb).rearrange("p (c t) -> p c t", c=8)
                    for c in range(8):
                        S.op("pe", lambda e: e.transpose(out=pv[:, c, :], in_=xn[:, c * 128:(c + 1) * 128], identity=identb[:]),
                             reads=[bl, cb[0]], writes=[pbb])
                    for c in range(8):
                        S.op("act", lambda e: e.activation(out=hdst[:, c, j * 128:(j + 1) * 128], in_=pv[:, c, :], func=AF.Identity,
                                                          scale=mult[:, c:c + 1], bias=shift[:, c:c + 1]), reads=[pbb, CONST], writes=[b_hdst])

                with ExitStack() as st:
                    names = (["mq", "mk", "mv", "gif", "av", "mo", "aq", "ak"] if full else ["mk", "mv", "gif", "av", "ak"])
                    W = {}
                    for j in range(4):
                        blk = u * 4 + j
                        S.dma("sp", xs[:, j, :], x_d[blk * 128:(blk + 1) * 128, :], writes=[b_xs[j]])
                    for nm in names:
                        W[nm] = wload(st, "w_" + nm, nm)
                    for j in range(4):
                        norm_block(st, j, mult1, shift1, hT, b_hT, "n1")
                    if flagged:
                        S.op("dve", lambda e: e.tensor_copy(out=mVA[:, :, :, 128:130], in_=flag[:, 0:1].unsqueeze(1).unsqueeze(1).to_broadcast([128, 4, 4, 2])),
                             reads=[cb[4]], writes=[b_mVA])
                    else:
                        S.op("dve", lambda e: e.memset(mVA[:, :, :, 128:130], 1.0), writes=[b_mVA])
                    acc = [sbt(st, "acc%d" % i, [128, 512]) for i in range(2)]
                    et = [sbt(st, "et%d" % i, [128, 512]) for i in range(2)]
                    b_acc = [S.buf("acc%d" % i) for i in range(2)]
                    b_et = [S.buf("et%d" % i) for i in range(2)]
                    chunks = list(range(8)) if full else list(range(4, 8))
                    def mconv(ci, c):
                        i = ci % 2
                        wt, wbf = W["mq" if c < 4 else "mk"]
                        wv = wt[:].rearrange("p (c n) -> p c n", c=8)
                        k = c % 4
                        pb, pbb = nbank()
                        for dc in range(8):
                            S.op("pe", lambda e: e.matmul(pb[:], lhsT=wv[:, dc, k * 128:(k + 1) * 128], rhs=hT[:, dc, :], start=(dc == 0), stop=(dc == 7)),
                                 reads=[wbf, b_hT], writes=[pbb])
                        wsel = mcwf if flagged else mcw
                        S.op("act", lambda e: e.activation(out=acc[i][:], in_=pb[:], func=AF.Identity, scale=wsel[:, c, 3:4], bias=mcb[:, c:c + 1]),
                             reads=[pbb, CONST] + cb, writes=[b_acc[i]])
                        for tp in range(3):
                            sh = 3 - tp
                            S.op("dve", lambda e: e.scalar_tensor_tensor(out=acc[i][:, sh:512], in0=pb[:, 0:512 - sh], scalar=wsel[:, c, tp:tp + 1], in1=acc[i][:, sh:512],
                                                                        op0=ALU.mult, op1=ALU.add), reads=[pbb, b_acc[i], CONST], writes=[b_acc[i]])
                        for tp in range(3):
                            sh = 3 - tp
                            S.op("dve", lambda e: e.scalar_tensor_tensor(out=acc[i][:, 0:sh], in0=mhalo[:, c, 3 - sh:3], scalar=mcw[:, c, tp:tp + 1], in1=acc[i][:, 0:sh],
                                                                        op0=ALU.mult, op1=ALU.add), reads=[b_mhalo, b_acc[i]] + cb, writes=[b_acc[i]])
                        if flagged:
                            S.op("act", lambda e: e.activation(out=mhalo[:, c, :], in_=pb[:, 509:512], func=AF.Copy, scale=flag[:, 0:1]), reads=[pbb, cb[4]], writes=[b_mhalo])
                        else:
                            S.op("act", lambda e: e.copy(out=mhalo[:, c, :], in_=pb[:, 509:512]), reads=[pbb], writes=[b_mhalo])
                        yield
                        S.op("act", lambda e: e.activation(out=et[i][:], in_=acc[i][:], func=AF.Exp, scale=-1.0), reads=[b_acc[i]], writes=[b_et[i]])
                        sk = (128.0 ** 0.5) if c < 4 else 1.0
                        S.op("dve", lambda e: e.tensor_scalar(out=et[i][:], in0=et[i][:], scalar1=1.0, scalar2=sk, op0=ALU.add, op1=ALU.mult),
                             reads=[b_et[i]], writes=[b_et[i]])
                        S.op("dve", lambda e: e.reciprocal(out=et[i][:], in_=et[i][:]), reads=[b_et[i]], writes=[b_et[i]])
                        S.op("dve", lambda e: e.tensor_tensor(out=mqT[:, c, :], in0=acc[i][:], in1=et[i][:], op=ALU.mult),
                             reads=[b_acc[i], b_et[i]], writes=[b_mqT])
                    mgens = [mconv(ci, c) for ci, c in enumerate(chunks)]
                    for step in range(len(mgens) + 1):
                        if step < len(mgens):
                            next(mgens[step], None)
                        if step >= 1:
                            next(mgens[step - 1], None)
                    sq = [sbt(st, "sq%d" % i, [128, 512]) for i in range(2)]
                    qn = [sbt(st, "qn%d" % i, [128, 512]) for i in range(2)]
                    qb = [sbt(st, "qb%d" % i, [128, 512], BF16) for i in range(2)]
                    rs = [sbt(st, "rs%d" % i, [128, 16]) for i in range(2)]
                    KTb = [sbt(st, "KTb%d" % i, [128, 4, 128], BF16) for i in range(2)]
                    VAb = [sbt(st, "VAb%d" % i, [128, 4, 130], BF16) for i in range(2)]
                    b_t = [S.buf("tmj%d" % i) for i in range(2)]
                    b_KTb = [S.buf("KTb%d" % i) for i in range(2)]
                    b_VAb = [S.buf("VAb%d" % i) for i in range(2)]
                    for i in range(2):
                        if flagged:
                            S.op("dve", lambda e: e.tensor_copy(out=VAb[i][:, :, 128:130], in_=flag[:, 0:1].unsqueeze(1).to_broadcast([128, 4, 2])),
                                 reads=[cb[4]], writes=[b_VAb[i]])
                        else:
                            S.op("dve", lambda e: e.memset(VAb[i][:, :, 128:130], 1.0), writes=[b_VAb[i]])
                    for j in range(4):
                        blk = u * 4 + j
                        i = j % 2
                        tsl = slice(j * 128, (j + 1) * 128)

                        def proj(nm):
                            wt, wbf = W[nm]
                            n = PIECES[nm][1] // 8
                            wv = wt[:].rearrange("p (c n) -> p c n", c=8)
                            pb, pbb = nbank()
                            for dc in range(8):
                                S.op("pe", lambda e: e.matmul(pb[:, 0:n], lhsT=hT[:, dc, tsl], rhs=wv[:, dc, :], start=(dc == 0), stop=(dc == 7)),
                                     reads=[wbf, b_hT], writes=[pbb])
                            return pb, pbb

                        def evac_scaled(out_ap, in_ap, rd, wr):
                            if flagged:
                                S.op("act", lambda e: e.activation(out=out_ap, in_=in_ap, func=AF.Copy, scale=flag[:, 0:1]), reads=rd + [cb[4]], writes=wr)
                            else:
                                S.op("act", lambda e: e.copy(out=out_ap, in_=in_ap), reads=rd, writes=wr)

                        pb, pbb = proj("mv")
                        evac_scaled(mVA[:, j, :, 0:128], pb[:].rearrange("p (h d) -> p h d", h=4), [pbb], [b_mVA])
                        pb, pbb = proj("gif")
                        S.op("dve", lambda e: e.tensor_tensor(out=gif[:, j, :], in0=pb[:, 0:8], in1=gifb[:], op=ALU.add), reads=[pbb, cb[9]], writes=[b_gif])
                        pb, pbb = proj("av")
                        evac_scaled(VAb[i][:, :, 0:128], pb[:].rearrange("p (h d) -> p h d", h=4), [pbb], [b_VAb[i]])
                        S.dma("pool", va_d.rearrange("h p (b c) -> p h b c", c=130)[:, :, blk, :], VAb[i][:], reads=[b_VAb[i]], writes=[b_vad])
                        if full:
                            pb, pbb = proj("mo")
                            S.op("act", lambda e: e.activation(out=sigmo[:, j, :], in_=pb[:], func=AF.Exp, scale=-1.0), reads=[pbb], writes=[b_sig])
                            S.op("dve", lambda e: e.tensor_scalar(out=sigmo[:, j, :], in0=sigmo[:, j, :], scalar1=1.0, scalar2=None, op0=ALU.add), reads=[b_sig], writes=[b_sig])
                            S.op("dve", lambda e: e.reciprocal(out=sigmo[:, j, :], in_=sigmo[:, j, :]), reads=[b_sig], writes=[b_sig])
                        for nm in (["aq", "ak"] if full else ["ak"]):
                            pb, pbb = proj(nm)
                            gt = gq if nm == "aq" else gk
                            S.op("act", lambda e: e.activation(out=sq[i][:], in_=pb[:], func=AF.Square, scale=0.125), reads=[pbb], writes=[b_t[i]])
                            S.op("dve", lambda e: e.tensor_reduce(out=rs[i][:, 0:8], in_=sq[i][:].rearrange("p (g d) -> p g d", d=64), axis=AX.X, op=ALU.add),
                                 reads=[b_t[i]], writes=[b_t[i]])
                            S.op("act", lambda e: e.activation(out=rs[i][:, 8:16], in_=rs[i][:, 0:8], func=AF.Ln, bias=epst[:, 0:1], scale=1.0), reads=[b_t[i], CONST], writes=[b_t[i]])
                            S.op("act", lambda e: e.activation(out=rs[i][:, 8:16], in_=rs[i][:, 8:16], func=AF.Exp, scale=-0.5), reads=[b_t[i]], writes=[b_t[i]])
                            S.op("dve", lambda e: e.tensor_tensor(out=qn[i][:].rearrange("p (g d) -> p g d", d=64), in0=pb[:].rearrange("p (g d) -> p g d", d=64),
                                                                 in1=rs[i][:, 8:16].unsqueeze(2).to_broadcast([128, 8, 64]), op=ALU.mult), reads=[pbb, b_t[i]], writes=[b_t[i]])
                            S.op("dve", lambda e: e.tensor_tensor(out=qb[i][:], in0=qn[i][:], in1=gt[:], op=ALU.mult), reads=[b_t[i], CONST] + cb, writes=[b_t[i]])
                            tb, tbb = nbank()
                            tv = bfview(tb)[:, 0:512].rearrange("p (h t) -> p h t", h=4)
                            for h in range(4):
                                S.op("pe", lambda e: e.transpose(out=tv[:, h, :], in_=qb[i][:, h * 128:(h + 1) * 128], identity=identb[:]), reads=[b_t[i], cb[0]], writes=[tbb])
                            if nm == "aq":
                                S.op("dve", lambda e: e.tensor_copy(out=Qbd[0:64, :, j, 0:128], in_=tv[0:64, :, :]), reads=[tbb], writes=[b_Qbd])
                                S.op("dve", lambda e: e.tensor_copy(out=Qbd[64:128, :, j, 128:256], in_=tv[64:128, :, :]), reads=[tbb], writes=[b_Qbd])
                            else:
                                S.op("act", lambda e: e.copy(out=KTb[i][:], in_=tv), reads=[tbb], writes=[b_KTb[i]])
                                S.dma("pool", kt_d.rearrange("h p k -> p h k")[:, :, blk * 128:(blk + 1) * 128], KTb[i][:], reads=[b_KTb[i]], writes=[b_ktd])
                    S.barrier()

                with ExitStack() as st:
                    def mlstm_block(j):
                        i = j % 2
                        tsl = slice(j * 128, (j + 1) * 128)
                        lf = sbt(st, "lf%d" % j, [128, 4])
                        e1 = sbt(st, "e1%d" % j, [128, 4])
                        e2 = sbt(st, "e2%d" % j, [128, 4])
                        wsx = sbt(st, "wsx%d" % j, [128, 4])
                        RL = sbt(st, "RL%d" % j, [128, 4, 128])
                        Et = sbt(st, "Et%d" % j, [128, 512])
                        Kw = sbt(st, "Kw%d" % j, [128, 4, 128], BF16)
                        bm = S.buf("ml%d" % j)
                        b_RL = S.buf("RL")
                        b_Et = S.buf("Et")
                        b_Kw = S.buf("Kw")
                        S.op("act", lambda e: e.activation(out=lf[:], in_=gif[:, j, 4:8], func=AF.Exp, scale=-1.0), reads=[b_gif], writes=[bm])
                        S.op("act", lambda e: e.activation(out=lf[:], in_=lf[:], func=AF.Ln, bias=onet[:, 0:1], scale=1.0), reads=[bm, CONST], writes=[bm])
                        S.op("dve", lambda e: e.tensor_scalar(out=lf[:], in0=lf[:], scalar1=-1.0, scalar2=None, op0=ALU.mult), reads=[bm], writes=[bm])
                        p1, p1b = nbank()
                        S.op("pe", lambda e: e.matmul(p1[:, 0:4], lhsT=umask[:], rhs=lf[:], start=True, stop=True), reads=[bm, cb[2]], writes=[p1b])
                        S.op("dve", lambda e: e.tensor_tensor(out=RL[:], in0=umask[:].unsqueeze(1).to_broadcast([128, 4, 128]),
                                                             in1=lf[:].unsqueeze(2).to_broadcast([128, 4, 128]), op=ALU.mult), reads=[bm, cb[2]], writes=[b_RL])
                        RLf = RL[:].rearrange("p h t -> p (h t)")
                        pBc, pBcb = nbank()
                        S.op("pe", lambda e: e.matmul(pBc[:], lhsT=ones_f[:], rhs=RLf, start=True, stop=True), reads=[b_RL, CONST], writes=[pBcb])
                        S.op("dve", lambda e: e.tensor_tensor(out=e1[:], in0=gif[:, j, 0:4], in1=p1[:, 0:4], op=ALU.subtract), reads=[b_gif, p1b], writes=[bm])
                        blast = pBc[:].rearrange("p (h t) -> p h t", h=4)[:, :, 127]
                        S.op("dve", lambda e: e.tensor_tensor(out=e2[:], in0=e1[:], in1=blast, op=ALU.add), reads=[bm, pBcb], writes=[bm])
                        S.op("act", lambda e: e.activation(out=wsx[:], in_=e2[:], func=AF.Exp), reads=[bm], writes=[bm])
                        S.op("act", lambda e: e.activation(out=Et[:], in_=pBc[:], func=AF.Exp), reads=[pBcb], writes=[b_Et])
                        tb, tbb = nbank()
                        tv = bfview(tb)[:, 0:512].rearrange("p (h t) -> p h t", h=4)
                        for h in range(4):
                            S.op("pe", lambda e: e.transpose(out=tv[:, h, :], in_=mqT[:, 4 + h, tsl], identity=identb[:]), reads=[b_mqT, cb[0]], writes=[tbb])
                        S.op("dve", lambda e: e.tensor_tensor(out=Kw[:], in0=tv, in1=wsx[:].unsqueeze(2).to_broadcast([128, 4, 128]), op=ALU.mult),
                             reads=[tbb, bm], writes=[b_Kw])
                        yield
                        if full:
                            DT = sbt(st, "DT%d" % j, [128, 4, 128])
                            PT = sbt(st, "PT%d" % j, [128, 4, 128], BF16)
                            qpT = sbt(st, "qpT%d" % j, [128, 4, 128], BF16)
                            rden = sbt(st, "rden%d" % j, [128, 4])
                            hmr = sbt(st, "hmr%d" % j, [128, 4, 128])
                            hsq = sbt(st, "hsq%d" % j, [128, 4, 128])
                            hss = sbt(st, "hss%d" % j, [128, 8])
                            gs = sbt(st, "gs%d" % j, [128, 512])
                            hmf = sbt(st, "hmf%d" % j, [128, 4, 128], BF16)
                            b_o = S.buf("mo%d" % j)
                            b_DT = S.buf("DT")
                            b_PT = S.buf("PT")
                            b_qp = S.buf("qpT")
                            pBm, pBmb = nbank()
                            S.op("pe", lambda e: e.matmul(pBm[:], lhsT=ones_f[:], rhs=RLf, start=True, stop=False), reads=[b_RL, CONST], writes=[pBmb])
                            S.op("pe", lambda e: e.matmul(pBm[:], lhsT=identf[:], rhs=negm4[:], start=False, stop=True), reads=[cb[1], cb[3]], writes=[pBmb])
                            for h in range(4):
                                S.op("act", lambda e: e.activation(out=DT[:, h, :], in_=pBm[:, h * 128:(h + 1) * 128], func=AF.Exp, bias=e1[:, h:h + 1], scale=1.0),
                                     reads=[pBmb, bm], writes=[b_DT])
                            S.op("dve", lambda e: e.tensor_tensor(out=qpT[:], in0=mqT[:, 0:4, tsl], in1=Et[:].rearrange("p (h t) -> p h t", h=4), op=ALU.mult),
                                 reads=[b_mqT, b_Et], writes=[b_qp])
                            pA, pAb = nbank()
                            for h in range(4):
                                S.op("pe", lambda e: e.matmul(pA[:, h * 128:(h + 1) * 128], lhsT=mqT[:, 4 + h, tsl], rhs=mqT[:, h, tsl], start=True, stop=True),
                                     reads=[b_mqT], writes=[pAb])
                            S.op("dve", lambda e: e.tensor_tensor(out=PT[:], in0=pA[:].rearrange("p (h t) -> p h t", h=4), in1=DT[:], op=ALU.mult),
                                 reads=[pAb, b_DT], writes=[b_PT])
                        yield
                        if full:
                            pN = [nbank(), nbank()]
                            for h in range(4):
                                pn, pnb = pN[h // 2]
                                o = (h % 2) * 130
                                S.op("pe", lambda e: e.matmul(pn[:, o:o + 130], lhsT=PT[:, h, :], rhs=mVA[:, j, h, :], start=True, stop=False), reads=[b_PT, b_mVA], writes=[pnb])
                                S.op("pe", lambda e: e.matmul(pn[:, o:o + 130], lhsT=qpT[:, h, :], rhs=Sstb[:, h, :], start=False, stop=True), reads=[b_qp, b_Sb], writes=[pnb])
                            for hp in range(2):
                                pn, pnb = pN[hp]
                                pv = pn[:, 0:260].rearrange("p (h c) -> p h c", h=2)
                                S.op("act", lambda e: e.activation(out=rden[:, 2 * hp:2 * hp + 2], in_=pv[:, :, 128], func=AF.Abs), reads=[pnb], writes=[b_o])
                            S.op("dve", lambda e: e.tensor_scalar(out=rden[:], in0=rden[:], scalar1=1.0, scalar2=None, op0=ALU.max), reads=[b_o], writes=[b_o])
                            S.op("dve", lambda e: e.reciprocal(out=rden[:], in_=rden[:]), reads=[b_o], writes=[b_o])
                            for hp in range(2):
                                pn, pnb = pN[hp]
                                pv = pn[:, 0:260].rearrange("p (h c) -> p h c", h=2)
                                S.op("dve", lambda e: e.tensor_tensor(out=hmr[:, 2 * hp:2 * hp + 2, :], in0=pv[:, :, 0:128],
                                                                     in1=rden[:, 2 * hp:2 * hp + 2].unsqueeze(2).to_broadcast([128, 2, 128]), op=ALU.mult),
                                     reads=[pnb, b_o], writes=[b_o])
                            S.op("act", lambda e: e.activation(out=hsq[:], in_=hmr[:], func=AF.Square, scale=1.0 / math.sqrt(128.0)), reads=[b_o], writes=[b_o])
                            S.op("dve", lambda e: e.tensor_reduce(out=hss[:, 0:4], in_=hsq[:], axis=AX.X, op=ALU.add), reads=[b_o], writes=[b_o])
                            S.op("act", lambda e: e.activation(out=hss[:, 4:8], in_=hss[:, 0:4], func=AF.Ln, bias=epst[:, 0:1], scale=1.0), reads=[b_o, CONST], writes=[b_o])
                            S.op("act", lambda e: e.activation(out=hss[:, 4:8], in_=hss[:, 4:8], func=AF.Exp, scale=-0.5), reads=[b_o], writes=[b_o])
                            S.op("dve", lambda e: e.tensor_tensor(out=gs[:], in0=mng[:], in1=sigmo[:, j, :], op=ALU.mult), reads=[b_sig] + cb, writes=[b_o])
                            for h in range(4):
                                S.op("dve", lambda e: e.scalar_tensor_tensor(out=hmf[:, h, :], in0=hmr[:, h, :], scalar=hss[:, 4 + h:5 + h], in1=gs[:, h * 128:(h + 1) * 128],
                                                                            op0=ALU.mult, op1=ALU.mult), reads=[b_o], writes=[b_o])
                            tb2, tb2b = nbank()
                            tv2 = bfview(tb2)[:, 0:512].rearrange("p (h t) -> p h t", h=4)
                            for h in range(4):
                                S.op("pe", lambda e: e.transpose(out=tv2[:, h, :], in_=hmf[:, h, :], identity=identb[:]), reads=[b_o, cb[0]], writes=[tb2b])
                            S.op("act", lambda e: e.copy(out=hmT[:, :, tsl], in_=tv2), reads=[tb2b], writes=[b_hmT])
                        pC = [nbank(), nbank()]
                        for h in range(4):
                            pc_, pcb = pC[h // 2]
                            o = (h % 2) * 130
                            S.op("pe", lambda e: e.matmul(pc_[:, o:o + 130], lhsT=Kw[:, h, :], rhs=mVA[:, j, h, :], start=True, stop=True), reads=[b_Kw, b_mVA], writes=[pcb])
                        Ev = Et[:].rearrange("p (h t) -> p h t", h=4)
                        for h in range(4):
                            pc_, pcb = pC[h // 2]
                            o = (h % 2) * 130
                            S.op("dve", lambda e: e.scalar_tensor_tensor(out=Sst[:, h, :], in0=Sst[:, h, :], scalar=Ev[:, h, 127:128], in1=pc_[:, o:o + 130],
                                                                        op0=ALU.mult, op1=ALU.add), reads=[b_S, b_Et, pcb], writes=[b_S])
                        S.op("act", lambda e: e.copy(out=Sstb[:], in_=Sst[:]), reads=[b_S], writes=[b_Sb])
                    gens = [mlstm_block(j) for j in range(4)]
                    for ph in range(3):
                        for g_ in gens:
                            next(g_, None)
                    S.barrier()

                if full:
                    with ExitStack() as st:
                        NKC = 8
                        kch = [sbt(st, "kch%d" % i, [128, NKC * 128], BF16) for i in range(3)]
                        vch = [sbt(st, "vch%d" % i, [128, NKC, 130], BF16) for i in range(3)]
                        b_kch = [S.buf("kch%d" % i) for i in range(3)]
                        b_vch = [S.buf("vch%d" % i) for i in range(3)]
                        pex = [sbt(st, "pex%d" % i, [128, 512], BF16) for i in range(6)]
                        b_pex = [S.buf("pex%d" % i) for i in range(6)]
                        rr = sbt(st, "rr", [128, 8])
                        har = sbt(st, "har", [128, 4, 128])
                        hsq = sbt(st, "ahsq", [128, 4, 128])
                        hss = sbt(st, "ahss", [128, 8])
                        haf = sbt(st, "haf", [128, 4, 128], BF16)
                        b_a = S.buf("attn_o")
                        nkb_tot = u * 4 + 4
                        chunk_list = [(s0, min(NKC, nkb_tot - s0)) for s0 in range(0, nkb_tot, NKC)]
                        ring = 0
                        LCH = len(chunk_list)
                        loaded = set()

                        def ensure_chunk(g):
                            if g >= 4 * LCH or g in loaded:
                                return
                            loaded.add(g)
                            h_, ci_ = g // LCH, g % LCH
                            s0, nk = chunk_list[ci_]
                            ri = g % 3
                            S.dma("sp", kch[ri][:, 0:nk * 128], kt_d[h_, :, s0 * 128:(s0 + nk) * 128], reads=[b_ktd], writes=[b_kch[ri]])
                            S.dma("sp", vch[ri][:, 0:nk, :], va_d[h_, :, s0 * 130:(s0 + nk) * 130].rearrange("p (b c) -> p b c", c=130), reads=[b_vad], writes=[b_vch[ri]])

                        for h in range(4):
                            accs = [(banks[jq], bbufs[jq]) for jq in range(4)]
                            for jq in range(4):
                                S.op("dve", lambda e: e.memset(accs[jq][0][:, 0:260], 0.0), writes=[accs[jq][1]])
                            items = []
                            for ci, (s0, nk) in enumerate(chunk_list):
                                g = h * LCH + ci
                                ri = g % 3
                                for kk in range(nk):
                                    kb = s0 + kk
                                    for p2 in range(2):
                                        j0 = 2 * p2
                                        if kb <= u * 4 + j0 - 2:
                                            items.append((ri, kk, kb, (j0, j0 + 1), g))
                                        else:
                                            for jq in (j0, j0 + 1):
                                                if kb <= u * 4 + jq:
                                                    items.append((ri, kk, kb, (jq,), g))

                            LAG = 3
                            NS = 4
                            NPX = len(pex)

                            def emit_scores(it, idx):
                                ri, kk, kb, jqs, ci = it
                                ensure_chunk(ci)
                                ensure_chunk(ci + 1)
                                nq = len(jqs)
                                sl = idx % NS
                                sp_, spb = banks[4 + sl], bbufs[4 + sl]
                                sv = sp_[:, 0:256 * nq]
                                px = idx % NPX
                                delta = u * 4 + jqs[0] - kb
                                near = (nq == 1) and delta <= 1
                                rhs = Qbd[:, h, jqs[0]:jqs[0] + nq, :].rearrange("p j c -> p (j c)")
                                S.op("pe", lambda e: e.matmul(sv, lhsT=kch[ri][:, kk * 128:(kk + 1) * 128], rhs=rhs, start=True, stop=not near),
                                     reads=[b_kch[ri], b_Qbd], writes=[spb])
                                if near:
                                    S.op("pe", lambda e: e.matmul(sv, lhsT=identb[:], rhs=biasb[:, h, delta, :], start=False, stop=True), reads=[CONST, cb[0]], writes=[spb])
                                    S.op("act", lambda e: e.activation(out=pex[px][:, 0:256 * nq], in_=sv, func=AF.Exp), reads=[spb], writes=[b_pex[px]])
                                else:
                                    S.op("act", lambda e: e.activation(out=pex[px][:, 0:256 * nq], in_=sv, func=AF.Exp, bias=farb[:, h:h + 1], scale=1.0),
                                         reads=[spb, cb[14]], writes=[b_pex[px]])

                            def emit_pv(it, idx):
                                ri, kk, kb, jqs, ci = it
                                px = idx % NPX
                                for a_, jq in enumerate(jqs):
                                    ab, abb = accs[jq]
                                    for m in range(2):
                                        c0 = a_ * 256 + m * 128
                                        S.op("pe", lambda e: e.matmul(ab[:, m * 130:(m + 1) * 130], lhsT=pex[px][:, c0:c0 + 128], rhs=vch[ri][:, kk, :],
                                                                     start=False, stop=False, skip_group_check=True), reads=[b_pex[px], b_vch[ri]], writes=[abb])

                            for idx in range(len(items) + LAG):
                                if idx < len(items):
                                    emit_scores(items[idx], idx)
                                if idx >= LAG:
                                    emit_pv(items[idx - LAG], idx - LAG)
                            for jq in range(4):
                                ab, abb = accs[jq]
                                av = ab[:, 0:260].rearrange("p (m c) -> p m c", m=2)
                                S.op("dve", lambda e: e.tensor_scalar(out=rr[:, 2 * jq:2 * jq + 2], in0=av[:, :, 128], scalar1=1e-30, scalar2=None, op0=ALU.max), reads=[abb], writes=[b_a])
                                S.op("dve", lambda e: e.reciprocal(out=rr[:, 2 * jq:2 * jq + 2], in_=rr[:, 2 * jq:2 * jq + 2]), reads=[b_a], writes=[b_a])
                                S.op("dve", lambda e: e.tensor_tensor(out=rr[:, 2 * jq + 1:2 * jq + 2], in0=rr[:, 2 * jq + 1:2 * jq + 2], in1=neglam[:], op=ALU.mult),
                                     reads=[b_a, CONST], writes=[b_a])
                                S.op("dve", lambda e: e.tensor_scalar(out=har[:, jq, :], in0=av[:, 0, 0:128], scalar1=rr[:, 2 * jq:2 * jq + 1], scalar2=None, op0=ALU.mult),
                                     reads=[abb, b_a], writes=[b_a])
                                S.op("dve", lambda e: e.scalar_tensor_tensor(out=har[:, jq, :], in0=av[:, 1, 0:128], scalar=rr[:, 2 * jq + 1:2 * jq + 2], in1=har[:, jq, :],
                                                                            op0=ALU.mult, op1=ALU.add), reads=[abb, b_a], writes=[b_a])
                            S.op("act", lambda e: e.activation(out=hsq[:], in_=har[:], func=AF.Square, scale=1.0 / math.sqrt(128.0)), reads=[b_a], writes=[b_a])
                            S.op("dve", lambda e: e.tensor_reduce(out=hss[:, 0:4], in_=hsq[:], axis=AX.X, op=ALU.add), reads=[b_a], writes=[b_a])
                            S.op("act", lambda e: e.activation(out=hss[:, 4:8], in_=hss[:, 0:4], func=AF.Ln, bias=epst[:, 0:1], scale=1.0), reads=[b_a, CONST], writes=[b_a])
                            S.op("act", lambda e: e.activation(out=hss[:, 4:8], in_=hss[:, 4:8], func=AF.Exp, scale=-0.5), reads=[b_a], writes=[b_a])
                            for jq in range(4):
                                S.op("dve", lambda e: e.scalar_tensor_tensor(out=haf[:, jq, :], in0=har[:, jq, :], scalar=hss[:, 4 + jq:5 + jq], in1=ang[:, h * 128:(h + 1) * 128],
                                                                            op0=ALU.mult, op1=ALU.mult), reads=[b_a, CONST] + cb, writes=[b_a])
                            tb, tbb = banks[6 + h % 2], bbufs[6 + h % 2]
                            tv = bfview(tb)[:, 0:512].rearrange("p (j t) -> p j t", j=4)
                            for jq in range(4):
                                S.op("pe", lambda e: e.transpose(out=tv[:, jq, :], in_=haf[:, jq, :], identity=identb[:]), reads=[b_a, cb[0]], writes=[tbb])
                            S.op("act", lambda e: e.copy(out=haT[:, h, :], in_=tv.rearrange("p j t -> p (j t)")), reads=[tbb], writes=[b_haT])
                        S.barrier()

                    h2T = hT
                    b_h2T = b_hT
                    with ExitStack() as st:
                        W = {}
                        for nm in ("bm", "gm0", "ba", "ga0", "gm1", "ga1", "out0", "out1"):
                            W[nm] = wload(st, "w_" + nm, nm)
                        yT = sbt(st, "yT", [128, 8, 512], BF16)
                        b_yT = S.buf("yT")
                        sg = [[sbt(st, "sg%d_%d" % (i, a), [128, 512]) for a in range(2)] for i in range(2)]
                        ty = [[sbt(st, "ty%d_%d" % (i, a), [128, 512]) for a in range(2)] for i in range(2)]
                        b_sg = [S.buf("sg%d" % i) for i in range(2)]
                        for c in range(8):
                            i = c % 2
                            res = []
                            for a, (bw, gw, srcT, b_src) in enumerate((("bm", "gm", hmT, b_hmT), ("ba", "ga", haT, b_haT))):
                                wt, wbf = W[bw]
                                wv = wt[:].rearrange("p (k n) -> p k n", k=4)
                                py, pyb = nbank()
                                for k in range(4):
                                    S.op("pe", lambda e: e.matmul(py[:], lhsT=wv[:, k, c * 128:(c + 1) * 128], rhs=srcT[:, k, :], start=(k == 0), stop=(k == 3)),
                                         reads=[wbf, b_src], writes=[pyb])
                                gt_, gbf = W[gw + str(c // 4)]
                                gv = gt_[:].rearrange("p (c n) -> p c n", c=8)
                                pg, pgb = nbank()
                                for dc in range(8):
                                    S.op("pe", lambda e: e.matmul(pg[:], lhsT=gv[:, dc, (c % 4) * 128:(c % 4 + 1) * 128], rhs=hT[:, dc, :], start=(dc == 0), stop=(dc == 7)),
                                         reads=[gbf, b_hT], writes=[pgb])
                                S.op("act", lambda e: e.activation(out=sg[i][a][:], in_=pg[:], func=AF.Sigmoid), reads=[pgb], writes=[b_sg[i]])
                                S.op("dve", lambda e: e.tensor_tensor(out=ty[i][a][:], in0=py[:], in1=sg[i][a][:], op=ALU.mult), reads=[pyb, b_sg[i]], writes=[b_sg[i]])
                            S.op("dve", lambda e: e.tensor_tensor(out=yT[:, c, :], in0=ty[i][0][:], in1=ty[i][1][:], op=ALU.add), reads=[b_sg[i]], writes=[b_yT])
                        tx = [sbt(st, "tx%d" % i, [128, 512]) for i in range(2)]
                        b_tx = [S.buf("tx%d" % i) for i in range(2)]
                        for j in range(4):
                            tsl = slice(j * 128, (j + 1) * 128)
                            for n in range(2):
                                wt, wbf = W["out%d" % n]
                                wv = wt[:].rearrange("p (c n) -> p c n", c=8)
                                po, pob = nbank()
                                for c in range(8):
                                    S.op("pe", lambda e: e.matmul(po[:], lhsT=yT[:, c, tsl], rhs=wv[:, c, :], start=(c == 0), stop=(c == 7)), reads=[wbf, b_yT], writes=[pob])
                                i = (j * 2 + n) % 2
                                S.op("dve", lambda e: e.tensor_tensor(out=tx[i][:], in0=po[:], in1=gate1[:, n * 512:(n + 1) * 512], op=ALU.mult), reads=[pob, CONST], writes=[b_tx[i]])
                                S.op("dve", lambda e: e.tensor_tensor(out=xs[:, j, n * 512:(n + 1) * 512], in0=xs[:, j, n * 512:(n + 1) * 512], in1=tx[i][:], op=ALU.add),
                                     reads=[b_tx[i], b_xs[j]], writes=[b_xs[j]])
                        for j in range(4):
                            norm_block(st, j, mult2, shift2, h2T, b_h2T, "n2")
                        S.barrier()

                    with ExitStack() as st:
                        actT = sbt(st, "actT", [128, NF, 512], BF16)
                        b_actT = S.buf("actT")
                        wu = [None, None, None]
                        uu = [sbt(st, "fu%d" % i, [128, 512]) for i in range(8)]
                        b_uu = [S.buf("fu%d" % i) for i in range(8)]
                        fe = [sbt(st, "fe%d" % i, [128, 512]) for i in range(2)]
                        b_fe = [S.buf("fe%d" % i) for i in range(2)]
                        wus = [sbt(st, "wup%d" % i, [128, 4096], BF16) for i in range(3)]
                        b_wus = [S.buf("wup%d" % i) for i in range(3)]

                        def ldup(jj):
                            off, n = PIECES["up%d" % jj]
                            S.dma("sp", wus[jj % 3][:], wb_d[:, off:off + n], reads=[b_wb], writes=[b_wus[jj % 3]])

                        ldup(0)
                        ldup(1)
                        wd, wdb = wload(st, "w_down", "down")
                        wdv = wd[:].rearrange("p (f n) -> p f n", f=NF)
                        def ffn_piece(jj):
                            if jj + 2 < 11:
                                ldup(jj + 2)
                            wv = wus[jj % 3][:].rearrange("p (c n) -> p c n", c=8)
                            wbf = b_wus[jj % 3]
                            for k in range(4):
                                q = jj * 4 + k
                                pb, pbb = nbank()
                                for dc in range(8):
                                    S.op("pe", lambda e: e.matmul(pb[:], lhsT=wv[:, dc, k * 128:(k + 1) * 128], rhs=h2T[:, dc, :], start=(dc == 0), stop=(dc == 7)),
                                         reads=[wbf, b_h2T], writes=[pbb])
                                kk_ = (jj % 2) * 4 + k
                                wsel = fcwf if flagged else fcw
                                S.op("act", lambda e: e.activation(out=uu[kk_][:], in_=pb[:], func=AF.Identity, scale=wsel[:, q, 2:3], bias=fcb[:, q:q + 1]),
                                     reads=[pbb, CONST] + cb, writes=[b_uu[kk_]])
                                for tp in range(2):
                                    sh = 2 - tp
                                    S.op("dve", lambda e: e.scalar_tensor_tensor(out=uu[kk_][:, sh:512], in0=pb[:, 0:512 - sh], scalar=wsel[:, q, tp:tp + 1], in1=uu[kk_][:, sh:512],
                                                                                op0=ALU.mult, op1=ALU.add), reads=[pbb, b_uu[kk_], CONST], writes=[b_uu[kk_]])
                                for tp in range(2):
                                    sh = 2 - tp
                                    S.op("dve", lambda e: e.scalar_tensor_tensor(out=uu[kk_][:, 0:sh], in0=fhalo[:, q, 2 - sh:2], scalar=fcw[:, q, tp:tp + 1], in1=uu[kk_][:, 0:sh],
                                                                                op0=ALU.mult, op1=ALU.add), reads=[b_fhalo, b_uu[kk_]] + cb, writes=[b_uu[kk_]])
                                if flagged:
                                    S.op("act", lambda e: e.activation(out=fhalo[:, q, :], in_=pb[:, 510:512], func=AF.Copy, scale=flag[:, 0:1]), reads=[pbb, cb[4]], writes=[b_fhalo])
                                else:
                                    S.op("act", lambda e: e.copy(out=fhalo[:, q, :], in_=pb[:, 510:512]), reads=[pbb], writes=[b_fhalo])
                            yield
                            for i in range(2):
                                f = 2 * jj + i
                                uv = uu[(jj % 2) * 4 + i]
                                ug = uu[(jj % 2) * 4 + 2 + i]
                                b_uv = b_uu[(jj % 2) * 4 + i]
                                b_ug = b_uu[(jj % 2) * 4 + 2 + i]
                                S.op("act", lambda e: e.activation(out=fe[i][:], in_=ug[:], func=AF.Silu), reads=[b_ug], writes=[b_fe[i]])
                                S.op("dve", lambda e: e.tensor_tensor(out=actT[:, f, :], in0=uv[:], in1=fe[i][:], op=ALU.mult), reads=[b_uv, b_fe[i]], writes=[b_actT])
                        fgens = [ffn_piece(jj) for jj in range(11)]
                        for step in range(12):
                            if step < 11:
                                next(fgens[step], None)
                            if step >= 1:
                                next(fgens[step - 1], None)
                        tx = fe
                        b_tx = b_fe
                        for j in range(4):
                            tsl = slice(j * 128, (j + 1) * 128)
                            blk = u * 4 + j
                            for n in range(2):
                                po, pob = nbank()
                                for f in range(NF):
                                    S.op("pe", lambda e: e.matmul(po[:], lhsT=actT[:, f, tsl], rhs=wdv[:, f, n * 512:(n + 1) * 512], start=(f == 0), stop=(f == NF - 1)),
                                         reads=[wdb, b_actT], writes=[pob])
                                i = (j * 2 + n) % 2
                                S.op("dve", lambda e: e.tensor_tensor(out=tx[i][:], in0=po[:], in1=gate2[:, n * 512:(n + 1) * 512], op=ALU.mult), reads=[pob, CONST], writes=[b_tx[i]])
                                S.op("dve", lambda e: e.tensor_tensor(out=xs[:, j, n * 512:(n + 1) * 512], in0=xs[:, j, n * 512:(n + 1) * 512], in1=tx[i][:], op=ALU.add),
                                     reads=[b_tx[i], b_xs[j]], writes=[b_xs[j]])
                            if own:
                                ob = blk - NFLAG * 4
                                S.dma("pool", out_d[ob * 128:(ob + 1) * 128, :], xs[:, j, :], reads=[b_xs[j]], writes=[b_out])
                        S.barrier()
        S.barrier()
    return nc


def _t5_bucket(n):
    n = np.maximum(n, 0)
    max_exact = 16
    nf = np.maximum(n, 1).astype(np.float32)
    large = max_exact + (np.log(nf / np.float32(max_exact)) / np.float32(math.log(128 / max_exact)) * np.float32(32 - max_exact)).astype(np.int32)
    large = np.minimum(large, 31)
    return np.where(n < max_exact, n, large)


def _fm(v, nch):
    return np.ascontiguousarray(np.asarray(v, np.float32).reshape(nch, 128).T)


def _piece_fm(w):
    n = w.shape[1]
    return w.reshape(8, 128, n).transpose(1, 0, 2).reshape(128, 8 * n)


def prepare_inputs(NCTX, NFULL, inputs):
    f32 = np.float32
    g = {k: np.asarray(v) for k, v in inputs.items()}
    NU = NCTX + NFULL
    half_tok = (NU // 2) * 512
    x = g["x"].astype(f32, copy=False)
    B = x.shape[0]
    assert x.shape[1] == 2 * half_tok
    w_in = g["w_in"][0]
    cols = {}
    o = 0
    for nm, n in (("mqk", 1024), ("mv", 512), ("mo", 512), ("mi", 4), ("mf", 4), ("aq", 512), ("ak", 512), ("av", 512), ("gm", 1024), ("ga", 1024)):
        cols[nm] = w_in[:, o:o + n]
        o += n

    def perm_qk(w):
        return w.reshape(1024, 2, 4, 64).transpose(0, 2, 1, 3).reshape(1024, 512)

    wall = np.zeros((128, WPAD), f32)

    def put(nm, arr):
        off, n = PIECES[nm]
        assert arr.shape == (128, n), (nm, arr.shape, n)
        wall[:, off:off + n] = arr

    put("mq", _piece_fm(cols["mqk"][:, 0:512]))
    put("mk", _piece_fm(cols["mqk"][:, 512:1024]))
    put("mv", _piece_fm(cols["mv"]))
    put("mo", _piece_fm(cols["mo"]))
    put("aq", _piece_fm(perm_qk(cols["aq"])))
    put("ak", _piece_fm(perm_qk(cols["ak"])))
    put("av", _piece_fm(cols["av"]))
    put("gm0", _piece_fm(cols["gm"][:, 0:512]))
    put("gm1", _piece_fm(cols["gm"][:, 512:1024]))
    put("ga0", _piece_fm(cols["ga"][:, 0:512]))
    put("ga1", _piece_fm(cols["ga"][:, 512:1024]))
    put("gif", _piece_fm(np.concatenate([cols["mi"], cols["mf"]], axis=1)))
    put("bm", g["w_branch_m"][0].reshape(4, 128, 1024).transpose(1, 0, 2).reshape(128, 4096))
    put("ba", g["w_branch_a"][0].reshape(4, 128, 1024).transpose(1, 0, 2).reshape(128, 4096))
    put("out0", _piece_fm(g["w_out"][0][:, 0:512]))
    put("out1", _piece_fm(g["w_out"][0][:, 512:1024]))
    w_up = g["w_up"][0]
    fcw_full = g["ffn_conv_w"][0]
    fcb_full = g["ffn_conv_b"][0]
    chunk_cols = []
    for jj in range(11):
        cc = [np.arange((2 * jj + i) * 128, (2 * jj + i + 1) * 128) for i in range(2)]
        cc += [DFF + np.arange((2 * jj + i) * 128, (2 * jj + i + 1) * 128) for i in range(2)]
        idx = np.concatenate(cc)
        chunk_cols.append(idx)
        put("up%d" % jj, _piece_fm(w_up[:, idx]))
    allidx = np.concatenate(chunk_cols)
    fcw = fcw_full[:, allidx].reshape(3, 44, 128).transpose(2, 1, 0).reshape(128, 44 * 3)
    fcb = fcb_full[allidx].reshape(44, 128).T
    put("down", g["w_down"][0].reshape(NF, 128, 1024).transpose(1, 0, 2).reshape(128, NF * 1024))

    w_ada = g["w_ada"][0]
    wada = np.stack([_piece_fm(w_ada[:, p * 512:(p + 1) * 512]) for p in range(12)]).astype(f32)
    bada = g["b_ada"][0].astype(f32)
    mcw = g["m_conv_w"][0].reshape(4, 8, 128).transpose(2, 1, 0).reshape(128, 32)
    mcb = _fm(g["m_conv_b"][0], 8)
    gifb = np.concatenate([g["m_igate_b"][0], g["m_fgate_b"][0]]).astype(f32)
    gq = np.tile(g["a_qnorm_g"][0], 8).astype(f32)
    gk = np.tile(g["a_knorm_g"][0], 8).astype(f32)
    rel = g["rel_bias"].astype(f32)
    kk = np.arange(128)[:, None]
    qq = np.arange(128)[None, :]
    biasg = np.zeros((128, 4, 2, 2, 128), f32)
    maskn = np.zeros((128, 4, 2, 2, 128), f32)
    for dl in range(2):
        dist = qq - kk + 128 * dl
        bidx = _t5_bucket(dist)
        for h in range(4):
            t = rel[bidx, h]
            biasg[:, h, dl, 0, :] = t
            biasg[:, h, dl, 1, :] = t
        if dl == 0:
            mk = np.where(dist < 0, NEG, 0.0).astype(f32)
            maskn[:, :, 0, :, :] = mk[:, None, None, :]
    farb = rel[31, :].astype(f32)
    umask = (kk <= qq).astype(f32)
    negm4 = np.tile(np.where(kk <= qq, 0.0, NEG).astype(f32), (1, 4))
    import ml_dtypes
    common = dict(
        wada=wada, bada=bada, badafm=_fm(bada, 48), g1fm=_fm(g["norm1_g"][0], 8), g2fm=_fm(g["norm2_g"][0], 8),
        wall=wall, mcw=np.ascontiguousarray(mcw, f32), mcb=mcb, fcw=np.ascontiguousarray(fcw, f32), fcb=np.ascontiguousarray(fcb, f32),
        gifb=gifb, mng=g["m_norm_g"][0].astype(f32), ang=g["a_norm_g"][0].astype(f32), gq=gq, gk=gk,
        alam=g["a_lambda"][0].reshape(256).astype(f32), biasg=biasg.reshape(128, -1), maskn=maskn.reshape(128, -1), farb=farb,
        identb=np.eye(128).astype(ml_dtypes.bfloat16), identf=np.eye(128, dtype=f32), umask=umask, negm4=negm4,
    )
    in_maps = []
    for b in range(B):
        cfm = _fm(g["c"][b], 8)
        for hf in range(2):
            if hf == 0:
                xl = np.concatenate([np.zeros((half_tok, D), f32), x[b, 0:half_tok]], axis=0)
                fl = np.zeros((128, 1), f32)
            else:
                xl = x[b]
                fl = np.ones((128, 1), f32)
            m = dict(common)
            m["x"] = np.ascontiguousarray(xl)
            m["cfm"] = cfm
            m["flag"] = fl
            in_maps.append(m)
    return in_maps


_NC_CACHE = {}


def run(NCTX, NFULL, inputs):
    key = (NCTX, NFULL)
    if key not in _NC_CACHE:
        _NC_CACHE[key] = build_program(NCTX, NFULL)
    nc = _NC_CACHE[key]
    in_maps = prepare_inputs(NCTX, NFULL, inputs)
    res = run_bass_kernel_spmd(nc, in_maps, core_ids=list(range(len(in_maps))))
    B = len(in_maps) // 2
    half_tok = ((NCTX + NFULL) // 2) * 512
    out = np.empty((B, 2 * half_tok, D), np.float32)
    for b in range(B):
        for hf in range(2):
            out[b, hf * half_tok:(hf + 1) * half_tok] = res.results[b * 2 + hf]["out"]
    return out


def kernel(**inputs):
    return run(7, 9, inputs)
```

```python
import math
from contextlib import ExitStack

import numpy as np

import concourse.bass as bass
import concourse.mybir as mybir
from concourse.bass_utils import run_bass_kernel_spmd

F32 = mybir.dt.float32
BF16 = mybir.dt.bfloat16
AF = mybir.ActivationFunctionType
ALU = mybir.AluOpType
AX = mybir.AxisListType

D = 1024
DC = 8
DFF = 2816
NF = 22
EPS = 1e-6
LAM_INIT = 0.8 - 0.6 * math.exp(-0.3 * 0)
NEG = -30000.0
OPT_SELF_WAR = True
OPT_XN_ACT = True
OPT_SILU = True
OPT_GFOLD = True
OPT_TINY = True
OPT_BATCH = True
OPT_PARTMAIN = True

PIECES = {}
_off = 0
for _nm, _n in (("mq", 4096), ("mk", 4096), ("mv", 4096), ("mo", 4096), ("aq", 4096), ("ak", 4096),
                ("av", 4096), ("gm0", 4096), ("gm1", 4096), ("ga0", 4096), ("ga1", 4096), ("gif", 64),
                ("bm", 4096), ("ba", 4096), ("out0", 4096), ("out1", 4096)):
    PIECES[_nm] = (_off, _n)
    _off += _n
for _j in range(11):
    PIECES["up%d" % _j] = (_off, 4096)
    _off += 4096
PIECES["down"] = (_off, NF * 1024)
_off += NF * 1024
WTOT = _off
WPAD = ((WTOT + 4095) // 4096) * 4096


class Buf:
    __slots__ = ("name", "w", "rs", "sem", "semv", "grp", "psum")

    def __init__(self, name, grp=None):
        self.psum = False
        self.name = name
        self.w = None
        self.rs = []
        self.sem = None
        self.semv = 0
        self.grp = grp


class Sched:
    def __init__(self, nc, stack):
        self.nc = nc
        self.stack = stack
        self.engs = {}
        for nm, h in (("pe", nc.tensor), ("act", nc.scalar), ("dve", nc.vector), ("pool", nc.gpsimd), ("sp", nc.sync)):
            sem = stack.enter_context(nc.semaphore("s_" + nm))
            self.engs[nm] = dict(h=h, sem=sem, cnt=0, seen={})
        self.groups = {}
        self.dma_ev = {}
        self.free_sems = {}
        self.live = []
        self.nsem = 0

    def buf(self, name, grp=None):
        return Buf(name, grp)

    def _need(self, e, deps):
        E = self.engs[e]
        best = {}
        for d in deps:
            if d is None:
                continue
            sem, val, en = d
            if en == e and e == "pe":
                continue
            k = id(sem)
            if k not in best or best[k][1] < val:
                best[k] = (sem, val)
        for k, (sem, val) in best.items():
            if E["seen"].get(k, 0) >= val:
                continue
            E["h"].wait_ge(sem, val)
            E["seen"][k] = val

    def op(self, e, fn, reads=(), writes=()):
        E = self.engs[e]
        deps = []
        for b in reads:
            deps.append(b.w)
            if b.psum:
                for r in b.rs:
                    if r[2] != e:
                        deps.append(r)
        for b in writes:
            if b.w is not None and b.w[2] != e:
                deps.append(b.w)
            for r in b.rs:
                if OPT_SELF_WAR or r[2] != e:
                    deps.append(r)
        self._need(e, deps)
        ins = fn(E["h"])
        E["cnt"] += 1
        ins.then_inc(E["sem"], 1)
        ev = (E["sem"], E["cnt"], e)
        for b in reads:
            b.rs.append(ev)
        for b in writes:
            b.w = ev
            b.rs = []
        return ins

    def dma(self, q, out, in_, reads=(), writes=()):
        E = self.engs[q]
        deps = []
        for b in reads:
            deps.append(b.w)
        for b in writes:
            if b.grp in ("wbw", "kvw", "outw"):
                continue
            deps.append(b.w)
            deps.extend(b.rs)
        self._need(q, deps)
        ins = E["h"].dma_start(out=out, in_=in_)
        cands = list(writes) + list(reads)
        tgt = ([b for b in cands if b.grp is None] + cands)[0]
        if tgt.grp is not None:
            gk = tgt.grp
            if gk not in self.groups:
                self.groups[gk] = [self.stack.enter_context(self.nc.semaphore("g_" + gk)), 0]
            g = self.groups[gk]
            g[1] += 16
            sem, val = g[0], g[1]
        else:
            if tgt.sem is None:
                tgt.sem = {}
                self.live.append(tgt)
            if q not in tgt.sem:
                fs = self.free_sems.setdefault(q, [])
                if not fs:
                    self.nsem += 1
                    fs.append([self.stack.enter_context(self.nc.semaphore("dp%d" % self.nsem)), 0])
                tgt.sem[q] = fs.pop()
            ent = tgt.sem[q]
            ent[1] += 16
            sem, val = ent[0], ent[1]
        ins.then_inc(sem, 16)
        self.dma_ev[id(sem)] = (sem, val)
        ev = (sem, val, "dma")
        for b in reads:
            b.rs.append(ev)
        for b in writes:
            b.w = ev
            b.rs = []
        return ins

    def barrier(self, engines=("pe", "act", "dve", "pool", "sp")):
        deps = [(E["sem"], E["cnt"], "x") for E in self.engs.values() if E["cnt"] > 0]
        deps += [(s, v, "dma") for (s, v) in self.dma_ev.values()]
        for e in engines:
            self._need(e, deps)
        self.dma_ev = {}
        for b in self.live:
            for qq, ent in b.sem.items():
                self.free_sems[qq].append(ent)
            b.sem = None
        self.live = []

    def seal(self, bufs, grp):
        g = self.groups[grp]
        for b in bufs:
            b.w = (g[0], g[1], "dma")


def build_program(NCTX, NFULL, debug=False):
    NU = NCTX + NFULL
    NFLAG = NU // 2
    NBLK = NU * 4
    NTOK = NU * 512
    NOWN = (NFULL - 1) * 512
    assert NU == 2 * (NFULL - 1)

    nc = bass.Bass("TRN2", target_bir_lowering=False)

    def din(name, shape, dt=F32):
        return nc.dram_tensor(name, list(shape), dt, kind="ExternalInput").ap()

    x_d = din("x", [NTOK, D])
    cfm_d = din("cfm", [128, 8])
    wada_d = din("wada", [12, 128, 8 * 512])
    bada_d = din("bada", [6144])
    badafm_d = din("badafm", [128, 48])
    g1fm_d = din("g1fm", [128, 8])
    g2fm_d = din("g2fm", [128, 8])
    wall_d = din("wall", [128, WPAD])
    mcw_d = din("mcw", [128, 8 * 4])
    mcb_d = din("mcb", [128, 8])
    fcw_d = din("fcw", [128, 44 * 3])
    fcb_d = din("fcb", [128, 44])
    gifb_d = din("gifb", [8])
    mng_d = din("mng", [512])
    ang_d = din("ang", [512])
    gq_d = din("gq", [512])
    gk_d = din("gk", [512])
    gqfm_d = din("gqfm", [128, 1])
    gkfm_d = din("gkfm", [128, 1])
    alam_d = din("alam", [256])
    biasg_d = din("biasg", [128, 4 * 2 * 2 * 128])
    maskn_d = din("maskn", [128, 4 * 2 * 2 * 128])
    farb_d = din("farb", [4])
    flag_d = din("flag", [128, 1])
    identb_d = din("identb", [128, 128], BF16)
    identf_d = din("identf", [128, 128])
    umask_d = din("umask", [128, 128])
    negm4_d = din("negm4", [128, 512])
    out_d = nc.dram_tensor("out", [NOWN, D], F32, kind="ExternalOutput").ap()
    wb_d = nc.dram_tensor("wb_scr", [128, WPAD], BF16, kind="Internal").ap()
    kt_d = nc.dram_tensor("kt_scr", [4, 128, NBLK * 128], BF16, kind="Internal").ap()
    va_d = nc.dram_tensor("va_scr", [4, 128, NBLK * 130], BF16, kind="Internal").ap()

    with ExitStack() as top:
        S = Sched(nc, top)

        uid = [0]

        def sbt(st, name, shape, dt=F32):
            uid[0] += 1
            return st.enter_context(nc.sbuf_tensor("s%d_%s" % (uid[0], name), list(shape), dt))

        banks = [top.enter_context(nc.psum_tensor("bank%d" % i, [128, 512], F32)) for i in range(8)]
        bbufs = [S.buf("bank%d" % i) for i in range(8)]
        for b_ in bbufs:
            b_.psum = True
        bank_rr = [0]

        def nbank(lo=0, hi=8):
            i = lo + (bank_rr[0] % (hi - lo))
            bank_rr[0] += 1
            return banks[i], bbufs[i]

        def bfview(bank):
            return bank[:].bitcast(BF16)

        identb = sbt(top, "identb", [128, 128], BF16)
        identf = sbt(top, "identf", [128, 128])
        umask = sbt(top, "umask", [128, 128])
        negm4 = sbt(top, "negm4", [128, 512])
        ones_f = sbt(top, "ones_f", [128, 128])
        flag = sbt(top, "flag", [128, 1])
        epst = sbt(top, "epst", [128, 1])
        onet = sbt(top, "onet", [128, 1])
        mult1 = sbt(top, "mult1", [128, 8])
        shift1 = sbt(top, "shift1", [128, 8])
        mult2 = sbt(top, "mult2", [128, 8])
        shift2 = sbt(top, "shift2", [128, 8])
        gate1 = sbt(top, "gate1", [128, D])
        gate2 = sbt(top, "gate2", [128, D])
        mcw = sbt(top, "mcw", [128, 8, 4])
        mcb = sbt(top, "mcb", [128, 8])
        mcwf = sbt(top, "mcwf", [128, 8, 4])
        fcwf = sbt(top, "fcwf", [128, 44, 3])
        mcnb = sbt(top, "mcnb", [128, 8])
        fcw = sbt(top, "fcw", [128, 44, 3])
        fcb = sbt(top, "fcb", [128, 44])
        fcnb = sbt(top, "fcnb", [128, 44])
        gifb = sbt(top, "gifb", [128, 8])
        mng = sbt(top, "mng", [128, 512])
        ang = sbt(top, "ang", [128, 512])
        gq = sbt(top, "gq", [128, 512])
        gk = sbt(top, "gk", [128, 512])
        gqfm = sbt(top, "gqfm", [128, 1])
        gkfm = sbt(top, "gkfm", [128, 1])
        mbc = sbt(top, "mbc", [128, 8, 3])
        fbc = sbt(top, "fbc", [128, 44, 2])
        ctmp = sbt(top, "ctmp", [128, 44])
        b_mbc = S.buf("mbc")
        b_fbc = S.buf("fbc")
        biasb = sbt(top, "biasb", [128, 4, 2, 256], BF16)
        farb = sbt(top, "farb", [128, 4])
        neglam = sbt(top, "neglam", [128, 1])
        Sst = sbt(top, "Sst", [128, 4, 130])
        Sstb = sbt(top, "Sstb", [128, 4, 130], BF16)
        mhalo = sbt(top, "mhalo", [128, 8, 3])
        fhalo = sbt(top, "fhalo", [128, 44, 2])
        CONST = S.buf("const")
        b_S = S.buf("Sst")
        b_Sb = S.buf("Sstb")
        b_mhalo = S.buf("mhalo")
        b_fhalo = S.buf("fhalo")
        b_wb = S.buf("wb_scr", grp="wbw")
        b_ktd = S.buf("kt_scr", grp="kvw")
        b_vad = S.buf("va_scr", grp="kvw")
        b_out = S.buf("outd", grp="outw")

        def ld_const(t, src, grp="cst"):
            b = S.buf("c_" + t.name, grp=grp)
            S.dma("sp", t[:], src, writes=[b])
            return b

        cb = []
        cb.append(ld_const(identb, identb_d[:, :]))
        cb.append(ld_const(identf, identf_d[:, :]))
        cb.append(ld_const(umask, umask_d[:, :]))
        cb.append(ld_const(negm4, negm4_d[:, :]))
        cb.append(ld_const(flag, flag_d[:, :]))
        cb.append(ld_const(mcw, mcw_d.rearrange("p (c k) -> p c k", k=4)))
        cb.append(ld_const(mcb, mcb_d[:, :]))
        cb.append(ld_const(fcw, fcw_d.rearrange("p (c k) -> p c k", k=3)))
        cb.append(ld_const(fcb, fcb_d[:, :]))
        cb.append(ld_const(gifb, gifb_d.partition_broadcast(128)))
        cb.append(ld_const(mng, mng_d.partition_broadcast(128)))
        cb.append(ld_const(ang, ang_d.partition_broadcast(128)))
        cb.append(ld_const(gq, gq_d.partition_broadcast(128)))
        cb.append(ld_const(gk, gk_d.partition_broadcast(128)))
        cb.append(ld_const(farb, farb_d.partition_broadcast(128)))
        cb.append(ld_const(gqfm, gqfm_d[:, :]))
        cb.append(ld_const(gkfm, gkfm_d[:, :]))
        S.seal(cb, "cst")
        S.op("dve", lambda e: e.memset(ones_f[:], 1.0), writes=[CONST])
        S.op("dve", lambda e: e.memset(epst[:], EPS), writes=[CONST])
        S.op("dve", lambda e: e.memset(onet[:], 1.0), writes=[CONST])
        S.op("dve", lambda e: e.memset(Sst[:], 0.0), writes=[b_S])
        S.op("dve", lambda e: e.memset(Sstb[:], 0.0), writes=[b_Sb])
        S.op("dve", lambda e: e.memset(mhalo[:], 0.0), writes=[b_mhalo])
        S.op("dve", lambda e: e.memset(fhalo[:], 0.0), writes=[b_fhalo])
        S.op("dve", lambda e: e.tensor_scalar(out=mcnb[:], in0=mcb[:], scalar1=-1.0, scalar2=None, op0=ALU.mult), reads=cb, writes=[CONST])
        S.op("dve", lambda e: e.tensor_scalar(out=fcnb[:], in0=fcb[:], scalar1=-1.0, scalar2=None, op0=ALU.mult), reads=cb, writes=[CONST])
        S.op("dve", lambda e: e.tensor_scalar(out=gqfm[:], in0=gqfm[:], scalar1=0.125, scalar2=None, op0=ALU.mult), reads=cb, writes=[CONST])
        S.op("dve", lambda e: e.tensor_scalar(out=mcwf[:], in0=mcw[:], scalar1=flag[:, 0:1], scalar2=None, op0=ALU.mult), reads=cb, writes=[CONST])
        S.op("dve", lambda e: e.tensor_scalar(out=fcwf[:], in0=fcw[:], scalar1=flag[:, 0:1], scalar2=None, op0=ALU.mult), reads=cb, writes=[CONST])
        S.op("dve", lambda e: e.tensor_scalar(out=ang[:], in0=ang[:], scalar1=1.0 - LAM_INIT, scalar2=None, op0=ALU.mult), reads=cb, writes=[CONST])

        with ExitStack() as st:
            cfm = sbt(st, "cfm", [128, 8])
            sc = sbt(st, "sc", [128, 8])
            sct = sbt(st, "sct", [128, 8])
            scbc = sbt(st, "scbc", [128, 8, 128])
            badafm = sbt(st, "badafm", [128, 48])
            g1fm = sbt(st, "g1fm", [128, 8])
            g2fm = sbt(st, "g2fm", [128, 8])
            modfm = sbt(st, "modfm", [128, 48])
            alam = sbt(st, "alam", [128, 256])
            lt = sbt(st, "lt", [128, 128])
            ls = sbt(st, "ls", [128, 2])
            biasg = sbt(st, "biasg", [128, 2048])
            maskn = sbt(st, "maskn", [128, 2048])
            wst = [sbt(st, "wst%d" % i, [128, 4096]) for i in range(2)]
            wbo = [sbt(st, "wbo%d" % i, [128, 4096], BF16) for i in range(2)]
            bbt = [sbt(st, "bbt%d" % i, [128, 512]) for i in range(2)]
            b_wst = [S.buf("wst%d" % i) for i in range(2)]
            b_wbo = [S.buf("wbo%d" % i) for i in range(2)]
            b_bbt = [S.buf("bbt%d" % i) for i in range(2)]
            L = S.buf("prel")
            b_l = [ld_const(cfm, cfm_d[:, :], "cst2"), ld_const(badafm, badafm_d[:, :], "cst2"), ld_const(g1fm, g1fm_d[:, :], "cst2"),
                   ld_const(g2fm, g2fm_d[:, :], "cst2"), ld_const(alam, alam_d.partition_broadcast(128), "cst2"),
                   ld_const(biasg, biasg_d[:, :], "cst2"), ld_const(maskn, maskn_d[:, :], "cst2")]
            S.seal(b_l, "cst2")
            S.op("act", lambda e: e.activation(out=sct[:], in_=cfm[:], func=AF.Exp, scale=-1.0), reads=b_l, writes=[L])
            S.op("dve", lambda e: e.tensor_scalar(out=sct[:], in0=sct[:], scalar1=1.0, scalar2=None, op0=ALU.add), reads=[L], writes=[L])
            S.op("dve", lambda e: e.reciprocal(out=sct[:], in_=sct[:]), reads=[L], writes=[L])
            S.op("dve", lambda e: e.tensor_tensor(out=sc[:], in0=cfm[:], in1=sct[:], op=ALU.mult), reads=[L], writes=[L])
            S.op("dve", lambda e: e.tensor_copy(out=scbc[:], in_=sc[:].unsqueeze(2).to_broadcast([128, 8, 128])), reads=[L], writes=[L])
            S.op("dve", lambda e: e.tensor_tensor(out=lt[:, 0:64], in0=alam[:, 0:64], in1=alam[:, 64:128], op=ALU.mult), reads=b_l, writes=[L])
            S.op("dve", lambda e: e.tensor_tensor(out=lt[:, 64:128], in0=alam[:, 128:192], in1=alam[:, 192:256], op=ALU.mult), reads=[L], writes=[L])
            S.op("dve", lambda e: e.tensor_reduce(out=ls[:], in_=lt[:].rearrange("p (a b) -> p a b", a=2), axis=AX.X, op=ALU.add), reads=[L], writes=[L])
            S.op("act", lambda e: e.activation(out=ls[:], in_=ls[:], func=AF.Exp), reads=[L], writes=[L])
            S.op("dve", lambda e: e.tensor_tensor(out=neglam[:], in0=ls[:, 1:2], in1=ls[:, 0:1], op=ALU.subtract), reads=[L], writes=[CONST])
            S.op("dve", lambda e: e.tensor_scalar(out=neglam[:], in0=neglam[:], scalar1=-LAM_INIT, scalar2=None, op0=ALU.add), reads=[CONST], writes=[CONST])
            S.op("dve", lambda e: e.tensor_tensor(out=biasb[:].rearrange("p a b c -> p (a b c)"), in0=biasg[:], in1=maskn[:], op=ALU.add), reads=b_l, writes=[CONST])

            fm_ps, fm_b = banks[7], bbufs[7]
            fm_cols = {0: 0, 1: 4, 2: 8, 3: 12, 6: 24, 7: 28, 8: 32, 9: 36}
            for pc in range(12):
                i = pc % 2
                S.dma("sp", wst[i][:], wada_d[pc, :, :], writes=[b_wst[i]])
                wv = wst[i][:].rearrange("p (c n) -> p c n", c=8)
                if pc in (4, 5, 10, 11):
                    S.dma("sp", bbt[i][:], bada_d[pc * 512:(pc + 1) * 512].partition_broadcast(128), writes=[b_bbt[i]])
                    pb, pbb = nbank(0, 6)
                    for dc in range(8):
                        S.op("pe", lambda e: e.matmul(pb[:], lhsT=scbc[:, dc, :], rhs=wv[:, dc, :], start=(dc == 0), stop=(dc == 7)),
                             reads=[L, b_wst[i]], writes=[pbb])
                    gt = gate1 if pc < 6 else gate2
                    off = (pc % 2) * 512
                    S.op("dve", lambda e: e.tensor_tensor(out=gt[:, off:off + 512], in0=pb[:], in1=bbt[i][:], op=ALU.add),
                         reads=[pbb, b_bbt[i]], writes=[CONST])
                else:
                    for k in range(4):
                        col = fm_cols[pc] + k
                        for dc in range(8):
                            S.op("pe", lambda e: e.matmul(fm_ps[:, col:col + 1], lhsT=wv[:, dc, k * 128:(k + 1) * 128], rhs=sc[:, dc:dc + 1],
                                                         start=(dc == 0), stop=(dc == 7)), reads=[L, b_wst[i]], writes=[fm_b])
            S.op("dve", lambda e: e.tensor_tensor(out=modfm[:, 0:16], in0=fm_ps[:, 0:16], in1=badafm[:, 0:16], op=ALU.add), reads=[fm_b] + b_l, writes=[L])
            S.op("dve", lambda e: e.tensor_tensor(out=modfm[:, 24:40], in0=fm_ps[:, 24:40], in1=badafm[:, 24:40], op=ALU.add), reads=[fm_b] + b_l, writes=[L])
            S.op("dve", lambda e: e.scalar_tensor_tensor(out=mult1[:], in0=modfm[:, 8:16], scalar=1.0, in1=g1fm[:], op0=ALU.add, op1=ALU.mult), reads=[L], writes=[CONST])
            S.op("dve", lambda e: e.tensor_copy(out=shift1[:], in_=modfm[:, 0:8]), reads=[L], writes=[CONST])
            S.op("dve", lambda e: e.scalar_tensor_tensor(out=mult2[:], in0=modfm[:, 32:40], scalar=1.0, in1=g2fm[:], op0=ALU.add, op1=ALU.mult), reads=[L], writes=[CONST])
            S.op("dve", lambda e: e.tensor_copy(out=shift2[:], in_=modfm[:, 24:32]), reads=[L], writes=[CONST])

            cast_eng = ["dve", "act"]
            for ch in range(WPAD // 4096):
                i = ch % 2
                S.dma("sp", wst[i][:], wall_d[:, ch * 4096:(ch + 1) * 4096], writes=[b_wst[i]])
                ce = cast_eng[ch % 2]
                if ce == "act":
                    S.op("act", lambda e: e.copy(out=wbo[i][:], in_=wst[i][:]), reads=[b_wst[i]], writes=[b_wbo[i]])
                else:
                    S.op(ce, lambda e: e.tensor_copy(out=wbo[i][:], in_=wst[i][:]), reads=[b_wst[i]], writes=[b_wbo[i]])
                S.dma("pool", wb_d[:, ch * 4096:(ch + 1) * 4096], wbo[i][:], reads=[b_wbo[i]], writes=[b_wb])
            S.barrier()

        def wload(st, name, piece, shape3=None):
            off, n = PIECES[piece]
            t = sbt(st, name, [128, n], BF16)
            b = S.buf(name)
            S.dma("sp", t[:], wb_d[:, off:off + n], reads=[b_wb], writes=[b])
            return t, b

        for u in range(NU):
            full = u >= NCTX
            flagged = u < NFLAG
            own = u >= NFLAG
            with ExitStack() as su:
                xs = sbt(su, "xs", [128, 4, D])
                hT = sbt(su, "hT", [128, 8, 512], BF16)
                mqT = sbt(su, "mqT", [128, 8, 512], BF16)
                mVA = sbt(su, "mVA", [128, 4, 4, 130], BF16)
                sigmo = sbt(su, "sigmo", [128, 4, 512])
                gif = sbt(su, "gif", [128, 4, 8])
                Qbd = sbt(su, "Qbd", [128, 4, 4, 256], BF16)
                hmT = sbt(su, "hmT", [128, 4, 512], BF16)
                haT = sbt(su, "haT", [128, 4, 512], BF16)
                b_xs = [S.buf("xs%d" % j) for j in range(4)]
                b_hT = S.buf("hT")
                b_mqT = S.buf("mqT")
                b_mVA = S.buf("mVA")
                b_sig = S.buf("sigmo")
                b_gif = S.buf("gif")
                b_Qbd = S.buf("Qbd")
                b_hmT = S.buf("hmT")
                b_haT = S.buf("haT")
                if full:
                    S.op("dve", lambda e: e.memset(Qbd[:], 0.0), writes=[b_Qbd])

                def norm_block(st, j, mult, shift, hdst, b_hdst, tagp):
                    junk = sbt(st, tagp + "junk%d" % j, [128, D], BF16)
                    xn = sbt(st, tagp + "xn%d" % j, [128, D], BF16)
                    ss = sbt(st, tagp + "ss%d" % j, [128, 2])
                    bl = S.buf("nb")
                    S.op("act", lambda e: e.activation(out=junk[:], in_=xs[:, j, :], func=AF.Square, scale=1.0 / 32.0, accum_out=ss[:, 0:1]),
                         reads=[b_xs[j]], writes=[bl])
                    S.op("act", lambda e: e.activation(out=ss[:, 1:2], in_=ss[:, 0:1], func=AF.Ln, bias=epst[:, 0:1], scale=1.0), reads=[bl, CONST], writes=[bl])
                    S.op("act", lambda e: e.activation(out=ss[:, 1:2], in_=ss[:, 1:2], func=AF.Exp, scale=-0.5), reads=[bl], writes=[bl])
                    if OPT_XN_ACT:
                        S.op("act", lambda e: e.activation(out=xn[:], in_=xs[:, j, :], func=AF.Copy, scale=ss[:, 1:2]),
                             reads=[bl, b_xs[j]], writes=[bl])
                    else:
                        S.op("dve", lambda e: e.tensor_scalar(out=xn[:], in0=xs[:, j, :], scalar1=ss[:, 1:2], scalar2=None, op0=ALU.mult),
                             reads=[bl, b_xs[j]], writes=[bl])
                    pb, pbb = nbank()
                    pv = bfview(pb).rearrange("p (c t) -> p c t", c=8)
                    for c in range(8):
                        S.op("pe", lambda e: e.transpose(out=pv[:, c, :], in_=xn[:, c * 128:(c + 1) * 128], identity=identb[:]),
                             reads=[bl, cb[0]], writes=[pbb])
                    for c in range(8):
                        S.op("act", lambda e: e.activation(out=hdst[:, c, j * 128:(j + 1) * 128], in_=pv[:, c, :], func=AF.Identity,
                                                          scale=mult[:, c:c + 1], bias=shift[:, c:c + 1]), reads=[pbb, CONST], writes=[b_hdst])

                with ExitStack() as st:
                    names = (["mq", "mk", "mv", "gif", "av", "mo", "aq", "ak"] if full else ["mk", "mv", "gif", "av", "ak"])
                    W = {}
                    for j in range(4):
                        blk = u * 4 + j
                        S.dma("sp", xs[:, j, :], x_d[blk * 128:(blk + 1) * 128, :], writes=[b_xs[j]])
                    for nm in names:
                        W[nm] = wload(st, "w_" + nm, nm)
                    for j in range(4):
                        norm_block(st, j, mult1, shift1, hT, b_hT, "n1")
                    if flagged:
                        S.op("dve", lambda e: e.tensor_copy(out=mVA[:, :, :, 128:130], in_=flag[:, 0:1].unsqueeze(1).unsqueeze(1).to_broadcast([128, 4, 4, 2])),
                             reads=[cb[4]], writes=[b_mVA])
                    else:
                        S.op("dve", lambda e: e.memset(mVA[:, :, :, 128:130], 1.0), writes=[b_mVA])
                    acc = [sbt(st, "acc%d" % i, [128, 512]) for i in range(4)]
                    et = [sbt(st, "et%d" % i, [128, 512]) for i in range(4)]
                    b_acc = [S.buf("acc%d" % i) for i in range(4)]
                    b_et = [S.buf("et%d" % i) for i in range(4)]
                    hv = mhalo
                    if not OPT_BATCH:
                        S.op("dve", lambda e: e.memset(mbc[:], 0.0), writes=[b_mbc])
                    if OPT_BATCH:
                        S.op("dve", lambda e: e.tensor_tensor(out=mbc[:, :, 2], in0=hv[:, :, 2], in1=mcw[:, :, 0], op=ALU.mult), reads=[b_mhalo] + cb, writes=[b_mbc])
                        S.op("dve", lambda e: e.tensor_tensor(out=mbc[:, :, 1], in0=hv[:, :, 2], in1=mcw[:, :, 1], op=ALU.mult), reads=[b_mhalo] + cb, writes=[b_mbc])
                        S.op("dve", lambda e: e.tensor_tensor(out=mbc[:, :, 0], in0=hv[:, :, 2], in1=mcw[:, :, 2], op=ALU.mult), reads=[b_mhalo] + cb, writes=[b_mbc])
                        S.op("dve", lambda e: e.tensor_tensor(out=ctmp[:, 0:8], in0=hv[:, :, 1], in1=mcw[:, :, 0], op=ALU.mult), reads=[b_mhalo] + cb, writes=[b_mbc])
                        S.op("dve", lambda e: e.tensor_tensor(out=mbc[:, :, 1], in0=mbc[:, :, 1], in1=ctmp[:, 0:8], op=ALU.add), reads=[b_mbc], writes=[b_mbc])
                        S.op("dve", lambda e: e.tensor_tensor(out=ctmp[:, 8:16], in0=hv[:, :, 1], in1=mcw[:, :, 1], op=ALU.mult), reads=[b_mhalo] + cb, writes=[b_mbc])
                        S.op("dve", lambda e: e.tensor_tensor(out=mbc[:, :, 0], in0=mbc[:, :, 0], in1=ctmp[:, 8:16], op=ALU.add), reads=[b_mbc], writes=[b_mbc])
                        S.op("dve", lambda e: e.tensor_tensor(out=ctmp[:, 16:24], in0=hv[:, :, 0], in1=mcw[:, :, 0], op=ALU.mult), reads=[b_mhalo] + cb, writes=[b_mbc])
                        S.op("dve", lambda e: e.tensor_tensor(out=mbc[:, :, 0], in0=mbc[:, :, 0], in1=ctmp[:, 16:24], op=ALU.add), reads=[b_mbc], writes=[b_mbc])
                        S.op("dve", lambda e: e.tensor_tensor(out=mbc[:], in0=mbc[:], in1=mcb[:].unsqueeze(2).to_broadcast([128, 8, 3]), op=ALU.add), reads=[b_mbc] + cb, writes=[b_mbc])
                    groups = ([[0, 1, 2, 3], [4, 5, 6, 7]] if full else [[4, 5, 6, 7]])
                    wsel = mcwf if flagged else mcw
                    for grp in groups:
                        pbs = []
                        for k, c in enumerate(grp):
                            wt, wbf = W["mq" if c < 4 else "mk"]
                            wv = wt[:].rearrange("p (c n) -> p c n", c=8)
                            pb, pbb = nbank()
                            pbs.append((pb, pbb))
                            for dc in range(8):
                                S.op("pe", lambda e: e.matmul(pb[:], lhsT=wv[:, dc, k * 128:(k + 1) * 128], rhs=hT[:, dc, :], start=(dc == 0), stop=(dc == 7)),
                                     reads=[wbf, b_hT], writes=[pbb])
                            m0 = 3 if OPT_PARTMAIN else 0
                            S.op("act", lambda e: e.activation(out=acc[k][:, m0:512], in_=pb[:, m0:512], func=AF.Identity, scale=wsel[:, c, 3:4], bias=mcb[:, c:c + 1]),
                                 reads=[pbb, CONST] + cb, writes=[b_acc[k]])
                            for col in (range(3) if OPT_TINY else ()):
                                S.op("act", lambda e: e.activation(out=acc[k][:, col:col + 1], in_=pb[:, col:col + 1], func=AF.Identity, scale=wsel[:, c, 3:4],
                                                                  bias=mbc[:, c, col:col + 1]), reads=[pbb, CONST, b_mbc] + cb, writes=[b_acc[k]])
                            if flagged:
                                S.op("act", lambda e: e.activation(out=mhalo[:, c, :], in_=pb[:, 509:512], func=AF.Copy, scale=flag[:, 0:1]), reads=[pbb, cb[4], b_mbc], writes=[b_mhalo])
                            else:
                                S.op("act", lambda e: e.copy(out=mhalo[:, c, :], in_=pb[:, 509:512]), reads=[pbb, b_mbc], writes=[b_mhalo])
                        for tp in (2, 1, 0):
                            sh = 3 - tp
                            for k, c in enumerate(grp):
                                pb, pbb = pbs[k]
                                S.op("dve", lambda e: e.scalar_tensor_tensor(out=acc[k][:, sh:512], in0=pb[:, 0:512 - sh], scalar=wsel[:, c, tp:tp + 1], in1=acc[k][:, sh:512],
                                                                            op0=ALU.mult, op1=ALU.add), reads=[pbb, b_acc[k], CONST], writes=[b_acc[k]])
                        for k, c in enumerate(grp):
                            if not OPT_SILU:
                                S.op("act", lambda e: e.activation(out=et[k][:], in_=acc[k][:], func=AF.Exp, scale=-1.0), reads=[b_acc[k]], writes=[b_et[k]])
                                sk = (128.0 ** 0.5) if c < 4 else 1.0
                                S.op("dve", lambda e: e.tensor_scalar(out=et[k][:], in0=et[k][:], scalar1=1.0, scalar2=sk, op0=ALU.add, op1=ALU.mult), reads=[b_et[k]], writes=[b_et[k]])
                                S.op("dve", lambda e: e.reciprocal(out=et[k][:], in_=et[k][:]), reads=[b_et[k]], writes=[b_et[k]])
                                S.op("dve", lambda e: e.tensor_tensor(out=mqT[:, c, :], in0=acc[k][:], in1=et[k][:], op=ALU.mult), reads=[b_acc[k], b_et[k]], writes=[b_mqT])
                            elif c < 4:
                                S.op("act", lambda e: e.activation(out=et[k][:], in_=acc[k][:], func=AF.Silu), reads=[b_acc[k]], writes=[b_et[k]])
                                S.op("dve", lambda e: e.tensor_scalar(out=mqT[:, c, :], in0=et[k][:], scalar1=128.0 ** -0.5, scalar2=None, op0=ALU.mult),
                                     reads=[b_et[k]], writes=[b_mqT])
                            else:
                                S.op("act", lambda e: e.activation(out=mqT[:, c, :], in_=acc[k][:], func=AF.Silu), reads=[b_acc[k]], writes=[b_mqT])
                    sq = [sbt(st, "sq%d" % i, [128, 512]) for i in range(2)]
                    qb = [sbt(st, "qb%d" % i, [128, 512], BF16) for i in range(2)]
                    rs = [sbt(st, "rs%d" % i, [128, 16]) for i in range(2)]
                    KTb = [sbt(st, "KTb%d" % i, [128, 4, 128], BF16) for i in range(2)]
                    VAb = [sbt(st, "VAb%d" % i, [128, 4, 130], BF16) for i in range(2)]
                    b_t = [S.buf("tmj%d" % i) for i in range(2)]
                    b_KTb = [S.buf("KTb%d" % i) for i in range(2)]
                    b_VAb = [S.buf("VAb%d" % i) for i in range(2)]
                    for i in range(2):
                        if flagged:
                            S.op("dve", lambda e: e.tensor_copy(out=VAb[i][:, :, 128:130], in_=flag[:, 0:1].unsqueeze(1).to_broadcast([128, 4, 2])),
                                 reads=[cb[4]], writes=[b_VAb[i]])
                        else:
                            S.op("dve", lambda e: e.memset(VAb[i][:, :, 128:130], 1.0), writes=[b_VAb[i]])
                    for j in range(4):
                        blk = u * 4 + j
                        i = j % 2
                        tsl = slice(j * 128, (j + 1) * 128)

                        def proj(nm):
                            wt, wbf = W[nm]
                            n = PIECES[nm][1] // 8
                            wv = wt[:].rearrange("p (c n) -> p c n", c=8)
                            pb, pbb = nbank()
                            for dc in range(8):
                                S.op("pe", lambda e: e.matmul(pb[:, 0:n], lhsT=hT[:, dc, tsl], rhs=wv[:, dc, :], start=(dc == 0), stop=(dc == 7)),
                                     reads=[wbf, b_hT], writes=[pbb])
                            return pb, pbb

                        def evac_scaled(out_ap, in_ap, rd, wr):
                            if flagged:
                                S.op("act", lambda e: e.activation(out=out_ap, in_=in_ap, func=AF.Copy, scale=flag[:, 0:1]), reads=rd + [cb[4]], writes=wr)
                            else:
                                S.op("act", lambda e: e.copy(out=out_ap, in_=in_ap), reads=rd, writes=wr)

                        pb, pbb = proj("mv")
                        evac_scaled(mVA[:, j, :, 0:128], pb[:].rearrange("p (h d) -> p h d", h=4), [pbb], [b_mVA])
                        pb, pbb = proj("gif")
                        S.op("dve", lambda e: e.tensor_tensor(out=gif[:, j, :], in0=pb[:, 0:8], in1=gifb[:], op=ALU.add), reads=[pbb, cb[9]], writes=[b_gif])
                        pb, pbb = proj("av")
                        evac_scaled(VAb[i][:, :, 0:128], pb[:].rearrange("p (h d) -> p h d", h=4), [pbb], [b_VAb[i]])
                        S.dma("pool", va_d.rearrange("h p (b c) -> p h b c", c=130)[:, :, blk, :], VAb[i][:], reads=[b_VAb[i]], writes=[b_vad])
                        if full:
                            pb, pbb = proj("mo")
                            S.op("act", lambda e: e.activation(out=sigmo[:, j, :], in_=pb[:], func=AF.Exp, scale=-1.0), reads=[pbb], writes=[b_sig])
                            S.op("dve", lambda e: e.tensor_scalar(out=sigmo[:, j, :], in0=sigmo[:, j, :], scalar1=1.0, scalar2=None, op0=ALU.add), reads=[b_sig], writes=[b_sig])
                            S.op("dve", lambda e: e.reciprocal(out=sigmo[:, j, :], in_=sigmo[:, j, :]), reads=[b_sig], writes=[b_sig])
                        for nm in (["aq", "ak"] if full else ["ak"]):
                            pb, pbb = proj(nm)
                            S.op("act", lambda e: e.activation(out=sq[i][:], in_=pb[:], func=AF.Square, scale=0.125), reads=[pbb], writes=[b_t[i]])
                            S.op("dve", lambda e: e.tensor_reduce(out=rs[i][:, 0:8], in_=sq[i][:].rearrange("p (g d) -> p g d", d=64), axis=AX.X, op=ALU.add),
                                 reads=[b_t[i]], writes=[b_t[i]])
                            S.op("act", lambda e: e.activation(out=rs[i][:, 8:16], in_=rs[i][:, 0:8], func=AF.Ln, bias=epst[:, 0:1], scale=1.0), reads=[b_t[i], CONST], writes=[b_t[i]])
                            S.op("act", lambda e: e.activation(out=rs[i][:, 8:16], in_=rs[i][:, 8:16], func=AF.Exp, scale=-0.5), reads=[b_t[i]], writes=[b_t[i]])
                            S.op("dve", lambda e: e.tensor_tensor(out=qb[i][:].rearrange("p (g d) -> p g d", d=64), in0=pb[:].rearrange("p (g d) -> p g d", d=64),
                                                                 in1=rs[i][:, 8:16].unsqueeze(2).to_broadcast([128, 8, 64]), op=ALU.mult), reads=[pbb, b_t[i]], writes=[b_t[i]])
                            tb, tbb = nbank()
                            tv = bfview(tb)[:, 0:512].rearrange("p (h t) -> p h t", h=4)
                            for h in range(4):
                                S.op("pe", lambda e: e.transpose(out=tv[:, h, :], in_=qb[i][:, h * 128:(h + 1) * 128], identity=identb[:]), reads=[b_t[i], cb[0]], writes=[tbb])
                            if nm == "aq":
                                if OPT_GFOLD:
                                    S.op("act", lambda e: e.activation(out=Qbd[0:64, :, j, 0:128], in_=tv[0:64, :, :], func=AF.Copy, scale=gqfm[0:64, 0:1]), reads=[tbb, CONST] + cb, writes=[b_Qbd])
                                    S.op("act", lambda e: e.activation(out=Qbd[64:128, :, j, 128:256], in_=tv[64:128, :, :], func=AF.Copy, scale=gqfm[64:128, 0:1]), reads=[tbb, CONST] + cb, writes=[b_Qbd])
                                else:
                                    S.op("act", lambda e: e.copy(out=Qbd[0:64, :, j, 0:128], in_=tv[0:64, :, :]), reads=[tbb, CONST] + cb, writes=[b_Qbd])
                                    S.op("act", lambda e: e.copy(out=Qbd[64:128, :, j, 128:256], in_=tv[64:128, :, :]), reads=[tbb, CONST] + cb, writes=[b_Qbd])
                            else:
                                if OPT_GFOLD:
                                    S.op("act", lambda e: e.activation(out=KTb[i][:], in_=tv, func=AF.Copy, scale=gkfm[:, 0:1]), reads=[tbb] + cb, writes=[b_KTb[i]])
                                else:
                                    S.op("act", lambda e: e.copy(out=KTb[i][:], in_=tv), reads=[tbb] + cb, writes=[b_KTb[i]])
                                S.dma("pool", kt_d.rearrange("h p k -> p h k")[:, :, blk * 128:(blk + 1) * 128], KTb[i][:], reads=[b_KTb[i]], writes=[b_ktd])
                    S.barrier()

                with ExitStack() as st:
                    def mlstm_block(j):
                        i = j % 2
                        tsl = slice(j * 128, (j + 1) * 128)
                        lf = sbt(st, "lf%d" % j, [128, 4])
                        e1 = sbt(st, "e1%d" % j, [128, 4])
                        e2 = sbt(st, "e2%d" % j, [128, 4])
                        wsx = sbt(st, "wsx%d" % j, [128, 4])
                        RL = sbt(st, "RL%d" % j, [128, 4, 128])
                        Et = sbt(st, "Et%d" % j, [128, 512])
                        Kw = sbt(st, "Kw%d" % j, [128, 4, 128], BF16)
                        bm = S.buf("ml%d" % j)
                        b_RL = S.buf("RL")
                        b_Et = S.buf("Et")
                        b_Kw = S.buf("Kw")
                        S.op("act", lambda e: e.activation(out=lf[:], in_=gif[:, j, 4:8], func=AF.Exp, scale=-1.0), reads=[b_gif], writes=[bm])
                        S.op("act", lambda e: e.activation(out=lf[:], in_=lf[:], func=AF.Ln, bias=onet[:, 0:1], scale=1.0), reads=[bm, CONST], writes=[bm])
                        S.op("dve", lambda e: e.tensor_scalar(out=lf[:], in0=lf[:], scalar1=-1.0, scalar2=None, op0=ALU.mult), reads=[bm], writes=[bm])
                        p1, p1b = nbank()
                        S.op("pe", lambda e: e.matmul(p1[:, 0:4], lhsT=umask[:], rhs=lf[:], start=True, stop=True), reads=[bm, cb[2]], writes=[p1b])
                        S.op("dve", lambda e: e.tensor_tensor(out=RL[:], in0=umask[:].unsqueeze(1).to_broadcast([128, 4, 128]),
                                                             in1=lf[:].unsqueeze(2).to_broadcast([128, 4, 128]), op=ALU.mult), reads=[bm, cb[2]], writes=[b_RL])
                        RLf = RL[:].rearrange("p h t -> p (h t)")
                        pBc, pBcb = nbank()
                        S.op("pe", lambda e: e.matmul(pBc[:], lhsT=ones_f[:], rhs=RLf, start=True, stop=True), reads=[b_RL, CONST], writes=[pBcb])
                        S.op("dve", lambda e: e.tensor_tensor(out=e1[:], in0=gif[:, j, 0:4], in1=p1[:, 0:4], op=ALU.subtract), reads=[b_gif, p1b], writes=[bm])
                        blast = pBc[:].rearrange("p (h t) -> p h t", h=4)[:, :, 127]
                        S.op("dve", lambda e: e.tensor_tensor(out=e2[:], in0=e1[:], in1=blast, op=ALU.add), reads=[bm, pBcb], writes=[bm])
                        S.op("act", lambda e: e.activation(out=wsx[:], in_=e2[:], func=AF.Exp), reads=[bm], writes=[bm])
                        S.op("act", lambda e: e.activation(out=Et[:], in_=pBc[:], func=AF.Exp), reads=[pBcb], writes=[b_Et])
                        tb, tbb = nbank()
                        tv = bfview(tb)[:, 0:512].rearrange("p (h t) -> p h t", h=4)
                        for h in range(4):
                            S.op("pe", lambda e: e.transpose(out=tv[:, h, :], in_=mqT[:, 4 + h, tsl], identity=identb[:]), reads=[b_mqT, cb[0]], writes=[tbb])
                        S.op("dve", lambda e: e.tensor_tensor(out=Kw[:], in0=tv, in1=wsx[:].unsqueeze(2).to_broadcast([128, 4, 128]), op=ALU.mult),
                             reads=[tbb, bm], writes=[b_Kw])
                        yield
                        if full:
                            DT = sbt(st, "DT%d" % j, [128, 4, 128])
                            PT = sbt(st, "PT%d" % j, [128, 4, 128], BF16)
                            qpT = sbt(st, "qpT%d" % j, [128, 4, 128], BF16)
                            rden = sbt(st, "rden%d" % j, [128, 4])
                            hmr = sbt(st, "hmr%d" % j, [128, 4, 128])
                            hsq = sbt(st, "hsq%d" % j, [128, 4, 128])
                            hss = sbt(st, "hss%d" % j, [128, 8])
                            gs = sbt(st, "gs%d" % j, [128, 512])
                            hmf = sbt(st, "hmf%d" % j, [128, 4, 128], BF16)
                            b_o = S.buf("mo%d" % j)
                            b_DT = S.buf("DT")
                            b_PT = S.buf("PT")
                            b_qp = S.buf("qpT")
                            pBm, pBmb = nbank()
                            S.op("pe", lambda e: e.matmul(pBm[:], lhsT=ones_f[:], rhs=RLf, start=True, stop=False), reads=[b_RL, CONST], writes=[pBmb])
                            S.op("pe", lambda e: e.matmul(pBm[:], lhsT=identf[:], rhs=negm4[:], start=False, stop=True), reads=[cb[1], cb[3]], writes=[pBmb])
                            for h in range(4):
                                S.op("act", lambda e: e.activation(out=DT[:, h, :], in_=pBm[:, h * 128:(h + 1) * 128], func=AF.Exp, bias=e1[:, h:h + 1], scale=1.0),
                                     reads=[pBmb, bm], writes=[b_DT])
                            S.op("dve", lambda e: e.tensor_tensor(out=qpT[:], in0=mqT[:, 0:4, tsl], in1=Et[:].rearrange("p (h t) -> p h t", h=4), op=ALU.mult),
                                 reads=[b_mqT, b_Et], writes=[b_qp])
                            pA, pAb = nbank()
                            for h in range(4):
                                S.op("pe", lambda e: e.matmul(pA[:, h * 128:(h + 1) * 128], lhsT=mqT[:, 4 + h, tsl], rhs=mqT[:, h, tsl], start=True, stop=True),
                                     reads=[b_mqT], writes=[pAb])
                            S.op("dve", lambda e: e.tensor_tensor(out=PT[:], in0=pA[:].rearrange("p (h t) -> p h t", h=4), in1=DT[:], op=ALU.mult),
                                 reads=[pAb, b_DT], writes=[b_PT])
                        yield
                        if full:
                            pN = [nbank(), nbank()]
                            for h in range(4):
                                pn, pnb = pN[h // 2]
                                o = (h % 2) * 130
                                S.op("pe", lambda e: e.matmul(pn[:, o:o + 130], lhsT=PT[:, h, :], rhs=mVA[:, j, h, :], start=True, stop=False), reads=[b_PT, b_mVA], writes=[pnb])
                                S.op("pe", lambda e: e.matmul(pn[:, o:o + 130], lhsT=qpT[:, h, :], rhs=Sstb[:, h, :], start=False, stop=True), reads=[b_qp, b_Sb], writes=[pnb])
                            for hp in range(2):
                                pn, pnb = pN[hp]
                                pv = pn[:, 0:260].rearrange("p (h c) -> p h c", h=2)
                                S.op("act", lambda e: e.activation(out=rden[:, 2 * hp:2 * hp + 2], in_=pv[:, :, 128], func=AF.Abs), reads=[pnb], writes=[b_o])
                            S.op("dve", lambda e: e.tensor_scalar(out=rden[:], in0=rden[:], scalar1=1.0, scalar2=None, op0=ALU.max), reads=[b_o], writes=[b_o])
                            S.op("dve", lambda e: e.reciprocal(out=rden[:], in_=rden[:]), reads=[b_o], writes=[b_o])
                            for hp in range(2):
                                pn, pnb = pN[hp]
                                pv = pn[:, 0:260].rearrange("p (h c) -> p h c", h=2)
                                S.op("dve", lambda e: e.tensor_tensor(out=hmr[:, 2 * hp:2 * hp + 2, :], in0=pv[:, :, 0:128],
                                                                     in1=rden[:, 2 * hp:2 * hp + 2].unsqueeze(2).to_broadcast([128, 2, 128]), op=ALU.mult),
                                     reads=[pnb, b_o], writes=[b_o])
                            S.op("act", lambda e: e.activation(out=hsq[:], in_=hmr[:], func=AF.Square, scale=1.0 / math.sqrt(128.0)), reads=[b_o], writes=[b_o])
                            S.op("dve", lambda e: e.tensor_reduce(out=hss[:, 0:4], in_=hsq[:], axis=AX.X, op=ALU.add), reads=[b_o], writes=[b_o])
                            S.op("act", lambda e: e.activation(out=hss[:, 4:8], in_=hss[:, 0:4], func=AF.Ln, bias=epst[:, 0:1], scale=1.0), reads=[b_o, CONST], writes=[b_o])
                            S.op("act", lambda e: e.activation(out=hss[:, 4:8], in_=hss[:, 4:8], func=AF.Exp, scale=-0.5), reads=[b_o], writes=[b_o])
                            S.op("dve", lambda e: e.tensor_tensor(out=gs[:], in0=mng[:], in1=sigmo[:, j, :], op=ALU.mult), reads=[b_sig] + cb, writes=[b_o])
                            for h in range(4):
                                S.op("dve", lambda e: e.scalar_tensor_tensor(out=hmf[:, h, :], in0=hmr[:, h, :], scalar=hss[:, 4 + h:5 + h], in1=gs[:, h * 128:(h + 1) * 128],
                                                                            op0=ALU.mult, op1=ALU.mult), reads=[b_o], writes=[b_o])
                            tb2, tb2b = nbank()
                            tv2 = bfview(tb2)[:, 0:512].rearrange("p (h t) -> p h t", h=4)
                            for h in range(4):
                                S.op("pe", lambda e: e.transpose(out=tv2[:, h, :], in_=hmf[:, h, :], identity=identb[:]), reads=[b_o, cb[0]], writes=[tb2b])
                            S.op("act", lambda e: e.copy(out=hmT[:, :, tsl], in_=tv2), reads=[tb2b], writes=[b_hmT])
                        pC = [nbank(), nbank()]
                        for h in range(4):
                            pc_, pcb = pC[h // 2]
                            o = (h % 2) * 130
                            S.op("pe", lambda e: e.matmul(pc_[:, o:o + 130], lhsT=Kw[:, h, :], rhs=mVA[:, j, h, :], start=True, stop=True), reads=[b_Kw, b_mVA], writes=[pcb])
                        Ev = Et[:].rearrange("p (h t) -> p h t", h=4)
                        for h in range(4):
                            pc_, pcb = pC[h // 2]
                            o = (h % 2) * 130
                            S.op("dve", lambda e: e.scalar_tensor_tensor(out=Sst[:, h, :], in0=Sst[:, h, :], scalar=Ev[:, h, 127:128], in1=pc_[:, o:o + 130],
                                                                        op0=ALU.mult, op1=ALU.add), reads=[b_S, b_Et, pcb], writes=[b_S])
                        S.op("act", lambda e: e.copy(out=Sstb[:], in_=Sst[:]), reads=[b_S], writes=[b_Sb])
                    gens = [mlstm_block(j) for j in range(4)]
                    for ph in range(3):
                        for g_ in gens:
                            next(g_, None)
                    S.barrier()

                if full:
                    with ExitStack() as st:
                        NKC = 8
                        kch = [sbt(st, "kch%d" % i, [128, NKC * 128], BF16) for i in range(3)]
                        vch = [sbt(st, "vch%d" % i, [128, NKC, 130], BF16) for i in range(3)]
                        b_kch = [S.buf("kch%d" % i) for i in range(3)]
                        b_vch = [S.buf("vch%d" % i) for i in range(3)]
                        pex = [sbt(st, "pex%d" % i, [128, 512], BF16) for i in range(6)]
                        b_pex = [S.buf("pex%d" % i) for i in range(6)]
                        rr = sbt(st, "rr", [128, 8])
                        har = sbt(st, "har", [128, 4, 128])
                        hsq = sbt(st, "ahsq", [128, 4, 128])
                        hss = sbt(st, "ahss", [128, 8])
                        haf = sbt(st, "haf", [128, 4, 128], BF16)
                        b_a = S.buf("attn_o")
                        nkb_tot = u * 4 + 4
                        chunk_list = [(s0, min(NKC, nkb_tot - s0)) for s0 in range(0, nkb_tot, NKC)]
                        ring = 0
                        LCH = len(chunk_list)
                        loaded = set()

                        def ensure_chunk(g):
                            if g >= 4 * LCH or g in loaded:
                                return
                            loaded.add(g)
                            h_, ci_ = g // LCH, g % LCH
                            s0, nk = chunk_list[ci_]
                            ri = g % 3
                            S.dma("sp", kch[ri][:, 0:nk * 128], kt_d[h_, :, s0 * 128:(s0 + nk) * 128], reads=[b_ktd], writes=[b_kch[ri]])
                            S.dma("sp", vch[ri][:, 0:nk, :], va_d[h_, :, s0 * 130:(s0 + nk) * 130].rearrange("p (b c) -> p b c", c=130), reads=[b_vad], writes=[b_vch[ri]])

                        for h in range(4):
                            accs = [(banks[jq], bbufs[jq]) for jq in range(4)]
                            for jq in range(4):
                                S.op("dve", lambda e: e.memset(accs[jq][0][:, 0:260], 0.0), writes=[accs[jq][1]])
                            items = []
                            for ci, (s0, nk) in enumerate(chunk_list):
                                g = h * LCH + ci
                                ri = g % 3
                                for kk in range(nk):
                                    kb = s0 + kk
                                    for p2 in range(2):
                                        j0 = 2 * p2
                                        if kb <= u * 4 + j0 - 2:
                                            items.append((ri, kk, kb, (j0, j0 + 1), g))
                                        else:
                                            for jq in (j0, j0 + 1):
                                                if kb <= u * 4 + jq:
                                                    items.append((ri, kk, kb, (jq,), g))

                            LAG = 3
                            NS = 4
                            NPX = len(pex)

                            def emit_scores(it, idx):
                                ri, kk, kb, jqs, ci = it
                                ensure_chunk(ci)
                                ensure_chunk(ci + 1)
                                nq = len(jqs)
                                sl = idx % NS
                                sp_, spb = banks[4 + sl], bbufs[4 + sl]
                                sv = sp_[:, 0:256 * nq]
                                px = idx % NPX
                                delta = u * 4 + jqs[0] - kb
                                near = (nq == 1) and delta <= 1
                                rhs = Qbd[:, h, jqs[0]:jqs[0] + nq, :].rearrange("p j c -> p (j c)")
                                S.op("pe", lambda e: e.matmul(sv, lhsT=kch[ri][:, kk * 128:(kk + 1) * 128], rhs=rhs, start=True, stop=not near),
                                     reads=[b_kch[ri], b_Qbd], writes=[spb])
                                if near:
                                    S.op("pe", lambda e: e.matmul(sv, lhsT=identb[:], rhs=biasb[:, h, delta, :], start=False, stop=True), reads=[CONST, cb[0]], writes=[spb])
                                    S.op("act", lambda e: e.activation(out=pex[px][:, 0:256 * nq], in_=sv, func=AF.Exp), reads=[spb], writes=[b_pex[px]])
                                else:
                                    S.op("act", lambda e: e.activation(out=pex[px][:, 0:256 * nq], in_=sv, func=AF.Exp, bias=farb[:, h:h + 1], scale=1.0),
                                         reads=[spb, cb[14]], writes=[b_pex[px]])

                            def emit_pv(it, idx):
                                ri, kk, kb, jqs, ci = it
                                px = idx % NPX
                                for a_, jq in enumerate(jqs):
                                    ab, abb = accs[jq]
                                    for m in range(2):
                                        c0 = a_ * 256 + m * 128
                                        S.op("pe", lambda e: e.matmul(ab[:, m * 130:(m + 1) * 130], lhsT=pex[px][:, c0:c0 + 128], rhs=vch[ri][:, kk, :],
                                                                     start=False, stop=False, skip_group_check=True), reads=[b_pex[px], b_vch[ri]], writes=[abb])

                            for idx in range(len(items) + LAG):
                                if idx < len(items):
                                    emit_scores(items[idx], idx)
                                if idx >= LAG:
                                    emit_pv(items[idx - LAG], idx - LAG)
                            for jq in range(4):
                                ab, abb = accs[jq]
                                av = ab[:, 0:260].rearrange("p (m c) -> p m c", m=2)
                                S.op("dve", lambda e: e.tensor_scalar(out=rr[:, 2 * jq:2 * jq + 2], in0=av[:, :, 128], scalar1=1e-30, scalar2=None, op0=ALU.max), reads=[abb], writes=[b_a])
                                S.op("dve", lambda e: e.reciprocal(out=rr[:, 2 * jq:2 * jq + 2], in_=rr[:, 2 * jq:2 * jq + 2]), reads=[b_a], writes=[b_a])
                                S.op("dve", lambda e: e.tensor_tensor(out=rr[:, 2 * jq + 1:2 * jq + 2], in0=rr[:, 2 * jq + 1:2 * jq + 2], in1=neglam[:], op=ALU.mult),
                                     reads=[b_a, CONST], writes=[b_a])
                                S.op("dve", lambda e: e.tensor_scalar(out=har[:, jq, :], in0=av[:, 0, 0:128], scalar1=rr[:, 2 * jq:2 * jq + 1], scalar2=None, op0=ALU.mult),
                                     reads=[abb, b_a], writes=[b_a])
                                S.op("dve", lambda e: e.scalar_tensor_tensor(out=har[:, jq, :], in0=av[:, 1, 0:128], scalar=rr[:, 2 * jq + 1:2 * jq + 2], in1=har[:, jq, :],
                                                                            op0=ALU.mult, op1=ALU.add), reads=[abb, b_a], writes=[b_a])
                            S.op("act", lambda e: e.activation(out=hsq[:], in_=har[:], func=AF.Square, scale=1.0 / math.sqrt(128.0)), reads=[b_a], writes=[b_a])
                            S.op("dve", lambda e: e.tensor_reduce(out=hss[:, 0:4], in_=hsq[:], axis=AX.X, op=ALU.add), reads=[b_a], writes=[b_a])
                            S.op("act", lambda e: e.activation(out=hss[:, 4:8], in_=hss[:, 0:4], func=AF.Ln, bias=epst[:, 0:1], scale=1.0), reads=[b_a, CONST], writes=[b_a])
                            S.op("act", lambda e: e.activation(out=hss[:, 4:8], in_=hss[:, 4:8], func=AF.Exp, scale=-0.5), reads=[b_a], writes=[b_a])
                            for jq in range(4):
                                S.op("dve", lambda e: e.scalar_tensor_tensor(out=haf[:, jq, :], in0=har[:, jq, :], scalar=hss[:, 4 + jq:5 + jq], in1=ang[:, h * 128:(h + 1) * 128],
                                                                            op0=ALU.mult, op1=ALU.mult), reads=[b_a, CONST] + cb, writes=[b_a])
                            tb, tbb = banks[6 + h % 2], bbufs[6 + h % 2]
                            tv = bfview(tb)[:, 0:512].rearrange("p (j t) -> p j t", j=4)
                            for jq in range(4):
                                S.op("pe", lambda e: e.transpose(out=tv[:, jq, :], in_=haf[:, jq, :], identity=identb[:]), reads=[b_a, cb[0]], writes=[tbb])
                            S.op("act", lambda e: e.copy(out=haT[:, h, :], in_=tv.rearrange("p j t -> p (j t)")), reads=[tbb], writes=[b_haT])
                        S.barrier()

                    h2T = hT
                    b_h2T = b_hT
                    with ExitStack() as st:
                        W = {}
                        for nm in ("bm", "gm0", "ba", "ga0", "gm1", "ga1", "out0", "out1"):
                            W[nm] = wload(st, "w_" + nm, nm)
                        yT = sbt(st, "yT", [128, 8, 512], BF16)
                        b_yT = S.buf("yT")
                        sg = [[sbt(st, "sg%d_%d" % (i, a), [128, 512]) for a in range(2)] for i in range(2)]
                        ty = [[sbt(st, "ty%d_%d" % (i, a), [128, 512]) for a in range(2)] for i in range(2)]
                        b_sg = [S.buf("sg%d" % i) for i in range(2)]
                        for c in range(8):
                            i = c % 2
                            res = []
                            for a, (bw, gw, srcT, b_src) in enumerate((("bm", "gm", hmT, b_hmT), ("ba", "ga", haT, b_haT))):
                                wt, wbf = W[bw]
                                wv = wt[:].rearrange("p (k n) -> p k n", k=4)
                                py, pyb = nbank()
                                for k in range(4):
                                    S.op("pe", lambda e: e.matmul(py[:], lhsT=wv[:, k, c * 128:(c + 1) * 128], rhs=srcT[:, k, :], start=(k == 0), stop=(k == 3)),
                                         reads=[wbf, b_src], writes=[pyb])
                                gt_, gbf = W[gw + str(c // 4)]
                                gv = gt_[:].rearrange("p (c n) -> p c n", c=8)
                                pg, pgb = nbank()
                                for dc in range(8):
                                    S.op("pe", lambda e: e.matmul(pg[:], lhsT=gv[:, dc, (c % 4) * 128:(c % 4 + 1) * 128], rhs=hT[:, dc, :], start=(dc == 0), stop=(dc == 7)),
                                         reads=[gbf, b_hT], writes=[pgb])
                                S.op("act", lambda e: e.activation(out=sg[i][a][:], in_=pg[:], func=AF.Sigmoid), reads=[pgb], writes=[b_sg[i]])
                                S.op("dve", lambda e: e.tensor_tensor(out=ty[i][a][:], in0=py[:], in1=sg[i][a][:], op=ALU.mult), reads=[pyb, b_sg[i]], writes=[b_sg[i]])
                            S.op("dve", lambda e: e.tensor_tensor(out=yT[:, c, :], in0=ty[i][0][:], in1=ty[i][1][:], op=ALU.add), reads=[b_sg[i]], writes=[b_yT])
                        tx = [sbt(st, "tx%d" % i, [128, 512]) for i in range(2)]
                        b_tx = [S.buf("tx%d" % i) for i in range(2)]
                        for j in range(4):
                            tsl = slice(j * 128, (j + 1) * 128)
                            for n in range(2):
                                wt, wbf = W["out%d" % n]
                                wv = wt[:].rearrange("p (c n) -> p c n", c=8)
                                po, pob = nbank()
                                for c in range(8):
                                    S.op("pe", lambda e: e.matmul(po[:], lhsT=yT[:, c, tsl], rhs=wv[:, c, :], start=(c == 0), stop=(c == 7)), reads=[wbf, b_yT], writes=[pob])
                                i = (j * 2 + n) % 2
                                S.op("dve", lambda e: e.tensor_tensor(out=tx[i][:], in0=po[:], in1=gate1[:, n * 512:(n + 1) * 512], op=ALU.mult), reads=[pob, CONST], writes=[b_tx[i]])
                                S.op("dve", lambda e: e.tensor_tensor(out=xs[:, j, n * 512:(n + 1) * 512], in0=xs[:, j, n * 512:(n + 1) * 512], in1=tx[i][:], op=ALU.add),
                                     reads=[b_tx[i], b_xs[j]], writes=[b_xs[j]])
                        for j in range(4):
                            norm_block(st, j, mult2, shift2, h2T, b_h2T, "n2")
                        S.barrier()

                    with ExitStack() as st:
                        actT = sbt(st, "actT", [128, NF, 512], BF16)
                        b_actT = S.buf("actT")
                        wu = [None, None, None]
                        uu = [sbt(st, "fu%d" % i, [128, 512]) for i in range(8)]
                        b_uu = [S.buf("fu%d" % i) for i in range(8)]
                        fe = [sbt(st, "fe%d" % i, [128, 512]) for i in range(2)]
                        b_fe = [S.buf("fe%d" % i) for i in range(2)]
                        wus = [sbt(st, "wup%d" % i, [128, 4096], BF16) for i in range(3)]
                        b_wus = [S.buf("wup%d" % i) for i in range(3)]

                        def ldup(jj):
                            off, n = PIECES["up%d" % jj]
                            S.dma("sp", wus[jj % 3][:], wb_d[:, off:off + n], reads=[b_wb], writes=[b_wus[jj % 3]])

                        ldup(0)
                        ldup(1)
                        wd, wdb = wload(st, "w_down", "down")
                        wdv = wd[:].rearrange("p (f n) -> p f n", f=NF)
                        if not OPT_BATCH:
                            S.op("dve", lambda e: e.memset(fbc[:], 0.0), writes=[b_fbc])
                        if OPT_BATCH:
                            S.op("dve", lambda e: e.tensor_tensor(out=fbc[:, :, 1], in0=fhalo[:, :, 1], in1=fcw[:, :, 0], op=ALU.mult), reads=[b_fhalo] + cb, writes=[b_fbc])
                            S.op("dve", lambda e: e.tensor_tensor(out=fbc[:, :, 0], in0=fhalo[:, :, 1], in1=fcw[:, :, 1], op=ALU.mult), reads=[b_fhalo] + cb, writes=[b_fbc])
                            S.op("dve", lambda e: e.tensor_tensor(out=ctmp[:], in0=fhalo[:, :, 0], in1=fcw[:, :, 0], op=ALU.mult), reads=[b_fhalo] + cb, writes=[b_fbc])
                            S.op("dve", lambda e: e.tensor_tensor(out=fbc[:, :, 0], in0=fbc[:, :, 0], in1=ctmp[:], op=ALU.add), reads=[b_fbc], writes=[b_fbc])
                            S.op("dve", lambda e: e.tensor_tensor(out=fbc[:], in0=fbc[:], in1=fcb[:].unsqueeze(2).to_broadcast([128, 44, 2]), op=ALU.add), reads=[b_fbc] + cb, writes=[b_fbc])

                        def ffn_piece(jj):
                            if jj + 2 < 11:
                                ldup(jj + 2)
                            wv = wus[jj % 3][:].rearrange("p (c n) -> p c n", c=8)
                            wbf = b_wus[jj % 3]
                            wsel = fcwf if flagged else fcw
                            pbs = []
                            for k in range(4):
                                q = jj * 4 + k
                                pb, pbb = nbank()
                                pbs.append((pb, pbb))
                                for dc in range(8):
                                    S.op("pe", lambda e: e.matmul(pb[:], lhsT=wv[:, dc, k * 128:(k + 1) * 128], rhs=h2T[:, dc, :], start=(dc == 0), stop=(dc == 7)),
                                         reads=[wbf, b_h2T], writes=[pbb])
                                kk_ = (jj % 2) * 4 + k
                                m0 = 2 if OPT_PARTMAIN else 0
                                S.op("act", lambda e: e.activation(out=uu[kk_][:, m0:512], in_=pb[:, m0:512], func=AF.Identity, scale=wsel[:, q, 2:3], bias=fcb[:, q:q + 1]),
                                     reads=[pbb, CONST] + cb, writes=[b_uu[kk_]])
                                for col in (range(2) if OPT_TINY else ()):
                                    S.op("act", lambda e: e.activation(out=uu[kk_][:, col:col + 1], in_=pb[:, col:col + 1], func=AF.Identity, scale=wsel[:, q, 2:3],
                                                                      bias=fbc[:, q, col:col + 1]), reads=[pbb, CONST, b_fbc] + cb, writes=[b_uu[kk_]])
                                if flagged:
                                    S.op("act", lambda e: e.activation(out=fhalo[:, q, :], in_=pb[:, 510:512], func=AF.Copy, scale=flag[:, 0:1]), reads=[pbb, cb[4], b_fbc], writes=[b_fhalo])
                                else:
                                    S.op("act", lambda e: e.copy(out=fhalo[:, q, :], in_=pb[:, 510:512]), reads=[pbb, b_fbc], writes=[b_fhalo])
                            for tp in (1, 0):
                                sh = 2 - tp
                                for k in range(4):
                                    q = jj * 4 + k
                                    kk_ = (jj % 2) * 4 + k
                                    pb, pbb = pbs[k]
                                    S.op("dve", lambda e: e.scalar_tensor_tensor(out=uu[kk_][:, sh:512], in0=pb[:, 0:512 - sh], scalar=wsel[:, q, tp:tp + 1], in1=uu[kk_][:, sh:512],
                                                                                op0=ALU.mult, op1=ALU.add), reads=[pbb, b_uu[kk_], CONST], writes=[b_uu[kk_]])
                            yield
                            for i in range(2):
                                f = 2 * jj + i
                                uv = uu[(jj % 2) * 4 + i]
                                ug = uu[(jj % 2) * 4 + 2 + i]
                                b_uv = b_uu[(jj % 2) * 4 + i]
                                b_ug = b_uu[(jj % 2) * 4 + 2 + i]
                                S.op("act", lambda e: e.activation(out=fe[i][:], in_=ug[:], func=AF.Silu), reads=[b_ug], writes=[b_fe[i]])
                                S.op("dve", lambda e: e.tensor_tensor(out=actT[:, f, :], in0=uv[:], in1=fe[i][:], op=ALU.mult), reads=[b_uv, b_fe[i]], writes=[b_actT])
                        fgens = [ffn_piece(jj) for jj in range(11)]
                        for step in range(12):
                            if step < 11:
                                next(fgens[step], None)
                            if step >= 1:
                                next(fgens[step - 1], None)
                        tx = fe
                        b_tx = b_fe
                        for j in range(4):
                            tsl = slice(j * 128, (j + 1) * 128)
                            blk = u * 4 + j
                            for n in range(2):
                                po, pob = nbank()
                                for f in range(NF):
                                    S.op("pe", lambda e: e.matmul(po[:], lhsT=actT[:, f, tsl], rhs=wdv[:, f, n * 512:(n + 1) * 512], start=(f == 0), stop=(f == NF - 1)),
                                         reads=[wdb, b_actT], writes=[pob])
                                i = (j * 2 + n) % 2
                                S.op("dve", lambda e: e.tensor_tensor(out=tx[i][:], in0=po[:], in1=gate2[:, n * 512:(n + 1) * 512], op=ALU.mult), reads=[pob, CONST], writes=[b_tx[i]])
                                S.op("dve", lambda e: e.tensor_tensor(out=xs[:, j, n * 512:(n + 1) * 512], in0=xs[:, j, n * 512:(n + 1) * 512], in1=tx[i][:], op=ALU.add),
                                     reads=[b_tx[i], b_xs[j]], writes=[b_xs[j]])
                            if own:
                                ob = blk - NFLAG * 4
                                S.dma("pool", out_d[ob * 128:(ob + 1) * 128, :], xs[:, j, :], reads=[b_xs[j]], writes=[b_out])
                        S.barrier()
        S.barrier()
    return nc


def _t5_bucket(n):
    n = np.maximum(n, 0)
    max_exact = 16
    nf = np.maximum(n, 1).astype(np.float32)
    large = max_exact + (np.log(nf / np.float32(max_exact)) / np.float32(math.log(128 / max_exact)) * np.float32(32 - max_exact)).astype(np.int32)
    large = np.minimum(large, 31)
    return np.where(n < max_exact, n, large)


def _fm(v, nch):
    return np.ascontiguousarray(np.asarray(v, np.float32).reshape(nch, 128).T)


def _piece_fm(w):
    n = w.shape[1]
    return w.reshape(8, 128, n).transpose(1, 0, 2).reshape(128, 8 * n)


def prepare_inputs(NCTX, NFULL, inputs):
    f32 = np.float32
    g = {k: np.asarray(v) for k, v in inputs.items()}
    NU = NCTX + NFULL
    half_tok = (NU // 2) * 512
    x = g["x"].astype(f32, copy=False)
    B = x.shape[0]
    assert x.shape[1] == 2 * half_tok
    w_in = g["w_in"][0]
    cols = {}
    o = 0
    for nm, n in (("mqk", 1024), ("mv", 512), ("mo", 512), ("mi", 4), ("mf", 4), ("aq", 512), ("ak", 512), ("av", 512), ("gm", 1024), ("ga", 1024)):
        cols[nm] = w_in[:, o:o + n]
        o += n

    def perm_qk(w):
        return w.reshape(1024, 2, 4, 64).transpose(0, 2, 1, 3).reshape(1024, 512)

    wall = np.zeros((128, WPAD), f32)

    def put(nm, arr):
        off, n = PIECES[nm]
        assert arr.shape == (128, n), (nm, arr.shape, n)
        wall[:, off:off + n] = arr

    put("mq", _piece_fm(cols["mqk"][:, 0:512]))
    put("mk", _piece_fm(cols["mqk"][:, 512:1024]))
    put("mv", _piece_fm(cols["mv"]))
    put("mo", _piece_fm(cols["mo"]))
    put("aq", _piece_fm(perm_qk(cols["aq"])))
    put("ak", _piece_fm(perm_qk(cols["ak"])))
    put("av", _piece_fm(cols["av"]))
    put("gm0", _piece_fm(cols["gm"][:, 0:512]))
    put("gm1", _piece_fm(cols["gm"][:, 512:1024]))
    put("ga0", _piece_fm(cols["ga"][:, 0:512]))
    put("ga1", _piece_fm(cols["ga"][:, 512:1024]))
    put("gif", _piece_fm(np.concatenate([cols["mi"], cols["mf"]], axis=1)))
    put("bm", g["w_branch_m"][0].reshape(4, 128, 1024).transpose(1, 0, 2).reshape(128, 4096))
    put("ba", g["w_branch_a"][0].reshape(4, 128, 1024).transpose(1, 0, 2).reshape(128, 4096))
    put("out0", _piece_fm(g["w_out"][0][:, 0:512]))
    put("out1", _piece_fm(g["w_out"][0][:, 512:1024]))
    w_up = g["w_up"][0]
    fcw_full = g["ffn_conv_w"][0]
    fcb_full = g["ffn_conv_b"][0]
    chunk_cols = []
    for jj in range(11):
        cc = [np.arange((2 * jj + i) * 128, (2 * jj + i + 1) * 128) for i in range(2)]
        cc += [DFF + np.arange((2 * jj + i) * 128, (2 * jj + i + 1) * 128) for i in range(2)]
        idx = np.concatenate(cc)
        chunk_cols.append(idx)
        put("up%d" % jj, _piece_fm(w_up[:, idx]))
    allidx = np.concatenate(chunk_cols)
    fcw = fcw_full[:, allidx].reshape(3, 44, 128).transpose(2, 1, 0).reshape(128, 44 * 3)
    fcb = fcb_full[allidx].reshape(44, 128).T
    put("down", g["w_down"][0].reshape(NF, 128, 1024).transpose(1, 0, 2).reshape(128, NF * 1024))

    w_ada = g["w_ada"][0]
    wada = np.stack([_piece_fm(w_ada[:, p * 512:(p + 1) * 512]) for p in range(12)]).astype(f32)
    bada = g["b_ada"][0].astype(f32)
    mcw = g["m_conv_w"][0].reshape(4, 8, 128).transpose(2, 1, 0).reshape(128, 32)
    mcb = _fm(g["m_conv_b"][0], 8)
    gifb = np.concatenate([g["m_igate_b"][0], g["m_fgate_b"][0]]).astype(f32)
    gq = np.tile(g["a_qnorm_g"][0], 8).astype(f32)
    gk = np.tile(g["a_knorm_g"][0], 8).astype(f32)
    rel = g["rel_bias"].astype(f32)
    kk = np.arange(128)[:, None]
    qq = np.arange(128)[None, :]
    biasg = np.zeros((128, 4, 2, 2, 128), f32)
    maskn = np.zeros((128, 4, 2, 2, 128), f32)
    for dl in range(2):
        dist = qq - kk + 128 * dl
        bidx = _t5_bucket(dist)
        for h in range(4):
            t = rel[bidx, h]
            biasg[:, h, dl, 0, :] = t
            biasg[:, h, dl, 1, :] = t
        if dl == 0:
            mk = np.where(dist < 0, NEG, 0.0).astype(f32)
            maskn[:, :, 0, :, :] = mk[:, None, None, :]
    farb = rel[31, :].astype(f32)
    umask = (kk <= qq).astype(f32)
    negm4 = np.tile(np.where(kk <= qq, 0.0, NEG).astype(f32), (1, 4))
    import ml_dtypes
    common = dict(
        wada=wada, bada=bada, badafm=_fm(bada, 48), g1fm=_fm(g["norm1_g"][0], 8), g2fm=_fm(g["norm2_g"][0], 8),
        wall=wall, mcw=np.ascontiguousarray(mcw, f32), mcb=mcb, fcw=np.ascontiguousarray(fcw, f32), fcb=np.ascontiguousarray(fcb, f32),
        gifb=gifb, mng=g["m_norm_g"][0].astype(f32), ang=g["a_norm_g"][0].astype(f32), gq=gq, gk=gk,
        gqfm=np.tile(g["a_qnorm_g"][0], 2).reshape(128, 1).astype(f32), gkfm=np.tile(g["a_knorm_g"][0], 2).reshape(128, 1).astype(f32),
        alam=g["a_lambda"][0].reshape(256).astype(f32), biasg=biasg.reshape(128, -1), maskn=maskn.reshape(128, -1), farb=farb,
        identb=np.eye(128).astype(ml_dtypes.bfloat16), identf=np.eye(128, dtype=f32), umask=umask, negm4=negm4,
    )
    in_maps = []
    for b in range(B):
        cfm = _fm(g["c"][b], 8)
        for hf in range(2):
            if hf == 0:
                xl = np.concatenate([np.zeros((half_tok, D), f32), x[b, 0:half_tok]], axis=0)
                fl = np.zeros((128, 1), f32)
            else:
                xl = x[b]
                fl = np.ones((128, 1), f32)
            m = dict(common)
            m["x"] = np.ascontiguousarray(xl)
            m["cfm"] = cfm
            m["flag"] = fl
            in_maps.append(m)
    return in_maps


_NC_CACHE = {}


def run(NCTX, NFULL, inputs):
    key = (NCTX, NFULL)
    if key not in _NC_CACHE:
        _NC_CACHE[key] = build_program(NCTX, NFULL)
    nc = _NC_CACHE[key]
    in_maps = prepare_inputs(NCTX, NFULL, inputs)
    res = run_bass_kernel_spmd(nc, in_maps, core_ids=list(range(len(in_maps))))
    B = len(in_maps) // 2
    half_tok = ((NCTX + NFULL) // 2) * 512
    out = np.empty((B, 2 * half_tok, D), np.float32)
    for b in range(B):
        for hf in range(2):
            out[b, hf * half_tok:(hf + 1) * half_tok] = res.results[b * 2 + hf]["out"]
    return out


def kernel(**inputs):
    return run(7, 9, inputs)
```

```python
import math
from contextlib import ExitStack

import numpy as np

import concourse.bass as bass
import concourse.mybir as mybir
from concourse.bass_utils import run_bass_kernel_spmd

F32 = mybir.dt.float32
BF16 = mybir.dt.bfloat16
AF = mybir.ActivationFunctionType
ALU = mybir.AluOpType
AX = mybir.AxisListType

D = 1024
DC = 8
DFF = 2816
NF = 22
EPS = 1e-6
LAM_INIT = 0.8 - 0.6 * math.exp(-0.3 * 0)
NEG = -30000.0
OPT_SELF_WAR = True
OPT_XN_ACT = True
OPT_SILU = True
OPT_GFOLD = True
OPT_TINY = True
OPT_BATCH = True
OPT_PARTMAIN = True

PIECES = {}
_off = 0
for _nm, _n in (("mq", 4096), ("mk", 4096), ("mv", 4096), ("mo", 4096), ("aq", 4096), ("ak", 4096),
                ("av", 4096), ("gm0", 4096), ("gm1", 4096), ("ga0", 4096), ("ga1", 4096), ("gif", 64),
                ("bm", 4096), ("ba", 4096), ("out0", 4096), ("out1", 4096)):
    PIECES[_nm] = (_off, _n)
    _off += _n
for _j in range(11):
    PIECES["up%d" % _j] = (_off, 4096)
    _off += 4096
PIECES["down"] = (_off, NF * 1024)
_off += NF * 1024
WTOT = _off
WPAD = ((WTOT + 4095) // 4096) * 4096


class Buf:
    __slots__ = ("name", "w", "rs", "sem", "semv", "grp", "psum")

    def __init__(self, name, grp=None):
        self.psum = False
        self.name = name
        self.w = None
        self.rs = []
        self.sem = None
        self.semv = 0
        self.grp = grp


class Sched:
    def __init__(self, nc, stack):
        self.nc = nc
        self.stack = stack
        self.engs = {}
        for nm, h in (("pe", nc.tensor), ("act", nc.scalar), ("dve", nc.vector), ("pool", nc.gpsimd), ("sp", nc.sync)):
            sem = stack.enter_context(nc.semaphore("s_" + nm))
            self.engs[nm] = dict(h=h, sem=sem, cnt=0, seen={})
        self.groups = {}
        self.dma_ev = {}
        self.free_sems = {}
        self.live = []
        self.nsem = 0

    def buf(self, name, grp=None):
        return Buf(name, grp)

    def _need(self, e, deps):
        E = self.engs[e]
        best = {}
        for d in deps:
            if d is None:
                continue
            sem, val, en = d
            if en == e and e == "pe":
                continue
            k = id(sem)
            if k not in best or best[k][1] < val:
                best[k] = (sem, val)
        for k, (sem, val) in best.items():
            if E["seen"].get(k, 0) >= val:
                continue
            E["h"].wait_ge(sem, val)
            E["seen"][k] = val

    def op(self, e, fn, reads=(), writes=()):
        E = self.engs[e]
        deps = []
        for b in reads:
            deps.append(b.w)
            if b.psum:
                for r in b.rs:
                    if r[2] != e:
                        deps.append(r)
        for b in writes:
            if b.w is not None and b.w[2] != e:
                deps.append(b.w)
            for r in b.rs:
                if OPT_SELF_WAR or r[2] != e:
                    deps.append(r)
        self._need(e, deps)
        ins = fn(E["h"])
        E["cnt"] += 1
        ins.then_inc(E["sem"], 1)
        ev = (E["sem"], E["cnt"], e)
        for b in reads:
            b.rs.append(ev)
        for b in writes:
            b.w = ev
            b.rs = []
        return ins

    def dma(self, q, out, in_, reads=(), writes=()):
        E = self.engs[q]
        deps = []
        for b in reads:
            deps.append(b.w)
        for b in writes:
            if b.grp in ("wbw", "kvw", "outw"):
                continue
            deps.append(b.w)
            deps.extend(b.rs)
        self._need(q, deps)
        ins = E["h"].dma_start(out=out, in_=in_)
        cands = list(writes) + list(reads)
        tgt = ([b for b in cands if b.grp is None] + cands)[0]
        if tgt.grp is not None:
            gk = tgt.grp
            if gk not in self.groups:
                self.groups[gk] = [self.stack.enter_context(self.nc.semaphore("g_" + gk)), 0]
            g = self.groups[gk]
            g[1] += 16
            sem, val = g[0], g[1]
        else:
            if tgt.sem is None:
                tgt.sem = {}
                self.live.append(tgt)
            if q not in tgt.sem:
                fs = self.free_sems.setdefault(q, [])
                if not fs:
                    self.nsem += 1
                    fs.append([self.stack.enter_context(self.nc.semaphore("dp%d" % self.nsem)), 0])
                tgt.sem[q] = fs.pop()
            ent = tgt.sem[q]
            ent[1] += 16
            sem, val = ent[0], ent[1]
        ins.then_inc(sem, 16)
        self.dma_ev[id(sem)] = (sem, val)
        ev = (sem, val, "dma")
        for b in reads:
            b.rs.append(ev)
        for b in writes:
            b.w = ev
            b.rs = []
        return ins

    def barrier(self, engines=("pe", "act", "dve", "pool", "sp")):
        deps = [(E["sem"], E["cnt"], "x") for E in self.engs.values() if E["cnt"] > 0]
        deps += [(s, v, "dma") for (s, v) in self.dma_ev.values()]
        for e in engines:
            self._need(e, deps)
        self.dma_ev = {}
        for b in self.live:
            for qq, ent in b.sem.items():
                self.free_sems[qq].append(ent)
            b.sem = None
        self.live = []

    def seal(self, bufs, grp):
        g = self.groups[grp]
        for b in bufs:
            b.w = (g[0], g[1], "dma")


def build_program(NCTX, NFULL, debug=False):
    NU = NCTX + NFULL
    NFLAG = NU // 2
    NBLK = NU * 4
    NTOK = NU * 512
    NOWN = (NFULL - 1) * 512
    assert NU == 2 * (NFULL - 1)

    nc = bass.Bass("TRN2", target_bir_lowering=False)

    def din(name, shape, dt=F32):
        return nc.dram_tensor(name, list(shape), dt, kind="ExternalInput").ap()

    x_d = din("x", [NTOK, D])
    cfm_d = din("cfm", [128, 8])
    wada_d = din("wada", [12, 128, 8 * 512])
    bada_d = din("bada", [6144])
    badafm_d = din("badafm", [128, 48])
    g1fm_d = din("g1fm", [128, 8])
    g2fm_d = din("g2fm", [128, 8])
    wall_d = din("wall", [128, WPAD])
    mcw_d = din("mcw", [128, 8 * 4])
    mcb_d = din("mcb", [128, 8])
    fcw_d = din("fcw", [128, 44 * 3])
    fcb_d = din("fcb", [128, 44])
    gifb_d = din("gifb", [8])
    mng_d = din("mng", [512])
    ang_d = din("ang", [512])
    gq_d = din("gq", [512])
    gk_d = din("gk", [512])
    gqfm_d = din("gqfm", [128, 1])
    gkfm_d = din("gkfm", [128, 1])
    alam_d = din("alam", [256])
    biasg_d = din("biasg", [128, 4 * 2 * 2 * 128])
    maskn_d = din("maskn", [128, 4 * 2 * 2 * 128])
    farb_d = din("farb", [4])
    flag_d = din("flag", [128, 1])
    identb_d = din("identb", [128, 128], BF16)
    identf_d = din("identf", [128, 128])
    umask_d = din("umask", [128, 128])
    negm4_d = din("negm4", [128, 512])
    out_d = nc.dram_tensor("out", [NOWN, D], F32, kind="ExternalOutput").ap()
    wb_d = nc.dram_tensor("wb_scr", [128, WPAD], BF16, kind="Internal").ap()
    kt_d = nc.dram_tensor("kt_scr", [4, 128, NBLK * 128], BF16, kind="Internal").ap()
    va_d = nc.dram_tensor("va_scr", [4, 128, NBLK * 130], BF16, kind="Internal").ap()

    with ExitStack() as top:
        S = Sched(nc, top)

        uid = [0]

        def sbt(st, name, shape, dt=F32):
            uid[0] += 1
            return st.enter_context(nc.sbuf_tensor("s%d_%s" % (uid[0], name), list(shape), dt))

        banks = [top.enter_context(nc.psum_tensor("bank%d" % i, [128, 512], F32)) for i in range(8)]
        bbufs = [S.buf("bank%d" % i) for i in range(8)]
        for b_ in bbufs:
            b_.psum = True
        bank_rr = [0]

        def nbank(lo=0, hi=8):
            i = lo + (bank_rr[0] % (hi - lo))
            bank_rr[0] += 1
            return banks[i], bbufs[i]

        def bfview(bank):
            return bank[:].bitcast(BF16)

        identb = sbt(top, "identb", [128, 128], BF16)
        identf = sbt(top, "identf", [128, 128])
        umask = sbt(top, "umask", [128, 128])
        negm4 = sbt(top, "negm4", [128, 512])
        ones_f = sbt(top, "ones_f", [128, 128])
        flag = sbt(top, "flag", [128, 1])
        epst = sbt(top, "epst", [128, 1])
        onet = sbt(top, "onet", [128, 1])
        mult1 = sbt(top, "mult1", [128, 8])
        shift1 = sbt(top, "shift1", [128, 8])
        mult2 = sbt(top, "mult2", [128, 8])
        shift2 = sbt(top, "shift2", [128, 8])
        gate1 = sbt(top, "gate1", [128, D])
        gate2 = sbt(top, "gate2", [128, D])
        mcw = sbt(top, "mcw", [128, 8, 4])
        mcb = sbt(top, "mcb", [128, 8])
        mcwf = sbt(top, "mcwf", [128, 8, 4])
        fcwf = sbt(top, "fcwf", [128, 44, 3])
        mcnb = sbt(top, "mcnb", [128, 8])
        fcw = sbt(top, "fcw", [128, 44, 3])
        fcb = sbt(top, "fcb", [128, 44])
        fcnb = sbt(top, "fcnb", [128, 44])
        gifb = sbt(top, "gifb", [128, 8])
        mng = sbt(top, "mng", [128, 512])
        ang = sbt(top, "ang", [128, 512])
        gq = sbt(top, "gq", [128, 512])
        gk = sbt(top, "gk", [128, 512])
        gqfm = sbt(top, "gqfm", [128, 1])
        gkfm = sbt(top, "gkfm", [128, 1])
        mbc = sbt(top, "mbc", [128, 8, 3])
        fbc = sbt(top, "fbc", [128, 44, 2])
        ctmp = sbt(top, "ctmp", [128, 44])
        b_mbc = S.buf("mbc")
        b_fbc = S.buf("fbc")
        biasb = sbt(top, "biasb", [128, 4, 2, 256], BF16)
        farb = sbt(top, "farb", [128, 4])
        neglam = sbt(top, "neglam", [128, 1])
        Sst = sbt(top, "Sst", [128, 4, 130])
        Sstb = sbt(top, "Sstb", [128, 4, 130], BF16)
        mhalo = sbt(top, "mhalo", [128, 8, 3])
        fhalo = sbt(top, "fhalo", [128, 44, 2])
        CONST = S.buf("const")
        b_S = S.buf("Sst")
        b_Sb = S.buf("Sstb")
        b_mhalo = S.buf("mhalo")
        b_fhalo = S.buf("fhalo")
        b_wb = S.buf("wb_scr", grp="wbw")
        b_ktd = S.buf("kt_scr", grp="kvw")
        b_vad = S.buf("va_scr", grp="kvw")
        b_out = S.buf("outd", grp="outw")

        def ld_const(t, src, grp="cst"):
            b = S.buf("c_" + t.name, grp=grp)
            S.dma("sp", t[:], src, writes=[b])
            return b

        cb = []
        cb.append(ld_const(identb, identb_d[:, :]))
        cb.append(ld_const(identf, identf_d[:, :]))
        cb.append(ld_const(umask, umask_d[:, :]))
        cb.append(ld_const(negm4, negm4_d[:, :]))
        cb.append(ld_const(flag, flag_d[:, :]))
        cb.append(ld_const(mcw, mcw_d.rearrange("p (c k) -> p c k", k=4)))
        cb.append(ld_const(mcb, mcb_d[:, :]))
        cb.append(ld_const(fcw, fcw_d.rearrange("p (c k) -> p c k", k=3)))
        cb.append(ld_const(fcb, fcb_d[:, :]))
        cb.append(ld_const(gifb, gifb_d.partition_broadcast(128)))
        cb.append(ld_const(mng, mng_d.partition_broadcast(128)))
        cb.append(ld_const(ang, ang_d.partition_broadcast(128)))
        cb.append(ld_const(gq, gq_d.partition_broadcast(128)))
        cb.append(ld_const(gk, gk_d.partition_broadcast(128)))
        cb.append(ld_const(farb, farb_d.partition_broadcast(128)))
        cb.append(ld_const(gqfm, gqfm_d[:, :]))
        cb.append(ld_const(gkfm, gkfm_d[:, :]))
        S.seal(cb, "cst")
        S.op("dve", lambda e: e.memset(ones_f[:], 1.0), writes=[CONST])
        S.op("dve", lambda e: e.memset(epst[:], EPS), writes=[CONST])
        S.op("dve", lambda e: e.memset(onet[:], 1.0), writes=[CONST])
        S.op("dve", lambda e: e.memset(Sst[:], 0.0), writes=[b_S])
        S.op("dve", lambda e: e.memset(Sstb[:], 0.0), writes=[b_Sb])
        S.op("dve", lambda e: e.memset(mhalo[:], 0.0), writes=[b_mhalo])
        S.op("dve", lambda e: e.memset(fhalo[:], 0.0), writes=[b_fhalo])
        S.op("dve", lambda e: e.tensor_scalar(out=mcnb[:], in0=mcb[:], scalar1=-1.0, scalar2=None, op0=ALU.mult), reads=cb, writes=[CONST])
        S.op("dve", lambda e: e.tensor_scalar(out=fcnb[:], in0=fcb[:], scalar1=-1.0, scalar2=None, op0=ALU.mult), reads=cb, writes=[CONST])
        S.op("dve", lambda e: e.tensor_scalar(out=gqfm[:], in0=gqfm[:], scalar1=0.125, scalar2=None, op0=ALU.mult), reads=cb, writes=[CONST])
        S.op("dve", lambda e: e.tensor_scalar(out=mcwf[:], in0=mcw[:], scalar1=flag[:, 0:1], scalar2=None, op0=ALU.mult), reads=cb, writes=[CONST])
        S.op("dve", lambda e: e.tensor_scalar(out=fcwf[:], in0=fcw[:], scalar1=flag[:, 0:1], scalar2=None, op0=ALU.mult), reads=cb, writes=[CONST])
        S.op("dve", lambda e: e.tensor_scalar(out=ang[:], in0=ang[:], scalar1=1.0 - LAM_INIT, scalar2=None, op0=ALU.mult), reads=cb, writes=[CONST])

        with ExitStack() as st:
            cfm = sbt(st, "cfm", [128, 8])
            sc = sbt(st, "sc", [128, 8])
            sct = sbt(st, "sct", [128, 8])
            scbc = sbt(st, "scbc", [128, 8, 128])
            badafm = sbt(st, "badafm", [128, 48])
            g1fm = sbt(st, "g1fm", [128, 8])
            g2fm = sbt(st, "g2fm", [128, 8])
            modfm = sbt(st, "modfm", [128, 48])
            alam = sbt(st, "alam", [128, 256])
            lt = sbt(st, "lt", [128, 128])
            ls = sbt(st, "ls", [128, 2])
            biasg = sbt(st, "biasg", [128, 2048])
            maskn = sbt(st, "maskn", [128, 2048])
            wst = [sbt(st, "wst%d" % i, [128, 4096]) for i in range(2)]
            wbo = [sbt(st, "wbo%d" % i, [128, 4096], BF16) for i in range(2)]
            bbt = [sbt(st, "bbt%d" % i, [128, 512]) for i in range(2)]
            b_wst = [S.buf("wst%d" % i) for i in range(2)]
            b_wbo = [S.buf("wbo%d" % i) for i in range(2)]
            b_bbt = [S.buf("bbt%d" % i) for i in range(2)]
            L = S.buf("prel")
            b_l = [ld_const(cfm, cfm_d[:, :], "cst2"), ld_const(badafm, badafm_d[:, :], "cst2"), ld_const(g1fm, g1fm_d[:, :], "cst2"),
                   ld_const(g2fm, g2fm_d[:, :], "cst2"), ld_const(alam, alam_d.partition_broadcast(128), "cst2"),
                   ld_const(biasg, biasg_d[:, :], "cst2"), ld_const(maskn, maskn_d[:, :], "cst2")]
            S.seal(b_l, "cst2")
            S.op("act", lambda e: e.activation(out=sct[:], in_=cfm[:], func=AF.Exp, scale=-1.0), reads=b_l, writes=[L])
            S.op("dve", lambda e: e.tensor_scalar(out=sct[:], in0=sct[:], scalar1=1.0, scalar2=None, op0=ALU.add), reads=[L], writes=[L])
            S.op("dve", lambda e: e.reciprocal(out=sct[:], in_=sct[:]), reads=[L], writes=[L])
            S.op("dve", lambda e: e.tensor_tensor(out=sc[:], in0=cfm[:], in1=sct[:], op=ALU.mult), reads=[L], writes=[L])
            S.op("dve", lambda e: e.tensor_copy(out=scbc[:], in_=sc[:].unsqueeze(2).to_broadcast([128, 8, 128])), reads=[L], writes=[L])
            S.op("dve", lambda e: e.tensor_tensor(out=lt[:, 0:64], in0=alam[:, 0:64], in1=alam[:, 64:128], op=ALU.mult), reads=b_l, writes=[L])
            S.op("dve", lambda e: e.tensor_tensor(out=lt[:, 64:128], in0=alam[:, 128:192], in1=alam[:, 192:256], op=ALU.mult), reads=[L], writes=[L])
            S.op("dve", lambda e: e.tensor_reduce(out=ls[:], in_=lt[:].rearrange("p (a b) -> p a b", a=2), axis=AX.X, op=ALU.add), reads=[L], writes=[L])
            S.op("act", lambda e: e.activation(out=ls[:], in_=ls[:], func=AF.Exp), reads=[L], writes=[L])
            S.op("dve", lambda e: e.tensor_tensor(out=neglam[:], in0=ls[:, 1:2], in1=ls[:, 0:1], op=ALU.subtract), reads=[L], writes=[CONST])
            S.op("dve", lambda e: e.tensor_scalar(out=neglam[:], in0=neglam[:], scalar1=-LAM_INIT, scalar2=None, op0=ALU.add), reads=[CONST], writes=[CONST])
            S.op("dve", lambda e: e.tensor_tensor(out=biasb[:].rearrange("p a b c -> p (a b c)"), in0=biasg[:], in1=maskn[:], op=ALU.add), reads=b_l, writes=[CONST])

            fm_ps, fm_b = banks[7], bbufs[7]
            fm_cols = {0: 0, 1: 4, 2: 8, 3: 12, 6: 24, 7: 28, 8: 32, 9: 36}
            for pc in range(12):
                i = pc % 2
                S.dma("sp", wst[i][:], wada_d[pc, :, :], writes=[b_wst[i]])
                wv = wst[i][:].rearrange("p (c n) -> p c n", c=8)
                if pc in (4, 5, 10, 11):
                    S.dma("sp", bbt[i][:], bada_d[pc * 512:(pc + 1) * 512].partition_broadcast(128), writes=[b_bbt[i]])
                    pb, pbb = nbank(0, 6)
                    for dc in range(8):
                        S.op("pe", lambda e: e.matmul(pb[:], lhsT=scbc[:, dc, :], rhs=wv[:, dc, :], start=(dc == 0), stop=(dc == 7)),
                             reads=[L, b_wst[i]], writes=[pbb])
                    gt = gate1 if pc < 6 else gate2
                    off = (pc % 2) * 512
                    S.op("dve", lambda e: e.tensor_tensor(out=gt[:, off:off + 512], in0=pb[:], in1=bbt[i][:], op=ALU.add),
                         reads=[pbb, b_bbt[i]], writes=[CONST])
                else:
                    for k in range(4):
                        col = fm_cols[pc] + k
                        for dc in range(8):
                            S.op("pe", lambda e: e.matmul(fm_ps[:, col:col + 1], lhsT=wv[:, dc, k * 128:(k + 1) * 128], rhs=sc[:, dc:dc + 1],
                                                         start=(dc == 0), stop=(dc == 7)), reads=[L, b_wst[i]], writes=[fm_b])
            S.op("dve", lambda e: e.tensor_tensor(out=modfm[:, 0:16], in0=fm_ps[:, 0:16], in1=badafm[:, 0:16], op=ALU.add), reads=[fm_b] + b_l, writes=[L])
            S.op("dve", lambda e: e.tensor_tensor(out=modfm[:, 24:40], in0=fm_ps[:, 24:40], in1=badafm[:, 24:40], op=ALU.add), reads=[fm_b] + b_l, writes=[L])
            S.op("dve", lambda e: e.scalar_tensor_tensor(out=mult1[:], in0=modfm[:, 8:16], scalar=1.0, in1=g1fm[:], op0=ALU.add, op1=ALU.mult), reads=[L], writes=[CONST])
            S.op("dve", lambda e: e.tensor_copy(out=shift1[:], in_=modfm[:, 0:8]), reads=[L], writes=[CONST])
            S.op("dve", lambda e: e.scalar_tensor_tensor(out=mult2[:], in0=modfm[:, 32:40], scalar=1.0, in1=g2fm[:], op0=ALU.add, op1=ALU.mult), reads=[L], writes=[CONST])
            S.op("dve", lambda e: e.tensor_copy(out=shift2[:], in_=modfm[:, 24:32]), reads=[L], writes=[CONST])

            cast_eng = ["dve", "act"]
            for ch in range(WPAD // 4096):
                i = ch % 2
                S.dma("sp", wst[i][:], wall_d[:, ch * 4096:(ch + 1) * 4096], writes=[b_wst[i]])
                ce = cast_eng[ch % 2]
                if ce == "act":
                    S.op("act", lambda e: e.copy(out=wbo[i][:], in_=wst[i][:]), reads=[b_wst[i]], writes=[b_wbo[i]])
                else:
                    S.op(ce, lambda e: e.tensor_copy(out=wbo[i][:], in_=wst[i][:]), reads=[b_wst[i]], writes=[b_wbo[i]])
                S.dma("pool", wb_d[:, ch * 4096:(ch + 1) * 4096], wbo[i][:], reads=[b_wbo[i]], writes=[b_wb])
            S.barrier()

        def wload(st, name, piece, shape3=None):
            off, n = PIECES[piece]
            t = sbt(st, name, [128, n], BF16)
            b = S.buf(name)
            S.dma("sp", t[:], wb_d[:, off:off + n], reads=[b_wb], writes=[b])
            return t, b

        for u in range(NU):
            full = u >= NCTX
            flagged = u < NFLAG
            own = u >= NFLAG
            with ExitStack() as su:
                xs = sbt(su, "xs", [128, 4, D])
                hT = sbt(su, "hT", [128, 8, 512], BF16)
                mqT = sbt(su, "mqT", [128, 8, 512], BF16)
                mVA = sbt(su, "mVA", [128, 4, 4, 130], BF16)
                sigmo = sbt(su, "sigmo", [128, 4, 512])
                gif = sbt(su, "gif", [128, 4, 8])
                Qbd = sbt(su, "Qbd", [128, 4, 4, 256], BF16)
                hmT = sbt(su, "hmT", [128, 4, 512], BF16)
                haT = sbt(su, "haT", [128, 4, 512], BF16)
                b_xs = [S.buf("xs%d" % j) for j in range(4)]
                b_hT = S.buf("hT")
                b_mqT = S.buf("mqT")
                b_mVA = S.buf("mVA")
                b_sig = S.buf("sigmo")
                b_gif = S.buf("gif")
                b_Qbd = S.buf("Qbd")
                b_hmT = S.buf("hmT")
                b_haT = S.buf("haT")
                if full:
                    S.op("dve", lambda e: e.memset(Qbd[:], 0.0), writes=[b_Qbd])

                def norm_block(st, j, mult, shift, hdst, b_hdst, tagp):
                    junk = sbt(st, tagp + "junk%d" % j, [128, D], BF16)
                    xn = sbt(st, tagp + "xn%d" % j, [128, D], BF16)
                    ss = sbt(st, tagp + "ss%d" % j, [128, 2])
                    bl = S.buf("nb")
                    S.op("act", lambda e: e.activation(out=junk[:], in_=xs[:, j, :], func=AF.Square, scale=1.0 / 32.0, accum_out=ss[:, 0:1]),
                         reads=[b_xs[j]], writes=[bl])
                    S.op("act", lambda e: e.activation(out=ss[:, 1:2], in_=ss[:, 0:1], func=AF.Ln, bias=epst[:, 0:1], scale=1.0), reads=[bl, CONST], writes=[bl])
                    S.op("act", lambda e: e.activation(out=ss[:, 1:2], in_=ss[:, 1:2], func=AF.Exp, scale=-0.5), reads=[bl], writes=[bl])
                    if OPT_XN_ACT:
                        S.op("act", lambda e: e.activation(out=xn[:], in_=xs[:, j, :], func=AF.Copy, scale=ss[:, 1:2]),
                             reads=[bl, b_xs[j]], writes=[bl])
                    else:
                        S.op("dve", lambda e: e.tensor_scalar(out=xn[:], in0=xs[:, j, :], scalar1=ss[:, 1:2], scalar2=None, op0=ALU.mult),
                             reads=[bl, b_xs[j]], writes=[bl])
                    pb, pbb = nbank()
                    pv = bfview(pb).rearrange("p (c t) -> p c t", c=8)
                    for c in range(8):
                        S.op("pe", lambda e: e.transpose(out=pv[:, c, :], in_=xn[:, c * 128:(c + 1) * 128], identity=identb[:]),
                             reads=[bl, cb[0]], writes=[pbb])
                    for c in range(8):
                        S.op("act", lambda e: e.activation(out=hdst[:, c, j * 128:(j + 1) * 128], in_=pv[:, c, :], func=AF.Identity,
                                                          scale=mult[:, c:c + 1], bias=shift[:, c:c + 1]), reads=[pbb, CONST], writes=[b_hdst])

                with ExitStack() as st:
                    names = (["mq", "mk", "mv", "gif", "av", "mo", "aq", "ak"] if full else ["mk", "mv", "gif", "av", "ak"])
                    W = {}
                    for j in range(4):
                        blk = u * 4 + j
                        S.dma("sp", xs[:, j, :], x_d[blk * 128:(blk + 1) * 128, :], writes=[b_xs[j]])
                    for nm in names:
                        W[nm] = wload(st, "w_" + nm, nm)
                    for j in range(4):
                        norm_block(st, j, mult1, shift1, hT, b_hT, "n1")
                    if flagged:
                        S.op("dve", lambda e: e.tensor_copy(out=mVA[:, :, :, 128:130], in_=flag[:, 0:1].unsqueeze(1).unsqueeze(1).to_broadcast([128, 4, 4, 2])),
                             reads=[cb[4]], writes=[b_mVA])
                    else:
                        S.op("dve", lambda e: e.memset(mVA[:, :, :, 128:130], 1.0), writes=[b_mVA])
                    acc = [sbt(st, "acc%d" % i, [128, 512]) for i in range(4)]
                    et = [sbt(st, "et%d" % i, [128, 512]) for i in range(4)]
                    b_acc = [S.buf("acc%d" % i) for i in range(4)]
                    b_et = [S.buf("et%d" % i) for i in range(4)]
                    hv = mhalo
                    if not OPT_BATCH:
                        S.op("dve", lambda e: e.memset(mbc[:], 0.0), writes=[b_mbc])
                    if OPT_BATCH:
                        S.op("dve", lambda e: e.tensor_tensor(out=mbc[:, :, 2], in0=hv[:, :, 2], in1=mcw[:, :, 0], op=ALU.mult), reads=[b_mhalo] + cb, writes=[b_mbc])
                        S.op("dve", lambda e: e.tensor_tensor(out=mbc[:, :, 1], in0=hv[:, :, 2], in1=mcw[:, :, 1], op=ALU.mult), reads=[b_mhalo] + cb, writes=[b_mbc])
                        S.op("dve", lambda e: e.tensor_tensor(out=mbc[:, :, 0], in0=hv[:, :, 2], in1=mcw[:, :, 2], op=ALU.mult), reads=[b_mhalo] + cb, writes=[b_mbc])
                        S.op("dve", lambda e: e.tensor_tensor(out=ctmp[:, 0:8], in0=hv[:, :, 1], in1=mcw[:, :, 0], op=ALU.mult), reads=[b_mhalo] + cb, writes=[b_mbc])
                        S.op("dve", lambda e: e.tensor_tensor(out=mbc[:, :, 1], in0=mbc[:, :, 1], in1=ctmp[:, 0:8], op=ALU.add), reads=[b_mbc], writes=[b_mbc])
                        S.op("dve", lambda e: e.tensor_tensor(out=ctmp[:, 8:16], in0=hv[:, :, 1], in1=mcw[:, :, 1], op=ALU.mult), reads=[b_mhalo] + cb, writes=[b_mbc])
                        S.op("dve", lambda e: e.tensor_tensor(out=mbc[:, :, 0], in0=mbc[:, :, 0], in1=ctmp[:, 8:16], op=ALU.add), reads=[b_mbc], writes=[b_mbc])
                        S.op("dve", lambda e: e.tensor_tensor(out=ctmp[:, 16:24], in0=hv[:, :, 0], in1=mcw[:, :, 0], op=ALU.mult), reads=[b_mhalo] + cb, writes=[b_mbc])
                        S.op("dve", lambda e: e.tensor_tensor(out=mbc[:, :, 0], in0=mbc[:, :, 0], in1=ctmp[:, 16:24], op=ALU.add), reads=[b_mbc], writes=[b_mbc])
                        S.op("dve", lambda e: e.tensor_tensor(out=mbc[:], in0=mbc[:], in1=mcb[:].unsqueeze(2).to_broadcast([128, 8, 3]), op=ALU.add), reads=[b_mbc] + cb, writes=[b_mbc])
                    groups = ([[0, 1, 2, 3], [4, 5, 6, 7]] if full else [[4, 5, 6, 7]])
                    wsel = mcwf if flagged else mcw
                    for grp in groups:
                        pbs = []
                        for k, c in enumerate(grp):
                            wt, wbf = W["mq" if c < 4 else "mk"]
                            wv = wt[:].rearrange("p (c n) -> p c n", c=8)
                            pb, pbb = nbank()
                            pbs.append((pb, pbb))
                            for dc in range(8):
                                S.op("pe", lambda e: e.matmul(pb[:], lhsT=wv[:, dc, k * 128:(k + 1) * 128], rhs=hT[:, dc, :], start=(dc == 0), stop=(dc == 7)),
                                     reads=[wbf, b_hT], writes=[pbb])
                            m0 = 3 if OPT_PARTMAIN else 0
                            S.op("act", lambda e: e.activation(out=acc[k][:, m0:512], in_=pb[:, m0:512], func=AF.Identity, scale=wsel[:, c, 3:4], bias=mcb[:, c:c + 1]),
                                 reads=[pbb, CONST] + cb, writes=[b_acc[k]])
                            for col in (range(3) if OPT_TINY else ()):
                                S.op("act", lambda e: e.activation(out=acc[k][:, col:col + 1], in_=pb[:, col:col + 1], func=AF.Identity, scale=wsel[:, c, 3:4],
                                                                  bias=mbc[:, c, col:col + 1]), reads=[pbb, CONST, b_mbc] + cb, writes=[b_acc[k]])
                            if flagged:
                                S.op("act", lambda e: e.activation(out=mhalo[:, c, :], in_=pb[:, 509:512], func=AF.Copy, scale=flag[:, 0:1]), reads=[pbb, cb[4], b_mbc], writes=[b_mhalo])
                            else:
                                S.op("act", lambda e: e.copy(out=mhalo[:, c, :], in_=pb[:, 509:512]), reads=[pbb, b_mbc], writes=[b_mhalo])
                        for tp in (2, 1, 0):
                            sh = 3 - tp
                            for k, c in enumerate(grp):
                                pb, pbb = pbs[k]
                                S.op("dve", lambda e: e.scalar_tensor_tensor(out=acc[k][:, sh:512], in0=pb[:, 0:512 - sh], scalar=wsel[:, c, tp:tp + 1], in1=acc[k][:, sh:512],
                                                                            op0=ALU.mult, op1=ALU.add), reads=[pbb, b_acc[k], CONST], writes=[b_acc[k]])
                        for k, c in enumerate(grp):
                            if not OPT_SILU:
                                S.op("act", lambda e: e.activation(out=et[k][:], in_=acc[k][:], func=AF.Exp, scale=-1.0), reads=[b_acc[k]], writes=[b_et[k]])
                                sk = (128.0 ** 0.5) if c < 4 else 1.0
                                S.op("dve", lambda e: e.tensor_scalar(out=et[k][:], in0=et[k][:], scalar1=1.0, scalar2=sk, op0=ALU.add, op1=ALU.mult), reads=[b_et[k]], writes=[b_et[k]])
                                S.op("dve", lambda e: e.reciprocal(out=et[k][:], in_=et[k][:]), reads=[b_et[k]], writes=[b_et[k]])
                                S.op("dve", lambda e: e.tensor_tensor(out=mqT[:, c, :], in0=acc[k][:], in1=et[k][:], op=ALU.mult), reads=[b_acc[k], b_et[k]], writes=[b_mqT])
                            elif c < 4:
                                S.op("act", lambda e: e.activation(out=et[k][:], in_=acc[k][:], func=AF.Silu), reads=[b_acc[k]], writes=[b_et[k]])
                                S.op("dve", lambda e: e.tensor_scalar(out=mqT[:, c, :], in0=et[k][:], scalar1=128.0 ** -0.5, scalar2=None, op0=ALU.mult),
                                     reads=[b_et[k]], writes=[b_mqT])
                            else:
                                S.op("act", lambda e: e.activation(out=mqT[:, c, :], in_=acc[k][:], func=AF.Silu), reads=[b_acc[k]], writes=[b_mqT])
                    sq = [sbt(st, "sq%d" % i, [128, 512]) for i in range(2)]
                    qb = [sbt(st, "qb%d" % i, [128, 512], BF16) for i in range(2)]
                    rs = [sbt(st, "rs%d" % i, [128, 16]) for i in range(2)]
                    KTb = [sbt(st, "KTb%d" % i, [128, 4, 128], BF16) for i in range(2)]
                    VAb = [sbt(st, "VAb%d" % i, [128, 4, 130], BF16) for i in range(2)]
                    b_t = [S.buf("tmj%d" % i) for i in range(2)]
                    b_KTb = [S.buf("KTb%d" % i) for i in range(2)]
                    b_VAb = [S.buf("VAb%d" % i) for i in range(2)]
                    for i in range(2):
                        if flagged:
                            S.op("dve", lambda e: e.tensor_copy(out=VAb[i][:, :, 128:130], in_=flag[:, 0:1].unsqueeze(1).to_broadcast([128, 4, 2])),
                                 reads=[cb[4]], writes=[b_VAb[i]])
                        else:
                            S.op("dve", lambda e: e.memset(VAb[i][:, :, 128:130], 1.0), writes=[b_VAb[i]])
                    for j in range(4):
                        blk = u * 4 + j
                        i = j % 2
                        tsl = slice(j * 128, (j + 1) * 128)

                        def proj(nm):
                            wt, wbf = W[nm]
                            n = PIECES[nm][1] // 8
                            wv = wt[:].rearrange("p (c n) -> p c n", c=8)
                            pb, pbb = nbank()
                            for dc in range(8):
                                S.op("pe", lambda e: e.matmul(pb[:, 0:n], lhsT=hT[:, dc, tsl], rhs=wv[:, dc, :], start=(dc == 0), stop=(dc == 7)),
                                     reads=[wbf, b_hT], writes=[pbb])
                            return pb, pbb

                        def evac_scaled(out_ap, in_ap, rd, wr):
                            if flagged:
                                S.op("act", lambda e: e.activation(out=out_ap, in_=in_ap, func=AF.Copy, scale=flag[:, 0:1]), reads=rd + [cb[4]], writes=wr)
                            else:
                                S.op("act", lambda e: e.copy(out=out_ap, in_=in_ap), reads=rd, writes=wr)

                        pb, pbb = proj("mv")
                        evac_scaled(mVA[:, j, :, 0:128], pb[:].rearrange("p (h d) -> p h d", h=4), [pbb], [b_mVA])
                        pb, pbb = proj("gif")
                        S.op("dve", lambda e: e.tensor_tensor(out=gif[:, j, :], in0=pb[:, 0:8], in1=gifb[:], op=ALU.add), reads=[pbb, cb[9]], writes=[b_gif])
                        pb, pbb = proj("av")
                        evac_scaled(VAb[i][:, :, 0:128], pb[:].rearrange("p (h d) -> p h d", h=4), [pbb], [b_VAb[i]])
                        S.dma("pool", va_d.rearrange("h p (b c) -> p h b c", c=130)[:, :, blk, :], VAb[i][:], reads=[b_VAb[i]], writes=[b_vad])
                        if full:
                            pb, pbb = proj("mo")
                            S.op("act", lambda e: e.activation(out=sigmo[:, j, :], in_=pb[:], func=AF.Exp, scale=-1.0), reads=[pbb], writes=[b_sig])
                            S.op("dve", lambda e: e.tensor_scalar(out=sigmo[:, j, :], in0=sigmo[:, j, :], scalar1=1.0, scalar2=None, op0=ALU.add), reads=[b_sig], writes=[b_sig])
                            S.op("dve", lambda e: e.reciprocal(out=sigmo[:, j, :], in_=sigmo[:, j, :]), reads=[b_sig], writes=[b_sig])
                        for nm in (["aq", "ak"] if full else ["ak"]):
                            pb, pbb = proj(nm)
                            S.op("act", lambda e: e.activation(out=sq[i][:], in_=pb[:], func=AF.Square, scale=0.125), reads=[pbb], writes=[b_t[i]])
                            S.op("dve", lambda e: e.tensor_reduce(out=rs[i][:, 0:8], in_=sq[i][:].rearrange("p (g d) -> p g d", d=64), axis=AX.X, op=ALU.add),
                                 reads=[b_t[i]], writes=[b_t[i]])
                            S.op("act", lambda e: e.activation(out=rs[i][:, 8:16], in_=rs[i][:, 0:8], func=AF.Ln, bias=epst[:, 0:1], scale=1.0), reads=[b_t[i], CONST], writes=[b_t[i]])
                            S.op("act", lambda e: e.activation(out=rs[i][:, 8:16], in_=rs[i][:, 8:16], func=AF.Exp, scale=-0.5), reads=[b_t[i]], writes=[b_t[i]])
                            S.op("dve", lambda e: e.tensor_tensor(out=qb[i][:].rearrange("p (g d) -> p g d", d=64), in0=pb[:].rearrange("p (g d) -> p g d", d=64),
                                                                 in1=rs[i][:, 8:16].unsqueeze(2).to_broadcast([128, 8, 64]), op=ALU.mult), reads=[pbb, b_t[i]], writes=[b_t[i]])
                            tb, tbb = nbank()
                            tv = bfview(tb)[:, 0:512].rearrange("p (h t) -> p h t", h=4)
                            for h in range(4):
                                S.op("pe", lambda e: e.transpose(out=tv[:, h, :], in_=qb[i][:, h * 128:(h + 1) * 128], identity=identb[:]), reads=[b_t[i], cb[0]], writes=[tbb])
                            if nm == "aq":
                                if OPT_GFOLD:
                                    S.op("act", lambda e: e.activation(out=Qbd[0:64, :, j, 0:128], in_=tv[0:64, :, :], func=AF.Copy, scale=gqfm[0:64, 0:1]), reads=[tbb, CONST] + cb, writes=[b_Qbd])
                                    S.op("act", lambda e: e.activation(out=Qbd[64:128, :, j, 128:256], in_=tv[64:128, :, :], func=AF.Copy, scale=gqfm[64:128, 0:1]), reads=[tbb, CONST] + cb, writes=[b_Qbd])
                                else:
                                    S.op("act", lambda e: e.copy(out=Qbd[0:64, :, j, 0:128], in_=tv[0:64, :, :]), reads=[tbb, CONST] + cb, writes=[b_Qbd])
                                    S.op("act", lambda e: e.copy(out=Qbd[64:128, :, j, 128:256], in_=tv[64:128, :, :]), reads=[tbb, CONST] + cb, writes=[b_Qbd])
                            else:
                                if OPT_GFOLD:
                                    S.op("act", lambda e: e.activation(out=KTb[i][:], in_=tv, func=AF.Copy, scale=gkfm[:, 0:1]), reads=[tbb] + cb, writes=[b_KTb[i]])
                                else:
                                    S.op("act", lambda e: e.copy(out=KTb[i][:], in_=tv), reads=[tbb] + cb, writes=[b_KTb[i]])
                                S.dma("pool", kt_d.rearrange("h p k -> p h k")[:, :, blk * 128:(blk + 1) * 128], KTb[i][:], reads=[b_KTb[i]], writes=[b_ktd])
                    S.barrier()

                def mlstm_block(j, st, bankfn):
                    tsl = slice(j * 128, (j + 1) * 128)
                    lf = sbt(st, "lf%d" % j, [128, 4])
                    e1 = sbt(st, "e1%d" % j, [128, 4])
                    e2 = sbt(st, "e2%d" % j, [128, 4])
                    wsx = sbt(st, "wsx%d" % j, [128, 4])
                    RL = sbt(st, "RL%d" % j, [128, 4, 128])
                    Et = sbt(st, "Et%d" % j, [128, 512])
                    Kw = sbt(st, "Kw%d" % j, [128, 4, 128], BF16)
                    bm = S.buf("ml%d" % j)
                    b_e1 = S.buf("e1")
                    b_ws = S.buf("wsx")
                    b_RL = S.buf("RL")
                    b_Et = S.buf("Et")
                    b_Kw = S.buf("Kw")
                    S.op("act", lambda e: e.activation(out=lf[:], in_=gif[:, j, 4:8], func=AF.Exp, scale=-1.0), reads=[b_gif], writes=[bm])
                    yield
                    S.op("act", lambda e: e.activation(out=lf[:], in_=lf[:], func=AF.Ln, bias=onet[:, 0:1], scale=1.0), reads=[bm, CONST], writes=[bm])
                    yield
                    S.op("dve", lambda e: e.tensor_scalar(out=lf[:], in0=lf[:], scalar1=-1.0, scalar2=None, op0=ALU.mult), reads=[bm], writes=[bm])
                    yield
                    p1, p1b = bankfn()
                    S.op("pe", lambda e: e.matmul(p1[:, 0:4], lhsT=umask[:], rhs=lf[:], start=True, stop=True), reads=[bm, cb[2]], writes=[p1b])
                    S.op("dve", lambda e: e.tensor_tensor(out=RL[:], in0=umask[:].unsqueeze(1).to_broadcast([128, 4, 128]),
                                                         in1=lf[:].unsqueeze(2).to_broadcast([128, 4, 128]), op=ALU.mult), reads=[bm, cb[2]], writes=[b_RL])
                    yield
                    S.op("dve", lambda e: e.tensor_tensor(out=e1[:], in0=gif[:, j, 0:4], in1=p1[:, 0:4], op=ALU.subtract), reads=[b_gif, p1b], writes=[b_e1])
                    yield
                    RLf = RL[:].rearrange("p h t -> p (h t)")
                    pBc, pBcb = bankfn()
                    S.op("pe", lambda e: e.matmul(pBc[:], lhsT=ones_f[:], rhs=RLf, start=True, stop=True), reads=[b_RL, CONST], writes=[pBcb])
                    yield
                    blast = pBc[:].rearrange("p (h t) -> p h t", h=4)[:, :, 127]
                    S.op("dve", lambda e: e.tensor_tensor(out=e2[:], in0=e1[:], in1=blast, op=ALU.add), reads=[b_e1, pBcb], writes=[b_ws])
                    yield
                    S.op("act", lambda e: e.activation(out=Et[:], in_=pBc[:], func=AF.Exp), reads=[pBcb], writes=[b_Et])
                    S.op("act", lambda e: e.activation(out=wsx[:], in_=e2[:], func=AF.Exp), reads=[b_ws], writes=[b_ws])
                    yield
                    tb, tbb = bankfn()
                    tv = bfview(tb)[:, 0:512].rearrange("p (h t) -> p h t", h=4)
                    for h in range(4):
                        S.op("pe", lambda e: e.transpose(out=tv[:, h, :], in_=mqT[:, 4 + h, tsl], identity=identb[:]), reads=[b_mqT, cb[0]], writes=[tbb])
                    yield
                    S.op("dve", lambda e: e.tensor_tensor(out=Kw[:], in0=tv, in1=wsx[:].unsqueeze(2).to_broadcast([128, 4, 128]), op=ALU.mult),
                         reads=[tbb, b_ws], writes=[b_Kw])
                    yield
                    if full:
                        DT = sbt(st, "DT%d" % j, [128, 4, 128])
                        PT = sbt(st, "PT%d" % j, [128, 4, 128], BF16)
                        qpT = sbt(st, "qpT%d" % j, [128, 4, 128], BF16)
                        numS = sbt(st, "numS%d" % j, [128, 4, 130])
                        rden = sbt(st, "rden%d" % j, [128, 4])
                        hmr = sbt(st, "hmr%d" % j, [128, 4, 128])
                        hsq = sbt(st, "hsq%d" % j, [128, 4, 128])
                        hss = sbt(st, "hss%d" % j, [128, 8])
                        gs = sbt(st, "gs%d" % j, [128, 512])
                        hmf = sbt(st, "hmf%d" % j, [128, 4, 128], BF16)
                        b_o = S.buf("mo%d" % j)
                        b_gs = S.buf("gs")
                        b_num = S.buf("numS")
                        b_DT = S.buf("DT")
                        b_PT = S.buf("PT")
                        b_qp = S.buf("qpT")
                        pBm, pBmb = bankfn()
                        S.op("pe", lambda e: e.matmul(pBm[:], lhsT=ones_f[:], rhs=RLf, start=True, stop=False), reads=[b_RL, CONST], writes=[pBmb])
                        S.op("pe", lambda e: e.matmul(pBm[:], lhsT=identf[:], rhs=negm4[:], start=False, stop=True), reads=[cb[1], cb[3]], writes=[pBmb])
                        S.op("dve", lambda e: e.tensor_tensor(out=qpT[:], in0=mqT[:, 0:4, tsl], in1=Et[:].rearrange("p (h t) -> p h t", h=4), op=ALU.mult),
                             reads=[b_mqT, b_Et], writes=[b_qp])
                        S.op("dve", lambda e: e.tensor_tensor(out=gs[:], in0=mng[:], in1=sigmo[:, j, :], op=ALU.mult), reads=[b_sig] + cb, writes=[b_gs])
                        yield
                        for h in range(4):
                            S.op("act", lambda e: e.activation(out=DT[:, h, :], in_=pBm[:, h * 128:(h + 1) * 128], func=AF.Exp, bias=e1[:, h:h + 1], scale=1.0),
                                 reads=[pBmb, b_e1], writes=[b_DT])
                        yield
                        pA, pAb = bankfn()
                        for h in range(4):
                            S.op("pe", lambda e: e.matmul(pA[:, h * 128:(h + 1) * 128], lhsT=mqT[:, 4 + h, tsl], rhs=mqT[:, h, tsl], start=True, stop=True),
                                 reads=[b_mqT], writes=[pAb])
                        yield
                        S.op("dve", lambda e: e.tensor_tensor(out=PT[:], in0=pA[:].rearrange("p (h t) -> p h t", h=4), in1=DT[:], op=ALU.mult),
                             reads=[pAb, b_DT], writes=[b_PT])
                        yield
                    yield "AB"
                    if full:
                        for hp in range(2):
                            pn, pnb = bankfn()
                            for hh in range(2):
                                h = 2 * hp + hh
                                o = hh * 130
                                S.op("pe", lambda e: e.matmul(pn[:, o:o + 130], lhsT=PT[:, h, :], rhs=mVA[:, j, h, :], start=True, stop=False), reads=[b_PT, b_mVA], writes=[pnb])
                                S.op("pe", lambda e: e.matmul(pn[:, o:o + 130], lhsT=qpT[:, h, :], rhs=Sstb[:, h, :], start=False, stop=True), reads=[b_qp, b_Sb], writes=[pnb])
                            yield
                            S.op("act", lambda e: e.copy(out=numS[:, 2 * hp:2 * hp + 2, :], in_=pn[:, 0:260].rearrange("p (h c) -> p h c", h=2)), reads=[pnb], writes=[b_num])
                            yield
                    Ev = Et[:].rearrange("p (h t) -> p h t", h=4)
                    for hp in range(2):
                        pc_, pcb = bankfn()
                        for hh in range(2):
                            h = 2 * hp + hh
                            o = hh * 130
                            S.op("pe", lambda e: e.matmul(pc_[:, o:o + 130], lhsT=Kw[:, h, :], rhs=mVA[:, j, h, :], start=True, stop=True), reads=[b_Kw, b_mVA], writes=[pcb])
                        yield
                        for hh in range(2):
                            h = 2 * hp + hh
                            o = hh * 130
                            S.op("dve", lambda e: e.scalar_tensor_tensor(out=Sst[:, h, :], in0=Sst[:, h, :], scalar=Ev[:, h, 127:128], in1=pc_[:, o:o + 130],
                                                                        op0=ALU.mult, op1=ALU.add), reads=[b_S, b_Et, pcb], writes=[b_S])
                        yield
                    S.op("act", lambda e: e.copy(out=Sstb[:], in_=Sst[:]), reads=[b_S], writes=[b_Sb])
                    yield "S"
                    if full:
                        S.op("act", lambda e: e.activation(out=rden[:], in_=numS[:, :, 128], func=AF.Abs), reads=[b_num], writes=[b_o])
                        yield
                        S.op("dve", lambda e: e.tensor_scalar(out=rden[:], in0=rden[:], scalar1=1.0, scalar2=None, op0=ALU.max), reads=[b_o], writes=[b_o])
                        S.op("dve", lambda e: e.reciprocal(out=rden[:], in_=rden[:]), reads=[b_o], writes=[b_o])
                        S.op("dve", lambda e: e.tensor_tensor(out=hmr[:], in0=numS[:, :, 0:128], in1=rden[:].unsqueeze(2).to_broadcast([128, 4, 128]), op=ALU.mult),
                             reads=[b_num, b_o], writes=[b_o])
                        yield
                        S.op("act", lambda e: e.activation(out=hsq[:], in_=hmr[:], func=AF.Square, scale=1.0 / math.sqrt(128.0)), reads=[b_o], writes=[b_o])
                        yield
                        S.op("dve", lambda e: e.tensor_reduce(out=hss[:, 0:4], in_=hsq[:], axis=AX.X, op=ALU.add), reads=[b_o], writes=[b_o])
                        yield
                        S.op("act", lambda e: e.activation(out=hss[:, 4:8], in_=hss[:, 0:4], func=AF.Ln, bias=epst[:, 0:1], scale=1.0), reads=[b_o, CONST], writes=[b_o])
                        S.op("act", lambda e: e.activation(out=hss[:, 4:8], in_=hss[:, 4:8], func=AF.Exp, scale=-0.5), reads=[b_o], writes=[b_o])
                        yield
                        for h in range(4):
                            S.op("dve", lambda e: e.scalar_tensor_tensor(out=hmf[:, h, :], in0=hmr[:, h, :], scalar=hss[:, 4 + h:5 + h], in1=gs[:, h * 128:(h + 1) * 128],
                                                                        op0=ALU.mult, op1=ALU.mult), reads=[b_o, b_gs], writes=[b_o])
                        yield
                        tb2, tb2b = bankfn()
                        tv2 = bfview(tb2)[:, 0:512].rearrange("p (h t) -> p h t", h=4)
                        for h in range(4):
                            S.op("pe", lambda e: e.transpose(out=tv2[:, h, :], in_=hmf[:, h, :], identity=identb[:]), reads=[b_o, cb[0]], writes=[tb2b])
                        yield
                        S.op("act", lambda e: e.copy(out=hmT[:, :, tsl], in_=tv2), reads=[tb2b], writes=[b_hmT])
                        yield

                def mlstm_driver(st, bankfn):
                    for j in range(4):
                        for r in mlstm_block(j, st, bankfn):
                            yield

                if not full:
                    with ExitStack() as st:
                        for _ in mlstm_driver(st, nbank):
                            pass
                        S.barrier()

                if full:
                    with ExitStack() as st:
                        NKC = 8
                        kch = [sbt(st, "kch%d" % i, [128, NKC * 128], BF16) for i in range(3)]
                        vch = [sbt(st, "vch%d" % i, [128, NKC, 130], BF16) for i in range(3)]
                        b_kch = [S.buf("kch%d" % i) for i in range(3)]
                        b_vch = [S.buf("vch%d" % i) for i in range(3)]
                        pex = [sbt(st, "pex%d" % i, [128, 512], BF16) for i in range(6)]
                        b_pex = [S.buf("pex%d" % i) for i in range(6)]
                        rr = sbt(st, "rr", [128, 8])
                        har = sbt(st, "har", [128, 4, 128])
                        hsq = sbt(st, "ahsq", [128, 4, 128])
                        hss = sbt(st, "ahss", [128, 8])
                        haf = sbt(st, "haf", [128, 4, 128], BF16)
                        b_a = S.buf("attn_o")
                        nkb_tot = u * 4 + 4
                        chunk_list = [(s0, min(NKC, nkb_tot - s0)) for s0 in range(0, nkb_tot, NKC)]
                        ring = 0
                        LCH = len(chunk_list)
                        mdrv = mlstm_driver(st, lambda: (banks[7], bbufs[7]))
                        n_items_est = 4 * sum((2 if kb_ <= u * 4 - 2 else 0) + sum(1 for p2 in range(2) for jq in (2 * p2, 2 * p2 + 1) if kb_ > u * 4 + 2 * p2 - 2 and kb_ <= u * 4 + jq)
                                              for kb_ in range(nkb_tot))
                        MSTEP = max(1, n_items_est // 150)
                        gcount = [0]
                        loaded = set()

                        def ensure_chunk(g):
                            if g >= 4 * LCH or g in loaded:
                                return
                            loaded.add(g)
                            h_, ci_ = g // LCH, g % LCH
                            s0, nk = chunk_list[ci_]
                            ri = g % 3
                            S.dma("sp", kch[ri][:, 0:nk * 128], kt_d[h_, :, s0 * 128:(s0 + nk) * 128], reads=[b_ktd], writes=[b_kch[ri]])
                            S.dma("sp", vch[ri][:, 0:nk, :], va_d[h_, :, s0 * 130:(s0 + nk) * 130].rearrange("p (b c) -> p b c", c=130), reads=[b_vad], writes=[b_vch[ri]])

                        for h in range(4):
                            accs = [(banks[jq], bbufs[jq]) for jq in range(4)]
                            for jq in range(4):
                                S.op("dve", lambda e: e.memset(accs[jq][0][:, 0:260], 0.0), writes=[accs[jq][1]])
                            items = []
                            for ci, (s0, nk) in enumerate(chunk_list):
                                g = h * LCH + ci
                                ri = g % 3
                                for kk in range(nk):
                                    kb = s0 + kk
                                    for p2 in range(2):
                                        j0 = 2 * p2
                                        if kb <= u * 4 + j0 - 2:
                                            items.append((ri, kk, kb, (j0, j0 + 1), g))
                                        else:
                                            for jq in (j0, j0 + 1):
                                                if kb <= u * 4 + jq:
                                                    items.append((ri, kk, kb, (jq,), g))

                            LAG = 3
                            NS = 3
                            NPX = len(pex)

                            def emit_scores(it, idx):
                                ri, kk, kb, jqs, ci = it
                                ensure_chunk(ci)
                                ensure_chunk(ci + 1)
                                nq = len(jqs)
                                sl = idx % NS
                                sp_, spb = banks[4 + sl], bbufs[4 + sl]
                                sv = sp_[:, 0:256 * nq]
                                px = idx % NPX
                                delta = u * 4 + jqs[0] - kb
                                near = (nq == 1) and delta <= 1
                                rhs = Qbd[:, h, jqs[0]:jqs[0] + nq, :].rearrange("p j c -> p (j c)")
                                S.op("pe", lambda e: e.matmul(sv, lhsT=kch[ri][:, kk * 128:(kk + 1) * 128], rhs=rhs, start=True, stop=not near),
                                     reads=[b_kch[ri], b_Qbd], writes=[spb])
                                if near:
                                    S.op("pe", lambda e: e.matmul(sv, lhsT=identb[:], rhs=biasb[:, h, delta, :], start=False, stop=True), reads=[CONST, cb[0]], writes=[spb])
                                    S.op("act", lambda e: e.activation(out=pex[px][:, 0:256 * nq], in_=sv, func=AF.Exp), reads=[spb], writes=[b_pex[px]])
                                else:
                                    S.op("act", lambda e: e.activation(out=pex[px][:, 0:256 * nq], in_=sv, func=AF.Exp, bias=farb[:, h:h + 1], scale=1.0),
                                         reads=[spb, cb[14]], writes=[b_pex[px]])

                            def emit_pv(it, idx):
                                ri, kk, kb, jqs, ci = it
                                px = idx % NPX
                                for a_, jq in enumerate(jqs):
                                    ab, abb = accs[jq]
                                    for m in range(2):
                                        c0 = a_ * 256 + m * 128
                                        S.op("pe", lambda e: e.matmul(ab[:, m * 130:(m + 1) * 130], lhsT=pex[px][:, c0:c0 + 128], rhs=vch[ri][:, kk, :],
                                                                     start=False, stop=False, skip_group_check=True), reads=[b_pex[px], b_vch[ri]], writes=[abb])

                            for idx in range(len(items) + LAG):
                                if idx < len(items):
                                    emit_scores(items[idx], idx)
                                if idx >= LAG:
                                    emit_pv(items[idx - LAG], idx - LAG)
                                gcount[0] += 1
                                if gcount[0] % MSTEP == 0:
                                    next(mdrv, None)
                            for jq in range(4):
                                ab, abb = accs[jq]
                                av = ab[:, 0:260].rearrange("p (m c) -> p m c", m=2)
                                S.op("dve", lambda e: e.tensor_scalar(out=rr[:, 2 * jq:2 * jq + 2], in0=av[:, :, 128], scalar1=1e-30, scalar2=None, op0=ALU.max), reads=[abb], writes=[b_a])
                                S.op("dve", lambda e: e.reciprocal(out=rr[:, 2 * jq:2 * jq + 2], in_=rr[:, 2 * jq:2 * jq + 2]), reads=[b_a], writes=[b_a])
                                S.op("dve", lambda e: e.tensor_tensor(out=rr[:, 2 * jq + 1:2 * jq + 2], in0=rr[:, 2 * jq + 1:2 * jq + 2], in1=neglam[:], op=ALU.mult),
                                     reads=[b_a, CONST], writes=[b_a])
                                S.op("dve", lambda e: e.tensor_scalar(out=har[:, jq, :], in0=av[:, 0, 0:128], scalar1=rr[:, 2 * jq:2 * jq + 1], scalar2=None, op0=ALU.mult),
                                     reads=[abb, b_a], writes=[b_a])
                                S.op("dve", lambda e: e.scalar_tensor_tensor(out=har[:, jq, :], in0=av[:, 1, 0:128], scalar=rr[:, 2 * jq + 1:2 * jq + 2], in1=har[:, jq, :],
                                                                            op0=ALU.mult, op1=ALU.add), reads=[abb, b_a], writes=[b_a])
                            S.op("act", lambda e: e.activation(out=hsq[:], in_=har[:], func=AF.Square, scale=1.0 / math.sqrt(128.0)), reads=[b_a], writes=[b_a])
                            S.op("dve", lambda e: e.tensor_reduce(out=hss[:, 0:4], in_=hsq[:], axis=AX.X, op=ALU.add), reads=[b_a], writes=[b_a])
                            S.op("act", lambda e: e.activation(out=hss[:, 4:8], in_=hss[:, 0:4], func=AF.Ln, bias=epst[:, 0:1], scale=1.0), reads=[b_a, CONST], writes=[b_a])
                            S.op("act", lambda e: e.activation(out=hss[:, 4:8], in_=hss[:, 4:8], func=AF.Exp, scale=-0.5), reads=[b_a], writes=[b_a])
                            for jq in range(4):
                                S.op("dve", lambda e: e.scalar_tensor_tensor(out=haf[:, jq, :], in0=har[:, jq, :], scalar=hss[:, 4 + jq:5 + jq], in1=ang[:, h * 128:(h + 1) * 128],
                                                                            op0=ALU.mult, op1=ALU.mult), reads=[b_a, CONST] + cb, writes=[b_a])
                            tb, tbb = banks[4 + h % 3], bbufs[4 + h % 3]
                            tv = bfview(tb)[:, 0:512].rearrange("p (j t) -> p j t", j=4)
                            for jq in range(4):
                                S.op("pe", lambda e: e.transpose(out=tv[:, jq, :], in_=haf[:, jq, :], identity=identb[:]), reads=[b_a, cb[0]], writes=[tbb])
                            S.op("act", lambda e: e.copy(out=haT[:, h, :], in_=tv.rearrange("p j t -> p (j t)")), reads=[tbb], writes=[b_haT])
                        for _ in mdrv:
                            pass
                        S.barrier()

                    h2T = hT
                    b_h2T = b_hT
                    with ExitStack() as st:
                        W = {}
                        for nm in ("bm", "gm0", "ba", "ga0", "gm1", "ga1", "out0", "out1"):
                            W[nm] = wload(st, "w_" + nm, nm)
                        yT = sbt(st, "yT", [128, 8, 512], BF16)
                        b_yT = S.buf("yT")
                        sg = [[sbt(st, "sg%d_%d" % (i, a), [128, 512]) for a in range(2)] for i in range(2)]
                        ty = [[sbt(st, "ty%d_%d" % (i, a), [128, 512]) for a in range(2)] for i in range(2)]
                        b_sg = [S.buf("sg%d" % i) for i in range(2)]
                        for c in range(8):
                            i = c % 2
                            res = []
                            for a, (bw, gw, srcT, b_src) in enumerate((("bm", "gm", hmT, b_hmT), ("ba", "ga", haT, b_haT))):
                                wt, wbf = W[bw]
                                wv = wt[:].rearrange("p (k n) -> p k n", k=4)
                                py, pyb = nbank()
                                for k in range(4):
                                    S.op("pe", lambda e: e.matmul(py[:], lhsT=wv[:, k, c * 128:(c + 1) * 128], rhs=srcT[:, k, :], start=(k == 0), stop=(k == 3)),
                                         reads=[wbf, b_src], writes=[pyb])
                                gt_, gbf = W[gw + str(c // 4)]
                                gv = gt_[:].rearrange("p (c n) -> p c n", c=8)
                                pg, pgb = nbank()
                                for dc in range(8):
                                    S.op("pe", lambda e: e.matmul(pg[:], lhsT=gv[:, dc, (c % 4) * 128:(c % 4 + 1) * 128], rhs=hT[:, dc, :], start=(dc == 0), stop=(dc == 7)),
                                         reads=[gbf, b_hT], writes=[pgb])
                                S.op("act", lambda e: e.activation(out=sg[i][a][:], in_=pg[:], func=AF.Sigmoid), reads=[pgb], writes=[b_sg[i]])
                                S.op("dve", lambda e: e.tensor_tensor(out=ty[i][a][:], in0=py[:], in1=sg[i][a][:], op=ALU.mult), reads=[pyb, b_sg[i]], writes=[b_sg[i]])
                            S.op("dve", lambda e: e.tensor_tensor(out=yT[:, c, :], in0=ty[i][0][:], in1=ty[i][1][:], op=ALU.add), reads=[b_sg[i]], writes=[b_yT])
                        tx = [sbt(st, "tx%d" % i, [128, 512]) for i in range(2)]
                        b_tx = [S.buf("tx%d" % i) for i in range(2)]
                        for j in range(4):
                            tsl = slice(j * 128, (j + 1) * 128)
                            for n in range(2):
                                wt, wbf = W["out%d" % n]
                                wv = wt[:].rearrange("p (c n) -> p c n", c=8)
                                po, pob = nbank()
                                for c in range(8):
                                    S.op("pe", lambda e: e.matmul(po[:], lhsT=yT[:, c, tsl], rhs=wv[:, c, :], start=(c == 0), stop=(c == 7)), reads=[wbf, b_yT], writes=[pob])
                                i = (j * 2 + n) % 2
                                S.op("dve", lambda e: e.tensor_tensor(out=tx[i][:], in0=po[:], in1=gate1[:, n * 512:(n + 1) * 512], op=ALU.mult), reads=[pob, CONST], writes=[b_tx[i]])
                                S.op("dve", lambda e: e.tensor_tensor(out=xs[:, j, n * 512:(n + 1) * 512], in0=xs[:, j, n * 512:(n + 1) * 512], in1=tx[i][:], op=ALU.add),
                                     reads=[b_tx[i], b_xs[j]], writes=[b_xs[j]])
                        for j in range(4):
                            norm_block(st, j, mult2, shift2, h2T, b_h2T, "n2")
                        S.barrier()

                    with ExitStack() as st:
                        actT = sbt(st, "actT", [128, NF, 512], BF16)
                        b_actT = S.buf("actT")
                        wu = [None, None, None]
                        uu = [sbt(st, "fu%d" % i, [128, 512]) for i in range(8)]
                        b_uu = [S.buf("fu%d" % i) for i in range(8)]
                        fe = [sbt(st, "fe%d" % i, [128, 512]) for i in range(2)]
                        b_fe = [S.buf("fe%d" % i) for i in range(2)]
                        wus = [sbt(st, "wup%d" % i, [128, 4096], BF16) for i in range(3)]
                        b_wus = [S.buf("wup%d" % i) for i in range(3)]

                        def ldup(jj):
                            off, n = PIECES["up%d" % jj]
                            S.dma("sp", wus[jj % 3][:], wb_d[:, off:off + n], reads=[b_wb], writes=[b_wus[jj % 3]])

                        ldup(0)
                        ldup(1)
                        wd, wdb = wload(st, "w_down", "down")
                        wdv = wd[:].rearrange("p (f n) -> p f n", f=NF)
                        if not OPT_BATCH:
                            S.op("dve", lambda e: e.memset(fbc[:], 0.0), writes=[b_fbc])
                        if OPT_BATCH:
                            S.op("dve", lambda e: e.tensor_tensor(out=fbc[:, :, 1], in0=fhalo[:, :, 1], in1=fcw[:, :, 0], op=ALU.mult), reads=[b_fhalo] + cb, writes=[b_fbc])
                            S.op("dve", lambda e: e.tensor_tensor(out=fbc[:, :, 0], in0=fhalo[:, :, 1], in1=fcw[:, :, 1], op=ALU.mult), reads=[b_fhalo] + cb, writes=[b_fbc])
                            S.op("dve", lambda e: e.tensor_tensor(out=ctmp[:], in0=fhalo[:, :, 0], in1=fcw[:, :, 0], op=ALU.mult), reads=[b_fhalo] + cb, writes=[b_fbc])
                            S.op("dve", lambda e: e.tensor_tensor(out=fbc[:, :, 0], in0=fbc[:, :, 0], in1=ctmp[:], op=ALU.add), reads=[b_fbc], writes=[b_fbc])
                            S.op("dve", lambda e: e.tensor_tensor(out=fbc[:], in0=fbc[:], in1=fcb[:].unsqueeze(2).to_broadcast([128, 44, 2]), op=ALU.add), reads=[b_fbc] + cb, writes=[b_fbc])

                        def ffn_piece(jj):
                            if jj + 2 < 11:
                                ldup(jj + 2)
                            wv = wus[jj % 3][:].rearrange("p (c n) -> p c n", c=8)
                            wbf = b_wus[jj % 3]
                            wsel = fcwf if flagged else fcw
                            pbs = []
                            for k in range(4):
                                q = jj * 4 + k
                                pb, pbb = nbank()
                                pbs.append((pb, pbb))
                                for dc in range(8):
                                    S.op("pe", lambda e: e.matmul(pb[:], lhsT=wv[:, dc, k * 128:(k + 1) * 128], rhs=h2T[:, dc, :], start=(dc == 0), stop=(dc == 7)),
                                         reads=[wbf, b_h2T], writes=[pbb])
                                kk_ = (jj % 2) * 4 + k
                                m0 = 2 if OPT_PARTMAIN else 0
                                S.op("act", lambda e: e.activation(out=uu[kk_][:, m0:512], in_=pb[:, m0:512], func=AF.Identity, scale=wsel[:, q, 2:3], bias=fcb[:, q:q + 1]),
                                     reads=[pbb, CONST] + cb, writes=[b_uu[kk_]])
                                for col in (range(2) if OPT_TINY else ()):
                                    S.op("act", lambda e: e.activation(out=uu[kk_][:, col:col + 1], in_=pb[:, col:col + 1], func=AF.Identity, scale=wsel[:, q, 2:3],
                                                                      bias=fbc[:, q, col:col + 1]), reads=[pbb, CONST, b_fbc] + cb, writes=[b_uu[kk_]])
                                if flagged:
                                    S.op("act", lambda e: e.activation(out=fhalo[:, q, :], in_=pb[:, 510:512], func=AF.Copy, scale=flag[:, 0:1]), reads=[pbb, cb[4], b_fbc], writes=[b_fhalo])
                                else:
                                    S.op("act", lambda e: e.copy(out=fhalo[:, q, :], in_=pb[:, 510:512]), reads=[pbb, b_fbc], writes=[b_fhalo])
                            for tp in (1, 0):
                                sh = 2 - tp
                                for k in range(4):
                                    q = jj * 4 + k
                                    kk_ = (jj % 2) * 4 + k
                                    pb, pbb = pbs[k]
                                    S.op("dve", lambda e: e.scalar_tensor_tensor(out=uu[kk_][:, sh:512], in0=pb[:, 0:512 - sh], scalar=wsel[:, q, tp:tp + 1], in1=uu[kk_][:, sh:512],
                                                                                op0=ALU.mult, op1=ALU.add), reads=[pbb, b_uu[kk_], CONST], writes=[b_uu[kk_]])
                            yield
                            for i in range(2):
                                f = 2 * jj + i
                                uv = uu[(jj % 2) * 4 + i]
                                ug = uu[(jj % 2) * 4 + 2 + i]
                                b_uv = b_uu[(jj % 2) * 4 + i]
                                b_ug = b_uu[(jj % 2) * 4 + 2 + i]
                                S.op("act", lambda e: e.activation(out=fe[i][:], in_=ug[:], func=AF.Silu), reads=[b_ug], writes=[b_fe[i]])
                                S.op("dve", lambda e: e.tensor_tensor(out=actT[:, f, :], in0=uv[:], in1=fe[i][:], op=ALU.mult), reads=[b_uv, b_fe[i]], writes=[b_actT])
                        fgens = [ffn_piece(jj) for jj in range(11)]
                        for step in range(12):
                            if step < 11:
                                next(fgens[step], None)
                            if step >= 1:
                                next(fgens[step - 1], None)
                        tx = fe
                        b_tx = b_fe
                        for j in range(4):
                            tsl = slice(j * 128, (j + 1) * 128)
                            blk = u * 4 + j
                            for n in range(2):
                                po, pob = nbank()
                                for f in range(NF):
                                    S.op("pe", lambda e: e.matmul(po[:], lhsT=actT[:, f, tsl], rhs=wdv[:, f, n * 512:(n + 1) * 512], start=(f == 0), stop=(f == NF - 1)),
                                         reads=[wdb, b_actT], writes=[pob])
                                i = (j * 2 + n) % 2
                                S.op("dve", lambda e: e.tensor_tensor(out=tx[i][:], in0=po[:], in1=gate2[:, n * 512:(n + 1) * 512], op=ALU.mult), reads=[pob, CONST], writes=[b_tx[i]])
                                S.op("dve", lambda e: e.tensor_tensor(out=xs[:, j, n * 512:(n + 1) * 512], in0=xs[:, j, n * 512:(n + 1) * 512], in1=tx[i][:], op=ALU.add),
                                     reads=[b_tx[i], b_xs[j]], writes=[b_xs[j]])
                            if own:
                                ob = blk - NFLAG * 4
                                S.dma("pool", out_d[ob * 128:(ob + 1) * 128, :], xs[:, j, :], reads=[b_xs[j]], writes=[b_out])
                        S.barrier()
        S.barrier()
    return nc


def _t5_bucket(n):
    n = np.maximum(n, 0)
    max_exact = 16
    nf = np.maximum(n, 1).astype(np.float32)
    large = max_exact + (np.log(nf / np.float32(max_exact)) / np.float32(math.log(128 / max_exact)) * np.float32(32 - max_exact)).astype(np.int32)
    large = np.minimum(large, 31)
    return np.where(n < max_exact, n, large)


def _fm(v, nch):
    return np.ascontiguousarray(np.asarray(v, np.float32).reshape(nch, 128).T)


def _piece_fm(w):
    n = w.shape[1]
    return w.reshape(8, 128, n).transpose(1, 0, 2).reshape(128, 8 * n)


def prepare_inputs(NCTX, NFULL, inputs):
    f32 = np.float32
    g = {k: np.asarray(v) for k, v in inputs.items()}
    NU = NCTX + NFULL
    half_tok = (NU // 2) * 512
    x = g["x"].astype(f32, copy=False)
    B = x.shape[0]
    assert x.shape[1] == 2 * half_tok
    w_in = g["w_in"][0]
    cols = {}
    o = 0
    for nm, n in (("mqk", 1024), ("mv", 512), ("mo", 512), ("mi", 4), ("mf", 4), ("aq", 512), ("ak", 512), ("av", 512), ("gm", 1024), ("ga", 1024)):
        cols[nm] = w_in[:, o:o + n]
        o += n

    def perm_qk(w):
        return w.reshape(1024, 2, 4, 64).transpose(0, 2, 1, 3).reshape(1024, 512)

    wall = np.zeros((128, WPAD), f32)

    def put(nm, arr):
        off, n = PIECES[nm]
        assert arr.shape == (128, n), (nm, arr.shape, n)
        wall[:, off:off + n] = arr

    put("mq", _piece_fm(cols["mqk"][:, 0:512]))
    put("mk", _piece_fm(cols["mqk"][:, 512:1024]))
    put("mv", _piece_fm(cols["mv"]))
    put("mo", _piece_fm(cols["mo"]))
    put("aq", _piece_fm(perm_qk(cols["aq"])))
    put("ak", _piece_fm(perm_qk(cols["ak"])))
    put("av", _piece_fm(cols["av"]))
    put("gm0", _piece_fm(cols["gm"][:, 0:512]))
    put("gm1", _piece_fm(cols["gm"][:, 512:1024]))
    put("ga0", _piece_fm(cols["ga"][:, 0:512]))
    put("ga1", _piece_fm(cols["ga"][:, 512:1024]))
    put("gif", _piece_fm(np.concatenate([cols["mi"], cols["mf"]], axis=1)))
    put("bm", g["w_branch_m"][0].reshape(4, 128, 1024).transpose(1, 0, 2).reshape(128, 4096))
    put("ba", g["w_branch_a"][0].reshape(4, 128, 1024).transpose(1, 0, 2).reshape(128, 4096))
    put("out0", _piece_fm(g["w_out"][0][:, 0:512]))
    put("out1", _piece_fm(g["w_out"][0][:, 512:1024]))
    w_up = g["w_up"][0]
    fcw_full = g["ffn_conv_w"][0]
    fcb_full = g["ffn_conv_b"][0]
    chunk_cols = []
    for jj in range(11):
        cc = [np.arange((2 * jj + i) * 128, (2 * jj + i + 1) * 128) for i in range(2)]
        cc += [DFF + np.arange((2 * jj + i) * 128, (2 * jj + i + 1) * 128) for i in range(2)]
        idx = np.concatenate(cc)
        chunk_cols.append(idx)
        put("up%d" % jj, _piece_fm(w_up[:, idx]))
    allidx = np.concatenate(chunk_cols)
    fcw = fcw_full[:, allidx].reshape(3, 44, 128).transpose(2, 1, 0).reshape(128, 44 * 3)
    fcb = fcb_full[allidx].reshape(44, 128).T
    put("down", g["w_down"][0].reshape(NF, 128, 1024).transpose(1, 0, 2).reshape(128, NF * 1024))

    w_ada = g["w_ada"][0]
    wada = np.stack([_piece_fm(w_ada[:, p * 512:(p + 1) * 512]) for p in range(12)]).astype(f32)
    bada = g["b_ada"][0].astype(f32)
    mcw = g["m_conv_w"][0].reshape(4, 8, 128).transpose(2, 1, 0).reshape(128, 32)
    mcb = _fm(g["m_conv_b"][0], 8)
    gifb = np.concatenate([g["m_igate_b"][0], g["m_fgate_b"][0]]).astype(f32)
    gq = np.tile(g["a_qnorm_g"][0], 8).astype(f32)
    gk = np.tile(g["a_knorm_g"][0], 8).astype(f32)
    rel = g["rel_bias"].astype(f32)
    kk = np.arange(128)[:, None]
    qq = np.arange(128)[None, :]
    biasg = np.zeros((128, 4, 2, 2, 128), f32)
    maskn = np.zeros((128, 4, 2, 2, 128), f32)
    for dl in range(2):
        dist = qq - kk + 128 * dl
        bidx = _t5_bucket(dist)
        for h in range(4):
            t = rel[bidx, h]
            biasg[:, h, dl, 0, :] = t
            biasg[:, h, dl, 1, :] = t
        if dl == 0:
            mk = np.where(dist < 0, NEG, 0.0).astype(f32)
            maskn[:, :, 0, :, :] = mk[:, None, None, :]
    farb = rel[31, :].astype(f32)
    umask = (kk <= qq).astype(f32)
    negm4 = np.tile(np.where(kk <= qq, 0.0, NEG).astype(f32), (1, 4))
    import ml_dtypes
    common = dict(
        wada=wada, bada=bada, badafm=_fm(bada, 48), g1fm=_fm(g["norm1_g"][0], 8), g2fm=_fm(g["norm2_g"][0], 8),
        wall=wall, mcw=np.ascontiguousarray(mcw, f32), mcb=mcb, fcw=np.ascontiguousarray(fcw, f32), fcb=np.ascontiguousarray(fcb, f32),
        gifb=gifb, mng=g["m_norm_g"][0].astype(f32), ang=g["a_norm_g"][0].astype(f32), gq=gq, gk=gk,
        gqfm=np.tile(g["a_qnorm_g"][0], 2).reshape(128, 1).astype(f32), gkfm=np.tile(g["a_knorm_g"][0], 2).reshape(128, 1).astype(f32),
        alam=g["a_lambda"][0].reshape(256).astype(f32), biasg=biasg.reshape(128, -1), maskn=maskn.reshape(128, -1), farb=farb,
        identb=np.eye(128).astype(ml_dtypes.bfloat16), identf=np.eye(128, dtype=f32), umask=umask, negm4=negm4,
    )
    in_maps = []
    for b in range(B):
        cfm = _fm(g["c"][b], 8)
        for hf in range(2):
            if hf == 0:
                xl = np.concatenate([np.zeros((half_tok, D), f32), x[b, 0:half_tok]], axis=0)
                fl = np.zeros((128, 1), f32)
            else:
                xl = x[b]
                fl = np.ones((128, 1), f32)
            m = dict(common)
            m["x"] = np.ascontiguousarray(xl)
            m["cfm"] = cfm
            m["flag"] = fl
            in_maps.append(m)
    return in_maps


_NC_CACHE = {}


def run(NCTX, NFULL, inputs):
    key = (NCTX, NFULL)
    if key not in _NC_CACHE:
        _NC_CACHE[key] = build_program(NCTX, NFULL)
    nc = _NC_CACHE[key]
    in_maps = prepare_inputs(NCTX, NFULL, inputs)
    res = run_bass_kernel_spmd(nc, in_maps, core_ids=list(range(len(in_maps))))
    B = len(in_maps) // 2
    half_tok = ((NCTX + NFULL) // 2) * 512
    out = np.empty((B, 2 * half_tok, D), np.float32)
    for b in range(B):
        for hf in range(2):
            out[b, hf * half_tok:(hf + 1) * half_tok] = res.results[b * 2 + hf]["out"]
    return out


def kernel(**inputs):
    return run(7, 9, inputs)
```

```python
import math
from contextlib import ExitStack

import numpy as np

import concourse.bass as bass
import concourse.mybir as mybir
from concourse.bass_utils import run_bass_kernel_spmd

F32 = mybir.dt.float32
BF16 = mybir.dt.bfloat16
AF = mybir.ActivationFunctionType
ALU = mybir.AluOpType
AX = mybir.AxisListType

D = 1024
DC = 8
DFF = 2816
NF = 22
EPS = 1e-6
LAM_INIT = 0.8 - 0.6 * math.exp(-0.3 * 0)
NEG = -30000.0
OPT_SELF_WAR = True
OPT_XN_ACT = True
OPT_SILU = True
OPT_GFOLD = True
OPT_TINY = True
OPT_BATCH = True
OPT_PARTMAIN = True

PIECES = {}
_off = 0
for _nm, _n in (("mq", 4096), ("mk", 4096), ("mv", 4096), ("mo", 4096), ("aq", 4096), ("ak", 4096),
                ("av", 4096), ("gm0", 4096), ("gm1", 4096), ("ga0", 4096), ("ga1", 4096), ("gif", 64),
                ("bm", 4096), ("ba", 4096), ("out0", 4096), ("out1", 4096)):
    PIECES[_nm] = (_off, _n)
    _off += _n
for _j in range(11):
    PIECES["up%d" % _j] = (_off, 4096)
    _off += 4096
PIECES["down"] = (_off, NF * 1024)
_off += NF * 1024
WTOT = _off
WPAD = ((WTOT + 4095) // 4096) * 4096


class Buf:
    __slots__ = ("name", "w", "rs", "sem", "semv", "grp", "psum", "persist")

    def __init__(self, name, grp=None):
        self.psum = False
        self.persist = False
        self.name = name
        self.w = None
        self.rs = []
        self.sem = None
        self.semv = 0
        self.grp = grp


class Sched:
    def __init__(self, nc, stack):
        self.nc = nc
        self.stack = stack
        self.engs = {}
        for nm, h in (("pe", nc.tensor), ("act", nc.scalar), ("dve", nc.vector), ("pool", nc.gpsimd), ("sp", nc.sync)):
            sem = stack.enter_context(nc.semaphore("s_" + nm))
            self.engs[nm] = dict(h=h, sem=sem, cnt=0, seen={})
        self.groups = {}
        self.dma_ev = {}
        self.free_sems = {}
        self.live = []
        self.nsem = 0

    def buf(self, name, grp=None):
        return Buf(name, grp)

    def _need(self, e, deps):
        E = self.engs[e]
        best = {}
        for d in deps:
            if d is None:
                continue
            sem, val, en = d
            if en == e and e == "pe":
                continue
            k = id(sem)
            if k not in best or best[k][1] < val:
                best[k] = (sem, val)
        for k, (sem, val) in best.items():
            if E["seen"].get(k, 0) >= val:
                continue
            E["h"].wait_ge(sem, val)
            E["seen"][k] = val

    def op(self, e, fn, reads=(), writes=()):
        E = self.engs[e]
        deps = []
        for b in reads:
            deps.append(b.w)
            if b.psum:
                for r in b.rs:
                    if r[2] != e:
                        deps.append(r)
        for b in writes:
            if b.w is not None and b.w[2] != e:
                deps.append(b.w)
            for r in b.rs:
                if OPT_SELF_WAR or r[2] != e:
                    deps.append(r)
        self._need(e, deps)
        ins = fn(E["h"])
        E["cnt"] += 1
        ins.then_inc(E["sem"], 1)
        ev = (E["sem"], E["cnt"], e)
        for b in reads:
            b.rs.append(ev)
        for b in writes:
            b.w = ev
            b.rs = []
        return ins

    def dma(self, q, out, in_, reads=(), writes=()):
        E = self.engs[q]
        deps = []
        for b in reads:
            deps.append(b.w)
        for b in writes:
            if b.grp in ("wbw", "kvw", "outw"):
                continue
            deps.append(b.w)
            deps.extend(b.rs)
        self._need(q, deps)
        ins = E["h"].dma_start(out=out, in_=in_)
        cands = list(writes) + list(reads)
        tgt = ([b for b in cands if b.grp is None] + cands)[0]
        if tgt.grp is not None:
            gk = tgt.grp
            if gk not in self.groups:
                self.groups[gk] = [self.stack.enter_context(self.nc.semaphore("g_" + gk)), 0]
            g = self.groups[gk]
            g[1] += 16
            sem, val = g[0], g[1]
        else:
            if tgt.sem is None:
                tgt.sem = {}
                self.live.append(tgt)
            if q not in tgt.sem:
                fs = self.free_sems.setdefault(q, [])
                if not fs:
                    self.nsem += 1
                    fs.append([self.stack.enter_context(self.nc.semaphore("dp%d" % self.nsem)), 0])
                tgt.sem[q] = fs.pop()
            ent = tgt.sem[q]
            ent[1] += 16
            sem, val = ent[0], ent[1]
        ins.then_inc(sem, 16)
        self.dma_ev[id(sem)] = (sem, val)
        ev = (sem, val, "dma")
        for b in reads:
            b.rs.append(ev)
        for b in writes:
            b.w = ev
            b.rs = []
        return ins

    def barrier(self, engines=("pe", "act", "dve", "pool", "sp")):
        deps = [(E["sem"], E["cnt"], "x") for E in self.engs.values() if E["cnt"] > 0]
        deps += [(s, v, "dma") for (s, v) in self.dma_ev.values()]
        for e in engines:
            self._need(e, deps)
        self.dma_ev = {}
        keep = []
        for b in self.live:
            if b.persist:
                keep.append(b)
                continue
            for qq, ent in b.sem.items():
                self.free_sems[qq].append(ent)
            b.sem = None
        self.live = keep

    def seal(self, bufs, grp):
        g = self.groups[grp]
        for b in bufs:
            b.w = (g[0], g[1], "dma")


def build_program(NCTX, NFULL, debug=False):
    NU = NCTX + NFULL
    NFLAG = NU // 2
    NBLK = NU * 4
    NTOK = NU * 512
    NOWN = (NFULL - 1) * 512
    assert NU == 2 * (NFULL - 1)

    nc = bass.Bass("TRN2", target_bir_lowering=False)

    def din(name, shape, dt=F32):
        return nc.dram_tensor(name, list(shape), dt, kind="ExternalInput").ap()

    x_d = din("x", [NTOK, D])
    cfm_d = din("cfm", [128, 8])
    wada_d = din("wada", [12, 128, 8 * 512])
    bada_d = din("bada", [6144])
    badafm_d = din("badafm", [128, 48])
    g1fm_d = din("g1fm", [128, 8])
    g2fm_d = din("g2fm", [128, 8])
    wall_d = din("wall", [128, WPAD])
    mcw_d = din("mcw", [128, 8 * 4])
    mcb_d = din("mcb", [128, 8])
    fcw_d = din("fcw", [128, 44 * 3])
    fcb_d = din("fcb", [128, 44])
    gifb_d = din("gifb", [8])
    mng_d = din("mng", [512])
    ang_d = din("ang", [512])
    gq_d = din("gq", [512])
    gk_d = din("gk", [512])
    gqfm_d = din("gqfm", [128, 1])
    gkfm_d = din("gkfm", [128, 1])
    alam_d = din("alam", [256])
    biasg_d = din("biasg", [128, 4 * 2 * 2 * 128])
    maskn_d = din("maskn", [128, 4 * 2 * 2 * 128])
    farb_d = din("farb", [4])
    flag_d = din("flag", [128, 1])
    identb_d = din("identb", [128, 128], BF16)
    identf_d = din("identf", [128, 128])
    umask_d = din("umask", [128, 128])
    negm4_d = din("negm4", [128, 512])
    out_d = nc.dram_tensor("out", [NOWN, D], F32, kind="ExternalOutput").ap()
    wb_d = nc.dram_tensor("wb_scr", [128, WPAD], BF16, kind="Internal").ap()
    kt_d = nc.dram_tensor("kt_scr", [4, 128, NBLK * 128], BF16, kind="Internal").ap()
    va_d = nc.dram_tensor("va_scr", [4, 128, NBLK * 130], BF16, kind="Internal").ap()

    with ExitStack() as top:
        S = Sched(nc, top)

        uid = [0]

        def sbt(st, name, shape, dt=F32):
            uid[0] += 1
            return st.enter_context(nc.sbuf_tensor("s%d_%s" % (uid[0], name), list(shape), dt))

        banks = [top.enter_context(nc.psum_tensor("bank%d" % i, [128, 512], F32)) for i in range(8)]
        bbufs = [S.buf("bank%d" % i) for i in range(8)]
        for b_ in bbufs:
            b_.psum = True
        bank_rr = [0]

        def nbank(lo=0, hi=8):
            i = lo + (bank_rr[0] % (hi - lo))
            bank_rr[0] += 1
            return banks[i], bbufs[i]

        def bfview(bank):
            return bank[:].bitcast(BF16)

        identb = sbt(top, "identb", [128, 128], BF16)
        identf = sbt(top, "identf", [128, 128])
        umask = sbt(top, "umask", [128, 128])
        negm4 = sbt(top, "negm4", [128, 512])
        ones_f = sbt(top, "ones_f", [128, 128])
        flag = sbt(top, "flag", [128, 1])
        epst = sbt(top, "epst", [128, 1])
        onet = sbt(top, "onet", [128, 1])
        mult1 = sbt(top, "mult1", [128, 8])
        shift1 = sbt(top, "shift1", [128, 8])
        mult2 = sbt(top, "mult2", [128, 8])
        shift2 = sbt(top, "shift2", [128, 8])
        gate1 = sbt(top, "gate1", [128, D])
        gate2 = sbt(top, "gate2", [128, D])
        mcw = sbt(top, "mcw", [128, 8, 4])
        mcb = sbt(top, "mcb", [128, 8])
        mcwf = sbt(top, "mcwf", [128, 8, 4])
        fcwf = sbt(top, "fcwf", [128, 44, 3])
        mcnb = sbt(top, "mcnb", [128, 8])
        fcw = sbt(top, "fcw", [128, 44, 3])
        fcb = sbt(top, "fcb", [128, 44])
        fcnb = sbt(top, "fcnb", [128, 44])
        gifb = sbt(top, "gifb", [128, 8])
        mng = sbt(top, "mng", [128, 512])
        ang = sbt(top, "ang", [128, 512])
        gq = sbt(top, "gq", [128, 512])
        gk = sbt(top, "gk", [128, 512])
        gqfm = sbt(top, "gqfm", [128, 1])
        gkfm = sbt(top, "gkfm", [128, 1])
        mbc = sbt(top, "mbc", [128, 8, 3])
        fbc = sbt(top, "fbc", [128, 44, 2])
        ctmp = sbt(top, "ctmp", [128, 44])
        b_mbc = S.buf("mbc")
        b_fbc = S.buf("fbc")
        biasb = sbt(top, "biasb", [128, 4, 2, 256], BF16)
        farb = sbt(top, "farb", [128, 4])
        neglam = sbt(top, "neglam", [128, 1])
        Sst = sbt(top, "Sst", [128, 4, 130])
        Sstb = sbt(top, "Sstb", [128, 4, 130], BF16)
        mhalo = sbt(top, "mhalo", [128, 8, 3])
        fhalo = sbt(top, "fhalo", [128, 44, 2])
        CONST = S.buf("const")
        b_S = S.buf("Sst")
        b_Sb = S.buf("Sstb")
        b_mhalo = S.buf("mhalo")
        b_fhalo = S.buf("fhalo")
        b_wb = S.buf("wb_scr", grp="wbw")
        b_ktd = S.buf("kt_scr", grp="kvw")
        b_vad = S.buf("va_scr", grp="kvw")
        b_out = S.buf("outd", grp="outw")

        def ld_const(t, src, grp="cst"):
            b = S.buf("c_" + t.name, grp=grp)
            S.dma("sp", t[:], src, writes=[b])
            return b

        cb = []
        cb.append(ld_const(identb, identb_d[:, :]))
        cb.append(ld_const(identf, identf_d[:, :]))
        cb.append(ld_const(umask, umask_d[:, :]))
        cb.append(ld_const(negm4, negm4_d[:, :]))
        cb.append(ld_const(flag, flag_d[:, :]))
        cb.append(ld_const(mcw, mcw_d.rearrange("p (c k) -> p c k", k=4)))
        cb.append(ld_const(mcb, mcb_d[:, :]))
        cb.append(ld_const(fcw, fcw_d.rearrange("p (c k) -> p c k", k=3)))
        cb.append(ld_const(fcb, fcb_d[:, :]))
        cb.append(ld_const(gifb, gifb_d.partition_broadcast(128)))
        cb.append(ld_const(mng, mng_d.partition_broadcast(128)))
        cb.append(ld_const(ang, ang_d.partition_broadcast(128)))
        cb.append(ld_const(gq, gq_d.partition_broadcast(128)))
        cb.append(ld_const(gk, gk_d.partition_broadcast(128)))
        cb.append(ld_const(farb, farb_d.partition_broadcast(128)))
        cb.append(ld_const(gqfm, gqfm_d[:, :]))
        cb.append(ld_const(gkfm, gkfm_d[:, :]))
        S.seal(cb, "cst")
        S.op("dve", lambda e: e.memset(ones_f[:], 1.0), writes=[CONST])
        S.op("dve", lambda e: e.memset(epst[:], EPS), writes=[CONST])
        S.op("dve", lambda e: e.memset(onet[:], 1.0), writes=[CONST])
        S.op("dve", lambda e: e.memset(Sst[:], 0.0), writes=[b_S])
        S.op("dve", lambda e: e.memset(Sstb[:], 0.0), writes=[b_Sb])
        S.op("dve", lambda e: e.memset(mhalo[:], 0.0), writes=[b_mhalo])
        S.op("dve", lambda e: e.memset(fhalo[:], 0.0), writes=[b_fhalo])
        S.op("dve", lambda e: e.tensor_scalar(out=mcnb[:], in0=mcb[:], scalar1=-1.0, scalar2=None, op0=ALU.mult), reads=cb, writes=[CONST])
        S.op("dve", lambda e: e.tensor_scalar(out=fcnb[:], in0=fcb[:], scalar1=-1.0, scalar2=None, op0=ALU.mult), reads=cb, writes=[CONST])
        S.op("dve", lambda e: e.tensor_scalar(out=gqfm[:], in0=gqfm[:], scalar1=0.125, scalar2=None, op0=ALU.mult), reads=cb, writes=[CONST])
        S.op("dve", lambda e: e.tensor_scalar(out=mcwf[:], in0=mcw[:], scalar1=flag[:, 0:1], scalar2=None, op0=ALU.mult), reads=cb, writes=[CONST])
        S.op("dve", lambda e: e.tensor_scalar(out=fcwf[:], in0=fcw[:], scalar1=flag[:, 0:1], scalar2=None, op0=ALU.mult), reads=cb, writes=[CONST])
        S.op("dve", lambda e: e.tensor_scalar(out=ang[:], in0=ang[:], scalar1=1.0 - LAM_INIT, scalar2=None, op0=ALU.mult), reads=cb, writes=[CONST])

        with ExitStack() as st:
            cfm = sbt(st, "cfm", [128, 8])
            sc = sbt(st, "sc", [128, 8])
            sct = sbt(st, "sct", [128, 8])
            scbc = sbt(st, "scbc", [128, 8, 128])
            badafm = sbt(st, "badafm", [128, 48])
            g1fm = sbt(st, "g1fm", [128, 8])
            g2fm = sbt(st, "g2fm", [128, 8])
            modfm = sbt(st, "modfm", [128, 48])
            alam = sbt(st, "alam", [128, 256])
            lt = sbt(st, "lt", [128, 128])
            ls = sbt(st, "ls", [128, 2])
            biasg = sbt(st, "biasg", [128, 2048])
            maskn = sbt(st, "maskn", [128, 2048])
            wst = [sbt(st, "wst%d" % i, [128, 4096]) for i in range(2)]
            wbo = [sbt(st, "wbo%d" % i, [128, 4096], BF16) for i in range(2)]
            bbt = [sbt(st, "bbt%d" % i, [128, 512]) for i in range(2)]
            b_wst = [S.buf("wst%d" % i) for i in range(2)]
            b_wbo = [S.buf("wbo%d" % i) for i in range(2)]
            b_bbt = [S.buf("bbt%d" % i) for i in range(2)]
            L = S.buf("prel")
            b_l = [ld_const(cfm, cfm_d[:, :], "cst2"), ld_const(badafm, badafm_d[:, :], "cst2"), ld_const(g1fm, g1fm_d[:, :], "cst2"),
                   ld_const(g2fm, g2fm_d[:, :], "cst2"), ld_const(alam, alam_d.partition_broadcast(128), "cst2"),
                   ld_const(biasg, biasg_d[:, :], "cst2"), ld_const(maskn, maskn_d[:, :], "cst2")]
            S.seal(b_l, "cst2")
            S.op("act", lambda e: e.activation(out=sct[:], in_=cfm[:], func=AF.Exp, scale=-1.0), reads=b_l, writes=[L])
            S.op("dve", lambda e: e.tensor_scalar(out=sct[:], in0=sct[:], scalar1=1.0, scalar2=None, op0=ALU.add), reads=[L], writes=[L])
            S.op("dve", lambda e: e.reciprocal(out=sct[:], in_=sct[:]), reads=[L], writes=[L])
            S.op("dve", lambda e: e.tensor_tensor(out=sc[:], in0=cfm[:], in1=sct[:], op=ALU.mult), reads=[L], writes=[L])
            S.op("dve", lambda e: e.tensor_copy(out=scbc[:], in_=sc[:].unsqueeze(2).to_broadcast([128, 8, 128])), reads=[L], writes=[L])
            S.op("dve", lambda e: e.tensor_tensor(out=lt[:, 0:64], in0=alam[:, 0:64], in1=alam[:, 64:128], op=ALU.mult), reads=b_l, writes=[L])
            S.op("dve", lambda e: e.tensor_tensor(out=lt[:, 64:128], in0=alam[:, 128:192], in1=alam[:, 192:256], op=ALU.mult), reads=[L], writes=[L])
            S.op("dve", lambda e: e.tensor_reduce(out=ls[:], in_=lt[:].rearrange("p (a b) -> p a b", a=2), axis=AX.X, op=ALU.add), reads=[L], writes=[L])
            S.op("act", lambda e: e.activation(out=ls[:], in_=ls[:], func=AF.Exp), reads=[L], writes=[L])
            S.op("dve", lambda e: e.tensor_tensor(out=neglam[:], in0=ls[:, 1:2], in1=ls[:, 0:1], op=ALU.subtract), reads=[L], writes=[CONST])
            S.op("dve", lambda e: e.tensor_scalar(out=neglam[:], in0=neglam[:], scalar1=-LAM_INIT, scalar2=None, op0=ALU.add), reads=[CONST], writes=[CONST])
            S.op("dve", lambda e: e.tensor_tensor(out=biasb[:].rearrange("p a b c -> p (a b c)"), in0=biasg[:], in1=maskn[:], op=ALU.add), reads=b_l, writes=[CONST])

            fm_ps, fm_b = banks[7], bbufs[7]
            fm_cols = {0: 0, 1: 4, 2: 8, 3: 12, 6: 24, 7: 28, 8: 32, 9: 36}
            for pc in range(12):
                i = pc % 2
                S.dma("sp", wst[i][:], wada_d[pc, :, :], writes=[b_wst[i]])
                wv = wst[i][:].rearrange("p (c n) -> p c n", c=8)
                if pc in (4, 5, 10, 11):
                    S.dma("sp", bbt[i][:], bada_d[pc * 512:(pc + 1) * 512].partition_broadcast(128), writes=[b_bbt[i]])
                    pb, pbb = nbank(0, 6)
                    for dc in range(8):
                        S.op("pe", lambda e: e.matmul(pb[:], lhsT=scbc[:, dc, :], rhs=wv[:, dc, :], start=(dc == 0), stop=(dc == 7)),
                             reads=[L, b_wst[i]], writes=[pbb])
                    gt = gate1 if pc < 6 else gate2
                    off = (pc % 2) * 512
                    S.op("dve", lambda e: e.tensor_tensor(out=gt[:, off:off + 512], in0=pb[:], in1=bbt[i][:], op=ALU.add),
                         reads=[pbb, b_bbt[i]], writes=[CONST])
                else:
                    for k in range(4):
                        col = fm_cols[pc] + k
                        for dc in range(8):
                            S.op("pe", lambda e: e.matmul(fm_ps[:, col:col + 1], lhsT=wv[:, dc, k * 128:(k + 1) * 128], rhs=sc[:, dc:dc + 1],
                                                         start=(dc == 0), stop=(dc == 7)), reads=[L, b_wst[i]], writes=[fm_b])
            S.op("dve", lambda e: e.tensor_tensor(out=modfm[:, 0:16], in0=fm_ps[:, 0:16], in1=badafm[:, 0:16], op=ALU.add), reads=[fm_b] + b_l, writes=[L])
            S.op("dve", lambda e: e.tensor_tensor(out=modfm[:, 24:40], in0=fm_ps[:, 24:40], in1=badafm[:, 24:40], op=ALU.add), reads=[fm_b] + b_l, writes=[L])
            S.op("dve", lambda e: e.scalar_tensor_tensor(out=mult1[:], in0=modfm[:, 8:16], scalar=1.0, in1=g1fm[:], op0=ALU.add, op1=ALU.mult), reads=[L], writes=[CONST])
            S.op("dve", lambda e: e.tensor_copy(out=shift1[:], in_=modfm[:, 0:8]), reads=[L], writes=[CONST])
            S.op("dve", lambda e: e.scalar_tensor_tensor(out=mult2[:], in0=modfm[:, 32:40], scalar=1.0, in1=g2fm[:], op0=ALU.add, op1=ALU.mult), reads=[L], writes=[CONST])
            S.op("dve", lambda e: e.tensor_copy(out=shift2[:], in_=modfm[:, 24:32]), reads=[L], writes=[CONST])

            cast_eng = ["dve", "act"]
            for ch in range(WPAD // 4096):
                i = ch % 2
                S.dma("sp", wst[i][:], wall_d[:, ch * 4096:(ch + 1) * 4096], writes=[b_wst[i]])
                ce = cast_eng[ch % 2]
                if ce == "act":
                    S.op("act", lambda e: e.copy(out=wbo[i][:], in_=wst[i][:]), reads=[b_wst[i]], writes=[b_wbo[i]])
                else:
                    S.op(ce, lambda e: e.tensor_copy(out=wbo[i][:], in_=wst[i][:]), reads=[b_wst[i]], writes=[b_wbo[i]])
                S.dma("pool", wb_d[:, ch * 4096:(ch + 1) * 4096], wbo[i][:], reads=[b_wbo[i]], writes=[b_wb])
            S.barrier()

        njunk = sbt(top, "njunk", [128, D], BF16)
        ncache = {}
        pf = [sbt(top, "pf%d" % i, [128, 4096], BF16) for i in range(4)]
        b_pf = [S.buf("pf%d" % i) for i in range(4)]
        for b_ in b_pf:
            b_.persist = True
        prefetched = {}

        def prefetch(slots, pieces):
            for sl, piece in zip(slots, pieces):
                off, n = PIECES[piece]
                S.dma("sp", pf[sl][:, 0:n], wb_d[:, off:off + n], reads=[b_wb], writes=[b_pf[sl]])
                prefetched[piece] = (pf[sl], b_pf[sl])

        def wload(st, name, piece, shape3=None):
            if piece in prefetched:
                return prefetched.pop(piece)
            off, n = PIECES[piece]
            t = sbt(st, name, [128, n], BF16)
            b = S.buf(name)
            S.dma("sp", t[:], wb_d[:, off:off + n], reads=[b_wb], writes=[b])
            return t, b

        for u in range(NU):
            full = u >= NCTX
            flagged = u < NFLAG
            own = u >= NFLAG
            with ExitStack() as su:
                xs = sbt(su, "xs", [128, 4, D])
                hT = sbt(su, "hT", [128, 8, 512], BF16)
                sa = su.enter_context(ExitStack())
                mqT = sbt(sa, "mqT", [128, 8, 512], BF16)
                mVA = sbt(sa, "mVA", [128, 4, 4, 130], BF16)
                sigmo = sbt(sa, "sigmo", [128, 4, 512])
                gif = sbt(sa, "gif", [128, 4, 8])
                Qbd = sbt(sa, "Qbd", [128, 4, 4, 256], BF16)
                hmT = sbt(sa, "hmT", [128, 4, 512], BF16)
                haT = sbt(sa, "haT", [128, 4, 512], BF16)
                b_xs = [S.buf("xs%d" % j) for j in range(4)]
                b_hT = S.buf("hT")
                b_hT2 = S.buf("hT2")
                b_mqT = S.buf("mqT")
                b_mVA = S.buf("mVA")
                b_sig = S.buf("sigmo")
                b_gif = S.buf("gif")
                b_Qbd = S.buf("Qbd")
                b_hmT = S.buf("hmT")
                b_haT = S.buf("haT")
                if full:
                    S.op("dve", lambda e: e.memset(Qbd[:], 0.0), writes=[b_Qbd])

                def norm_block(st, j, mult, shift, hdst, b_hdst, tagp):
                    junk = njunk
                    if not hasattr(st, "ncache"):
                        st.ncache = {}
                    ck = (tagp, j % 2)
                    if ck not in st.ncache:
                        st.ncache[ck] = (sbt(st, tagp + "xn%d" % (j % 2), [128, D], BF16), S.buf("xn"))
                    xn, b_xn = st.ncache[ck]
                    ss = sbt(st, tagp + "ss%d" % j, [128, 2])
                    bl = S.buf("nb")
                    S.op("act", lambda e: e.activation(out=junk[:], in_=xs[:, j, :], func=AF.Square, scale=1.0 / 32.0, accum_out=ss[:, 0:1]),
                         reads=[b_xs[j]], writes=[bl])
                    S.op("act", lambda e: e.activation(out=ss[:, 1:2], in_=ss[:, 0:1], func=AF.Ln, bias=epst[:, 0:1], scale=1.0), reads=[bl, CONST], writes=[bl])
                    S.op("act", lambda e: e.activation(out=ss[:, 1:2], in_=ss[:, 1:2], func=AF.Exp, scale=-0.5), reads=[bl], writes=[bl])
                    S.op("dve", lambda e: e.tensor_scalar(out=xn[:], in0=xs[:, j, :], scalar1=ss[:, 1:2], scalar2=None, op0=ALU.mult),
                         reads=[bl, b_xs[j]], writes=[b_xn])
                    for half in range(2):
                        pb, pbb = nbank()
                        pv = bfview(pb)[:, 0:512].rearrange("p (c t) -> p c t", c=4)
                        for cc in range(4):
                            c = half * 4 + cc
                            S.op("pe", lambda e: e.transpose(out=pv[:, cc, :], in_=xn[:, c * 128:(c + 1) * 128], identity=identb[:]),
                                 reads=[b_xn, cb[0]], writes=[pbb])
                        for cc in range(4):
                            c = half * 4 + cc
                            if half == 0:
                                S.op("act", lambda e: e.activation(out=hdst[:, c, j * 128:(j + 1) * 128], in_=pv[:, cc, :], func=AF.Identity,
                                                                  scale=mult[:, c:c + 1], bias=shift[:, c:c + 1]), reads=[pbb, CONST], writes=[b_hdst[0]])
                            else:
                                S.op("dve", lambda e: e.tensor_scalar(out=hdst[:, c, j * 128:(j + 1) * 128], in0=pv[:, cc, :], scalar1=mult[:, c:c + 1],
                                                                     scalar2=shift[:, c:c + 1], op0=ALU.mult, op1=ALU.add), reads=[pbb, CONST], writes=[b_hdst[1]])

                with ExitStack() as st:
                    names = (["mq", "mk", "mv", "gif", "av", "mo", "aq", "ak"] if full else ["mk", "mv", "gif", "av", "ak"])
                    W = {}
                    for j in range(4):
                        blk = u * 4 + j
                        S.dma("sp", xs[:, j, :], x_d[blk * 128:(blk + 1) * 128, :], writes=[b_xs[j]])
                    for nm in names:
                        W[nm] = wload(st, "w_" + nm, nm)
                    if not full and u + 1 < NU:
                        nxt_full = (u + 1) >= NCTX
                        slots = (2, 3) if (nxt_full or (u + 1) % 2 == 1) else (0, 1)
                        if nxt_full:
                            slots = (2, 3)
                        prefetch(slots, ["mq", "mk"] if nxt_full else ["mk", "mv"])
                    for j in range(4):
                        norm_block(st, j, mult1, shift1, hT, (b_hT, b_hT2), "n1")
                    if flagged:
                        S.op("dve", lambda e: e.tensor_copy(out=mVA[:, :, :, 128:130], in_=flag[:, 0:1].unsqueeze(1).unsqueeze(1).to_broadcast([128, 4, 4, 2])),
                             reads=[cb[4]], writes=[b_mVA])
                    else:
                        S.op("dve", lambda e: e.memset(mVA[:, :, :, 128:130], 1.0), writes=[b_mVA])
                    acc = [sbt(st, "acc%d" % i, [128, 512]) for i in range(4)]
                    et = [sbt(st, "et%d" % i, [128, 512]) for i in range(4)]
                    b_acc = [S.buf("acc%d" % i) for i in range(4)]
                    b_et = [S.buf("et%d" % i) for i in range(4)]
                    hv = mhalo
                    if not OPT_BATCH:
                        S.op("dve", lambda e: e.memset(mbc[:], 0.0), writes=[b_mbc])
                    if OPT_BATCH:
                        S.op("dve", lambda e: e.tensor_tensor(out=mbc[:, :, 2], in0=hv[:, :, 2], in1=mcw[:, :, 0], op=ALU.mult), reads=[b_mhalo] + cb, writes=[b_mbc])
                        S.op("dve", lambda e: e.tensor_tensor(out=mbc[:, :, 1], in0=hv[:, :, 2], in1=mcw[:, :, 1], op=ALU.mult), reads=[b_mhalo] + cb, writes=[b_mbc])
                        S.op("dve", lambda e: e.tensor_tensor(out=mbc[:, :, 0], in0=hv[:, :, 2], in1=mcw[:, :, 2], op=ALU.mult), reads=[b_mhalo] + cb, writes=[b_mbc])
                        S.op("dve", lambda e: e.tensor_tensor(out=ctmp[:, 0:8], in0=hv[:, :, 1], in1=mcw[:, :, 0], op=ALU.mult), reads=[b_mhalo] + cb, writes=[b_mbc])
                        S.op("dve", lambda e: e.tensor_tensor(out=mbc[:, :, 1], in0=mbc[:, :, 1], in1=ctmp[:, 0:8], op=ALU.add), reads=[b_mbc], writes=[b_mbc])
                        S.op("dve", lambda e: e.tensor_tensor(out=ctmp[:, 8:16], in0=hv[:, :, 1], in1=mcw[:, :, 1], op=ALU.mult), reads=[b_mhalo] + cb, writes=[b_mbc])
                        S.op("dve", lambda e: e.tensor_tensor(out=mbc[:, :, 0], in0=mbc[:, :, 0], in1=ctmp[:, 8:16], op=ALU.add), reads=[b_mbc], writes=[b_mbc])
                        S.op("dve", lambda e: e.tensor_tensor(out=ctmp[:, 16:24], in0=hv[:, :, 0], in1=mcw[:, :, 0], op=ALU.mult), reads=[b_mhalo] + cb, writes=[b_mbc])
                        S.op("dve", lambda e: e.tensor_tensor(out=mbc[:, :, 0], in0=mbc[:, :, 0], in1=ctmp[:, 16:24], op=ALU.add), reads=[b_mbc], writes=[b_mbc])
                        S.op("dve", lambda e: e.tensor_tensor(out=mbc[:], in0=mbc[:], in1=mcb[:].unsqueeze(2).to_broadcast([128, 8, 3]), op=ALU.add), reads=[b_mbc] + cb, writes=[b_mbc])
                    groups = ([[0, 1, 2, 3], [4, 5, 6, 7]] if full else [[4, 5, 6, 7]])
                    wsel = mcwf if flagged else mcw
                    for grp in groups:
                        pbs = []
                        for k, c in enumerate(grp):
                            wt, wbf = W["mq" if c < 4 else "mk"]
                            wv = wt[:].rearrange("p (c n) -> p c n", c=8)
                            pb, pbb = nbank()
                            pbs.append((pb, pbb))
                            for dc in range(8):
                                S.op("pe", lambda e: e.matmul(pb[:], lhsT=wv[:, dc, k * 128:(k + 1) * 128], rhs=hT[:, dc, :], start=(dc == 0), stop=(dc == 7)),
                                     reads=[wbf, b_hT, b_hT2], writes=[pbb])
                            m0 = 3 if OPT_PARTMAIN else 0
                            S.op("act", lambda e: e.activation(out=acc[k][:, m0:512], in_=pb[:, m0:512], func=AF.Identity, scale=wsel[:, c, 3:4], bias=mcb[:, c:c + 1]),
                                 reads=[pbb, CONST] + cb, writes=[b_acc[k]])
                            for col in (range(3) if OPT_TINY else ()):
                                S.op("act", lambda e: e.activation(out=acc[k][:, col:col + 1], in_=pb[:, col:col + 1], func=AF.Identity, scale=wsel[:, c, 3:4],
                                                                  bias=mbc[:, c, col:col + 1]), reads=[pbb, CONST, b_mbc] + cb, writes=[b_acc[k]])
                            if flagged:
                                S.op("act", lambda e: e.activation(out=mhalo[:, c, :], in_=pb[:, 509:512], func=AF.Copy, scale=flag[:, 0:1]), reads=[pbb, cb[4], b_mbc], writes=[b_mhalo])
                            else:
                                S.op("act", lambda e: e.copy(out=mhalo[:, c, :], in_=pb[:, 509:512]), reads=[pbb, b_mbc], writes=[b_mhalo])
                        for tp in (2, 1, 0):
                            sh = 3 - tp
                            for k, c in enumerate(grp):
                                pb, pbb = pbs[k]
                                S.op("dve", lambda e: e.scalar_tensor_tensor(out=acc[k][:, sh:512], in0=pb[:, 0:512 - sh], scalar=wsel[:, c, tp:tp + 1], in1=acc[k][:, sh:512],
                                                                            op0=ALU.mult, op1=ALU.add), reads=[pbb, b_acc[k], CONST], writes=[b_acc[k]])
                        for k, c in enumerate(grp):
                            if not OPT_SILU:
                                S.op("act", lambda e: e.activation(out=et[k][:], in_=acc[k][:], func=AF.Exp, scale=-1.0), reads=[b_acc[k]], writes=[b_et[k]])
                                sk = (128.0 ** 0.5) if c < 4 else 1.0
                                S.op("dve", lambda e: e.tensor_scalar(out=et[k][:], in0=et[k][:], scalar1=1.0, scalar2=sk, op0=ALU.add, op1=ALU.mult), reads=[b_et[k]], writes=[b_et[k]])
                                S.op("dve", lambda e: e.reciprocal(out=et[k][:], in_=et[k][:]), reads=[b_et[k]], writes=[b_et[k]])
                                S.op("dve", lambda e: e.tensor_tensor(out=mqT[:, c, :], in0=acc[k][:], in1=et[k][:], op=ALU.mult), reads=[b_acc[k], b_et[k]], writes=[b_mqT])
                            elif c < 4:
                                S.op("act", lambda e: e.activation(out=et[k][:], in_=acc[k][:], func=AF.Silu), reads=[b_acc[k]], writes=[b_et[k]])
                                S.op("dve", lambda e: e.tensor_scalar(out=mqT[:, c, :], in0=et[k][:], scalar1=128.0 ** -0.5, scalar2=None, op0=ALU.mult),
                                     reads=[b_et[k]], writes=[b_mqT])
                            else:
                                S.op("act", lambda e: e.activation(out=mqT[:, c, :], in_=acc[k][:], func=AF.Silu), reads=[b_acc[k]], writes=[b_mqT])
                    sq = [sbt(st, "sq%d" % i, [128, 512]) for i in range(2)]
                    qb = [sbt(st, "qb%d" % i, [128, 512], BF16) for i in range(2)]
                    rs = [sbt(st, "rs%d" % i, [128, 16]) for i in range(2)]
                    KTb = [sbt(st, "KTb%d" % i, [128, 4, 128], BF16) for i in range(2)]
                    VAb = [sbt(st, "VAb%d" % i, [128, 4, 130], BF16) for i in range(2)]
                    b_t = [S.buf("tmj%d" % i) for i in range(2)]
                    b_KTb = [S.buf("KTb%d" % i) for i in range(2)]
                    b_VAb = [S.buf("VAb%d" % i) for i in range(2)]
                    for i in range(2):
                        if flagged:
                            S.op("dve", lambda e: e.tensor_copy(out=VAb[i][:, :, 128:130], in_=flag[:, 0:1].unsqueeze(1).to_broadcast([128, 4, 2])),
                                 reads=[cb[4]], writes=[b_VAb[i]])
                        else:
                            S.op("dve", lambda e: e.memset(VAb[i][:, :, 128:130], 1.0), writes=[b_VAb[i]])
                    for j in range(4):
                        blk = u * 4 + j
                        i = j % 2
                        tsl = slice(j * 128, (j + 1) * 128)

                        def proj(nm):
                            wt, wbf = W[nm]
                            n = PIECES[nm][1] // 8
                            wv = wt[:].rearrange("p (c n) -> p c n", c=8)
                            pb, pbb = nbank()
                            for dc in range(8):
                                S.op("pe", lambda e: e.matmul(pb[:, 0:n], lhsT=hT[:, dc, tsl], rhs=wv[:, dc, :], start=(dc == 0), stop=(dc == 7)),
                                     reads=[wbf, b_hT, b_hT2], writes=[pbb])
                            return pb, pbb

                        def evac_scaled(out_ap, in_ap, rd, wr):
                            if flagged:
                                S.op("act", lambda e: e.activation(out=out_ap, in_=in_ap, func=AF.Copy, scale=flag[:, 0:1]), reads=rd + [cb[4]], writes=wr)
                            else:
                                S.op("act", lambda e: e.copy(out=out_ap, in_=in_ap), reads=rd, writes=wr)

                        pb, pbb = proj("mv")
                        evac_scaled(mVA[:, j, :, 0:128], pb[:].rearrange("p (h d) -> p h d", h=4), [pbb], [b_mVA])
                        pb, pbb = proj("gif")
                        S.op("dve", lambda e: e.tensor_tensor(out=gif[:, j, :], in0=pb[:, 0:8], in1=gifb[:], op=ALU.add), reads=[pbb, cb[9]], writes=[b_gif])
                        pb, pbb = proj("av")
                        evac_scaled(VAb[i][:, :, 0:128], pb[:].rearrange("p (h d) -> p h d", h=4), [pbb], [b_VAb[i]])
                        S.dma("pool", va_d.rearrange("h p (b c) -> p h b c", c=130)[:, :, blk, :], VAb[i][:], reads=[b_VAb[i]], writes=[b_vad])
                        if full:
                            pb, pbb = proj("mo")
                            S.op("act", lambda e: e.activation(out=sigmo[:, j, :], in_=pb[:], func=AF.Exp, scale=-1.0), reads=[pbb], writes=[b_sig])
                            S.op("dve", lambda e: e.tensor_scalar(out=sigmo[:, j, :], in0=sigmo[:, j, :], scalar1=1.0, scalar2=None, op0=ALU.add), reads=[b_sig], writes=[b_sig])
                            S.op("dve", lambda e: e.reciprocal(out=sigmo[:, j, :], in_=sigmo[:, j, :]), reads=[b_sig], writes=[b_sig])
                        for nm in (["aq", "ak"] if full else ["ak"]):
                            pb, pbb = proj(nm)
                            S.op("act", lambda e: e.activation(out=sq[i][:], in_=pb[:], func=AF.Square, scale=0.125), reads=[pbb], writes=[b_t[i]])
                            S.op("dve", lambda e: e.tensor_reduce(out=rs[i][:, 0:8], in_=sq[i][:].rearrange("p (g d) -> p g d", d=64), axis=AX.X, op=ALU.add),
                                 reads=[b_t[i]], writes=[b_t[i]])
                            S.op("act", lambda e: e.activation(out=rs[i][:, 8:16], in_=rs[i][:, 0:8], func=AF.Ln, bias=epst[:, 0:1], scale=1.0), reads=[b_t[i], CONST], writes=[b_t[i]])
                            S.op("act", lambda e: e.activation(out=rs[i][:, 8:16], in_=rs[i][:, 8:16], func=AF.Exp, scale=-0.5), reads=[b_t[i]], writes=[b_t[i]])
                            S.op("dve", lambda e: e.tensor_tensor(out=qb[i][:].rearrange("p (g d) -> p g d", d=64), in0=pb[:].rearrange("p (g d) -> p g d", d=64),
                                                                 in1=rs[i][:, 8:16].unsqueeze(2).to_broadcast([128, 8, 64]), op=ALU.mult), reads=[pbb, b_t[i]], writes=[b_t[i]])
                            tb, tbb = nbank()
                            tv = bfview(tb)[:, 0:512].rearrange("p (h t) -> p h t", h=4)
                            for h in range(4):
                                S.op("pe", lambda e: e.transpose(out=tv[:, h, :], in_=qb[i][:, h * 128:(h + 1) * 128], identity=identb[:]), reads=[b_t[i], cb[0]], writes=[tbb])
                            if nm == "aq":
                                if OPT_GFOLD:
                                    S.op("act", lambda e: e.activation(out=Qbd[0:64, :, j, 0:128], in_=tv[0:64, :, :], func=AF.Copy, scale=gqfm[0:64, 0:1]), reads=[tbb, CONST] + cb, writes=[b_Qbd])
                                    S.op("act", lambda e: e.activation(out=Qbd[64:128, :, j, 128:256], in_=tv[64:128, :, :], func=AF.Copy, scale=gqfm[64:128, 0:1]), reads=[tbb, CONST] + cb, writes=[b_Qbd])
                                else:
                                    S.op("act", lambda e: e.copy(out=Qbd[0:64, :, j, 0:128], in_=tv[0:64, :, :]), reads=[tbb, CONST] + cb, writes=[b_Qbd])
                                    S.op("act", lambda e: e.copy(out=Qbd[64:128, :, j, 128:256], in_=tv[64:128, :, :]), reads=[tbb, CONST] + cb, writes=[b_Qbd])
                            else:
                                if OPT_GFOLD:
                                    S.op("act", lambda e: e.activation(out=KTb[i][:], in_=tv, func=AF.Copy, scale=gkfm[:, 0:1]), reads=[tbb] + cb, writes=[b_KTb[i]])
                                else:
                                    S.op("act", lambda e: e.copy(out=KTb[i][:], in_=tv), reads=[tbb] + cb, writes=[b_KTb[i]])
                                S.dma("pool", kt_d.rearrange("h p k -> p h k")[:, :, blk * 128:(blk + 1) * 128], KTb[i][:], reads=[b_KTb[i]], writes=[b_ktd])
                    S.barrier()

                def mlstm_block(j, st, bankfn, cache):
                    def sbt_c(st_, name, shape, dt=F32):
                        if name not in cache:
                            cache[name] = sbt(st_, name, shape, dt)
                        return cache[name]

                    def buf_c(name):
                        k_ = "B_" + name
                        if k_ not in cache:
                            cache[k_] = S.buf(name)
                        return cache[k_]
                    tsl = slice(j * 128, (j + 1) * 128)
                    lf = sbt_c(st, "lf%d" % (j % 2), [128, 4])
                    e1 = sbt_c(st, "e1%d" % (j % 2), [128, 4])
                    e2 = sbt_c(st, "e2%d" % (j % 2), [128, 4])
                    wsx = sbt_c(st, "wsx%d" % (j % 2), [128, 4])
                    RL = sbt_c(st, "RL%d" % (j % 2), [128, 4, 128])
                    Et = sbt_c(st, "Et%d" % (j % 2), [128, 512])
                    Kw = sbt_c(st, "Kw%d" % (j % 2), [128, 4, 128], BF16)
                    bm = buf_c("ml%d" % (j % 2))
                    b_e1 = buf_c("e1%d" % (j % 2))
                    b_ws = buf_c("wsx%d" % (j % 2))
                    b_RL = buf_c("RL%d" % (j % 2))
                    b_Et = buf_c("Et%d" % (j % 2))
                    b_Kw = buf_c("Kw%d" % (j % 2))
                    S.op("act", lambda e: e.activation(out=lf[:], in_=gif[:, j, 4:8], func=AF.Exp, scale=-1.0), reads=[b_gif], writes=[bm])
                    yield
                    S.op("act", lambda e: e.activation(out=lf[:], in_=lf[:], func=AF.Ln, bias=onet[:, 0:1], scale=1.0), reads=[bm, CONST], writes=[bm])
                    yield
                    S.op("dve", lambda e: e.tensor_scalar(out=lf[:], in0=lf[:], scalar1=-1.0, scalar2=None, op0=ALU.mult), reads=[bm], writes=[bm])
                    yield
                    p1, p1b = bankfn()
                    S.op("pe", lambda e: e.matmul(p1[:, 0:4], lhsT=umask[:], rhs=lf[:], start=True, stop=True), reads=[bm, cb[2]], writes=[p1b])
                    S.op("dve", lambda e: e.tensor_tensor(out=RL[:], in0=umask[:].unsqueeze(1).to_broadcast([128, 4, 128]),
                                                         in1=lf[:].unsqueeze(2).to_broadcast([128, 4, 128]), op=ALU.mult), reads=[bm, cb[2]], writes=[b_RL])
                    yield
                    S.op("dve", lambda e: e.tensor_tensor(out=e1[:], in0=gif[:, j, 0:4], in1=p1[:, 0:4], op=ALU.subtract), reads=[b_gif, p1b], writes=[b_e1])
                    yield
                    RLf = RL[:].rearrange("p h t -> p (h t)")
                    pBc, pBcb = bankfn()
                    S.op("pe", lambda e: e.matmul(pBc[:], lhsT=ones_f[:], rhs=RLf, start=True, stop=True), reads=[b_RL, CONST], writes=[pBcb])
                    yield
                    blast = pBc[:].rearrange("p (h t) -> p h t", h=4)[:, :, 127]
                    S.op("dve", lambda e: e.tensor_tensor(out=e2[:], in0=e1[:], in1=blast, op=ALU.add), reads=[b_e1, pBcb], writes=[b_ws])
                    yield
                    S.op("act", lambda e: e.activation(out=Et[:], in_=pBc[:], func=AF.Exp), reads=[pBcb], writes=[b_Et])
                    S.op("act", lambda e: e.activation(out=wsx[:], in_=e2[:], func=AF.Exp), reads=[b_ws], writes=[b_ws])
                    yield
                    tb, tbb = bankfn()
                    tv = bfview(tb)[:, 0:512].rearrange("p (h t) -> p h t", h=4)
                    for h in range(4):
                        S.op("pe", lambda e: e.transpose(out=tv[:, h, :], in_=mqT[:, 4 + h, tsl], identity=identb[:]), reads=[b_mqT, cb[0]], writes=[tbb])
                    yield
                    S.op("dve", lambda e: e.tensor_tensor(out=Kw[:], in0=tv, in1=wsx[:].unsqueeze(2).to_broadcast([128, 4, 128]), op=ALU.mult),
                         reads=[tbb, b_ws], writes=[b_Kw])
                    yield
                    if full:
                        DT = sbt_c(st, "DT%d" % (j % 2), [128, 4, 128])
                        PT = sbt_c(st, "PT%d" % (j % 2), [128, 4, 128], BF16)
                        qpT = sbt_c(st, "qpT%d" % (j % 2), [128, 4, 128], BF16)
                        numS = sbt_c(st, "numS%d" % (j % 2), [128, 4, 130])
                        rden = sbt_c(st, "rden%d" % (j % 2), [128, 4])
                        hmr = sbt_c(st, "hmr%d" % (j % 2), [128, 4, 128])
                        hsq = sbt_c(st, "hsq%d" % (j % 2), [128, 4, 128])
                        hss = sbt_c(st, "hss%d" % (j % 2), [128, 8])
                        gs = sbt_c(st, "gs%d" % (j % 2), [128, 512])
                        hmf = sbt_c(st, "hmf%d" % (j % 2), [128, 4, 128], BF16)
                        b_o = buf_c("mo%d" % (j % 2))
                        b_gs = buf_c("gs%d" % (j % 2))
                        b_num = buf_c("numS%d" % (j % 2))
                        b_DT = buf_c("DT%d" % (j % 2))
                        b_PT = buf_c("PT%d" % (j % 2))
                        b_qp = buf_c("qpT%d" % (j % 2))
                        pBm, pBmb = bankfn()
                        S.op("pe", lambda e: e.matmul(pBm[:], lhsT=ones_f[:], rhs=RLf, start=True, stop=False), reads=[b_RL, CONST], writes=[pBmb])
                        S.op("pe", lambda e: e.matmul(pBm[:], lhsT=identf[:], rhs=negm4[:], start=False, stop=True), reads=[cb[1], cb[3]], writes=[pBmb])
                        S.op("dve", lambda e: e.tensor_tensor(out=qpT[:], in0=mqT[:, 0:4, tsl], in1=Et[:].rearrange("p (h t) -> p h t", h=4), op=ALU.mult),
                             reads=[b_mqT, b_Et], writes=[b_qp])
                        S.op("dve", lambda e: e.tensor_tensor(out=gs[:], in0=mng[:], in1=sigmo[:, j, :], op=ALU.mult), reads=[b_sig] + cb, writes=[b_gs])
                        yield
                        for h in range(4):
                            S.op("act", lambda e: e.activation(out=DT[:, h, :], in_=pBm[:, h * 128:(h + 1) * 128], func=AF.Exp, bias=e1[:, h:h + 1], scale=1.0),
                                 reads=[pBmb, b_e1], writes=[b_DT])
                        yield
                        pA, pAb = bankfn()
                        for h in range(4):
                            S.op("pe", lambda e: e.matmul(pA[:, h * 128:(h + 1) * 128], lhsT=mqT[:, 4 + h, tsl], rhs=mqT[:, h, tsl], start=True, stop=True),
                                 reads=[b_mqT], writes=[pAb])
                        yield
                        S.op("dve", lambda e: e.tensor_tensor(out=PT[:], in0=pA[:].rearrange("p (h t) -> p h t", h=4), in1=DT[:], op=ALU.mult),
                             reads=[pAb, b_DT], writes=[b_PT])
                        yield
                    yield "AB"
                    if full:
                        for hp in range(2):
                            pn, pnb = bankfn()
                            for hh in range(2):
                                h = 2 * hp + hh
                                o = hh * 130
                                S.op("pe", lambda e: e.matmul(pn[:, o:o + 130], lhsT=PT[:, h, :], rhs=mVA[:, j, h, :], start=True, stop=False), reads=[b_PT, b_mVA], writes=[pnb])
                                S.op("pe", lambda e: e.matmul(pn[:, o:o + 130], lhsT=qpT[:, h, :], rhs=Sstb[:, h, :], start=False, stop=True), reads=[b_qp, b_Sb], writes=[pnb])
                            yield
                            S.op("act", lambda e: e.copy(out=numS[:, 2 * hp:2 * hp + 2, :], in_=pn[:, 0:260].rearrange("p (h c) -> p h c", h=2)), reads=[pnb], writes=[b_num])
                            yield
                    Ev = Et[:].rearrange("p (h t) -> p h t", h=4)
                    for hp in range(2):
                        pc_, pcb = bankfn()
                        for hh in range(2):
                            h = 2 * hp + hh
                            o = hh * 130
                            S.op("pe", lambda e: e.matmul(pc_[:, o:o + 130], lhsT=Kw[:, h, :], rhs=mVA[:, j, h, :], start=True, stop=True), reads=[b_Kw, b_mVA], writes=[pcb])
                        yield
                        for hh in range(2):
                            h = 2 * hp + hh
                            o = hh * 130
                            S.op("dve", lambda e: e.scalar_tensor_tensor(out=Sst[:, h, :], in0=Sst[:, h, :], scalar=Ev[:, h, 127:128], in1=pc_[:, o:o + 130],
                                                                        op0=ALU.mult, op1=ALU.add), reads=[b_S, b_Et, pcb], writes=[b_S])
                        yield
                    S.op("act", lambda e: e.copy(out=Sstb[:], in_=Sst[:]), reads=[b_S], writes=[b_Sb])
                    yield "S"
                    if full:
                        S.op("act", lambda e: e.activation(out=rden[:], in_=numS[:, :, 128], func=AF.Abs), reads=[b_num], writes=[b_o])
                        yield
                        S.op("dve", lambda e: e.tensor_scalar(out=rden[:], in0=rden[:], scalar1=1.0, scalar2=None, op0=ALU.max), reads=[b_o], writes=[b_o])
                        S.op("dve", lambda e: e.reciprocal(out=rden[:], in_=rden[:]), reads=[b_o], writes=[b_o])
                        S.op("dve", lambda e: e.tensor_tensor(out=hmr[:], in0=numS[:, :, 0:128], in1=rden[:].unsqueeze(2).to_broadcast([128, 4, 128]), op=ALU.mult),
                             reads=[b_num, b_o], writes=[b_o])
                        yield
                        S.op("act", lambda e: e.activation(out=hsq[:], in_=hmr[:], func=AF.Square, scale=1.0 / math.sqrt(128.0)), reads=[b_o], writes=[b_o])
                        yield
                        S.op("dve", lambda e: e.tensor_reduce(out=hss[:, 0:4], in_=hsq[:], axis=AX.X, op=ALU.add), reads=[b_o], writes=[b_o])
                        yield
                        S.op("act", lambda e: e.activation(out=hss[:, 4:8], in_=hss[:, 0:4], func=AF.Ln, bias=epst[:, 0:1], scale=1.0), reads=[b_o, CONST], writes=[b_o])
                        S.op("act", lambda e: e.activation(out=hss[:, 4:8], in_=hss[:, 4:8], func=AF.Exp, scale=-0.5), reads=[b_o], writes=[b_o])
                        yield
                        for h in range(4):
                            S.op("dve", lambda e: e.scalar_tensor_tensor(out=hmf[:, h, :], in0=hmr[:, h, :], scalar=hss[:, 4 + h:5 + h], in1=gs[:, h * 128:(h + 1) * 128],
                                                                        op0=ALU.mult, op1=ALU.mult), reads=[b_o, b_gs], writes=[b_o])
                        yield
                        tb2, tb2b = bankfn()
                        tv2 = bfview(tb2)[:, 0:512].rearrange("p (h t) -> p h t", h=4)
                        for h in range(4):
                            S.op("pe", lambda e: e.transpose(out=tv2[:, h, :], in_=hmf[:, h, :], identity=identb[:]), reads=[b_o, cb[0]], writes=[tb2b])
                        yield
                        S.op("act", lambda e: e.copy(out=hmT[:, :, tsl], in_=tv2), reads=[tb2b], writes=[b_hmT])
                        yield

                def mlstm_driver(st, bankfn):
                    cache = {}
                    for j in range(4):
                        for r in mlstm_block(j, st, bankfn, cache):
                            yield

                if not full:
                    with ExitStack() as st:
                        for _ in mlstm_driver(st, nbank):
                            pass
                        S.barrier()

                if full:
                    with ExitStack() as st:
                        NKC = 8
                        kch = [sbt(st, "kch%d" % i, [128, NKC * 128], BF16) for i in range(3)]
                        vch = [sbt(st, "vch%d" % i, [128, NKC, 130], BF16) for i in range(3)]
                        b_kch = [S.buf("kch%d" % i) for i in range(3)]
                        b_vch = [S.buf("vch%d" % i) for i in range(3)]
                        pex = [sbt(st, "pex%d" % i, [128, 512], BF16) for i in range(6)]
                        b_pex = [S.buf("pex%d" % i) for i in range(6)]
                        rr = sbt(st, "rr", [128, 8])
                        har = sbt(st, "har", [128, 4, 128])
                        hsq = sbt(st, "ahsq", [128, 4, 128])
                        hss = sbt(st, "ahss", [128, 8])
                        haf = sbt(st, "haf", [128, 4, 128], BF16)
                        b_a = S.buf("attn_o")
                        nkb_tot = u * 4 + 4
                        chunk_list = [(s0, min(NKC, nkb_tot - s0)) for s0 in range(0, nkb_tot, NKC)]
                        ring = 0
                        LCH = len(chunk_list)
                        prefetch((0, 1), ["bm", "gm0"])
                        mdrv = mlstm_driver(st, lambda: (banks[7], bbufs[7]))
                        n_items_est = 4 * sum((2 if kb_ <= u * 4 - 2 else 0) + sum(1 for p2 in range(2) for jq in (2 * p2, 2 * p2 + 1) if kb_ > u * 4 + 2 * p2 - 2 and kb_ <= u * 4 + jq)
                                              for kb_ in range(nkb_tot))
                        MSTEP = max(1, n_items_est // 150)
                        gcount = [0]
                        loaded = set()

                        def ensure_chunk(g):
                            if g >= 4 * LCH or g in loaded:
                                return
                            loaded.add(g)
                            h_, ci_ = g // LCH, g % LCH
                            s0, nk = chunk_list[ci_]
                            ri = g % 3
                            S.dma("sp", kch[ri][:, 0:nk * 128], kt_d[h_, :, s0 * 128:(s0 + nk) * 128], reads=[b_ktd], writes=[b_kch[ri]])
                            S.dma("sp", vch[ri][:, 0:nk, :], va_d[h_, :, s0 * 130:(s0 + nk) * 130].rearrange("p (b c) -> p b c", c=130), reads=[b_vad], writes=[b_vch[ri]])

                        for h in range(4):
                            accs = [(banks[jq], bbufs[jq]) for jq in range(4)]
                            for jq in range(4):
                                S.op("dve", lambda e: e.memset(accs[jq][0][:, 0:260], 0.0), writes=[accs[jq][1]])
                            items = []
                            for ci, (s0, nk) in enumerate(chunk_list):
                                g = h * LCH + ci
                                ri = g % 3
                                for kk in range(nk):
                                    kb = s0 + kk
                                    for p2 in range(2):
                                        j0 = 2 * p2
                                        if kb <= u * 4 + j0 - 2:
                                            items.append((ri, kk, kb, (j0, j0 + 1), g))
                                        else:
                                            for jq in (j0, j0 + 1):
                                                if kb <= u * 4 + jq:
                                                    items.append((ri, kk, kb, (jq,), g))

                            LAG = 3
                            NS = 3
                            NPX = len(pex)

                            def emit_scores(it, idx):
                                ri, kk, kb, jqs, ci = it
                                ensure_chunk(ci)
                                ensure_chunk(ci + 1)
                                nq = len(jqs)
                                sl = idx % NS
                                sp_, spb = banks[4 + sl], bbufs[4 + sl]
                                sv = sp_[:, 0:256 * nq]
                                px = idx % NPX
                                delta = u * 4 + jqs[0] - kb
                                near = (nq == 1) and delta <= 1
                                rhs = Qbd[:, h, jqs[0]:jqs[0] + nq, :].rearrange("p j c -> p (j c)")
                                S.op("pe", lambda e: e.matmul(sv, lhsT=kch[ri][:, kk * 128:(kk + 1) * 128], rhs=rhs, start=True, stop=not near),
                                     reads=[b_kch[ri], b_Qbd], writes=[spb])
                                if near:
                                    S.op("pe", lambda e: e.matmul(sv, lhsT=identb[:], rhs=biasb[:, h, delta, :], start=False, stop=True), reads=[CONST, cb[0]], writes=[spb])
                                    S.op("act", lambda e: e.activation(out=pex[px][:, 0:256 * nq], in_=sv, func=AF.Exp), reads=[spb], writes=[b_pex[px]])
                                else:
                                    S.op("act", lambda e: e.activation(out=pex[px][:, 0:256 * nq], in_=sv, func=AF.Exp, bias=farb[:, h:h + 1], scale=1.0),
                                         reads=[spb, cb[14]], writes=[b_pex[px]])

                            def emit_pv(it, idx):
                                ri, kk, kb, jqs, ci = it
                                px = idx % NPX
                                for a_, jq in enumerate(jqs):
                                    ab, abb = accs[jq]
                                    for m in range(2):
                                        c0 = a_ * 256 + m * 128
                                        S.op("pe", lambda e: e.matmul(ab[:, m * 130:(m + 1) * 130], lhsT=pex[px][:, c0:c0 + 128], rhs=vch[ri][:, kk, :],
                                                                     start=False, stop=False, skip_group_check=True), reads=[b_pex[px], b_vch[ri]], writes=[abb])

                            for idx in range(len(items) + LAG):
                                if idx < len(items):
                                    emit_scores(items[idx], idx)
                                if idx >= LAG:
                                    emit_pv(items[idx - LAG], idx - LAG)
                                gcount[0] += 1
                                if gcount[0] % MSTEP == 0:
                                    next(mdrv, None)
                            for jq in range(4):
                                ab, abb = accs[jq]
                                av = ab[:, 0:260].rearrange("p (m c) -> p m c", m=2)
                                S.op("dve", lambda e: e.tensor_scalar(out=rr[:, 2 * jq:2 * jq + 2], in0=av[:, :, 128], scalar1=1e-30, scalar2=None, op0=ALU.max), reads=[abb], writes=[b_a])
                                S.op("dve", lambda e: e.reciprocal(out=rr[:, 2 * jq:2 * jq + 2], in_=rr[:, 2 * jq:2 * jq + 2]), reads=[b_a], writes=[b_a])
                                S.op("dve", lambda e: e.tensor_tensor(out=rr[:, 2 * jq + 1:2 * jq + 2], in0=rr[:, 2 * jq + 1:2 * jq + 2], in1=neglam[:], op=ALU.mult),
                                     reads=[b_a, CONST], writes=[b_a])
                                S.op("dve", lambda e: e.tensor_scalar(out=har[:, jq, :], in0=av[:, 0, 0:128], scalar1=rr[:, 2 * jq:2 * jq + 1], scalar2=None, op0=ALU.mult),
                                     reads=[abb, b_a], writes=[b_a])
                                S.op("dve", lambda e: e.scalar_tensor_tensor(out=har[:, jq, :], in0=av[:, 1, 0:128], scalar=rr[:, 2 * jq + 1:2 * jq + 2], in1=har[:, jq, :],
                                                                            op0=ALU.mult, op1=ALU.add), reads=[abb, b_a], writes=[b_a])
                            S.op("act", lambda e: e.activation(out=hsq[:], in_=har[:], func=AF.Square, scale=1.0 / math.sqrt(128.0)), reads=[b_a], writes=[b_a])
                            S.op("dve", lambda e: e.tensor_reduce(out=hss[:, 0:4], in_=hsq[:], axis=AX.X, op=ALU.add), reads=[b_a], writes=[b_a])
                            S.op("act", lambda e: e.activation(out=hss[:, 4:8], in_=hss[:, 0:4], func=AF.Ln, bias=epst[:, 0:1], scale=1.0), reads=[b_a, CONST], writes=[b_a])
                            S.op("act", lambda e: e.activation(out=hss[:, 4:8], in_=hss[:, 4:8], func=AF.Exp, scale=-0.5), reads=[b_a], writes=[b_a])
                            for jq in range(4):
                                S.op("dve", lambda e: e.scalar_tensor_tensor(out=haf[:, jq, :], in0=har[:, jq, :], scalar=hss[:, 4 + jq:5 + jq], in1=ang[:, h * 128:(h + 1) * 128],
                                                                            op0=ALU.mult, op1=ALU.mult), reads=[b_a, CONST] + cb, writes=[b_a])
                            tb, tbb = banks[4 + h % 3], bbufs[4 + h % 3]
                            tv = bfview(tb)[:, 0:512].rearrange("p (j t) -> p j t", j=4)
                            for jq in range(4):
                                S.op("pe", lambda e: e.transpose(out=tv[:, jq, :], in_=haf[:, jq, :], identity=identb[:]), reads=[b_a, cb[0]], writes=[tbb])
                            S.op("act", lambda e: e.copy(out=haT[:, h, :], in_=tv.rearrange("p j t -> p (j t)")), reads=[tbb], writes=[b_haT])
                        for _ in mdrv:
                            pass
                        S.barrier()

                    h2T = hT
                    b_h2T = b_hT
                    with ExitStack() as st:
                        W = {}
                        for nm in ("bm", "gm0", "ba", "ga0", "gm1", "ga1", "out0", "out1"):
                            W[nm] = wload(st, "w_" + nm, nm)
                        prefetch((2, 3), ["up0", "up1"])
                        yT = sbt(st, "yT", [128, 8, 512], BF16)
                        b_yT = S.buf("yT")
                        sg = [[sbt(st, "sg%d_%d" % (i, a), [128, 512]) for a in range(2)] for i in range(2)]
                        ty = [[sbt(st, "ty%d_%d" % (i, a), [128, 512]) for a in range(2)] for i in range(2)]
                        b_sg = [S.buf("sg%d" % i) for i in range(2)]
                        for c in range(8):
                            i = c % 2
                            res = []
                            for a, (bw, gw, srcT, b_src) in enumerate((("bm", "gm", hmT, b_hmT), ("ba", "ga", haT, b_haT))):
                                wt, wbf = W[bw]
                                wv = wt[:].rearrange("p (k n) -> p k n", k=4)
                                py, pyb = nbank()
                                for k in range(4):
                                    S.op("pe", lambda e: e.matmul(py[:], lhsT=wv[:, k, c * 128:(c + 1) * 128], rhs=srcT[:, k, :], start=(k == 0), stop=(k == 3)),
                                         reads=[wbf, b_src], writes=[pyb])
                                gt_, gbf = W[gw + str(c // 4)]
                                gv = gt_[:].rearrange("p (c n) -> p c n", c=8)
                                pg, pgb = nbank()
                                for dc in range(8):
                                    S.op("pe", lambda e: e.matmul(pg[:], lhsT=gv[:, dc, (c % 4) * 128:(c % 4 + 1) * 128], rhs=hT[:, dc, :], start=(dc == 0), stop=(dc == 7)),
                                         reads=[gbf, b_hT, b_hT2], writes=[pgb])
                                S.op("act", lambda e: e.activation(out=sg[i][a][:], in_=pg[:], func=AF.Sigmoid), reads=[pgb], writes=[b_sg[i]])
                                S.op("dve", lambda e: e.tensor_tensor(out=ty[i][a][:], in0=py[:], in1=sg[i][a][:], op=ALU.mult), reads=[pyb, b_sg[i]], writes=[b_sg[i]])
                            S.op("dve", lambda e: e.tensor_tensor(out=yT[:, c, :], in0=ty[i][0][:], in1=ty[i][1][:], op=ALU.add), reads=[b_sg[i]], writes=[b_yT])
                        tx = [sbt(st, "tx%d" % i, [128, 512]) for i in range(2)]
                        b_tx = [S.buf("tx%d" % i) for i in range(2)]
                        for j in range(4):
                            tsl = slice(j * 128, (j + 1) * 128)
                            for n in range(2):
                                wt, wbf = W["out%d" % n]
                                wv = wt[:].rearrange("p (c n) -> p c n", c=8)
                                po, pob = nbank()
                                for c in range(8):
                                    S.op("pe", lambda e: e.matmul(po[:], lhsT=yT[:, c, tsl], rhs=wv[:, c, :], start=(c == 0), stop=(c == 7)), reads=[wbf, b_yT], writes=[pob])
                                i = (j * 2 + n) % 2
                                S.op("dve", lambda e: e.tensor_tensor(out=tx[i][:], in0=po[:], in1=gate1[:, n * 512:(n + 1) * 512], op=ALU.mult), reads=[pob, CONST], writes=[b_tx[i]])
                                S.op("dve", lambda e: e.tensor_tensor(out=xs[:, j, n * 512:(n + 1) * 512], in0=xs[:, j, n * 512:(n + 1) * 512], in1=tx[i][:], op=ALU.add),
                                     reads=[b_tx[i], b_xs[j]], writes=[b_xs[j]])
                        for j in range(4):
                            norm_block(st, j, mult2, shift2, h2T, (b_hT, b_hT2), "n2")
                        S.barrier()

                    sa.close()
                    with ExitStack() as st:
                        actT = sbt(st, "actT", [128, NF, 512], BF16)
                        b_actT = S.buf("actT")
                        wu = [None, None, None]
                        uu = [sbt(st, "fu%d" % i, [128, 512]) for i in range(8)]
                        b_uu = [S.buf("fu%d" % i) for i in range(8)]
                        fe = [sbt(st, "fe%d" % i, [128, 512]) for i in range(2)]
                        b_fe = [S.buf("fe%d" % i) for i in range(2)]
                        wus = [pf[2], pf[3], sbt(st, "wup2", [128, 4096], BF16)]
                        b_wus = [b_pf[2], b_pf[3], S.buf("wup2")]

                        def ldup(jj):
                            if ("up%d" % jj) in prefetched:
                                prefetched.pop("up%d" % jj)
                                return
                            off, n = PIECES["up%d" % jj]
                            S.dma("sp", wus[jj % 3][:], wb_d[:, off:off + n], reads=[b_wb], writes=[b_wus[jj % 3]])

                        ldup(0)
                        ldup(1)
                        wd = sbt(st, "w_down", [128, NF * 1024], BF16)
                        wdb = S.buf("w_down")
                        wdv = wd[:].rearrange("p (f n) -> p f n", f=NF)
                        if not OPT_BATCH:
                            S.op("dve", lambda e: e.memset(fbc[:], 0.0), writes=[b_fbc])
                        if OPT_BATCH:
                            S.op("dve", lambda e: e.tensor_tensor(out=fbc[:, :, 1], in0=fhalo[:, :, 1], in1=fcw[:, :, 0], op=ALU.mult), reads=[b_fhalo] + cb, writes=[b_fbc])
                            S.op("dve", lambda e: e.tensor_tensor(out=fbc[:, :, 0], in0=fhalo[:, :, 1], in1=fcw[:, :, 1], op=ALU.mult), reads=[b_fhalo] + cb, writes=[b_fbc])
                            S.op("dve", lambda e: e.tensor_tensor(out=ctmp[:], in0=fhalo[:, :, 0], in1=fcw[:, :, 0], op=ALU.mult), reads=[b_fhalo] + cb, writes=[b_fbc])
                            S.op("dve", lambda e: e.tensor_tensor(out=fbc[:, :, 0], in0=fbc[:, :, 0], in1=ctmp[:], op=ALU.add), reads=[b_fbc], writes=[b_fbc])
                            S.op("dve", lambda e: e.tensor_tensor(out=fbc[:], in0=fbc[:], in1=fcb[:].unsqueeze(2).to_broadcast([128, 44, 2]), op=ALU.add), reads=[b_fbc] + cb, writes=[b_fbc])

                        def ffn_piece(jj):
                            if jj + 2 < 11:
                                ldup(jj + 2)
                            if jj == 2:
                                off_d, n_d = PIECES["down"]
                                S.dma("sp", wd[:], wb_d[:, off_d:off_d + n_d], reads=[b_wb], writes=[wdb])
                            if jj == 4 and u + 1 < NU:
                                prefetch((0, 1), ["mq", "mk"])
                            wv = wus[jj % 3][:].rearrange("p (c n) -> p c n", c=8)
                            wbf = b_wus[jj % 3]
                            wsel = fcwf if flagged else fcw
                            pbs = []
                            for k in range(4):
                                q = jj * 4 + k
                                pb, pbb = nbank()
                                pbs.append((pb, pbb))
                                for dc in range(8):
                                    S.op("pe", lambda e: e.matmul(pb[:], lhsT=wv[:, dc, k * 128:(k + 1) * 128], rhs=h2T[:, dc, :], start=(dc == 0), stop=(dc == 7)),
                                         reads=[wbf, b_hT, b_hT2], writes=[pbb])
                                kk_ = (jj % 2) * 4 + k
                                m0 = 2 if OPT_PARTMAIN else 0
                                S.op("act", lambda e: e.activation(out=uu[kk_][:, m0:512], in_=pb[:, m0:512], func=AF.Identity, scale=wsel[:, q, 2:3], bias=fcb[:, q:q + 1]),
                                     reads=[pbb, CONST] + cb, writes=[b_uu[kk_]])
                                for col in (range(2) if OPT_TINY else ()):
                                    S.op("act", lambda e: e.activation(out=uu[kk_][:, col:col + 1], in_=pb[:, col:col + 1], func=AF.Identity, scale=wsel[:, q, 2:3],
                                                                      bias=fbc[:, q, col:col + 1]), reads=[pbb, CONST, b_fbc] + cb, writes=[b_uu[kk_]])
                                if flagged:
                                    S.op("act", lambda e: e.activation(out=fhalo[:, q, :], in_=pb[:, 510:512], func=AF.Copy, scale=flag[:, 0:1]), reads=[pbb, cb[4], b_fbc], writes=[b_fhalo])
                                else:
                                    S.op("act", lambda e: e.copy(out=fhalo[:, q, :], in_=pb[:, 510:512]), reads=[pbb, b_fbc], writes=[b_fhalo])
                            for tp in (1, 0):
                                sh = 2 - tp
                                for k in range(4):
                                    q = jj * 4 + k
                                    kk_ = (jj % 2) * 4 + k
                                    pb, pbb = pbs[k]
                                    S.op("dve", lambda e: e.scalar_tensor_tensor(out=uu[kk_][:, sh:512], in0=pb[:, 0:512 - sh], scalar=wsel[:, q, tp:tp + 1], in1=uu[kk_][:, sh:512],
                                                                                op0=ALU.mult, op1=ALU.add), reads=[pbb, b_uu[kk_], CONST], writes=[b_uu[kk_]])
                            yield
                            for i in range(2):
                                f = 2 * jj + i
                                uv = uu[(jj % 2) * 4 + i]
                                ug = uu[(jj % 2) * 4 + 2 + i]
                                b_uv = b_uu[(jj % 2) * 4 + i]
                                b_ug = b_uu[(jj % 2) * 4 + 2 + i]
                                S.op("act", lambda e: e.activation(out=fe[i][:], in_=ug[:], func=AF.Silu), reads=[b_ug], writes=[b_fe[i]])
                                S.op("dve", lambda e: e.tensor_tensor(out=actT[:, f, :], in0=uv[:], in1=fe[i][:], op=ALU.mult), reads=[b_uv, b_fe[i]], writes=[b_actT])
                        fgens = [ffn_piece(jj) for jj in range(11)]
                        for step in range(12):
                            if step < 11:
                                next(fgens[step], None)
                            if step >= 1:
                                next(fgens[step - 1], None)
                        tx = fe
                        b_tx = b_fe
                        for j in range(4):
                            tsl = slice(j * 128, (j + 1) * 128)
                            blk = u * 4 + j
                            for n in range(2):
                                po, pob = nbank()
                                for f in range(NF):
                                    S.op("pe", lambda e: e.matmul(po[:], lhsT=actT[:, f, tsl], rhs=wdv[:, f, n * 512:(n + 1) * 512], start=(f == 0), stop=(f == NF - 1)),
                                         reads=[wdb, b_actT], writes=[pob])
                                i = (j * 2 + n) % 2
                                S.op("dve", lambda e: e.tensor_tensor(out=tx[i][:], in0=po[:], in1=gate2[:, n * 512:(n + 1) * 512], op=ALU.mult), reads=[pob, CONST], writes=[b_tx[i]])
                                S.op("dve", lambda e: e.tensor_tensor(out=xs[:, j, n * 512:(n + 1) * 512], in0=xs[:, j, n * 512:(n + 1) * 512], in1=tx[i][:], op=ALU.add),
                                     reads=[b_tx[i], b_xs[j]], writes=[b_xs[j]])
                            if own:
                                ob = blk - NFLAG * 4
                                S.dma("pool", out_d[ob * 128:(ob + 1) * 128, :], xs[:, j, :], reads=[b_xs[j]], writes=[b_out])
                        S.barrier()
        S.barrier()
    return nc


def _t5_bucket(n):
    n = np.maximum(n, 0)
    max_exact = 16
    nf = np.maximum(n, 1).astype(np.float32)
    large = max_exact + (np.log(nf / np.float32(max_exact)) / np.float32(math.log(128 / max_exact)) * np.float32(32 - max_exact)).astype(np.int32)
    large = np.minimum(large, 31)
    return np.where(n < max_exact, n, large)


def _fm(v, nch):
    return np.ascontiguousarray(np.asarray(v, np.float32).reshape(nch, 128).T)


def _piece_fm(w):
    n = w.shape[1]
    return w.reshape(8, 128, n).transpose(1, 0, 2).reshape(128, 8 * n)


def prepare_inputs(NCTX, NFULL, inputs):
    f32 = np.float32
    g = {k: np.asarray(v) for k, v in inputs.items()}
    NU = NCTX + NFULL
    half_tok = (NU // 2) * 512
    x = g["x"].astype(f32, copy=False)
    B = x.shape[0]
    assert x.shape[1] == 2 * half_tok
    w_in = g["w_in"][0]
    cols = {}
    o = 0
    for nm, n in (("mqk", 1024), ("mv", 512), ("mo", 512), ("mi", 4), ("mf", 4), ("aq", 512), ("ak", 512), ("av", 512), ("gm", 1024), ("ga", 1024)):
        cols[nm] = w_in[:, o:o + n]
        o += n

    def perm_qk(w):
        return w.reshape(1024, 2, 4, 64).transpose(0, 2, 1, 3).reshape(1024, 512)

    wall = np.zeros((128, WPAD), f32)

    def put(nm, arr):
        off, n = PIECES[nm]
        assert arr.shape == (128, n), (nm, arr.shape, n)
        wall[:, off:off + n] = arr

    put("mq", _piece_fm(cols["mqk"][:, 0:512]))
    put("mk", _piece_fm(cols["mqk"][:, 512:1024]))
    put("mv", _piece_fm(cols["mv"]))
    put("mo", _piece_fm(cols["mo"]))
    put("aq", _piece_fm(perm_qk(cols["aq"])))
    put("ak", _piece_fm(perm_qk(cols["ak"])))
    put("av", _piece_fm(cols["av"]))
    put("gm0", _piece_fm(cols["gm"][:, 0:512]))
    put("gm1", _piece_fm(cols["gm"][:, 512:1024]))
    put("ga0", _piece_fm(cols["ga"][:, 0:512]))
    put("ga1", _piece_fm(cols["ga"][:, 512:1024]))
    put("gif", _piece_fm(np.concatenate([cols["mi"], cols["mf"]], axis=1)))
    put("bm", g["w_branch_m"][0].reshape(4, 128, 1024).transpose(1, 0, 2).reshape(128, 4096))
    put("ba", g["w_branch_a"][0].reshape(4, 128, 1024).transpose(1, 0, 2).reshape(128, 4096))
    put("out0", _piece_fm(g["w_out"][0][:, 0:512]))
    put("out1", _piece_fm(g["w_out"][0][:, 512:1024]))
    w_up = g["w_up"][0]
    fcw_full = g["ffn_conv_w"][0]
    fcb_full = g["ffn_conv_b"][0]
    chunk_cols = []
    for jj in range(11):
        cc = [np.arange((2 * jj + i) * 128, (2 * jj + i + 1) * 128) for i in range(2)]
        cc += [DFF + np.arange((2 * jj + i) * 128, (2 * jj + i + 1) * 128) for i in range(2)]
        idx = np.concatenate(cc)
        chunk_cols.append(idx)
        put("up%d" % jj, _piece_fm(w_up[:, idx]))
    allidx = np.concatenate(chunk_cols)
    fcw = fcw_full[:, allidx].reshape(3, 44, 128).transpose(2, 1, 0).reshape(128, 44 * 3)
    fcb = fcb_full[allidx].reshape(44, 128).T
    put("down", g["w_down"][0].reshape(NF, 128, 1024).transpose(1, 0, 2).reshape(128, NF * 1024))

    w_ada = g["w_ada"][0]
    wada = np.stack([_piece_fm(w_ada[:, p * 512:(p + 1) * 512]) for p in range(12)]).astype(f32)
    bada = g["b_ada"][0].astype(f32)
    mcw = g["m_conv_w"][0].reshape(4, 8, 128).transpose(2, 1, 0).reshape(128, 32)
    mcb = _fm(g["m_conv_b"][0], 8)
    gifb = np.concatenate([g["m_igate_b"][0], g["m_fgate_b"][0]]).astype(f32)
    gq = np.tile(g["a_qnorm_g"][0], 8).astype(f32)
    gk = np.tile(g["a_knorm_g"][0], 8).astype(f32)
    rel = g["rel_bias"].astype(f32)
    kk = np.arange(128)[:, None]
    qq = np.arange(128)[None, :]
    biasg = np.zeros((128, 4, 2, 2, 128), f32)
    maskn = np.zeros((128, 4, 2, 2, 128), f32)
    for dl in range(2):
        dist = qq - kk + 128 * dl
        bidx = _t5_bucket(dist)
        for h in range(4):
            t = rel[bidx, h]
            biasg[:, h, dl, 0, :] = t
            biasg[:, h, dl, 1, :] = t
        if dl == 0:
            mk = np.where(dist < 0, NEG, 0.0).astype(f32)
            maskn[:, :, 0, :, :] = mk[:, None, None, :]
    farb = rel[31, :].astype(f32)
    umask = (kk <= qq).astype(f32)
    negm4 = np.tile(np.where(kk <= qq, 0.0, NEG).astype(f32), (1, 4))
    import ml_dtypes
    common = dict(
        wada=wada, bada=bada, badafm=_fm(bada, 48), g1fm=_fm(g["norm1_g"][0], 8), g2fm=_fm(g["norm2_g"][0], 8),
        wall=wall, mcw=np.ascontiguousarray(mcw, f32), mcb=mcb, fcw=np.ascontiguousarray(fcw, f32), fcb=np.ascontiguousarray(fcb, f32),
        gifb=gifb, mng=g["m_norm_g"][0].astype(f32), ang=g["a_norm_g"][0].astype(f32), gq=gq, gk=gk,
        gqfm=np.tile(g["a_qnorm_g"][0], 2).reshape(128, 1).astype(f32), gkfm=np.tile(g["a_knorm_g"][0], 2).reshape(128, 1).astype(f32),
        alam=g["a_lambda"][0].reshape(256).astype(f32), biasg=biasg.reshape(128, -1), maskn=maskn.reshape(128, -1), farb=farb,
        identb=np.eye(128).astype(ml_dtypes.bfloat16), identf=np.eye(128, dtype=f32), umask=umask, negm4=negm4,
    )
    in_maps = []
    for b in range(B):
        cfm = _fm(g["c"][b], 8)
        for hf in range(2):
            if hf == 0:
                xl = np.concatenate([np.zeros((half_tok, D), f32), x[b, 0:half_tok]], axis=0)
                fl = np.zeros((128, 1), f32)
            else:
                xl = x[b]
                fl = np.ones((128, 1), f32)
            m = dict(common)
            m["x"] = np.ascontiguousarray(xl)
            m["cfm"] = cfm
            m["flag"] = fl
            in_maps.append(m)
    return in_maps


_NC_CACHE = {}


def run(NCTX, NFULL, inputs):
    key = (NCTX, NFULL)
    if key not in _NC_CACHE:
        _NC_CACHE[key] = build_program(NCTX, NFULL)
    nc = _NC_CACHE[key]
    in_maps = prepare_inputs(NCTX, NFULL, inputs)
    res = run_bass_kernel_spmd(nc, in_maps, core_ids=list(range(len(in_maps))))
    B = len(in_maps) // 2
    half_tok = ((NCTX + NFULL) // 2) * 512
    out = np.empty((B, 2 * half_tok, D), np.float32)
    for b in range(B):
        for hf in range(2):
            out[b, hf * half_tok:(hf + 1) * half_tok] = res.results[b * 2 + hf]["out"]
    return out


def kernel(**inputs):
    return run(7, 9, inputs)
```

```python
import math
from contextlib import ExitStack

import numpy as np

import concourse.bass as bass
import concourse.mybir as mybir
from concourse.bass_utils import run_bass_kernel_spmd

F32 = mybir.dt.float32
BF16 = mybir.dt.bfloat16
AF = mybir.ActivationFunctionType
ALU = mybir.AluOpType
AX = mybir.AxisListType

D = 1024
DC = 8
DFF = 2816
NF = 22
EPS = 1e-6
LAM_INIT = 0.8 - 0.6 * math.exp(-0.3 * 0)
NEG = -30000.0
OPT_SELF_WAR = True
OPT_XN_ACT = True
OPT_SILU = True
OPT_GFOLD = True
OPT_TINY = True
OPT_BATCH = True
OPT_PARTMAIN = True

PIECES = {}
_off = 0
for _nm, _n in (("mq", 4096), ("mk", 4096), ("mv", 4096), ("mo", 4096), ("aq", 4096), ("ak", 4096),
                ("av", 4096), ("gm0", 4096), ("gm1", 4096), ("ga0", 4096), ("ga1", 4096), ("gif", 64),
                ("bm", 4096), ("ba", 4096), ("out0", 4096), ("out1", 4096)):
    PIECES[_nm] = (_off, _n)
    _off += _n
for _j in range(11):
    PIECES["up%d" % _j] = (_off, 4096)
    _off += 4096
PIECES["down"] = (_off, NF * 1024)
_off += NF * 1024
WTOT = _off
WPAD = ((WTOT + 4095) // 4096) * 4096


class Buf:
    __slots__ = ("name", "w", "rs", "sem", "semv", "grp", "psum", "persist")

    def __init__(self, name, grp=None):
        self.psum = False
        self.persist = False
        self.name = name
        self.w = None
        self.rs = []
        self.sem = None
        self.semv = 0
        self.grp = grp


class Sched:
    def __init__(self, nc, stack):
        self.nc = nc
        self.stack = stack
        self.engs = {}
        for nm, h in (("pe", nc.tensor), ("act", nc.scalar), ("dve", nc.vector), ("pool", nc.gpsimd), ("sp", nc.sync)):
            sem = stack.enter_context(nc.semaphore("s_" + nm))
            self.engs[nm] = dict(h=h, sem=sem, cnt=0, seen={})
        self.groups = {}
        self.dma_ev = {}
        self.free_sems = {}
        self.live = []
        self.nsem = 0

    def buf(self, name, grp=None):
        return Buf(name, grp)

    def _need(self, e, deps):
        E = self.engs[e]
        best = {}
        for d in deps:
            if d is None:
                continue
            sem, val, en = d
            if en == e and e == "pe":
                continue
            k = id(sem)
            if k not in best or best[k][1] < val:
                best[k] = (sem, val)
        for k, (sem, val) in best.items():
            if E["seen"].get(k, 0) >= val:
                continue
            E["h"].wait_ge(sem, val)
            E["seen"][k] = val

    def op(self, e, fn, reads=(), writes=()):
        E = self.engs[e]
        deps = []
        for b in reads:
            deps.append(b.w)
            if b.psum:
                for r in b.rs:
                    if r[2] != e:
                        deps.append(r)
        for b in writes:
            if b.w is not None and b.w[2] != e:
                deps.append(b.w)
            for r in b.rs:
                if OPT_SELF_WAR or r[2] != e:
                    deps.append(r)
        self._need(e, deps)
        ins = fn(E["h"])
        E["cnt"] += 1
        ins.then_inc(E["sem"], 1)
        ev = (E["sem"], E["cnt"], e)
        for b in reads:
            b.rs.append(ev)
        for b in writes:
            b.w = ev
            b.rs = []
        return ins

    def dma(self, q, out, in_, reads=(), writes=()):
        E = self.engs[q]
        deps = []
        for b in reads:
            deps.append(b.w)
        for b in writes:
            if b.grp in ("wbw", "kvw", "outw"):
                continue
            deps.append(b.w)
            deps.extend(b.rs)
        self._need(q, deps)
        ins = E["h"].dma_start(out=out, in_=in_)
        cands = list(writes) + list(reads)
        tgt = ([b for b in cands if b.grp is None] + cands)[0]
        if tgt.grp is not None:
            gk = tgt.grp
            if gk not in self.groups:
                self.groups[gk] = [self.stack.enter_context(self.nc.semaphore("g_" + gk)), 0]
            g = self.groups[gk]
            g[1] += 16
            sem, val = g[0], g[1]
        else:
            if tgt.sem is None:
                tgt.sem = {}
                self.live.append(tgt)
            if q not in tgt.sem:
                fs = self.free_sems.setdefault(q, [])
                if not fs:
                    self.nsem += 1
                    fs.append([self.stack.enter_context(self.nc.semaphore("dp%d" % self.nsem)), 0])
                tgt.sem[q] = fs.pop()
            ent = tgt.sem[q]
            ent[1] += 16
            sem, val = ent[0], ent[1]
        ins.then_inc(sem, 16)
        self.dma_ev[id(sem)] = (sem, val)
        ev = (sem, val, "dma")
        for b in reads:
            b.rs.append(ev)
        for b in writes:
            b.w = ev
            b.rs = []
        return ins

    def barrier(self, engines=("pe", "act", "dve", "pool", "sp")):
        deps = [(E["sem"], E["cnt"], "x") for E in self.engs.values() if E["cnt"] > 0]
        deps += [(s, v, "dma") for (s, v) in self.dma_ev.values()]
        for e in engines:
            self._need(e, deps)
        self.dma_ev = {}
        keep = []
        for b in self.live:
            if b.persist:
                keep.append(b)
                continue
            for qq, ent in b.sem.items():
                self.free_sems[qq].append(ent)
            b.sem = None
        self.live = keep

    def seal(self, bufs, grp):
        g = self.groups[grp]
        for b in bufs:
            b.w = (g[0], g[1], "dma")


def build_program(NCTX, NFULL, debug=False):
    NU = NCTX + NFULL
    NFLAG = NU // 2
    NBLK = NU * 4
    NTOK = NU * 512
    NOWN = (NFULL - 1) * 512
    assert NU == 2 * (NFULL - 1)

    nc = bass.Bass("TRN2", target_bir_lowering=False)

    def din(name, shape, dt=F32):
        return nc.dram_tensor(name, list(shape), dt, kind="ExternalInput").ap()

    x_d = din("x", [NTOK, D])
    cfm_d = din("cfm", [128, 8])
    wada_d = din("wada", [12, 128, 8 * 512])
    bada_d = din("bada", [6144])
    badafm_d = din("badafm", [128, 48])
    g1fm_d = din("g1fm", [128, 8])
    g2fm_d = din("g2fm", [128, 8])
    wall_d = din("wall", [128, WPAD])
    mcw_d = din("mcw", [128, 8 * 4])
    mcb_d = din("mcb", [128, 8])
    fcw_d = din("fcw", [128, 44 * 3])
    fcb_d = din("fcb", [128, 44])
    gifb_d = din("gifb", [8])
    mng_d = din("mng", [512])
    ang_d = din("ang", [512])
    gq_d = din("gq", [512])
    gk_d = din("gk", [512])
    gqfm_d = din("gqfm", [128, 1])
    gkfm_d = din("gkfm", [128, 1])
    alam_d = din("alam", [256])
    biasg_d = din("biasg", [128, 4 * 2 * 2 * 128])
    maskn_d = din("maskn", [128, 4 * 2 * 2 * 128])
    farb_d = din("farb", [4])
    flag_d = din("flag", [128, 1])
    identb_d = din("identb", [128, 128], BF16)
    identf_d = din("identf", [128, 128])
    umask_d = din("umask", [128, 128])
    negm4_d = din("negm4", [128, 512])
    out_d = nc.dram_tensor("out", [NOWN, D], F32, kind="ExternalOutput").ap()
    wb_d = nc.dram_tensor("wb_scr", [128, WPAD], BF16, kind="Internal").ap()
    kt_d = nc.dram_tensor("kt_scr", [4, 128, NBLK * 128], BF16, kind="Internal").ap()
    va_d = nc.dram_tensor("va_scr", [4, 128, NBLK * 130], BF16, kind="Internal").ap()

    with ExitStack() as top:
        S = Sched(nc, top)

        uid = [0]

        def sbt(st, name, shape, dt=F32):
            uid[0] += 1
            return st.enter_context(nc.sbuf_tensor("s%d_%s" % (uid[0], name), list(shape), dt))

        banks = [top.enter_context(nc.psum_tensor("bank%d" % i, [128, 512], F32)) for i in range(8)]
        bbufs = [S.buf("bank%d" % i) for i in range(8)]
        for b_ in bbufs:
            b_.psum = True
        bank_rr = [0]

        def nbank(lo=0, hi=8):
            i = lo + (bank_rr[0] % (hi - lo))
            bank_rr[0] += 1
            return banks[i], bbufs[i]

        def bfview(bank):
            return bank[:].bitcast(BF16)

        identb = sbt(top, "identb", [128, 128], BF16)
        identf = sbt(top, "identf", [128, 128])
        umask = sbt(top, "umask", [128, 128])
        negm4 = sbt(top, "negm4", [128, 512])
        ones_f = sbt(top, "ones_f", [128, 128])
        flag = sbt(top, "flag", [128, 1])
        epst = sbt(top, "epst", [128, 1])
        onet = sbt(top, "onet", [128, 1])
        mult1 = sbt(top, "mult1", [128, 8])
        shift1 = sbt(top, "shift1", [128, 8])
        mult2 = sbt(top, "mult2", [128, 8])
        shift2 = sbt(top, "shift2", [128, 8])
        gate1 = sbt(top, "gate1", [128, D])
        gate2 = sbt(top, "gate2", [128, D])
        mcw = sbt(top, "mcw", [128, 8, 4])
        mcb = sbt(top, "mcb", [128, 8])
        mcwf = sbt(top, "mcwf", [128, 8, 4])
        fcwf = sbt(top, "fcwf", [128, 44, 3])
        mcnb = sbt(top, "mcnb", [128, 8])
        fcw = sbt(top, "fcw", [128, 44, 3])
        fcb = sbt(top, "fcb", [128, 44])
        fcnb = sbt(top, "fcnb", [128, 44])
        gifb = sbt(top, "gifb", [128, 8])
        mng = sbt(top, "mng", [128, 512])
        ang = sbt(top, "ang", [128, 512])
        gq = sbt(top, "gq", [128, 512])
        gk = sbt(top, "gk", [128, 512])
        gqfm = sbt(top, "gqfm", [128, 1])
        gkfm = sbt(top, "gkfm", [128, 1])
        mbc = sbt(top, "mbc", [128, 8, 3])
        fbc = sbt(top, "fbc", [128, 44, 2])
        ctmp = sbt(top, "ctmp", [128, 44])
        b_mbc = S.buf("mbc")
        b_fbc = S.buf("fbc")
        biasb = sbt(top, "biasb", [128, 4, 2, 256], BF16)
        farb = sbt(top, "farb", [128, 4])
        neglam = sbt(top, "neglam", [128, 1])
        Sst = sbt(top, "Sst", [128, 4, 130])
        Sstb = sbt(top, "Sstb", [128, 4, 130], BF16)
        mhalo = sbt(top, "mhalo", [128, 8, 3])
        fhalo = sbt(top, "fhalo", [128, 44, 2])
        CONST = S.buf("const")
        b_S = S.buf("Sst")
        b_Sb = S.buf("Sstb")
        b_mhalo = S.buf("mhalo")
        b_fhalo = S.buf("fhalo")
        b_wb = S.buf("wb_scr", grp="wbw")
        b_ktd = S.buf("kt_scr", grp="kvw")
        b_vad = S.buf("va_scr", grp="kvw")
        b_out = S.buf("outd", grp="outw")

        def ld_const(t, src, grp="cst"):
            b = S.buf("c_" + t.name, grp=grp)
            S.dma("sp", t[:], src, writes=[b])
            return b

        cb = []
        cb.append(ld_const(identb, identb_d[:, :]))
        cb.append(ld_const(identf, identf_d[:, :]))
        cb.append(ld_const(umask, umask_d[:, :]))
        cb.append(ld_const(negm4, negm4_d[:, :]))
        cb.append(ld_const(flag, flag_d[:, :]))
        cb.append(ld_const(mcw, mcw_d.rearrange("p (c k) -> p c k", k=4)))
        cb.append(ld_const(mcb, mcb_d[:, :]))
        cb.append(ld_const(fcw, fcw_d.rearrange("p (c k) -> p c k", k=3)))
        cb.append(ld_const(fcb, fcb_d[:, :]))
        cb.append(ld_const(gifb, gifb_d.partition_broadcast(128)))
        cb.append(ld_const(mng, mng_d.partition_broadcast(128)))
        cb.append(ld_const(ang, ang_d.partition_broadcast(128)))
        cb.append(ld_const(gq, gq_d.partition_broadcast(128)))
        cb.append(ld_const(gk, gk_d.partition_broadcast(128)))
        cb.append(ld_const(farb, farb_d.partition_broadcast(128)))
        cb.append(ld_const(gqfm, gqfm_d[:, :]))
        cb.append(ld_const(gkfm, gkfm_d[:, :]))
        S.seal(cb, "cst")
        S.op("dve", lambda e: e.memset(ones_f[:], 1.0), writes=[CONST])
        S.op("dve", lambda e: e.memset(epst[:], EPS), writes=[CONST])
        S.op("dve", lambda e: e.memset(onet[:], 1.0), writes=[CONST])
        S.op("dve", lambda e: e.memset(Sst[:], 0.0), writes=[b_S])
        S.op("dve", lambda e: e.memset(Sstb[:], 0.0), writes=[b_Sb])
        S.op("dve", lambda e: e.memset(mhalo[:], 0.0), writes=[b_mhalo])
        S.op("dve", lambda e: e.memset(fhalo[:], 0.0), writes=[b_fhalo])
        S.op("dve", lambda e: e.tensor_scalar(out=mcnb[:], in0=mcb[:], scalar1=-1.0, scalar2=None, op0=ALU.mult), reads=cb, writes=[CONST])
        S.op("dve", lambda e: e.tensor_scalar(out=fcnb[:], in0=fcb[:], scalar1=-1.0, scalar2=None, op0=ALU.mult), reads=cb, writes=[CONST])
        S.op("dve", lambda e: e.tensor_scalar(out=gqfm[:], in0=gqfm[:], scalar1=0.125, scalar2=None, op0=ALU.mult), reads=cb, writes=[CONST])
        S.op("dve", lambda e: e.tensor_scalar(out=mcwf[:], in0=mcw[:], scalar1=flag[:, 0:1], scalar2=None, op0=ALU.mult), reads=cb, writes=[CONST])
        S.op("dve", lambda e: e.tensor_scalar(out=fcwf[:], in0=fcw[:], scalar1=flag[:, 0:1], scalar2=None, op0=ALU.mult), reads=cb, writes=[CONST])
        S.op("dve", lambda e: e.tensor_scalar(out=ang[:], in0=ang[:], scalar1=1.0 - LAM_INIT, scalar2=None, op0=ALU.mult), reads=cb, writes=[CONST])

        with ExitStack() as st:
            cfm = sbt(st, "cfm", [128, 8])
            sc = sbt(st, "sc", [128, 8])
            sct = sbt(st, "sct", [128, 8])
            scbc = sbt(st, "scbc", [128, 8, 128])
            badafm = sbt(st, "badafm", [128, 48])
            g1fm = sbt(st, "g1fm", [128, 8])
            g2fm = sbt(st, "g2fm", [128, 8])
            modfm = sbt(st, "modfm", [128, 48])
            alam = sbt(st, "alam", [128, 256])
            lt = sbt(st, "lt", [128, 128])
            ls = sbt(st, "ls", [128, 2])
            biasg = sbt(st, "biasg", [128, 2048])
            maskn = sbt(st, "maskn", [128, 2048])
            wst = [sbt(st, "wst%d" % i, [128, 4096]) for i in range(2)]
            wbo = [sbt(st, "wbo%d" % i, [128, 4096], BF16) for i in range(2)]
            bbt = [sbt(st, "bbt%d" % i, [128, 512]) for i in range(2)]
            b_wst = [S.buf("wst%d" % i) for i in range(2)]
            b_wbo = [S.buf("wbo%d" % i) for i in range(2)]
            b_bbt = [S.buf("bbt%d" % i) for i in range(2)]
            L = S.buf("prel")
            b_l = [ld_const(cfm, cfm_d[:, :], "cst2"), ld_const(badafm, badafm_d[:, :], "cst2"), ld_const(g1fm, g1fm_d[:, :], "cst2"),
                   ld_const(g2fm, g2fm_d[:, :], "cst2"), ld_const(alam, alam_d.partition_broadcast(128), "cst2"),
                   ld_const(biasg, biasg_d[:, :], "cst2"), ld_const(maskn, maskn_d[:, :], "cst2")]
            S.seal(b_l, "cst2")
            S.op("act", lambda e: e.activation(out=sct[:], in_=cfm[:], func=AF.Exp, scale=-1.0), reads=b_l, writes=[L])
            S.op("dve", lambda e: e.tensor_scalar(out=sct[:], in0=sct[:], scalar1=1.0, scalar2=None, op0=ALU.add), reads=[L], writes=[L])
            S.op("dve", lambda e: e.reciprocal(out=sct[:], in_=sct[:]), reads=[L], writes=[L])
            S.op("dve", lambda e: e.tensor_tensor(out=sc[:], in0=cfm[:], in1=sct[:], op=ALU.mult), reads=[L], writes=[L])
            S.op("dve", lambda e: e.tensor_copy(out=scbc[:], in_=sc[:].unsqueeze(2).to_broadcast([128, 8, 128])), reads=[L], writes=[L])
            S.op("dve", lambda e: e.tensor_tensor(out=lt[:, 0:64], in0=alam[:, 0:64], in1=alam[:, 64:128], op=ALU.mult), reads=b_l, writes=[L])
            S.op("dve", lambda e: e.tensor_tensor(out=lt[:, 64:128], in0=alam[:, 128:192], in1=alam[:, 192:256], op=ALU.mult), reads=[L], writes=[L])
            S.op("dve", lambda e: e.tensor_reduce(out=ls[:], in_=lt[:].rearrange("p (a b) -> p a b", a=2), axis=AX.X, op=ALU.add), reads=[L], writes=[L])
            S.op("act", lambda e: e.activation(out=ls[:], in_=ls[:], func=AF.Exp), reads=[L], writes=[L])
            S.op("dve", lambda e: e.tensor_tensor(out=neglam[:], in0=ls[:, 1:2], in1=ls[:, 0:1], op=ALU.subtract), reads=[L], writes=[CONST])
            S.op("dve", lambda e: e.tensor_scalar(out=neglam[:], in0=neglam[:], scalar1=-LAM_INIT, scalar2=None, op0=ALU.add), reads=[CONST], writes=[CONST])
            S.op("dve", lambda e: e.tensor_tensor(out=biasb[:].rearrange("p a b c -> p (a b c)"), in0=biasg[:], in1=maskn[:], op=ALU.add), reads=b_l, writes=[CONST])

            fm_ps, fm_b = banks[7], bbufs[7]
            fm_cols = {0: 0, 1: 4, 2: 8, 3: 12, 6: 24, 7: 28, 8: 32, 9: 36}
            for pc in range(12):
                i = pc % 2
                S.dma("sp", wst[i][:], wada_d[pc, :, :], writes=[b_wst[i]])
                wv = wst[i][:].rearrange("p (c n) -> p c n", c=8)
                if pc in (4, 5, 10, 11):
                    S.dma("sp", bbt[i][:], bada_d[pc * 512:(pc + 1) * 512].partition_broadcast(128), writes=[b_bbt[i]])
                    pb, pbb = nbank(0, 6)
                    for dc in range(8):
                        S.op("pe", lambda e: e.matmul(pb[:], lhsT=scbc[:, dc, :], rhs=wv[:, dc, :], start=(dc == 0), stop=(dc == 7)),
                             reads=[L, b_wst[i]], writes=[pbb])
                    gt = gate1 if pc < 6 else gate2
                    off = (pc % 2) * 512
                    S.op("dve", lambda e: e.tensor_tensor(out=gt[:, off:off + 512], in0=pb[:], in1=bbt[i][:], op=ALU.add),
                         reads=[pbb, b_bbt[i]], writes=[CONST])
                else:
                    for k in range(4):
                        col = fm_cols[pc] + k
                        for dc in range(8):
                            S.op("pe", lambda e: e.matmul(fm_ps[:, col:col + 1], lhsT=wv[:, dc, k * 128:(k + 1) * 128], rhs=sc[:, dc:dc + 1],
                                                         start=(dc == 0), stop=(dc == 7)), reads=[L, b_wst[i]], writes=[fm_b])
            S.op("dve", lambda e: e.tensor_tensor(out=modfm[:, 0:16], in0=fm_ps[:, 0:16], in1=badafm[:, 0:16], op=ALU.add), reads=[fm_b] + b_l, writes=[L])
            S.op("dve", lambda e: e.tensor_tensor(out=modfm[:, 24:40], in0=fm_ps[:, 24:40], in1=badafm[:, 24:40], op=ALU.add), reads=[fm_b] + b_l, writes=[L])
            S.op("dve", lambda e: e.scalar_tensor_tensor(out=mult1[:], in0=modfm[:, 8:16], scalar=1.0, in1=g1fm[:], op0=ALU.add, op1=ALU.mult), reads=[L], writes=[CONST])
            S.op("dve", lambda e: e.tensor_copy(out=shift1[:], in_=modfm[:, 0:8]), reads=[L], writes=[CONST])
            S.op("dve", lambda e: e.scalar_tensor_tensor(out=mult2[:], in0=modfm[:, 32:40], scalar=1.0, in1=g2fm[:], op0=ALU.add, op1=ALU.mult), reads=[L], writes=[CONST])
            S.op("dve", lambda e: e.tensor_copy(out=shift2[:], in_=modfm[:, 24:32]), reads=[L], writes=[CONST])

            cast_eng = ["dve", "act"]
            for ch in range(WPAD // 4096):
                i = ch % 2
                S.dma("sp", wst[i][:], wall_d[:, ch * 4096:(ch + 1) * 4096], writes=[b_wst[i]])
                ce = cast_eng[ch % 2]
                if ce == "act":
                    S.op("act", lambda e: e.copy(out=wbo[i][:], in_=wst[i][:]), reads=[b_wst[i]], writes=[b_wbo[i]])
                else:
                    S.op(ce, lambda e: e.tensor_copy(out=wbo[i][:], in_=wst[i][:]), reads=[b_wst[i]], writes=[b_wbo[i]])
                S.dma("pool", wb_d[:, ch * 4096:(ch + 1) * 4096], wbo[i][:], reads=[b_wbo[i]], writes=[b_wb])
            S.barrier()

        njunk = sbt(top, "njunk", [128, D], BF16)
        ncache = {}
        pf = [sbt(top, "pf%d" % i, [128, 4096], BF16) for i in range(4)]
        b_pf = [S.buf("pf%d" % i) for i in range(4)]
        for b_ in b_pf:
            b_.persist = True
        prefetched = {}

        def prefetch(slots, pieces):
            for sl, piece in zip(slots, pieces):
                off, n = PIECES[piece]
                S.dma("sp", pf[sl][:, 0:n], wb_d[:, off:off + n], reads=[b_wb], writes=[b_pf[sl]])
                prefetched[piece] = (pf[sl], b_pf[sl])

        def wload(st, name, piece, shape3=None):
            if piece in prefetched:
                return prefetched.pop(piece)
            off, n = PIECES[piece]
            t = sbt(st, name, [128, n], BF16)
            b = S.buf(name)
            S.dma("sp", t[:], wb_d[:, off:off + n], reads=[b_wb], writes=[b])
            return t, b

        for u in range(NU):
            full = u >= NCTX
            flagged = u < NFLAG
            own = u >= NFLAG
            with ExitStack() as su:
                xs = sbt(su, "xs", [128, 4, D])
                hT = sbt(su, "hT", [128, 8, 512], BF16)
                sa = su.enter_context(ExitStack())
                mqT = sbt(sa, "mqT", [128, 8, 512], BF16)
                mVA = sbt(sa, "mVA", [128, 4, 4, 130], BF16)
                sigmo = sbt(sa, "sigmo", [128, 4, 512])
                gif = sbt(sa, "gif", [128, 4, 8])
                Qbd = sbt(sa, "Qbd", [128, 4, 4, 256], BF16)
                hmT = sbt(sa, "hmT", [128, 4, 512], BF16)
                haT = sbt(sa, "haT", [128, 4, 512], BF16)
                b_xs = [S.buf("xs%d" % j) for j in range(4)]
                b_hT = S.buf("hT")
                b_hT2 = S.buf("hT2")
                b_mqT = S.buf("mqT")
                b_mVA = S.buf("mVA")
                b_sig = S.buf("sigmo")
                b_gif = S.buf("gif")
                b_Qbd = S.buf("Qbd")
                b_hmT = S.buf("hmT")
                b_haT = S.buf("haT")
                if full:
                    S.op("dve", lambda e: e.memset(Qbd[:], 0.0), writes=[b_Qbd])

                def norm_block(st, j, mult, shift, hdst, b_hdst, tagp):
                    junk = njunk
                    if not hasattr(st, "ncache"):
                        st.ncache = {}
                    ck = (tagp, j % 2)
                    if ck not in st.ncache:
                        st.ncache[ck] = (sbt(st, tagp + "xn%d" % (j % 2), [128, D], BF16), S.buf("xn"))
                    xn, b_xn = st.ncache[ck]
                    ss = sbt(st, tagp + "ss%d" % j, [128, 2])
                    bl = S.buf("nb")
                    S.op("act", lambda e: e.activation(out=junk[:], in_=xs[:, j, :], func=AF.Square, scale=1.0 / 32.0, accum_out=ss[:, 0:1]),
                         reads=[b_xs[j]], writes=[bl])
                    S.op("act", lambda e: e.activation(out=ss[:, 1:2], in_=ss[:, 0:1], func=AF.Ln, bias=epst[:, 0:1], scale=1.0), reads=[bl, CONST], writes=[bl])
                    S.op("act", lambda e: e.activation(out=ss[:, 1:2], in_=ss[:, 1:2], func=AF.Exp, scale=-0.5), reads=[bl], writes=[bl])
                    S.op("dve", lambda e: e.tensor_scalar(out=xn[:], in0=xs[:, j, :], scalar1=ss[:, 1:2], scalar2=None, op0=ALU.mult),
                         reads=[bl, b_xs[j]], writes=[b_xn])
                    for half in range(2):
                        pb, pbb = nbank()
                        pv = bfview(pb)[:, 0:512].rearrange("p (c t) -> p c t", c=4)
                        for cc in range(4):
                            c = half * 4 + cc
                            S.op("pe", lambda e: e.transpose(out=pv[:, cc, :], in_=xn[:, c * 128:(c + 1) * 128], identity=identb[:]),
                                 reads=[b_xn, cb[0]], writes=[pbb])
                        for cc in range(4):
                            c = half * 4 + cc
                            if half == 0:
                                S.op("act", lambda e: e.activation(out=hdst[:, c, j * 128:(j + 1) * 128], in_=pv[:, cc, :], func=AF.Identity,
                                                                  scale=mult[:, c:c + 1], bias=shift[:, c:c + 1]), reads=[pbb, CONST], writes=[b_hdst[0]])
                            else:
                                S.op("dve", lambda e: e.tensor_scalar(out=hdst[:, c, j * 128:(j + 1) * 128], in0=pv[:, cc, :], scalar1=mult[:, c:c + 1],
                                                                     scalar2=shift[:, c:c + 1], op0=ALU.mult, op1=ALU.add), reads=[pbb, CONST], writes=[b_hdst[1]])

                with ExitStack() as st:
                    names = (["mq", "mk", "mv", "gif", "av", "mo", "aq", "ak"] if full else ["mk", "mv", "gif", "av", "ak"])
                    W = {}
                    for j in range(4):
                        blk = u * 4 + j
                        S.dma("sp", xs[:, j, :], x_d[blk * 128:(blk + 1) * 128, :], writes=[b_xs[j]])
                    for nm in names:
                        W[nm] = wload(st, "w_" + nm, nm)
                    if not full and u + 1 < NU:
                        nxt_full = (u + 1) >= NCTX
                        slots = (2, 3) if (nxt_full or (u + 1) % 2 == 1) else (0, 1)
                        if nxt_full:
                            slots = (2, 3)
                        prefetch(slots, ["mq", "mk"] if nxt_full else ["mk", "mv"])
                    for j in range(4):
                        norm_block(st, j, mult1, shift1, hT, (b_hT, b_hT2), "n1")
                    if flagged:
                        S.op("dve", lambda e: e.tensor_copy(out=mVA[:, :, :, 128:130], in_=flag[:, 0:1].unsqueeze(1).unsqueeze(1).to_broadcast([128, 4, 4, 2])),
                             reads=[cb[4]], writes=[b_mVA])
                    else:
                        S.op("dve", lambda e: e.memset(mVA[:, :, :, 128:130], 1.0), writes=[b_mVA])
                    acc = [sbt(st, "acc%d" % i, [128, 512]) for i in range(4)]
                    et = [sbt(st, "et%d" % i, [128, 512]) for i in range(4)]
                    b_acc = [S.buf("acc%d" % i) for i in range(4)]
                    b_et = [S.buf("et%d" % i) for i in range(4)]
                    hv = mhalo
                    if not OPT_BATCH:
                        S.op("dve", lambda e: e.memset(mbc[:], 0.0), writes=[b_mbc])
                    if OPT_BATCH:
                        S.op("dve", lambda e: e.tensor_tensor(out=mbc[:, :, 2], in0=hv[:, :, 2], in1=mcw[:, :, 0], op=ALU.mult), reads=[b_mhalo] + cb, writes=[b_mbc])
                        S.op("dve", lambda e: e.tensor_tensor(out=mbc[:, :, 1], in0=hv[:, :, 2], in1=mcw[:, :, 1], op=ALU.mult), reads=[b_mhalo] + cb, writes=[b_mbc])
                        S.op("dve", lambda e: e.tensor_tensor(out=mbc[:, :, 0], in0=hv[:, :, 2], in1=mcw[:, :, 2], op=ALU.mult), reads=[b_mhalo] + cb, writes=[b_mbc])
                        S.op("dve", lambda e: e.tensor_tensor(out=ctmp[:, 0:8], in0=hv[:, :, 1], in1=mcw[:, :, 0], op=ALU.mult), reads=[b_mhalo] + cb, writes=[b_mbc])
                        S.op("dve", lambda e: e.tensor_tensor(out=mbc[:, :, 1], in0=mbc[:, :, 1], in1=ctmp[:, 0:8], op=ALU.add), reads=[b_mbc], writes=[b_mbc])
                        S.op("dve", lambda e: e.tensor_tensor(out=ctmp[:, 8:16], in0=hv[:, :, 1], in1=mcw[:, :, 1], op=ALU.mult), reads=[b_mhalo] + cb, writes=[b_mbc])
                        S.op("dve", lambda e: e.tensor_tensor(out=mbc[:, :, 0], in0=mbc[:, :, 0], in1=ctmp[:, 8:16], op=ALU.add), reads=[b_mbc], writes=[b_mbc])
                        S.op("dve", lambda e: e.tensor_tensor(out=ctmp[:, 16:24], in0=hv[:, :, 0], in1=mcw[:, :, 0], op=ALU.mult), reads=[b_mhalo] + cb, writes=[b_mbc])
                        S.op("dve", lambda e: e.tensor_tensor(out=mbc[:, :, 0], in0=mbc[:, :, 0], in1=ctmp[:, 16:24], op=ALU.add), reads=[b_mbc], writes=[b_mbc])
                        S.op("dve", lambda e: e.tensor_tensor(out=mbc[:], in0=mbc[:], in1=mcb[:].unsqueeze(2).to_broadcast([128, 8, 3]), op=ALU.add), reads=[b_mbc] + cb, writes=[b_mbc])
                    groups = ([[0, 1, 2, 3], [4, 5, 6, 7]] if full else [[4, 5, 6, 7]])
                    wsel = mcwf if flagged else mcw
                    for grp in groups:
                        pbs = []
                        for k, c in enumerate(grp):
                            wt, wbf = W["mq" if c < 4 else "mk"]
                            wv = wt[:].rearrange("p (c n) -> p c n", c=8)
                            pb, pbb = nbank()
                            pbs.append((pb, pbb))
                            for dc in range(8):
                                S.op("pe", lambda e: e.matmul(pb[:], lhsT=wv[:, dc, k * 128:(k + 1) * 128], rhs=hT[:, dc, :], start=(dc == 0), stop=(dc == 7)),
                                     reads=[wbf, b_hT, b_hT2], writes=[pbb])
                            m0 = 3 if OPT_PARTMAIN else 0
                            S.op("act", lambda e: e.activation(out=acc[k][:, m0:512], in_=pb[:, m0:512], func=AF.Identity, scale=wsel[:, c, 3:4], bias=mcb[:, c:c + 1]),
                                 reads=[pbb, CONST] + cb, writes=[b_acc[k]])
                            for col in (range(3) if OPT_TINY else ()):
                                S.op("act", lambda e: e.activation(out=acc[k][:, col:col + 1], in_=pb[:, col:col + 1], func=AF.Identity, scale=wsel[:, c, 3:4],
                                                                  bias=mbc[:, c, col:col + 1]), reads=[pbb, CONST, b_mbc] + cb, writes=[b_acc[k]])
                            if flagged:
                                S.op("act", lambda e: e.activation(out=mhalo[:, c, :], in_=pb[:, 509:512], func=AF.Copy, scale=flag[:, 0:1]), reads=[pbb, cb[4], b_mbc], writes=[b_mhalo])
                            else:
                                S.op("act", lambda e: e.copy(out=mhalo[:, c, :], in_=pb[:, 509:512]), reads=[pbb, b_mbc], writes=[b_mhalo])
                        for tp in (2, 1, 0):
                            sh = 3 - tp
                            for k, c in enumerate(grp):
                                pb, pbb = pbs[k]
                                S.op("dve", lambda e: e.scalar_tensor_tensor(out=acc[k][:, sh:512], in0=pb[:, 0:512 - sh], scalar=wsel[:, c, tp:tp + 1], in1=acc[k][:, sh:512],
                                                                            op0=ALU.mult, op1=ALU.add), reads=[pbb, b_acc[k], CONST], writes=[b_acc[k]])
                        for k, c in enumerate(grp):
                            if not OPT_SILU:
                                S.op("act", lambda e: e.activation(out=et[k][:], in_=acc[k][:], func=AF.Exp, scale=-1.0), reads=[b_acc[k]], writes=[b_et[k]])
                                sk = (128.0 ** 0.5) if c < 4 else 1.0
                                S.op("dve", lambda e: e.tensor_scalar(out=et[k][:], in0=et[k][:], scalar1=1.0, scalar2=sk, op0=ALU.add, op1=ALU.mult), reads=[b_et[k]], writes=[b_et[k]])
                                S.op("dve", lambda e: e.reciprocal(out=et[k][:], in_=et[k][:]), reads=[b_et[k]], writes=[b_et[k]])
                                S.op("dve", lambda e: e.tensor_tensor(out=mqT[:, c, :], in0=acc[k][:], in1=et[k][:], op=ALU.mult), reads=[b_acc[k], b_et[k]], writes=[b_mqT])
                            elif c < 4:
                                S.op("act", lambda e: e.activation(out=et[k][:], in_=acc[k][:], func=AF.Silu), reads=[b_acc[k]], writes=[b_et[k]])
                                S.op("dve", lambda e: e.tensor_scalar(out=mqT[:, c, :], in0=et[k][:], scalar1=128.0 ** -0.5, scalar2=None, op0=ALU.mult),
                                     reads=[b_et[k]], writes=[b_mqT])
                            else:
                                S.op("act", lambda e: e.activation(out=mqT[:, c, :], in_=acc[k][:], func=AF.Silu), reads=[b_acc[k]], writes=[b_mqT])
                    sq = [sbt(st, "sq%d" % i, [128, 512]) for i in range(2)]
                    qb = [sbt(st, "qb%d" % i, [128, 512], BF16) for i in range(2)]
                    rs = [sbt(st, "rs%d" % i, [128, 16]) for i in range(2)]
                    KTb = [sbt(st, "KTb%d" % i, [128, 4, 128], BF16) for i in range(2)]
                    VAb = [sbt(st, "VAb%d" % i, [128, 4, 130], BF16) for i in range(2)]
                    b_t = [S.buf("tmj%d" % i) for i in range(2)]
                    b_KTb = [S.buf("KTb%d" % i) for i in range(2)]
                    b_VAb = [S.buf("VAb%d" % i) for i in range(2)]
                    for i in range(2):
                        if flagged:
                            S.op("dve", lambda e: e.tensor_copy(out=VAb[i][:, :, 128:130], in_=flag[:, 0:1].unsqueeze(1).to_broadcast([128, 4, 2])),
                                 reads=[cb[4]], writes=[b_VAb[i]])
                        else:
                            S.op("dve", lambda e: e.memset(VAb[i][:, :, 128:130], 1.0), writes=[b_VAb[i]])
                    for j in range(4):
                        blk = u * 4 + j
                        i = j % 2
                        tsl = slice(j * 128, (j + 1) * 128)

                        def proj(nm):
                            wt, wbf = W[nm]
                            n = PIECES[nm][1] // 8
                            wv = wt[:].rearrange("p (c n) -> p c n", c=8)
                            pb, pbb = nbank()
                            for dc in range(8):
                                S.op("pe", lambda e: e.matmul(pb[:, 0:n], lhsT=hT[:, dc, tsl], rhs=wv[:, dc, :], start=(dc == 0), stop=(dc == 7)),
                                     reads=[wbf, b_hT, b_hT2], writes=[pbb])
                            return pb, pbb

                        def evac_scaled(out_ap, in_ap, rd, wr):
                            if flagged:
                                S.op("act", lambda e: e.activation(out=out_ap, in_=in_ap, func=AF.Copy, scale=flag[:, 0:1]), reads=rd + [cb[4]], writes=wr)
                            else:
                                S.op("act", lambda e: e.copy(out=out_ap, in_=in_ap), reads=rd, writes=wr)

                        pb, pbb = proj("mv")
                        evac_scaled(mVA[:, j, :, 0:128], pb[:].rearrange("p (h d) -> p h d", h=4), [pbb], [b_mVA])
                        pb, pbb = proj("gif")
                        S.op("dve", lambda e: e.tensor_tensor(out=gif[:, j, :], in0=pb[:, 0:8], in1=gifb[:], op=ALU.add), reads=[pbb, cb[9]], writes=[b_gif])
                        pb, pbb = proj("av")
                        evac_scaled(VAb[i][:, :, 0:128], pb[:].rearrange("p (h d) -> p h d", h=4), [pbb], [b_VAb[i]])
                        S.dma("pool", va_d.rearrange("h p (b c) -> p h b c", c=130)[:, :, blk, :], VAb[i][:], reads=[b_VAb[i]], writes=[b_vad])
                        if full:
                            pb, pbb = proj("mo")
                            S.op("act", lambda e: e.activation(out=sigmo[:, j, :], in_=pb[:], func=AF.Exp, scale=-1.0), reads=[pbb], writes=[b_sig])
                            S.op("dve", lambda e: e.tensor_scalar(out=sigmo[:, j, :], in0=sigmo[:, j, :], scalar1=1.0, scalar2=None, op0=ALU.add), reads=[b_sig], writes=[b_sig])
                            S.op("dve", lambda e: e.reciprocal(out=sigmo[:, j, :], in_=sigmo[:, j, :]), reads=[b_sig], writes=[b_sig])
                        for nm in (["aq", "ak"] if full else ["ak"]):
                            pb, pbb = proj(nm)
                            S.op("act", lambda e: e.activation(out=sq[i][:], in_=pb[:], func=AF.Square, scale=0.125), reads=[pbb], writes=[b_t[i]])
                            S.op("dve", lambda e: e.tensor_reduce(out=rs[i][:, 0:8], in_=sq[i][:].rearrange("p (g d) -> p g d", d=64), axis=AX.X, op=ALU.add),
                                 reads=[b_t[i]], writes=[b_t[i]])
                            S.op("act", lambda e: e.activation(out=rs[i][:, 8:16], in_=rs[i][:, 0:8], func=AF.Ln, bias=epst[:, 0:1], scale=1.0), reads=[b_t[i], CONST], writes=[b_t[i]])
                            S.op("act", lambda e: e.activation(out=rs[i][:, 8:16], in_=rs[i][:, 8:16], func=AF.Exp, scale=-0.5), reads=[b_t[i]], writes=[b_t[i]])
                            S.op("dve", lambda e: e.tensor_tensor(out=qb[i][:].rearrange("p (g d) -> p g d", d=64), in0=pb[:].rearrange("p (g d) -> p g d", d=64),
                                                                 in1=rs[i][:, 8:16].unsqueeze(2).to_broadcast([128, 8, 64]), op=ALU.mult), reads=[pbb, b_t[i]], writes=[b_t[i]])
                            tb, tbb = nbank()
                            tv = bfview(tb)[:, 0:512].rearrange("p (h t) -> p h t", h=4)
                            for h in range(4):
                                S.op("pe", lambda e: e.transpose(out=tv[:, h, :], in_=qb[i][:, h * 128:(h + 1) * 128], identity=identb[:]), reads=[b_t[i], cb[0]], writes=[tbb])
                            if nm == "aq":
                                if OPT_GFOLD:
                                    S.op("act", lambda e: e.activation(out=Qbd[0:64, :, j, 0:128], in_=tv[0:64, :, :], func=AF.Copy, scale=gqfm[0:64, 0:1]), reads=[tbb, CONST] + cb, writes=[b_Qbd])
                                    S.op("act", lambda e: e.activation(out=Qbd[64:128, :, j, 128:256], in_=tv[64:128, :, :], func=AF.Copy, scale=gqfm[64:128, 0:1]), reads=[tbb, CONST] + cb, writes=[b_Qbd])
                                else:
                                    S.op("act", lambda e: e.copy(out=Qbd[0:64, :, j, 0:128], in_=tv[0:64, :, :]), reads=[tbb, CONST] + cb, writes=[b_Qbd])
                                    S.op("act", lambda e: e.copy(out=Qbd[64:128, :, j, 128:256], in_=tv[64:128, :, :]), reads=[tbb, CONST] + cb, writes=[b_Qbd])
                            else:
                                if OPT_GFOLD:
                                    S.op("act", lambda e: e.activation(out=KTb[i][:], in_=tv, func=AF.Copy, scale=gkfm[:, 0:1]), reads=[tbb] + cb, writes=[b_KTb[i]])
                                else:
                                    S.op("act", lambda e: e.copy(out=KTb[i][:], in_=tv), reads=[tbb] + cb, writes=[b_KTb[i]])
                                S.dma("pool", kt_d.rearrange("h p k -> p h k")[:, :, blk * 128:(blk + 1) * 128], KTb[i][:], reads=[b_KTb[i]], writes=[b_ktd])
                    S.barrier()

                def mlstm_block(j, st, bankfn, cache):
                    def sbt_c(st_, name, shape, dt=F32):
                        if name not in cache:
                            cache[name] = sbt(st_, name, shape, dt)
                        return cache[name]

                    def buf_c(name):
                        k_ = "B_" + name
                        if k_ not in cache:
                            cache[k_] = S.buf(name)
                        return cache[k_]
                    tsl = slice(j * 128, (j + 1) * 128)
                    lf = sbt_c(st, "lf%d" % (j % 2), [128, 4])
                    e1 = sbt_c(st, "e1%d" % (j % 2), [128, 4])
                    e2 = sbt_c(st, "e2%d" % (j % 2), [128, 4])
                    wsx = sbt_c(st, "wsx%d" % (j % 2), [128, 4])
                    RL = sbt_c(st, "RL%d" % (j % 2), [128, 4, 128])
                    Et = sbt_c(st, "Et%d" % (j % 2), [128, 512])
                    Kw = sbt_c(st, "Kw%d" % (j % 2), [128, 4, 128], BF16)
                    bm = buf_c("ml%d" % (j % 2))
                    b_e1 = buf_c("e1%d" % (j % 2))
                    b_ws = buf_c("wsx%d" % (j % 2))
                    b_RL = buf_c("RL%d" % (j % 2))
                    b_Et = buf_c("Et%d" % (j % 2))
                    b_Kw = buf_c("Kw%d" % (j % 2))
                    S.op("act", lambda e: e.activation(out=lf[:], in_=gif[:, j, 4:8], func=AF.Exp, scale=-1.0), reads=[b_gif], writes=[bm])
                    yield
                    S.op("act", lambda e: e.activation(out=lf[:], in_=lf[:], func=AF.Ln, bias=onet[:, 0:1], scale=1.0), reads=[bm, CONST], writes=[bm])
                    yield
                    S.op("dve", lambda e: e.tensor_scalar(out=lf[:], in0=lf[:], scalar1=-1.0, scalar2=None, op0=ALU.mult), reads=[bm], writes=[bm])
                    yield
                    p1, p1b = bankfn()
                    S.op("pe", lambda e: e.matmul(p1[:, 0:4], lhsT=umask[:], rhs=lf[:], start=True, stop=True), reads=[bm, cb[2]], writes=[p1b])
                    S.op("dve", lambda e: e.tensor_tensor(out=RL[:], in0=umask[:].unsqueeze(1).to_broadcast([128, 4, 128]),
                                                         in1=lf[:].unsqueeze(2).to_broadcast([128, 4, 128]), op=ALU.mult), reads=[bm, cb[2]], writes=[b_RL])
                    yield
                    S.op("dve", lambda e: e.tensor_tensor(out=e1[:], in0=gif[:, j, 0:4], in1=p1[:, 0:4], op=ALU.subtract), reads=[b_gif, p1b], writes=[b_e1])
                    yield
                    RLf = RL[:].rearrange("p h t -> p (h t)")
                    pBc, pBcb = bankfn()
                    S.op("pe", lambda e: e.matmul(pBc[:], lhsT=ones_f[:], rhs=RLf, start=True, stop=True), reads=[b_RL, CONST], writes=[pBcb])
                    yield
                    blast = pBc[:].rearrange("p (h t) -> p h t", h=4)[:, :, 127]
                    S.op("dve", lambda e: e.tensor_tensor(out=e2[:], in0=e1[:], in1=blast, op=ALU.add), reads=[b_e1, pBcb], writes=[b_ws])
                    yield
                    S.op("act", lambda e: e.activation(out=Et[:], in_=pBc[:], func=AF.Exp), reads=[pBcb], writes=[b_Et])
                    S.op("act", lambda e: e.activation(out=wsx[:], in_=e2[:], func=AF.Exp), reads=[b_ws], writes=[b_ws])
                    yield
                    tb, tbb = bankfn()
                    tv = bfview(tb)[:, 0:512].rearrange("p (h t) -> p h t", h=4)
                    for h in range(4):
                        S.op("pe", lambda e: e.transpose(out=tv[:, h, :], in_=mqT[:, 4 + h, tsl], identity=identb[:]), reads=[b_mqT, cb[0]], writes=[tbb])
                    yield
                    S.op("dve", lambda e: e.tensor_tensor(out=Kw[:], in0=tv, in1=wsx[:].unsqueeze(2).to_broadcast([128, 4, 128]), op=ALU.mult),
                         reads=[tbb, b_ws], writes=[b_Kw])
                    yield
                    if full:
                        DT = sbt_c(st, "DT%d" % (j % 2), [128, 4, 128])
                        PT = sbt_c(st, "PT%d" % (j % 2), [128, 4, 128], BF16)
                        qpT = sbt_c(st, "qpT%d" % (j % 2), [128, 4, 128], BF16)
                        numS = sbt_c(st, "numS%d" % (j % 2), [128, 4, 130])
                        rden = sbt_c(st, "rden%d" % (j % 2), [128, 4])
                        hmr = sbt_c(st, "hmr%d" % (j % 2), [128, 4, 128])
                        hsq = sbt_c(st, "hsq%d" % (j % 2), [128, 4, 128])
                        hss = sbt_c(st, "hss%d" % (j % 2), [128, 8])
                        gs = sbt_c(st, "gs%d" % (j % 2), [128, 512])
                        hmf = sbt_c(st, "hmf%d" % (j % 2), [128, 4, 128], BF16)
                        b_o = buf_c("mo%d" % (j % 2))
                        b_gs = buf_c("gs%d" % (j % 2))
                        b_num = buf_c("numS%d" % (j % 2))
                        b_DT = buf_c("DT%d" % (j % 2))
                        b_PT = buf_c("PT%d" % (j % 2))
                        b_qp = buf_c("qpT%d" % (j % 2))
                        pBm, pBmb = bankfn()
                        S.op("pe", lambda e: e.matmul(pBm[:], lhsT=ones_f[:], rhs=RLf, start=True, stop=False), reads=[b_RL, CONST], writes=[pBmb])
                        S.op("pe", lambda e: e.matmul(pBm[:], lhsT=identf[:], rhs=negm4[:], start=False, stop=True), reads=[cb[1], cb[3]], writes=[pBmb])
                        S.op("dve", lambda e: e.tensor_tensor(out=qpT[:], in0=mqT[:, 0:4, tsl], in1=Et[:].rearrange("p (h t) -> p h t", h=4), op=ALU.mult),
                             reads=[b_mqT, b_Et], writes=[b_qp])
                        S.op("dve", lambda e: e.tensor_tensor(out=gs[:], in0=mng[:], in1=sigmo[:, j, :], op=ALU.mult), reads=[b_sig] + cb, writes=[b_gs])
                        yield
                        for h in range(4):
                            S.op("act", lambda e: e.activation(out=DT[:, h, :], in_=pBm[:, h * 128:(h + 1) * 128], func=AF.Exp, bias=e1[:, h:h + 1], scale=1.0),
                                 reads=[pBmb, b_e1], writes=[b_DT])
                        yield
                        pA, pAb = bankfn()
                        for h in range(4):
                            S.op("pe", lambda e: e.matmul(pA[:, h * 128:(h + 1) * 128], lhsT=mqT[:, 4 + h, tsl], rhs=mqT[:, h, tsl], start=True, stop=True),
                                 reads=[b_mqT], writes=[pAb])
                        yield
                        S.op("dve", lambda e: e.tensor_tensor(out=PT[:], in0=pA[:].rearrange("p (h t) -> p h t", h=4), in1=DT[:], op=ALU.mult),
                             reads=[pAb, b_DT], writes=[b_PT])
                        yield
                    yield "AB"
                    if full:
                        for hp in range(2):
                            pn, pnb = bankfn()
                            for hh in range(2):
                                h = 2 * hp + hh
                                o = hh * 130
                                S.op("pe", lambda e: e.matmul(pn[:, o:o + 130], lhsT=PT[:, h, :], rhs=mVA[:, j, h, :], start=True, stop=False), reads=[b_PT, b_mVA], writes=[pnb])
                                S.op("pe", lambda e: e.matmul(pn[:, o:o + 130], lhsT=qpT[:, h, :], rhs=Sstb[:, h, :], start=False, stop=True), reads=[b_qp, b_Sb], writes=[pnb])
                            yield
                            S.op("dve", lambda e: e.tensor_copy(out=numS[:, 2 * hp:2 * hp + 2, :], in_=pn[:, 0:260].rearrange("p (h c) -> p h c", h=2)), reads=[pnb], writes=[b_num])
                            yield
                    Ev = Et[:].rearrange("p (h t) -> p h t", h=4)
                    for hp in range(2):
                        pc_, pcb = bankfn()
                        for hh in range(2):
                            h = 2 * hp + hh
                            o = hh * 130
                            S.op("pe", lambda e: e.matmul(pc_[:, o:o + 130], lhsT=Kw[:, h, :], rhs=mVA[:, j, h, :], start=True, stop=True), reads=[b_Kw, b_mVA], writes=[pcb])
                        yield
                        for hh in range(2):
                            h = 2 * hp + hh
                            o = hh * 130
                            S.op("dve", lambda e: e.scalar_tensor_tensor(out=Sst[:, h, :], in0=Sst[:, h, :], scalar=Ev[:, h, 127:128], in1=pc_[:, o:o + 130],
                                                                        op0=ALU.mult, op1=ALU.add), reads=[b_S, b_Et, pcb], writes=[b_S])
                        yield
                    S.op("dve", lambda e: e.tensor_copy(out=Sstb[:], in_=Sst[:]), reads=[b_S], writes=[b_Sb])
                    yield "S"
                    if full:
                        S.op("act", lambda e: e.activation(out=rden[:], in_=numS[:, :, 128], func=AF.Abs), reads=[b_num], writes=[b_o])
                        yield
                        S.op("dve", lambda e: e.tensor_scalar(out=rden[:], in0=rden[:], scalar1=1.0, scalar2=None, op0=ALU.max), reads=[b_o], writes=[b_o])
                        S.op("dve", lambda e: e.reciprocal(out=rden[:], in_=rden[:]), reads=[b_o], writes=[b_o])
                        S.op("dve", lambda e: e.tensor_tensor(out=hmr[:], in0=numS[:, :, 0:128], in1=rden[:].unsqueeze(2).to_broadcast([128, 4, 128]), op=ALU.mult),
                             reads=[b_num, b_o], writes=[b_o])
                        yield
                        S.op("act", lambda e: e.activation(out=hsq[:], in_=hmr[:], func=AF.Square, scale=1.0 / math.sqrt(128.0)), reads=[b_o], writes=[b_o])
                        yield
                        S.op("dve", lambda e: e.tensor_reduce(out=hss[:, 0:4], in_=hsq[:], axis=AX.X, op=ALU.add), reads=[b_o], writes=[b_o])
                        yield
                        S.op("act", lambda e: e.activation(out=hss[:, 4:8], in_=hss[:, 0:4], func=AF.Ln, bias=epst[:, 0:1], scale=1.0), reads=[b_o, CONST], writes=[b_o])
                        S.op("act", lambda e: e.activation(out=hss[:, 4:8], in_=hss[:, 4:8], func=AF.Exp, scale=-0.5), reads=[b_o], writes=[b_o])
                        yield
                        for h in range(4):
                            S.op("dve", lambda e: e.scalar_tensor_tensor(out=hmf[:, h, :], in0=hmr[:, h, :], scalar=hss[:, 4 + h:5 + h], in1=gs[:, h * 128:(h + 1) * 128],
                                                                        op0=ALU.mult, op1=ALU.mult), reads=[b_o, b_gs], writes=[b_o])
                        yield
                        tb2, tb2b = bankfn()
                        tv2 = bfview(tb2)[:, 0:512].rearrange("p (h t) -> p h t", h=4)
                        for h in range(4):
                            S.op("pe", lambda e: e.transpose(out=tv2[:, h, :], in_=hmf[:, h, :], identity=identb[:]), reads=[b_o, cb[0]], writes=[tb2b])
                        yield
                        S.op("dve", lambda e: e.tensor_copy(out=hmT[:, :, tsl], in_=tv2), reads=[tb2b], writes=[b_hmT])
                        yield

                def mlstm_driver(st, bankfn):
                    cache = {}
                    for j in range(4):
                        for r in mlstm_block(j, st, bankfn, cache):
                            yield

                if not full:
                    with ExitStack() as st:
                        for _ in mlstm_driver(st, nbank):
                            pass
                        S.barrier()

                if full:
                    with ExitStack() as st:
                        NKC = 8
                        kch = [sbt(st, "kch%d" % i, [128, NKC * 128], BF16) for i in range(3)]
                        vch = [sbt(st, "vch%d" % i, [128, NKC, 130], BF16) for i in range(3)]
                        b_kch = [S.buf("kch%d" % i) for i in range(3)]
                        b_vch = [S.buf("vch%d" % i) for i in range(3)]
                        pex = [sbt(st, "pex%d" % i, [128, 512], BF16) for i in range(6)]
                        b_pex = [S.buf("pex%d" % i) for i in range(6)]
                        rr = sbt(st, "rr", [128, 8])
                        har = sbt(st, "har", [128, 4, 128])
                        hsq = sbt(st, "ahsq", [128, 4, 128])
                        hss = sbt(st, "ahss", [128, 8])
                        haf = sbt(st, "haf", [128, 4, 128], BF16)
                        b_a = S.buf("attn_o")
                        nkb_tot = u * 4 + 4
                        chunk_list = [(s0, min(NKC, nkb_tot - s0)) for s0 in range(0, nkb_tot, NKC)]
                        ring = 0
                        LCH = len(chunk_list)
                        prefetch((0, 1), ["bm", "gm0"])
                        mdrv = mlstm_driver(st, lambda: (banks[7], bbufs[7]))
                        n_items_est = 4 * sum((2 if kb_ <= u * 4 - 2 else 0) + sum(1 for p2 in range(2) for jq in (2 * p2, 2 * p2 + 1) if kb_ > u * 4 + 2 * p2 - 2 and kb_ <= u * 4 + jq)
                                              for kb_ in range(nkb_tot))
                        MSTEP = max(1, n_items_est // 150)
                        gcount = [0]
                        loaded = set()

                        def ensure_chunk(g):
                            if g >= 4 * LCH or g in loaded:
                                return
                            loaded.add(g)
                            h_, ci_ = g // LCH, g % LCH
                            s0, nk = chunk_list[ci_]
                            ri = g % 3
                            S.dma("sp", kch[ri][:, 0:nk * 128], kt_d[h_, :, s0 * 128:(s0 + nk) * 128], reads=[b_ktd], writes=[b_kch[ri]])
                            S.dma("sp", vch[ri][:, 0:nk, :], va_d[h_, :, s0 * 130:(s0 + nk) * 130].rearrange("p (b c) -> p b c", c=130), reads=[b_vad], writes=[b_vch[ri]])

                        for h in range(4):
                            accs = [(banks[jq], bbufs[jq]) for jq in range(4)]
                            for jq in range(4):
                                S.op("dve", lambda e: e.memset(accs[jq][0][:, 0:260], 0.0), writes=[accs[jq][1]])
                            items = []
                            for ci, (s0, nk) in enumerate(chunk_list):
                                g = h * LCH + ci
                                ri = g % 3
                                for kk in range(nk):
                                    kb = s0 + kk
                                    for p2 in range(2):
                                        j0 = 2 * p2
                                        if u == NCTX:
                                            if p2 == 1 and kb <= u * 4 + 3:
                                                items.append((ri, kk, kb, (3,), g))
                                            continue
                                        if kb <= u * 4 + j0 - 2:
                                            items.append((ri, kk, kb, (j0, j0 + 1), g))
                                        else:
                                            for jq in (j0, j0 + 1):
                                                if kb <= u * 4 + jq:
                                                    items.append((ri, kk, kb, (jq,), g))

                            LAG = 3
                            NS = 3
                            NPX = len(pex)

                            def emit_scores(it, idx):
                                ri, kk, kb, jqs, ci = it
                                ensure_chunk(ci)
                                ensure_chunk(ci + 1)
                                nq = len(jqs)
                                sl = idx % NS
                                sp_, spb = banks[4 + sl], bbufs[4 + sl]
                                sv = sp_[:, 0:256 * nq]
                                px = idx % NPX
                                delta = u * 4 + jqs[0] - kb
                                near = (nq == 1) and delta <= 1
                                rhs = Qbd[:, h, jqs[0]:jqs[0] + nq, :].rearrange("p j c -> p (j c)")
                                S.op("pe", lambda e: e.matmul(sv, lhsT=kch[ri][:, kk * 128:(kk + 1) * 128], rhs=rhs, start=True, stop=not near),
                                     reads=[b_kch[ri], b_Qbd], writes=[spb])
                                if near:
                                    S.op("pe", lambda e: e.matmul(sv, lhsT=identb[:], rhs=biasb[:, h, delta, :], start=False, stop=True), reads=[CONST, cb[0]], writes=[spb])
                                    S.op("act", lambda e: e.activation(out=pex[px][:, 0:256 * nq], in_=sv, func=AF.Exp), reads=[spb], writes=[b_pex[px]])
                                else:
                                    S.op("act", lambda e: e.activation(out=pex[px][:, 0:256 * nq], in_=sv, func=AF.Exp, bias=farb[:, h:h + 1], scale=1.0),
                                         reads=[spb, cb[14]], writes=[b_pex[px]])

                            def emit_pv(it, idx):
                                ri, kk, kb, jqs, ci = it
                                px = idx % NPX
                                for a_, jq in enumerate(jqs):
                                    ab, abb = accs[jq]
                                    for m in range(2):
                                        c0 = a_ * 256 + m * 128
                                        S.op("pe", lambda e: e.matmul(ab[:, m * 130:(m + 1) * 130], lhsT=pex[px][:, c0:c0 + 128], rhs=vch[ri][:, kk, :],
                                                                     start=False, stop=False, skip_group_check=True), reads=[b_pex[px], b_vch[ri]], writes=[abb])

                            for idx in range(len(items) + LAG):
                                if idx < len(items):
                                    emit_scores(items[idx], idx)
                                if idx >= LAG:
                                    emit_pv(items[idx - LAG], idx - LAG)
                                gcount[0] += 1
                                if gcount[0] % MSTEP == 0:
                                    next(mdrv, None)
                            for jq in range(4):
                                ab, abb = accs[jq]
                                av = ab[:, 0:260].rearrange("p (m c) -> p m c", m=2)
                                S.op("dve", lambda e: e.tensor_scalar(out=rr[:, 2 * jq:2 * jq + 2], in0=av[:, :, 128], scalar1=1e-30, scalar2=None, op0=ALU.max), reads=[abb], writes=[b_a])
                                S.op("dve", lambda e: e.reciprocal(out=rr[:, 2 * jq:2 * jq + 2], in_=rr[:, 2 * jq:2 * jq + 2]), reads=[b_a], writes=[b_a])
                                S.op("dve", lambda e: e.tensor_tensor(out=rr[:, 2 * jq + 1:2 * jq + 2], in0=rr[:, 2 * jq + 1:2 * jq + 2], in1=neglam[:], op=ALU.mult),
                                     reads=[b_a, CONST], writes=[b_a])
                                S.op("dve", lambda e: e.tensor_scalar(out=har[:, jq, :], in0=av[:, 0, 0:128], scalar1=rr[:, 2 * jq:2 * jq + 1], scalar2=None, op0=ALU.mult),
                                     reads=[abb, b_a], writes=[b_a])
                                S.op("dve", lambda e: e.scalar_tensor_tensor(out=har[:, jq, :], in0=av[:, 1, 0:128], scalar=rr[:, 2 * jq + 1:2 * jq + 2], in1=har[:, jq, :],
                                                                            op0=ALU.mult, op1=ALU.add), reads=[abb, b_a], writes=[b_a])
                            S.op("act", lambda e: e.activation(out=hsq[:], in_=har[:], func=AF.Square, scale=1.0 / math.sqrt(128.0)), reads=[b_a], writes=[b_a])
                            S.op("dve", lambda e: e.tensor_reduce(out=hss[:, 0:4], in_=hsq[:], axis=AX.X, op=ALU.add), reads=[b_a], writes=[b_a])
                            S.op("act", lambda e: e.activation(out=hss[:, 4:8], in_=hss[:, 0:4], func=AF.Ln, bias=epst[:, 0:1], scale=1.0), reads=[b_a, CONST], writes=[b_a])
                            S.op("act", lambda e: e.activation(out=hss[:, 4:8], in_=hss[:, 4:8], func=AF.Exp, scale=-0.5), reads=[b_a], writes=[b_a])
                            for jq in range(4):
                                S.op("dve", lambda e: e.scalar_tensor_tensor(out=haf[:, jq, :], in0=har[:, jq, :], scalar=hss[:, 4 + jq:5 + jq], in1=ang[:, h * 128:(h + 1) * 128],
                                                                            op0=ALU.mult, op1=ALU.mult), reads=[b_a, CONST] + cb, writes=[b_a])
                            tb, tbb = banks[4 + h % 3], bbufs[4 + h % 3]
                            tv = bfview(tb)[:, 0:512].rearrange("p (j t) -> p j t", j=4)
                            for jq in range(4):
                                S.op("pe", lambda e: e.transpose(out=tv[:, jq, :], in_=haf[:, jq, :], identity=identb[:]), reads=[b_a, cb[0]], writes=[tbb])
                            S.op("dve", lambda e: e.tensor_copy(out=haT[:, h, :], in_=tv.rearrange("p j t -> p (j t)")), reads=[tbb], writes=[b_haT])
                        for _ in mdrv:
                            pass
                        S.barrier()

                    h2T = hT
                    b_h2T = b_hT
                    with ExitStack() as st:
                        W = {}
                        for nm in ("bm", "gm0", "ba", "ga0", "gm1", "ga1", "out0", "out1"):
                            W[nm] = wload(st, "w_" + nm, nm)
                        prefetch((2, 3), ["up0", "up1"])
                        yT = sbt(st, "yT", [128, 8, 512], BF16)
                        b_yT = S.buf("yT")
                        sg = [[sbt(st, "sg%d_%d" % (i, a), [128, 512]) for a in range(2)] for i in range(2)]
                        ty = [[sbt(st, "ty%d_%d" % (i, a), [128, 512]) for a in range(2)] for i in range(2)]
                        b_sg = [S.buf("sg%d" % i) for i in range(2)]
                        for c in range(8):
                            i = c % 2
                            res = []
                            for a, (bw, gw, srcT, b_src) in enumerate((("bm", "gm", hmT, b_hmT), ("ba", "ga", haT, b_haT))):
                                wt, wbf = W[bw]
                                wv = wt[:].rearrange("p (k n) -> p k n", k=4)
                                py, pyb = nbank()
                                for k in range(4):
                                    S.op("pe", lambda e: e.matmul(py[:], lhsT=wv[:, k, c * 128:(c + 1) * 128], rhs=srcT[:, k, :], start=(k == 0), stop=(k == 3)),
                                         reads=[wbf, b_src], writes=[pyb])
                                gt_, gbf = W[gw + str(c // 4)]
                                gv = gt_[:].rearrange("p (c n) -> p c n", c=8)
                                pg, pgb = nbank()
                                for dc in range(8):
                                    S.op("pe", lambda e: e.matmul(pg[:], lhsT=gv[:, dc, (c % 4) * 128:(c % 4 + 1) * 128], rhs=hT[:, dc, :], start=(dc == 0), stop=(dc == 7)),
                                         reads=[gbf, b_hT, b_hT2], writes=[pgb])
                                S.op("act", lambda e: e.activation(out=sg[i][a][:], in_=pg[:], func=AF.Sigmoid), reads=[pgb], writes=[b_sg[i]])
                                S.op("dve", lambda e: e.tensor_tensor(out=ty[i][a][:], in0=py[:], in1=sg[i][a][:], op=ALU.mult), reads=[pyb, b_sg[i]], writes=[b_sg[i]])
                            S.op("dve", lambda e: e.tensor_tensor(out=yT[:, c, :], in0=ty[i][0][:], in1=ty[i][1][:], op=ALU.add), reads=[b_sg[i]], writes=[b_yT])
                        tx = [sbt(st, "tx%d" % i, [128, 512]) for i in range(2)]
                        b_tx = [S.buf("tx%d" % i) for i in range(2)]
                        for j in range(4):
                            tsl = slice(j * 128, (j + 1) * 128)
                            for n in range(2):
                                wt, wbf = W["out%d" % n]
                                wv = wt[:].rearrange("p (c n) -> p c n", c=8)
                                po, pob = nbank()
                                for c in range(8):
                                    S.op("pe", lambda e: e.matmul(po[:], lhsT=yT[:, c, tsl], rhs=wv[:, c, :], start=(c == 0), stop=(c == 7)), reads=[wbf, b_yT], writes=[pob])
                                i = (j * 2 + n) % 2
                                S.op("dve", lambda e: e.tensor_tensor(out=tx[i][:], in0=po[:], in1=gate1[:, n * 512:(n + 1) * 512], op=ALU.mult), reads=[pob, CONST], writes=[b_tx[i]])
                                S.op("dve", lambda e: e.tensor_tensor(out=xs[:, j, n * 512:(n + 1) * 512], in0=xs[:, j, n * 512:(n + 1) * 512], in1=tx[i][:], op=ALU.add),
                                     reads=[b_tx[i], b_xs[j]], writes=[b_xs[j]])
                        for j in range(4):
                            norm_block(st, j, mult2, shift2, h2T, (b_hT, b_hT2), "n2")
                        S.barrier()

                    sa.close()
                    with ExitStack() as st:
                        actT = sbt(st, "actT", [128, NF, 512], BF16)
                        b_actT = S.buf("actT")
                        wu = [None, None, None]
                        uu = [sbt(st, "fu%d" % i, [128, 512]) for i in range(8)]
                        b_uu = [S.buf("fu%d" % i) for i in range(8)]
                        fe = [sbt(st, "fe%d" % i, [128, 512]) for i in range(2)]
                        b_fe = [S.buf("fe%d" % i) for i in range(2)]
                        wus = [pf[2], pf[3], sbt(st, "wup2", [128, 4096], BF16)]
                        b_wus = [b_pf[2], b_pf[3], S.buf("wup2")]

                        def ldup(jj):
                            if ("up%d" % jj) in prefetched:
                                prefetched.pop("up%d" % jj)
                                return
                            off, n = PIECES["up%d" % jj]
                            S.dma("sp", wus[jj % 3][:], wb_d[:, off:off + n], reads=[b_wb], writes=[b_wus[jj % 3]])

                        ldup(0)
                        ldup(1)
                        wd = sbt(st, "w_down", [128, NF * 1024], BF16)
                        wdb = S.buf("w_down")
                        wdv = wd[:].rearrange("p (f n) -> p f n", f=NF)
                        if not OPT_BATCH:
                            S.op("dve", lambda e: e.memset(fbc[:], 0.0), writes=[b_fbc])
                        if OPT_BATCH:
                            S.op("dve", lambda e: e.tensor_tensor(out=fbc[:, :, 1], in0=fhalo[:, :, 1], in1=fcw[:, :, 0], op=ALU.mult), reads=[b_fhalo] + cb, writes=[b_fbc])
                            S.op("dve", lambda e: e.tensor_tensor(out=fbc[:, :, 0], in0=fhalo[:, :, 1], in1=fcw[:, :, 1], op=ALU.mult), reads=[b_fhalo] + cb, writes=[b_fbc])
                            S.op("dve", lambda e: e.tensor_tensor(out=ctmp[:], in0=fhalo[:, :, 0], in1=fcw[:, :, 0], op=ALU.mult), reads=[b_fhalo] + cb, writes=[b_fbc])
                            S.op("dve", lambda e: e.tensor_tensor(out=fbc[:, :, 0], in0=fbc[:, :, 0], in1=ctmp[:], op=ALU.add), reads=[b_fbc], writes=[b_fbc])
                            S.op("dve", lambda e: e.tensor_tensor(out=fbc[:], in0=fbc[:], in1=fcb[:].unsqueeze(2).to_broadcast([128, 44, 2]), op=ALU.add), reads=[b_fbc] + cb, writes=[b_fbc])

                        def ffn_piece(jj):
                            if jj + 2 < 11:
                                ldup(jj + 2)
                            if jj == 2:
                                off_d, n_d = PIECES["down"]
                                S.dma("sp", wd[:], wb_d[:, off_d:off_d + n_d], reads=[b_wb], writes=[wdb])
                            if jj == 4 and u + 1 < NU:
                                prefetch((0, 1), ["mq", "mk"])
                            wv = wus[jj % 3][:].rearrange("p (c n) -> p c n", c=8)
                            wbf = b_wus[jj % 3]
                            wsel = fcwf if flagged else fcw
                            pbs = []
                            for k in range(4):
                                q = jj * 4 + k
                                pb, pbb = nbank()
                                pbs.append((pb, pbb))
                                for dc in range(8):
                                    S.op("pe", lambda e: e.matmul(pb[:], lhsT=wv[:, dc, k * 128:(k + 1) * 128], rhs=h2T[:, dc, :], start=(dc == 0), stop=(dc == 7)),
                                         reads=[wbf, b_hT, b_hT2], writes=[pbb])
                                kk_ = (jj % 2) * 4 + k
                                m0 = 2 if OPT_PARTMAIN else 0
                                S.op("act", lambda e: e.activation(out=uu[kk_][:, m0:512], in_=pb[:, m0:512], func=AF.Identity, scale=wsel[:, q, 2:3], bias=fcb[:, q:q + 1]),
                                     reads=[pbb, CONST] + cb, writes=[b_uu[kk_]])
                                for col in (range(2) if OPT_TINY else ()):
                                    S.op("act", lambda e: e.activation(out=uu[kk_][:, col:col + 1], in_=pb[:, col:col + 1], func=AF.Identity, scale=wsel[:, q, 2:3],
                                                                      bias=fbc[:, q, col:col + 1]), reads=[pbb, CONST, b_fbc] + cb, writes=[b_uu[kk_]])
                                if flagged:
                                    S.op("act", lambda e: e.activation(out=fhalo[:, q, :], in_=pb[:, 510:512], func=AF.Copy, scale=flag[:, 0:1]), reads=[pbb, cb[4], b_fbc], writes=[b_fhalo])
                                else:
                                    S.op("act", lambda e: e.copy(out=fhalo[:, q, :], in_=pb[:, 510:512]), reads=[pbb, b_fbc], writes=[b_fhalo])
                            for tp in (1, 0):
                                sh = 2 - tp
                                for k in range(4):
                                    q = jj * 4 + k
                                    kk_ = (jj % 2) * 4 + k
                                    pb, pbb = pbs[k]
                                    S.op("dve", lambda e: e.scalar_tensor_tensor(out=uu[kk_][:, sh:512], in0=pb[:, 0:512 - sh], scalar=wsel[:, q, tp:tp + 1], in1=uu[kk_][:, sh:512],
                                                                                op0=ALU.mult, op1=ALU.add), reads=[pbb, b_uu[kk_], CONST], writes=[b_uu[kk_]])
                            yield
                            for i in range(2):
                                f = 2 * jj + i
                                uv = uu[(jj % 2) * 4 + i]
                                ug = uu[(jj % 2) * 4 + 2 + i]
                                b_uv = b_uu[(jj % 2) * 4 + i]
                                b_ug = b_uu[(jj % 2) * 4 + 2 + i]
                                S.op("act", lambda e: e.activation(out=fe[i][:], in_=ug[:], func=AF.Silu), reads=[b_ug], writes=[b_fe[i]])
                                S.op("dve", lambda e: e.tensor_tensor(out=actT[:, f, :], in0=uv[:], in1=fe[i][:], op=ALU.mult), reads=[b_uv, b_fe[i]], writes=[b_actT])
                        fgens = [ffn_piece(jj) for jj in range(11)]
                        for step in range(12):
                            if step < 11:
                                next(fgens[step], None)
                            if step >= 1:
                                next(fgens[step - 1], None)
                        tx = fe
                        b_tx = b_fe
                        for j in range(4):
                            tsl = slice(j * 128, (j + 1) * 128)
                            blk = u * 4 + j
                            for n in range(2):
                                po, pob = nbank()
                                for f in range(NF):
                                    S.op("pe", lambda e: e.matmul(po[:], lhsT=actT[:, f, tsl], rhs=wdv[:, f, n * 512:(n + 1) * 512], start=(f == 0), stop=(f == NF - 1)),
                                         reads=[wdb, b_actT], writes=[pob])
                                i = (j * 2 + n) % 2
                                S.op("dve", lambda e: e.tensor_tensor(out=tx[i][:], in0=po[:], in1=gate2[:, n * 512:(n + 1) * 512], op=ALU.mult), reads=[pob, CONST], writes=[b_tx[i]])
                                S.op("dve", lambda e: e.tensor_tensor(out=xs[:, j, n * 512:(n + 1) * 512], in0=xs[:, j, n * 512:(n + 1) * 512], in1=tx[i][:], op=ALU.add),
                                     reads=[b_tx[i], b_xs[j]], writes=[b_xs[j]])
                            if own:
                                ob = blk - NFLAG * 4
                                S.dma("pool", out_d[ob * 128:(ob + 1) * 128, :], xs[:, j, :], reads=[b_xs[j]], writes=[b_out])
                        S.barrier()
        S.barrier()
    return nc


def _t5_bucket(n):
    n = np.maximum(n, 0)
    max_exact = 16
    nf = np.maximum(n, 1).astype(np.float32)
    large = max_exact + (np.log(nf / np.float32(max_exact)) / np.float32(math.log(128 / max_exact)) * np.float32(32 - max_exact)).astype(np.int32)
    large = np.minimum(large, 31)
    return np.where(n < max_exact, n, large)


def _fm(v, nch):
    return np.ascontiguousarray(np.asarray(v, np.float32).reshape(nch, 128).T)


def _piece_fm(w):
    n = w.shape[1]
    return w.reshape(8, 128, n).transpose(1, 0, 2).reshape(128, 8 * n)


def prepare_inputs(NCTX, NFULL, inputs):
    f32 = np.float32
    g = {k: np.asarray(v) for k, v in inputs.items()}
    NU = NCTX + NFULL
    half_tok = (NU // 2) * 512
    x = g["x"].astype(f32, copy=False)
    B = x.shape[0]
    assert x.shape[1] == 2 * half_tok
    w_in = g["w_in"][0]
    cols = {}
    o = 0
    for nm, n in (("mqk", 1024), ("mv", 512), ("mo", 512), ("mi", 4), ("mf", 4), ("aq", 512), ("ak", 512), ("av", 512), ("gm", 1024), ("ga", 1024)):
        cols[nm] = w_in[:, o:o + n]
        o += n

    def perm_qk(w):
        return w.reshape(1024, 2, 4, 64).transpose(0, 2, 1, 3).reshape(1024, 512)

    wall = np.zeros((128, WPAD), f32)

    def put(nm, arr):
        off, n = PIECES[nm]
        assert arr.shape == (128, n), (nm, arr.shape, n)
        wall[:, off:off + n] = arr

    put("mq", _piece_fm(cols["mqk"][:, 0:512]))
    put("mk", _piece_fm(cols["mqk"][:, 512:1024]))
    put("mv", _piece_fm(cols["mv"]))
    put("mo", _piece_fm(cols["mo"]))
    put("aq", _piece_fm(perm_qk(cols["aq"])))
    put("ak", _piece_fm(perm_qk(cols["ak"])))
    put("av", _piece_fm(cols["av"]))
    put("gm0", _piece_fm(cols["gm"][:, 0:512]))
    put("gm1", _piece_fm(cols["gm"][:, 512:1024]))
    put("ga0", _piece_fm(cols["ga"][:, 0:512]))
    put("ga1", _piece_fm(cols["ga"][:, 512:1024]))
    put("gif", _piece_fm(np.concatenate([cols["mi"], cols["mf"]], axis=1)))
    put("bm", g["w_branch_m"][0].reshape(4, 128, 1024).transpose(1, 0, 2).reshape(128, 4096))
    put("ba", g["w_branch_a"][0].reshape(4, 128, 1024).transpose(1, 0, 2).reshape(128, 4096))
    put("out0", _piece_fm(g["w_out"][0][:, 0:512]))
    put("out1", _piece_fm(g["w_out"][0][:, 512:1024]))
    w_up = g["w_up"][0]
    fcw_full = g["ffn_conv_w"][0]
    fcb_full = g["ffn_conv_b"][0]
    chunk_cols = []
    for jj in range(11):
        cc = [np.arange((2 * jj + i) * 128, (2 * jj + i + 1) * 128) for i in range(2)]
        cc += [DFF + np.arange((2 * jj + i) * 128, (2 * jj + i + 1) * 128) for i in range(2)]
        idx = np.concatenate(cc)
        chunk_cols.append(idx)
        put("up%d" % jj, _piece_fm(w_up[:, idx]))
    allidx = np.concatenate(chunk_cols)
    fcw = fcw_full[:, allidx].reshape(3, 44, 128).transpose(2, 1, 0).reshape(128, 44 * 3)
    fcb = fcb_full[allidx].reshape(44, 128).T
    put("down", g["w_down"][0].reshape(NF, 128, 1024).transpose(1, 0, 2).reshape(128, NF * 1024))

    w_ada = g["w_ada"][0]
    wada = np.stack([_piece_fm(w_ada[:, p * 512:(p + 1) * 512]) for p in range(12)]).astype(f32)
    bada = g["b_ada"][0].astype(f32)
    mcw = g["m_conv_w"][0].reshape(4, 8, 128).transpose(2, 1, 0).reshape(128, 32)
    mcb = _fm(g["m_conv_b"][0], 8)
    gifb = np.concatenate([g["m_igate_b"][0], g["m_fgate_b"][0]]).astype(f32)
    gq = np.tile(g["a_qnorm_g"][0], 8).astype(f32)
    gk = np.tile(g["a_knorm_g"][0], 8).astype(f32)
    rel = g["rel_bias"].astype(f32)
    kk = np.arange(128)[:, None]
    qq = np.arange(128)[None, :]
    biasg = np.zeros((128, 4, 2, 2, 128), f32)
    maskn = np.zeros((128, 4, 2, 2, 128), f32)
    for dl in range(2):
        dist = qq - kk + 128 * dl
        bidx = _t5_bucket(dist)
        for h in range(4):
            t = rel[bidx, h]
            biasg[:, h, dl, 0, :] = t
            biasg[:, h, dl, 1, :] = t
        if dl == 0:
            mk = np.where(dist < 0, NEG, 0.0).astype(f32)
            maskn[:, :, 0, :, :] = mk[:, None, None, :]
    farb = rel[31, :].astype(f32)
    umask = (kk <= qq).astype(f32)
    negm4 = np.tile(np.where(kk <= qq, 0.0, NEG).astype(f32), (1, 4))
    import ml_dtypes
    common = dict(
        wada=wada, bada=bada, badafm=_fm(bada, 48), g1fm=_fm(g["norm1_g"][0], 8), g2fm=_fm(g["norm2_g"][0], 8),
        wall=wall, mcw=np.ascontiguousarray(mcw, f32), mcb=mcb, fcw=np.ascontiguousarray(fcw, f32), fcb=np.ascontiguousarray(fcb, f32),
        gifb=gifb, mng=g["m_norm_g"][0].astype(f32), ang=g["a_norm_g"][0].astype(f32), gq=gq, gk=gk,
        gqfm=np.tile(g["a_qnorm_g"][0], 2).reshape(128, 1).astype(f32), gkfm=np.tile(g["a_knorm_g"][0], 2).reshape(128, 1).astype(f32),
        alam=g["a_lambda"][0].reshape(256).astype(f32), biasg=biasg.reshape(128, -1), maskn=maskn.reshape(128, -1), farb=farb,
        identb=np.eye(128).astype(ml_dtypes.bfloat16), identf=np.eye(128, dtype=f32), umask=umask, negm4=negm4,
    )
    in_maps = []
    for b in range(B):
        cfm = _fm(g["c"][b], 8)
        for hf in range(2):
            if hf == 0:
                xl = np.concatenate([np.zeros((half_tok, D), f32), x[b, 0:half_tok]], axis=0)
                fl = np.zeros((128, 1), f32)
            else:
                xl = x[b]
                fl = np.ones((128, 1), f32)
            m = dict(common)
            m["x"] = np.ascontiguousarray(xl)
            m["cfm"] = cfm
            m["flag"] = fl
            in_maps.append(m)
    return in_maps


_NC_CACHE = {}


def run(NCTX, NFULL, inputs):
    key = (NCTX, NFULL)
    if key not in _NC_CACHE:
        _NC_CACHE[key] = build_program(NCTX, NFULL)
    nc = _NC_CACHE[key]
    in_maps = prepare_inputs(NCTX, NFULL, inputs)
    res = run_bass_kernel_spmd(nc, in_maps, core_ids=list(range(len(in_maps))))
    B = len(in_maps) // 2
    half_tok = ((NCTX + NFULL) // 2) * 512
    out = np.empty((B, 2 * half_tok, D), np.float32)
    for b in range(B):
        for hf in range(2):
            out[b, hf * half_tok:(hf + 1) * half_tok] = res.results[b * 2 + hf]["out"]
    return out


def kernel(**inputs):
    return run(7, 9, inputs)
```

```python
import math
from contextlib import ExitStack

import numpy as np

import concourse.bass as bass
import concourse.mybir as mybir
from concourse.bass_utils import run_bass_kernel_spmd

F32 = mybir.dt.float32
BF16 = mybir.dt.bfloat16
AF = mybir.ActivationFunctionType
ALU = mybir.AluOpType
AX = mybir.AxisListType

D = 1024
DC = 8
DFF = 2816
NF = 22
EPS = 1e-6
LAM_INIT = 0.8 - 0.6 * math.exp(-0.3 * 0)
NEG = -30000.0
OPT_SELF_WAR = True
OPT_XN_ACT = True
OPT_SILU = True
OPT_GFOLD = True
OPT_TINY = True
OPT_BATCH = True
OPT_PARTMAIN = True

PIECES = {}
_off = 0
for _nm, _n in (("mq", 4096), ("mk", 4096), ("mv", 4096), ("mo", 4096), ("aq", 4096), ("ak", 4096),
                ("av", 4096), ("gm0", 4096), ("gm1", 4096), ("ga0", 4096), ("ga1", 4096), ("gif", 64),
                ("bm", 4096), ("ba", 4096), ("out0", 4096), ("out1", 4096)):
    PIECES[_nm] = (_off, _n)
    _off += _n
for _j in range(11):
    PIECES["up%d" % _j] = (_off, 4096)
    _off += 4096
PIECES["down"] = (_off, NF * 1024)
_off += NF * 1024
WTOT = _off
WPAD = ((WTOT + 4095) // 4096) * 4096


class Buf:
    __slots__ = ("name", "w", "rs", "sem", "semv", "grp", "psum", "persist")

    def __init__(self, name, grp=None):
        self.psum = False
        self.persist = False
        self.name = name
        self.w = None
        self.rs = []
        self.sem = None
        self.semv = 0
        self.grp = grp


class Sched:
    def __init__(self, nc, stack):
        self.nc = nc
        self.stack = stack
        self.engs = {}
        for nm, h in (("pe", nc.tensor), ("act", nc.scalar), ("dve", nc.vector), ("pool", nc.gpsimd), ("sp", nc.sync)):
            sem = stack.enter_context(nc.semaphore("s_" + nm))
            self.engs[nm] = dict(h=h, sem=sem, cnt=0, seen={})
        self.groups = {}
        self.dma_ev = {}
        self.free_sems = {}
        self.live = []
        self.nsem = 0

    def buf(self, name, grp=None):
        return Buf(name, grp)

    def _need(self, e, deps):
        E = self.engs[e]
        best = {}
        for d in deps:
            if d is None:
                continue
            sem, val, en = d
            if en == e and e == "pe":
                continue
            k = id(sem)
            if k not in best or best[k][1] < val:
                best[k] = (sem, val)
        for k, (sem, val) in best.items():
            if E["seen"].get(k, 0) >= val:
                continue
            E["h"].wait_ge(sem, val)
            E["seen"][k] = val

    def op(self, e, fn, reads=(), writes=()):
        E = self.engs[e]
        deps = []
        for b in reads:
            deps.append(b.w)
            if b.psum:
                for r in b.rs:
                    if r[2] != e:
                        deps.append(r)
        for b in writes:
            if b.w is not None and b.w[2] != e:
                deps.append(b.w)
            for r in b.rs:
                if OPT_SELF_WAR or r[2] != e:
                    deps.append(r)
        self._need(e, deps)
        ins = fn(E["h"])
        E["cnt"] += 1
        ins.then_inc(E["sem"], 1)
        ev = (E["sem"], E["cnt"], e)
        for b in reads:
            b.rs.append(ev)
        for b in writes:
            b.w = ev
            b.rs = []
        return ins

    def dma(self, q, out, in_, reads=(), writes=()):
        E = self.engs[q]
        deps = []
        for b in reads:
            deps.append(b.w)
        for b in writes:
            if b.grp in ("wbw", "kvw", "outw"):
                continue
            deps.append(b.w)
            deps.extend(b.rs)
        self._need(q, deps)
        ins = E["h"].dma_start(out=out, in_=in_)
        cands = list(writes) + list(reads)
        tgt = ([b for b in cands if b.grp is None] + cands)[0]
        if tgt.grp is not None:
            gk = tgt.grp
            if gk not in self.groups:
                self.groups[gk] = [self.stack.enter_context(self.nc.semaphore("g_" + gk)), 0]
            g = self.groups[gk]
            g[1] += 16
            sem, val = g[0], g[1]
        else:
            if tgt.sem is None:
                tgt.sem = {}
                self.live.append(tgt)
            if q not in tgt.sem:
                fs = self.free_sems.setdefault(q, [])
                if not fs:
                    self.nsem += 1
                    fs.append([self.stack.enter_context(self.nc.semaphore("dp%d" % self.nsem)), 0])
                tgt.sem[q] = fs.pop()
            ent = tgt.sem[q]
            ent[1] += 16
            sem, val = ent[0], ent[1]
        ins.then_inc(sem, 16)
        self.dma_ev[id(sem)] = (sem, val)
        ev = (sem, val, "dma")
        for b in reads:
            b.rs.append(ev)
        for b in writes:
            b.w = ev
            b.rs = []
        return ins

    def barrier(self, engines=("pe", "act", "dve", "pool", "sp")):
        deps = [(E["sem"], E["cnt"], "x") for E in self.engs.values() if E["cnt"] > 0]
        deps += [(s, v, "dma") for (s, v) in self.dma_ev.values()]
        for e in engines:
            self._need(e, deps)
        self.dma_ev = {}
        keep = []
        for b in self.live:
            if b.persist:
                keep.append(b)
                continue
            for qq, ent in b.sem.items():
                self.free_sems[qq].append(ent)
            b.sem = None
        self.live = keep

    def seal(self, bufs, grp):
        g = self.groups[grp]
        for b in bufs:
            b.w = (g[0], g[1], "dma")


def build_program(NCTX, NFULL, debug=False):
    NU = NCTX + NFULL
    NFLAG = NU // 2
    NBLK = NU * 4
    NTOK = NU * 512
    NOWN = (NFULL - 1) * 512
    assert NU == 2 * (NFULL - 1)

    nc = bass.Bass("TRN2", target_bir_lowering=False)

    def din(name, shape, dt=F32):
        return nc.dram_tensor(name, list(shape), dt, kind="ExternalInput").ap()

    x_d = din("x", [NTOK, D])
    cfm_d = din("cfm", [128, 8])
    wada_d = din("wada", [12, 128, 8 * 512])
    bada_d = din("bada", [6144])
    badafm_d = din("badafm", [128, 48])
    g1fm_d = din("g1fm", [128, 8])
    g2fm_d = din("g2fm", [128, 8])
    wall_d = din("wall", [128, WPAD])
    mcw_d = din("mcw", [128, 8 * 4])
    mcb_d = din("mcb", [128, 8])
    fcw_d = din("fcw", [128, 44 * 3])
    fcb_d = din("fcb", [128, 44])
    gifb_d = din("gifb", [8])
    mng_d = din("mng", [512])
    ang_d = din("ang", [512])
    gq_d = din("gq", [512])
    gk_d = din("gk", [512])
    gqfm_d = din("gqfm", [128, 1])
    gkfm_d = din("gkfm", [128, 1])
    alam_d = din("alam", [256])
    biasg_d = din("biasg", [128, 4 * 2 * 2 * 128])
    maskn_d = din("maskn", [128, 4 * 2 * 2 * 128])
    farb_d = din("farb", [4])
    flag_d = din("flag", [128, 1])
    identb_d = din("identb", [128, 128], BF16)
    identf_d = din("identf", [128, 128])
    umask_d = din("umask", [128, 128])
    negm4_d = din("negm4", [128, 512])
    out_d = nc.dram_tensor("out", [NOWN, D], F32, kind="ExternalOutput").ap()
    wb_d = nc.dram_tensor("wb_scr", [128, WPAD], BF16, kind="Internal").ap()
    kt_d = nc.dram_tensor("kt_scr", [4, 128, NBLK * 128], BF16, kind="Internal").ap()
    va_d = nc.dram_tensor("va_scr", [4, 128, NBLK * 130], BF16, kind="Internal").ap()

    with ExitStack() as top:
        S = Sched(nc, top)

        uid = [0]

        def sbt(st, name, shape, dt=F32):
            uid[0] += 1
            return st.enter_context(nc.sbuf_tensor("s%d_%s" % (uid[0], name), list(shape), dt))

        banks = [top.enter_context(nc.psum_tensor("bank%d" % i, [128, 512], F32)) for i in range(8)]
        bbufs = [S.buf("bank%d" % i) for i in range(8)]
        for b_ in bbufs:
            b_.psum = True
        bank_rr = [0]

        def nbank(lo=0, hi=8):
            i = lo + (bank_rr[0] % (hi - lo))
            bank_rr[0] += 1
            return banks[i], bbufs[i]

        def bfview(bank):
            return bank[:].bitcast(BF16)

        identb = sbt(top, "identb", [128, 128], BF16)
        identf = sbt(top, "identf", [128, 128])
        umask = sbt(top, "umask", [128, 128])
        negm4 = sbt(top, "negm4", [128, 512])
        ones_f = sbt(top, "ones_f", [128, 128])
        flag = sbt(top, "flag", [128, 1])
        epst = sbt(top, "epst", [128, 1])
        onet = sbt(top, "onet", [128, 1])
        mult1 = sbt(top, "mult1", [128, 8])
        shift1 = sbt(top, "shift1", [128, 8])
        mult2 = sbt(top, "mult2", [128, 8])
        shift2 = sbt(top, "shift2", [128, 8])
        gate1 = sbt(top, "gate1", [128, D])
        gate2 = sbt(top, "gate2", [128, D])
        mcw = sbt(top, "mcw", [128, 8, 4])
        mcb = sbt(top, "mcb", [128, 8])
        mcwf = sbt(top, "mcwf", [128, 8, 4])
        fcwf = sbt(top, "fcwf", [128, 44, 3])
        mcnb = sbt(top, "mcnb", [128, 8])
        fcw = sbt(top, "fcw", [128, 44, 3])
        fcb = sbt(top, "fcb", [128, 44])
        fcnb = sbt(top, "fcnb", [128, 44])
        gifb = sbt(top, "gifb", [128, 8])
        mng = sbt(top, "mng", [128, 512])
        ang = sbt(top, "ang", [128, 512])
        gq = sbt(top, "gq", [128, 512])
        gk = sbt(top, "gk", [128, 512])
        gqfm = sbt(top, "gqfm", [128, 1])
        gkfm = sbt(top, "gkfm", [128, 1])
        mbc = sbt(top, "mbc", [128, 8, 3])
        fbc = sbt(top, "fbc", [128, 44, 2])
        ctmp = sbt(top, "ctmp", [128, 44])
        b_mbc = S.buf("mbc")
        b_fbc = S.buf("fbc")
        biasb = sbt(top, "biasb", [128, 4, 2, 256], BF16)
        farb = sbt(top, "farb", [128, 4])
        neglam = sbt(top, "neglam", [128, 1])
        Sst = sbt(top, "Sst", [128, 4, 130])
        Sstb = sbt(top, "Sstb", [128, 4, 130], BF16)
        mhalo = sbt(top, "mhalo", [128, 8, 3])
        fhalo = sbt(top, "fhalo", [128, 44, 2])
        CONST = S.buf("const")
        b_S = S.buf("Sst")
        b_Sb = S.buf("Sstb")
        b_mhalo = S.buf("mhalo")
        b_fhalo = S.buf("fhalo")
        b_wb = S.buf("wb_scr", grp="wbw")
        b_ktd = S.buf("kt_scr", grp="kvw")
        b_vad = S.buf("va_scr", grp="kvw")
        b_out = S.buf("outd", grp="outw")

        def ld_const(t, src, grp="cst"):
            b = S.buf("c_" + t.name, grp=grp)
            S.dma("sp", t[:], src, writes=[b])
            return b

        cb = []
        cb.append(ld_const(identb, identb_d[:, :]))
        cb.append(ld_const(identf, identf_d[:, :]))
        cb.append(ld_const(umask, umask_d[:, :]))
        cb.append(ld_const(negm4, negm4_d[:, :]))
        cb.append(ld_const(flag, flag_d[:, :]))
        cb.append(ld_const(mcw, mcw_d.rearrange("p (c k) -> p c k", k=4)))
        cb.append(ld_const(mcb, mcb_d[:, :]))
        cb.append(ld_const(fcw, fcw_d.rearrange("p (c k) -> p c k", k=3)))
        cb.append(ld_const(fcb, fcb_d[:, :]))
        cb.append(ld_const(gifb, gifb_d.partition_broadcast(128)))
        cb.append(ld_const(mng, mng_d.partition_broadcast(128)))
        cb.append(ld_const(ang, ang_d.partition_broadcast(128)))
        cb.append(ld_const(gq, gq_d.partition_broadcast(128)))
        cb.append(ld_const(gk, gk_d.partition_broadcast(128)))
        cb.append(ld_const(farb, farb_d.partition_broadcast(128)))
        cb.append(ld_const(gqfm, gqfm_d[:, :]))
        cb.append(ld_const(gkfm, gkfm_d[:, :]))
        S.seal(cb, "cst")
        S.op("dve", lambda e: e.memset(ones_f[:], 1.0), writes=[CONST])
        S.op("dve", lambda e: e.memset(epst[:], EPS), writes=[CONST])
        S.op("dve", lambda e: e.memset(onet[:], 1.0), writes=[CONST])
        S.op("dve", lambda e: e.memset(Sst[:], 0.0), writes=[b_S])
        S.op("dve", lambda e: e.memset(Sstb[:], 0.0), writes=[b_Sb])
        S.op("dve", lambda e: e.memset(mhalo[:], 0.0), writes=[b_mhalo])
        S.op("dve", lambda e: e.memset(fhalo[:], 0.0), writes=[b_fhalo])
        S.op("dve", lambda e: e.tensor_scalar(out=mcnb[:], in0=mcb[:], scalar1=-1.0, scalar2=None, op0=ALU.mult), reads=cb, writes=[CONST])
        S.op("dve", lambda e: e.tensor_scalar(out=fcnb[:], in0=fcb[:], scalar1=-1.0, scalar2=None, op0=ALU.mult), reads=cb, writes=[CONST])
        S.op("dve", lambda e: e.tensor_scalar(out=gqfm[:], in0=gqfm[:], scalar1=0.125, scalar2=None, op0=ALU.mult), reads=cb, writes=[CONST])
        S.op("dve", lambda e: e.tensor_scalar(out=mcwf[:], in0=mcw[:], scalar1=flag[:, 0:1], scalar2=None, op0=ALU.mult), reads=cb, writes=[CONST])
        S.op("dve", lambda e: e.tensor_scalar(out=fcwf[:], in0=fcw[:], scalar1=flag[:, 0:1], scalar2=None, op0=ALU.mult), reads=cb, writes=[CONST])
        S.op("dve", lambda e: e.tensor_scalar(out=ang[:], in0=ang[:], scalar1=1.0 - LAM_INIT, scalar2=None, op0=ALU.mult), reads=cb, writes=[CONST])

        with ExitStack() as st:
            cfm = sbt(st, "cfm", [128, 8])
            sc = sbt(st, "sc", [128, 8])
            sct = sbt(st, "sct", [128, 8])
            scbc = sbt(st, "scbc", [128, 8, 128])
            badafm = sbt(st, "badafm", [128, 48])
            g1fm = sbt(st, "g1fm", [128, 8])
            g2fm = sbt(st, "g2fm", [128, 8])
            modfm = sbt(st, "modfm", [128, 48])
            alam = sbt(st, "alam", [128, 256])
            lt = sbt(st, "lt", [128, 128])
            ls = sbt(st, "ls", [128, 2])
            biasg = sbt(st, "biasg", [128, 2048])
            maskn = sbt(st, "maskn", [128, 2048])
            wst = [sbt(st, "wst%d" % i, [128, 4096]) for i in range(2)]
            wbo = [sbt(st, "wbo%d" % i, [128, 4096], BF16) for i in range(2)]
            bbt = [sbt(st, "bbt%d" % i, [128, 512]) for i in range(2)]
            b_wst = [S.buf("wst%d" % i) for i in range(2)]
            b_wbo = [S.buf("wbo%d" % i) for i in range(2)]
            b_bbt = [S.buf("bbt%d" % i) for i in range(2)]
            L = S.buf("prel")
            b_l = [ld_const(cfm, cfm_d[:, :], "cst2"), ld_const(badafm, badafm_d[:, :], "cst2"), ld_const(g1fm, g1fm_d[:, :], "cst2"),
                   ld_const(g2fm, g2fm_d[:, :], "cst2"), ld_const(alam, alam_d.partition_broadcast(128), "cst2"),
                   ld_const(biasg, biasg_d[:, :], "cst2"), ld_const(maskn, maskn_d[:, :], "cst2")]
            S.seal(b_l, "cst2")
            S.op("act", lambda e: e.activation(out=sct[:], in_=cfm[:], func=AF.Exp, scale=-1.0), reads=b_l, writes=[L])
            S.op("dve", lambda e: e.tensor_scalar(out=sct[:], in0=sct[:], scalar1=1.0, scalar2=None, op0=ALU.add), reads=[L], writes=[L])
            S.op("dve", lambda e: e.reciprocal(out=sct[:], in_=sct[:]), reads=[L], writes=[L])
            S.op("dve", lambda e: e.tensor_tensor(out=sc[:], in0=cfm[:], in1=sct[:], op=ALU.mult), reads=[L], writes=[L])
            S.op("dve", lambda e: e.tensor_copy(out=scbc[:], in_=sc[:].unsqueeze(2).to_broadcast([128, 8, 128])), reads=[L], writes=[L])
            S.op("dve", lambda e: e.tensor_tensor(out=lt[:, 0:64], in0=alam[:, 0:64], in1=alam[:, 64:128], op=ALU.mult), reads=b_l, writes=[L])
            S.op("dve", lambda e: e.tensor_tensor(out=lt[:, 64:128], in0=alam[:, 128:192], in1=alam[:, 192:256], op=ALU.mult), reads=[L], writes=[L])
            S.op("dve", lambda e: e.tensor_reduce(out=ls[:], in_=lt[:].rearrange("p (a b) -> p a b", a=2), axis=AX.X, op=ALU.add), reads=[L], writes=[L])
            S.op("act", lambda e: e.activation(out=ls[:], in_=ls[:], func=AF.Exp), reads=[L], writes=[L])
            S.op("dve", lambda e: e.tensor_tensor(out=neglam[:], in0=ls[:, 1:2], in1=ls[:, 0:1], op=ALU.subtract), reads=[L], writes=[CONST])
            S.op("dve", lambda e: e.tensor_scalar(out=neglam[:], in0=neglam[:], scalar1=-LAM_INIT, scalar2=None, op0=ALU.add), reads=[CONST], writes=[CONST])
            S.op("dve", lambda e: e.tensor_tensor(out=biasb[:].rearrange("p a b c -> p (a b c)"), in0=biasg[:], in1=maskn[:], op=ALU.add), reads=b_l, writes=[CONST])

            fm_ps, fm_b = banks[7], bbufs[7]
            fm_cols = {0: 0, 1: 4, 2: 8, 3: 12, 6: 24, 7: 28, 8: 32, 9: 36}
            for pc in range(12):
                i = pc % 2
                S.dma("sp", wst[i][:], wada_d[pc, :, :], writes=[b_wst[i]])
                wv = wst[i][:].rearrange("p (c n) -> p c n", c=8)
                if pc in (4, 5, 10, 11):
                    S.dma("sp", bbt[i][:], bada_d[pc * 512:(pc + 1) * 512].partition_broadcast(128), writes=[b_bbt[i]])
                    pb, pbb = nbank(0, 6)
                    for dc in range(8):
                        S.op("pe", lambda e: e.matmul(pb[:], lhsT=scbc[:, dc, :], rhs=wv[:, dc, :], start=(dc == 0), stop=(dc == 7)),
                             reads=[L, b_wst[i]], writes=[pbb])
                    gt = gate1 if pc < 6 else gate2
                    off = (pc % 2) * 512
                    S.op("dve", lambda e: e.tensor_tensor(out=gt[:, off:off + 512], in0=pb[:], in1=bbt[i][:], op=ALU.add),
                         reads=[pbb, b_bbt[i]], writes=[CONST])
                else:
                    for k in range(4):
                        col = fm_cols[pc] + k
                        for dc in range(8):
                            S.op("pe", lambda e: e.matmul(fm_ps[:, col:col + 1], lhsT=wv[:, dc, k * 128:(k + 1) * 128], rhs=sc[:, dc:dc + 1],
                                                         start=(dc == 0), stop=(dc == 7)), reads=[L, b_wst[i]], writes=[fm_b])
            S.op("dve", lambda e: e.tensor_tensor(out=modfm[:, 0:16], in0=fm_ps[:, 0:16], in1=badafm[:, 0:16], op=ALU.add), reads=[fm_b] + b_l, writes=[L])
            S.op("dve", lambda e: e.tensor_tensor(out=modfm[:, 24:40], in0=fm_ps[:, 24:40], in1=badafm[:, 24:40], op=ALU.add), reads=[fm_b] + b_l, writes=[L])
            S.op("dve", lambda e: e.scalar_tensor_tensor(out=mult1[:], in0=modfm[:, 8:16], scalar=1.0, in1=g1fm[:], op0=ALU.add, op1=ALU.mult), reads=[L], writes=[CONST])
            S.op("dve", lambda e: e.tensor_copy(out=shift1[:], in_=modfm[:, 0:8]), reads=[L], writes=[CONST])
            S.op("dve", lambda e: e.scalar_tensor_tensor(out=mult2[:], in0=modfm[:, 32:40], scalar=1.0, in1=g2fm[:], op0=ALU.add, op1=ALU.mult), reads=[L], writes=[CONST])
            S.op("dve", lambda e: e.tensor_copy(out=shift2[:], in_=modfm[:, 24:32]), reads=[L], writes=[CONST])

            cast_eng = ["dve", "act"]
            for ch in range(WPAD // 4096):
                i = ch % 2
                S.dma("sp", wst[i][:], wall_d[:, ch * 4096:(ch + 1) * 4096], writes=[b_wst[i]])
                ce = cast_eng[ch % 2]
                if ce == "act":
                    S.op("act", lambda e: e.copy(out=wbo[i][:], in_=wst[i][:]), reads=[b_wst[i]], writes=[b_wbo[i]])
                else:
                    S.op(ce, lambda e: e.tensor_copy(out=wbo[i][:], in_=wst[i][:]), reads=[b_wst[i]], writes=[b_wbo[i]])
                S.dma("pool", wb_d[:, ch * 4096:(ch + 1) * 4096], wbo[i][:], reads=[b_wbo[i]], writes=[b_wb])
            S.barrier()

        njunk = sbt(top, "njunk", [128, D], BF16)
        ncache = {}
        pf = [sbt(top, "pf%d" % i, [128, 4096], BF16) for i in range(4)]
        b_pf = [S.buf("pf%d" % i) for i in range(4)]
        for b_ in b_pf:
            b_.persist = True
        prefetched = {}

        def prefetch(slots, pieces):
            for sl, piece in zip(slots, pieces):
                off, n = PIECES[piece]
                S.dma("sp", pf[sl][:, 0:n], wb_d[:, off:off + n], reads=[b_wb], writes=[b_pf[sl]])
                prefetched[piece] = (pf[sl], b_pf[sl])

        def wload(st, name, piece, shape3=None):
            if piece in prefetched:
                return prefetched.pop(piece)
            off, n = PIECES[piece]
            t = sbt(st, name, [128, n], BF16)
            b = S.buf(name)
            S.dma("sp", t[:], wb_d[:, off:off + n], reads=[b_wb], writes=[b])
            return t, b

        for u in range(NU):
            full = u >= NCTX
            flagged = u < NFLAG
            own = u >= NFLAG
            with ExitStack() as su:
                xs = sbt(su, "xs", [128, 4, D])
                hT = sbt(su, "hT", [128, 8, 512], BF16)
                sa = su.enter_context(ExitStack())
                mqT = sbt(sa, "mqT", [128, 8, 512], BF16)
                mVA = sbt(sa, "mVA", [128, 4, 4, 130], BF16)
                sigmo = sbt(sa, "sigmo", [128, 4, 512])
                gif = sbt(sa, "gif", [128, 4, 8])
                Qbd = sbt(sa, "Qbd", [128, 4, 4, 256], BF16)
                hmT = sbt(sa, "hmT", [128, 4, 512], BF16)
                haT = sbt(sa, "haT", [128, 4, 512], BF16)
                b_xs = [S.buf("xs%d" % j) for j in range(4)]
                b_hT = S.buf("hT")
                b_hT2 = S.buf("hT2")
                b_mqT = S.buf("mqT")
                b_mVA = S.buf("mVA")
                b_sig = S.buf("sigmo")
                b_gif = S.buf("gif")
                b_Qbd = S.buf("Qbd")
                b_hmT = S.buf("hmT")
                b_haT = S.buf("haT")
                if full:
                    S.op("dve", lambda e: e.memset(Qbd[:], 0.0), writes=[b_Qbd])

                def norm_block(st, j, mult, shift, hdst, b_hdst, tagp):
                    junk = njunk
                    if not hasattr(st, "ncache"):
                        st.ncache = {}
                    ck = (tagp, j % 2)
                    if ck not in st.ncache:
                        st.ncache[ck] = (sbt(st, tagp + "xn%d" % (j % 2), [128, D], BF16), S.buf("xn"))
                    xn, b_xn = st.ncache[ck]
                    ss = sbt(st, tagp + "ss%d" % j, [128, 2])
                    bl = S.buf("nb")
                    S.op("act", lambda e: e.activation(out=junk[:], in_=xs[:, j, :], func=AF.Square, scale=1.0 / 32.0, accum_out=ss[:, 0:1]),
                         reads=[b_xs[j]], writes=[bl])
                    S.op("act", lambda e: e.activation(out=ss[:, 1:2], in_=ss[:, 0:1], func=AF.Ln, bias=epst[:, 0:1], scale=1.0), reads=[bl, CONST], writes=[bl])
                    S.op("act", lambda e: e.activation(out=ss[:, 1:2], in_=ss[:, 1:2], func=AF.Exp, scale=-0.5), reads=[bl], writes=[bl])
                    S.op("dve", lambda e: e.tensor_scalar(out=xn[:], in0=xs[:, j, :], scalar1=ss[:, 1:2], scalar2=None, op0=ALU.mult),
                         reads=[bl, b_xs[j]], writes=[b_xn])
                    for half in range(2):
                        pb, pbb = nbank()
                        pv = bfview(pb)[:, 0:512].rearrange("p (c t) -> p c t", c=4)
                        for cc in range(4):
                            c = half * 4 + cc
                            S.op("pe", lambda e: e.transpose(out=pv[:, cc, :], in_=xn[:, c * 128:(c + 1) * 128], identity=identb[:]),
                                 reads=[b_xn, cb[0]], writes=[pbb])
                        for cc in range(4):
                            c = half * 4 + cc
                            if half == 0:
                                S.op("act", lambda e: e.activation(out=hdst[:, c, j * 128:(j + 1) * 128], in_=pv[:, cc, :], func=AF.Identity,
                                                                  scale=mult[:, c:c + 1], bias=shift[:, c:c + 1]), reads=[pbb, CONST], writes=[b_hdst[0]])
                            else:
                                S.op("dve", lambda e: e.tensor_scalar(out=hdst[:, c, j * 128:(j + 1) * 128], in0=pv[:, cc, :], scalar1=mult[:, c:c + 1],
                                                                     scalar2=shift[:, c:c + 1], op0=ALU.mult, op1=ALU.add), reads=[pbb, CONST], writes=[b_hdst[1]])

                with ExitStack() as st:
                    names = (["mq", "mk", "mv", "gif", "av", "mo", "aq", "ak"] if full else ["mk", "mv", "gif", "av", "ak"])
                    W = {}
                    for j in range(4):
                        blk = u * 4 + j
                        S.dma("sp", xs[:, j, :], x_d[blk * 128:(blk + 1) * 128, :], writes=[b_xs[j]])
                    for nm in names:
                        W[nm] = wload(st, "w_" + nm, nm)
                    if not full and u + 1 < NU:
                        nxt_full = (u + 1) >= NCTX
                        slots = (2, 3) if (nxt_full or (u + 1) % 2 == 1) else (0, 1)
                        if nxt_full:
                            slots = (2, 3)
                        prefetch(slots, ["mq", "mk"] if nxt_full else ["mk", "mv"])
                    for j in range(4):
                        norm_block(st, j, mult1, shift1, hT, (b_hT, b_hT2), "n1")
                    if flagged:
                        S.op("dve", lambda e: e.tensor_copy(out=mVA[:, :, :, 128:130], in_=flag[:, 0:1].unsqueeze(1).unsqueeze(1).to_broadcast([128, 4, 4, 2])),
                             reads=[cb[4]], writes=[b_mVA])
                    else:
                        S.op("dve", lambda e: e.memset(mVA[:, :, :, 128:130], 1.0), writes=[b_mVA])
                    acc = [sbt(st, "acc%d" % i, [128, 512]) for i in range(4)]
                    et = [sbt(st, "et%d" % i, [128, 512]) for i in range(4)]
                    b_acc = [S.buf("acc%d" % i) for i in range(4)]
                    b_et = [S.buf("et%d" % i) for i in range(4)]
                    hv = mhalo
                    if not OPT_BATCH:
                        S.op("dve", lambda e: e.memset(mbc[:], 0.0), writes=[b_mbc])
                    if OPT_BATCH:
                        S.op("dve", lambda e: e.tensor_tensor(out=mbc[:, :, 2], in0=hv[:, :, 2], in1=mcw[:, :, 0], op=ALU.mult), reads=[b_mhalo] + cb, writes=[b_mbc])
                        S.op("dve", lambda e: e.tensor_tensor(out=mbc[:, :, 1], in0=hv[:, :, 2], in1=mcw[:, :, 1], op=ALU.mult), reads=[b_mhalo] + cb, writes=[b_mbc])
                        S.op("dve", lambda e: e.tensor_tensor(out=mbc[:, :, 0], in0=hv[:, :, 2], in1=mcw[:, :, 2], op=ALU.mult), reads=[b_mhalo] + cb, writes=[b_mbc])
                        S.op("dve", lambda e: e.tensor_tensor(out=ctmp[:, 0:8], in0=hv[:, :, 1], in1=mcw[:, :, 0], op=ALU.mult), reads=[b_mhalo] + cb, writes=[b_mbc])
                        S.op("dve", lambda e: e.tensor_tensor(out=mbc[:, :, 1], in0=mbc[:, :, 1], in1=ctmp[:, 0:8], op=ALU.add), reads=[b_mbc], writes=[b_mbc])
                        S.op("dve", lambda e: e.tensor_tensor(out=ctmp[:, 8:16], in0=hv[:, :, 1], in1=mcw[:, :, 1], op=ALU.mult), reads=[b_mhalo] + cb, writes=[b_mbc])
                        S.op("dve", lambda e: e.tensor_tensor(out=mbc[:, :, 0], in0=mbc[:, :, 0], in1=ctmp[:, 8:16], op=ALU.add), reads=[b_mbc], writes=[b_mbc])
                        S.op("dve", lambda e: e.tensor_tensor(out=ctmp[:, 16:24], in0=hv[:, :, 0], in1=mcw[:, :, 0], op=ALU.mult), reads=[b_mhalo] + cb, writes=[b_mbc])
                        S.op("dve", lambda e: e.tensor_tensor(out=mbc[:, :, 0], in0=mbc[:, :, 0], in1=ctmp[:, 16:24], op=ALU.add), reads=[b_mbc], writes=[b_mbc])
                    groups = ([[0, 1, 2, 3], [4, 5, 6, 7]] if full else [[4, 5, 6, 7]])
                    wsel = mcwf if flagged else mcw
                    for grp in groups:
                        pbs = []
                        for k, c in enumerate(grp):
                            wt, wbf = W["mq" if c < 4 else "mk"]
                            wv = wt[:].rearrange("p (c n) -> p c n", c=8)
                            pb, pbb = nbank()
                            pbs.append((pb, pbb))
                            for dc in range(8):
                                S.op("pe", lambda e: e.matmul(pb[:], lhsT=wv[:, dc, k * 128:(k + 1) * 128], rhs=hT[:, dc, :], start=(dc == 0), stop=(dc == 7)),
                                     reads=[wbf, b_hT, b_hT2], writes=[pbb])
                            S.op("act", lambda e: e.activation(out=acc[k][:], in_=pb[:], func=AF.Identity, scale=wsel[:, c, 3:4], bias=mcb[:, c:c + 1]),
                                 reads=[pbb, CONST] + cb, writes=[b_acc[k]])
                        for tp in (2, 1, 0):
                            sh = 3 - tp
                            for k, c in enumerate(grp):
                                pb, pbb = pbs[k]
                                S.op("dve", lambda e: e.scalar_tensor_tensor(out=acc[k][:, sh:512], in0=pb[:, 0:512 - sh], scalar=wsel[:, c, tp:tp + 1], in1=acc[k][:, sh:512],
                                                                            op0=ALU.mult, op1=ALU.add), reads=[pbb, b_acc[k], CONST], writes=[b_acc[k]])
                        for k, c in enumerate(grp):
                            pb, pbb = pbs[k]
                            S.op("dve", lambda e: e.tensor_tensor(out=acc[k][:, 0:3], in0=acc[k][:, 0:3], in1=mbc[:, c, :], op=ALU.add), reads=[b_acc[k], b_mbc], writes=[b_acc[k]])
                            if flagged:
                                S.op("dve", lambda e: e.tensor_scalar(out=mhalo[:, c, :], in0=pb[:, 509:512], scalar1=flag[:, 0:1], scalar2=None, op0=ALU.mult), reads=[pbb, cb[4], b_mbc], writes=[b_mhalo])
                            else:
                                S.op("dve", lambda e: e.tensor_copy(out=mhalo[:, c, :], in_=pb[:, 509:512]), reads=[pbb, b_mbc], writes=[b_mhalo])
                        for k, c in enumerate(grp):
                            if not OPT_SILU:
                                S.op("act", lambda e: e.activation(out=et[k][:], in_=acc[k][:], func=AF.Exp, scale=-1.0), reads=[b_acc[k]], writes=[b_et[k]])
                                sk = (128.0 ** 0.5) if c < 4 else 1.0
                                S.op("dve", lambda e: e.tensor_scalar(out=et[k][:], in0=et[k][:], scalar1=1.0, scalar2=sk, op0=ALU.add, op1=ALU.mult), reads=[b_et[k]], writes=[b_et[k]])
                                S.op("dve", lambda e: e.reciprocal(out=et[k][:], in_=et[k][:]), reads=[b_et[k]], writes=[b_et[k]])
                                S.op("dve", lambda e: e.tensor_tensor(out=mqT[:, c, :], in0=acc[k][:], in1=et[k][:], op=ALU.mult), reads=[b_acc[k], b_et[k]], writes=[b_mqT])
                            elif c < 4:
                                S.op("act", lambda e: e.activation(out=et[k][:], in_=acc[k][:], func=AF.Silu), reads=[b_acc[k]], writes=[b_et[k]])
                                S.op("dve", lambda e: e.tensor_scalar(out=mqT[:, c, :], in0=et[k][:], scalar1=128.0 ** -0.5, scalar2=None, op0=ALU.mult),
                                     reads=[b_et[k]], writes=[b_mqT])
                            else:
                                S.op("act", lambda e: e.activation(out=mqT[:, c, :], in_=acc[k][:], func=AF.Silu), reads=[b_acc[k]], writes=[b_mqT])
                    sq = [sbt(st, "sq%d" % i, [128, 512]) for i in range(2)]
                    qb = [sbt(st, "qb%d" % i, [128, 512], BF16) for i in range(2)]
                    rs = [sbt(st, "rs%d" % i, [128, 16]) for i in range(2)]
                    KTb = [sbt(st, "KTb%d" % i, [128, 4, 128], BF16) for i in range(2)]
                    VAb = [sbt(st, "VAb%d" % i, [128, 4, 130], BF16) for i in range(2)]
                    b_t = [S.buf("tmj%d" % i) for i in range(2)]
                    b_KTb = [S.buf("KTb%d" % i) for i in range(2)]
                    b_VAb = [S.buf("VAb%d" % i) for i in range(2)]
                    for i in range(2):
                        if flagged:
                            S.op("dve", lambda e: e.tensor_copy(out=VAb[i][:, :, 128:130], in_=flag[:, 0:1].unsqueeze(1).to_broadcast([128, 4, 2])),
                                 reads=[cb[4]], writes=[b_VAb[i]])
                        else:
                            S.op("dve", lambda e: e.memset(VAb[i][:, :, 128:130], 1.0), writes=[b_VAb[i]])
                    for j in range(4):
                        blk = u * 4 + j
                        i = j % 2
                        tsl = slice(j * 128, (j + 1) * 128)

                        def proj(nm):
                            wt, wbf = W[nm]
                            n = PIECES[nm][1] // 8
                            wv = wt[:].rearrange("p (c n) -> p c n", c=8)
                            pb, pbb = nbank()
                            for dc in range(8):
                                S.op("pe", lambda e: e.matmul(pb[:, 0:n], lhsT=hT[:, dc, tsl], rhs=wv[:, dc, :], start=(dc == 0), stop=(dc == 7)),
                                     reads=[wbf, b_hT, b_hT2], writes=[pbb])
                            return pb, pbb

                        def evac_scaled(out_ap, in_ap, rd, wr):
                            if flagged:
                                S.op("act", lambda e: e.activation(out=out_ap, in_=in_ap, func=AF.Copy, scale=flag[:, 0:1]), reads=rd + [cb[4]], writes=wr)
                            else:
                                S.op("act", lambda e: e.copy(out=out_ap, in_=in_ap), reads=rd, writes=wr)

                        pb, pbb = proj("mv")
                        evac_scaled(mVA[:, j, :, 0:128], pb[:].rearrange("p (h d) -> p h d", h=4), [pbb], [b_mVA])
                        pb, pbb = proj("gif")
                        S.op("dve", lambda e: e.tensor_tensor(out=gif[:, j, :], in0=pb[:, 0:8], in1=gifb[:], op=ALU.add), reads=[pbb, cb[9]], writes=[b_gif])
                        pb, pbb = proj("av")
                        evac_scaled(VAb[i][:, :, 0:128], pb[:].rearrange("p (h d) -> p h d", h=4), [pbb], [b_VAb[i]])
                        S.dma("pool", va_d.rearrange("h p (b c) -> p h b c", c=130)[:, :, blk, :], VAb[i][:], reads=[b_VAb[i]], writes=[b_vad])
                        if full:
                            pb, pbb = proj("mo")
                            S.op("act", lambda e: e.activation(out=sigmo[:, j, :], in_=pb[:], func=AF.Exp, scale=-1.0), reads=[pbb], writes=[b_sig])
                            S.op("dve", lambda e: e.tensor_scalar(out=sigmo[:, j, :], in0=sigmo[:, j, :], scalar1=1.0, scalar2=None, op0=ALU.add), reads=[b_sig], writes=[b_sig])
                            S.op("dve", lambda e: e.reciprocal(out=sigmo[:, j, :], in_=sigmo[:, j, :]), reads=[b_sig], writes=[b_sig])
                        for nm in (["aq", "ak"] if full else ["ak"]):
                            pb, pbb = proj(nm)
                            S.op("act", lambda e: e.activation(out=sq[i][:], in_=pb[:], func=AF.Square, scale=0.125), reads=[pbb], writes=[b_t[i]])
                            S.op("dve", lambda e: e.tensor_reduce(out=rs[i][:, 0:8], in_=sq[i][:].rearrange("p (g d) -> p g d", d=64), axis=AX.X, op=ALU.add),
                                 reads=[b_t[i]], writes=[b_t[i]])
                            S.op("act", lambda e: e.activation(out=rs[i][:, 8:16], in_=rs[i][:, 0:8], func=AF.Ln, bias=epst[:, 0:1], scale=1.0), reads=[b_t[i], CONST], writes=[b_t[i]])
                            S.op("act", lambda e: e.activation(out=rs[i][:, 8:16], in_=rs[i][:, 8:16], func=AF.Exp, scale=-0.5), reads=[b_t[i]], writes=[b_t[i]])
                            S.op("dve", lambda e: e.tensor_tensor(out=qb[i][:].rearrange("p (g d) -> p g d", d=64), in0=pb[:].rearrange("p (g d) -> p g d", d=64),
                                                                 in1=rs[i][:, 8:16].unsqueeze(2).to_broadcast([128, 8, 64]), op=ALU.mult), reads=[pbb, b_t[i]], writes=[b_t[i]])
                            tb, tbb = nbank()
                            tv = bfview(tb)[:, 0:512].rearrange("p (h t) -> p h t", h=4)
                            for h in range(4):
                                S.op("pe", lambda e: e.transpose(out=tv[:, h, :], in_=qb[i][:, h * 128:(h + 1) * 128], identity=identb[:]), reads=[b_t[i], cb[0]], writes=[tbb])
                            if nm == "aq":
                                if OPT_GFOLD:
                                    S.op("act", lambda e: e.activation(out=Qbd[0:64, :, j, 0:128], in_=tv[0:64, :, :], func=AF.Copy, scale=gqfm[0:64, 0:1]), reads=[tbb, CONST] + cb, writes=[b_Qbd])
                                    S.op("act", lambda e: e.activation(out=Qbd[64:128, :, j, 128:256], in_=tv[64:128, :, :], func=AF.Copy, scale=gqfm[64:128, 0:1]), reads=[tbb, CONST] + cb, writes=[b_Qbd])
                                else:
                                    S.op("act", lambda e: e.copy(out=Qbd[0:64, :, j, 0:128], in_=tv[0:64, :, :]), reads=[tbb, CONST] + cb, writes=[b_Qbd])
                                    S.op("act", lambda e: e.copy(out=Qbd[64:128, :, j, 128:256], in_=tv[64:128, :, :]), reads=[tbb, CONST] + cb, writes=[b_Qbd])
                            else:
                                if OPT_GFOLD:
                                    S.op("act", lambda e: e.activation(out=KTb[i][:], in_=tv, func=AF.Copy, scale=gkfm[:, 0:1]), reads=[tbb] + cb, writes=[b_KTb[i]])
                                else:
                                    S.op("act", lambda e: e.copy(out=KTb[i][:], in_=tv), reads=[tbb] + cb, writes=[b_KTb[i]])
                                S.dma("pool", kt_d.rearrange("h p k -> p h k")[:, :, blk * 128:(blk + 1) * 128], KTb[i][:], reads=[b_KTb[i]], writes=[b_ktd])
                    S.barrier()

                def mlstm_block(j, st, bankfn, cache):
                    def sbt_c(st_, name, shape, dt=F32):
                        if name not in cache:
                            cache[name] = sbt(st_, name, shape, dt)
                        return cache[name]

                    def buf_c(name):
                        k_ = "B_" + name
                        if k_ not in cache:
                            cache[k_] = S.buf(name)
                        return cache[k_]
                    tsl = slice(j * 128, (j + 1) * 128)
                    lf = sbt_c(st, "lf%d" % (j % 2), [128, 4])
                    e1 = sbt_c(st, "e1%d" % (j % 2), [128, 4])
                    e2 = sbt_c(st, "e2%d" % (j % 2), [128, 4])
                    wsx = sbt_c(st, "wsx%d" % (j % 2), [128, 4])
                    RL = sbt_c(st, "RL%d" % (j % 2), [128, 4, 128])
                    Et = sbt_c(st, "Et%d" % (j % 2), [128, 512])
                    Kw = sbt_c(st, "Kw%d" % (j % 2), [128, 4, 128], BF16)
                    bm = buf_c("ml%d" % (j % 2))
                    b_e1 = buf_c("e1%d" % (j % 2))
                    b_ws = buf_c("wsx%d" % (j % 2))
                    b_RL = buf_c("RL%d" % (j % 2))
                    b_Et = buf_c("Et%d" % (j % 2))
                    b_Kw = buf_c("Kw%d" % (j % 2))
                    S.op("act", lambda e: e.activation(out=lf[:], in_=gif[:, j, 4:8], func=AF.Exp, scale=-1.0), reads=[b_gif], writes=[bm])
                    yield
                    S.op("act", lambda e: e.activation(out=lf[:], in_=lf[:], func=AF.Ln, bias=onet[:, 0:1], scale=1.0), reads=[bm, CONST], writes=[bm])
                    yield
                    S.op("dve", lambda e: e.tensor_scalar(out=lf[:], in0=lf[:], scalar1=-1.0, scalar2=None, op0=ALU.mult), reads=[bm], writes=[bm])
                    yield
                    p1, p1b = bankfn()
                    S.op("pe", lambda e: e.matmul(p1[:, 0:4], lhsT=umask[:], rhs=lf[:], start=True, stop=True), reads=[bm, cb[2]], writes=[p1b])
                    S.op("dve", lambda e: e.tensor_tensor(out=RL[:], in0=umask[:].unsqueeze(1).to_broadcast([128, 4, 128]),
                                                         in1=lf[:].unsqueeze(2).to_broadcast([128, 4, 128]), op=ALU.mult), reads=[bm, cb[2]], writes=[b_RL])
                    yield
                    S.op("dve", lambda e: e.tensor_tensor(out=e1[:], in0=gif[:, j, 0:4], in1=p1[:, 0:4], op=ALU.subtract), reads=[b_gif, p1b], writes=[b_e1])
                    yield
                    RLf = RL[:].rearrange("p h t -> p (h t)")
                    pBc, pBcb = bankfn()
                    S.op("pe", lambda e: e.matmul(pBc[:], lhsT=ones_f[:], rhs=RLf, start=True, stop=True), reads=[b_RL, CONST], writes=[pBcb])
                    yield
                    blast = pBc[:].rearrange("p (h t) -> p h t", h=4)[:, :, 127]
                    S.op("dve", lambda e: e.tensor_tensor(out=e2[:], in0=e1[:], in1=blast, op=ALU.add), reads=[b_e1, pBcb], writes=[b_ws])
                    yield
                    S.op("act", lambda e: e.activation(out=Et[:], in_=pBc[:], func=AF.Exp), reads=[pBcb], writes=[b_Et])
                    S.op("act", lambda e: e.activation(out=wsx[:], in_=e2[:], func=AF.Exp), reads=[b_ws], writes=[b_ws])
                    yield
                    tb, tbb = bankfn()
                    tv = bfview(tb)[:, 0:512].rearrange("p (h t) -> p h t", h=4)
                    for h in range(4):
                        S.op("pe", lambda e: e.transpose(out=tv[:, h, :], in_=mqT[:, 4 + h, tsl], identity=identb[:]), reads=[b_mqT, cb[0]], writes=[tbb])
                    yield
                    S.op("dve", lambda e: e.tensor_tensor(out=Kw[:], in0=tv, in1=wsx[:].unsqueeze(2).to_broadcast([128, 4, 128]), op=ALU.mult),
                         reads=[tbb, b_ws], writes=[b_Kw])
                    yield
                    if full:
                        DT = sbt_c(st, "DT%d" % (j % 2), [128, 4, 128])
                        PT = sbt_c(st, "PT%d" % (j % 2), [128, 4, 128], BF16)
                        qpT = sbt_c(st, "qpT%d" % (j % 2), [128, 4, 128], BF16)
                        numS = sbt_c(st, "numS%d" % (j % 2), [128, 4, 130])
                        rden = sbt_c(st, "rden%d" % (j % 2), [128, 4])
                        hmr = sbt_c(st, "hmr%d" % (j % 2), [128, 4, 128])
                        hsq = sbt_c(st, "hsq%d" % (j % 2), [128, 4, 128])
                        hss = sbt_c(st, "hss%d" % (j % 2), [128, 8])
                        gs = sbt_c(st, "gs%d" % (j % 2), [128, 512])
                        hmf = sbt_c(st, "hmf%d" % (j % 2), [128, 4, 128], BF16)
                        b_o = buf_c("mo%d" % (j % 2))
                        b_gs = buf_c("gs%d" % (j % 2))
                        b_num = buf_c("numS%d" % (j % 2))
                        b_DT = buf_c("DT%d" % (j % 2))
                        b_PT = buf_c("PT%d" % (j % 2))
                        b_qp = buf_c("qpT%d" % (j % 2))
                        pBm, pBmb = bankfn()
                        S.op("pe", lambda e: e.matmul(pBm[:], lhsT=ones_f[:], rhs=RLf, start=True, stop=False), reads=[b_RL, CONST], writes=[pBmb])
                        S.op("pe", lambda e: e.matmul(pBm[:], lhsT=identf[:], rhs=negm4[:], start=False, stop=True), reads=[cb[1], cb[3]], writes=[pBmb])
                        S.op("dve", lambda e: e.tensor_tensor(out=qpT[:], in0=mqT[:, 0:4, tsl], in1=Et[:].rearrange("p (h t) -> p h t", h=4), op=ALU.mult),
                             reads=[b_mqT, b_Et], writes=[b_qp])
                        S.op("dve", lambda e: e.tensor_tensor(out=gs[:], in0=mng[:], in1=sigmo[:, j, :], op=ALU.mult), reads=[b_sig] + cb, writes=[b_gs])
                        yield
                        for h in range(4):
                            S.op("act", lambda e: e.activation(out=DT[:, h, :], in_=pBm[:, h * 128:(h + 1) * 128], func=AF.Exp, bias=e1[:, h:h + 1], scale=1.0),
                                 reads=[pBmb, b_e1], writes=[b_DT])
                        yield
                        pA, pAb = bankfn()
                        for h in range(4):
                            S.op("pe", lambda e: e.matmul(pA[:, h * 128:(h + 1) * 128], lhsT=mqT[:, 4 + h, tsl], rhs=mqT[:, h, tsl], start=True, stop=True),
                                 reads=[b_mqT], writes=[pAb])
                        yield
                        S.op("dve", lambda e: e.tensor_tensor(out=PT[:], in0=pA[:].rearrange("p (h t) -> p h t", h=4), in1=DT[:], op=ALU.mult),
                             reads=[pAb, b_DT], writes=[b_PT])
                        yield
                    yield "AB"
                    if full:
                        for hp in range(2):
                            pn, pnb = bankfn()
                            for hh in range(2):
                                h = 2 * hp + hh
                                o = hh * 130
                                S.op("pe", lambda e: e.matmul(pn[:, o:o + 130], lhsT=PT[:, h, :], rhs=mVA[:, j, h, :], start=True, stop=False), reads=[b_PT, b_mVA], writes=[pnb])
                                S.op("pe", lambda e: e.matmul(pn[:, o:o + 130], lhsT=qpT[:, h, :], rhs=Sstb[:, h, :], start=False, stop=True), reads=[b_qp, b_Sb], writes=[pnb])
                            yield
                            S.op("dve", lambda e: e.tensor_copy(out=numS[:, 2 * hp:2 * hp + 2, :], in_=pn[:, 0:260].rearrange("p (h c) -> p h c", h=2)), reads=[pnb], writes=[b_num])
                            yield
                    Ev = Et[:].rearrange("p (h t) -> p h t", h=4)
                    for hp in range(2):
                        pc_, pcb = bankfn()
                        for hh in range(2):
                            h = 2 * hp + hh
                            o = hh * 130
                            S.op("pe", lambda e: e.matmul(pc_[:, o:o + 130], lhsT=Kw[:, h, :], rhs=mVA[:, j, h, :], start=True, stop=True), reads=[b_Kw, b_mVA], writes=[pcb])
                        yield
                        for hh in range(2):
                            h = 2 * hp + hh
                            o = hh * 130
                            S.op("dve", lambda e: e.scalar_tensor_tensor(out=Sst[:, h, :], in0=Sst[:, h, :], scalar=Ev[:, h, 127:128], in1=pc_[:, o:o + 130],
                                                                        op0=ALU.mult, op1=ALU.add), reads=[b_S, b_Et, pcb], writes=[b_S])
                        yield
                    S.op("dve", lambda e: e.tensor_copy(out=Sstb[:], in_=Sst[:]), reads=[b_S], writes=[b_Sb])
                    yield "S"
                    if full:
                        S.op("act", lambda e: e.activation(out=rden[:], in_=numS[:, :, 128], func=AF.Abs), reads=[b_num], writes=[b_o])
                        yield
                        S.op("dve", lambda e: e.tensor_scalar(out=rden[:], in0=rden[:], scalar1=1.0, scalar2=None, op0=ALU.max), reads=[b_o], writes=[b_o])
                        S.op("dve", lambda e: e.reciprocal(out=rden[:], in_=rden[:]), reads=[b_o], writes=[b_o])
                        S.op("dve", lambda e: e.tensor_tensor(out=hmr[:], in0=numS[:, :, 0:128], in1=rden[:].unsqueeze(2).to_broadcast([128, 4, 128]), op=ALU.mult),
                             reads=[b_num, b_o], writes=[b_o])
                        yield
                        S.op("act", lambda e: e.activation(out=hsq[:], in_=hmr[:], func=AF.Square, scale=1.0 / math.sqrt(128.0)), reads=[b_o], writes=[b_o])
                        yield
                        S.op("dve", lambda e: e.tensor_reduce(out=hss[:, 0:4], in_=hsq[:], axis=AX.X, op=ALU.add), reads=[b_o], writes=[b_o])
                        yield
                        S.op("act", lambda e: e.activation(out=hss[:, 4:8], in_=hss[:, 0:4], func=AF.Ln, bias=epst[:, 0:1], scale=1.0), reads=[b_o, CONST], writes=[b_o])
                        S.op("act", lambda e: e.activation(out=hss[:, 4:8], in_=hss[:, 4:8], func=AF.Exp, scale=-0.5), reads=[b_o], writes=[b_o])
                        yield
                        for h in range(4):
                            S.op("dve", lambda e: e.scalar_tensor_tensor(out=hmf[:, h, :], in0=hmr[:, h, :], scalar=hss[:, 4 + h:5 + h], in1=gs[:, h * 128:(h + 1) * 128],
                                                                        op0=ALU.mult, op1=ALU.mult), reads=[b_o, b_gs], writes=[b_o])
                        yield
                        tb2, tb2b = bankfn()
                        tv2 = bfview(tb2)[:, 0:512].rearrange("p (h t) -> p h t", h=4)
                        for h in range(4):
                            S.op("pe", lambda e: e.transpose(out=tv2[:, h, :], in_=hmf[:, h, :], identity=identb[:]), reads=[b_o, cb[0]], writes=[tb2b])
                        yield
                        S.op("dve", lambda e: e.tensor_copy(out=hmT[:, :, tsl], in_=tv2), reads=[tb2b], writes=[b_hmT])
                        yield

                def mlstm_driver(st, bankfn):
                    cache = {}
                    for j in range(4):
                        for r in mlstm_block(j, st, bankfn, cache):
                            yield

                if not full:
                    with ExitStack() as st:
                        for _ in mlstm_driver(st, nbank):
                            pass
                        S.barrier()

                if full:
                    with ExitStack() as st:
                        NKC = 8
                        kch = [sbt(st, "kch%d" % i, [128, NKC * 128], BF16) for i in range(3)]
                        vch = [sbt(st, "vch%d" % i, [128, NKC, 130], BF16) for i in range(3)]
                        b_kch = [S.buf("kch%d" % i) for i in range(3)]
                        b_vch = [S.buf("vch%d" % i) for i in range(3)]
                        pex = [sbt(st, "pex%d" % i, [128, 512], BF16) for i in range(6)]
                        b_pex = [S.buf("pex%d" % i) for i in range(6)]
                        rr = sbt(st, "rr", [128, 8])
                        har = sbt(st, "har", [128, 4, 128])
                        hsq = sbt(st, "ahsq", [128, 4, 128])
                        hss = sbt(st, "ahss", [128, 8])
                        haf = sbt(st, "haf", [128, 4, 128], BF16)
                        b_a = S.buf("attn_o")
                        nkb_tot = u * 4 + 4
                        chunk_list = [(s0, min(NKC, nkb_tot - s0)) for s0 in range(0, nkb_tot, NKC)]
                        ring = 0
                        LCH = len(chunk_list)
                        prefetch((0, 1), ["bm", "gm0"])
                        mdrv = mlstm_driver(st, lambda: (banks[7], bbufs[7]))
                        n_items_est = 4 * sum((2 if kb_ <= u * 4 - 2 else 0) + sum(1 for p2 in range(2) for jq in (2 * p2, 2 * p2 + 1) if kb_ > u * 4 + 2 * p2 - 2 and kb_ <= u * 4 + jq)
                                              for kb_ in range(nkb_tot))
                        MSTEP = max(1, n_items_est // 150)
                        gcount = [0]
                        loaded = set()

                        def ensure_chunk(g):
                            if g >= 4 * LCH or g in loaded:
                                return
                            loaded.add(g)
                            h_, ci_ = g // LCH, g % LCH
                            s0, nk = chunk_list[ci_]
                            ri = g % 3
                            S.dma("sp", kch[ri][:, 0:nk * 128], kt_d[h_, :, s0 * 128:(s0 + nk) * 128], reads=[b_ktd], writes=[b_kch[ri]])
                            S.dma("sp", vch[ri][:, 0:nk, :], va_d[h_, :, s0 * 130:(s0 + nk) * 130].rearrange("p (b c) -> p b c", c=130), reads=[b_vad], writes=[b_vch[ri]])

                        for h in range(4):
                            accs = [(banks[jq], bbufs[jq]) for jq in range(4)]
                            for jq in range(4):
                                S.op("dve", lambda e: e.memset(accs[jq][0][:, 0:260], 0.0), writes=[accs[jq][1]])
                            items = []
                            for ci, (s0, nk) in enumerate(chunk_list):
                                g = h * LCH + ci
                                ri = g % 3
                                for kk in range(nk):
                                    kb = s0 + kk
                                    for p2 in range(2):
                                        j0 = 2 * p2
                                        if u == NCTX:
                                            if p2 == 1 and kb <= u * 4 + 3:
                                                items.append((ri, kk, kb, (3,), g))
                                            continue
                                        if kb <= u * 4 + j0 - 2:
                                            items.append((ri, kk, kb, (j0, j0 + 1), g))
                                        else:
                                            for jq in (j0, j0 + 1):
                                                if kb <= u * 4 + jq:
                                                    items.append((ri, kk, kb, (jq,), g))

                            LAG = 3
                            NS = 3
                            NPX = len(pex)

                            def emit_scores(it, idx):
                                ri, kk, kb, jqs, ci = it
                                ensure_chunk(ci)
                                ensure_chunk(ci + 1)
                                nq = len(jqs)
                                sl = idx % NS
                                sp_, spb = banks[4 + sl], bbufs[4 + sl]
                                sv = sp_[:, 0:256 * nq]
                                px = idx % NPX
                                delta = u * 4 + jqs[0] - kb
                                near = (nq == 1) and delta <= 1
                                rhs = Qbd[:, h, jqs[0]:jqs[0] + nq, :].rearrange("p j c -> p (j c)")
                                S.op("pe", lambda e: e.matmul(sv, lhsT=kch[ri][:, kk * 128:(kk + 1) * 128], rhs=rhs, start=True, stop=not near),
                                     reads=[b_kch[ri], b_Qbd], writes=[spb])
                                if near:
                                    S.op("pe", lambda e: e.matmul(sv, lhsT=identb[:], rhs=biasb[:, h, delta, :], start=False, stop=True), reads=[CONST, cb[0]], writes=[spb])
                                    S.op("act", lambda e: e.activation(out=pex[px][:, 0:256 * nq], in_=sv, func=AF.Exp), reads=[spb], writes=[b_pex[px]])
                                else:
                                    S.op("act", lambda e: e.activation(out=pex[px][:, 0:256 * nq], in_=sv, func=AF.Exp, bias=farb[:, h:h + 1], scale=1.0),
                                         reads=[spb, cb[14]], writes=[b_pex[px]])

                            def emit_pv(it, idx):
                                ri, kk, kb, jqs, ci = it
                                px = idx % NPX
                                for a_, jq in enumerate(jqs):
                                    ab, abb = accs[jq]
                                    for m in range(2):
                                        c0 = a_ * 256 + m * 128
                                        S.op("pe", lambda e: e.matmul(ab[:, m * 130:(m + 1) * 130], lhsT=pex[px][:, c0:c0 + 128], rhs=vch[ri][:, kk, :],
                                                                     start=False, stop=False, skip_group_check=True), reads=[b_pex[px], b_vch[ri]], writes=[abb])

                            for idx in range(len(items) + LAG):
                                if idx < len(items):
                                    emit_scores(items[idx], idx)
                                if idx >= LAG:
                                    emit_pv(items[idx - LAG], idx - LAG)
                                gcount[0] += 1
                                if gcount[0] % MSTEP == 0:
                                    next(mdrv, None)
                            for jq in range(4):
                                ab, abb = accs[jq]
                                av = ab[:, 0:260].rearrange("p (m c) -> p m c", m=2)
                                S.op("dve", lambda e: e.tensor_scalar(out=rr[:, 2 * jq:2 * jq + 2], in0=av[:, :, 128], scalar1=1e-30, scalar2=None, op0=ALU.max), reads=[abb], writes=[b_a])
                                S.op("dve", lambda e: e.reciprocal(out=rr[:, 2 * jq:2 * jq + 2], in_=rr[:, 2 * jq:2 * jq + 2]), reads=[b_a], writes=[b_a])
                                S.op("dve", lambda e: e.tensor_tensor(out=rr[:, 2 * jq + 1:2 * jq + 2], in0=rr[:, 2 * jq + 1:2 * jq + 2], in1=neglam[:], op=ALU.mult),
                                     reads=[b_a, CONST], writes=[b_a])
                                S.op("dve", lambda e: e.tensor_scalar(out=har[:, jq, :], in0=av[:, 0, 0:128], scalar1=rr[:, 2 * jq:2 * jq + 1], scalar2=None, op0=ALU.mult),
                                     reads=[abb, b_a], writes=[b_a])
                                S.op("dve", lambda e: e.scalar_tensor_tensor(out=har[:, jq, :], in0=av[:, 1, 0:128], scalar=rr[:, 2 * jq + 1:2 * jq + 2], in1=har[:, jq, :],
                                                                            op0=ALU.mult, op1=ALU.add), reads=[abb, b_a], writes=[b_a])
                            S.op("act", lambda e: e.activation(out=hsq[:], in_=har[:], func=AF.Square, scale=1.0 / math.sqrt(128.0)), reads=[b_a], writes=[b_a])
                            S.op("dve", lambda e: e.tensor_reduce(out=hss[:, 0:4], in_=hsq[:], axis=AX.X, op=ALU.add), reads=[b_a], writes=[b_a])
                            S.op("act", lambda e: e.activation(out=hss[:, 4:8], in_=hss[:, 0:4], func=AF.Ln, bias=epst[:, 0:1], scale=1.0), reads=[b_a, CONST], writes=[b_a])
                            S.op("act", lambda e: e.activation(out=hss[:, 4:8], in_=hss[:, 4:8], func=AF.Exp, scale=-0.5), reads=[b_a], writes=[b_a])
                            for jq in range(4):
                                S.op("dve", lambda e: e.scalar_tensor_tensor(out=haf[:, jq, :], in0=har[:, jq, :], scalar=hss[:, 4 + jq:5 + jq], in1=ang[:, h * 128:(h + 1) * 128],
                                                                            op0=ALU.mult, op1=ALU.mult), reads=[b_a, CONST] + cb, writes=[b_a])
                            tb, tbb = banks[4 + h % 3], bbufs[4 + h % 3]
                            tv = bfview(tb)[:, 0:512].rearrange("p (j t) -> p j t", j=4)
                            for jq in range(4):
                                S.op("pe", lambda e: e.transpose(out=tv[:, jq, :], in_=haf[:, jq, :], identity=identb[:]), reads=[b_a, cb[0]], writes=[tbb])
                            S.op("dve", lambda e: e.tensor_copy(out=haT[:, h, :], in_=tv.rearrange("p j t -> p (j t)")), reads=[tbb], writes=[b_haT])
                        for _ in mdrv:
                            pass
                        S.barrier()

                    h2T = hT
                    b_h2T = b_hT
                    with ExitStack() as st:
                        W = {}
                        for nm in ("bm", "gm0", "ba", "ga0", "gm1", "ga1", "out0", "out1"):
                            W[nm] = wload(st, "w_" + nm, nm)
                        prefetch((2, 3), ["up0", "up1"])
                        yT = sbt(st, "yT", [128, 8, 512], BF16)
                        b_yT = S.buf("yT")
                        sg = [[sbt(st, "sg%d_%d" % (i, a), [128, 512]) for a in range(2)] for i in range(2)]
                        ty = [[sbt(st, "ty%d_%d" % (i, a), [128, 512]) for a in range(2)] for i in range(2)]
                        b_sg = [S.buf("sg%d" % i) for i in range(2)]
                        for c in range(8):
                            i = c % 2
                            res = []
                            for a, (bw, gw, srcT, b_src) in enumerate((("bm", "gm", hmT, b_hmT), ("ba", "ga", haT, b_haT))):
                                wt, wbf = W[bw]
                                wv = wt[:].rearrange("p (k n) -> p k n", k=4)
                                py, pyb = nbank()
                                for k in range(4):
                                    S.op("pe", lambda e: e.matmul(py[:], lhsT=wv[:, k, c * 128:(c + 1) * 128], rhs=srcT[:, k, :], start=(k == 0), stop=(k == 3)),
                                         reads=[wbf, b_src], writes=[pyb])
                                gt_, gbf = W[gw + str(c // 4)]
                                gv = gt_[:].rearrange("p (c n) -> p c n", c=8)
                                pg, pgb = nbank()
                                for dc in range(8):
                                    S.op("pe", lambda e: e.matmul(pg[:], lhsT=gv[:, dc, (c % 4) * 128:(c % 4 + 1) * 128], rhs=hT[:, dc, :], start=(dc == 0), stop=(dc == 7)),
                                         reads=[gbf, b_hT, b_hT2], writes=[pgb])
                                S.op("act", lambda e: e.activation(out=sg[i][a][:], in_=pg[:], func=AF.Sigmoid), reads=[pgb], writes=[b_sg[i]])
                                S.op("dve", lambda e: e.tensor_tensor(out=ty[i][a][:], in0=py[:], in1=sg[i][a][:], op=ALU.mult), reads=[pyb, b_sg[i]], writes=[b_sg[i]])
                            S.op("dve", lambda e: e.tensor_tensor(out=yT[:, c, :], in0=ty[i][0][:], in1=ty[i][1][:], op=ALU.add), reads=[b_sg[i]], writes=[b_yT])
                        tx = [sbt(st, "tx%d" % i, [128, 512]) for i in range(2)]
                        b_tx = [S.buf("tx%d" % i) for i in range(2)]
                        for j in range(4):
                            tsl = slice(j * 128, (j + 1) * 128)
                            for n in range(2):
                                wt, wbf = W["out%d" % n]
                                wv = wt[:].rearrange("p (c n) -> p c n", c=8)
                                po, pob = nbank()
                                for c in range(8):
                                    S.op("pe", lambda e: e.matmul(po[:], lhsT=yT[:, c, tsl], rhs=wv[:, c, :], start=(c == 0), stop=(c == 7)), reads=[wbf, b_yT], writes=[pob])
                                i = (j * 2 + n) % 2
                                S.op("dve", lambda e: e.tensor_tensor(out=tx[i][:], in0=po[:], in1=gate1[:, n * 512:(n + 1) * 512], op=ALU.mult), reads=[pob, CONST], writes=[b_tx[i]])
                                S.op("dve", lambda e: e.tensor_tensor(out=xs[:, j, n * 512:(n + 1) * 512], in0=xs[:, j, n * 512:(n + 1) * 512], in1=tx[i][:], op=ALU.add),
                                     reads=[b_tx[i], b_xs[j]], writes=[b_xs[j]])
                        for j in range(4):
                            norm_block(st, j, mult2, shift2, h2T, (b_hT, b_hT2), "n2")
                        S.barrier()

                    sa.close()
                    with ExitStack() as st:
                        actT = sbt(st, "actT", [128, NF, 512], BF16)
                        b_actT = S.buf("actT")
                        wu = [None, None, None]
                        uu = [sbt(st, "fu%d" % i, [128, 512]) for i in range(8)]
                        b_uu = [S.buf("fu%d" % i) for i in range(8)]
                        fe = [sbt(st, "fe%d" % i, [128, 512]) for i in range(2)]
                        b_fe = [S.buf("fe%d" % i) for i in range(2)]
                        wus = [pf[2], pf[3], sbt(st, "wup2", [128, 4096], BF16)]
                        b_wus = [b_pf[2], b_pf[3], S.buf("wup2")]

                        def ldup(jj):
                            if ("up%d" % jj) in prefetched:
                                prefetched.pop("up%d" % jj)
                                return
                            off, n = PIECES["up%d" % jj]
                            S.dma("sp", wus[jj % 3][:], wb_d[:, off:off + n], reads=[b_wb], writes=[b_wus[jj % 3]])

                        ldup(0)
                        ldup(1)
                        wd = sbt(st, "w_down", [128, NF * 1024], BF16)
                        wdb = S.buf("w_down")
                        wdv = wd[:].rearrange("p (f n) -> p f n", f=NF)
                        if not OPT_BATCH:
                            S.op("dve", lambda e: e.memset(fbc[:], 0.0), writes=[b_fbc])
                        if OPT_BATCH:
                            S.op("dve", lambda e: e.tensor_tensor(out=fbc[:, :, 1], in0=fhalo[:, :, 1], in1=fcw[:, :, 0], op=ALU.mult), reads=[b_fhalo] + cb, writes=[b_fbc])
                            S.op("dve", lambda e: e.tensor_tensor(out=fbc[:, :, 0], in0=fhalo[:, :, 1], in1=fcw[:, :, 1], op=ALU.mult), reads=[b_fhalo] + cb, writes=[b_fbc])
                            S.op("dve", lambda e: e.tensor_tensor(out=ctmp[:], in0=fhalo[:, :, 0], in1=fcw[:, :, 0], op=ALU.mult), reads=[b_fhalo] + cb, writes=[b_fbc])
                            S.op("dve", lambda e: e.tensor_tensor(out=fbc[:, :, 0], in0=fbc[:, :, 0], in1=ctmp[:], op=ALU.add), reads=[b_fbc], writes=[b_fbc])

                        def ffn_piece(jj):
                            if jj + 2 < 11:
                                ldup(jj + 2)
                            if jj == 2:
                                off_d, n_d = PIECES["down"]
                                S.dma("sp", wd[:], wb_d[:, off_d:off_d + n_d], reads=[b_wb], writes=[wdb])
                            if jj == 4 and u + 1 < NU:
                                prefetch((0, 1), ["mq", "mk"])
                            wv = wus[jj % 3][:].rearrange("p (c n) -> p c n", c=8)
                            wbf = b_wus[jj % 3]
                            wsel = fcwf if flagged else fcw
                            pbs = []
                            for k in range(4):
                                q = jj * 4 + k
                                pb, pbb = nbank()
                                pbs.append((pb, pbb))
                                for dc in range(8):
                                    S.op("pe", lambda e: e.matmul(pb[:], lhsT=wv[:, dc, k * 128:(k + 1) * 128], rhs=h2T[:, dc, :], start=(dc == 0), stop=(dc == 7)),
                                         reads=[wbf, b_hT, b_hT2], writes=[pbb])
                                kk_ = (jj % 2) * 4 + k
                                S.op("act", lambda e: e.activation(out=uu[kk_][:], in_=pb[:], func=AF.Identity, scale=wsel[:, q, 2:3], bias=fcb[:, q:q + 1]),
                                     reads=[pbb, CONST] + cb, writes=[b_uu[kk_]])
                            for tp in (1, 0):
                                sh = 2 - tp
                                for k in range(4):
                                    q = jj * 4 + k
                                    kk_ = (jj % 2) * 4 + k
                                    pb, pbb = pbs[k]
                                    S.op("dve", lambda e: e.scalar_tensor_tensor(out=uu[kk_][:, sh:512], in0=pb[:, 0:512 - sh], scalar=wsel[:, q, tp:tp + 1], in1=uu[kk_][:, sh:512],
                                                                                op0=ALU.mult, op1=ALU.add), reads=[pbb, b_uu[kk_], CONST], writes=[b_uu[kk_]])
                            for k in range(4):
                                q = jj * 4 + k
                                kk_ = (jj % 2) * 4 + k
                                pb, pbb = pbs[k]
                                S.op("dve", lambda e: e.tensor_tensor(out=uu[kk_][:, 0:2], in0=uu[kk_][:, 0:2], in1=fbc[:, q, :], op=ALU.add), reads=[b_uu[kk_], b_fbc], writes=[b_uu[kk_]])
                                if flagged:
                                    S.op("dve", lambda e: e.tensor_scalar(out=fhalo[:, q, :], in0=pb[:, 510:512], scalar1=flag[:, 0:1], scalar2=None, op0=ALU.mult), reads=[pbb, cb[4], b_fbc], writes=[b_fhalo])
                                else:
                                    S.op("dve", lambda e: e.tensor_copy(out=fhalo[:, q, :], in_=pb[:, 510:512]), reads=[pbb, b_fbc], writes=[b_fhalo])
                            yield
                            for i in range(2):
                                f = 2 * jj + i
                                uv = uu[(jj % 2) * 4 + i]
                                ug = uu[(jj % 2) * 4 + 2 + i]
                                b_uv = b_uu[(jj % 2) * 4 + i]
                                b_ug = b_uu[(jj % 2) * 4 + 2 + i]
                                S.op("act", lambda e: e.activation(out=fe[i][:], in_=ug[:], func=AF.Silu), reads=[b_ug], writes=[b_fe[i]])
                                S.op("dve", lambda e: e.tensor_tensor(out=actT[:, f, :], in0=uv[:], in1=fe[i][:], op=ALU.mult), reads=[b_uv, b_fe[i]], writes=[b_actT])
                        fgens = [ffn_piece(jj) for jj in range(11)]
                        for step in range(12):
                            if step < 11:
                                next(fgens[step], None)
                            if step >= 1:
                                next(fgens[step - 1], None)
                        tx = fe
                        b_tx = b_fe
                        for j in range(4):
                            tsl = slice(j * 128, (j + 1) * 128)
                            blk = u * 4 + j
                            for n in range(2):
                                po, pob = nbank()
                                for f in range(NF):
                                    S.op("pe", lambda e: e.matmul(po[:], lhsT=actT[:, f, tsl], rhs=wdv[:, f, n * 512:(n + 1) * 512], start=(f == 0), stop=(f == NF - 1)),
                                         reads=[wdb, b_actT], writes=[pob])
                                i = (j * 2 + n) % 2
                                S.op("dve", lambda e: e.tensor_tensor(out=tx[i][:], in0=po[:], in1=gate2[:, n * 512:(n + 1) * 512], op=ALU.mult), reads=[pob, CONST], writes=[b_tx[i]])
                                S.op("dve", lambda e: e.tensor_tensor(out=xs[:, j, n * 512:(n + 1) * 512], in0=xs[:, j, n * 512:(n + 1) * 512], in1=tx[i][:], op=ALU.add),
                                     reads=[b_tx[i], b_xs[j]], writes=[b_xs[j]])
                            if own:
                                ob = blk - NFLAG * 4
                                S.dma("pool", out_d[ob * 128:(ob + 1) * 128, :], xs[:, j, :], reads=[b_xs[j]], writes=[b_out])
                        S.barrier()
        S.barrier()
    return nc


def _t5_bucket(n):
    n = np.maximum(n, 0)
    max_exact = 16
    nf = np.maximum(n, 1).astype(np.float32)
    large = max_exact + (np.log(nf / np.float32(max_exact)) / np.float32(math.log(128 / max_exact)) * np.float32(32 - max_exact)).astype(np.int32)
    large = np.minimum(large, 31)
    return np.where(n < max_exact, n, large)


def _fm(v, nch):
    return np.ascontiguousarray(np.asarray(v, np.float32).reshape(nch, 128).T)


def _piece_fm(w):
    n = w.shape[1]
    return w.reshape(8, 128, n).transpose(1, 0, 2).reshape(128, 8 * n)


def prepare_inputs(NCTX, NFULL, inputs):
    f32 = np.float32
    g = {k: np.asarray(v) for k, v in inputs.items()}
    NU = NCTX + NFULL
    half_tok = (NU // 2) * 512
    x = g["x"].astype(f32, copy=False)
    B = x.shape[0]
    assert x.shape[1] == 2 * half_tok
    w_in = g["w_in"][0]
    cols = {}
    o = 0
    for nm, n in (("mqk", 1024), ("mv", 512), ("mo", 512), ("mi", 4), ("mf", 4), ("aq", 512), ("ak", 512), ("av", 512), ("gm", 1024), ("ga", 1024)):
        cols[nm] = w_in[:, o:o + n]
        o += n

    def perm_qk(w):
        return w.reshape(1024, 2, 4, 64).transpose(0, 2, 1, 3).reshape(1024, 512)

    wall = np.zeros((128, WPAD), f32)

    def put(nm, arr):
        off, n = PIECES[nm]
        assert arr.shape == (128, n), (nm, arr.shape, n)
        wall[:, off:off + n] = arr

    put("mq", _piece_fm(cols["mqk"][:, 0:512]))
    put("mk", _piece_fm(cols["mqk"][:, 512:1024]))
    put("mv", _piece_fm(cols["mv"]))
    put("mo", _piece_fm(cols["mo"]))
    put("aq", _piece_fm(perm_qk(cols["aq"])))
    put("ak", _piece_fm(perm_qk(cols["ak"])))
    put("av", _piece_fm(cols["av"]))
    put("gm0", _piece_fm(cols["gm"][:, 0:512]))
    put("gm1", _piece_fm(cols["gm"][:, 512:1024]))
    put("ga0", _piece_fm(cols["ga"][:, 0:512]))
    put("ga1", _piece_fm(cols["ga"][:, 512:1024]))
    put("gif", _piece_fm(np.concatenate([cols["mi"], cols["mf"]], axis=1)))
    put("bm", g["w_branch_m"][0].reshape(4, 128, 1024).transpose(1, 0, 2).reshape(128, 4096))
    put("ba", g["w_branch_a"][0].reshape(4, 128, 1024).transpose(1, 0, 2).reshape(128, 4096))
    put("out0", _piece_fm(g["w_out"][0][:, 0:512]))
    put("out1", _piece_fm(g["w_out"][0][:, 512:1024]))
    w_up = g["w_up"][0]
    fcw_full = g["ffn_conv_w"][0]
    fcb_full = g["ffn_conv_b"][0]
    chunk_cols = []
    for jj in range(11):
        cc = [np.arange((2 * jj + i) * 128, (2 * jj + i + 1) * 128) for i in range(2)]
        cc += [DFF + np.arange((2 * jj + i) * 128, (2 * jj + i + 1) * 128) for i in range(2)]
        idx = np.concatenate(cc)
        chunk_cols.append(idx)
        put("up%d" % jj, _piece_fm(w_up[:, idx]))
    allidx = np.concatenate(chunk_cols)
    fcw = fcw_full[:, allidx].reshape(3, 44, 128).transpose(2, 1, 0).reshape(128, 44 * 3)
    fcb = fcb_full[allidx].reshape(44, 128).T
    put("down", g["w_down"][0].reshape(NF, 128, 1024).transpose(1, 0, 2).reshape(128, NF * 1024))

    w_ada = g["w_ada"][0]
    wada = np.stack([_piece_fm(w_ada[:, p * 512:(p + 1) * 512]) for p in range(12)]).astype(f32)
    bada = g["b_ada"][0].astype(f32)
    mcw = g["m_conv_w"][0].reshape(4, 8, 128).transpose(2, 1, 0).reshape(128, 32)
    mcb = _fm(g["m_conv_b"][0], 8)
    gifb = np.concatenate([g["m_igate_b"][0], g["m_fgate_b"][0]]).astype(f32)
    gq = np.tile(g["a_qnorm_g"][0], 8).astype(f32)
    gk = np.tile(g["a_knorm_g"][0], 8).astype(f32)
    rel = g["rel_bias"].astype(f32)
    kk = np.arange(128)[:, None]
    qq = np.arange(128)[None, :]
    biasg = np.zeros((128, 4, 2, 2, 128), f32)
    maskn = np.zeros((128, 4, 2, 2, 128), f32)
    for dl in range(2):
        dist = qq - kk + 128 * dl
        bidx = _t5_bucket(dist)
        for h in range(4):
            t = rel[bidx, h]
            biasg[:, h, dl, 0, :] = t
            biasg[:, h, dl, 1, :] = t
        if dl == 0:
            mk = np.where(dist < 0, NEG, 0.0).astype(f32)
            maskn[:, :, 0, :, :] = mk[:, None, None, :]
    farb = rel[31, :].astype(f32)
    umask = (kk <= qq).astype(f32)
    negm4 = np.tile(np.where(kk <= qq, 0.0, NEG).astype(f32), (1, 4))
    import ml_dtypes
    common = dict(
        wada=wada, bada=bada, badafm=_fm(bada, 48), g1fm=_fm(g["norm1_g"][0], 8), g2fm=_fm(g["norm2_g"][0], 8),
        wall=wall, mcw=np.ascontiguousarray(mcw, f32), mcb=mcb, fcw=np.ascontiguousarray(fcw, f32), fcb=np.ascontiguousarray(fcb, f32),
        gifb=gifb, mng=g["m_norm_g"][0].astype(f32), ang=g["a_norm_g"][0].astype(f32), gq=gq, gk=gk,
        gqfm=np.tile(g["a_qnorm_g"][0], 2).reshape(128, 1).astype(f32), gkfm=np.tile(g["a_knorm_g"][0], 2).reshape(128, 1).astype(f32),
        alam=g["a_lambda"][0].reshape(256).astype(f32), biasg=biasg.reshape(128, -1), maskn=maskn.reshape(128, -1), farb=farb,
        identb=np.eye(128).astype(ml_dtypes.bfloat16), identf=np.eye(128, dtype=f32), umask=umask, negm4=negm4,
    )
    in_maps = []
    for b in range(B):
        cfm = _fm(g["c"][b], 8)
        for hf in range(2):
            if hf == 0:
                xl = np.concatenate([np.zeros((half_tok, D), f32), x[b, 0:half_tok]], axis=0)
                fl = np.zeros((128, 1), f32)
            else:
                xl = x[b]
                fl = np.ones((128, 1), f32)
            m = dict(common)
            m["x"] = np.ascontiguousarray(xl)
            m["cfm"] = cfm
            m["flag"] = fl
            in_maps.append(m)
    return in_maps


_NC_CACHE = {}


def run(NCTX, NFULL, inputs):
    key = (NCTX, NFULL)
    if key not in _NC_CACHE:
        _NC_CACHE[key] = build_program(NCTX, NFULL)
    nc = _NC_CACHE[key]
    in_maps = prepare_inputs(NCTX, NFULL, inputs)
    res = run_bass_kernel_spmd(nc, in_maps, core_ids=list(range(len(in_maps))))
    B = len(in_maps) // 2
    half_tok = ((NCTX + NFULL) // 2) * 512
    out = np.empty((B, 2 * half_tok, D), np.float32)
    for b in range(B):
        for hf in range(2):
            out[b, hf * half_tok:(hf + 1) * half_tok] = res.results[b * 2 + hf]["out"]
    return out


def kernel(**inputs):
    return run(7, 9, inputs)
```

```python
import math
from contextlib import ExitStack

import numpy as np

import concourse.bass as bass
import concourse.mybir as mybir
from concourse.bass_utils import run_bass_kernel_spmd

F32 = mybir.dt.float32
BF16 = mybir.dt.bfloat16
AF = mybir.ActivationFunctionType
ALU = mybir.AluOpType
AX = mybir.AxisListType

D = 1024
DC = 8
DFF = 2816
NF = 22
EPS = 1e-6
LAM_INIT = 0.8 - 0.6 * math.exp(-0.3 * 0)
NEG = -30000.0
OPT_SELF_WAR = True
OPT_XN_ACT = True
OPT_SILU = True
OPT_GFOLD = True
OPT_TINY = True
OPT_BATCH = True
OPT_PARTMAIN = True

PIECES = {}
_off = 0
for _nm, _n in (("mq", 4096), ("mk", 4096), ("mv", 4096), ("mo", 4096), ("aq", 4096), ("ak", 4096),
                ("av", 4096), ("gm0", 4096), ("gm1", 4096), ("ga0", 4096), ("ga1", 4096), ("gif", 64),
                ("bm", 4096), ("ba", 4096), ("out0", 4096), ("out1", 4096)):
    PIECES[_nm] = (_off, _n)
    _off += _n
for _j in range(11):
    PIECES["up%d" % _j] = (_off, 4096)
    _off += 4096
PIECES["down"] = (_off, NF * 1024)
_off += NF * 1024
WTOT = _off
WPAD = ((WTOT + 4095) // 4096) * 4096


class Buf:
    __slots__ = ("name", "w", "rs", "sem", "semv", "grp", "psum", "persist")

    def __init__(self, name, grp=None):
        self.psum = False
        self.persist = False
        self.name = name
        self.w = None
        self.rs = []
        self.sem = None
        self.semv = 0
        self.grp = grp


class Sched:
    def __init__(self, nc, stack):
        self.nc = nc
        self.stack = stack
        self.engs = {}
        for nm, h in (("pe", nc.tensor), ("act", nc.scalar), ("dve", nc.vector), ("pool", nc.gpsimd), ("sp", nc.sync)):
            sem = stack.enter_context(nc.semaphore("s_" + nm))
            self.engs[nm] = dict(h=h, sem=sem, cnt=0, seen={})
        self.groups = {}
        self.dma_ev = {}
        self.free_sems = {}
        self.live = []
        self.nsem = 0

    def buf(self, name, grp=None):
        return Buf(name, grp)

    def _need(self, e, deps):
        E = self.engs[e]
        best = {}
        for d in deps:
            if d is None:
                continue
            sem, val, en = d
            if en == e and e == "pe":
                continue
            k = id(sem)
            if k not in best or best[k][1] < val:
                best[k] = (sem, val)
        for k, (sem, val) in best.items():
            if E["seen"].get(k, 0) >= val:
                continue
            E["h"].wait_ge(sem, val)
            E["seen"][k] = val

    def op(self, e, fn, reads=(), writes=()):
        E = self.engs[e]
        deps = []
        for b in reads:
            deps.append(b.w)
            if b.psum:
                for r in b.rs:
                    if r[2] != e:
                        deps.append(r)
        for b in writes:
            if b.w is not None and b.w[2] != e:
                deps.append(b.w)
            for r in b.rs:
                if OPT_SELF_WAR or r[2] != e:
                    deps.append(r)
        self._need(e, deps)
        ins = fn(E["h"])
        E["cnt"] += 1
        ins.then_inc(E["sem"], 1)
        ev = (E["sem"], E["cnt"], e)
        for b in reads:
            b.rs.append(ev)
        for b in writes:
            b.w = ev
            b.rs = []
        return ins

    def dma(self, q, out, in_, reads=(), writes=()):
        E = self.engs[q]
        deps = []
        for b in reads:
            deps.append(b.w)
        for b in writes:
            if b.grp in ("wbw", "kvw", "outw"):
                continue
            deps.append(b.w)
            deps.extend(b.rs)
        self._need(q, deps)
        ins = E["h"].dma_start(out=out, in_=in_)
        cands = list(writes) + list(reads)
        tgt = ([b for b in cands if b.grp is None] + cands)[0]
        if tgt.grp is not None:
            gk = tgt.grp
            if gk not in self.groups:
                self.groups[gk] = [self.stack.enter_context(self.nc.semaphore("g_" + gk)), 0]
            g = self.groups[gk]
            g[1] += 16
            sem, val = g[0], g[1]
        else:
            if tgt.sem is None:
                tgt.sem = {}
                self.live.append(tgt)
            if q not in tgt.sem:
                fs = self.free_sems.setdefault(q, [])
                if not fs:
                    self.nsem += 1
                    fs.append([self.stack.enter_context(self.nc.semaphore("dp%d" % self.nsem)), 0])
                tgt.sem[q] = fs.pop()
            ent = tgt.sem[q]
            ent[1] += 16
            sem, val = ent[0], ent[1]
        ins.then_inc(sem, 16)
        self.dma_ev[id(sem)] = (sem, val)
        ev = (sem, val, "dma")
        for b in reads:
            b.rs.append(ev)
        for b in writes:
            b.w = ev
            b.rs = []
        return ins

    def barrier(self, engines=("pe", "act", "dve", "pool", "sp")):
        deps = [(E["sem"], E["cnt"], "x") for E in self.engs.values() if E["cnt"] > 0]
        deps += [(s, v, "dma") for (s, v) in self.dma_ev.values()]
        for e in engines:
            self._need(e, deps)
        self.dma_ev = {}
        keep = []
        for b in self.live:
            if b.persist:
                keep.append(b)
                continue
            for qq, ent in b.sem.items():
                self.free_sems[qq].append(ent)
            b.sem = None
        self.live = keep

    def seal(self, bufs, grp):
        g = self.groups[grp]
        for b in bufs:
            b.w = (g[0], g[1], "dma")


def build_program(NCTX, NFULL, debug=False):
    NU = NCTX + NFULL
    NFLAG = NU // 2
    NBLK = NU * 4
    NTOK = NU * 512
    NOWN = (NFULL - 1) * 512
    assert NU == 2 * (NFULL - 1)

    nc = bass.Bass("TRN2", target_bir_lowering=False)

    def din(name, shape, dt=F32):
        return nc.dram_tensor(name, list(shape), dt, kind="ExternalInput").ap()

    x_d = din("x", [NTOK, D])
    cfm_d = din("cfm", [128, 8])
    wada_d = din("wada", [12, 128, 8 * 512])
    bada_d = din("bada", [6144])
    badafm_d = din("badafm", [128, 48])
    g1fm_d = din("g1fm", [128, 8])
    g2fm_d = din("g2fm", [128, 8])
    wall_d = din("wall", [128, WPAD])
    mcw_d = din("mcw", [128, 8 * 4])
    mcb_d = din("mcb", [128, 8])
    fcw_d = din("fcw", [128, 44 * 3])
    fcb_d = din("fcb", [128, 44])
    gifb_d = din("gifb", [8])
    mng_d = din("mng", [512])
    ang_d = din("ang", [512])
    gq_d = din("gq", [512])
    gk_d = din("gk", [512])
    gqfm_d = din("gqfm", [128, 1])
    gkfm_d = din("gkfm", [128, 1])
    alam_d = din("alam", [256])
    biasg_d = din("biasg", [128, 4 * 2 * 2 * 128])
    maskn_d = din("maskn", [128, 4 * 2 * 2 * 128])
    farb_d = din("farb", [4])
    flag_d = din("flag", [128, 1])
    identb_d = din("identb", [128, 128], BF16)
    identf_d = din("identf", [128, 128])
    umask_d = din("umask", [128, 128])
    negm4_d = din("negm4", [128, 512])
    out_d = nc.dram_tensor("out", [NOWN, D], F32, kind="ExternalOutput").ap()
    wb_d = nc.dram_tensor("wb_scr", [128, WPAD], BF16, kind="Internal").ap()
    kt_d = nc.dram_tensor("kt_scr", [4, 128, NBLK * 128], BF16, kind="Internal").ap()
    va_d = nc.dram_tensor("va_scr", [4, 128, NBLK * 130], BF16, kind="Internal").ap()

    with ExitStack() as top:
        S = Sched(nc, top)

        uid = [0]

        def sbt(st, name, shape, dt=F32):
            uid[0] += 1
            return st.enter_context(nc.sbuf_tensor("s%d_%s" % (uid[0], name), list(shape), dt))

        banks = [top.enter_context(nc.psum_tensor("bank%d" % i, [128, 512], F32)) for i in range(8)]
        bbufs = [S.buf("bank%d" % i) for i in range(8)]
        for b_ in bbufs:
            b_.psum = True
        bank_rr = [0]

        def nbank(lo=0, hi=8):
            i = lo + (bank_rr[0] % (hi - lo))
            bank_rr[0] += 1
            return banks[i], bbufs[i]

        def bfview(bank):
            return bank[:].bitcast(BF16)

        identb = sbt(top, "identb", [128, 128], BF16)
        identf = sbt(top, "identf", [128, 128])
        umask = sbt(top, "umask", [128, 128])
        negm4 = sbt(top, "negm4", [128, 512])
        ones_f = sbt(top, "ones_f", [128, 128])
        flag = sbt(top, "flag", [128, 1])
        epst = sbt(top, "epst", [128, 1])
        onet = sbt(top, "onet", [128, 1])
        mult1 = sbt(top, "mult1", [128, 8])
        shift1 = sbt(top, "shift1", [128, 8])
        mult2 = sbt(top, "mult2", [128, 8])
        shift2 = sbt(top, "shift2", [128, 8])
        gate1 = sbt(top, "gate1", [128, D])
        gate2 = sbt(top, "gate2", [128, D])
        mcw = sbt(top, "mcw", [128, 8, 4])
        mcb = sbt(top, "mcb", [128, 8])
        mcwf = sbt(top, "mcwf", [128, 8, 4])
        fcwf = sbt(top, "fcwf", [128, 44, 3])
        mcnb = sbt(top, "mcnb", [128, 8])
        fcw = sbt(top, "fcw", [128, 44, 3])
        fcb = sbt(top, "fcb", [128, 44])
        fcnb = sbt(top, "fcnb", [128, 44])
        gifb = sbt(top, "gifb", [128, 8])
        mng = sbt(top, "mng", [128, 512])
        ang = sbt(top, "ang", [128, 512])
        gq = sbt(top, "gq", [128, 512])
        gk = sbt(top, "gk", [128, 512])
        gqfm = sbt(top, "gqfm", [128, 1])
        gkfm = sbt(top, "gkfm", [128, 1])
        mbc = sbt(top, "mbc", [128, 8, 3])
        fbc = sbt(top, "fbc", [128, 44, 2])
        ctmp = sbt(top, "ctmp", [128, 44])
        b_mbc = S.buf("mbc")
        b_fbc = S.buf("fbc")
        biasb = sbt(top, "biasb", [128, 4, 2, 256], BF16)
        farb = sbt(top, "farb", [128, 4])
        neglam = sbt(top, "neglam", [128, 1])
        Sst = sbt(top, "Sst", [128, 4, 130])
        Sstb = sbt(top, "Sstb", [128, 4, 130], BF16)
        mhalo = sbt(top, "mhalo", [128, 8, 3])
        fhalo = sbt(top, "fhalo", [128, 44, 2])
        CONST = S.buf("const")
        b_S = S.buf("Sst")
        b_Sb = S.buf("Sstb")
        b_mhalo = S.buf("mhalo")
        b_fhalo = S.buf("fhalo")
        b_wb = S.buf("wb_scr", grp="wbw")
        b_ktd = S.buf("kt_scr", grp="kvw")
        b_vad = S.buf("va_scr", grp="kvw")
        b_out = S.buf("outd", grp="outw")

        def ld_const(t, src, grp="cst"):
            b = S.buf("c_" + t.name, grp=grp)
            S.dma("sp", t[:], src, writes=[b])
            return b

        cb = []
        cb.append(ld_const(identb, identb_d[:, :]))
        cb.append(ld_const(identf, identf_d[:, :]))
        cb.append(ld_const(umask, umask_d[:, :]))
        cb.append(ld_const(negm4, negm4_d[:, :]))
        cb.append(ld_const(flag, flag_d[:, :]))
        cb.append(ld_const(mcw, mcw_d.rearrange("p (c k) -> p c k", k=4)))
        cb.append(ld_const(mcb, mcb_d[:, :]))
        cb.append(ld_const(fcw, fcw_d.rearrange("p (c k) -> p c k", k=3)))
        cb.append(ld_const(fcb, fcb_d[:, :]))
        cb.append(ld_const(gifb, gifb_d.partition_broadcast(128)))
        cb.append(ld_const(mng, mng_d.partition_broadcast(128)))
        cb.append(ld_const(ang, ang_d.partition_broadcast(128)))
        cb.append(ld_const(gq, gq_d.partition_broadcast(128)))
        cb.append(ld_const(gk, gk_d.partition_broadcast(128)))
        cb.append(ld_const(farb, farb_d.partition_broadcast(128)))
        cb.append(ld_const(gqfm, gqfm_d[:, :]))
        cb.append(ld_const(gkfm, gkfm_d[:, :]))
        S.seal(cb, "cst")
        S.op("dve", lambda e: e.memset(ones_f[:], 1.0), writes=[CONST])
        S.op("dve", lambda e: e.memset(epst[:], EPS), writes=[CONST])
        S.op("dve", lambda e: e.memset(onet[:], 1.0), writes=[CONST])
        S.op("dve", lambda e: e.memset(Sst[:], 0.0), writes=[b_S])
        S.op("dve", lambda e: e.memset(Sstb[:], 0.0), writes=[b_Sb])
        S.op("dve", lambda e: e.memset(mhalo[:], 0.0), writes=[b_mhalo])
        S.op("dve", lambda e: e.memset(fhalo[:], 0.0), writes=[b_fhalo])
        S.op("dve", lambda e: e.tensor_scalar(out=mcnb[:], in0=mcb[:], scalar1=-1.0, scalar2=None, op0=ALU.mult), reads=cb, writes=[CONST])
        S.op("dve", lambda e: e.tensor_scalar(out=fcnb[:], in0=fcb[:], scalar1=-1.0, scalar2=None, op0=ALU.mult), reads=cb, writes=[CONST])
        S.op("dve", lambda e: e.tensor_scalar(out=gqfm[:], in0=gqfm[:], scalar1=0.125, scalar2=None, op0=ALU.mult), reads=cb, writes=[CONST])
        S.op("dve", lambda e: e.tensor_scalar(out=mcwf[:], in0=mcw[:], scalar1=flag[:, 0:1], scalar2=None, op0=ALU.mult), reads=cb, writes=[CONST])
        S.op("dve", lambda e: e.tensor_scalar(out=fcwf[:], in0=fcw[:], scalar1=flag[:, 0:1], scalar2=None, op0=ALU.mult), reads=cb, writes=[CONST])
        S.op("dve", lambda e: e.tensor_scalar(out=ang[:], in0=ang[:], scalar1=1.0 - LAM_INIT, scalar2=None, op0=ALU.mult), reads=cb, writes=[CONST])

        with ExitStack() as st:
            cfm = sbt(st, "cfm", [128, 8])
            sc = sbt(st, "sc", [128, 8])
            sct = sbt(st, "sct", [128, 8])
            scbc = sbt(st, "scbc", [128, 8, 128])
            badafm = sbt(st, "badafm", [128, 48])
            g1fm = sbt(st, "g1fm", [128, 8])
            g2fm = sbt(st, "g2fm", [128, 8])
            modfm = sbt(st, "modfm", [128, 48])
            alam = sbt(st, "alam", [128, 256])
            lt = sbt(st, "lt", [128, 128])
            ls = sbt(st, "ls", [128, 2])
            biasg = sbt(st, "biasg", [128, 2048])
            maskn = sbt(st, "maskn", [128, 2048])
            wst = [sbt(st, "wst%d" % i, [128, 4096]) for i in range(2)]
            wbo = [sbt(st, "wbo%d" % i, [128, 4096], BF16) for i in range(2)]
            bbt = [sbt(st, "bbt%d" % i, [128, 512]) for i in range(2)]
            b_wst = [S.buf("wst%d" % i) for i in range(2)]
            b_wbo = [S.buf("wbo%d" % i) for i in range(2)]
            b_bbt = [S.buf("bbt%d" % i) for i in range(2)]
            L = S.buf("prel")
            b_l = [ld_const(cfm, cfm_d[:, :], "cst2"), ld_const(badafm, badafm_d[:, :], "cst2"), ld_const(g1fm, g1fm_d[:, :], "cst2"),
                   ld_const(g2fm, g2fm_d[:, :], "cst2"), ld_const(alam, alam_d.partition_broadcast(128), "cst2"),
                   ld_const(biasg, biasg_d[:, :], "cst2"), ld_const(maskn, maskn_d[:, :], "cst2")]
            S.seal(b_l, "cst2")
            S.op("act", lambda e: e.activation(out=sct[:], in_=cfm[:], func=AF.Exp, scale=-1.0), reads=b_l, writes=[L])
            S.op("dve", lambda e: e.tensor_scalar(out=sct[:], in0=sct[:], scalar1=1.0, scalar2=None, op0=ALU.add), reads=[L], writes=[L])
            S.op("dve", lambda e: e.reciprocal(out=sct[:], in_=sct[:]), reads=[L], writes=[L])
            S.op("dve", lambda e: e.tensor_tensor(out=sc[:], in0=cfm[:], in1=sct[:], op=ALU.mult), reads=[L], writes=[L])
            S.op("dve", lambda e: e.tensor_copy(out=scbc[:], in_=sc[:].unsqueeze(2).to_broadcast([128, 8, 128])), reads=[L], writes=[L])
            S.op("dve", lambda e: e.tensor_tensor(out=lt[:, 0:64], in0=alam[:, 0:64], in1=alam[:, 64:128], op=ALU.mult), reads=b_l, writes=[L])
            S.op("dve", lambda e: e.tensor_tensor(out=lt[:, 64:128], in0=alam[:, 128:192], in1=alam[:, 192:256], op=ALU.mult), reads=[L], writes=[L])
            S.op("dve", lambda e: e.tensor_reduce(out=ls[:], in_=lt[:].rearrange("p (a b) -> p a b", a=2), axis=AX.X, op=ALU.add), reads=[L], writes=[L])
            S.op("act", lambda e: e.activation(out=ls[:], in_=ls[:], func=AF.Exp), reads=[L], writes=[L])
            S.op("dve", lambda e: e.tensor_tensor(out=neglam[:], in0=ls[:, 1:2], in1=ls[:, 0:1], op=ALU.subtract), reads=[L], writes=[CONST])
            S.op("dve", lambda e: e.tensor_scalar(out=neglam[:], in0=neglam[:], scalar1=-LAM_INIT, scalar2=None, op0=ALU.add), reads=[CONST], writes=[CONST])
            S.op("dve", lambda e: e.tensor_tensor(out=biasb[:].rearrange("p a b c -> p (a b c)"), in0=biasg[:], in1=maskn[:], op=ALU.add), reads=b_l, writes=[CONST])

            fm_ps, fm_b = banks[7], bbufs[7]
            fm_cols = {0: 0, 1: 4, 2: 8, 3: 12, 6: 24, 7: 28, 8: 32, 9: 36}
            for pc in range(12):
                i = pc % 2
                S.dma("sp", wst[i][:], wada_d[pc, :, :], writes=[b_wst[i]])
                wv = wst[i][:].rearrange("p (c n) -> p c n", c=8)
                if pc in (4, 5, 10, 11):
                    S.dma("sp", bbt[i][:], bada_d[pc * 512:(pc + 1) * 512].partition_broadcast(128), writes=[b_bbt[i]])
                    pb, pbb = nbank(0, 6)
                    for dc in range(8):
                        S.op("pe", lambda e: e.matmul(pb[:], lhsT=scbc[:, dc, :], rhs=wv[:, dc, :], start=(dc == 0), stop=(dc == 7)),
                             reads=[L, b_wst[i]], writes=[pbb])
                    gt = gate1 if pc < 6 else gate2
                    off = (pc % 2) * 512
                    S.op("dve", lambda e: e.tensor_tensor(out=gt[:, off:off + 512], in0=pb[:], in1=bbt[i][:], op=ALU.add),
                         reads=[pbb, b_bbt[i]], writes=[CONST])
                else:
                    for k in range(4):
                        col = fm_cols[pc] + k
                        for dc in range(8):
                            S.op("pe", lambda e: e.matmul(fm_ps[:, col:col + 1], lhsT=wv[:, dc, k * 128:(k + 1) * 128], rhs=sc[:, dc:dc + 1],
                                                         start=(dc == 0), stop=(dc == 7)), reads=[L, b_wst[i]], writes=[fm_b])
            S.op("dve", lambda e: e.tensor_tensor(out=modfm[:, 0:16], in0=fm_ps[:, 0:16], in1=badafm[:, 0:16], op=ALU.add), reads=[fm_b] + b_l, writes=[L])
            S.op("dve", lambda e: e.tensor_tensor(out=modfm[:, 24:40], in0=fm_ps[:, 24:40], in1=badafm[:, 24:40], op=ALU.add), reads=[fm_b] + b_l, writes=[L])
            S.op("dve", lambda e: e.scalar_tensor_tensor(out=mult1[:], in0=modfm[:, 8:16], scalar=1.0, in1=g1fm[:], op0=ALU.add, op1=ALU.mult), reads=[L], writes=[CONST])
            S.op("dve", lambda e: e.tensor_copy(out=shift1[:], in_=modfm[:, 0:8]), reads=[L], writes=[CONST])
            S.op("dve", lambda e: e.scalar_tensor_tensor(out=mult2[:], in0=modfm[:, 32:40], scalar=1.0, in1=g2fm[:], op0=ALU.add, op1=ALU.mult), reads=[L], writes=[CONST])
            S.op("dve", lambda e: e.tensor_copy(out=shift2[:], in_=modfm[:, 24:32]), reads=[L], writes=[CONST])

            cast_eng = ["dve", "act"]
            for ch in range(WPAD // 4096):
                i = ch % 2
                S.dma("sp", wst[i][:], wall_d[:, ch * 4096:(ch + 1) * 4096], writes=[b_wst[i]])
                ce = cast_eng[ch % 2]
                if ce == "act":
                    S.op("act", lambda e: e.copy(out=wbo[i][:], in_=wst[i][:]), reads=[b_wst[i]], writes=[b_wbo[i]])
                else:
                    S.op(ce, lambda e: e.tensor_copy(out=wbo[i][:], in_=wst[i][:]), reads=[b_wst[i]], writes=[b_wbo[i]])
                S.dma("pool", wb_d[:, ch * 4096:(ch + 1) * 4096], wbo[i][:], reads=[b_wbo[i]], writes=[b_wb])
            S.barrier()

        njunk = sbt(top, "njunk", [128, D], BF16)
        ncache = {}
        pf = [sbt(top, "pf%d" % i, [128, 4096], BF16) for i in range(4)]
        b_pf = [S.buf("pf%d" % i) for i in range(4)]
        for b_ in b_pf:
            b_.persist = True
        prefetched = {}

        def prefetch(slots, pieces):
            for sl, piece in zip(slots, pieces):
                off, n = PIECES[piece]
                S.dma("sp", pf[sl][:, 0:n], wb_d[:, off:off + n], reads=[b_wb], writes=[b_pf[sl]])
                prefetched[piece] = (pf[sl], b_pf[sl])

        def wload(st, name, piece, shape3=None):
            if piece in prefetched:
                return prefetched.pop(piece)
            off, n = PIECES[piece]
            t = sbt(st, name, [128, n], BF16)
            b = S.buf(name)
            S.dma("sp", t[:], wb_d[:, off:off + n], reads=[b_wb], writes=[b])
            return t, b

        for u in range(NU):
            full = u >= NCTX
            flagged = u < NFLAG
            own = u >= NFLAG
            with ExitStack() as su:
                xs = sbt(su, "xs", [128, 4, D])
                hT = sbt(su, "hT", [128, 8, 512], BF16)
                sa = su.enter_context(ExitStack())
                mqT = sbt(sa, "mqT", [128, 8, 512], BF16)
                mVA = sbt(sa, "mVA", [128, 4, 4, 130], BF16)
                sigmo = sbt(sa, "sigmo", [128, 4, 512])
                gif = sbt(sa, "gif", [128, 4, 8])
                Qbd = sbt(sa, "Qbd", [128, 4, 4, 256], BF16)
                hmT = sbt(sa, "hmT", [128, 4, 512], BF16)
                haT = sbt(sa, "haT", [128, 4, 512], BF16)
                b_xs = [S.buf("xs%d" % j) for j in range(4)]
                b_hT = S.buf("hT")
                b_hT2 = S.buf("hT2")
                b_mqT = S.buf("mqT")
                b_mVA = S.buf("mVA")
                b_sig = S.buf("sigmo")
                b_gif = S.buf("gif")
                b_Qbd = S.buf("Qbd")
                b_hmT = S.buf("hmT")
                b_haT = S.buf("haT")
                if full:
                    S.op("dve", lambda e: e.memset(Qbd[:], 0.0), writes=[b_Qbd])

                def norm_block(st, j, mult, shift, hdst, b_hdst, tagp):
                    junk = njunk
                    if not hasattr(st, "ncache"):
                        st.ncache = {}
                    ck = (tagp, j % 2)
                    if ck not in st.ncache:
                        st.ncache[ck] = (sbt(st, tagp + "xn%d" % (j % 2), [128, D], BF16), S.buf("xn"))
                    xn, b_xn = st.ncache[ck]
                    ss = sbt(st, tagp + "ss%d" % j, [128, 2])
                    bl = S.buf("nb")
                    S.op("act", lambda e: e.activation(out=junk[:], in_=xs[:, j, :], func=AF.Square, scale=1.0 / 32.0, accum_out=ss[:, 0:1]),
                         reads=[b_xs[j]], writes=[bl])
                    S.op("act", lambda e: e.activation(out=ss[:, 1:2], in_=ss[:, 0:1], func=AF.Ln, bias=epst[:, 0:1], scale=1.0), reads=[bl, CONST], writes=[bl])
                    S.op("act", lambda e: e.activation(out=ss[:, 1:2], in_=ss[:, 1:2], func=AF.Exp, scale=-0.5), reads=[bl], writes=[bl])
                    S.op("dve", lambda e: e.tensor_scalar(out=xn[:], in0=xs[:, j, :], scalar1=ss[:, 1:2], scalar2=None, op0=ALU.mult),
                         reads=[bl, b_xs[j]], writes=[b_xn])
                    for half in range(2):
                        pb, pbb = nbank()
                        pv = bfview(pb)[:, 0:512].rearrange("p (c t) -> p c t", c=4)
                        for cc in range(4):
                            c = half * 4 + cc
                            S.op("pe", lambda e: e.transpose(out=pv[:, cc, :], in_=xn[:, c * 128:(c + 1) * 128], identity=identb[:]),
                                 reads=[b_xn, cb[0]], writes=[pbb])
                        for cc in range(4):
                            c = half * 4 + cc
                            if half == 0:
                                S.op("act", lambda e: e.activation(out=hdst[:, c, j * 128:(j + 1) * 128], in_=pv[:, cc, :], func=AF.Identity,
                                                                  scale=mult[:, c:c + 1], bias=shift[:, c:c + 1]), reads=[pbb, CONST], writes=[b_hdst[0]])
                            else:
                                S.op("dve", lambda e: e.tensor_scalar(out=hdst[:, c, j * 128:(j + 1) * 128], in0=pv[:, cc, :], scalar1=mult[:, c:c + 1],
                                                                     scalar2=shift[:, c:c + 1], op0=ALU.mult, op1=ALU.add), reads=[pbb, CONST], writes=[b_hdst[1]])

                with ExitStack() as st:
                    names = (["mq", "mk", "mv", "gif", "av", "mo", "aq", "ak"] if full else ["mk", "mv", "gif", "av", "ak"])
                    W = {}
                    for j in range(4):
                        blk = u * 4 + j
                        S.dma("sp", xs[:, j, :], x_d[blk * 128:(blk + 1) * 128, :], writes=[b_xs[j]])
                    for nm in names:
                        W[nm] = wload(st, "w_" + nm, nm)
                    if not full and u + 1 < NU:
                        nxt_full = (u + 1) >= NCTX
                        slots = (2, 3) if (nxt_full or (u + 1) % 2 == 1) else (0, 1)
                        if nxt_full:
                            slots = (2, 3)
                        prefetch(slots, ["mq", "mk"] if nxt_full else ["mk", "mv"])
                    for j in range(4):
                        norm_block(st, j, mult1, shift1, hT, (b_hT, b_hT2), "n1")
                    if flagged:
                        S.op("dve", lambda e: e.tensor_copy(out=mVA[:, :, :, 128:130], in_=flag[:, 0:1].unsqueeze(1).unsqueeze(1).to_broadcast([128, 4, 4, 2])),
                             reads=[cb[4]], writes=[b_mVA])
                    else:
                        S.op("dve", lambda e: e.memset(mVA[:, :, :, 128:130], 1.0), writes=[b_mVA])
                    acc = [sbt(st, "acc%d" % i, [128, 512]) for i in range(4)]
                    et = [sbt(st, "et%d" % i, [128, 512]) for i in range(4)]
                    b_acc = [S.buf("acc%d" % i) for i in range(4)]
                    b_et = [S.buf("et%d" % i) for i in range(4)]
                    hv = mhalo
                    if not OPT_BATCH:
                        S.op("dve", lambda e: e.memset(mbc[:], 0.0), writes=[b_mbc])
                    if OPT_BATCH:
                        S.op("dve", lambda e: e.tensor_tensor(out=mbc[:, :, 2], in0=hv[:, :, 2], in1=mcw[:, :, 0], op=ALU.mult), reads=[b_mhalo] + cb, writes=[b_mbc])
                        S.op("dve", lambda e: e.tensor_tensor(out=mbc[:, :, 1], in0=hv[:, :, 2], in1=mcw[:, :, 1], op=ALU.mult), reads=[b_mhalo] + cb, writes=[b_mbc])
                        S.op("dve", lambda e: e.tensor_tensor(out=mbc[:, :, 0], in0=hv[:, :, 2], in1=mcw[:, :, 2], op=ALU.mult), reads=[b_mhalo] + cb, writes=[b_mbc])
                        S.op("dve", lambda e: e.tensor_tensor(out=ctmp[:, 0:8], in0=hv[:, :, 1], in1=mcw[:, :, 0], op=ALU.mult), reads=[b_mhalo] + cb, writes=[b_mbc])
                        S.op("dve", lambda e: e.tensor_tensor(out=mbc[:, :, 1], in0=mbc[:, :, 1], in1=ctmp[:, 0:8], op=ALU.add), reads=[b_mbc], writes=[b_mbc])
                        S.op("dve", lambda e: e.tensor_tensor(out=ctmp[:, 8:16], in0=hv[:, :, 1], in1=mcw[:, :, 1], op=ALU.mult), reads=[b_mhalo] + cb, writes=[b_mbc])
                        S.op("dve", lambda e: e.tensor_tensor(out=mbc[:, :, 0], in0=mbc[:, :, 0], in1=ctmp[:, 8:16], op=ALU.add), reads=[b_mbc], writes=[b_mbc])
                        S.op("dve", lambda e: e.tensor_tensor(out=ctmp[:, 16:24], in0=hv[:, :, 0], in1=mcw[:, :, 0], op=ALU.mult), reads=[b_mhalo] + cb, writes=[b_mbc])
                        S.op("dve", lambda e: e.tensor_tensor(out=mbc[:, :, 0], in0=mbc[:, :, 0], in1=ctmp[:, 16:24], op=ALU.add), reads=[b_mbc], writes=[b_mbc])
                        S.op("dve", lambda e: e.tensor_tensor(out=mbc[:], in0=mbc[:], in1=mcb[:].unsqueeze(2).to_broadcast([128, 8, 3]), op=ALU.add), reads=[b_mbc] + cb, writes=[b_mbc])
                    groups = ([[0, 1, 2, 3], [4, 5, 6, 7]] if full else [[4, 5, 6, 7]])
                    wsel = mcwf if flagged else mcw
                    for grp in groups:
                        pbs = []
                        for k, c in enumerate(grp):
                            wt, wbf = W["mq" if c < 4 else "mk"]
                            wv = wt[:].rearrange("p (c n) -> p c n", c=8)
                            pb, pbb = nbank()
                            pbs.append((pb, pbb))
                            for dc in range(8):
                                S.op("pe", lambda e: e.matmul(pb[:], lhsT=wv[:, dc, k * 128:(k + 1) * 128], rhs=hT[:, dc, :], start=(dc == 0), stop=(dc == 7)),
                                     reads=[wbf, b_hT, b_hT2], writes=[pbb])
                            m0 = 3 if OPT_PARTMAIN else 0
                            S.op("act", lambda e: e.activation(out=acc[k][:, m0:512], in_=pb[:, m0:512], func=AF.Identity, scale=wsel[:, c, 3:4], bias=mcb[:, c:c + 1]),
                                 reads=[pbb, CONST] + cb, writes=[b_acc[k]])
                            for col in (range(3) if OPT_TINY else ()):
                                S.op("act", lambda e: e.activation(out=acc[k][:, col:col + 1], in_=pb[:, col:col + 1], func=AF.Identity, scale=wsel[:, c, 3:4],
                                                                  bias=mbc[:, c, col:col + 1]), reads=[pbb, CONST, b_mbc] + cb, writes=[b_acc[k]])
                            if flagged:
                                S.op("act", lambda e: e.activation(out=mhalo[:, c, :], in_=pb[:, 509:512], func=AF.Copy, scale=flag[:, 0:1]), reads=[pbb, cb[4], b_mbc], writes=[b_mhalo])
                            else:
                                S.op("act", lambda e: e.copy(out=mhalo[:, c, :], in_=pb[:, 509:512]), reads=[pbb, b_mbc], writes=[b_mhalo])
                        for tp in (2, 1, 0):
                            sh = 3 - tp
                            for k, c in enumerate(grp):
                                pb, pbb = pbs[k]
                                S.op("dve", lambda e: e.scalar_tensor_tensor(out=acc[k][:, sh:512], in0=pb[:, 0:512 - sh], scalar=wsel[:, c, tp:tp + 1], in1=acc[k][:, sh:512],
                                                                            op0=ALU.mult, op1=ALU.add), reads=[pbb, b_acc[k], CONST], writes=[b_acc[k]])
                        for k, c in enumerate(grp):
                            if not OPT_SILU:
                                S.op("act", lambda e: e.activation(out=et[k][:], in_=acc[k][:], func=AF.Exp, scale=-1.0), reads=[b_acc[k]], writes=[b_et[k]])
                                sk = (128.0 ** 0.5) if c < 4 else 1.0
                                S.op("dve", lambda e: e.tensor_scalar(out=et[k][:], in0=et[k][:], scalar1=1.0, scalar2=sk, op0=ALU.add, op1=ALU.mult), reads=[b_et[k]], writes=[b_et[k]])
                                S.op("dve", lambda e: e.reciprocal(out=et[k][:], in_=et[k][:]), reads=[b_et[k]], writes=[b_et[k]])
                                S.op("dve", lambda e: e.tensor_tensor(out=mqT[:, c, :], in0=acc[k][:], in1=et[k][:], op=ALU.mult), reads=[b_acc[k], b_et[k]], writes=[b_mqT])
                            elif c < 4:
                                S.op("act", lambda e: e.activation(out=et[k][:], in_=acc[k][:], func=AF.Silu), reads=[b_acc[k]], writes=[b_et[k]])
                                S.op("dve", lambda e: e.tensor_scalar(out=mqT[:, c, :], in0=et[k][:], scalar1=128.0 ** -0.5, scalar2=None, op0=ALU.mult),
                                     reads=[b_et[k]], writes=[b_mqT])
                            else:
                                S.op("act", lambda e: e.activation(out=mqT[:, c, :], in_=acc[k][:], func=AF.Silu), reads=[b_acc[k]], writes=[b_mqT])
                    sq = [sbt(st, "sq%d" % i, [128, 512]) for i in range(4)]
                    qb = [sbt(st, "qb%d" % i, [128, 512], BF16) for i in range(4)]
                    rs = [sbt(st, "rs%d" % i, [128, 16]) for i in range(4)]
                    KTb = [sbt(st, "KTb%d" % i, [128, 4, 128], BF16) for i in range(4)]
                    VAb = [sbt(st, "VAb%d" % i, [128, 4, 130], BF16) for i in range(4)]
                    b_t = [S.buf("tmj%d" % i) for i in range(4)]
                    b_KTb = [S.buf("KTb%d" % i) for i in range(4)]
                    b_VAb = [S.buf("VAb%d" % i) for i in range(4)]
                    for i in range(4):
                        if flagged:
                            S.op("dve", lambda e: e.tensor_copy(out=VAb[i][:, :, 128:130], in_=flag[:, 0:1].unsqueeze(1).to_broadcast([128, 4, 2])),
                                 reads=[cb[4]], writes=[b_VAb[i]])
                        else:
                            S.op("dve", lambda e: e.memset(VAb[i][:, :, 128:130], 1.0), writes=[b_VAb[i]])
                    def tm_block(j):
                        blk = u * 4 + j
                        i = j
                        tsl = slice(j * 128, (j + 1) * 128)
                        bsel = [0]

                        def bk():
                            bi = 2 * j + (bsel[0] % 2)
                            bsel[0] += 1
                            return banks[bi], bbufs[bi]

                        def proj(nm):
                            wt, wbf = W[nm]
                            n = PIECES[nm][1] // 8
                            wv = wt[:].rearrange("p (c n) -> p c n", c=8)
                            pb, pbb = bk()
                            for dc in range(8):
                                S.op("pe", lambda e: e.matmul(pb[:, 0:n], lhsT=hT[:, dc, tsl], rhs=wv[:, dc, :], start=(dc == 0), stop=(dc == 7)),
                                     reads=[wbf, b_hT, b_hT2], writes=[pbb])
                            return pb, pbb

                        def evac_scaled(out_ap, in_ap, rd, wr):
                            if flagged:
                                S.op("act", lambda e: e.activation(out=out_ap, in_=in_ap, func=AF.Copy, scale=flag[:, 0:1]), reads=rd + [cb[4]], writes=wr)
                            else:
                                S.op("act", lambda e: e.copy(out=out_ap, in_=in_ap), reads=rd, writes=wr)

                        pb, pbb = proj("mv")
                        yield
                        evac_scaled(mVA[:, j, :, 0:128], pb[:].rearrange("p (h d) -> p h d", h=4), [pbb], [b_mVA])
                        pb, pbb = proj("gif")
                        yield
                        S.op("dve", lambda e: e.tensor_tensor(out=gif[:, j, :], in0=pb[:, 0:8], in1=gifb[:], op=ALU.add), reads=[pbb, cb[9]], writes=[b_gif])
                        pb, pbb = proj("av")
                        yield
                        evac_scaled(VAb[i][:, :, 0:128], pb[:].rearrange("p (h d) -> p h d", h=4), [pbb], [b_VAb[i]])
                        S.dma("pool", va_d.rearrange("h p (b c) -> p h b c", c=130)[:, :, blk, :], VAb[i][:], reads=[b_VAb[i]], writes=[b_vad])
                        if full:
                            pb, pbb = proj("mo")
                            yield
                            S.op("act", lambda e: e.activation(out=sigmo[:, j, :], in_=pb[:], func=AF.Exp, scale=-1.0), reads=[pbb], writes=[b_sigj[j]])
                            yield
                            S.op("dve", lambda e: e.tensor_scalar(out=sigmo[:, j, :], in0=sigmo[:, j, :], scalar1=1.0, scalar2=None, op0=ALU.add), reads=[b_sigj[j]], writes=[b_sigj[j]])
                            S.op("dve", lambda e: e.reciprocal(out=sigmo[:, j, :], in_=sigmo[:, j, :]), reads=[b_sigj[j]], writes=[b_sigj[j], b_sig])
                        for nm in (["aq", "ak"] if full else ["ak"]):
                            pb, pbb = proj(nm)
                            yield
                            S.op("act", lambda e: e.activation(out=sq[i][:], in_=pb[:], func=AF.Square, scale=0.125), reads=[pbb], writes=[b_t[i]])
                            yield
                            S.op("dve", lambda e: e.tensor_reduce(out=rs[i][:, 0:8], in_=sq[i][:].rearrange("p (g d) -> p g d", d=64), axis=AX.X, op=ALU.add),
                                 reads=[b_t[i]], writes=[b_t[i]])
                            yield
                            S.op("act", lambda e: e.activation(out=rs[i][:, 8:16], in_=rs[i][:, 0:8], func=AF.Ln, bias=epst[:, 0:1], scale=1.0), reads=[b_t[i], CONST], writes=[b_t[i]])
                            S.op("act", lambda e: e.activation(out=rs[i][:, 8:16], in_=rs[i][:, 8:16], func=AF.Exp, scale=-0.5), reads=[b_t[i]], writes=[b_t[i]])
                            yield
                            S.op("dve", lambda e: e.tensor_tensor(out=qb[i][:].rearrange("p (g d) -> p g d", d=64), in0=pb[:].rearrange("p (g d) -> p g d", d=64),
                                                                 in1=rs[i][:, 8:16].unsqueeze(2).to_broadcast([128, 8, 64]), op=ALU.mult), reads=[pbb, b_t[i]], writes=[b_t[i]])
                            yield
                            tb, tbb = bk()
                            tv = bfview(tb)[:, 0:512].rearrange("p (h t) -> p h t", h=4)
                            for h in range(4):
                                S.op("pe", lambda e: e.transpose(out=tv[:, h, :], in_=qb[i][:, h * 128:(h + 1) * 128], identity=identb[:]), reads=[b_t[i], cb[0]], writes=[tbb])
                            yield
                            if nm == "aq":
                                S.op("act", lambda e: e.activation(out=Qbd[0:64, :, j, 0:128], in_=tv[0:64, :, :], func=AF.Copy, scale=gqfm[0:64, 0:1]), reads=[tbb, CONST] + cb, writes=[b_Qbd])
                                S.op("act", lambda e: e.activation(out=Qbd[64:128, :, j, 128:256], in_=tv[64:128, :, :], func=AF.Copy, scale=gqfm[64:128, 0:1]), reads=[tbb, CONST] + cb, writes=[b_Qbd])
                            else:
                                S.op("act", lambda e: e.activation(out=KTb[i][:], in_=tv, func=AF.Copy, scale=gkfm[:, 0:1]), reads=[tbb] + cb, writes=[b_KTb[i]])
                                S.dma("pool", kt_d.rearrange("h p k -> p h k")[:, :, blk * 128:(blk + 1) * 128], KTb[i][:], reads=[b_KTb[i]], writes=[b_ktd])
                            yield

                    b_sigj = [S.buf("sigj%d" % j_) for j_ in range(4)]
                    tgens = [tm_block(j_) for j_ in range(4)]
                    alive = list(tgens)
                    while alive:
                        nxt = []
                        for g_ in alive:
                            try:
                                next(g_)
                                nxt.append(g_)
                            except StopIteration:
                                pass
                        alive = nxt
                    S.barrier()

                def mlstm_block(j, st, bankfn, cache):
                    def sbt_c(st_, name, shape, dt=F32):
                        if name not in cache:
                            cache[name] = sbt(st_, name, shape, dt)
                        return cache[name]

                    def buf_c(name):
                        k_ = "B_" + name
                        if k_ not in cache:
                            cache[k_] = S.buf(name)
                        return cache[k_]
                    tsl = slice(j * 128, (j + 1) * 128)
                    lf = sbt_c(st, "lf%d" % (j % 2), [128, 4])
                    e1 = sbt_c(st, "e1%d" % (j % 2), [128, 4])
                    e2 = sbt_c(st, "e2%d" % (j % 2), [128, 4])
                    wsx = sbt_c(st, "wsx%d" % (j % 2), [128, 4])
                    RL = sbt_c(st, "RL%d" % (j % 2), [128, 4, 128])
                    Et = sbt_c(st, "Et%d" % (j % 2), [128, 512])
                    Kw = sbt_c(st, "Kw%d" % (j % 2), [128, 4, 128], BF16)
                    bm = buf_c("ml%d" % (j % 2))
                    b_e1 = buf_c("e1%d" % (j % 2))
                    b_ws = buf_c("wsx%d" % (j % 2))
                    b_RL = buf_c("RL%d" % (j % 2))
                    b_Et = buf_c("Et%d" % (j % 2))
                    b_Kw = buf_c("Kw%d" % (j % 2))
                    S.op("act", lambda e: e.activation(out=lf[:], in_=gif[:, j, 4:8], func=AF.Exp, scale=-1.0), reads=[b_gif], writes=[bm])
                    yield
                    S.op("act", lambda e: e.activation(out=lf[:], in_=lf[:], func=AF.Ln, bias=onet[:, 0:1], scale=1.0), reads=[bm, CONST], writes=[bm])
                    yield
                    S.op("dve", lambda e: e.tensor_scalar(out=lf[:], in0=lf[:], scalar1=-1.0, scalar2=None, op0=ALU.mult), reads=[bm], writes=[bm])
                    yield
                    p1, p1b = bankfn()
                    S.op("pe", lambda e: e.matmul(p1[:, 0:4], lhsT=umask[:], rhs=lf[:], start=True, stop=True), reads=[bm, cb[2]], writes=[p1b])
                    S.op("dve", lambda e: e.tensor_tensor(out=RL[:], in0=umask[:].unsqueeze(1).to_broadcast([128, 4, 128]),
                                                         in1=lf[:].unsqueeze(2).to_broadcast([128, 4, 128]), op=ALU.mult), reads=[bm, cb[2]], writes=[b_RL])
                    yield
                    S.op("dve", lambda e: e.tensor_tensor(out=e1[:], in0=gif[:, j, 0:4], in1=p1[:, 0:4], op=ALU.subtract), reads=[b_gif, p1b], writes=[b_e1])
                    yield
                    RLf = RL[:].rearrange("p h t -> p (h t)")
                    pBc, pBcb = bankfn()
                    S.op("pe", lambda e: e.matmul(pBc[:], lhsT=ones_f[:], rhs=RLf, start=True, stop=True), reads=[b_RL, CONST], writes=[pBcb])
                    yield
                    blast = pBc[:].rearrange("p (h t) -> p h t", h=4)[:, :, 127]
                    S.op("dve", lambda e: e.tensor_tensor(out=e2[:], in0=e1[:], in1=blast, op=ALU.add), reads=[b_e1, pBcb], writes=[b_ws])
                    yield
                    S.op("act", lambda e: e.activation(out=Et[:], in_=pBc[:], func=AF.Exp), reads=[pBcb], writes=[b_Et])
                    S.op("act", lambda e: e.activation(out=wsx[:], in_=e2[:], func=AF.Exp), reads=[b_ws], writes=[b_ws])
                    yield
                    tb, tbb = bankfn()
                    tv = bfview(tb)[:, 0:512].rearrange("p (h t) -> p h t", h=4)
                    for h in range(4):
                        S.op("pe", lambda e: e.transpose(out=tv[:, h, :], in_=mqT[:, 4 + h, tsl], identity=identb[:]), reads=[b_mqT, cb[0]], writes=[tbb])
                    yield
                    S.op("dve", lambda e: e.tensor_tensor(out=Kw[:], in0=tv, in1=wsx[:].unsqueeze(2).to_broadcast([128, 4, 128]), op=ALU.mult),
                         reads=[tbb, b_ws], writes=[b_Kw])
                    yield
                    if full:
                        DT = sbt_c(st, "DT%d" % (j % 2), [128, 4, 128])
                        PT = sbt_c(st, "PT%d" % (j % 2), [128, 4, 128], BF16)
                        qpT = sbt_c(st, "qpT%d" % (j % 2), [128, 4, 128], BF16)
                        numS = sbt_c(st, "numS%d" % (j % 2), [128, 4, 130])
                        rden = sbt_c(st, "rden%d" % (j % 2), [128, 4])
                        hmr = sbt_c(st, "hmr%d" % (j % 2), [128, 4, 128])
                        hsq = sbt_c(st, "hsq%d" % (j % 2), [128, 4, 128])
                        hss = sbt_c(st, "hss%d" % (j % 2), [128, 8])
                        gs = sbt_c(st, "gs%d" % (j % 2), [128, 512])
                        hmf = sbt_c(st, "hmf%d" % (j % 2), [128, 4, 128], BF16)
                        b_o = buf_c("mo%d" % (j % 2))
                        b_gs = buf_c("gs%d" % (j % 2))
                        b_num = buf_c("numS%d" % (j % 2))
                        b_DT = buf_c("DT%d" % (j % 2))
                        b_PT = buf_c("PT%d" % (j % 2))
                        b_qp = buf_c("qpT%d" % (j % 2))
                        pBm, pBmb = bankfn()
                        S.op("pe", lambda e: e.matmul(pBm[:], lhsT=ones_f[:], rhs=RLf, start=True, stop=False), reads=[b_RL, CONST], writes=[pBmb])
                        S.op("pe", lambda e: e.matmul(pBm[:], lhsT=identf[:], rhs=negm4[:], start=False, stop=True), reads=[cb[1], cb[3]], writes=[pBmb])
                        S.op("dve", lambda e: e.tensor_tensor(out=qpT[:], in0=mqT[:, 0:4, tsl], in1=Et[:].rearrange("p (h t) -> p h t", h=4), op=ALU.mult),
                             reads=[b_mqT, b_Et], writes=[b_qp])
                        S.op("dve", lambda e: e.tensor_tensor(out=gs[:], in0=mng[:], in1=sigmo[:, j, :], op=ALU.mult), reads=[b_sig] + cb, writes=[b_gs])
                        yield
                        for h in range(4):
                            S.op("act", lambda e: e.activation(out=DT[:, h, :], in_=pBm[:, h * 128:(h + 1) * 128], func=AF.Exp, bias=e1[:, h:h + 1], scale=1.0),
                                 reads=[pBmb, b_e1], writes=[b_DT])
                        yield
                        pA, pAb = bankfn()
                        for h in range(4):
                            S.op("pe", lambda e: e.matmul(pA[:, h * 128:(h + 1) * 128], lhsT=mqT[:, 4 + h, tsl], rhs=mqT[:, h, tsl], start=True, stop=True),
                                 reads=[b_mqT], writes=[pAb])
                        yield
                        S.op("dve", lambda e: e.tensor_tensor(out=PT[:], in0=pA[:].rearrange("p (h t) -> p h t", h=4), in1=DT[:], op=ALU.mult),
                             reads=[pAb, b_DT], writes=[b_PT])
                        yield
                    yield "AB"
                    if full:
                        for hp in range(2):
                            pn, pnb = bankfn()
                            for hh in range(2):
                                h = 2 * hp + hh
                                o = hh * 130
                                S.op("pe", lambda e: e.matmul(pn[:, o:o + 130], lhsT=PT[:, h, :], rhs=mVA[:, j, h, :], start=True, stop=False), reads=[b_PT, b_mVA], writes=[pnb])
                                S.op("pe", lambda e: e.matmul(pn[:, o:o + 130], lhsT=qpT[:, h, :], rhs=Sstb[:, h, :], start=False, stop=True), reads=[b_qp, b_Sb], writes=[pnb])
                            yield
                            S.op("dve", lambda e: e.tensor_copy(out=numS[:, 2 * hp:2 * hp + 2, :], in_=pn[:, 0:260].rearrange("p (h c) -> p h c", h=2)), reads=[pnb], writes=[b_num])
                            yield
                    Ev = Et[:].rearrange("p (h t) -> p h t", h=4)
                    for hp in range(2):
                        pc_, pcb = bankfn()
                        for hh in range(2):
                            h = 2 * hp + hh
                            o = hh * 130
                            S.op("pe", lambda e: e.matmul(pc_[:, o:o + 130], lhsT=Kw[:, h, :], rhs=mVA[:, j, h, :], start=True, stop=True), reads=[b_Kw, b_mVA], writes=[pcb])
                        yield
                        for hh in range(2):
                            h = 2 * hp + hh
                            o = hh * 130
                            S.op("dve", lambda e: e.scalar_tensor_tensor(out=Sst[:, h, :], in0=Sst[:, h, :], scalar=Ev[:, h, 127:128], in1=pc_[:, o:o + 130],
                                                                        op0=ALU.mult, op1=ALU.add), reads=[b_S, b_Et, pcb], writes=[b_S])
                        yield
                    S.op("dve", lambda e: e.tensor_copy(out=Sstb[:], in_=Sst[:]), reads=[b_S], writes=[b_Sb])
                    yield "S"
                    if full:
                        S.op("act", lambda e: e.activation(out=rden[:], in_=numS[:, :, 128], func=AF.Abs), reads=[b_num], writes=[b_o])
                        yield
                        S.op("dve", lambda e: e.tensor_scalar(out=rden[:], in0=rden[:], scalar1=1.0, scalar2=None, op0=ALU.max), reads=[b_o], writes=[b_o])
                        S.op("dve", lambda e: e.reciprocal(out=rden[:], in_=rden[:]), reads=[b_o], writes=[b_o])
                        S.op("dve", lambda e: e.tensor_tensor(out=hmr[:], in0=numS[:, :, 0:128], in1=rden[:].unsqueeze(2).to_broadcast([128, 4, 128]), op=ALU.mult),
                             reads=[b_num, b_o], writes=[b_o])
                        yield
                        S.op("act", lambda e: e.activation(out=hsq[:], in_=hmr[:], func=AF.Square, scale=1.0 / math.sqrt(128.0)), reads=[b_o], writes=[b_o])
                        yield
                        S.op("dve", lambda e: e.tensor_reduce(out=hss[:, 0:4], in_=hsq[:], axis=AX.X, op=ALU.add), reads=[b_o], writes=[b_o])
                        yield
                        S.op("act", lambda e: e.activation(out=hss[:, 4:8], in_=hss[:, 0:4], func=AF.Ln, bias=epst[:, 0:1], scale=1.0), reads=[b_o, CONST], writes=[b_o])
                        S.op("act", lambda e: e.activation(out=hss[:, 4:8], in_=hss[:, 4:8], func=AF.Exp, scale=-0.5), reads=[b_o], writes=[b_o])
                        yield
                        for h in range(4):
                            S.op("dve", lambda e: e.scalar_tensor_tensor(out=hmf[:, h, :], in0=hmr[:, h, :], scalar=hss[:, 4 + h:5 + h], in1=gs[:, h * 128:(h + 1) * 128],
                                                                        op0=ALU.mult, op1=ALU.mult), reads=[b_o, b_gs], writes=[b_o])
                        yield
                        tb2, tb2b = bankfn()
                        tv2 = bfview(tb2)[:, 0:512].rearrange("p (h t) -> p h t", h=4)
                        for h in range(4):
                            S.op("pe", lambda e: e.transpose(out=tv2[:, h, :], in_=hmf[:, h, :], identity=identb[:]), reads=[b_o, cb[0]], writes=[tb2b])
                        yield
                        S.op("dve", lambda e: e.tensor_copy(out=hmT[:, :, tsl], in_=tv2), reads=[tb2b], writes=[b_hmT])
                        yield

                def mlstm_driver(st, bankfn):
                    cache = {}
                    for j in range(4):
                        for r in mlstm_block(j, st, bankfn, cache):
                            yield

                if not full:
                    with ExitStack() as st:
                        for _ in mlstm_driver(st, nbank):
                            pass
                        S.barrier()

                if full:
                    with ExitStack() as st:
                        NKC = 8
                        kch = [sbt(st, "kch%d" % i, [128, NKC * 128], BF16) for i in range(3)]
                        vch = [sbt(st, "vch%d" % i, [128, NKC, 130], BF16) for i in range(3)]
                        b_kch = [S.buf("kch%d" % i) for i in range(3)]
                        b_vch = [S.buf("vch%d" % i) for i in range(3)]
                        pex = [sbt(st, "pex%d" % i, [128, 512], BF16) for i in range(6)]
                        b_pex = [S.buf("pex%d" % i) for i in range(6)]
                        rr = sbt(st, "rr", [128, 8])
                        har = sbt(st, "har", [128, 4, 128])
                        hsq = sbt(st, "ahsq", [128, 4, 128])
                        hss = sbt(st, "ahss", [128, 8])
                        haf = sbt(st, "haf", [128, 4, 128], BF16)
                        b_a = S.buf("attn_o")
                        nkb_tot = u * 4 + 4
                        chunk_list = [(s0, min(NKC, nkb_tot - s0)) for s0 in range(0, nkb_tot, NKC)]
                        ring = 0
                        LCH = len(chunk_list)
                        prefetch((0, 1), ["bm", "gm0"])
                        mdrv = mlstm_driver(st, lambda: (banks[7], bbufs[7]))
                        n_items_est = 4 * sum((2 if kb_ <= u * 4 - 2 else 0) + sum(1 for p2 in range(2) for jq in (2 * p2, 2 * p2 + 1) if kb_ > u * 4 + 2 * p2 - 2 and kb_ <= u * 4 + jq)
                                              for kb_ in range(nkb_tot))
                        MSTEP = max(1, n_items_est // 150)
                        gcount = [0]
                        loaded = set()

                        def ensure_chunk(g):
                            if g >= 4 * LCH or g in loaded:
                                return
                            loaded.add(g)
                            h_, ci_ = g // LCH, g % LCH
                            s0, nk = chunk_list[ci_]
                            ri = g % 3
                            S.dma("sp", kch[ri][:, 0:nk * 128], kt_d[h_, :, s0 * 128:(s0 + nk) * 128], reads=[b_ktd], writes=[b_kch[ri]])
                            S.dma("sp", vch[ri][:, 0:nk, :], va_d[h_, :, s0 * 130:(s0 + nk) * 130].rearrange("p (b c) -> p b c", c=130), reads=[b_vad], writes=[b_vch[ri]])

                        for h in range(4):
                            accs = [(banks[jq], bbufs[jq]) for jq in range(4)]
                            for jq in range(4):
                                S.op("dve", lambda e: e.memset(accs[jq][0][:, 0:260], 0.0), writes=[accs[jq][1]])
                            items = []
                            for ci, (s0, nk) in enumerate(chunk_list):
                                g = h * LCH + ci
                                ri = g % 3
                                for kk in range(nk):
                                    kb = s0 + kk
                                    for p2 in range(2):
                                        j0 = 2 * p2
                                        if u == NCTX:
                                            if p2 == 1 and kb <= u * 4 + 3:
                                                items.append((ri, kk, kb, (3,), g))
                                            continue
                                        if kb <= u * 4 + j0 - 2:
                                            items.append((ri, kk, kb, (j0, j0 + 1), g))
                                        else:
                                            for jq in (j0, j0 + 1):
                                                if kb <= u * 4 + jq:
                                                    items.append((ri, kk, kb, (jq,), g))

                            LAG = 3
                            NS = 3
                            NPX = len(pex)

                            def emit_scores(it, idx):
                                ri, kk, kb, jqs, ci = it
                                ensure_chunk(ci)
                                ensure_chunk(ci + 1)
                                nq = len(jqs)
                                sl = idx % NS
                                sp_, spb = banks[4 + sl], bbufs[4 + sl]
                                sv = sp_[:, 0:256 * nq]
                                px = idx % NPX
                                delta = u * 4 + jqs[0] - kb
                                near = (nq == 1) and delta <= 1
                                rhs = Qbd[:, h, jqs[0]:jqs[0] + nq, :].rearrange("p j c -> p (j c)")
                                S.op("pe", lambda e: e.matmul(sv, lhsT=kch[ri][:, kk * 128:(kk + 1) * 128], rhs=rhs, start=True, stop=not near),
                                     reads=[b_kch[ri], b_Qbd], writes=[spb])
                                if near:
                                    S.op("pe", lambda e: e.matmul(sv, lhsT=identb[:], rhs=biasb[:, h, delta, :], start=False, stop=True), reads=[CONST, cb[0]], writes=[spb])
                                    S.op("act", lambda e: e.activation(out=pex[px][:, 0:256 * nq], in_=sv, func=AF.Exp), reads=[spb], writes=[b_pex[px]])
                                else:
                                    S.op("act", lambda e: e.activation(out=pex[px][:, 0:256 * nq], in_=sv, func=AF.Exp, bias=farb[:, h:h + 1], scale=1.0),
                                         reads=[spb, cb[14]], writes=[b_pex[px]])

                            def emit_pv(it, idx):
                                ri, kk, kb, jqs, ci = it
                                px = idx % NPX
                                for a_, jq in enumerate(jqs):
                                    ab, abb = accs[jq]
                                    for m in range(2):
                                        c0 = a_ * 256 + m * 128
                                        S.op("pe", lambda e: e.matmul(ab[:, m * 130:(m + 1) * 130], lhsT=pex[px][:, c0:c0 + 128], rhs=vch[ri][:, kk, :],
                                                                     start=False, stop=False, skip_group_check=True), reads=[b_pex[px], b_vch[ri]], writes=[abb])

                            for idx in range(len(items) + LAG):
                                if idx < len(items):
                                    emit_scores(items[idx], idx)
                                if idx >= LAG:
                                    emit_pv(items[idx - LAG], idx - LAG)
                                gcount[0] += 1
                                if gcount[0] % MSTEP == 0:
                                    next(mdrv, None)
                            for jq in range(4):
                                ab, abb = accs[jq]
                                av = ab[:, 0:260].rearrange("p (m c) -> p m c", m=2)
                                S.op("dve", lambda e: e.tensor_scalar(out=rr[:, 2 * jq:2 * jq + 2], in0=av[:, :, 128], scalar1=1e-30, scalar2=None, op0=ALU.max), reads=[abb], writes=[b_a])
                                S.op("dve", lambda e: e.reciprocal(out=rr[:, 2 * jq:2 * jq + 2], in_=rr[:, 2 * jq:2 * jq + 2]), reads=[b_a], writes=[b_a])
                                S.op("dve", lambda e: e.tensor_tensor(out=rr[:, 2 * jq + 1:2 * jq + 2], in0=rr[:, 2 * jq + 1:2 * jq + 2], in1=neglam[:], op=ALU.mult),
                                     reads=[b_a, CONST], writes=[b_a])
                                S.op("dve", lambda e: e.tensor_scalar(out=har[:, jq, :], in0=av[:, 0, 0:128], scalar1=rr[:, 2 * jq:2 * jq + 1], scalar2=None, op0=ALU.mult),
                                     reads=[abb, b_a], writes=[b_a])
                                S.op("dve", lambda e: e.scalar_tensor_tensor(out=har[:, jq, :], in0=av[:, 1, 0:128], scalar=rr[:, 2 * jq + 1:2 * jq + 2], in1=har[:, jq, :],
                                                                            op0=ALU.mult, op1=ALU.add), reads=[abb, b_a], writes=[b_a])
                            S.op("act", lambda e: e.activation(out=hsq[:], in_=har[:], func=AF.Square, scale=1.0 / math.sqrt(128.0)), reads=[b_a], writes=[b_a])
                            S.op("dve", lambda e: e.tensor_reduce(out=hss[:, 0:4], in_=hsq[:], axis=AX.X, op=ALU.add), reads=[b_a], writes=[b_a])
                            S.op("act", lambda e: e.activation(out=hss[:, 4:8], in_=hss[:, 0:4], func=AF.Ln, bias=epst[:, 0:1], scale=1.0), reads=[b_a, CONST], writes=[b_a])
                            S.op("act", lambda e: e.activation(out=hss[:, 4:8], in_=hss[:, 4:8], func=AF.Exp, scale=-0.5), reads=[b_a], writes=[b_a])
                            for jq in range(4):
                                S.op("dve", lambda e: e.scalar_tensor_tensor(out=haf[:, jq, :], in0=har[:, jq, :], scalar=hss[:, 4 + jq:5 + jq], in1=ang[:, h * 128:(h + 1) * 128],
                                                                            op0=ALU.mult, op1=ALU.mult), reads=[b_a, CONST] + cb, writes=[b_a])
                            tb, tbb = banks[4 + h % 3], bbufs[4 + h % 3]
                            tv = bfview(tb)[:, 0:512].rearrange("p (j t) -> p j t", j=4)
                            for jq in range(4):
                                S.op("pe", lambda e: e.transpose(out=tv[:, jq, :], in_=haf[:, jq, :], identity=identb[:]), reads=[b_a, cb[0]], writes=[tbb])
                            S.op("dve", lambda e: e.tensor_copy(out=haT[:, h, :], in_=tv.rearrange("p j t -> p (j t)")), reads=[tbb], writes=[b_haT])
                        for _ in mdrv:
                            pass
                        S.barrier()

                    h2T = hT
                    b_h2T = b_hT
                    with ExitStack() as st:
                        W = {}
                        for nm in ("bm", "gm0", "ba", "ga0", "gm1", "ga1", "out0", "out1"):
                            W[nm] = wload(st, "w_" + nm, nm)
                        prefetch((2, 3), ["up0", "up1"])
                        yT = sbt(st, "yT", [128, 8, 512], BF16)
                        b_yT = S.buf("yT")
                        sg = [[sbt(st, "sg%d_%d" % (i, a), [128, 512]) for a in range(2)] for i in range(2)]
                        ty = [[sbt(st, "ty%d_%d" % (i, a), [128, 512]) for a in range(2)] for i in range(2)]
                        b_sg = [S.buf("sg%d" % i) for i in range(2)]
                        for c in range(8):
                            i = c % 2
                            res = []
                            for a, (bw, gw, srcT, b_src) in enumerate((("bm", "gm", hmT, b_hmT), ("ba", "ga", haT, b_haT))):
                                wt, wbf = W[bw]
                                wv = wt[:].rearrange("p (k n) -> p k n", k=4)
                                py, pyb = nbank()
                                for k in range(4):
                                    S.op("pe", lambda e: e.matmul(py[:], lhsT=wv[:, k, c * 128:(c + 1) * 128], rhs=srcT[:, k, :], start=(k == 0), stop=(k == 3)),
                                         reads=[wbf, b_src], writes=[pyb])
                                gt_, gbf = W[gw + str(c // 4)]
                                gv = gt_[:].rearrange("p (c n) -> p c n", c=8)
                                pg, pgb = nbank()
                                for dc in range(8):
                                    S.op("pe", lambda e: e.matmul(pg[:], lhsT=gv[:, dc, (c % 4) * 128:(c % 4 + 1) * 128], rhs=hT[:, dc, :], start=(dc == 0), stop=(dc == 7)),
                                         reads=[gbf, b_hT, b_hT2], writes=[pgb])
                                S.op("act", lambda e: e.activation(out=sg[i][a][:], in_=pg[:], func=AF.Sigmoid), reads=[pgb], writes=[b_sg[i]])
                                S.op("dve", lambda e: e.tensor_tensor(out=ty[i][a][:], in0=py[:], in1=sg[i][a][:], op=ALU.mult), reads=[pyb, b_sg[i]], writes=[b_sg[i]])
                            S.op("dve", lambda e: e.tensor_tensor(out=yT[:, c, :], in0=ty[i][0][:], in1=ty[i][1][:], op=ALU.add), reads=[b_sg[i]], writes=[b_yT])
                        tx = [sbt(st, "tx%d" % i, [128, 512]) for i in range(2)]
                        b_tx = [S.buf("tx%d" % i) for i in range(2)]
                        for j in range(4):
                            tsl = slice(j * 128, (j + 1) * 128)
                            for n in range(2):
                                wt, wbf = W["out%d" % n]
                                wv = wt[:].rearrange("p (c n) -> p c n", c=8)
                                po, pob = nbank()
                                for c in range(8):
                                    S.op("pe", lambda e: e.matmul(po[:], lhsT=yT[:, c, tsl], rhs=wv[:, c, :], start=(c == 0), stop=(c == 7)), reads=[wbf, b_yT], writes=[pob])
                                i = (j * 2 + n) % 2
                                S.op("dve", lambda e: e.tensor_tensor(out=tx[i][:], in0=po[:], in1=gate1[:, n * 512:(n + 1) * 512], op=ALU.mult), reads=[pob, CONST], writes=[b_tx[i]])
                                S.op("dve", lambda e: e.tensor_tensor(out=xs[:, j, n * 512:(n + 1) * 512], in0=xs[:, j, n * 512:(n + 1) * 512], in1=tx[i][:], op=ALU.add),
                                     reads=[b_tx[i], b_xs[j]], writes=[b_xs[j]])
                        for j in range(4):
                            norm_block(st, j, mult2, shift2, h2T, (b_hT, b_hT2), "n2")
                        S.barrier()

                    sa.close()
                    with ExitStack() as st:
                        actT = sbt(st, "actT", [128, NF, 512], BF16)
                        b_actT = S.buf("actT")
                        wu = [None, None, None]
                        uu = [sbt(st, "fu%d" % i, [128, 512]) for i in range(8)]
                        b_uu = [S.buf("fu%d" % i) for i in range(8)]
                        fe = [sbt(st, "fe%d" % i, [128, 512]) for i in range(2)]
                        b_fe = [S.buf("fe%d" % i) for i in range(2)]
                        wus = [pf[2], pf[3], sbt(st, "wup2", [128, 4096], BF16)]
                        b_wus = [b_pf[2], b_pf[3], S.buf("wup2")]

                        def ldup(jj):
                            if ("up%d" % jj) in prefetched:
                                prefetched.pop("up%d" % jj)
                                return
                            off, n = PIECES["up%d" % jj]
                            S.dma("sp", wus[jj % 3][:], wb_d[:, off:off + n], reads=[b_wb], writes=[b_wus[jj % 3]])

                        ldup(0)
                        ldup(1)
                        wd = sbt(st, "w_down", [128, NF * 1024], BF16)
                        wdb = S.buf("w_down")
                        wdv = wd[:].rearrange("p (f n) -> p f n", f=NF)
                        if not OPT_BATCH:
                            S.op("dve", lambda e: e.memset(fbc[:], 0.0), writes=[b_fbc])
                        if OPT_BATCH:
                            S.op("dve", lambda e: e.tensor_tensor(out=fbc[:, :, 1], in0=fhalo[:, :, 1], in1=fcw[:, :, 0], op=ALU.mult), reads=[b_fhalo] + cb, writes=[b_fbc])
                            S.op("dve", lambda e: e.tensor_tensor(out=fbc[:, :, 0], in0=fhalo[:, :, 1], in1=fcw[:, :, 1], op=ALU.mult), reads=[b_fhalo] + cb, writes=[b_fbc])
                            S.op("dve", lambda e: e.tensor_tensor(out=ctmp[:], in0=fhalo[:, :, 0], in1=fcw[:, :, 0], op=ALU.mult), reads=[b_fhalo] + cb, writes=[b_fbc])
                            S.op("dve", lambda e: e.tensor_tensor(out=fbc[:, :, 0], in0=fbc[:, :, 0], in1=ctmp[:], op=ALU.add), reads=[b_fbc], writes=[b_fbc])
                            S.op("dve", lambda e: e.tensor_tensor(out=fbc[:], in0=fbc[:], in1=fcb[:].unsqueeze(2).to_broadcast([128, 44, 2]), op=ALU.add), reads=[b_fbc] + cb, writes=[b_fbc])

                        def ffn_piece(jj):
                            if jj + 2 < 11:
                                ldup(jj + 2)
                            if jj == 2:
                                off_d, n_d = PIECES["down"]
                                S.dma("sp", wd[:], wb_d[:, off_d:off_d + n_d], reads=[b_wb], writes=[wdb])
                            if jj == 4 and u + 1 < NU:
                                prefetch((0, 1), ["mq", "mk"])
                            wv = wus[jj % 3][:].rearrange("p (c n) -> p c n", c=8)
                            wbf = b_wus[jj % 3]
                            wsel = fcwf if flagged else fcw
                            pbs = []
                            for k in range(4):
                                q = jj * 4 + k
                                pb, pbb = nbank()
                                pbs.append((pb, pbb))
                                for dc in range(8):
                                    S.op("pe", lambda e: e.matmul(pb[:], lhsT=wv[:, dc, k * 128:(k + 1) * 128], rhs=h2T[:, dc, :], start=(dc == 0), stop=(dc == 7)),
                                         reads=[wbf, b_hT, b_hT2], writes=[pbb])
                                kk_ = (jj % 2) * 4 + k
                                m0 = 2 if OPT_PARTMAIN else 0
                                S.op("act", lambda e: e.activation(out=uu[kk_][:, m0:512], in_=pb[:, m0:512], func=AF.Identity, scale=wsel[:, q, 2:3], bias=fcb[:, q:q + 1]),
                                     reads=[pbb, CONST] + cb, writes=[b_uu[kk_]])
                                for col in (range(2) if OPT_TINY else ()):
                                    S.op("act", lambda e: e.activation(out=uu[kk_][:, col:col + 1], in_=pb[:, col:col + 1], func=AF.Identity, scale=wsel[:, q, 2:3],
                                                                      bias=fbc[:, q, col:col + 1]), reads=[pbb, CONST, b_fbc] + cb, writes=[b_uu[kk_]])
                                if flagged:
                                    S.op("act", lambda e: e.activation(out=fhalo[:, q, :], in_=pb[:, 510:512], func=AF.Copy, scale=flag[:, 0:1]), reads=[pbb, cb[4], b_fbc], writes=[b_fhalo])
                                else:
                                    S.op("act", lambda e: e.copy(out=fhalo[:, q, :], in_=pb[:, 510:512]), reads=[pbb, b_fbc], writes=[b_fhalo])
                            for tp in (1, 0):
                                sh = 2 - tp
                                for k in range(4):
                                    q = jj * 4 + k
                                    kk_ = (jj % 2) * 4 + k
                                    pb, pbb = pbs[k]
                                    S.op("dve", lambda e: e.scalar_tensor_tensor(out=uu[kk_][:, sh:512], in0=pb[:, 0:512 - sh], scalar=wsel[:, q, tp:tp + 1], in1=uu[kk_][:, sh:512],
                                                                                op0=ALU.mult, op1=ALU.add), reads=[pbb, b_uu[kk_], CONST], writes=[b_uu[kk_]])
                            yield
                            for i in range(2):
                                f = 2 * jj + i
                                uv = uu[(jj % 2) * 4 + i]
                                ug = uu[(jj % 2) * 4 + 2 + i]
                                b_uv = b_uu[(jj % 2) * 4 + i]
                                b_ug = b_uu[(jj % 2) * 4 + 2 + i]
                                S.op("act", lambda e: e.activation(out=fe[i][:], in_=ug[:], func=AF.Silu), reads=[b_ug], writes=[b_fe[i]])
                                S.op("dve", lambda e: e.tensor_tensor(out=actT[:, f, :], in0=uv[:], in1=fe[i][:], op=ALU.mult), reads=[b_uv, b_fe[i]], writes=[b_actT])
                        fgens = [ffn_piece(jj) for jj in range(11)]
                        for step in range(12):
                            if step < 11:
                                next(fgens[step], None)
                            if step >= 1:
                                next(fgens[step - 1], None)
                        tx = fe
                        b_tx = b_fe
                        for j in range(4):
                            tsl = slice(j * 128, (j + 1) * 128)
                            blk = u * 4 + j
                            for n in range(2):
                                po, pob = nbank()
                                for f in range(NF):
                                    S.op("pe", lambda e: e.matmul(po[:], lhsT=actT[:, f, tsl], rhs=wdv[:, f, n * 512:(n + 1) * 512], start=(f == 0), stop=(f == NF - 1)),
                                         reads=[wdb, b_actT], writes=[pob])
                                i = (j * 2 + n) % 2
                                S.op("dve", lambda e: e.tensor_tensor(out=tx[i][:], in0=po[:], in1=gate2[:, n * 512:(n + 1) * 512], op=ALU.mult), reads=[pob, CONST], writes=[b_tx[i]])
                                S.op("dve", lambda e: e.tensor_tensor(out=xs[:, j, n * 512:(n + 1) * 512], in0=xs[:, j, n * 512:(n + 1) * 512], in1=tx[i][:], op=ALU.add),
                                     reads=[b_tx[i], b_xs[j]], writes=[b_xs[j]])
                            if own:
                                ob = blk - NFLAG * 4
                                S.dma("pool", out_d[ob * 128:(ob + 1) * 128, :], xs[:, j, :], reads=[b_xs[j]], writes=[b_out])
                        S.barrier()
        S.barrier()
    return nc


def _t5_bucket(n):
    n = np.maximum(n, 0)
    max_exact = 16
    nf = np.maximum(n, 1).astype(np.float32)
    large = max_exact + (np.log(nf / np.float32(max_exact)) / np.float32(math.log(128 / max_exact)) * np.float32(32 - max_exact)).astype(np.int32)
    large = np.minimum(large, 31)
    return np.where(n < max_exact, n, large)


def _fm(v, nch):
    return np.ascontiguousarray(np.asarray(v, np.float32).reshape(nch, 128).T)


def _piece_fm(w):
    n = w.shape[1]
    return w.reshape(8, 128, n).transpose(1, 0, 2).reshape(128, 8 * n)


def prepare_inputs(NCTX, NFULL, inputs):
    f32 = np.float32
    g = {k: np.asarray(v) for k, v in inputs.items()}
    NU = NCTX + NFULL
    half_tok = (NU // 2) * 512
    x = g["x"].astype(f32, copy=False)
    B = x.shape[0]
    assert x.shape[1] == 2 * half_tok
    w_in = g["w_in"][0]
    cols = {}
    o = 0
    for nm, n in (("mqk", 1024), ("mv", 512), ("mo", 512), ("mi", 4), ("mf", 4), ("aq", 512), ("ak", 512), ("av", 512), ("gm", 1024), ("ga", 1024)):
        cols[nm] = w_in[:, o:o + n]
        o += n

    def perm_qk(w):
        return w.reshape(1024, 2, 4, 64).transpose(0, 2, 1, 3).reshape(1024, 512)

    wall = np.zeros((128, WPAD), f32)

    def put(nm, arr):
        off, n = PIECES[nm]
        assert arr.shape == (128, n), (nm, arr.shape, n)
        wall[:, off:off + n] = arr

    put("mq", _piece_fm(cols["mqk"][:, 0:512]))
    put("mk", _piece_fm(cols["mqk"][:, 512:1024]))
    put("mv", _piece_fm(cols["mv"]))
    put("mo", _piece_fm(cols["mo"]))
    put("aq", _piece_fm(perm_qk(cols["aq"])))
    put("ak", _piece_fm(perm_qk(cols["ak"])))
    put("av", _piece_fm(cols["av"]))
    put("gm0", _piece_fm(cols["gm"][:, 0:512]))
    put("gm1", _piece_fm(cols["gm"][:, 512:1024]))
    put("ga0", _piece_fm(cols["ga"][:, 0:512]))
    put("ga1", _piece_fm(cols["ga"][:, 512:1024]))
    put("gif", _piece_fm(np.concatenate([cols["mi"], cols["mf"]], axis=1)))
    put("bm", g["w_branch_m"][0].reshape(4, 128, 1024).transpose(1, 0, 2).reshape(128, 4096))
    put("ba", g["w_branch_a"][0].reshape(4, 128, 1024).transpose(1, 0, 2).reshape(128, 4096))
    put("out0", _piece_fm(g["w_out"][0][:, 0:512]))
    put("out1", _piece_fm(g["w_out"][0][:, 512:1024]))
    w_up = g["w_up"][0]
    fcw_full = g["ffn_conv_w"][0]
    fcb_full = g["ffn_conv_b"][0]
    chunk_cols = []
    for jj in range(11):
        cc = [np.arange((2 * jj + i) * 128, (2 * jj + i + 1) * 128) for i in range(2)]
        cc += [DFF + np.arange((2 * jj + i) * 128, (2 * jj + i + 1) * 128) for i in range(2)]
        idx = np.concatenate(cc)
        chunk_cols.append(idx)
        put("up%d" % jj, _piece_fm(w_up[:, idx]))
    allidx = np.concatenate(chunk_cols)
    fcw = fcw_full[:, allidx].reshape(3, 44, 128).transpose(2, 1, 0).reshape(128, 44 * 3)
    fcb = fcb_full[allidx].reshape(44, 128).T
    put("down", g["w_down"][0].reshape(NF, 128, 1024).transpose(1, 0, 2).reshape(128, NF * 1024))

    w_ada = g["w_ada"][0]
    wada = np.stack([_piece_fm(w_ada[:, p * 512:(p + 1) * 512]) for p in range(12)]).astype(f32)
    bada = g["b_ada"][0].astype(f32)
    mcw = g["m_conv_w"][0].reshape(4, 8, 128).transpose(2, 1, 0).reshape(128, 32)
    mcb = _fm(g["m_conv_b"][0], 8)
    gifb = np.concatenate([g["m_igate_b"][0], g["m_fgate_b"][0]]).astype(f32)
    gq = np.tile(g["a_qnorm_g"][0], 8).astype(f32)
    gk = np.tile(g["a_knorm_g"][0], 8).astype(f32)
    rel = g["rel_bias"].astype(f32)
    kk = np.arange(128)[:, None]
    qq = np.arange(128)[None, :]
    biasg = np.zeros((128, 4, 2, 2, 128), f32)
    maskn = np.zeros((128, 4, 2, 2, 128), f32)
    for dl in range(2):
        dist = qq - kk + 128 * dl
        bidx = _t5_bucket(dist)
        for h in range(4):
            t = rel[bidx, h]
            biasg[:, h, dl, 0, :] = t
            biasg[:, h, dl, 1, :] = t
        if dl == 0:
            mk = np.where(dist < 0, NEG, 0.0).astype(f32)
            maskn[:, :, 0, :, :] = mk[:, None, None, :]
    farb = rel[31, :].astype(f32)
    umask = (kk <= qq).astype(f32)
    negm4 = np.tile(np.where(kk <= qq, 0.0, NEG).astype(f32), (1, 4))
    import ml_dtypes
    common = dict(
        wada=wada, bada=bada, badafm=_fm(bada, 48), g1fm=_fm(g["norm1_g"][0], 8), g2fm=_fm(g["norm2_g"][0], 8),
        wall=wall, mcw=np.ascontiguousarray(mcw, f32), mcb=mcb, fcw=np.ascontiguousarray(fcw, f32), fcb=np.ascontiguousarray(fcb, f32),
        gifb=gifb, mng=g["m_norm_g"][0].astype(f32), ang=g["a_norm_g"][0].astype(f32), gq=gq, gk=gk,
        gqfm=np.tile(g["a_qnorm_g"][0], 2).reshape(128, 1).astype(f32), gkfm=np.tile(g["a_knorm_g"][0], 2).reshape(128, 1).astype(f32),
        alam=g["a_lambda"][0].reshape(256).astype(f32), biasg=biasg.reshape(128, -1), maskn=maskn.reshape(128, -1), farb=farb,
        identb=np.eye(128).astype(ml_dtypes.bfloat16), identf=np.eye(128, dtype=f32), umask=umask, negm4=negm4,
    )
    in_maps = []
    for b in range(B):
        cfm = _fm(g["c"][b], 8)
        for hf in range(2):
            if hf == 0:
                xl = np.concatenate([np.zeros((half_tok, D), f32), x[b, 0:half_tok]], axis=0)
                fl = np.zeros((128, 1), f32)
            else:
                xl = x[b]
                fl = np.ones((128, 1), f32)
            m = dict(common)
            m["x"] = np.ascontiguousarray(xl)
            m["cfm"] = cfm
            m["flag"] = fl
            in_maps.append(m)
    return in_maps


_NC_CACHE = {}


def run(NCTX, NFULL, inputs):
    key = (NCTX, NFULL)
    if key not in _NC_CACHE:
        _NC_CACHE[key] = build_program(NCTX, NFULL)
    nc = _NC_CACHE[key]
    in_maps = prepare_inputs(NCTX, NFULL, inputs)
    res = run_bass_kernel_spmd(nc, in_maps, core_ids=list(range(len(in_maps))))
    B = len(in_maps) // 2
    half_tok = ((NCTX + NFULL) // 2) * 512
    out = np.empty((B, 2 * half_tok, D), np.float32)
    for b in range(B):
        for hf in range(2):
            out[b, hf * half_tok:(hf + 1) * half_tok] = res.results[b * 2 + hf]["out"]
    return out


def kernel(**inputs):
    return run(7, 9, inputs)
```

```python
import math
from contextlib import ExitStack

import numpy as np

import concourse.bass as bass
import concourse.mybir as mybir
from concourse.bass_utils import run_bass_kernel_spmd

F32 = mybir.dt.float32
BF16 = mybir.dt.bfloat16
AF = mybir.ActivationFunctionType
ALU = mybir.AluOpType
AX = mybir.AxisListType

D = 1024
DC = 8
DFF = 2816
NF = 22
EPS = 1e-6
LAM_INIT = 0.8 - 0.6 * math.exp(-0.3 * 0)
NEG = -30000.0
OPT_SELF_WAR = True
OPT_XN_ACT = True
OPT_SILU = True
OPT_GFOLD = True
OPT_TINY = True
OPT_BATCH = True
OPT_PARTMAIN = True

PIECES = {}
_off = 0
for _nm, _n in (("mq", 4096), ("mk", 4096), ("mv", 4096), ("mo", 4096), ("aq", 4096), ("ak", 4096),
                ("av", 4096), ("gm0", 4096), ("gm1", 4096), ("ga0", 4096), ("ga1", 4096), ("gif", 64),
                ("bm", 4096), ("ba", 4096), ("out0", 4096), ("out1", 4096)):
    PIECES[_nm] = (_off, _n)
    _off += _n
for _j in range(11):
    PIECES["up%d" % _j] = (_off, 4096)
    _off += 4096
PIECES["down"] = (_off, NF * 1024)
_off += NF * 1024
WTOT = _off
WPAD = ((WTOT + 4095) // 4096) * 4096


class Buf:
    __slots__ = ("name", "w", "rs", "sem", "semv", "grp", "psum", "persist")

    def __init__(self, name, grp=None):
        self.psum = False
        self.persist = False
        self.name = name
        self.w = None
        self.rs = []
        self.sem = None
        self.semv = 0
        self.grp = grp


class Sched:
    def __init__(self, nc, stack):
        self.nc = nc
        self.stack = stack
        self.engs = {}
        for nm, h in (("pe", nc.tensor), ("act", nc.scalar), ("dve", nc.vector), ("pool", nc.gpsimd), ("sp", nc.sync)):
            sem = stack.enter_context(nc.semaphore("s_" + nm))
            self.engs[nm] = dict(h=h, sem=sem, cnt=0, seen={})
        self.groups = {}
        self.dma_ev = {}
        self.free_sems = {}
        self.live = []
        self.nsem = 0

    def buf(self, name, grp=None):
        return Buf(name, grp)

    def _need(self, e, deps):
        E = self.engs[e]
        best = {}
        for d in deps:
            if d is None:
                continue
            sem, val, en = d
            if en == e and e == "pe":
                continue
            k = id(sem)
            if k not in best or best[k][1] < val:
                best[k] = (sem, val)
        for k, (sem, val) in best.items():
            if E["seen"].get(k, 0) >= val:
                continue
            E["h"].wait_ge(sem, val)
            E["seen"][k] = val

    def op(self, e, fn, reads=(), writes=()):
        E = self.engs[e]
        deps = []
        for b in reads:
            deps.append(b.w)
            if b.psum:
                for r in b.rs:
                    if r[2] != e:
                        deps.append(r)
        for b in writes:
            if b.w is not None and b.w[2] != e:
                deps.append(b.w)
            for r in b.rs:
                if OPT_SELF_WAR or r[2] != e:
                    deps.append(r)
        self._need(e, deps)
        ins = fn(E["h"])
        E["cnt"] += 1
        ins.then_inc(E["sem"], 1)
        ev = (E["sem"], E["cnt"], e)
        for b in reads:
            b.rs.append(ev)
        for b in writes:
            b.w = ev
            b.rs = []
        return ins

    def dma(self, q, out, in_, reads=(), writes=()):
        E = self.engs[q]
        deps = []
        for b in reads:
            deps.append(b.w)
        for b in writes:
            if b.grp in ("wbw", "kvw", "outw"):
                continue
            deps.append(b.w)
            deps.extend(b.rs)
        self._need(q, deps)
        ins = E["h"].dma_start(out=out, in_=in_)
        cands = list(writes) + list(reads)
        tgt = ([b for b in cands if b.grp is None] + cands)[0]
        if tgt.grp is not None:
            gk = tgt.grp
            if gk not in self.groups:
                self.groups[gk] = [self.stack.enter_context(self.nc.semaphore("g_" + gk)), 0]
            g = self.groups[gk]
            g[1] += 16
            sem, val = g[0], g[1]
        else:
            if tgt.sem is None:
                tgt.sem = {}
                self.live.append(tgt)
            if q not in tgt.sem:
                fs = self.free_sems.setdefault(q, [])
                if not fs:
                    self.nsem += 1
                    fs.append([self.stack.enter_context(self.nc.semaphore("dp%d" % self.nsem)), 0])
                tgt.sem[q] = fs.pop()
            ent = tgt.sem[q]
            ent[1] += 16
            sem, val = ent[0], ent[1]
        ins.then_inc(sem, 16)
        self.dma_ev[id(sem)] = (sem, val)
        ev = (sem, val, "dma")
        for b in reads:
            b.rs.append(ev)
        for b in writes:
            b.w = ev
            b.rs = []
        return ins

    def barrier(self, engines=("pe", "act", "dve", "pool", "sp")):
        deps = [(E["sem"], E["cnt"], "x") for E in self.engs.values() if E["cnt"] > 0]
        deps += [(s, v, "dma") for (s, v) in self.dma_ev.values()]
        for e in engines:
            self._need(e, deps)
        self.dma_ev = {}
        keep = []
        for b in self.live:
            if b.persist:
                keep.append(b)
                continue
            for qq, ent in b.sem.items():
                self.free_sems[qq].append(ent)
            b.sem = None
        self.live = keep

    def seal(self, bufs, grp):
        g = self.groups[grp]
        for b in bufs:
            b.w = (g[0], g[1], "dma")


def build_program(NCTX, NFULL, debug=False):
    NU = NCTX + NFULL
    NFLAG = NU // 2
    NBLK = NU * 4
    NTOK = NU * 512
    NOWN = (NFULL - 1) * 512
    assert NU == 2 * (NFULL - 1)

    nc = bass.Bass("TRN2", target_bir_lowering=False)

    def din(name, shape, dt=F32):
        return nc.dram_tensor(name, list(shape), dt, kind="ExternalInput").ap()

    x_d = din("x", [NTOK, D])
    cfm_d = din("cfm", [128, 8])
    wada_d = din("wada", [12, 128, 8 * 512])
    bada_d = din("bada", [6144])
    badafm_d = din("badafm", [128, 48])
    g1fm_d = din("g1fm", [128, 8])
    g2fm_d = din("g2fm", [128, 8])
    wall_d = din("wall", [128, WPAD])
    mcw_d = din("mcw", [128, 8 * 4])
    mcb_d = din("mcb", [128, 8])
    fcw_d = din("fcw", [128, 44 * 3])
    fcb_d = din("fcb", [128, 44])
    gifb_d = din("gifb", [8])
    mng_d = din("mng", [512])
    ang_d = din("ang", [512])
    gq_d = din("gq", [512])
    gk_d = din("gk", [512])
    gqfm_d = din("gqfm", [128, 1])
    gkfm_d = din("gkfm", [128, 1])
    alam_d = din("alam", [256])
    biasg_d = din("biasg", [128, 4 * 2 * 2 * 128])
    maskn_d = din("maskn", [128, 4 * 2 * 2 * 128])
    farb_d = din("farb", [4])
    flag_d = din("flag", [128, 1])
    identb_d = din("identb", [128, 128], BF16)
    identf_d = din("identf", [128, 128])
    umask_d = din("umask", [128, 128])
    negm4_d = din("negm4", [128, 512])
    out_d = nc.dram_tensor("out", [NOWN, D], F32, kind="ExternalOutput").ap()
    wb_d = nc.dram_tensor("wb_scr", [128, WPAD], BF16, kind="Internal").ap()
    kt_d = nc.dram_tensor("kt_scr", [4, 128, NBLK * 128], BF16, kind="Internal").ap()
    va_d = nc.dram_tensor("va_scr", [4, 128, NBLK * 130], BF16, kind="Internal").ap()

    with ExitStack() as top:
        S = Sched(nc, top)

        uid = [0]

        def sbt(st, name, shape, dt=F32):
            uid[0] += 1
            return st.enter_context(nc.sbuf_tensor("s%d_%s" % (uid[0], name), list(shape), dt))

        banks = [top.enter_context(nc.psum_tensor("bank%d" % i, [128, 512], F32)) for i in range(8)]
        bbufs = [S.buf("bank%d" % i) for i in range(8)]
        for b_ in bbufs:
            b_.psum = True
        bank_rr = [0]

        def nbank(lo=0, hi=8):
            i = lo + (bank_rr[0] % (hi - lo))
            bank_rr[0] += 1
            return banks[i], bbufs[i]

        def bfview(bank):
            return bank[:].bitcast(BF16)

        identb = sbt(top, "identb", [128, 128], BF16)
        identf = sbt(top, "identf", [128, 128])
        umask = sbt(top, "umask", [128, 128])
        negm4 = sbt(top, "negm4", [128, 512])
        ones_f = sbt(top, "ones_f", [128, 128])
        flag = sbt(top, "flag", [128, 1])
        epst = sbt(top, "epst", [128, 1])
        onet = sbt(top, "onet", [128, 1])
        mult1 = sbt(top, "mult1", [128, 8])
        shift1 = sbt(top, "shift1", [128, 8])
        mult2 = sbt(top, "mult2", [128, 8])
        shift2 = sbt(top, "shift2", [128, 8])
        gate1 = sbt(top, "gate1", [128, D])
        gate2 = sbt(top, "gate2", [128, D])
        mcw = sbt(top, "mcw", [128, 8, 4])
        mcb = sbt(top, "mcb", [128, 8])
        mcwf = sbt(top, "mcwf", [128, 8, 4])
        fcwf = sbt(top, "fcwf", [128, 44, 3])
        mcnb = sbt(top, "mcnb", [128, 8])
        fcw = sbt(top, "fcw", [128, 44, 3])
        fcb = sbt(top, "fcb", [128, 44])
        fcnb = sbt(top, "fcnb", [128, 44])
        gifb = sbt(top, "gifb", [128, 8])
        mng = sbt(top, "mng", [128, 512])
        ang = sbt(top, "ang", [128, 512])
        gq = sbt(top, "gq", [128, 512])
        gk = sbt(top, "gk", [128, 512])
        gqfm = sbt(top, "gqfm", [128, 1])
        gkfm = sbt(top, "gkfm", [128, 1])
        mbc = sbt(top, "mbc", [128, 8, 3])
        fbc = sbt(top, "fbc", [128, 44, 2])
        ctmp = sbt(top, "ctmp", [128, 44])
        b_mbc = S.buf("mbc")
        b_fbc = S.buf("fbc")
        biasb = sbt(top, "biasb", [128, 4, 2, 256], BF16)
        farb = sbt(top, "farb", [128, 4])
        neglam = sbt(top, "neglam", [128, 1])
        Sst = sbt(top, "Sst", [128, 4, 130])
        Sstb = sbt(top, "Sstb", [128, 4, 130], BF16)
        mhalo = sbt(top, "mhalo", [128, 8, 3])
        fhalo = sbt(top, "fhalo", [128, 44, 2])
        CONST = S.buf("const")
        b_S = S.buf("Sst")
        b_Sb = S.buf("Sstb")
        b_mhalo = S.buf("mhalo")
        b_fhalo = S.buf("fhalo")
        b_wb = S.buf("wb_scr", grp="wbw")
        b_ktd = S.buf("kt_scr", grp="kvw")
        b_vad = S.buf("va_scr", grp="kvw")
        b_out = S.buf("outd", grp="outw")

        def ld_const(t, src, grp="cst"):
            b = S.buf("c_" + t.name, grp=grp)
            S.dma("sp", t[:], src, writes=[b])
            return b

        cb = []
        cb.append(ld_const(identb, identb_d[:, :]))
        cb.append(ld_const(identf, identf_d[:, :]))
        cb.append(ld_const(umask, umask_d[:, :]))
        cb.append(ld_const(negm4, negm4_d[:, :]))
        cb.append(ld_const(flag, flag_d[:, :]))
        cb.append(ld_const(mcw, mcw_d.rearrange("p (c k) -> p c k", k=4)))
        cb.append(ld_const(mcb, mcb_d[:, :]))
        cb.append(ld_const(fcw, fcw_d.rearrange("p (c k) -> p c k", k=3)))
        cb.append(ld_const(fcb, fcb_d[:, :]))
        cb.append(ld_const(gifb, gifb_d.partition_broadcast(128)))
        cb.append(ld_const(mng, mng_d.partition_broadcast(128)))
        cb.append(ld_const(ang, ang_d.partition_broadcast(128)))
        cb.append(ld_const(gq, gq_d.partition_broadcast(128)))
        cb.append(ld_const(gk, gk_d.partition_broadcast(128)))
        cb.append(ld_const(farb, farb_d.partition_broadcast(128)))
        cb.append(ld_const(gqfm, gqfm_d[:, :]))
        cb.append(ld_const(gkfm, gkfm_d[:, :]))
        S.seal(cb, "cst")
        S.op("dve", lambda e: e.memset(ones_f[:], 1.0), writes=[CONST])
        S.op("dve", lambda e: e.memset(epst[:], EPS), writes=[CONST])
        S.op("dve", lambda e: e.memset(onet[:], 1.0), writes=[CONST])
        S.op("dve", lambda e: e.memset(Sst[:], 0.0), writes=[b_S])
        S.op("dve", lambda e: e.memset(Sstb[:], 0.0), writes=[b_Sb])
        S.op("dve", lambda e: e.memset(mhalo[:], 0.0), writes=[b_mhalo])
        S.op("dve", lambda e: e.memset(fhalo[:], 0.0), writes=[b_fhalo])
        S.op("dve", lambda e: e.tensor_scalar(out=mcnb[:], in0=mcb[:], scalar1=-1.0, scalar2=None, op0=ALU.mult), reads=cb, writes=[CONST])
        S.op("dve", lambda e: e.tensor_scalar(out=fcnb[:], in0=fcb[:], scalar1=-1.0, scalar2=None, op0=ALU.mult), reads=cb, writes=[CONST])
        S.op("dve", lambda e: e.tensor_scalar(out=gqfm[:], in0=gqfm[:], scalar1=0.125, scalar2=None, op0=ALU.mult), reads=cb, writes=[CONST])
        S.op("dve", lambda e: e.tensor_scalar(out=mcwf[:], in0=mcw[:], scalar1=flag[:, 0:1], scalar2=None, op0=ALU.mult), reads=cb, writes=[CONST])
        S.op("dve", lambda e: e.tensor_scalar(out=fcwf[:], in0=fcw[:], scalar1=flag[:, 0:1], scalar2=None, op0=ALU.mult), reads=cb, writes=[CONST])
        S.op("dve", lambda e: e.tensor_scalar(out=ang[:], in0=ang[:], scalar1=1.0 - LAM_INIT, scalar2=None, op0=ALU.mult), reads=cb, writes=[CONST])

        with ExitStack() as st:
            cfm = sbt(st, "cfm", [128, 8])
            sc = sbt(st, "sc", [128, 8])
            sct = sbt(st, "sct", [128, 8])
            scbc = sbt(st, "scbc", [128, 8, 128])
            badafm = sbt(st, "badafm", [128, 48])
            g1fm = sbt(st, "g1fm", [128, 8])
            g2fm = sbt(st, "g2fm", [128, 8])
            modfm = sbt(st, "modfm", [128, 48])
            alam = sbt(st, "alam", [128, 256])
            lt = sbt(st, "lt", [128, 128])
            ls = sbt(st, "ls", [128, 2])
            biasg = sbt(st, "biasg", [128, 2048])
            maskn = sbt(st, "maskn", [128, 2048])
            wst = [sbt(st, "wst%d" % i, [128, 4096]) for i in range(2)]
            wbo = [sbt(st, "wbo%d" % i, [128, 4096], BF16) for i in range(2)]
            bbt = [sbt(st, "bbt%d" % i, [128, 512]) for i in range(2)]
            b_wst = [S.buf("wst%d" % i) for i in range(2)]
            b_wbo = [S.buf("wbo%d" % i) for i in range(2)]
            b_bbt = [S.buf("bbt%d" % i) for i in range(2)]
            L = S.buf("prel")
            b_l = [ld_const(cfm, cfm_d[:, :], "cst2"), ld_const(badafm, badafm_d[:, :], "cst2"), ld_const(g1fm, g1fm_d[:, :], "cst2"),
                   ld_const(g2fm, g2fm_d[:, :], "cst2"), ld_const(alam, alam_d.partition_broadcast(128), "cst2"),
                   ld_const(biasg, biasg_d[:, :], "cst2"), ld_const(maskn, maskn_d[:, :], "cst2")]
            S.seal(b_l, "cst2")
            S.op("act", lambda e: e.activation(out=sct[:], in_=cfm[:], func=AF.Exp, scale=-1.0), reads=b_l, writes=[L])
            S.op("dve", lambda e: e.tensor_scalar(out=sct[:], in0=sct[:], scalar1=1.0, scalar2=None, op0=ALU.add), reads=[L], writes=[L])
            S.op("dve", lambda e: e.reciprocal(out=sct[:], in_=sct[:]), reads=[L], writes=[L])
            S.op("dve", lambda e: e.tensor_tensor(out=sc[:], in0=cfm[:], in1=sct[:], op=ALU.mult), reads=[L], writes=[L])
            S.op("dve", lambda e: e.tensor_copy(out=scbc[:], in_=sc[:].unsqueeze(2).to_broadcast([128, 8, 128])), reads=[L], writes=[L])
            S.op("dve", lambda e: e.tensor_tensor(out=lt[:, 0:64], in0=alam[:, 0:64], in1=alam[:, 64:128], op=ALU.mult), reads=b_l, writes=[L])
            S.op("dve", lambda e: e.tensor_tensor(out=lt[:, 64:128], in0=alam[:, 128:192], in1=alam[:, 192:256], op=ALU.mult), reads=[L], writes=[L])
            S.op("dve", lambda e: e.tensor_reduce(out=ls[:], in_=lt[:].rearrange("p (a b) -> p a b", a=2), axis=AX.X, op=ALU.add), reads=[L], writes=[L])
            S.op("act", lambda e: e.activation(out=ls[:], in_=ls[:], func=AF.Exp), reads=[L], writes=[L])
            S.op("dve", lambda e: e.tensor_tensor(out=neglam[:], in0=ls[:, 1:2], in1=ls[:, 0:1], op=ALU.subtract), reads=[L], writes=[CONST])
            S.op("dve", lambda e: e.tensor_scalar(out=neglam[:], in0=neglam[:], scalar1=-LAM_INIT, scalar2=None, op0=ALU.add), reads=[CONST], writes=[CONST])
            S.op("dve", lambda e: e.tensor_tensor(out=biasb[:].rearrange("p a b c -> p (a b c)"), in0=biasg[:], in1=maskn[:], op=ALU.add), reads=b_l, writes=[CONST])

            fm_ps, fm_b = banks[7], bbufs[7]
            fm_cols = {0: 0, 1: 4, 2: 8, 3: 12, 6: 24, 7: 28, 8: 32, 9: 36}
            for pc in range(12):
                i = pc % 2
                S.dma("sp", wst[i][:], wada_d[pc, :, :], writes=[b_wst[i]])
                wv = wst[i][:].rearrange("p (c n) -> p c n", c=8)
                if pc in (4, 5, 10, 11):
                    S.dma("sp", bbt[i][:], bada_d[pc * 512:(pc + 1) * 512].partition_broadcast(128), writes=[b_bbt[i]])
                    pb, pbb = nbank(0, 6)
                    for dc in range(8):
                        S.op("pe", lambda e: e.matmul(pb[:], lhsT=scbc[:, dc, :], rhs=wv[:, dc, :], start=(dc == 0), stop=(dc == 7)),
                             reads=[L, b_wst[i]], writes=[pbb])
                    gt = gate1 if pc < 6 else gate2
                    off = (pc % 2) * 512
                    S.op("dve", lambda e: e.tensor_tensor(out=gt[:, off:off + 512], in0=pb[:], in1=bbt[i][:], op=ALU.add),
                         reads=[pbb, b_bbt[i]], writes=[CONST])
                else:
                    for k in range(4):
                        col = fm_cols[pc] + k
                        for dc in range(8):
                            S.op("pe", lambda e: e.matmul(fm_ps[:, col:col + 1], lhsT=wv[:, dc, k * 128:(k + 1) * 128], rhs=sc[:, dc:dc + 1],
                                                         start=(dc == 0), stop=(dc == 7)), reads=[L, b_wst[i]], writes=[fm_b])
            S.op("dve", lambda e: e.tensor_tensor(out=modfm[:, 0:16], in0=fm_ps[:, 0:16], in1=badafm[:, 0:16], op=ALU.add), reads=[fm_b] + b_l, writes=[L])
            S.op("dve", lambda e: e.tensor_tensor(out=modfm[:, 24:40], in0=fm_ps[:, 24:40], in1=badafm[:, 24:40], op=ALU.add), reads=[fm_b] + b_l, writes=[L])
            S.op("dve", lambda e: e.scalar_tensor_tensor(out=mult1[:], in0=modfm[:, 8:16], scalar=1.0, in1=g1fm[:], op0=ALU.add, op1=ALU.mult), reads=[L], writes=[CONST])
            S.op("dve", lambda e: e.tensor_copy(out=shift1[:], in_=modfm[:, 0:8]), reads=[L], writes=[CONST])
            S.op("dve", lambda e: e.scalar_tensor_tensor(out=mult2[:], in0=modfm[:, 32:40], scalar=1.0, in1=g2fm[:], op0=ALU.add, op1=ALU.mult), reads=[L], writes=[CONST])
            S.op("dve", lambda e: e.tensor_copy(out=shift2[:], in_=modfm[:, 24:32]), reads=[L], writes=[CONST])

            cast_eng = ["dve", "act"]
            for ch in range(WPAD // 4096):
                i = ch % 2
                S.dma("sp", wst[i][:], wall_d[:, ch * 4096:(ch + 1) * 4096], writes=[b_wst[i]])
                ce = cast_eng[ch % 2]
                if ce == "act":
                    S.op("act", lambda e: e.copy(out=wbo[i][:], in_=wst[i][:]), reads=[b_wst[i]], writes=[b_wbo[i]])
                else:
                    S.op(ce, lambda e: e.tensor_copy(out=wbo[i][:], in_=wst[i][:]), reads=[b_wst[i]], writes=[b_wbo[i]])
                S.dma("pool", wb_d[:, ch * 4096:(ch + 1) * 4096], wbo[i][:], reads=[b_wbo[i]], writes=[b_wb])
            S.barrier()

        njunk = sbt(top, "njunk", [128, D], BF16)
        ncache = {}
        pf = [sbt(top, "pf%d" % i, [128, 4096], BF16) for i in range(4)]
        b_pf = [S.buf("pf%d" % i) for i in range(4)]
        for b_ in b_pf:
            b_.persist = True
        prefetched = {}

        def prefetch(slots, pieces):
            for sl, piece in zip(slots, pieces):
                off, n = PIECES[piece]
                S.dma("sp", pf[sl][:, 0:n], wb_d[:, off:off + n], reads=[b_wb], writes=[b_pf[sl]])
                prefetched[piece] = (pf[sl], b_pf[sl])

        def wload(st, name, piece, shape3=None):
            if piece in prefetched:
                return prefetched.pop(piece)
            off, n = PIECES[piece]
            t = sbt(st, name, [128, n], BF16)
            b = S.buf(name)
            S.dma("sp", t[:], wb_d[:, off:off + n], reads=[b_wb], writes=[b])
            return t, b

        for u in range(NU):
            full = u >= NCTX
            flagged = u < NFLAG
            own = u >= NFLAG
            with ExitStack() as su:
                xs = sbt(su, "xs", [128, 4, D])
                hT = sbt(su, "hT", [128, 8, 512], BF16)
                sa = su.enter_context(ExitStack())
                mqT = sbt(sa, "mqT", [128, 8, 512], BF16)
                mVA = sbt(sa, "mVA", [128, 4, 4, 130], BF16)
                sigmo = sbt(sa, "sigmo", [128, 4, 512])
                gif = sbt(sa, "gif", [128, 4, 8])
                Qbd = sbt(sa, "Qbd", [128, 4, 4, 256], BF16)
                hmT = sbt(sa, "hmT", [128, 4, 512], BF16)
                haT = sbt(sa, "haT", [128, 4, 512], BF16)
                b_xs = [S.buf("xs%d" % j) for j in range(4)]
                b_hT = S.buf("hT")
                b_hT2 = S.buf("hT2")
                b_mqT = S.buf("mqT")
                b_mVA = S.buf("mVA")
                b_sig = S.buf("sigmo")
                b_gif = S.buf("gif")
                b_Qbd = S.buf("Qbd")
                b_hmT = S.buf("hmT")
                b_haT = S.buf("haT")
                if full:
                    S.op("dve", lambda e: e.memset(Qbd[:], 0.0), writes=[b_Qbd])

                def norm_block(st, j, mult, shift, hdst, b_hdst, tagp):
                    junk = njunk
                    if not hasattr(st, "ncache"):
                        st.ncache = {}
                    ck = (tagp, j % 2)
                    if ck not in st.ncache:
                        st.ncache[ck] = (sbt(st, tagp + "xn%d" % (j % 2), [128, D], BF16), S.buf("xn"))
                    xn, b_xn = st.ncache[ck]
                    ss = sbt(st, tagp + "ss%d" % j, [128, 2])
                    bl = S.buf("nb")
                    S.op("act", lambda e: e.activation(out=xn[:], in_=xs[:, j, :], func=AF.Square, scale=1.0 / 32.0, accum_out=ss[:, 0:1]),
                         reads=[b_xs[j]], writes=[bl, b_xn])
                    yield
                    S.op("act", lambda e: e.activation(out=ss[:, 1:2], in_=ss[:, 0:1], func=AF.Ln, bias=epst[:, 0:1], scale=1.0), reads=[bl, CONST], writes=[bl])
                    S.op("act", lambda e: e.activation(out=ss[:, 1:2], in_=ss[:, 1:2], func=AF.Exp, scale=-0.5), reads=[bl], writes=[bl])
                    yield
                    S.op("dve", lambda e: e.tensor_scalar(out=xn[:], in0=xs[:, j, :], scalar1=ss[:, 1:2], scalar2=None, op0=ALU.mult),
                         reads=[bl, b_xs[j]], writes=[b_xn])
                    yield
                    pvs = []
                    for half in range(2):
                        pb, pbb = nbank()
                        pv = bfview(pb)[:, 0:512].rearrange("p (c t) -> p c t", c=4)
                        pvs.append((pv, pbb))
                        for cc in range(4):
                            c = half * 4 + cc
                            S.op("pe", lambda e: e.transpose(out=pv[:, cc, :], in_=xn[:, c * 128:(c + 1) * 128], identity=identb[:]),
                                 reads=[b_xn, cb[0]], writes=[pbb])
                    yield
                    for half in range(2):
                        pv, pbb = pvs[half]
                        for cc in range(4):
                            c = half * 4 + cc
                            if half == 0:
                                S.op("act", lambda e: e.activation(out=hdst[:, c, j * 128:(j + 1) * 128], in_=pv[:, cc, :], func=AF.Identity,
                                                                  scale=mult[:, c:c + 1], bias=shift[:, c:c + 1]), reads=[pbb, CONST], writes=[b_hdst[0]])
                            else:
                                S.op("dve", lambda e: e.tensor_scalar(out=hdst[:, c, j * 128:(j + 1) * 128], in0=pv[:, cc, :], scalar1=mult[:, c:c + 1],
                                                                     scalar2=shift[:, c:c + 1], op0=ALU.mult, op1=ALU.add), reads=[pbb, CONST], writes=[b_hdst[1]])
                    yield

                def run_norms(st, mult, shift, hdst, b_hdst, tagp):
                    for pair_ in ((0, 1), (2, 3)):
                        alive_ = [norm_block(st, j_, mult, shift, hdst, b_hdst, tagp) for j_ in pair_]
                        while alive_:
                            nxt_ = []
                            for g_ in alive_:
                                try:
                                    next(g_)
                                    nxt_.append(g_)
                                except StopIteration:
                                    pass
                            alive_ = nxt_

                with ExitStack() as st:
                    names = (["mq", "mk", "mv", "gif", "av", "mo", "aq", "ak"] if full else ["mk", "mv", "gif", "av", "ak"])
                    W = {}
                    for j in range(4):
                        blk = u * 4 + j
                        S.dma("sp", xs[:, j, :], x_d[blk * 128:(blk + 1) * 128, :], writes=[b_xs[j]])
                    for nm in names:
                        W[nm] = wload(st, "w_" + nm, nm)
                    if not full and u + 1 < NU:
                        nxt_full = (u + 1) >= NCTX
                        slots = (2, 3) if (nxt_full or (u + 1) % 2 == 1) else (0, 1)
                        if nxt_full:
                            slots = (2, 3)
                        prefetch(slots, ["mq", "mk"] if nxt_full else ["mk", "mv"])
                    run_norms(st, mult1, shift1, hT, (b_hT, b_hT2), "n1")
                    if flagged:
                        S.op("dve", lambda e: e.tensor_copy(out=mVA[:, :, :, 128:130], in_=flag[:, 0:1].unsqueeze(1).unsqueeze(1).to_broadcast([128, 4, 4, 2])),
                             reads=[cb[4]], writes=[b_mVA])
                    else:
                        S.op("dve", lambda e: e.memset(mVA[:, :, :, 128:130], 1.0), writes=[b_mVA])
                    acc = [sbt(st, "acc%d" % i, [128, 512]) for i in range(4)]
                    et = [sbt(st, "et%d" % i, [128, 512]) for i in range(4)]
                    b_acc = [S.buf("acc%d" % i) for i in range(4)]
                    b_et = [S.buf("et%d" % i) for i in range(4)]
                    hv = mhalo
                    if not OPT_BATCH:
                        S.op("dve", lambda e: e.memset(mbc[:], 0.0), writes=[b_mbc])
                    if OPT_BATCH:
                        S.op("dve", lambda e: e.tensor_tensor(out=mbc[:, :, 2], in0=hv[:, :, 2], in1=mcw[:, :, 0], op=ALU.mult), reads=[b_mhalo] + cb, writes=[b_mbc])
                        S.op("dve", lambda e: e.tensor_tensor(out=mbc[:, :, 1], in0=hv[:, :, 2], in1=mcw[:, :, 1], op=ALU.mult), reads=[b_mhalo] + cb, writes=[b_mbc])
                        S.op("dve", lambda e: e.tensor_tensor(out=mbc[:, :, 0], in0=hv[:, :, 2], in1=mcw[:, :, 2], op=ALU.mult), reads=[b_mhalo] + cb, writes=[b_mbc])
                        S.op("dve", lambda e: e.tensor_tensor(out=ctmp[:, 0:8], in0=hv[:, :, 1], in1=mcw[:, :, 0], op=ALU.mult), reads=[b_mhalo] + cb, writes=[b_mbc])
                        S.op("dve", lambda e: e.tensor_tensor(out=mbc[:, :, 1], in0=mbc[:, :, 1], in1=ctmp[:, 0:8], op=ALU.add), reads=[b_mbc], writes=[b_mbc])
                        S.op("dve", lambda e: e.tensor_tensor(out=ctmp[:, 8:16], in0=hv[:, :, 1], in1=mcw[:, :, 1], op=ALU.mult), reads=[b_mhalo] + cb, writes=[b_mbc])
                        S.op("dve", lambda e: e.tensor_tensor(out=mbc[:, :, 0], in0=mbc[:, :, 0], in1=ctmp[:, 8:16], op=ALU.add), reads=[b_mbc], writes=[b_mbc])
                        S.op("dve", lambda e: e.tensor_tensor(out=ctmp[:, 16:24], in0=hv[:, :, 0], in1=mcw[:, :, 0], op=ALU.mult), reads=[b_mhalo] + cb, writes=[b_mbc])
                        S.op("dve", lambda e: e.tensor_tensor(out=mbc[:, :, 0], in0=mbc[:, :, 0], in1=ctmp[:, 16:24], op=ALU.add), reads=[b_mbc], writes=[b_mbc])
                        S.op("dve", lambda e: e.tensor_tensor(out=mbc[:], in0=mbc[:], in1=mcb[:].unsqueeze(2).to_broadcast([128, 8, 3]), op=ALU.add), reads=[b_mbc] + cb, writes=[b_mbc])
                    groups = ([[0, 1, 2, 3], [4, 5, 6, 7]] if full else [[4, 5, 6, 7]])
                    wsel = mcwf if flagged else mcw
                    for grp in groups:
                        pbs = []
                        for k, c in enumerate(grp):
                            wt, wbf = W["mq" if c < 4 else "mk"]
                            wv = wt[:].rearrange("p (c n) -> p c n", c=8)
                            pb, pbb = nbank()
                            pbs.append((pb, pbb))
                            for dc in range(8):
                                S.op("pe", lambda e: e.matmul(pb[:], lhsT=wv[:, dc, k * 128:(k + 1) * 128], rhs=hT[:, dc, :], start=(dc == 0), stop=(dc == 7)),
                                     reads=[wbf, b_hT, b_hT2], writes=[pbb])
                            m0 = 3 if OPT_PARTMAIN else 0
                            S.op("act", lambda e: e.activation(out=acc[k][:, m0:512], in_=pb[:, m0:512], func=AF.Identity, scale=wsel[:, c, 3:4], bias=mcb[:, c:c + 1]),
                                 reads=[pbb, CONST] + cb, writes=[b_acc[k]])
                            for col in (range(3) if OPT_TINY else ()):
                                S.op("act", lambda e: e.activation(out=acc[k][:, col:col + 1], in_=pb[:, col:col + 1], func=AF.Identity, scale=wsel[:, c, 3:4],
                                                                  bias=mbc[:, c, col:col + 1]), reads=[pbb, CONST, b_mbc] + cb, writes=[b_acc[k]])
                            if flagged:
                                S.op("act", lambda e: e.activation(out=mhalo[:, c, :], in_=pb[:, 509:512], func=AF.Copy, scale=flag[:, 0:1]), reads=[pbb, cb[4], b_mbc], writes=[b_mhalo])
                            else:
                                S.op("act", lambda e: e.copy(out=mhalo[:, c, :], in_=pb[:, 509:512]), reads=[pbb, b_mbc], writes=[b_mhalo])
                        for tp in (2, 1, 0):
                            sh = 3 - tp
                            for k, c in enumerate(grp):
                                pb, pbb = pbs[k]
                                S.op("dve", lambda e: e.scalar_tensor_tensor(out=acc[k][:, sh:512], in0=pb[:, 0:512 - sh], scalar=wsel[:, c, tp:tp + 1], in1=acc[k][:, sh:512],
                                                                            op0=ALU.mult, op1=ALU.add), reads=[pbb, b_acc[k], CONST], writes=[b_acc[k]])
                        for k, c in enumerate(grp):
                            if not OPT_SILU:
                                S.op("act", lambda e: e.activation(out=et[k][:], in_=acc[k][:], func=AF.Exp, scale=-1.0), reads=[b_acc[k]], writes=[b_et[k]])
                                sk = (128.0 ** 0.5) if c < 4 else 1.0
                                S.op("dve", lambda e: e.tensor_scalar(out=et[k][:], in0=et[k][:], scalar1=1.0, scalar2=sk, op0=ALU.add, op1=ALU.mult), reads=[b_et[k]], writes=[b_et[k]])
                                S.op("dve", lambda e: e.reciprocal(out=et[k][:], in_=et[k][:]), reads=[b_et[k]], writes=[b_et[k]])
                                S.op("dve", lambda e: e.tensor_tensor(out=mqT[:, c, :], in0=acc[k][:], in1=et[k][:], op=ALU.mult), reads=[b_acc[k], b_et[k]], writes=[b_mqT])
                            elif c < 4:
                                S.op("act", lambda e: e.activation(out=et[k][:], in_=acc[k][:], func=AF.Silu), reads=[b_acc[k]], writes=[b_et[k]])
                                S.op("dve", lambda e: e.tensor_scalar(out=mqT[:, c, :], in0=et[k][:], scalar1=128.0 ** -0.5, scalar2=None, op0=ALU.mult),
                                     reads=[b_et[k]], writes=[b_mqT])
                            else:
                                S.op("act", lambda e: e.activation(out=mqT[:, c, :], in_=acc[k][:], func=AF.Silu), reads=[b_acc[k]], writes=[b_mqT])
                    sq = [sbt(st, "sq%d" % i, [128, 512]) for i in range(4)]
                    qb = [sbt(st, "qb%d" % i, [128, 512], BF16) for i in range(4)]
                    rs = [sbt(st, "rs%d" % i, [128, 16]) for i in range(4)]
                    KTb = [sbt(st, "KTb%d" % i, [128, 4, 128], BF16) for i in range(4)]
                    VAb = [sbt(st, "VAb%d" % i, [128, 4, 130], BF16) for i in range(4)]
                    b_t = [S.buf("tmj%d" % i) for i in range(4)]
                    b_KTb = [S.buf("KTb%d" % i) for i in range(4)]
                    b_VAb = [S.buf("VAb%d" % i) for i in range(4)]
                    for i in range(4):
                        if flagged:
                            S.op("dve", lambda e: e.tensor_copy(out=VAb[i][:, :, 128:130], in_=flag[:, 0:1].unsqueeze(1).to_broadcast([128, 4, 2])),
                                 reads=[cb[4]], writes=[b_VAb[i]])
                        else:
                            S.op("dve", lambda e: e.memset(VAb[i][:, :, 128:130], 1.0), writes=[b_VAb[i]])
                    def tm_block(j):
                        blk = u * 4 + j
                        i = j
                        tsl = slice(j * 128, (j + 1) * 128)
                        bsel = [0]

                        def bk():
                            bi = 2 * j + (bsel[0] % 2)
                            bsel[0] += 1
                            return banks[bi], bbufs[bi]

                        def proj(nm):
                            wt, wbf = W[nm]
                            n = PIECES[nm][1] // 8
                            wv = wt[:].rearrange("p (c n) -> p c n", c=8)
                            pb, pbb = bk()
                            for dc in range(8):
                                S.op("pe", lambda e: e.matmul(pb[:, 0:n], lhsT=hT[:, dc, tsl], rhs=wv[:, dc, :], start=(dc == 0), stop=(dc == 7)),
                                     reads=[wbf, b_hT, b_hT2], writes=[pbb])
                            return pb, pbb

                        def evac_scaled(out_ap, in_ap, rd, wr):
                            if flagged:
                                S.op("act", lambda e: e.activation(out=out_ap, in_=in_ap, func=AF.Copy, scale=flag[:, 0:1]), reads=rd + [cb[4]], writes=wr)
                            else:
                                S.op("act", lambda e: e.copy(out=out_ap, in_=in_ap), reads=rd, writes=wr)

                        pb, pbb = proj("mv")
                        yield
                        evac_scaled(mVA[:, j, :, 0:128], pb[:].rearrange("p (h d) -> p h d", h=4), [pbb], [b_mVA])
                        pb, pbb = proj("gif")
                        yield
                        S.op("dve", lambda e: e.tensor_tensor(out=gif[:, j, :], in0=pb[:, 0:8], in1=gifb[:], op=ALU.add), reads=[pbb, cb[9]], writes=[b_gif])
                        pb, pbb = proj("av")
                        yield
                        evac_scaled(VAb[i][:, :, 0:128], pb[:].rearrange("p (h d) -> p h d", h=4), [pbb], [b_VAb[i]])
                        S.dma("pool", va_d.rearrange("h p (b c) -> p h b c", c=130)[:, :, blk, :], VAb[i][:], reads=[b_VAb[i]], writes=[b_vad])
                        if full:
                            pb, pbb = proj("mo")
                            yield
                            S.op("act", lambda e: e.activation(out=sigmo[:, j, :], in_=pb[:], func=AF.Exp, scale=-1.0), reads=[pbb], writes=[b_sigj[j]])
                            yield
                            S.op("dve", lambda e: e.tensor_scalar(out=sigmo[:, j, :], in0=sigmo[:, j, :], scalar1=1.0, scalar2=None, op0=ALU.add), reads=[b_sigj[j]], writes=[b_sigj[j]])
                            S.op("dve", lambda e: e.reciprocal(out=sigmo[:, j, :], in_=sigmo[:, j, :]), reads=[b_sigj[j]], writes=[b_sigj[j], b_sig])
                        for nm in (["aq", "ak"] if full else ["ak"]):
                            pb, pbb = proj(nm)
                            yield
                            S.op("act", lambda e: e.activation(out=sq[i][:], in_=pb[:], func=AF.Square, scale=0.125), reads=[pbb], writes=[b_t[i]])
                            yield
                            S.op("dve", lambda e: e.tensor_reduce(out=rs[i][:, 0:8], in_=sq[i][:].rearrange("p (g d) -> p g d", d=64), axis=AX.X, op=ALU.add),
                                 reads=[b_t[i]], writes=[b_t[i]])
                            yield
                            S.op("act", lambda e: e.activation(out=rs[i][:, 8:16], in_=rs[i][:, 0:8], func=AF.Ln, bias=epst[:, 0:1], scale=1.0), reads=[b_t[i], CONST], writes=[b_t[i]])
                            S.op("act", lambda e: e.activation(out=rs[i][:, 8:16], in_=rs[i][:, 8:16], func=AF.Exp, scale=-0.5), reads=[b_t[i]], writes=[b_t[i]])
                            yield
                            S.op("dve", lambda e: e.tensor_tensor(out=qb[i][:].rearrange("p (g d) -> p g d", d=64), in0=pb[:].rearrange("p (g d) -> p g d", d=64),
                                                                 in1=rs[i][:, 8:16].unsqueeze(2).to_broadcast([128, 8, 64]), op=ALU.mult), reads=[pbb, b_t[i]], writes=[b_t[i]])
                            yield
                            tb, tbb = bk()
                            tv = bfview(tb)[:, 0:512].rearrange("p (h t) -> p h t", h=4)
                            for h in range(4):
                                S.op("pe", lambda e: e.transpose(out=tv[:, h, :], in_=qb[i][:, h * 128:(h + 1) * 128], identity=identb[:]), reads=[b_t[i], cb[0]], writes=[tbb])
                            yield
                            if nm == "aq":
                                S.op("act", lambda e: e.activation(out=Qbd[0:64, :, j, 0:128], in_=tv[0:64, :, :], func=AF.Copy, scale=gqfm[0:64, 0:1]), reads=[tbb, CONST] + cb, writes=[b_Qbd])
                                S.op("act", lambda e: e.activation(out=Qbd[64:128, :, j, 128:256], in_=tv[64:128, :, :], func=AF.Copy, scale=gqfm[64:128, 0:1]), reads=[tbb, CONST] + cb, writes=[b_Qbd])
                            else:
                                S.op("act", lambda e: e.activation(out=KTb[i][:], in_=tv, func=AF.Copy, scale=gkfm[:, 0:1]), reads=[tbb] + cb, writes=[b_KTb[i]])
                                S.dma("pool", kt_d.rearrange("h p k -> p h k")[:, :, blk * 128:(blk + 1) * 128], KTb[i][:], reads=[b_KTb[i]], writes=[b_ktd])
                            yield

                    b_sigj = [S.buf("sigj%d" % j_) for j_ in range(4)]
                    tgens = [tm_block(j_) for j_ in range(4)]
                    alive = list(tgens)
                    while alive:
                        nxt = []
                        for g_ in alive:
                            try:
                                next(g_)
                                nxt.append(g_)
                            except StopIteration:
                                pass
                        alive = nxt
                    S.barrier()

                def mlstm_block(j, st, bankfn, cache):
                    def sbt_c(st_, name, shape, dt=F32):
                        if name not in cache:
                            cache[name] = sbt(st_, name, shape, dt)
                        return cache[name]

                    def buf_c(name):
                        k_ = "B_" + name
                        if k_ not in cache:
                            cache[k_] = S.buf(name)
                        return cache[k_]
                    tsl = slice(j * 128, (j + 1) * 128)
                    lf = sbt_c(st, "lf%d" % (j % 2), [128, 4])
                    e1 = sbt_c(st, "e1%d" % (j % 2), [128, 4])
                    e2 = sbt_c(st, "e2%d" % (j % 2), [128, 4])
                    wsx = sbt_c(st, "wsx%d" % (j % 2), [128, 4])
                    RL = sbt_c(st, "RL%d" % (j % 2), [128, 4, 128])
                    Et = sbt_c(st, "Et%d" % (j % 2), [128, 512])
                    Kw = sbt_c(st, "Kw%d" % (j % 2), [128, 4, 128], BF16)
                    bm = buf_c("ml%d" % (j % 2))
                    b_e1 = buf_c("e1%d" % (j % 2))
                    b_ws = buf_c("wsx%d" % (j % 2))
                    b_RL = buf_c("RL%d" % (j % 2))
                    b_Et = buf_c("Et%d" % (j % 2))
                    b_Kw = buf_c("Kw%d" % (j % 2))
                    S.op("act", lambda e: e.activation(out=lf[:], in_=gif[:, j, 4:8], func=AF.Exp, scale=-1.0), reads=[b_gif], writes=[bm])
                    yield
                    S.op("act", lambda e: e.activation(out=lf[:], in_=lf[:], func=AF.Ln, bias=onet[:, 0:1], scale=1.0), reads=[bm, CONST], writes=[bm])
                    yield
                    S.op("dve", lambda e: e.tensor_scalar(out=lf[:], in0=lf[:], scalar1=-1.0, scalar2=None, op0=ALU.mult), reads=[bm], writes=[bm])
                    yield
                    p1, p1b = bankfn()
                    S.op("pe", lambda e: e.matmul(p1[:, 0:4], lhsT=umask[:], rhs=lf[:], start=True, stop=True), reads=[bm, cb[2]], writes=[p1b])
                    S.op("dve", lambda e: e.tensor_tensor(out=RL[:], in0=umask[:].unsqueeze(1).to_broadcast([128, 4, 128]),
                                                         in1=lf[:].unsqueeze(2).to_broadcast([128, 4, 128]), op=ALU.mult), reads=[bm, cb[2]], writes=[b_RL])
                    yield
                    S.op("dve", lambda e: e.tensor_tensor(out=e1[:], in0=gif[:, j, 0:4], in1=p1[:, 0:4], op=ALU.subtract), reads=[b_gif, p1b], writes=[b_e1])
                    yield
                    RLf = RL[:].rearrange("p h t -> p (h t)")
                    pBc, pBcb = bankfn()
                    S.op("pe", lambda e: e.matmul(pBc[:], lhsT=ones_f[:], rhs=RLf, start=True, stop=True), reads=[b_RL, CONST], writes=[pBcb])
                    yield
                    blast = pBc[:].rearrange("p (h t) -> p h t", h=4)[:, :, 127]
                    S.op("dve", lambda e: e.tensor_tensor(out=e2[:], in0=e1[:], in1=blast, op=ALU.add), reads=[b_e1, pBcb], writes=[b_ws])
                    yield
                    S.op("act", lambda e: e.activation(out=Et[:], in_=pBc[:], func=AF.Exp), reads=[pBcb], writes=[b_Et])
                    S.op("act", lambda e: e.activation(out=wsx[:], in_=e2[:], func=AF.Exp), reads=[b_ws], writes=[b_ws])
                    yield
                    tb, tbb = bankfn()
                    tv = bfview(tb)[:, 0:512].rearrange("p (h t) -> p h t", h=4)
                    for h in range(4):
                        S.op("pe", lambda e: e.transpose(out=tv[:, h, :], in_=mqT[:, 4 + h, tsl], identity=identb[:]), reads=[b_mqT, cb[0]], writes=[tbb])
                    yield
                    S.op("dve", lambda e: e.tensor_tensor(out=Kw[:], in0=tv, in1=wsx[:].unsqueeze(2).to_broadcast([128, 4, 128]), op=ALU.mult),
                         reads=[tbb, b_ws], writes=[b_Kw])
                    yield
                    if full:
                        DT = sbt_c(st, "DT%d" % (j % 2), [128, 4, 128])
                        PT = sbt_c(st, "PT%d" % (j % 2), [128, 4, 128], BF16)
                        qpT = sbt_c(st, "qpT%d" % (j % 2), [128, 4, 128], BF16)
                        numS = sbt_c(st, "numS%d" % (j % 2), [128, 4, 130])
                        rden = sbt_c(st, "rden%d" % (j % 2), [128, 4])
                        hmr = sbt_c(st, "hmr%d" % (j % 2), [128, 4, 128])
                        hsq = sbt_c(st, "hsq%d" % (j % 2), [128, 4, 128])
                        hss = sbt_c(st, "hss%d" % (j % 2), [128, 8])
                        gs = sbt_c(st, "gs%d" % (j % 2), [128, 512])
                        hmf = sbt_c(st, "hmf%d" % (j % 2), [128, 4, 128], BF16)
                        b_o = buf_c("mo%d" % (j % 2))
                        b_gs = buf_c("gs%d" % (j % 2))
                        b_num = buf_c("numS%d" % (j % 2))
                        b_DT = buf_c("DT%d" % (j % 2))
                        b_PT = buf_c("PT%d" % (j % 2))
                        b_qp = buf_c("qpT%d" % (j % 2))
                        pBm, pBmb = bankfn()
                        S.op("pe", lambda e: e.matmul(pBm[:], lhsT=ones_f[:], rhs=RLf, start=True, stop=False), reads=[b_RL, CONST], writes=[pBmb])
                        S.op("pe", lambda e: e.matmul(pBm[:], lhsT=identf[:], rhs=negm4[:], start=False, stop=True), reads=[cb[1], cb[3]], writes=[pBmb])
                        S.op("dve", lambda e: e.tensor_tensor(out=qpT[:], in0=mqT[:, 0:4, tsl], in1=Et[:].rearrange("p (h t) -> p h t", h=4), op=ALU.mult),
                             reads=[b_mqT, b_Et], writes=[b_qp])
                        S.op("dve", lambda e: e.tensor_tensor(out=gs[:], in0=mng[:], in1=sigmo[:, j, :], op=ALU.mult), reads=[b_sig] + cb, writes=[b_gs])
                        yield
                        for h in range(4):
                            S.op("act", lambda e: e.activation(out=DT[:, h, :], in_=pBm[:, h * 128:(h + 1) * 128], func=AF.Exp, bias=e1[:, h:h + 1], scale=1.0),
                                 reads=[pBmb, b_e1], writes=[b_DT])
                        yield
                        pA, pAb = bankfn()
                        for h in range(4):
                            S.op("pe", lambda e: e.matmul(pA[:, h * 128:(h + 1) * 128], lhsT=mqT[:, 4 + h, tsl], rhs=mqT[:, h, tsl], start=True, stop=True),
                                 reads=[b_mqT], writes=[pAb])
                        yield
                        S.op("dve", lambda e: e.tensor_tensor(out=PT[:], in0=pA[:].rearrange("p (h t) -> p h t", h=4), in1=DT[:], op=ALU.mult),
                             reads=[pAb, b_DT], writes=[b_PT])
                        yield
                    yield "AB"
                    if full:
                        for hp in range(2):
                            pn, pnb = bankfn()
                            for hh in range(2):
                                h = 2 * hp + hh
                                o = hh * 130
                                S.op("pe", lambda e: e.matmul(pn[:, o:o + 130], lhsT=PT[:, h, :], rhs=mVA[:, j, h, :], start=True, stop=False), reads=[b_PT, b_mVA], writes=[pnb])
                                S.op("pe", lambda e: e.matmul(pn[:, o:o + 130], lhsT=qpT[:, h, :], rhs=Sstb[:, h, :], start=False, stop=True), reads=[b_qp, b_Sb], writes=[pnb])
                            yield
                            S.op("dve", lambda e: e.tensor_copy(out=numS[:, 2 * hp:2 * hp + 2, :], in_=pn[:, 0:260].rearrange("p (h c) -> p h c", h=2)), reads=[pnb], writes=[b_num])
                            yield
                    Ev = Et[:].rearrange("p (h t) -> p h t", h=4)
                    for hp in range(2):
                        pc_, pcb = bankfn()
                        for hh in range(2):
                            h = 2 * hp + hh
                            o = hh * 130
                            S.op("pe", lambda e: e.matmul(pc_[:, o:o + 130], lhsT=Kw[:, h, :], rhs=mVA[:, j, h, :], start=True, stop=True), reads=[b_Kw, b_mVA], writes=[pcb])
                        yield
                        for hh in range(2):
                            h = 2 * hp + hh
                            o = hh * 130
                            S.op("dve", lambda e: e.scalar_tensor_tensor(out=Sst[:, h, :], in0=Sst[:, h, :], scalar=Ev[:, h, 127:128], in1=pc_[:, o:o + 130],
                                                                        op0=ALU.mult, op1=ALU.add), reads=[b_S, b_Et, pcb], writes=[b_S])
                        yield
                    S.op("dve", lambda e: e.tensor_copy(out=Sstb[:], in_=Sst[:]), reads=[b_S], writes=[b_Sb])
                    yield "S"
                    if full:
                        S.op("act", lambda e: e.activation(out=rden[:], in_=numS[:, :, 128], func=AF.Abs), reads=[b_num], writes=[b_o])
                        yield
                        S.op("dve", lambda e: e.tensor_scalar(out=rden[:], in0=rden[:], scalar1=1.0, scalar2=None, op0=ALU.max), reads=[b_o], writes=[b_o])
                        S.op("dve", lambda e: e.reciprocal(out=rden[:], in_=rden[:]), reads=[b_o], writes=[b_o])
                        S.op("dve", lambda e: e.tensor_tensor(out=hmr[:], in0=numS[:, :, 0:128], in1=rden[:].unsqueeze(2).to_broadcast([128, 4, 128]), op=ALU.mult),
                             reads=[b_num, b_o], writes=[b_o])
                        yield
                        S.op("act", lambda e: e.activation(out=hsq[:], in_=hmr[:], func=AF.Square, scale=1.0 / math.sqrt(128.0)), reads=[b_o], writes=[b_o])
                        yield
                        S.op("dve", lambda e: e.tensor_reduce(out=hss[:, 0:4], in_=hsq[:], axis=AX.X, op=ALU.add), reads=[b_o], writes=[b_o])
                        yield
                        S.op("act", lambda e: e.activation(out=hss[:, 4:8], in_=hss[:, 0:4], func=AF.Ln, bias=epst[:, 0:1], scale=1.0), reads=[b_o, CONST], writes=[b_o])
                        S.op("act", lambda e: e.activation(out=hss[:, 4:8], in_=hss[:, 4:8], func=AF.Exp, scale=-0.5), reads=[b_o], writes=[b_o])
                        yield
                        for h in range(4):
                            S.op("dve", lambda e: e.scalar_tensor_tensor(out=hmf[:, h, :], in0=hmr[:, h, :], scalar=hss[:, 4 + h:5 + h], in1=gs[:, h * 128:(h + 1) * 128],
                                                                        op0=ALU.mult, op1=ALU.mult), reads=[b_o, b_gs], writes=[b_o])
                        yield
                        tb2, tb2b = bankfn()
                        tv2 = bfview(tb2)[:, 0:512].rearrange("p (h t) -> p h t", h=4)
                        for h in range(4):
                            S.op("pe", lambda e: e.transpose(out=tv2[:, h, :], in_=hmf[:, h, :], identity=identb[:]), reads=[b_o, cb[0]], writes=[tb2b])
                        yield
                        S.op("dve", lambda e: e.tensor_copy(out=hmT[:, :, tsl], in_=tv2), reads=[tb2b], writes=[b_hmT])
                        yield

                def mlstm_driver(st, bankfn):
                    cache = {}
                    for j in range(4):
                        for r in mlstm_block(j, st, bankfn, cache):
                            yield

                if not full:
                    with ExitStack() as st:
                        for _ in mlstm_driver(st, nbank):
                            pass
                        S.barrier()

                if full:
                    with ExitStack() as st:
                        NKC = 8
                        kch = [sbt(st, "kch%d" % i, [128, NKC * 128], BF16) for i in range(3)]
                        vch = [sbt(st, "vch%d" % i, [128, NKC, 130], BF16) for i in range(3)]
                        b_kch = [S.buf("kch%d" % i) for i in range(3)]
                        b_vch = [S.buf("vch%d" % i) for i in range(3)]
                        pex = [sbt(st, "pex%d" % i, [128, 512], BF16) for i in range(6)]
                        b_pex = [S.buf("pex%d" % i) for i in range(6)]
                        rr = sbt(st, "rr", [128, 8])
                        har = sbt(st, "har", [128, 4, 128])
                        hsq = sbt(st, "ahsq", [128, 4, 128])
                        hss = sbt(st, "ahss", [128, 8])
                        haf = sbt(st, "haf", [128, 4, 128], BF16)
                        b_a = S.buf("attn_o")
                        nkb_tot = u * 4 + 4
                        chunk_list = [(s0, min(NKC, nkb_tot - s0)) for s0 in range(0, nkb_tot, NKC)]
                        ring = 0
                        LCH = len(chunk_list)
                        prefetch((0, 1), ["bm", "gm0"])
                        mdrv = mlstm_driver(st, lambda: (banks[7], bbufs[7]))
                        n_items_est = 4 * sum((2 if kb_ <= u * 4 - 2 else 0) + sum(1 for p2 in range(2) for jq in (2 * p2, 2 * p2 + 1) if kb_ > u * 4 + 2 * p2 - 2 and kb_ <= u * 4 + jq)
                                              for kb_ in range(nkb_tot))
                        MSTEP = max(1, n_items_est // 150)
                        gcount = [0]
                        loaded = set()

                        def ensure_chunk(g):
                            if g >= 4 * LCH or g in loaded:
                                return
                            loaded.add(g)
                            h_, ci_ = g // LCH, g % LCH
                            s0, nk = chunk_list[ci_]
                            ri = g % 3
                            S.dma("sp", kch[ri][:, 0:nk * 128], kt_d[h_, :, s0 * 128:(s0 + nk) * 128], reads=[b_ktd], writes=[b_kch[ri]])
                            S.dma("sp", vch[ri][:, 0:nk, :], va_d[h_, :, s0 * 130:(s0 + nk) * 130].rearrange("p (b c) -> p b c", c=130), reads=[b_vad], writes=[b_vch[ri]])

                        accS = [sbt(st, "accS%d" % i, [128, 4, 2, 130]) for i in range(2)]
                        b_accS = [S.buf("accS%d" % i) for i in range(2)]
                        rr2 = [sbt(st, "rr2_%d" % i, [128, 8]) for i in range(2)]
                        har2 = [sbt(st, "har2_%d" % i, [128, 4, 128]) for i in range(2)]
                        hsq2 = [sbt(st, "hsq2_%d" % i, [128, 4, 128]) for i in range(2)]
                        hss2 = [sbt(st, "hss2_%d" % i, [128, 8]) for i in range(2)]
                        haf2 = [sbt(st, "haf2_%d" % i, [128, 4, 128], BF16) for i in range(2)]
                        b_ep = [S.buf("epi%d" % i) for i in range(2)]

                        def attn_epi(h, accs):
                            p = h % 2
                            aS, rr, har, hsq, hss, haf, b_a = accS[p], rr2[p], har2[p], hsq2[p], hss2[p], haf2[p], b_ep[p]
                            for jq in range(4):
                                ab, abb = accs[jq]
                                S.op("dve", lambda e: e.tensor_copy(out=aS[:, jq, :, :], in_=ab[:, 0:260].rearrange("p (m c) -> p m c", m=2)), reads=[abb], writes=[b_accS[p]])
                            yield
                            for jq in range(4):
                                av = aS[:, jq, :, :]
                                S.op("dve", lambda e: e.tensor_scalar(out=rr[:, 2 * jq:2 * jq + 2], in0=av[:, :, 128], scalar1=1e-30, scalar2=None, op0=ALU.max), reads=[b_accS[p]], writes=[b_a])
                                S.op("dve", lambda e: e.reciprocal(out=rr[:, 2 * jq:2 * jq + 2], in_=rr[:, 2 * jq:2 * jq + 2]), reads=[b_a], writes=[b_a])
                                S.op("dve", lambda e: e.tensor_tensor(out=rr[:, 2 * jq + 1:2 * jq + 2], in0=rr[:, 2 * jq + 1:2 * jq + 2], in1=neglam[:], op=ALU.mult),
                                     reads=[b_a, CONST], writes=[b_a])
                                S.op("dve", lambda e: e.tensor_scalar(out=har[:, jq, :], in0=av[:, 0, 0:128], scalar1=rr[:, 2 * jq:2 * jq + 1], scalar2=None, op0=ALU.mult),
                                     reads=[b_accS[p], b_a], writes=[b_a])
                                S.op("dve", lambda e: e.scalar_tensor_tensor(out=har[:, jq, :], in0=av[:, 1, 0:128], scalar=rr[:, 2 * jq + 1:2 * jq + 2], in1=har[:, jq, :],
                                                                            op0=ALU.mult, op1=ALU.add), reads=[b_accS[p], b_a], writes=[b_a])
                                yield
                            S.op("act", lambda e: e.activation(out=hsq[:], in_=har[:], func=AF.Square, scale=1.0 / math.sqrt(128.0)), reads=[b_a], writes=[b_a])
                            yield
                            S.op("dve", lambda e: e.tensor_reduce(out=hss[:, 0:4], in_=hsq[:], axis=AX.X, op=ALU.add), reads=[b_a], writes=[b_a])
                            yield
                            S.op("act", lambda e: e.activation(out=hss[:, 4:8], in_=hss[:, 0:4], func=AF.Ln, bias=epst[:, 0:1], scale=1.0), reads=[b_a, CONST], writes=[b_a])
                            S.op("act", lambda e: e.activation(out=hss[:, 4:8], in_=hss[:, 4:8], func=AF.Exp, scale=-0.5), reads=[b_a], writes=[b_a])
                            yield
                            for jq in range(4):
                                S.op("dve", lambda e: e.scalar_tensor_tensor(out=haf[:, jq, :], in0=har[:, jq, :], scalar=hss[:, 4 + jq:5 + jq], in1=ang[:, h * 128:(h + 1) * 128],
                                                                            op0=ALU.mult, op1=ALU.mult), reads=[b_a, CONST] + cb, writes=[b_a])
                            yield
                            tb, tbb = banks[4 + h % 3], bbufs[4 + h % 3]
                            tv = bfview(tb)[:, 0:512].rearrange("p (j t) -> p j t", j=4)
                            for jq in range(4):
                                S.op("pe", lambda e: e.transpose(out=tv[:, jq, :], in_=haf[:, jq, :], identity=identb[:]), reads=[b_a, cb[0]], writes=[tbb])
                            S.op("dve", lambda e: e.tensor_copy(out=haT[:, h, :], in_=tv.rearrange("p j t -> p (j t)")), reads=[tbb], writes=[b_haT])
                            yield

                        epi = [iter(())]
                        for h in range(4):
                            accs = [(banks[jq], bbufs[jq]) for jq in range(4)]
                            for jq in range(4):
                                S.op("dve", lambda e: e.memset(accs[jq][0][:, 0:260], 0.0), writes=[accs[jq][1]])
                            items = []
                            for ci, (s0, nk) in enumerate(chunk_list):
                                g = h * LCH + ci
                                ri = g % 3
                                for kk in range(nk):
                                    kb = s0 + kk
                                    for p2 in range(2):
                                        j0 = 2 * p2
                                        if u == NCTX:
                                            if p2 == 1 and kb <= u * 4 + 3:
                                                items.append((ri, kk, kb, (3,), g))
                                            continue
                                        if kb <= u * 4 + j0 - 2:
                                            items.append((ri, kk, kb, (j0, j0 + 1), g))
                                        else:
                                            for jq in (j0, j0 + 1):
                                                if kb <= u * 4 + jq:
                                                    items.append((ri, kk, kb, (jq,), g))

                            LAG = 3
                            NS = 3
                            NPX = len(pex)

                            def emit_scores(it, idx):
                                ri, kk, kb, jqs, ci = it
                                ensure_chunk(ci)
                                ensure_chunk(ci + 1)
                                nq = len(jqs)
                                sl = idx % NS
                                sp_, spb = banks[4 + sl], bbufs[4 + sl]
                                sv = sp_[:, 0:256 * nq]
                                px = idx % NPX
                                delta = u * 4 + jqs[0] - kb
                                near = (nq == 1) and delta <= 1
                                rhs = Qbd[:, h, jqs[0]:jqs[0] + nq, :].rearrange("p j c -> p (j c)")
                                S.op("pe", lambda e: e.matmul(sv, lhsT=kch[ri][:, kk * 128:(kk + 1) * 128], rhs=rhs, start=True, stop=not near),
                                     reads=[b_kch[ri], b_Qbd], writes=[spb])
                                if near:
                                    S.op("pe", lambda e: e.matmul(sv, lhsT=identb[:], rhs=biasb[:, h, delta, :], start=False, stop=True), reads=[CONST, cb[0]], writes=[spb])
                                    S.op("act", lambda e: e.activation(out=pex[px][:, 0:256 * nq], in_=sv, func=AF.Exp), reads=[spb], writes=[b_pex[px]])
                                else:
                                    S.op("act", lambda e: e.activation(out=pex[px][:, 0:256 * nq], in_=sv, func=AF.Exp, bias=farb[:, h:h + 1], scale=1.0),
                                         reads=[spb, cb[14]], writes=[b_pex[px]])

                            def emit_pv(it, idx):
                                ri, kk, kb, jqs, ci = it
                                px = idx % NPX
                                for a_, jq in enumerate(jqs):
                                    ab, abb = accs[jq]
                                    for m in range(2):
                                        c0 = a_ * 256 + m * 128
                                        S.op("pe", lambda e: e.matmul(ab[:, m * 130:(m + 1) * 130], lhsT=pex[px][:, c0:c0 + 128], rhs=vch[ri][:, kk, :],
                                                                     start=False, stop=False, skip_group_check=True), reads=[b_pex[px], b_vch[ri]], writes=[abb])

                            for idx in range(len(items) + LAG):
                                if idx < len(items):
                                    emit_scores(items[idx], idx)
                                if idx >= LAG:
                                    emit_pv(items[idx - LAG], idx - LAG)
                                gcount[0] += 1
                                if gcount[0] % MSTEP == 0:
                                    next(mdrv, None)
                                if idx % 6 == 5:
                                    next(epi[0], None)
                            for _ in epi[0]:
                                pass
                            epi[0] = attn_epi(h, accs)
                            next(epi[0], None)
                        for _ in epi[0]:
                            pass
                        for _ in mdrv:
                            pass
                        S.barrier()

                    h2T = hT
                    b_h2T = b_hT
                    with ExitStack() as st:
                        W = {}
                        for nm in ("bm", "gm0", "ba", "ga0", "gm1", "ga1", "out0", "out1"):
                            W[nm] = wload(st, "w_" + nm, nm)
                        prefetch((2, 3), ["up0", "up1"])
                        yT = sbt(st, "yT", [128, 8, 512], BF16)
                        b_yT = S.buf("yT")
                        sg = [[sbt(st, "sg%d_%d" % (i, a), [128, 512]) for a in range(2)] for i in range(2)]
                        ty = [[sbt(st, "ty%d_%d" % (i, a), [128, 512]) for a in range(2)] for i in range(2)]
                        b_sg = [S.buf("sg%d" % i) for i in range(2)]
                        for c in range(8):
                            i = c % 2
                            res = []
                            for a, (bw, gw, srcT, b_src) in enumerate((("bm", "gm", hmT, b_hmT), ("ba", "ga", haT, b_haT))):
                                wt, wbf = W[bw]
                                wv = wt[:].rearrange("p (k n) -> p k n", k=4)
                                py, pyb = nbank()
                                for k in range(4):
                                    S.op("pe", lambda e: e.matmul(py[:], lhsT=wv[:, k, c * 128:(c + 1) * 128], rhs=srcT[:, k, :], start=(k == 0), stop=(k == 3)),
                                         reads=[wbf, b_src], writes=[pyb])
                                gt_, gbf = W[gw + str(c // 4)]
                                gv = gt_[:].rearrange("p (c n) -> p c n", c=8)
                                pg, pgb = nbank()
                                for dc in range(8):
                                    S.op("pe", lambda e: e.matmul(pg[:], lhsT=gv[:, dc, (c % 4) * 128:(c % 4 + 1) * 128], rhs=hT[:, dc, :], start=(dc == 0), stop=(dc == 7)),
                                         reads=[gbf, b_hT, b_hT2], writes=[pgb])
                                S.op("act", lambda e: e.activation(out=sg[i][a][:], in_=pg[:], func=AF.Sigmoid), reads=[pgb], writes=[b_sg[i]])
                                S.op("dve", lambda e: e.tensor_tensor(out=ty[i][a][:], in0=py[:], in1=sg[i][a][:], op=ALU.mult), reads=[pyb, b_sg[i]], writes=[b_sg[i]])
                            S.op("dve", lambda e: e.tensor_tensor(out=yT[:, c, :], in0=ty[i][0][:], in1=ty[i][1][:], op=ALU.add), reads=[b_sg[i]], writes=[b_yT])
                        tx = [sbt(st, "tx%d" % i, [128, 512]) for i in range(2)]
                        b_tx = [S.buf("tx%d" % i) for i in range(2)]
                        for j in range(4):
                            tsl = slice(j * 128, (j + 1) * 128)
                            for n in range(2):
                                wt, wbf = W["out%d" % n]
                                wv = wt[:].rearrange("p (c n) -> p c n", c=8)
                                po, pob = nbank()
                                for c in range(8):
                                    S.op("pe", lambda e: e.matmul(po[:], lhsT=yT[:, c, tsl], rhs=wv[:, c, :], start=(c == 0), stop=(c == 7)), reads=[wbf, b_yT], writes=[pob])
                                i = (j * 2 + n) % 2
                                S.op("dve", lambda e: e.tensor_tensor(out=tx[i][:], in0=po[:], in1=gate1[:, n * 512:(n + 1) * 512], op=ALU.mult), reads=[pob, CONST], writes=[b_tx[i]])
                                S.op("dve", lambda e: e.tensor_tensor(out=xs[:, j, n * 512:(n + 1) * 512], in0=xs[:, j, n * 512:(n + 1) * 512], in1=tx[i][:], op=ALU.add),
                                     reads=[b_tx[i], b_xs[j]], writes=[b_xs[j]])
                        run_norms(st, mult2, shift2, h2T, (b_hT, b_hT2), "n2")
                        S.barrier()

                    sa.close()
                    with ExitStack() as st:
                        actT = sbt(st, "actT", [128, NF, 512], BF16)
                        b_actT = S.buf("actT")
                        wu = [None, None, None]
                        uu = [sbt(st, "fu%d" % i, [128, 512]) for i in range(8)]
                        b_uu = [S.buf("fu%d" % i) for i in range(8)]
                        fe = [sbt(st, "fe%d" % i, [128, 512]) for i in range(2)]
                        b_fe = [S.buf("fe%d" % i) for i in range(2)]
                        wus = [pf[2], pf[3], sbt(st, "wup2", [128, 4096], BF16)]
                        b_wus = [b_pf[2], b_pf[3], S.buf("wup2")]

                        def ldup(jj):
                            if ("up%d" % jj) in prefetched:
                                prefetched.pop("up%d" % jj)
                                return
                            off, n = PIECES["up%d" % jj]
                            S.dma("sp", wus[jj % 3][:], wb_d[:, off:off + n], reads=[b_wb], writes=[b_wus[jj % 3]])

                        ldup(0)
                        ldup(1)
                        wd = sbt(st, "w_down", [128, NF * 1024], BF16)
                        wdb = S.buf("w_down")
                        wdv = wd[:].rearrange("p (f n) -> p f n", f=NF)
                        if not OPT_BATCH:
                            S.op("dve", lambda e: e.memset(fbc[:], 0.0), writes=[b_fbc])
                        if OPT_BATCH:
                            S.op("dve", lambda e: e.tensor_tensor(out=fbc[:, :, 1], in0=fhalo[:, :, 1], in1=fcw[:, :, 0], op=ALU.mult), reads=[b_fhalo] + cb, writes=[b_fbc])
                            S.op("dve", lambda e: e.tensor_tensor(out=fbc[:, :, 0], in0=fhalo[:, :, 1], in1=fcw[:, :, 1], op=ALU.mult), reads=[b_fhalo] + cb, writes=[b_fbc])
                            S.op("dve", lambda e: e.tensor_tensor(out=ctmp[:], in0=fhalo[:, :, 0], in1=fcw[:, :, 0], op=ALU.mult), reads=[b_fhalo] + cb, writes=[b_fbc])
                            S.op("dve", lambda e: e.tensor_tensor(out=fbc[:, :, 0], in0=fbc[:, :, 0], in1=ctmp[:], op=ALU.add), reads=[b_fbc], writes=[b_fbc])
                            S.op("dve", lambda e: e.tensor_tensor(out=fbc[:], in0=fbc[:], in1=fcb[:].unsqueeze(2).to_broadcast([128, 44, 2]), op=ALU.add), reads=[b_fbc] + cb, writes=[b_fbc])

                        def ffn_piece(jj):
                            if jj + 2 < 11:
                                ldup(jj + 2)
                            if jj == 2:
                                off_d, n_d = PIECES["down"]
                                S.dma("sp", wd[:], wb_d[:, off_d:off_d + n_d], reads=[b_wb], writes=[wdb])
                            if jj == 4 and u + 1 < NU:
                                prefetch((0, 1), ["mq", "mk"])
                            wv = wus[jj % 3][:].rearrange("p (c n) -> p c n", c=8)
                            wbf = b_wus[jj % 3]
                            wsel = fcwf if flagged else fcw
                            pbs = []
                            for k in range(4):
                                q = jj * 4 + k
                                pb, pbb = nbank()
                                pbs.append((pb, pbb))
                                for dc in range(8):
                                    S.op("pe", lambda e: e.matmul(pb[:], lhsT=wv[:, dc, k * 128:(k + 1) * 128], rhs=h2T[:, dc, :], start=(dc == 0), stop=(dc == 7)),
                                         reads=[wbf, b_hT, b_hT2], writes=[pbb])
                                kk_ = (jj % 2) * 4 + k
                                m0 = 2 if OPT_PARTMAIN else 0
                                S.op("act", lambda e: e.activation(out=uu[kk_][:, m0:512], in_=pb[:, m0:512], func=AF.Identity, scale=wsel[:, q, 2:3], bias=fcb[:, q:q + 1]),
                                     reads=[pbb, CONST] + cb, writes=[b_uu[kk_]])
                                for col in (range(2) if OPT_TINY else ()):
                                    S.op("act", lambda e: e.activation(out=uu[kk_][:, col:col + 1], in_=pb[:, col:col + 1], func=AF.Identity, scale=wsel[:, q, 2:3],
                                                                      bias=fbc[:, q, col:col + 1]), reads=[pbb, CONST, b_fbc] + cb, writes=[b_uu[kk_]])
                                if flagged:
                                    S.op("act", lambda e: e.activation(out=fhalo[:, q, :], in_=pb[:, 510:512], func=AF.Copy, scale=flag[:, 0:1]), reads=[pbb, cb[4], b_fbc], writes=[b_fhalo])
                                else:
                                    S.op("act", lambda e: e.copy(out=fhalo[:, q, :], in_=pb[:, 510:512]), reads=[pbb, b_fbc], writes=[b_fhalo])
                            for tp in (1, 0):
                                sh = 2 - tp
                                for k in range(4):
                                    q = jj * 4 + k
                                    kk_ = (jj % 2) * 4 + k
                                    pb, pbb = pbs[k]
                                    S.op("dve", lambda e: e.scalar_tensor_tensor(out=uu[kk_][:, sh:512], in0=pb[:, 0:512 - sh], scalar=wsel[:, q, tp:tp + 1], in1=uu[kk_][:, sh:512],
                                                                                op0=ALU.mult, op1=ALU.add), reads=[pbb, b_uu[kk_], CONST], writes=[b_uu[kk_]])
                            yield
                            for i in range(2):
                                f = 2 * jj + i
                                uv = uu[(jj % 2) * 4 + i]
                                ug = uu[(jj % 2) * 4 + 2 + i]
                                b_uv = b_uu[(jj % 2) * 4 + i]
                                b_ug = b_uu[(jj % 2) * 4 + 2 + i]
                                S.op("act", lambda e: e.activation(out=fe[i][:], in_=ug[:], func=AF.Silu), reads=[b_ug], writes=[b_fe[i]])
                                S.op("dve", lambda e: e.tensor_tensor(out=actT[:, f, :], in0=uv[:], in1=fe[i][:], op=ALU.mult), reads=[b_uv, b_fe[i]], writes=[b_actT])
                        fgens = [ffn_piece(jj) for jj in range(11)]
                        for step in range(12):
                            if step < 11:
                                next(fgens[step], None)
                            if step >= 1:
                                next(fgens[step - 1], None)
                        tx = fe
                        b_tx = b_fe
                        for j in range(4):
                            tsl = slice(j * 128, (j + 1) * 128)
                            blk = u * 4 + j
                            for n in range(2):
                                po, pob = nbank()
                                for f in range(NF):
                                    S.op("pe", lambda e: e.matmul(po[:], lhsT=actT[:, f, tsl], rhs=wdv[:, f, n * 512:(n + 1) * 512], start=(f == 0), stop=(f == NF - 1)),
                                         reads=[wdb, b_actT], writes=[pob])
                                i = (j * 2 + n) % 2
                                S.op("dve", lambda e: e.tensor_tensor(out=tx[i][:], in0=po[:], in1=gate2[:, n * 512:(n + 1) * 512], op=ALU.mult), reads=[pob, CONST], writes=[b_tx[i]])
                                S.op("dve", lambda e: e.tensor_tensor(out=xs[:, j, n * 512:(n + 1) * 512], in0=xs[:, j, n * 512:(n + 1) * 512], in1=tx[i][:], op=ALU.add),
                                     reads=[b_tx[i], b_xs[j]], writes=[b_xs[j]])
                            if own:
                                ob = blk - NFLAG * 4
                                S.dma("pool", out_d[ob * 128:(ob + 1) * 128, :], xs[:, j, :], reads=[b_xs[j]], writes=[b_out])
                        S.barrier()
        S.barrier()
    return nc


def _t5_bucket(n):
    n = np.maximum(n, 0)
    max_exact = 16
    nf = np.maximum(n, 1).astype(np.float32)
    large = max_exact + (np.log(nf / np.float32(max_exact)) / np.float32(math.log(128 / max_exact)) * np.float32(32 - max_exact)).astype(np.int32)
    large = np.minimum(large, 31)
    return np.where(n < max_exact, n, large)


def _fm(v, nch):
    return np.ascontiguousarray(np.asarray(v, np.float32).reshape(nch, 128).T)


def _piece_fm(w):
    n = w.shape[1]
    return w.reshape(8, 128, n).transpose(1, 0, 2).reshape(128, 8 * n)


def prepare_inputs(NCTX, NFULL, inputs):
    f32 = np.float32
    g = {k: np.asarray(v) for k, v in inputs.items()}
    NU = NCTX + NFULL
    half_tok = (NU // 2) * 512
    x = g["x"].astype(f32, copy=False)
    B = x.shape[0]
    assert x.shape[1] == 2 * half_tok
    w_in = g["w_in"][0]
    cols = {}
    o = 0
    for nm, n in (("mqk", 1024), ("mv", 512), ("mo", 512), ("mi", 4), ("mf", 4), ("aq", 512), ("ak", 512), ("av", 512), ("gm", 1024), ("ga", 1024)):
        cols[nm] = w_in[:, o:o + n]
        o += n

    def perm_qk(w):
        return w.reshape(1024, 2, 4, 64).transpose(0, 2, 1, 3).reshape(1024, 512)

    wall = np.zeros((128, WPAD), f32)

    def put(nm, arr):
        off, n = PIECES[nm]
        assert arr.shape == (128, n), (nm, arr.shape, n)
        wall[:, off:off + n] = arr

    put("mq", _piece_fm(cols["mqk"][:, 0:512]))
    put("mk", _piece_fm(cols["mqk"][:, 512:1024]))
    put("mv", _piece_fm(cols["mv"]))
    put("mo", _piece_fm(cols["mo"]))
    put("aq", _piece_fm(perm_qk(cols["aq"])))
    put("ak", _piece_fm(perm_qk(cols["ak"])))
    put("av", _piece_fm(cols["av"]))
    put("gm0", _piece_fm(cols["gm"][:, 0:512]))
    put("gm1", _piece_fm(cols["gm"][:, 512:1024]))
    put("ga0", _piece_fm(cols["ga"][:, 0:512]))
    put("ga1", _piece_fm(cols["ga"][:, 512:1024]))
    put("gif", _piece_fm(np.concatenate([cols["mi"], cols["mf"]], axis=1)))
    put("bm", g["w_branch_m"][0].reshape(4, 128, 1024).transpose(1, 0, 2).reshape(128, 4096))
    put("ba", g["w_branch_a"][0].reshape(4, 128, 1024).transpose(1, 0, 2).reshape(128, 4096))
    put("out0", _piece_fm(g["w_out"][0][:, 0:512]))
    put("out1", _piece_fm(g["w_out"][0][:, 512:1024]))
    w_up = g["w_up"][0]
    fcw_full = g["ffn_conv_w"][0]
    fcb_full = g["ffn_conv_b"][0]
    chunk_cols = []
    for jj in range(11):
        cc = [np.arange((2 * jj + i) * 128, (2 * jj + i + 1) * 128) for i in range(2)]
        cc += [DFF + np.arange((2 * jj + i) * 128, (2 * jj + i + 1) * 128) for i in range(2)]
        idx = np.concatenate(cc)
        chunk_cols.append(idx)
        put("up%d" % jj, _piece_fm(w_up[:, idx]))
    allidx = np.concatenate(chunk_cols)
    fcw = fcw_full[:, allidx].reshape(3, 44, 128).transpose(2, 1, 0).reshape(128, 44 * 3)
    fcb = fcb_full[allidx].reshape(44, 128).T
    put("down", g["w_down"][0].reshape(NF, 128, 1024).transpose(1, 0, 2).reshape(128, NF * 1024))

    w_ada = g["w_ada"][0]
    wada = np.stack([_piece_fm(w_ada[:, p * 512:(p + 1) * 512]) for p in range(12)]).astype(f32)
    bada = g["b_ada"][0].astype(f32)
    mcw = g["m_conv_w"][0].reshape(4, 8, 128).transpose(2, 1, 0).reshape(128, 32)
    mcb = _fm(g["m_conv_b"][0], 8)
    gifb = np.concatenate([g["m_igate_b"][0], g["m_fgate_b"][0]]).astype(f32)
    gq = np.tile(g["a_qnorm_g"][0], 8).astype(f32)
    gk = np.tile(g["a_knorm_g"][0], 8).astype(f32)
    rel = g["rel_bias"].astype(f32)
    kk = np.arange(128)[:, None]
    qq = np.arange(128)[None, :]
    biasg = np.zeros((128, 4, 2, 2, 128), f32)
    maskn = np.zeros((128, 4, 2, 2, 128), f32)
    for dl in range(2):
        dist = qq - kk + 128 * dl
        bidx = _t5_bucket(dist)
        for h in range(4):
            t = rel[bidx, h]
            biasg[:, h, dl, 0, :] = t
            biasg[:, h, dl, 1, :] = t
        if dl == 0:
            mk = np.where(dist < 0, NEG, 0.0).astype(f32)
            maskn[:, :, 0, :, :] = mk[:, None, None, :]
    farb = rel[31, :].astype(f32)
    umask = (kk <= qq).astype(f32)
    negm4 = np.tile(np.where(kk <= qq, 0.0, NEG).astype(f32), (1, 4))
    import ml_dtypes
    common = dict(
        wada=wada, bada=bada, badafm=_fm(bada, 48), g1fm=_fm(g["norm1_g"][0], 8), g2fm=_fm(g["norm2_g"][0], 8),
        wall=wall, mcw=np.ascontiguousarray(mcw, f32), mcb=mcb, fcw=np.ascontiguousarray(fcw, f32), fcb=np.ascontiguousarray(fcb, f32),
        gifb=gifb, mng=g["m_norm_g"][0].astype(f32), ang=g["a_norm_g"][0].astype(f32), gq=gq, gk=gk,
        gqfm=np.tile(g["a_qnorm_g"][0], 2).reshape(128, 1).astype(f32), gkfm=np.tile(g["a_knorm_g"][0], 2).reshape(128, 1).astype(f32),
        alam=g["a_lambda"][0].reshape(256).astype(f32), biasg=biasg.reshape(128, -1), maskn=maskn.reshape(128, -1), farb=farb,
        identb=np.eye(128).astype(ml_dtypes.bfloat16), identf=np.eye(128, dtype=f32), umask=umask, negm4=negm4,
    )
    in_maps = []
    for b in range(B):
        cfm = _fm(g["c"][b], 8)
        for hf in range(2):
            if hf == 0:
                xl = np.concatenate([np.zeros((half_tok, D), f32), x[b, 0:half_tok]], axis=0)
                fl = np.zeros((128, 1), f32)
            else:
                xl = x[b]
                fl = np.ones((128, 1), f32)
            m = dict(common)
            m["x"] = np.ascontiguousarray(xl)
            m["cfm"] = cfm
            m["flag"] = fl
            in_maps.append(m)
    return in_maps


_NC_CACHE = {}


def run(NCTX, NFULL, inputs):
    key = (NCTX, NFULL)
    if key not in _NC_CACHE:
        _NC_CACHE[key] = build_program(NCTX, NFULL)
    nc = _NC_CACHE[key]
    in_maps = prepare_inputs(NCTX, NFULL, inputs)
    res = run_bass_kernel_spmd(nc, in_maps, core_ids=list(range(len(in_maps))))
    B = len(in_maps) // 2
    half_tok = ((NCTX + NFULL) // 2) * 512
    out = np.empty((B, 2 * half_tok, D), np.float32)
    for b in range(B):
        for hf in range(2):
            out[b, hf * half_tok:(hf + 1) * half_tok] = res.results[b * 2 + hf]["out"]
    return out


def kernel(**inputs):
    return run(7, 9, inputs)
```

```python
import math
from contextlib import ExitStack

import numpy as np

import concourse.bass as bass
import concourse.mybir as mybir
from concourse.bass_utils import run_bass_kernel_spmd

F32 = mybir.dt.float32
BF16 = mybir.dt.bfloat16
AF = mybir.ActivationFunctionType
ALU = mybir.AluOpType
AX = mybir.AxisListType

D = 1024
DC = 8
DFF = 2816
NF = 22
EPS = 1e-6
LAM_INIT = 0.8 - 0.6 * math.exp(-0.3 * 0)
NEG = -30000.0
OPT_SELF_WAR = True
OPT_XN_ACT = True
OPT_SILU = True
OPT_GFOLD = True
OPT_TINY = True
OPT_BATCH = True
OPT_PARTMAIN = True

PIECES = {}
_off = 0
for _nm, _n in (("mq", 4096), ("mk", 4096), ("mv", 4096), ("mo", 4096), ("aq", 4096), ("ak", 4096),
                ("av", 4096), ("gm0", 4096), ("gm1", 4096), ("ga0", 4096), ("ga1", 4096), ("gif", 64),
                ("bm", 4096), ("ba", 4096), ("out0", 4096), ("out1", 4096)):
    PIECES[_nm] = (_off, _n)
    _off += _n
for _j in range(11):
    PIECES["up%d" % _j] = (_off, 4096)
    _off += 4096
PIECES["down"] = (_off, NF * 1024)
_off += NF * 1024
WTOT = _off
WPAD = ((WTOT + 4095) // 4096) * 4096


class Buf:
    __slots__ = ("name", "w", "rs", "sem", "semv", "grp", "psum", "persist")

    def __init__(self, name, grp=None):
        self.psum = False
        self.persist = False
        self.name = name
        self.w = None
        self.rs = []
        self.sem = None
        self.semv = 0
        self.grp = grp


class Sched:
    def __init__(self, nc, stack):
        self.nc = nc
        self.stack = stack
        self.engs = {}
        for nm, h in (("pe", nc.tensor), ("act", nc.scalar), ("dve", nc.vector), ("pool", nc.gpsimd), ("sp", nc.sync)):
            sem = stack.enter_context(nc.semaphore("s_" + nm))
            self.engs[nm] = dict(h=h, sem=sem, cnt=0, seen={})
        self.groups = {}
        self.dma_ev = {}
        self.free_sems = {}
        self.live = []
        self.nsem = 0

    def buf(self, name, grp=None):
        return Buf(name, grp)

    def _need(self, e, deps):
        E = self.engs[e]
        best = {}
        for d in deps:
            if d is None:
                continue
            sem, val, en = d
            if en == e and e == "pe":
                continue
            k = id(sem)
            if k not in best or best[k][1] < val:
                best[k] = (sem, val)
        for k, (sem, val) in best.items():
            if E["seen"].get(k, 0) >= val:
                continue
            E["h"].wait_ge(sem, val)
            E["seen"][k] = val

    def op(self, e, fn, reads=(), writes=()):
        E = self.engs[e]
        deps = []
        for b in reads:
            deps.append(b.w)
            if b.psum:
                for r in b.rs:
                    if r[2] != e:
                        deps.append(r)
        for b in writes:
            if b.w is not None and b.w[2] != e:
                deps.append(b.w)
            for r in b.rs:
                if OPT_SELF_WAR or r[2] != e:
                    deps.append(r)
        self._need(e, deps)
        ins = fn(E["h"])
        E["cnt"] += 1
        ins.then_inc(E["sem"], 1)
        ev = (E["sem"], E["cnt"], e)
        for b in reads:
            b.rs.append(ev)
        for b in writes:
            b.w = ev
            b.rs = []
        return ins

    def dma(self, q, out, in_, reads=(), writes=()):
        E = self.engs[q]
        deps = []
        for b in reads:
            deps.append(b.w)
        for b in writes:
            if b.grp in ("wbw", "kvw", "outw"):
                continue
            deps.append(b.w)
            deps.extend(b.rs)
        self._need(q, deps)
        ins = E["h"].dma_start(out=out, in_=in_)
        cands = list(writes) + list(reads)
        tgt = ([b for b in cands if b.grp is None] + cands)[0]
        if tgt.grp is not None:
            gk = tgt.grp
            if gk not in self.groups:
                self.groups[gk] = [self.stack.enter_context(self.nc.semaphore("g_" + gk)), 0]
            g = self.groups[gk]
            g[1] += 16
            sem, val = g[0], g[1]
        else:
            if tgt.sem is None:
                tgt.sem = {}
                self.live.append(tgt)
            if q not in tgt.sem:
                fs = self.free_sems.setdefault(q, [])
                if not fs:
                    self.nsem += 1
                    fs.append([self.stack.enter_context(self.nc.semaphore("dp%d" % self.nsem)), 0])
                tgt.sem[q] = fs.pop()
            ent = tgt.sem[q]
            ent[1] += 16
            sem, val = ent[0], ent[1]
        ins.then_inc(sem, 16)
        self.dma_ev[id(sem)] = (sem, val)
        ev = (sem, val, "dma")
        for b in reads:
            b.rs.append(ev)
        for b in writes:
            b.w = ev
            b.rs = []
        return ins

    def barrier(self, engines=("pe", "act", "dve", "pool", "sp")):
        deps = [(E["sem"], E["cnt"], "x") for E in self.engs.values() if E["cnt"] > 0]
        deps += [(s, v, "dma") for (s, v) in self.dma_ev.values()]
        for e in engines:
            self._need(e, deps)
        self.dma_ev = {}
        keep = []
        for b in self.live:
            if b.persist:
                keep.append(b)
                continue
            for qq, ent in b.sem.items():
                self.free_sems[qq].append(ent)
            b.sem = None
        self.live = keep

    def seal(self, bufs, grp):
        g = self.groups[grp]
        for b in bufs:
            b.w = (g[0], g[1], "dma")


def build_program(NCTX, NFULL, debug=False):
    NU = NCTX + NFULL
    NFLAG = NU // 2
    NBLK = NU * 4
    NTOK = NU * 512
    NOWN = (NFULL - 1) * 512
    assert NU == 2 * (NFULL - 1)

    nc = bass.Bass("TRN2", target_bir_lowering=False)

    def din(name, shape, dt=F32):
        return nc.dram_tensor(name, list(shape), dt, kind="ExternalInput").ap()

    x_d = din("x", [NTOK, D])
    cfm_d = din("cfm", [128, 8])
    wada_d = din("wada", [12, 128, 8 * 512])
    bada_d = din("bada", [6144])
    badafm_d = din("badafm", [128, 48])
    g1fm_d = din("g1fm", [128, 8])
    g2fm_d = din("g2fm", [128, 8])
    wall_d = din("wall", [128, WPAD])
    mcw_d = din("mcw", [128, 8 * 4])
    mcb_d = din("mcb", [128, 8])
    fcw_d = din("fcw", [128, 44 * 3])
    fcb_d = din("fcb", [128, 44])
    gifb_d = din("gifb", [8])
    mng_d = din("mng", [512])
    ang_d = din("ang", [512])
    gq_d = din("gq", [512])
    gk_d = din("gk", [512])
    gqfm_d = din("gqfm", [128, 1])
    gkfm_d = din("gkfm", [128, 1])
    alam_d = din("alam", [256])
    biasg_d = din("biasg", [128, 4 * 2 * 2 * 128])
    maskn_d = din("maskn", [128, 4 * 2 * 2 * 128])
    farb_d = din("farb", [4])
    flag_d = din("flag", [128, 1])
    identb_d = din("identb", [128, 128], BF16)
    identf_d = din("identf", [128, 128])
    umask_d = din("umask", [128, 128])
    negm4_d = din("negm4", [128, 512])
    out_d = nc.dram_tensor("out", [NOWN, D], F32, kind="ExternalOutput").ap()
    wb_d = nc.dram_tensor("wb_scr", [128, WPAD], BF16, kind="Internal").ap()
    kt_d = nc.dram_tensor("kt_scr", [4, 128, NBLK * 128], BF16, kind="Internal").ap()
    va_d = nc.dram_tensor("va_scr", [4, 128, NBLK * 130], BF16, kind="Internal").ap()

    with ExitStack() as top:
        S = Sched(nc, top)

        uid = [0]

        def sbt(st, name, shape, dt=F32):
            uid[0] += 1
            return st.enter_context(nc.sbuf_tensor("s%d_%s" % (uid[0], name), list(shape), dt))

        banks = [top.enter_context(nc.psum_tensor("bank%d" % i, [128, 512], F32)) for i in range(8)]
        bbufs = [S.buf("bank%d" % i) for i in range(8)]
        for b_ in bbufs:
            b_.psum = True
        bank_rr = [0]

        def nbank(lo=0, hi=8):
            i = lo + (bank_rr[0] % (hi - lo))
            bank_rr[0] += 1
            return banks[i], bbufs[i]

        def bfview(bank):
            return bank[:].bitcast(BF16)

        identb = sbt(top, "identb", [128, 128], BF16)
        identf = sbt(top, "identf", [128, 128])
        umask = sbt(top, "umask", [128, 128])
        negm4 = sbt(top, "negm4", [128, 512])
        ones_f = sbt(top, "ones_f", [128, 128])
        flag = sbt(top, "flag", [128, 1])
        epst = sbt(top, "epst", [128, 1])
        onet = sbt(top, "onet", [128, 1])
        mult1 = sbt(top, "mult1", [128, 8])
        shift1 = sbt(top, "shift1", [128, 8])
        mult2 = sbt(top, "mult2", [128, 8])
        shift2 = sbt(top, "shift2", [128, 8])
        gate1 = sbt(top, "gate1", [128, D])
        gate2 = sbt(top, "gate2", [128, D])
        mcw = sbt(top, "mcw", [128, 8, 4])
        mcb = sbt(top, "mcb", [128, 8])
        mcwf = sbt(top, "mcwf", [128, 8, 4])
        fcwf = sbt(top, "fcwf", [128, 44, 3])
        mcnb = sbt(top, "mcnb", [128, 8])
        fcw = sbt(top, "fcw", [128, 44, 3])
        fcb = sbt(top, "fcb", [128, 44])
        fcnb = sbt(top, "fcnb", [128, 44])
        gifb = sbt(top, "gifb", [128, 8])
        mng = sbt(top, "mng", [128, 512])
        ang = sbt(top, "ang", [128, 512])
        gq = sbt(top, "gq", [128, 512])
        gk = sbt(top, "gk", [128, 512])
        gqfm = sbt(top, "gqfm", [128, 1])
        gkfm = sbt(top, "gkfm", [128, 1])
        mbc = sbt(top, "mbc", [128, 8, 3])
        fbc = sbt(top, "fbc", [128, 44, 2])
        ctmp = sbt(top, "ctmp", [128, 44])
        b_mbc = S.buf("mbc")
        b_fbc = S.buf("fbc")
        biasb = sbt(top, "biasb", [128, 4, 2, 256], BF16)
        farb = sbt(top, "farb", [128, 4])
        neglam = sbt(top, "neglam", [128, 1])
        Sst = sbt(top, "Sst", [128, 4, 130])
        Sstb = sbt(top, "Sstb", [128, 4, 130], BF16)
        mhalo = sbt(top, "mhalo", [128, 8, 3])
        fhalo = sbt(top, "fhalo", [128, 44, 2])
        CONST = S.buf("const")
        b_S = S.buf("Sst")
        b_Sb = S.buf("Sstb")
        b_mhalo = S.buf("mhalo")
        b_fhalo = S.buf("fhalo")
        b_wb = S.buf("wb_scr", grp="wbw")
        b_ktd = S.buf("kt_scr", grp="kvw")
        b_vad = S.buf("va_scr", grp="kvw")
        b_out = S.buf("outd", grp="outw")

        def ld_const(t, src, grp="cst"):
            b = S.buf("c_" + t.name, grp=grp)
            S.dma("sp", t[:], src, writes=[b])
            return b

        cb = []
        cb.append(ld_const(identb, identb_d[:, :]))
        cb.append(ld_const(identf, identf_d[:, :]))
        cb.append(ld_const(umask, umask_d[:, :]))
        cb.append(ld_const(negm4, negm4_d[:, :]))
        cb.append(ld_const(flag, flag_d[:, :]))
        cb.append(ld_const(mcw, mcw_d.rearrange("p (c k) -> p c k", k=4)))
        cb.append(ld_const(mcb, mcb_d[:, :]))
        cb.append(ld_const(fcw, fcw_d.rearrange("p (c k) -> p c k", k=3)))
        cb.append(ld_const(fcb, fcb_d[:, :]))
        cb.append(ld_const(gifb, gifb_d.partition_broadcast(128)))
        cb.append(ld_const(mng, mng_d.partition_broadcast(128)))
        cb.append(ld_const(ang, ang_d.partition_broadcast(128)))
        cb.append(ld_const(gq, gq_d.partition_broadcast(128)))
        cb.append(ld_const(gk, gk_d.partition_broadcast(128)))
        cb.append(ld_const(farb, farb_d.partition_broadcast(128)))
        cb.append(ld_const(gqfm, gqfm_d[:, :]))
        cb.append(ld_const(gkfm, gkfm_d[:, :]))
        S.seal(cb, "cst")
        S.op("dve", lambda e: e.memset(ones_f[:], 1.0), writes=[CONST])
        S.op("dve", lambda e: e.memset(epst[:], EPS), writes=[CONST])
        S.op("dve", lambda e: e.memset(onet[:], 1.0), writes=[CONST])
        S.op("dve", lambda e: e.memset(Sst[:], 0.0), writes=[b_S])
        S.op("dve", lambda e: e.memset(Sstb[:], 0.0), writes=[b_Sb])
        S.op("dve", lambda e: e.memset(mhalo[:], 0.0), writes=[b_mhalo])
        S.op("dve", lambda e: e.memset(fhalo[:], 0.0), writes=[b_fhalo])
        S.op("dve", lambda e: e.tensor_scalar(out=mcnb[:], in0=mcb[:], scalar1=-1.0, scalar2=None, op0=ALU.mult), reads=cb, writes=[CONST])
        S.op("dve", lambda e: e.tensor_scalar(out=fcnb[:], in0=fcb[:], scalar1=-1.0, scalar2=None, op0=ALU.mult), reads=cb, writes=[CONST])
        S.op("dve", lambda e: e.tensor_scalar(out=gqfm[:], in0=gqfm[:], scalar1=0.125, scalar2=None, op0=ALU.mult), reads=cb, writes=[CONST])
        S.op("dve", lambda e: e.tensor_scalar(out=mcwf[:], in0=mcw[:], scalar1=flag[:, 0:1], scalar2=None, op0=ALU.mult), reads=cb, writes=[CONST])
        S.op("dve", lambda e: e.tensor_scalar(out=fcwf[:], in0=fcw[:], scalar1=flag[:, 0:1], scalar2=None, op0=ALU.mult), reads=cb, writes=[CONST])
        S.op("dve", lambda e: e.tensor_scalar(out=ang[:], in0=ang[:], scalar1=1.0 - LAM_INIT, scalar2=None, op0=ALU.mult), reads=cb, writes=[CONST])

        with ExitStack() as st:
            cfm = sbt(st, "cfm", [128, 8])
            sc = sbt(st, "sc", [128, 8])
            sct = sbt(st, "sct", [128, 8])
            scbc = sbt(st, "scbc", [128, 8, 128])
            badafm = sbt(st, "badafm", [128, 48])
            g1fm = sbt(st, "g1fm", [128, 8])
            g2fm = sbt(st, "g2fm", [128, 8])
            modfm = sbt(st, "modfm", [128, 48])
            alam = sbt(st, "alam", [128, 256])
            lt = sbt(st, "lt", [128, 128])
            ls = sbt(st, "ls", [128, 2])
            biasg = sbt(st, "biasg", [128, 2048])
            maskn = sbt(st, "maskn", [128, 2048])
            wst = [sbt(st, "wst%d" % i, [128, 4096]) for i in range(2)]
            wbo = [sbt(st, "wbo%d" % i, [128, 4096], BF16) for i in range(2)]
            bbt = [sbt(st, "bbt%d" % i, [128, 512]) for i in range(2)]
            b_wst = [S.buf("wst%d" % i) for i in range(2)]
            b_wbo = [S.buf("wbo%d" % i) for i in range(2)]
            b_bbt = [S.buf("bbt%d" % i) for i in range(2)]
            L = S.buf("prel")
            b_l = [ld_const(cfm, cfm_d[:, :], "cst2"), ld_const(badafm, badafm_d[:, :], "cst2"), ld_const(g1fm, g1fm_d[:, :], "cst2"),
                   ld_const(g2fm, g2fm_d[:, :], "cst2"), ld_const(alam, alam_d.partition_broadcast(128), "cst2"),
                   ld_const(biasg, biasg_d[:, :], "cst2"), ld_const(maskn, maskn_d[:, :], "cst2")]
            S.seal(b_l, "cst2")
            S.op("act", lambda e: e.activation(out=sct[:], in_=cfm[:], func=AF.Exp, scale=-1.0), reads=b_l, writes=[L])
            S.op("dve", lambda e: e.tensor_scalar(out=sct[:], in0=sct[:], scalar1=1.0, scalar2=None, op0=ALU.add), reads=[L], writes=[L])
            S.op("dve", lambda e: e.reciprocal(out=sct[:], in_=sct[:]), reads=[L], writes=[L])
            S.op("dve", lambda e: e.tensor_tensor(out=sc[:], in0=cfm[:], in1=sct[:], op=ALU.mult), reads=[L], writes=[L])
            S.op("dve", lambda e: e.tensor_copy(out=scbc[:], in_=sc[:].unsqueeze(2).to_broadcast([128, 8, 128])), reads=[L], writes=[L])
            S.op("dve", lambda e: e.tensor_tensor(out=lt[:, 0:64], in0=alam[:, 0:64], in1=alam[:, 64:128], op=ALU.mult), reads=b_l, writes=[L])
            S.op("dve", lambda e: e.tensor_tensor(out=lt[:, 64:128], in0=alam[:, 128:192], in1=alam[:, 192:256], op=ALU.mult), reads=[L], writes=[L])
            S.op("dve", lambda e: e.tensor_reduce(out=ls[:], in_=lt[:].rearrange("p (a b) -> p a b", a=2), axis=AX.X, op=ALU.add), reads=[L], writes=[L])
            S.op("act", lambda e: e.activation(out=ls[:], in_=ls[:], func=AF.Exp), reads=[L], writes=[L])
            S.op("dve", lambda e: e.tensor_tensor(out=neglam[:], in0=ls[:, 1:2], in1=ls[:, 0:1], op=ALU.subtract), reads=[L], writes=[CONST])
            S.op("dve", lambda e: e.tensor_scalar(out=neglam[:], in0=neglam[:], scalar1=-LAM_INIT, scalar2=None, op0=ALU.add), reads=[CONST], writes=[CONST])
            S.op("dve", lambda e: e.tensor_tensor(out=biasb[:].rearrange("p a b c -> p (a b c)"), in0=biasg[:], in1=maskn[:], op=ALU.add), reads=b_l, writes=[CONST])

            fm_ps, fm_b = banks[7], bbufs[7]
            fm_cols = {0: 0, 1: 4, 2: 8, 3: 12, 6: 24, 7: 28, 8: 32, 9: 36}
            for pc in range(12):
                i = pc % 2
                S.dma("sp", wst[i][:], wada_d[pc, :, :], writes=[b_wst[i]])
                wv = wst[i][:].rearrange("p (c n) -> p c n", c=8)
                if pc in (4, 5, 10, 11):
                    S.dma("sp", bbt[i][:], bada_d[pc * 512:(pc + 1) * 512].partition_broadcast(128), writes=[b_bbt[i]])
                    pb, pbb = nbank(0, 6)
                    for dc in range(8):
                        S.op("pe", lambda e: e.matmul(pb[:], lhsT=scbc[:, dc, :], rhs=wv[:, dc, :], start=(dc == 0), stop=(dc == 7)),
                             reads=[L, b_wst[i]], writes=[pbb])
                    gt = gate1 if pc < 6 else gate2
                    off = (pc % 2) * 512
                    S.op("dve", lambda e: e.tensor_tensor(out=gt[:, off:off + 512], in0=pb[:], in1=bbt[i][:], op=ALU.add),
                         reads=[pbb, b_bbt[i]], writes=[CONST])
                else:
                    for k in range(4):
                        col = fm_cols[pc] + k
                        for dc in range(8):
                            S.op("pe", lambda e: e.matmul(fm_ps[:, col:col + 1], lhsT=wv[:, dc, k * 128:(k + 1) * 128], rhs=sc[:, dc:dc + 1],
                                                         start=(dc == 0), stop=(dc == 7)), reads=[L, b_wst[i]], writes=[fm_b])
            S.op("dve", lambda e: e.tensor_tensor(out=modfm[:, 0:16], in0=fm_ps[:, 0:16], in1=badafm[:, 0:16], op=ALU.add), reads=[fm_b] + b_l, writes=[L])
            S.op("dve", lambda e: e.tensor_tensor(out=modfm[:, 24:40], in0=fm_ps[:, 24:40], in1=badafm[:, 24:40], op=ALU.add), reads=[fm_b] + b_l, writes=[L])
            S.op("dve", lambda e: e.scalar_tensor_tensor(out=mult1[:], in0=modfm[:, 8:16], scalar=1.0, in1=g1fm[:], op0=ALU.add, op1=ALU.mult), reads=[L], writes=[CONST])
            S.op("dve", lambda e: e.tensor_copy(out=shift1[:], in_=modfm[:, 0:8]), reads=[L], writes=[CONST])
            S.op("dve", lambda e: e.scalar_tensor_tensor(out=mult2[:], in0=modfm[:, 32:40], scalar=1.0, in1=g2fm[:], op0=ALU.add, op1=ALU.mult), reads=[L], writes=[CONST])
            S.op("dve", lambda e: e.tensor_copy(out=shift2[:], in_=modfm[:, 24:32]), reads=[L], writes=[CONST])

            cast_eng = ["dve", "act"]
            for ch in range(WPAD // 4096):
                i = ch % 2
                S.dma("sp", wst[i][:], wall_d[:, ch * 4096:(ch + 1) * 4096], writes=[b_wst[i]])
                ce = cast_eng[ch % 2]
                if ce == "act":
                    S.op("act", lambda e: e.copy(out=wbo[i][:], in_=wst[i][:]), reads=[b_wst[i]], writes=[b_wbo[i]])
                else:
                    S.op(ce, lambda e: e.tensor_copy(out=wbo[i][:], in_=wst[i][:]), reads=[b_wst[i]], writes=[b_wbo[i]])
                S.dma("pool", wb_d[:, ch * 4096:(ch + 1) * 4096], wbo[i][:], reads=[b_wbo[i]], writes=[b_wb])
            S.barrier()

        njunk = sbt(top, "njunk", [128, D], BF16)
        ncache = {}
        pf = [sbt(top, "pf%d" % i, [128, 4096], BF16) for i in range(4)]
        b_pf = [S.buf("pf%d" % i) for i in range(4)]
        for b_ in b_pf:
            b_.persist = True
        prefetched = {}

        def prefetch(slots, pieces):
            for sl, piece in zip(slots, pieces):
                off, n = PIECES[piece]
                S.dma("sp", pf[sl][:, 0:n], wb_d[:, off:off + n], reads=[b_wb], writes=[b_pf[sl]])
                prefetched[piece] = (pf[sl], b_pf[sl])

        def wload(st, name, piece, shape3=None):
            if piece in prefetched:
                return prefetched.pop(piece)
            off, n = PIECES[piece]
            t = sbt(st, name, [128, n], BF16)
            b = S.buf(name)
            S.dma("sp", t[:], wb_d[:, off:off + n], reads=[b_wb], writes=[b])
            return t, b

        for u in range(NU):
            full = u >= NCTX
            flagged = u < NFLAG
            own = u >= NFLAG
            with ExitStack() as su:
                xs = sbt(su, "xs", [128, 4, D])
                hT = sbt(su, "hT", [128, 8, 512], BF16)
                sa = su.enter_context(ExitStack())
                mqT = sbt(sa, "mqT", [128, 8, 512], BF16)
                mVA = sbt(sa, "mVA", [128, 4, 4, 130], BF16)
                sigmo = sbt(sa, "sigmo", [128, 4, 512])
                gif = sbt(sa, "gif", [128, 4, 8])
                Qbd = sbt(sa, "Qbd", [128, 4, 4, 256], BF16)
                hmT = sbt(sa, "hmT", [128, 4, 512], BF16)
                haT = sbt(sa, "haT", [128, 4, 512], BF16)
                b_xs = [S.buf("xs%d" % j) for j in range(4)]
                b_hT = S.buf("hT")
                b_hT2 = S.buf("hT2")
                b_mqT = S.buf("mqT")
                b_mVA = S.buf("mVA")
                b_sig = S.buf("sigmo")
                b_gif = S.buf("gif")
                b_Qbd = S.buf("Qbd")
                b_hmT = S.buf("hmT")
                b_haT = S.buf("haT")
                if full:
                    S.op("dve", lambda e: e.memset(Qbd[:], 0.0), writes=[b_Qbd])

                def norm_block(st, j, mult, shift, hdst, b_hdst, tagp):
                    junk = njunk
                    if not hasattr(st, "ncache"):
                        st.ncache = {}
                    ck = (tagp, j % 2)
                    if ck not in st.ncache:
                        st.ncache[ck] = (sbt(st, tagp + "xn%d" % (j % 2), [128, D], BF16), S.buf("xn"))
                    xn, b_xn = st.ncache[ck]
                    ss = sbt(st, tagp + "ss%d" % j, [128, 2])
                    bl = S.buf("nb")
                    S.op("act", lambda e: e.activation(out=xn[:], in_=xs[:, j, :], func=AF.Square, scale=1.0 / 32.0, accum_out=ss[:, 0:1]),
                         reads=[b_xs[j]], writes=[bl, b_xn])
                    yield
                    S.op("act", lambda e: e.activation(out=ss[:, 1:2], in_=ss[:, 0:1], func=AF.Ln, bias=epst[:, 0:1], scale=1.0), reads=[bl, CONST], writes=[bl])
                    S.op("act", lambda e: e.activation(out=ss[:, 1:2], in_=ss[:, 1:2], func=AF.Exp, scale=-0.5), reads=[bl], writes=[bl])
                    yield
                    S.op("dve", lambda e: e.tensor_scalar(out=xn[:], in0=xs[:, j, :], scalar1=ss[:, 1:2], scalar2=None, op0=ALU.mult),
                         reads=[bl, b_xs[j]], writes=[b_xn])
                    yield
                    pvs = []
                    for half in range(2):
                        pb, pbb = nbank()
                        pv = bfview(pb)[:, 0:512].rearrange("p (c t) -> p c t", c=4)
                        pvs.append((pv, pbb))
                        for cc in range(4):
                            c = half * 4 + cc
                            S.op("pe", lambda e: e.transpose(out=pv[:, cc, :], in_=xn[:, c * 128:(c + 1) * 128], identity=identb[:]),
                                 reads=[b_xn, cb[0]], writes=[pbb])
                    yield
                    for half in range(2):
                        pv, pbb = pvs[half]
                        for cc in range(4):
                            c = half * 4 + cc
                            if half == 0:
                                S.op("act", lambda e: e.activation(out=hdst[:, c, j * 128:(j + 1) * 128], in_=pv[:, cc, :], func=AF.Identity,
                                                                  scale=mult[:, c:c + 1], bias=shift[:, c:c + 1]), reads=[pbb, CONST], writes=[b_hdst[0]])
                            else:
                                S.op("dve", lambda e: e.tensor_scalar(out=hdst[:, c, j * 128:(j + 1) * 128], in0=pv[:, cc, :], scalar1=mult[:, c:c + 1],
                                                                     scalar2=shift[:, c:c + 1], op0=ALU.mult, op1=ALU.add), reads=[pbb, CONST], writes=[b_hdst[1]])
                    yield

                def run_norms(st, mult, shift, hdst, b_hdst, tagp):
                    for pair_ in ((0, 1), (2, 3)):
                        alive_ = [norm_block(st, j_, mult, shift, hdst, b_hdst, tagp) for j_ in pair_]
                        while alive_:
                            nxt_ = []
                            for g_ in alive_:
                                try:
                                    next(g_)
                                    nxt_.append(g_)
                                except StopIteration:
                                    pass
                            alive_ = nxt_

                with ExitStack() as st:
                    names = (["mq", "mk", "mv", "gif", "av", "mo", "aq", "ak"] if full else ["mk", "mv", "gif", "av", "ak"])
                    W = {}
                    for j in range(4):
                        blk = u * 4 + j
                        S.dma("sp", xs[:, j, :], x_d[blk * 128:(blk + 1) * 128, :], writes=[b_xs[j]])
                    for nm in names:
                        W[nm] = wload(st, "w_" + nm, nm)
                    if not full and u + 1 < NU:
                        nxt_full = (u + 1) >= NCTX
                        slots = (2, 3) if (nxt_full or (u + 1) % 2 == 1) else (0, 1)
                        if nxt_full:
                            slots = (2, 3)
                        prefetch(slots, ["mq", "mk"] if nxt_full else ["mk", "mv"])
                    run_norms(st, mult1, shift1, hT, (b_hT, b_hT2), "n1")
                    if flagged:
                        S.op("dve", lambda e: e.tensor_copy(out=mVA[:, :, :, 128:130], in_=flag[:, 0:1].unsqueeze(1).unsqueeze(1).to_broadcast([128, 4, 4, 2])),
                             reads=[cb[4]], writes=[b_mVA])
                    else:
                        S.op("dve", lambda e: e.memset(mVA[:, :, :, 128:130], 1.0), writes=[b_mVA])
                    acc = [sbt(st, "acc%d" % i, [128, 512]) for i in range(4)]
                    et = [sbt(st, "et%d" % i, [128, 512]) for i in range(4)]
                    b_acc = [S.buf("acc%d" % i) for i in range(4)]
                    b_et = [S.buf("et%d" % i) for i in range(4)]
                    hv = mhalo
                    if not OPT_BATCH:
                        S.op("dve", lambda e: e.memset(mbc[:], 0.0), writes=[b_mbc])
                    if OPT_BATCH:
                        S.op("dve", lambda e: e.tensor_tensor(out=mbc[:, :, 2], in0=hv[:, :, 2], in1=mcw[:, :, 0], op=ALU.mult), reads=[b_mhalo] + cb, writes=[b_mbc])
                        S.op("dve", lambda e: e.tensor_tensor(out=mbc[:, :, 1], in0=hv[:, :, 2], in1=mcw[:, :, 1], op=ALU.mult), reads=[b_mhalo] + cb, writes=[b_mbc])
                        S.op("dve", lambda e: e.tensor_tensor(out=mbc[:, :, 0], in0=hv[:, :, 2], in1=mcw[:, :, 2], op=ALU.mult), reads=[b_mhalo] + cb, writes=[b_mbc])
                        S.op("dve", lambda e: e.tensor_tensor(out=ctmp[:, 0:8], in0=hv[:, :, 1], in1=mcw[:, :, 0], op=ALU.mult), reads=[b_mhalo] + cb, writes=[b_mbc])
                        S.op("dve", lambda e: e.tensor_tensor(out=mbc[:, :, 1], in0=mbc[:, :, 1], in1=ctmp[:, 0:8], op=ALU.add), reads=[b_mbc], writes=[b_mbc])
                        S.op("dve", lambda e: e.tensor_tensor(out=ctmp[:, 8:16], in0=hv[:, :, 1], in1=mcw[:, :, 1], op=ALU.mult), reads=[b_mhalo] + cb, writes=[b_mbc])
                        S.op("dve", lambda e: e.tensor_tensor(out=mbc[:, :, 0], in0=mbc[:, :, 0], in1=ctmp[:, 8:16], op=ALU.add), reads=[b_mbc], writes=[b_mbc])
                        S.op("dve", lambda e: e.tensor_tensor(out=ctmp[:, 16:24], in0=hv[:, :, 0], in1=mcw[:, :, 0], op=ALU.mult), reads=[b_mhalo] + cb, writes=[b_mbc])
                        S.op("dve", lambda e: e.tensor_tensor(out=mbc[:, :, 0], in0=mbc[:, :, 0], in1=ctmp[:, 16:24], op=ALU.add), reads=[b_mbc], writes=[b_mbc])
                        S.op("dve", lambda e: e.tensor_tensor(out=mbc[:], in0=mbc[:], in1=mcb[:].unsqueeze(2).to_broadcast([128, 8, 3]), op=ALU.add), reads=[b_mbc] + cb, writes=[b_mbc])
                    groups = ([[0, 1, 2, 3], [4, 5, 6, 7]] if full else [[4, 5, 6, 7]])
                    wsel = mcwf if flagged else mcw
                    for grp in groups:
                        pbs = []
                        for k, c in enumerate(grp):
                            wt, wbf = W["mq" if c < 4 else "mk"]
                            wv = wt[:].rearrange("p (c n) -> p c n", c=8)
                            pb, pbb = nbank()
                            pbs.append((pb, pbb))
                            for dc in range(8):
                                S.op("pe", lambda e: e.matmul(pb[:], lhsT=wv[:, dc, k * 128:(k + 1) * 128], rhs=hT[:, dc, :], start=(dc == 0), stop=(dc == 7)),
                                     reads=[wbf, b_hT, b_hT2], writes=[pbb])
                            m0 = 3 if OPT_PARTMAIN else 0
                            S.op("act", lambda e: e.activation(out=acc[k][:, m0:512], in_=pb[:, m0:512], func=AF.Identity, scale=wsel[:, c, 3:4], bias=mcb[:, c:c + 1]),
                                 reads=[pbb, CONST] + cb, writes=[b_acc[k]])
                            for col in (range(3) if OPT_TINY else ()):
                                S.op("act", lambda e: e.activation(out=acc[k][:, col:col + 1], in_=pb[:, col:col + 1], func=AF.Identity, scale=wsel[:, c, 3:4],
                                                                  bias=mbc[:, c, col:col + 1]), reads=[pbb, CONST, b_mbc] + cb, writes=[b_acc[k]])
                            if flagged:
                                S.op("act", lambda e: e.activation(out=mhalo[:, c, :], in_=pb[:, 509:512], func=AF.Copy, scale=flag[:, 0:1]), reads=[pbb, cb[4], b_mbc], writes=[b_mhalo])
                            else:
                                S.op("act", lambda e: e.copy(out=mhalo[:, c, :], in_=pb[:, 509:512]), reads=[pbb, b_mbc], writes=[b_mhalo])
                        for tp in (2, 1, 0):
                            sh = 3 - tp
                            for k, c in enumerate(grp):
                                pb, pbb = pbs[k]
                                S.op("dve", lambda e: e.scalar_tensor_tensor(out=acc[k][:, sh:512], in0=pb[:, 0:512 - sh], scalar=wsel[:, c, tp:tp + 1], in1=acc[k][:, sh:512],
                                                                            op0=ALU.mult, op1=ALU.add), reads=[pbb, b_acc[k], CONST], writes=[b_acc[k]])
                        for k, c in enumerate(grp):
                            if not OPT_SILU:
                                S.op("act", lambda e: e.activation(out=et[k][:], in_=acc[k][:], func=AF.Exp, scale=-1.0), reads=[b_acc[k]], writes=[b_et[k]])
                                sk = (128.0 ** 0.5) if c < 4 else 1.0
                                S.op("dve", lambda e: e.tensor_scalar(out=et[k][:], in0=et[k][:], scalar1=1.0, scalar2=sk, op0=ALU.add, op1=ALU.mult), reads=[b_et[k]], writes=[b_et[k]])
                                S.op("dve", lambda e: e.reciprocal(out=et[k][:], in_=et[k][:]), reads=[b_et[k]], writes=[b_et[k]])
                                S.op("dve", lambda e: e.tensor_tensor(out=mqT[:, c, :], in0=acc[k][:], in1=et[k][:], op=ALU.mult), reads=[b_acc[k], b_et[k]], writes=[b_mqT])
                            elif c < 4:
                                S.op("act", lambda e: e.activation(out=et[k][:], in_=acc[k][:], func=AF.Silu), reads=[b_acc[k]], writes=[b_et[k]])
                                S.op("dve", lambda e: e.tensor_scalar(out=mqT[:, c, :], in0=et[k][:], scalar1=128.0 ** -0.5, scalar2=None, op0=ALU.mult),
                                     reads=[b_et[k]], writes=[b_mqT])
                            else:
                                S.op("act", lambda e: e.activation(out=mqT[:, c, :], in_=acc[k][:], func=AF.Silu), reads=[b_acc[k]], writes=[b_mqT])
                    sq = [sbt(st, "sq%d" % i, [128, 512]) for i in range(4)]
                    qb = [sbt(st, "qb%d" % i, [128, 512], BF16) for i in range(4)]
                    rs = [sbt(st, "rs%d" % i, [128, 16]) for i in range(4)]
                    KTb = [sbt(st, "KTb%d" % i, [128, 4, 128], BF16) for i in range(4)]
                    VAb = [sbt(st, "VAb%d" % i, [128, 4, 130], BF16) for i in range(4)]
                    b_t = [S.buf("tmj%d" % i) for i in range(4)]
                    b_KTb = [S.buf("KTb%d" % i) for i in range(4)]
                    b_VAb = [S.buf("VAb%d" % i) for i in range(4)]
                    for i in range(4):
                        if flagged:
                            S.op("dve", lambda e: e.tensor_copy(out=VAb[i][:, :, 128:130], in_=flag[:, 0:1].unsqueeze(1).to_broadcast([128, 4, 2])),
                                 reads=[cb[4]], writes=[b_VAb[i]])
                        else:
                            S.op("dve", lambda e: e.memset(VAb[i][:, :, 128:130], 1.0), writes=[b_VAb[i]])
                    def tm_block(j):
                        blk = u * 4 + j
                        i = j
                        tsl = slice(j * 128, (j + 1) * 128)
                        bsel = [0]

                        def bk():
                            bi = 2 * j + (bsel[0] % 2)
                            bsel[0] += 1
                            return banks[bi], bbufs[bi]

                        def proj(nm):
                            wt, wbf = W[nm]
                            n = PIECES[nm][1] // 8
                            wv = wt[:].rearrange("p (c n) -> p c n", c=8)
                            pb, pbb = bk()
                            for dc in range(8):
                                S.op("pe", lambda e: e.matmul(pb[:, 0:n], lhsT=hT[:, dc, tsl], rhs=wv[:, dc, :], start=(dc == 0), stop=(dc == 7)),
                                     reads=[wbf, b_hT, b_hT2], writes=[pbb])
                            return pb, pbb

                        def evac_scaled(out_ap, in_ap, rd, wr):
                            if flagged:
                                S.op("act", lambda e: e.activation(out=out_ap, in_=in_ap, func=AF.Copy, scale=flag[:, 0:1]), reads=rd + [cb[4]], writes=wr)
                            else:
                                S.op("act", lambda e: e.copy(out=out_ap, in_=in_ap), reads=rd, writes=wr)

                        pb, pbb = proj("mv")
                        yield
                        evac_scaled(mVA[:, j, :, 0:128], pb[:].rearrange("p (h d) -> p h d", h=4), [pbb], [b_mVA])
                        pb, pbb = proj("gif")
                        yield
                        S.op("dve", lambda e: e.tensor_tensor(out=gif[:, j, :], in0=pb[:, 0:8], in1=gifb[:], op=ALU.add), reads=[pbb, cb[9]], writes=[b_gif])
                        pb, pbb = proj("av")
                        yield
                        evac_scaled(VAb[i][:, :, 0:128], pb[:].rearrange("p (h d) -> p h d", h=4), [pbb], [b_VAb[i]])
                        S.dma("pool", va_d.rearrange("h p (b c) -> p h b c", c=130)[:, :, blk, :], VAb[i][:], reads=[b_VAb[i]], writes=[b_vad])
                        if full:
                            pb, pbb = proj("mo")
                            yield
                            S.op("act", lambda e: e.activation(out=sigmo[:, j, :], in_=pb[:], func=AF.Exp, scale=-1.0), reads=[pbb], writes=[b_sigj[j]])
                            yield
                            S.op("dve", lambda e: e.tensor_scalar(out=sigmo[:, j, :], in0=sigmo[:, j, :], scalar1=1.0, scalar2=None, op0=ALU.add), reads=[b_sigj[j]], writes=[b_sigj[j]])
                            S.op("dve", lambda e: e.reciprocal(out=sigmo[:, j, :], in_=sigmo[:, j, :]), reads=[b_sigj[j]], writes=[b_sigj[j], b_sig])
                        for nm in (["aq", "ak"] if full else ["ak"]):
                            pb, pbb = proj(nm)
                            yield
                            S.op("act", lambda e: e.activation(out=sq[i][:], in_=pb[:], func=AF.Square, scale=0.125), reads=[pbb], writes=[b_t[i]])
                            yield
                            S.op("dve", lambda e: e.tensor_reduce(out=rs[i][:, 0:8], in_=sq[i][:].rearrange("p (g d) -> p g d", d=64), axis=AX.X, op=ALU.add),
                                 reads=[b_t[i]], writes=[b_t[i]])
                            yield
                            S.op("act", lambda e: e.activation(out=rs[i][:, 8:16], in_=rs[i][:, 0:8], func=AF.Ln, bias=epst[:, 0:1], scale=1.0), reads=[b_t[i], CONST], writes=[b_t[i]])
                            S.op("act", lambda e: e.activation(out=rs[i][:, 8:16], in_=rs[i][:, 8:16], func=AF.Exp, scale=-0.5), reads=[b_t[i]], writes=[b_t[i]])
                            yield
                            S.op("dve", lambda e: e.tensor_tensor(out=qb[i][:].rearrange("p (g d) -> p g d", d=64), in0=pb[:].rearrange("p (g d) -> p g d", d=64),
                                                                 in1=rs[i][:, 8:16].unsqueeze(2).to_broadcast([128, 8, 64]), op=ALU.mult), reads=[pbb, b_t[i]], writes=[b_t[i]])
                            yield
                            tb, tbb = bk()
                            tv = bfview(tb)[:, 0:512].rearrange("p (h t) -> p h t", h=4)
                            for h in range(4):
                                S.op("pe", lambda e: e.transpose(out=tv[:, h, :], in_=qb[i][:, h * 128:(h + 1) * 128], identity=identb[:]), reads=[b_t[i], cb[0]], writes=[tbb])
                            yield
                            if nm == "aq":
                                S.op("act", lambda e: e.activation(out=Qbd[0:64, :, j, 0:128], in_=tv[0:64, :, :], func=AF.Copy, scale=gqfm[0:64, 0:1]), reads=[tbb, CONST] + cb, writes=[b_Qbd])
                                S.op("act", lambda e: e.activation(out=Qbd[64:128, :, j, 128:256], in_=tv[64:128, :, :], func=AF.Copy, scale=gqfm[64:128, 0:1]), reads=[tbb, CONST] + cb, writes=[b_Qbd])
                            else:
                                S.op("act", lambda e: e.activation(out=KTb[i][:], in_=tv, func=AF.Copy, scale=gkfm[:, 0:1]), reads=[tbb] + cb, writes=[b_KTb[i]])
                                S.dma("pool", kt_d.rearrange("h p k -> p h k")[:, :, blk * 128:(blk + 1) * 128], KTb[i][:], reads=[b_KTb[i]], writes=[b_ktd])
                            yield

                    b_sigj = [S.buf("sigj%d" % j_) for j_ in range(4)]
                    tgens = [tm_block(j_) for j_ in range(4)]
                    alive = list(tgens)
                    while alive:
                        nxt = []
                        for g_ in alive:
                            try:
                                next(g_)
                                nxt.append(g_)
                            except StopIteration:
                                pass
                        alive = nxt
                    S.barrier()

                def mlstm_block(j, st, bankfn, cache):
                    def sbt_c(st_, name, shape, dt=F32):
                        if name not in cache:
                            cache[name] = sbt(st_, name, shape, dt)
                        return cache[name]

                    def buf_c(name):
                        k_ = "B_" + name
                        if k_ not in cache:
                            cache[k_] = S.buf(name)
                        return cache[k_]
                    tsl = slice(j * 128, (j + 1) * 128)
                    lf = sbt_c(st, "lf%d" % (j % 2), [128, 4])
                    e1 = sbt_c(st, "e1%d" % (j % 2), [128, 4])
                    e2 = sbt_c(st, "e2%d" % (j % 2), [128, 4])
                    wsx = sbt_c(st, "wsx%d" % (j % 2), [128, 4])
                    RL = sbt_c(st, "RL%d" % (j % 2), [128, 4, 128])
                    Et = sbt_c(st, "Et%d" % (j % 2), [128, 512])
                    Kw = sbt_c(st, "Kw%d" % (j % 2), [128, 4, 128], BF16)
                    bm = buf_c("ml%d" % (j % 2))
                    b_e1 = buf_c("e1%d" % (j % 2))
                    b_ws = buf_c("wsx%d" % (j % 2))
                    b_RL = buf_c("RL%d" % (j % 2))
                    b_Et = buf_c("Et%d" % (j % 2))
                    b_Kw = buf_c("Kw%d" % (j % 2))
                    S.op("act", lambda e: e.activation(out=lf[:], in_=gif[:, j, 4:8], func=AF.Exp, scale=-1.0), reads=[b_gif], writes=[bm])
                    yield
                    S.op("act", lambda e: e.activation(out=lf[:], in_=lf[:], func=AF.Ln, bias=onet[:, 0:1], scale=1.0), reads=[bm, CONST], writes=[bm])
                    yield
                    S.op("dve", lambda e: e.tensor_scalar(out=lf[:], in0=lf[:], scalar1=-1.0, scalar2=None, op0=ALU.mult), reads=[bm], writes=[bm])
                    yield
                    p1, p1b = bankfn()
                    S.op("pe", lambda e: e.matmul(p1[:, 0:4], lhsT=umask[:], rhs=lf[:], start=True, stop=True), reads=[bm, cb[2]], writes=[p1b])
                    S.op("dve", lambda e: e.tensor_tensor(out=RL[:], in0=umask[:].unsqueeze(1).to_broadcast([128, 4, 128]),
                                                         in1=lf[:].unsqueeze(2).to_broadcast([128, 4, 128]), op=ALU.mult), reads=[bm, cb[2]], writes=[b_RL])
                    yield
                    S.op("dve", lambda e: e.tensor_tensor(out=e1[:], in0=gif[:, j, 0:4], in1=p1[:, 0:4], op=ALU.subtract), reads=[b_gif, p1b], writes=[b_e1])
                    yield
                    RLf = RL[:].rearrange("p h t -> p (h t)")
                    pBc, pBcb = bankfn()
                    S.op("pe", lambda e: e.matmul(pBc[:], lhsT=ones_f[:], rhs=RLf, start=True, stop=True), reads=[b_RL, CONST], writes=[pBcb])
                    yield
                    blast = pBc[:].rearrange("p (h t) -> p h t", h=4)[:, :, 127]
                    S.op("dve", lambda e: e.tensor_tensor(out=e2[:], in0=e1[:], in1=blast, op=ALU.add), reads=[b_e1, pBcb], writes=[b_ws])
                    yield
                    S.op("act", lambda e: e.activation(out=Et[:], in_=pBc[:], func=AF.Exp), reads=[pBcb], writes=[b_Et])
                    S.op("act", lambda e: e.activation(out=wsx[:], in_=e2[:], func=AF.Exp), reads=[b_ws], writes=[b_ws])
                    yield
                    tb, tbb = bankfn()
                    tv = bfview(tb)[:, 0:512].rearrange("p (h t) -> p h t", h=4)
                    for h in range(4):
                        S.op("pe", lambda e: e.transpose(out=tv[:, h, :], in_=mqT[:, 4 + h, tsl], identity=identb[:]), reads=[b_mqT, cb[0]], writes=[tbb])
                    yield
                    S.op("dve", lambda e: e.tensor_tensor(out=Kw[:], in0=tv, in1=wsx[:].unsqueeze(2).to_broadcast([128, 4, 128]), op=ALU.mult),
                         reads=[tbb, b_ws], writes=[b_Kw])
                    yield
                    if full:
                        DT = sbt_c(st, "DT%d" % (j % 2), [128, 4, 128])
                        PT = sbt_c(st, "PT%d" % (j % 2), [128, 4, 128], BF16)
                        qpT = sbt_c(st, "qpT%d" % (j % 2), [128, 4, 128], BF16)
                        numS = sbt_c(st, "numS%d" % (j % 2), [128, 4, 130])
                        rden = sbt_c(st, "rden%d" % (j % 2), [128, 4])
                        hmr = sbt_c(st, "hmr%d" % (j % 2), [128, 4, 128])
                        hsq = sbt_c(st, "hsq%d" % (j % 2), [128, 4, 128])
                        hss = sbt_c(st, "hss%d" % (j % 2), [128, 8])
                        gs = sbt_c(st, "gs%d" % (j % 2), [128, 512])
                        hmf = sbt_c(st, "hmf%d" % (j % 2), [128, 4, 128], BF16)
                        b_o = buf_c("mo%d" % (j % 2))
                        b_gs = buf_c("gs%d" % (j % 2))
                        b_num = buf_c("numS%d" % (j % 2))
                        b_DT = buf_c("DT%d" % (j % 2))
                        b_PT = buf_c("PT%d" % (j % 2))
                        b_qp = buf_c("qpT%d" % (j % 2))
                        pBm, pBmb = bankfn()
                        S.op("pe", lambda e: e.matmul(pBm[:], lhsT=ones_f[:], rhs=RLf, start=True, stop=False), reads=[b_RL, CONST], writes=[pBmb])
                        S.op("pe", lambda e: e.matmul(pBm[:], lhsT=identf[:], rhs=negm4[:], start=False, stop=True), reads=[cb[1], cb[3]], writes=[pBmb])
                        S.op("dve", lambda e: e.tensor_tensor(out=qpT[:], in0=mqT[:, 0:4, tsl], in1=Et[:].rearrange("p (h t) -> p h t", h=4), op=ALU.mult),
                             reads=[b_mqT, b_Et], writes=[b_qp])
                        S.op("dve", lambda e: e.tensor_tensor(out=gs[:], in0=mng[:], in1=sigmo[:, j, :], op=ALU.mult), reads=[b_sig] + cb, writes=[b_gs])
                        yield
                        for h in range(4):
                            S.op("act", lambda e: e.activation(out=DT[:, h, :], in_=pBm[:, h * 128:(h + 1) * 128], func=AF.Exp, bias=e1[:, h:h + 1], scale=1.0),
                                 reads=[pBmb, b_e1], writes=[b_DT])
                        yield
                        pA, pAb = bankfn()
                        for h in range(4):
                            S.op("pe", lambda e: e.matmul(pA[:, h * 128:(h + 1) * 128], lhsT=mqT[:, 4 + h, tsl], rhs=mqT[:, h, tsl], start=True, stop=True),
                                 reads=[b_mqT], writes=[pAb])
                        yield
                        S.op("dve", lambda e: e.tensor_tensor(out=PT[:], in0=pA[:].rearrange("p (h t) -> p h t", h=4), in1=DT[:], op=ALU.mult),
                             reads=[pAb, b_DT], writes=[b_PT])
                        yield
                    yield "AB"
                    if full:
                        for hp in range(2):
                            pn, pnb = bankfn()
                            for hh in range(2):
                                h = 2 * hp + hh
                                o = hh * 130
                                S.op("pe", lambda e: e.matmul(pn[:, o:o + 130], lhsT=PT[:, h, :], rhs=mVA[:, j, h, :], start=True, stop=False), reads=[b_PT, b_mVA], writes=[pnb])
                                S.op("pe", lambda e: e.matmul(pn[:, o:o + 130], lhsT=qpT[:, h, :], rhs=Sstb[:, h, :], start=False, stop=True), reads=[b_qp, b_Sb], writes=[pnb])
                            yield
                            S.op("dve", lambda e: e.tensor_copy(out=numS[:, 2 * hp:2 * hp + 2, :], in_=pn[:, 0:260].rearrange("p (h c) -> p h c", h=2)), reads=[pnb], writes=[b_num])
                            yield
                    Ev = Et[:].rearrange("p (h t) -> p h t", h=4)
                    for hp in range(2):
                        pc_, pcb = bankfn()
                        for hh in range(2):
                            h = 2 * hp + hh
                            o = hh * 130
                            S.op("pe", lambda e: e.matmul(pc_[:, o:o + 130], lhsT=Kw[:, h, :], rhs=mVA[:, j, h, :], start=True, stop=True), reads=[b_Kw, b_mVA], writes=[pcb])
                        yield
                        for hh in range(2):
                            h = 2 * hp + hh
                            o = hh * 130
                            S.op("dve", lambda e: e.scalar_tensor_tensor(out=Sst[:, h, :], in0=Sst[:, h, :], scalar=Ev[:, h, 127:128], in1=pc_[:, o:o + 130],
                                                                        op0=ALU.mult, op1=ALU.add), reads=[b_S, b_Et, pcb], writes=[b_S])
                        yield
                    S.op("dve", lambda e: e.tensor_copy(out=Sstb[:], in_=Sst[:]), reads=[b_S], writes=[b_Sb])
                    yield "S"
                    if full:
                        S.op("act", lambda e: e.activation(out=rden[:], in_=numS[:, :, 128], func=AF.Abs), reads=[b_num], writes=[b_o])
                        yield
                        S.op("dve", lambda e: e.tensor_scalar(out=rden[:], in0=rden[:], scalar1=1.0, scalar2=None, op0=ALU.max), reads=[b_o], writes=[b_o])
                        S.op("dve", lambda e: e.reciprocal(out=rden[:], in_=rden[:]), reads=[b_o], writes=[b_o])
                        S.op("dve", lambda e: e.tensor_tensor(out=hmr[:], in0=numS[:, :, 0:128], in1=rden[:].unsqueeze(2).to_broadcast([128, 4, 128]), op=ALU.mult),
                             reads=[b_num, b_o], writes=[b_o])
                        yield
                        S.op("act", lambda e: e.activation(out=hsq[:], in_=hmr[:], func=AF.Square, scale=1.0 / math.sqrt(128.0)), reads=[b_o], writes=[b_o])
                        yield
                        S.op("dve", lambda e: e.tensor_reduce(out=hss[:, 0:4], in_=hsq[:], axis=AX.X, op=ALU.add), reads=[b_o], writes=[b_o])
                        yield
                        S.op("act", lambda e: e.activation(out=hss[:, 4:8], in_=hss[:, 0:4], func=AF.Ln, bias=epst[:, 0:1], scale=1.0), reads=[b_o, CONST], writes=[b_o])
                        S.op("act", lambda e: e.activation(out=hss[:, 4:8], in_=hss[:, 4:8], func=AF.Exp, scale=-0.5), reads=[b_o], writes=[b_o])
                        yield
                        for h in range(4):
                            S.op("dve", lambda e: e.scalar_tensor_tensor(out=hmf[:, h, :], in0=hmr[:, h, :], scalar=hss[:, 4 + h:5 + h], in1=gs[:, h * 128:(h + 1) * 128],
                                                                        op0=ALU.mult, op1=ALU.mult), reads=[b_o, b_gs], writes=[b_o])
                        yield
                        tb2, tb2b = bankfn()
                        tv2 = bfview(tb2)[:, 0:512].rearrange("p (h t) -> p h t", h=4)
                        for h in range(4):
                            S.op("pe", lambda e: e.transpose(out=tv2[:, h, :], in_=hmf[:, h, :], identity=identb[:]), reads=[b_o, cb[0]], writes=[tb2b])
                        yield
                        S.op("dve", lambda e: e.tensor_copy(out=hmT[:, :, tsl], in_=tv2), reads=[tb2b], writes=[b_hmT])
                        yield

                def mlstm_driver(st, bankfn):
                    cache = {}
                    for j in range(4):
                        for r in mlstm_block(j, st, bankfn, cache):
                            yield

                if not full:
                    with ExitStack() as st:
                        for _ in mlstm_driver(st, nbank):
                            pass
                        S.barrier()

                if full:
                    with ExitStack() as st:
                        NKC = 8
                        kch = [sbt(st, "kch%d" % i, [128, NKC * 128], BF16) for i in range(3)]
                        vch = [sbt(st, "vch%d" % i, [128, NKC, 130], BF16) for i in range(3)]
                        b_kch = [S.buf("kch%d" % i) for i in range(3)]
                        b_vch = [S.buf("vch%d" % i) for i in range(3)]
                        pex = [sbt(st, "pex%d" % i, [128, 512], BF16) for i in range(6)]
                        b_pex = [S.buf("pex%d" % i) for i in range(6)]
                        rr = sbt(st, "rr", [128, 8])
                        har = sbt(st, "har", [128, 4, 128])
                        hsq = sbt(st, "ahsq", [128, 4, 128])
                        hss = sbt(st, "ahss", [128, 8])
                        haf = sbt(st, "haf", [128, 4, 128], BF16)
                        b_a = S.buf("attn_o")
                        nkb_tot = u * 4 + 4
                        chunk_list = [(s0, min(NKC, nkb_tot - s0)) for s0 in range(0, nkb_tot, NKC)]
                        ring = 0
                        LCH = len(chunk_list)
                        prefetch((0, 1), ["bm", "gm0"])
                        mdrv = mlstm_driver(st, lambda: (banks[7], bbufs[7]))
                        n_items_est = 4 * sum((2 if kb_ <= u * 4 - 2 else 0) + sum(1 for p2 in range(2) for jq in (2 * p2, 2 * p2 + 1) if kb_ > u * 4 + 2 * p2 - 2 and kb_ <= u * 4 + jq)
                                              for kb_ in range(nkb_tot))
                        MSTEP = max(1, n_items_est // 150)
                        gcount = [0]
                        loaded = set()

                        def ensure_chunk(g):
                            if g >= 4 * LCH or g in loaded:
                                return
                            loaded.add(g)
                            h_, ci_ = g // LCH, g % LCH
                            s0, nk = chunk_list[ci_]
                            ri = g % 3
                            S.dma("sp", kch[ri][:, 0:nk * 128], kt_d[h_, :, s0 * 128:(s0 + nk) * 128], reads=[b_ktd], writes=[b_kch[ri]])
                            S.dma("sp", vch[ri][:, 0:nk, :], va_d[h_, :, s0 * 130:(s0 + nk) * 130].rearrange("p (b c) -> p b c", c=130), reads=[b_vad], writes=[b_vch[ri]])

                        accS = [sbt(st, "accS%d" % i, [128, 4, 2, 130]) for i in range(2)]
                        b_accS = [S.buf("accS%d" % i) for i in range(2)]
                        rr2 = [sbt(st, "rr2_%d" % i, [128, 8]) for i in range(2)]
                        har2 = [sbt(st, "har2_%d" % i, [128, 4, 128]) for i in range(2)]
                        hsq2 = [sbt(st, "hsq2_%d" % i, [128, 4, 128]) for i in range(2)]
                        hss2 = [sbt(st, "hss2_%d" % i, [128, 8]) for i in range(2)]
                        haf2 = [sbt(st, "haf2_%d" % i, [128, 4, 128], BF16) for i in range(2)]
                        b_ep = [S.buf("epi%d" % i) for i in range(2)]

                        def attn_epi(h, accs):
                            p = h % 2
                            aS, rr, har, hsq, hss, haf, b_a = accS[p], rr2[p], har2[p], hsq2[p], hss2[p], haf2[p], b_ep[p]
                            for jq in range(4):
                                ab, abb = accs[jq]
                                S.op("dve", lambda e: e.tensor_copy(out=aS[:, jq, :, :], in_=ab[:, 0:260].rearrange("p (m c) -> p m c", m=2)), reads=[abb], writes=[b_accS[p]])
                            yield
                            for jq in range(4):
                                av = aS[:, jq, :, :]
                                S.op("dve", lambda e: e.tensor_scalar(out=rr[:, 2 * jq:2 * jq + 2], in0=av[:, :, 128], scalar1=1e-30, scalar2=None, op0=ALU.max), reads=[b_accS[p]], writes=[b_a])
                                S.op("dve", lambda e: e.reciprocal(out=rr[:, 2 * jq:2 * jq + 2], in_=rr[:, 2 * jq:2 * jq + 2]), reads=[b_a], writes=[b_a])
                                S.op("dve", lambda e: e.tensor_tensor(out=rr[:, 2 * jq + 1:2 * jq + 2], in0=rr[:, 2 * jq + 1:2 * jq + 2], in1=neglam[:], op=ALU.mult),
                                     reads=[b_a, CONST], writes=[b_a])
                                S.op("dve", lambda e: e.tensor_scalar(out=har[:, jq, :], in0=av[:, 0, 0:128], scalar1=rr[:, 2 * jq:2 * jq + 1], scalar2=None, op0=ALU.mult),
                                     reads=[b_accS[p], b_a], writes=[b_a])
                                S.op("dve", lambda e: e.scalar_tensor_tensor(out=har[:, jq, :], in0=av[:, 1, 0:128], scalar=rr[:, 2 * jq + 1:2 * jq + 2], in1=har[:, jq, :],
                                                                            op0=ALU.mult, op1=ALU.add), reads=[b_accS[p], b_a], writes=[b_a])
                                yield
                            S.op("act", lambda e: e.activation(out=hsq[:], in_=har[:], func=AF.Square, scale=1.0 / math.sqrt(128.0)), reads=[b_a], writes=[b_a])
                            yield
                            S.op("dve", lambda e: e.tensor_reduce(out=hss[:, 0:4], in_=hsq[:], axis=AX.X, op=ALU.add), reads=[b_a], writes=[b_a])
                            yield
                            S.op("act", lambda e: e.activation(out=hss[:, 4:8], in_=hss[:, 0:4], func=AF.Ln, bias=epst[:, 0:1], scale=1.0), reads=[b_a, CONST], writes=[b_a])
                            S.op("act", lambda e: e.activation(out=hss[:, 4:8], in_=hss[:, 4:8], func=AF.Exp, scale=-0.5), reads=[b_a], writes=[b_a])
                            yield
                            for jq in range(4):
                                S.op("dve", lambda e: e.scalar_tensor_tensor(out=haf[:, jq, :], in0=har[:, jq, :], scalar=hss[:, 4 + jq:5 + jq], in1=ang[:, h * 128:(h + 1) * 128],
                                                                            op0=ALU.mult, op1=ALU.mult), reads=[b_a, CONST] + cb, writes=[b_a])
                            yield
                            tb, tbb = banks[4 + h % 3], bbufs[4 + h % 3]
                            tv = bfview(tb)[:, 0:512].rearrange("p (j t) -> p j t", j=4)
                            for jq in range(4):
                                S.op("pe", lambda e: e.transpose(out=tv[:, jq, :], in_=haf[:, jq, :], identity=identb[:]), reads=[b_a, cb[0]], writes=[tbb])
                            S.op("dve", lambda e: e.tensor_copy(out=haT[:, h, :], in_=tv.rearrange("p j t -> p (j t)")), reads=[tbb], writes=[b_haT])
                            yield

                        epi = [iter(())]
                        for h in range(4):
                            accs = [(banks[jq], bbufs[jq]) for jq in range(4)]
                            for jq in range(4):
                                S.op("dve", lambda e: e.memset(accs[jq][0][:, 0:260], 0.0), writes=[accs[jq][1]])
                            items = []
                            for ci, (s0, nk) in enumerate(chunk_list):
                                g = h * LCH + ci
                                ri = g % 3
                                for kk in range(nk):
                                    kb = s0 + kk
                                    for p2 in range(2):
                                        j0 = 2 * p2
                                        if u == NCTX:
                                            if p2 == 1 and kb <= u * 4 + 3:
                                                items.append((ri, kk, kb, (3,), g))
                                            continue
                                        if kb <= u * 4 + j0 - 2:
                                            items.append((ri, kk, kb, (j0, j0 + 1), g))
                                        else:
                                            for jq in (j0, j0 + 1):
                                                if kb <= u * 4 + jq:
                                                    items.append((ri, kk, kb, (jq,), g))

                            LAG = 3
                            NS = 3
                            NPX = len(pex)

                            def emit_scores(it, idx):
                                ri, kk, kb, jqs, ci = it
                                ensure_chunk(ci)
                                ensure_chunk(ci + 1)
                                nq = len(jqs)
                                sl = idx % NS
                                sp_, spb = banks[4 + sl], bbufs[4 + sl]
                                sv = sp_[:, 0:256 * nq]
                                px = idx % NPX
                                delta = u * 4 + jqs[0] - kb
                                near = (nq == 1) and delta <= 1
                                rhs = Qbd[:, h, jqs[0]:jqs[0] + nq, :].rearrange("p j c -> p (j c)")
                                S.op("pe", lambda e: e.matmul(sv, lhsT=kch[ri][:, kk * 128:(kk + 1) * 128], rhs=rhs, start=True, stop=not near),
                                     reads=[b_kch[ri], b_Qbd], writes=[spb])
                                if near:
                                    S.op("pe", lambda e: e.matmul(sv, lhsT=identb[:], rhs=biasb[:, h, delta, :], start=False, stop=True), reads=[CONST, cb[0]], writes=[spb])
                                    S.op("act", lambda e: e.activation(out=pex[px][:, 0:256 * nq], in_=sv, func=AF.Exp), reads=[spb], writes=[b_pex[px]])
                                else:
                                    S.op("act", lambda e: e.activation(out=pex[px][:, 0:256 * nq], in_=sv, func=AF.Exp, bias=farb[:, h:h + 1], scale=1.0),
                                         reads=[spb, cb[14]], writes=[b_pex[px]])

                            def emit_pv(it, idx):
                                ri, kk, kb, jqs, ci = it
                                px = idx % NPX
                                for a_, jq in enumerate(jqs):
                                    ab, abb = accs[jq]
                                    for m in range(2):
                                        c0 = a_ * 256 + m * 128
                                        S.op("pe", lambda e: e.matmul(ab[:, m * 130:(m + 1) * 130], lhsT=pex[px][:, c0:c0 + 128], rhs=vch[ri][:, kk, :],
                                                                     start=False, stop=False, skip_group_check=True), reads=[b_pex[px], b_vch[ri]], writes=[abb])

                            for idx in range(len(items) + LAG):
                                if idx < len(items):
                                    emit_scores(items[idx], idx)
                                if idx >= LAG:
                                    emit_pv(items[idx - LAG], idx - LAG)
                                gcount[0] += 1
                                if gcount[0] % MSTEP == 0:
                                    next(mdrv, None)
                                if idx % 6 == 5:
                                    next(epi[0], None)
                            for _ in epi[0]:
                                pass
                            epi[0] = attn_epi(h, accs)
                            next(epi[0], None)
                        for _ in epi[0]:
                            pass
                        for _ in mdrv:
                            pass
                        S.barrier()

                    h2T = hT
                    b_h2T = b_hT
                    with ExitStack() as st:
                        W = {}
                        for nm in ("bm", "gm0", "ba", "ga0", "gm1", "ga1", "out0", "out1"):
                            W[nm] = wload(st, "w_" + nm, nm)
                        prefetch((2, 3), ["up0", "up1"])
                        yT = sbt(st, "yT", [128, 8, 512], BF16)
                        b_yT = S.buf("yT")
                        sg = [[sbt(st, "sg%d_%d" % (i, a), [128, 512]) for a in range(2)] for i in range(2)]
                        ty = [[sbt(st, "ty%d_%d" % (i, a), [128, 512]) for a in range(2)] for i in range(2)]
                        b_sg = [S.buf("sg%d" % i) for i in range(2)]
                        for c in range(8):
                            i = c % 2
                            res = []
                            for a, (bw, gw, srcT, b_src) in enumerate((("bm", "gm", hmT, b_hmT), ("ba", "ga", haT, b_haT))):
                                wt, wbf = W[bw]
                                wv = wt[:].rearrange("p (k n) -> p k n", k=4)
                                py, pyb = nbank()
                                for k in range(4):
                                    S.op("pe", lambda e: e.matmul(py[:], lhsT=wv[:, k, c * 128:(c + 1) * 128], rhs=srcT[:, k, :], start=(k == 0), stop=(k == 3)),
                                         reads=[wbf, b_src], writes=[pyb])
                                gt_, gbf = W[gw + str(c // 4)]
                                gv = gt_[:].rearrange("p (c n) -> p c n", c=8)
                                pg, pgb = nbank()
                                for dc in range(8):
                                    S.op("pe", lambda e: e.matmul(pg[:], lhsT=gv[:, dc, (c % 4) * 128:(c % 4 + 1) * 128], rhs=hT[:, dc, :], start=(dc == 0), stop=(dc == 7)),
                                         reads=[gbf, b_hT, b_hT2], writes=[pgb])
                                S.op("act", lambda e: e.activation(out=sg[i][a][:], in_=pg[:], func=AF.Sigmoid), reads=[pgb], writes=[b_sg[i]])
                                S.op("dve", lambda e: e.tensor_tensor(out=ty[i][a][:], in0=py[:], in1=sg[i][a][:], op=ALU.mult), reads=[pyb, b_sg[i]], writes=[b_sg[i]])
                            S.op("dve", lambda e: e.tensor_tensor(out=yT[:, c, :], in0=ty[i][0][:], in1=ty[i][1][:], op=ALU.add), reads=[b_sg[i]], writes=[b_yT])
                        tx = [sbt(st, "tx%d" % i, [128, 512]) for i in range(2)]
                        b_tx = [S.buf("tx%d" % i) for i in range(2)]
                        for j in range(4):
                            tsl = slice(j * 128, (j + 1) * 128)
                            for n in range(2):
                                wt, wbf = W["out%d" % n]
                                wv = wt[:].rearrange("p (c n) -> p c n", c=8)
                                po, pob = nbank()
                                for c in range(8):
                                    S.op("pe", lambda e: e.matmul(po[:], lhsT=yT[:, c, tsl], rhs=wv[:, c, :], start=(c == 0), stop=(c == 7)), reads=[wbf, b_yT], writes=[pob])
                                i = (j * 2 + n) % 2
                                S.op("dve", lambda e: e.tensor_tensor(out=tx[i][:], in0=po[:], in1=gate1[:, n * 512:(n + 1) * 512], op=ALU.mult), reads=[pob, CONST], writes=[b_tx[i]])
                                S.op("dve", lambda e: e.tensor_tensor(out=xs[:, j, n * 512:(n + 1) * 512], in0=xs[:, j, n * 512:(n + 1) * 512], in1=tx[i][:], op=ALU.add),
                                     reads=[b_tx[i], b_xs[j]], writes=[b_xs[j]])
                        run_norms(st, mult2, shift2, h2T, (b_hT, b_hT2), "n2")
                        S.barrier()

                    sa.close()
                    with ExitStack() as st:
                        actT = sbt(st, "actT", [128, NF, 512], BF16)
                        b_actT = S.buf("actT")
                        wu = [None, None, None]
                        uu = [sbt(st, "fu%d" % i, [128, 512]) for i in range(8)]
                        b_uu = [S.buf("fu%d" % i) for i in range(8)]
                        fe = [sbt(st, "fe%d" % i, [128, 512]) for i in range(2)]
                        b_fe = [S.buf("fe%d" % i) for i in range(2)]
                        wus = [pf[2], pf[3], sbt(st, "wup2", [128, 4096], BF16)]
                        b_wus = [b_pf[2], b_pf[3], S.buf("wup2")]

                        def ldup(jj):
                            if ("up%d" % jj) in prefetched:
                                prefetched.pop("up%d" % jj)
                                return
                            off, n = PIECES["up%d" % jj]
                            S.dma("sp", wus[jj % 3][:], wb_d[:, off:off + n], reads=[b_wb], writes=[b_wus[jj % 3]])

                        ldup(0)
                        ldup(1)
                        wd = sbt(st, "w_down", [128, NF * 1024], BF16)
                        wdb = S.buf("w_down")
                        wdv = wd[:].rearrange("p (f n) -> p f n", f=NF)
                        if not OPT_BATCH:
                            S.op("dve", lambda e: e.memset(fbc[:], 0.0), writes=[b_fbc])
                        if OPT_BATCH:
                            S.op("dve", lambda e: e.tensor_tensor(out=fbc[:, :, 1], in0=fhalo[:, :, 1], in1=fcw[:, :, 0], op=ALU.mult), reads=[b_fhalo] + cb, writes=[b_fbc])
                            S.op("dve", lambda e: e.tensor_tensor(out=fbc[:, :, 0], in0=fhalo[:, :, 1], in1=fcw[:, :, 1], op=ALU.mult), reads=[b_fhalo] + cb, writes=[b_fbc])
                            S.op("dve", lambda e: e.tensor_tensor(out=ctmp[:], in0=fhalo[:, :, 0], in1=fcw[:, :, 0], op=ALU.mult), reads=[b_fhalo] + cb, writes=[b_fbc])
                            S.op("dve", lambda e: e.tensor_tensor(out=fbc[:, :, 0], in0=fbc[:, :, 0], in1=ctmp[:], op=ALU.add), reads=[b_fbc], writes=[b_fbc])
                            S.op("dve", lambda e: e.tensor_tensor(out=fbc[:], in0=fbc[:], in1=fcb[:].unsqueeze(2).to_broadcast([128, 44, 2]), op=ALU.add), reads=[b_fbc] + cb, writes=[b_fbc])

                        def ffn_piece(jj):
                            if jj + 2 < 11:
                                ldup(jj + 2)
                            if jj == 2 and u != NCTX:
                                off_d, n_d = PIECES["down"]
                                S.dma("sp", wd[:], wb_d[:, off_d:off_d + n_d], reads=[b_wb], writes=[wdb])
                            if jj == 4 and u + 1 < NU:
                                prefetch((0, 1), ["mq", "mk"])
                            wv = wus[jj % 3][:].rearrange("p (c n) -> p c n", c=8)
                            wbf = b_wus[jj % 3]
                            wsel = fcwf if flagged else fcw
                            pbs = []
                            for k in range(4):
                                q = jj * 4 + k
                                pb, pbb = nbank()
                                pbs.append((pb, pbb))
                                for dc in range(8):
                                    S.op("pe", lambda e: e.matmul(pb[:], lhsT=wv[:, dc, k * 128:(k + 1) * 128], rhs=h2T[:, dc, :], start=(dc == 0), stop=(dc == 7)),
                                         reads=[wbf, b_hT, b_hT2], writes=[pbb])
                                kk_ = (jj % 2) * 4 + k
                                m0 = 2 if OPT_PARTMAIN else 0
                                S.op("act", lambda e: e.activation(out=uu[kk_][:, m0:512], in_=pb[:, m0:512], func=AF.Identity, scale=wsel[:, q, 2:3], bias=fcb[:, q:q + 1]),
                                     reads=[pbb, CONST] + cb, writes=[b_uu[kk_]])
                                for col in (range(2) if OPT_TINY else ()):
                                    S.op("act", lambda e: e.activation(out=uu[kk_][:, col:col + 1], in_=pb[:, col:col + 1], func=AF.Identity, scale=wsel[:, q, 2:3],
                                                                      bias=fbc[:, q, col:col + 1]), reads=[pbb, CONST, b_fbc] + cb, writes=[b_uu[kk_]])
                                if flagged:
                                    S.op("act", lambda e: e.activation(out=fhalo[:, q, :], in_=pb[:, 510:512], func=AF.Copy, scale=flag[:, 0:1]), reads=[pbb, cb[4], b_fbc], writes=[b_fhalo])
                                else:
                                    S.op("act", lambda e: e.copy(out=fhalo[:, q, :], in_=pb[:, 510:512]), reads=[pbb, b_fbc], writes=[b_fhalo])
                            for tp in (1, 0):
                                sh = 2 - tp
                                for k in range(4):
                                    q = jj * 4 + k
                                    kk_ = (jj % 2) * 4 + k
                                    pb, pbb = pbs[k]
                                    S.op("dve", lambda e: e.scalar_tensor_tensor(out=uu[kk_][:, sh:512], in0=pb[:, 0:512 - sh], scalar=wsel[:, q, tp:tp + 1], in1=uu[kk_][:, sh:512],
                                                                                op0=ALU.mult, op1=ALU.add), reads=[pbb, b_uu[kk_], CONST], writes=[b_uu[kk_]])
                            yield
                            for i in (range(2) if u != NCTX else ()):
                                f = 2 * jj + i
                                uv = uu[(jj % 2) * 4 + i]
                                ug = uu[(jj % 2) * 4 + 2 + i]
                                b_uv = b_uu[(jj % 2) * 4 + i]
                                b_ug = b_uu[(jj % 2) * 4 + 2 + i]
                                S.op("act", lambda e: e.activation(out=fe[i][:], in_=ug[:], func=AF.Silu), reads=[b_ug], writes=[b_fe[i]])
                                S.op("dve", lambda e: e.tensor_tensor(out=actT[:, f, :], in0=uv[:], in1=fe[i][:], op=ALU.mult), reads=[b_uv, b_fe[i]], writes=[b_actT])
                        fgens = [ffn_piece(jj) for jj in range(11)]
                        for step in range(12):
                            if step < 11:
                                next(fgens[step], None)
                            if step >= 1:
                                next(fgens[step - 1], None)
                        tx = fe
                        b_tx = b_fe
                        for j in (range(4) if u != NCTX else ()):
                            tsl = slice(j * 128, (j + 1) * 128)
                            blk = u * 4 + j
                            for n in range(2):
                                po, pob = nbank()
                                for f in range(NF):
                                    S.op("pe", lambda e: e.matmul(po[:], lhsT=actT[:, f, tsl], rhs=wdv[:, f, n * 512:(n + 1) * 512], start=(f == 0), stop=(f == NF - 1)),
                                         reads=[wdb, b_actT], writes=[pob])
                                i = (j * 2 + n) % 2
                                S.op("dve", lambda e: e.tensor_tensor(out=tx[i][:], in0=po[:], in1=gate2[:, n * 512:(n + 1) * 512], op=ALU.mult), reads=[pob, CONST], writes=[b_tx[i]])
                                S.op("dve", lambda e: e.tensor_tensor(out=xs[:, j, n * 512:(n + 1) * 512], in0=xs[:, j, n * 512:(n + 1) * 512], in1=tx[i][:], op=ALU.add),
                                     reads=[b_tx[i], b_xs[j]], writes=[b_xs[j]])
                            if own:
                                ob = blk - NFLAG * 4
                                S.dma("pool", out_d[ob * 128:(ob + 1) * 128, :], xs[:, j, :], reads=[b_xs[j]], writes=[b_out])
                        S.barrier()
        S.barrier()
    return nc


def _t5_bucket(n):
    n = np.maximum(n, 0)
    max_exact = 16
    nf = np.maximum(n, 1).astype(np.float32)
    large = max_exact + (np.log(nf / np.float32(max_exact)) / np.float32(math.log(128 / max_exact)) * np.float32(32 - max_exact)).astype(np.int32)
    large = np.minimum(large, 31)
    return np.where(n < max_exact, n, large)


def _fm(v, nch):
    return np.ascontiguousarray(np.asarray(v, np.float32).reshape(nch, 128).T)


def _piece_fm(w):
    n = w.shape[1]
    return w.reshape(8, 128, n).transpose(1, 0, 2).reshape(128, 8 * n)


def prepare_inputs(NCTX, NFULL, inputs):
    f32 = np.float32
    g = {k: np.asarray(v) for k, v in inputs.items()}
    NU = NCTX + NFULL
    half_tok = (NU // 2) * 512
    x = g["x"].astype(f32, copy=False)
    B = x.shape[0]
    assert x.shape[1] == 2 * half_tok
    w_in = g["w_in"][0]
    cols = {}
    o = 0
    for nm, n in (("mqk", 1024), ("mv", 512), ("mo", 512), ("mi", 4), ("mf", 4), ("aq", 512), ("ak", 512), ("av", 512), ("gm", 1024), ("ga", 1024)):
        cols[nm] = w_in[:, o:o + n]
        o += n

    def perm_qk(w):
        return w.reshape(1024, 2, 4, 64).transpose(0, 2, 1, 3).reshape(1024, 512)

    wall = np.zeros((128, WPAD), f32)

    def put(nm, arr):
        off, n = PIECES[nm]
        assert arr.shape == (128, n), (nm, arr.shape, n)
        wall[:, off:off + n] = arr

    put("mq", _piece_fm(cols["mqk"][:, 0:512]))
    put("mk", _piece_fm(cols["mqk"][:, 512:1024]))
    put("mv", _piece_fm(cols["mv"]))
    put("mo", _piece_fm(cols["mo"]))
    put("aq", _piece_fm(perm_qk(cols["aq"])))
    put("ak", _piece_fm(perm_qk(cols["ak"])))
    put("av", _piece_fm(cols["av"]))
    put("gm0", _piece_fm(cols["gm"][:, 0:512]))
    put("gm1", _piece_fm(cols["gm"][:, 512:1024]))
    put("ga0", _piece_fm(cols["ga"][:, 0:512]))
    put("ga1", _piece_fm(cols["ga"][:, 512:1024]))
    put("gif", _piece_fm(np.concatenate([cols["mi"], cols["mf"]], axis=1)))
    put("bm", g["w_branch_m"][0].reshape(4, 128, 1024).transpose(1, 0, 2).reshape(128, 4096))
    put("ba", g["w_branch_a"][0].reshape(4, 128, 1024).transpose(1, 0, 2).reshape(128, 4096))
    put("out0", _piece_fm(g["w_out"][0][:, 0:512]))
    put("out1", _piece_fm(g["w_out"][0][:, 512:1024]))
    w_up = g["w_up"][0]
    fcw_full = g["ffn_conv_w"][0]
    fcb_full = g["ffn_conv_b"][0]
    chunk_cols = []
    for jj in range(11):
        cc = [np.arange((2 * jj + i) * 128, (2 * jj + i + 1) * 128) for i in range(2)]
        cc += [DFF + np.arange((2 * jj + i) * 128, (2 * jj + i + 1) * 128) for i in range(2)]
        idx = np.concatenate(cc)
        chunk_cols.append(idx)
        put("up%d" % jj, _piece_fm(w_up[:, idx]))
    allidx = np.concatenate(chunk_cols)
    fcw = fcw_full[:, allidx].reshape(3, 44, 128).transpose(2, 1, 0).reshape(128, 44 * 3)
    fcb = fcb_full[allidx].reshape(44, 128).T
    put("down", g["w_down"][0].reshape(NF, 128, 1024).transpose(1, 0, 2).reshape(128, NF * 1024))

    w_ada = g["w_ada"][0]
    wada = np.stack([_piece_fm(w_ada[:, p * 512:(p + 1) * 512]) for p in range(12)]).astype(f32)
    bada = g["b_ada"][0].astype(f32)
    mcw = g["m_conv_w"][0].reshape(4, 8, 128).transpose(2, 1, 0).reshape(128, 32)
    mcb = _fm(g["m_conv_b"][0], 8)
    gifb = np.concatenate([g["m_igate_b"][0], g["m_fgate_b"][0]]).astype(f32)
    gq = np.tile(g["a_qnorm_g"][0], 8).astype(f32)
    gk = np.tile(g["a_knorm_g"][0], 8).astype(f32)
    rel = g["rel_bias"].astype(f32)
    kk = np.arange(128)[:, None]
    qq = np.arange(128)[None, :]
    biasg = np.zeros((128, 4, 2, 2, 128), f32)
    maskn = np.zeros((128, 4, 2, 2, 128), f32)
    for dl in range(2):
        dist = qq - kk + 128 * dl
        bidx = _t5_bucket(dist)
        for h in range(4):
            t = rel[bidx, h]
            biasg[:, h, dl, 0, :] = t
            biasg[:, h, dl, 1, :] = t
        if dl == 0:
            mk = np.where(dist < 0, NEG, 0.0).astype(f32)
            maskn[:, :, 0, :, :] = mk[:, None, None, :]
    farb = rel[31, :].astype(f32)
    umask = (kk <= qq).astype(f32)
    negm4 = np.tile(np.where(kk <= qq, 0.0, NEG).astype(f32), (1, 4))
    import ml_dtypes
    common = dict(
        wada=wada, bada=bada, badafm=_fm(bada, 48), g1fm=_fm(g["norm1_g"][0], 8), g2fm=_fm(g["norm2_g"][0], 8),
        wall=wall, mcw=np.ascontiguousarray(mcw, f32), mcb=mcb, fcw=np.ascontiguousarray(fcw, f32), fcb=np.ascontiguousarray(fcb, f32),
        gifb=gifb, mng=g["m_norm_g"][0].astype(f32), ang=g["a_norm_g"][0].astype(f32), gq=gq, gk=gk,
        gqfm=np.tile(g["a_qnorm_g"][0], 2).reshape(128, 1).astype(f32), gkfm=np.tile(g["a_knorm_g"][0], 2).reshape(128, 1).astype(f32),
        alam=g["a_lambda"][0].reshape(256).astype(f32), biasg=biasg.reshape(128, -1), maskn=maskn.reshape(128, -1), farb=farb,
        identb=np.eye(128).astype(ml_dtypes.bfloat16), identf=np.eye(128, dtype=f32), umask=umask, negm4=negm4,
    )
    in_maps = []
    for b in range(B):
        cfm = _fm(g["c"][b], 8)
        for hf in range(2):
            if hf == 0:
                xl = np.concatenate([np.zeros((half_tok, D), f32), x[b, 0:half_tok]], axis=0)
                fl = np.zeros((128, 1), f32)
            else:
                xl = x[b]
                fl = np.ones((128, 1), f32)
            m = dict(common)
            m["x"] = np.ascontiguousarray(xl)
            m["cfm"] = cfm
            m["flag"] = fl
            in_maps.append(m)
    return in_maps


_NC_CACHE = {}


def run(NCTX, NFULL, inputs):
    key = (NCTX, NFULL)
    if key not in _NC_CACHE:
        _NC_CACHE[key] = build_program(NCTX, NFULL)
    nc = _NC_CACHE[key]
    in_maps = prepare_inputs(NCTX, NFULL, inputs)
    res = run_bass_kernel_spmd(nc, in_maps, core_ids=list(range(len(in_maps))))
    B = len(in_maps) // 2
    half_tok = ((NCTX + NFULL) // 2) * 512
    out = np.empty((B, 2 * half_tok, D), np.float32)
    for b in range(B):
        for hf in range(2):
            out[b, hf * half_tok:(hf + 1) * half_tok] = res.results[b * 2 + hf]["out"]
    return out


def kernel(**inputs):
    return run(7, 9, inputs)
```

```python
import math
from contextlib import ExitStack

import numpy as np

import concourse.bass as bass
import concourse.mybir as mybir
from concourse.bass_utils import run_bass_kernel_spmd

F32 = mybir.dt.float32
BF16 = mybir.dt.bfloat16
AF = mybir.ActivationFunctionType
ALU = mybir.AluOpType
AX = mybir.AxisListType

D = 1024
DC = 8
DFF = 2816
NF = 22
EPS = 1e-6
LAM_INIT = 0.8 - 0.6 * math.exp(-0.3 * 0)
NEG = -30000.0
OPT_SELF_WAR = True
OPT_XN_ACT = True
OPT_SILU = True
OPT_GFOLD = True
OPT_TINY = True
OPT_BATCH = True
OPT_PARTMAIN = True

PIECES = {}
_off = 0
for _nm, _n in (("mq", 4096), ("mk", 4096), ("mv", 4096), ("mo", 4096), ("aq", 4096), ("ak", 4096),
                ("av", 4096), ("gm0", 4096), ("gm1", 4096), ("ga0", 4096), ("ga1", 4096), ("gif", 64),
                ("bm", 4096), ("ba", 4096), ("out0", 4096), ("out1", 4096)):
    PIECES[_nm] = (_off, _n)
    _off += _n
for _j in range(11):
    PIECES["up%d" % _j] = (_off, 4096)
    _off += 4096
PIECES["down"] = (_off, NF * 1024)
_off += NF * 1024
WTOT = _off
WPAD = ((WTOT + 4095) // 4096) * 4096


class Buf:
    __slots__ = ("name", "w", "rs", "sem", "semv", "grp", "psum", "persist")

    def __init__(self, name, grp=None):
        self.psum = False
        self.persist = False
        self.name = name
        self.w = None
        self.rs = []
        self.sem = None
        self.semv = 0
        self.grp = grp


class Sched:
    def __init__(self, nc, stack):
        self.nc = nc
        self.stack = stack
        self.engs = {}
        for nm, h in (("pe", nc.tensor), ("act", nc.scalar), ("dve", nc.vector), ("pool", nc.gpsimd), ("sp", nc.sync)):
            sem = stack.enter_context(nc.semaphore("s_" + nm))
            self.engs[nm] = dict(h=h, sem=sem, cnt=0, seen={})
        self.groups = {}
        self.dma_ev = {}
        self.free_sems = {}
        self.live = []
        self.nsem = 0

    def buf(self, name, grp=None):
        return Buf(name, grp)

    def _need(self, e, deps):
        E = self.engs[e]
        best = {}
        for d in deps:
            if d is None:
                continue
            sem, val, en = d
            if en == e and e == "pe":
                continue
            k = id(sem)
            if k not in best or best[k][1] < val:
                best[k] = (sem, val)
        for k, (sem, val) in best.items():
            if E["seen"].get(k, 0) >= val:
                continue
            E["h"].wait_ge(sem, val)
            E["seen"][k] = val

    def op(self, e, fn, reads=(), writes=()):
        E = self.engs[e]
        deps = []
        for b in reads:
            deps.append(b.w)
            if b.psum:
                for r in b.rs:
                    if r[2] != e:
                        deps.append(r)
        for b in writes:
            if b.w is not None and b.w[2] != e:
                deps.append(b.w)
            for r in b.rs:
                if OPT_SELF_WAR or r[2] != e:
                    deps.append(r)
        self._need(e, deps)
        ins = fn(E["h"])
        E["cnt"] += 1
        ins.then_inc(E["sem"], 1)
        ev = (E["sem"], E["cnt"], e)
        for b in reads:
            b.rs.append(ev)
        for b in writes:
            b.w = ev
            b.rs = []
        return ins

    def dma(self, q, out, in_, reads=(), writes=()):
        E = self.engs[q]
        deps = []
        for b in reads:
            deps.append(b.w)
        for b in writes:
            if b.grp in ("wbw", "kvw", "outw"):
                continue
            deps.append(b.w)
            deps.extend(b.rs)
        self._need(q, deps)
        ins = E["h"].dma_start(out=out, in_=in_)
        cands = list(writes) + list(reads)
        tgt = ([b for b in cands if b.grp is None] + cands)[0]
        if tgt.grp is not None:
            gk = tgt.grp
            if gk not in self.groups:
                self.groups[gk] = [self.stack.enter_context(self.nc.semaphore("g_" + gk)), 0]
            g = self.groups[gk]
            g[1] += 16
            sem, val = g[0], g[1]
        else:
            if tgt.sem is None:
                tgt.sem = {}
                self.live.append(tgt)
            if q not in tgt.sem:
                fs = self.free_sems.setdefault(q, [])
                if not fs:
                    self.nsem += 1
                    fs.append([self.stack.enter_context(self.nc.semaphore("dp%d" % self.nsem)), 0])
                tgt.sem[q] = fs.pop()
            ent = tgt.sem[q]
            ent[1] += 16
            sem, val = ent[0], ent[1]
        ins.then_inc(sem, 16)
        self.dma_ev[id(sem)] = (sem, val)
        ev = (sem, val, "dma")
        for b in reads:
            b.rs.append(ev)
        for b in writes:
            b.w = ev
            b.rs = []
        return ins

    def barrier(self, engines=("pe", "act", "dve", "pool", "sp")):
        deps = [(E["sem"], E["cnt"], "x") for E in self.engs.values() if E["cnt"] > 0]
        deps += [(s, v, "dma") for (s, v) in self.dma_ev.values()]
        for e in engines:
            self._need(e, deps)
        self.dma_ev = {}
        keep = []
        for b in self.live:
            if b.persist:
                keep.append(b)
                continue
            for qq, ent in b.sem.items():
                self.free_sems[qq].append(ent)
            b.sem = None
        self.live = keep

    def seal(self, bufs, grp):
        g = self.groups[grp]
        for b in bufs:
            b.w = (g[0], g[1], "dma")


def build_program(NCTX, NFULL, debug=False):
    NU = NCTX + NFULL
    NFLAG = NU // 2
    NBLK = NU * 4
    NTOK = NU * 512
    NOWN = (NFULL - 1) * 512
    assert NU == 2 * (NFULL - 1)

    nc = bass.Bass("TRN2", target_bir_lowering=False)

    def din(name, shape, dt=F32):
        return nc.dram_tensor(name, list(shape), dt, kind="ExternalInput").ap()

    x_d = din("x", [NTOK, D])
    cfm_d = din("cfm", [128, 8])
    wada_d = din("wada", [12, 128, 8 * 512])
    bada_d = din("bada", [6144])
    badafm_d = din("badafm", [128, 48])
    g1fm_d = din("g1fm", [128, 8])
    g2fm_d = din("g2fm", [128, 8])
    wall_d = din("wall", [128, WPAD])
    mcw_d = din("mcw", [128, 8 * 4])
    mcb_d = din("mcb", [128, 8])
    fcw_d = din("fcw", [128, 44 * 3])
    fcb_d = din("fcb", [128, 44])
    gifb_d = din("gifb", [8])
    mng_d = din("mng", [512])
    ang_d = din("ang", [512])
    gq_d = din("gq", [512])
    gk_d = din("gk", [512])
    gqfm_d = din("gqfm", [128, 1])
    gkfm_d = din("gkfm", [128, 1])
    alam_d = din("alam", [256])
    biasg_d = din("biasg", [128, 4 * 2 * 2 * 128])
    maskn_d = din("maskn", [128, 4 * 2 * 2 * 128])
    farb_d = din("farb", [4])
    flag_d = din("flag", [128, 1])
    identb_d = din("identb", [128, 128], BF16)
    identf_d = din("identf", [128, 128])
    umask_d = din("umask", [128, 128])
    negm4_d = din("negm4", [128, 512])
    out_d = nc.dram_tensor("out", [NOWN, D], F32, kind="ExternalOutput").ap()
    wb_d = nc.dram_tensor("wb_scr", [128, WPAD], BF16, kind="Internal").ap()
    kt_d = nc.dram_tensor("kt_scr", [4, 128, NBLK * 128], BF16, kind="Internal").ap()
    va_d = nc.dram_tensor("va_scr", [4, 128, NBLK * 130], BF16, kind="Internal").ap()

    with ExitStack() as top:
        S = Sched(nc, top)

        uid = [0]

        def sbt(st, name, shape, dt=F32):
            uid[0] += 1
            return st.enter_context(nc.sbuf_tensor("s%d_%s" % (uid[0], name), list(shape), dt))

        banks = [top.enter_context(nc.psum_tensor("bank%d" % i, [128, 512], F32)) for i in range(8)]
        bbufs = [S.buf("bank%d" % i) for i in range(8)]
        for b_ in bbufs:
            b_.psum = True
        bank_rr = [0]

        def nbank(lo=0, hi=8):
            i = lo + (bank_rr[0] % (hi - lo))
            bank_rr[0] += 1
            return banks[i], bbufs[i]

        def bfview(bank):
            return bank[:].bitcast(BF16)

        identb = sbt(top, "identb", [128, 128], BF16)
        identf = sbt(top, "identf", [128, 128])
        umask = sbt(top, "umask", [128, 128])
        negm4 = sbt(top, "negm4", [128, 512])
        ones_f = sbt(top, "ones_f", [128, 128])
        flag = sbt(top, "flag", [128, 1])
        epst = sbt(top, "epst", [128, 1])
        onet = sbt(top, "onet", [128, 1])
        mult1 = sbt(top, "mult1", [128, 8])
        shift1 = sbt(top, "shift1", [128, 8])
        mult2 = sbt(top, "mult2", [128, 8])
        shift2 = sbt(top, "shift2", [128, 8])
        gate1 = sbt(top, "gate1", [128, D])
        gate2 = sbt(top, "gate2", [128, D])
        mcw = sbt(top, "mcw", [128, 8, 4])
        mcb = sbt(top, "mcb", [128, 8])
        mcwf = sbt(top, "mcwf", [128, 8, 4])
        fcwf = sbt(top, "fcwf", [128, 44, 3])
        mcnb = sbt(top, "mcnb", [128, 8])
        fcw = sbt(top, "fcw", [128, 44, 3])
        fcb = sbt(top, "fcb", [128, 44])
        fcnb = sbt(top, "fcnb", [128, 44])
        gifb = sbt(top, "gifb", [128, 8])
        mng = sbt(top, "mng", [128, 512])
        ang = sbt(top, "ang", [128, 512])
        gq = sbt(top, "gq", [128, 512])
        gk = sbt(top, "gk", [128, 512])
        gqfm = sbt(top, "gqfm", [128, 1])
        gkfm = sbt(top, "gkfm", [128, 1])
        mbc = sbt(top, "mbc", [128, 8, 3])
        fbc = sbt(top, "fbc", [128, 44, 2])
        ctmp = sbt(top, "ctmp", [128, 44])
        b_mbc = S.buf("mbc")
        b_fbc = S.buf("fbc")
        biasb = sbt(top, "biasb", [128, 4, 2, 256], BF16)
        farb = sbt(top, "farb", [128, 4])
        neglam = sbt(top, "neglam", [128, 1])
        Sst = sbt(top, "Sst", [128, 4, 130])
        Sstb = sbt(top, "Sstb", [128, 4, 130], BF16)
        mhalo = sbt(top, "mhalo", [128, 8, 3])
        fhalo = sbt(top, "fhalo", [128, 44, 2])
        CONST = S.buf("const")
        b_S = S.buf("Sst")
        b_Sb = S.buf("Sstb")
        b_mhalo = S.buf("mhalo")
        b_fhalo = S.buf("fhalo")
        b_wb = S.buf("wb_scr", grp="wbw")
        b_ktd = S.buf("kt_scr", grp="kvw")
        b_vad = S.buf("va_scr", grp="kvw")
        b_out = S.buf("outd", grp="outw")

        def ld_const(t, src, grp="cst"):
            b = S.buf("c_" + t.name, grp=grp)
            S.dma("sp", t[:], src, writes=[b])
            return b

        cb = []
        cb.append(ld_const(identb, identb_d[:, :]))
        cb.append(ld_const(identf, identf_d[:, :]))
        cb.append(ld_const(umask, umask_d[:, :]))
        cb.append(ld_const(negm4, negm4_d[:, :]))
        cb.append(ld_const(flag, flag_d[:, :]))
        cb.append(ld_const(mcw, mcw_d.rearrange("p (c k) -> p c k", k=4)))
        cb.append(ld_const(mcb, mcb_d[:, :]))
        cb.append(ld_const(fcw, fcw_d.rearrange("p (c k) -> p c k", k=3)))
        cb.append(ld_const(fcb, fcb_d[:, :]))
        cb.append(ld_const(gifb, gifb_d.partition_broadcast(128)))
        cb.append(ld_const(mng, mng_d.partition_broadcast(128)))
        cb.append(ld_const(ang, ang_d.partition_broadcast(128)))
        cb.append(ld_const(gq, gq_d.partition_broadcast(128)))
        cb.append(ld_const(gk, gk_d.partition_broadcast(128)))
        cb.append(ld_const(farb, farb_d.partition_broadcast(128)))
        cb.append(ld_const(gqfm, gqfm_d[:, :]))
        cb.append(ld_const(gkfm, gkfm_d[:, :]))
        S.seal(cb, "cst")
        S.op("dve", lambda e: e.memset(ones_f[:], 1.0), writes=[CONST])
        S.op("dve", lambda e: e.memset(epst[:], EPS), writes=[CONST])
        S.op("dve", lambda e: e.memset(onet[:], 1.0), writes=[CONST])
        S.op("dve", lambda e: e.memset(Sst[:], 0.0), writes=[b_S])
        S.op("dve", lambda e: e.memset(Sstb[:], 0.0), writes=[b_Sb])
        S.op("dve", lambda e: e.memset(mhalo[:], 0.0), writes=[b_mhalo])
        S.op("dve", lambda e: e.memset(fhalo[:], 0.0), writes=[b_fhalo])
        S.op("dve", lambda e: e.tensor_scalar(out=mcnb[:], in0=mcb[:], scalar1=-1.0, scalar2=None, op0=ALU.mult), reads=cb, writes=[CONST])
        S.op("dve", lambda e: e.tensor_scalar(out=fcnb[:], in0=fcb[:], scalar1=-1.0, scalar2=None, op0=ALU.mult), reads=cb, writes=[CONST])
        S.op("dve", lambda e: e.tensor_scalar(out=gqfm[:], in0=gqfm[:], scalar1=0.125, scalar2=None, op0=ALU.mult), reads=cb, writes=[CONST])
        S.op("dve", lambda e: e.tensor_scalar(out=mcwf[:], in0=mcw[:], scalar1=flag[:, 0:1], scalar2=None, op0=ALU.mult), reads=cb, writes=[CONST])
        S.op("dve", lambda e: e.tensor_scalar(out=fcwf[:], in0=fcw[:], scalar1=flag[:, 0:1], scalar2=None, op0=ALU.mult), reads=cb, writes=[CONST])
        S.op("dve", lambda e: e.tensor_scalar(out=ang[:], in0=ang[:], scalar1=1.0 - LAM_INIT, scalar2=None, op0=ALU.mult), reads=cb, writes=[CONST])

        with ExitStack() as st:
            cfm = sbt(st, "cfm", [128, 8])
            sc = sbt(st, "sc", [128, 8])
            sct = sbt(st, "sct", [128, 8])
            scbc = sbt(st, "scbc", [128, 8, 128])
            badafm = sbt(st, "badafm", [128, 48])
            g1fm = sbt(st, "g1fm", [128, 8])
            g2fm = sbt(st, "g2fm", [128, 8])
            modfm = sbt(st, "modfm", [128, 48])
            alam = sbt(st, "alam", [128, 256])
            lt = sbt(st, "lt", [128, 128])
            ls = sbt(st, "ls", [128, 2])
            biasg = sbt(st, "biasg", [128, 2048])
            maskn = sbt(st, "maskn", [128, 2048])
            wst = [sbt(st, "wst%d" % i, [128, 4096]) for i in range(2)]
            wbo = [sbt(st, "wbo%d" % i, [128, 4096], BF16) for i in range(2)]
            bbt = [sbt(st, "bbt%d" % i, [128, 512]) for i in range(2)]
            b_wst = [S.buf("wst%d" % i) for i in range(2)]
            b_wbo = [S.buf("wbo%d" % i) for i in range(2)]
            b_bbt = [S.buf("bbt%d" % i) for i in range(2)]
            L = S.buf("prel")
            b_l = [ld_const(cfm, cfm_d[:, :], "cst2"), ld_const(badafm, badafm_d[:, :], "cst2"), ld_const(g1fm, g1fm_d[:, :], "cst2"),
                   ld_const(g2fm, g2fm_d[:, :], "cst2"), ld_const(alam, alam_d.partition_broadcast(128), "cst2"),
                   ld_const(biasg, biasg_d[:, :], "cst2"), ld_const(maskn, maskn_d[:, :], "cst2")]
            S.seal(b_l, "cst2")
            S.op("act", lambda e: e.activation(out=sct[:], in_=cfm[:], func=AF.Exp, scale=-1.0), reads=b_l, writes=[L])
            S.op("dve", lambda e: e.tensor_scalar(out=sct[:], in0=sct[:], scalar1=1.0, scalar2=None, op0=ALU.add), reads=[L], writes=[L])
            S.op("dve", lambda e: e.reciprocal(out=sct[:], in_=sct[:]), reads=[L], writes=[L])
            S.op("dve", lambda e: e.tensor_tensor(out=sc[:], in0=cfm[:], in1=sct[:], op=ALU.mult), reads=[L], writes=[L])
            S.op("dve", lambda e: e.tensor_copy(out=scbc[:], in_=sc[:].unsqueeze(2).to_broadcast([128, 8, 128])), reads=[L], writes=[L])
            S.op("dve", lambda e: e.tensor_tensor(out=lt[:, 0:64], in0=alam[:, 0:64], in1=alam[:, 64:128], op=ALU.mult), reads=b_l, writes=[L])
            S.op("dve", lambda e: e.tensor_tensor(out=lt[:, 64:128], in0=alam[:, 128:192], in1=alam[:, 192:256], op=ALU.mult), reads=[L], writes=[L])
            S.op("dve", lambda e: e.tensor_reduce(out=ls[:], in_=lt[:].rearrange("p (a b) -> p a b", a=2), axis=AX.X, op=ALU.add), reads=[L], writes=[L])
            S.op("act", lambda e: e.activation(out=ls[:], in_=ls[:], func=AF.Exp), reads=[L], writes=[L])
            S.op("dve", lambda e: e.tensor_tensor(out=neglam[:], in0=ls[:, 1:2], in1=ls[:, 0:1], op=ALU.subtract), reads=[L], writes=[CONST])
            S.op("dve", lambda e: e.tensor_scalar(out=neglam[:], in0=neglam[:], scalar1=-LAM_INIT, scalar2=None, op0=ALU.add), reads=[CONST], writes=[CONST])
            S.op("dve", lambda e: e.tensor_tensor(out=biasb[:].rearrange("p a b c -> p (a b c)"), in0=biasg[:], in1=maskn[:], op=ALU.add), reads=b_l, writes=[CONST])

            fm_ps, fm_b = banks[7], bbufs[7]
            fm_cols = {0: 0, 1: 4, 2: 8, 3: 12, 6: 24, 7: 28, 8: 32, 9: 36}
            for pc in range(12):
                i = pc % 2
                S.dma("sp", wst[i][:], wada_d[pc, :, :], writes=[b_wst[i]])
                wv = wst[i][:].rearrange("p (c n) -> p c n", c=8)
                if pc in (4, 5, 10, 11):
                    S.dma("sp", bbt[i][:], bada_d[pc * 512:(pc + 1) * 512].partition_broadcast(128), writes=[b_bbt[i]])
                    pb, pbb = nbank(0, 6)
                    for dc in range(8):
                        S.op("pe", lambda e: e.matmul(pb[:], lhsT=scbc[:, dc, :], rhs=wv[:, dc, :], start=(dc == 0), stop=(dc == 7)),
                             reads=[L, b_wst[i]], writes=[pbb])
                    gt = gate1 if pc < 6 else gate2
                    off = (pc % 2) * 512
                    S.op("dve", lambda e: e.tensor_tensor(out=gt[:, off:off + 512], in0=pb[:], in1=bbt[i][:], op=ALU.add),
                         reads=[pbb, b_bbt[i]], writes=[CONST])
                else:
                    for k in range(4):
                        col = fm_cols[pc] + k
                        for dc in range(8):
                            S.op("pe", lambda e: e.matmul(fm_ps[:, col:col + 1], lhsT=wv[:, dc, k * 128:(k + 1) * 128], rhs=sc[:, dc:dc + 1],
                                                         start=(dc == 0), stop=(dc == 7)), reads=[L, b_wst[i]], writes=[fm_b])
            S.op("dve", lambda e: e.tensor_tensor(out=modfm[:, 0:16], in0=fm_ps[:, 0:16], in1=badafm[:, 0:16], op=ALU.add), reads=[fm_b] + b_l, writes=[L])
            S.op("dve", lambda e: e.tensor_tensor(out=modfm[:, 24:40], in0=fm_ps[:, 24:40], in1=badafm[:, 24:40], op=ALU.add), reads=[fm_b] + b_l, writes=[L])
            S.op("dve", lambda e: e.scalar_tensor_tensor(out=mult1[:], in0=modfm[:, 8:16], scalar=1.0, in1=g1fm[:], op0=ALU.add, op1=ALU.mult), reads=[L], writes=[CONST])
            S.op("dve", lambda e: e.tensor_copy(out=shift1[:], in_=modfm[:, 0:8]), reads=[L], writes=[CONST])
            S.op("dve", lambda e: e.scalar_tensor_tensor(out=mult2[:], in0=modfm[:, 32:40], scalar=1.0, in1=g2fm[:], op0=ALU.add, op1=ALU.mult), reads=[L], writes=[CONST])
            S.op("dve", lambda e: e.tensor_copy(out=shift2[:], in_=modfm[:, 24:32]), reads=[L], writes=[CONST])

            cast_eng = ["dve", "act"]
            for ch in range(WPAD // 4096):
                i = ch % 2
                S.dma("sp", wst[i][:], wall_d[:, ch * 4096:(ch + 1) * 4096], writes=[b_wst[i]])
                ce = cast_eng[ch % 2]
                if ce == "act":
                    S.op("act", lambda e: e.copy(out=wbo[i][:], in_=wst[i][:]), reads=[b_wst[i]], writes=[b_wbo[i]])
                else:
                    S.op(ce, lambda e: e.tensor_copy(out=wbo[i][:], in_=wst[i][:]), reads=[b_wst[i]], writes=[b_wbo[i]])
                S.dma("pool", wb_d[:, ch * 4096:(ch + 1) * 4096], wbo[i][:], reads=[b_wbo[i]], writes=[b_wb])
            S.barrier()

        njunk = sbt(top, "njunk", [128, D], BF16)
        ncache = {}
        pf = [sbt(top, "pf%d" % i, [128, 4096], BF16) for i in range(4)]
        b_pf = [S.buf("pf%d" % i) for i in range(4)]
        for b_ in b_pf:
            b_.persist = True
        prefetched = {}

        def prefetch(slots, pieces):
            for sl, piece in zip(slots, pieces):
                off, n = PIECES[piece]
                S.dma("sp", pf[sl][:, 0:n], wb_d[:, off:off + n], reads=[b_wb], writes=[b_pf[sl]])
                prefetched[piece] = (pf[sl], b_pf[sl])

        def wload(st, name, piece, shape3=None):
            if piece in prefetched:
                return prefetched.pop(piece)
            off, n = PIECES[piece]
            t = sbt(st, name, [128, n], BF16)
            b = S.buf(name)
            S.dma("sp", t[:], wb_d[:, off:off + n], reads=[b_wb], writes=[b])
            return t, b

        for u in range(NU):
            full = u >= NCTX
            flagged = u < NFLAG
            own = u >= NFLAG
            with ExitStack() as su:
                xs = sbt(su, "xs", [128, 4, D])
                hT = sbt(su, "hT", [128, 8, 512], BF16)
                sa = su.enter_context(ExitStack())
                mqT = sbt(sa, "mqT", [128, 8, 512], BF16)
                mVA = sbt(sa, "mVA", [128, 4, 4, 130], BF16)
                sigmo = sbt(sa, "sigmo", [128, 4, 512])
                gif = sbt(sa, "gif", [128, 4, 8])
                Qbd = sbt(sa, "Qbd", [128, 4, 4, 256], BF16)
                hmT = sbt(sa, "hmT", [128, 4, 512], BF16)
                haT = sbt(sa, "haT", [128, 4, 512], BF16)
                b_xs = [S.buf("xs%d" % j) for j in range(4)]
                b_hT = S.buf("hT")
                b_hT2 = S.buf("hT2")
                b_mqT = S.buf("mqT")
                b_mVA = S.buf("mVA")
                b_sig = S.buf("sigmo")
                b_gif = S.buf("gif")
                b_Qbd = S.buf("Qbd")
                b_hmT = S.buf("hmT")
                b_haT = S.buf("haT")
                if full:
                    S.op("dve", lambda e: e.memset(Qbd[:], 0.0), writes=[b_Qbd])
                    if u == NCTX:
                        S.op("dve", lambda e: e.memset(hmT[:, :, 0:384], 0.0), writes=[b_hmT])

                def norm_block(st, j, mult, shift, hdst, b_hdst, tagp):
                    junk = njunk
                    if not hasattr(st, "ncache"):
                        st.ncache = {}
                    ck = (tagp, j % 2)
                    if ck not in st.ncache:
                        st.ncache[ck] = (sbt(st, tagp + "xn%d" % (j % 2), [128, D], BF16), S.buf("xn"))
                    xn, b_xn = st.ncache[ck]
                    ss = sbt(st, tagp + "ss%d" % j, [128, 2])
                    bl = S.buf("nb")
                    S.op("act", lambda e: e.activation(out=xn[:], in_=xs[:, j, :], func=AF.Square, scale=1.0 / 32.0, accum_out=ss[:, 0:1]),
                         reads=[b_xs[j]], writes=[bl, b_xn])
                    yield
                    S.op("act", lambda e: e.activation(out=ss[:, 1:2], in_=ss[:, 0:1], func=AF.Ln, bias=epst[:, 0:1], scale=1.0), reads=[bl, CONST], writes=[bl])
                    S.op("act", lambda e: e.activation(out=ss[:, 1:2], in_=ss[:, 1:2], func=AF.Exp, scale=-0.5), reads=[bl], writes=[bl])
                    yield
                    S.op("dve", lambda e: e.tensor_scalar(out=xn[:], in0=xs[:, j, :], scalar1=ss[:, 1:2], scalar2=None, op0=ALU.mult),
                         reads=[bl, b_xs[j]], writes=[b_xn])
                    yield
                    pvs = []
                    for half in range(2):
                        pb, pbb = nbank()
                        pv = bfview(pb)[:, 0:512].rearrange("p (c t) -> p c t", c=4)
                        pvs.append((pv, pbb))
                        for cc in range(4):
                            c = half * 4 + cc
                            S.op("pe", lambda e: e.transpose(out=pv[:, cc, :], in_=xn[:, c * 128:(c + 1) * 128], identity=identb[:]),
                                 reads=[b_xn, cb[0]], writes=[pbb])
                    yield
                    for half in range(2):
                        pv, pbb = pvs[half]
                        for cc in range(4):
                            c = half * 4 + cc
                            if half == 0:
                                S.op("act", lambda e: e.activation(out=hdst[:, c, j * 128:(j + 1) * 128], in_=pv[:, cc, :], func=AF.Identity,
                                                                  scale=mult[:, c:c + 1], bias=shift[:, c:c + 1]), reads=[pbb, CONST], writes=[b_hdst[0]])
                            else:
                                S.op("dve", lambda e: e.tensor_scalar(out=hdst[:, c, j * 128:(j + 1) * 128], in0=pv[:, cc, :], scalar1=mult[:, c:c + 1],
                                                                     scalar2=shift[:, c:c + 1], op0=ALU.mult, op1=ALU.add), reads=[pbb, CONST], writes=[b_hdst[1]])
                    yield

                def run_norms(st, mult, shift, hdst, b_hdst, tagp):
                    for pair_ in ((0, 1), (2, 3)):
                        alive_ = [norm_block(st, j_, mult, shift, hdst, b_hdst, tagp) for j_ in pair_]
                        while alive_:
                            nxt_ = []
                            for g_ in alive_:
                                try:
                                    next(g_)
                                    nxt_.append(g_)
                                except StopIteration:
                                    pass
                            alive_ = nxt_

                with ExitStack() as st:
                    names = (["mq", "mk", "mv", "gif", "av", "mo", "aq", "ak"] if full else ["mk", "mv", "gif", "av", "ak"])
                    W = {}
                    for j in range(4):
                        blk = u * 4 + j
                        S.dma("sp", xs[:, j, :], x_d[blk * 128:(blk + 1) * 128, :], writes=[b_xs[j]])
                    for nm in names:
                        W[nm] = wload(st, "w_" + nm, nm)
                    if not full and u + 1 < NU:
                        nxt_full = (u + 1) >= NCTX
                        slots = (2, 3) if (nxt_full or (u + 1) % 2 == 1) else (0, 1)
                        if nxt_full:
                            slots = (2, 3)
                        prefetch(slots, ["mq", "mk"] if nxt_full else ["mk", "mv"])
                    run_norms(st, mult1, shift1, hT, (b_hT, b_hT2), "n1")
                    if flagged:
                        S.op("dve", lambda e: e.tensor_copy(out=mVA[:, :, :, 128:130], in_=flag[:, 0:1].unsqueeze(1).unsqueeze(1).to_broadcast([128, 4, 4, 2])),
                             reads=[cb[4]], writes=[b_mVA])
                    else:
                        S.op("dve", lambda e: e.memset(mVA[:, :, :, 128:130], 1.0), writes=[b_mVA])
                    acc = [sbt(st, "acc%d" % i, [128, 512]) for i in range(4)]
                    et = [sbt(st, "et%d" % i, [128, 512]) for i in range(4)]
                    b_acc = [S.buf("acc%d" % i) for i in range(4)]
                    b_et = [S.buf("et%d" % i) for i in range(4)]
                    hv = mhalo
                    if not OPT_BATCH:
                        S.op("dve", lambda e: e.memset(mbc[:], 0.0), writes=[b_mbc])
                    if OPT_BATCH:
                        S.op("dve", lambda e: e.tensor_tensor(out=mbc[:, :, 2], in0=hv[:, :, 2], in1=mcw[:, :, 0], op=ALU.mult), reads=[b_mhalo] + cb, writes=[b_mbc])
                        S.op("dve", lambda e: e.tensor_tensor(out=mbc[:, :, 1], in0=hv[:, :, 2], in1=mcw[:, :, 1], op=ALU.mult), reads=[b_mhalo] + cb, writes=[b_mbc])
                        S.op("dve", lambda e: e.tensor_tensor(out=mbc[:, :, 0], in0=hv[:, :, 2], in1=mcw[:, :, 2], op=ALU.mult), reads=[b_mhalo] + cb, writes=[b_mbc])
                        S.op("dve", lambda e: e.tensor_tensor(out=ctmp[:, 0:8], in0=hv[:, :, 1], in1=mcw[:, :, 0], op=ALU.mult), reads=[b_mhalo] + cb, writes=[b_mbc])
                        S.op("dve", lambda e: e.tensor_tensor(out=mbc[:, :, 1], in0=mbc[:, :, 1], in1=ctmp[:, 0:8], op=ALU.add), reads=[b_mbc], writes=[b_mbc])
                        S.op("dve", lambda e: e.tensor_tensor(out=ctmp[:, 8:16], in0=hv[:, :, 1], in1=mcw[:, :, 1], op=ALU.mult), reads=[b_mhalo] + cb, writes=[b_mbc])
                        S.op("dve", lambda e: e.tensor_tensor(out=mbc[:, :, 0], in0=mbc[:, :, 0], in1=ctmp[:, 8:16], op=ALU.add), reads=[b_mbc], writes=[b_mbc])
                        S.op("dve", lambda e: e.tensor_tensor(out=ctmp[:, 16:24], in0=hv[:, :, 0], in1=mcw[:, :, 0], op=ALU.mult), reads=[b_mhalo] + cb, writes=[b_mbc])
                        S.op("dve", lambda e: e.tensor_tensor(out=mbc[:, :, 0], in0=mbc[:, :, 0], in1=ctmp[:, 16:24], op=ALU.add), reads=[b_mbc], writes=[b_mbc])
                        S.op("dve", lambda e: e.tensor_tensor(out=mbc[:], in0=mbc[:], in1=mcb[:].unsqueeze(2).to_broadcast([128, 8, 3]), op=ALU.add), reads=[b_mbc] + cb, writes=[b_mbc])
                    groups = ([[0, 1, 2, 3], [4, 5, 6, 7]] if full else [[4, 5, 6, 7]])
                    wsel = mcwf if flagged else mcw
                    for grp in groups:
                        pbs = []
                        for k, c in enumerate(grp):
                            wt, wbf = W["mq" if c < 4 else "mk"]
                            wv = wt[:].rearrange("p (c n) -> p c n", c=8)
                            pb, pbb = nbank()
                            pbs.append((pb, pbb))
                            for dc in range(8):
                                S.op("pe", lambda e: e.matmul(pb[:], lhsT=wv[:, dc, k * 128:(k + 1) * 128], rhs=hT[:, dc, :], start=(dc == 0), stop=(dc == 7)),
                                     reads=[wbf, b_hT, b_hT2], writes=[pbb])
                            m0 = 3 if OPT_PARTMAIN else 0
                            S.op("act", lambda e: e.activation(out=acc[k][:, m0:512], in_=pb[:, m0:512], func=AF.Identity, scale=wsel[:, c, 3:4], bias=mcb[:, c:c + 1]),
                                 reads=[pbb, CONST] + cb, writes=[b_acc[k]])
                            for col in (range(3) if OPT_TINY else ()):
                                S.op("act", lambda e: e.activation(out=acc[k][:, col:col + 1], in_=pb[:, col:col + 1], func=AF.Identity, scale=wsel[:, c, 3:4],
                                                                  bias=mbc[:, c, col:col + 1]), reads=[pbb, CONST, b_mbc] + cb, writes=[b_acc[k]])
                            if flagged:
                                S.op("act", lambda e: e.activation(out=mhalo[:, c, :], in_=pb[:, 509:512], func=AF.Copy, scale=flag[:, 0:1]), reads=[pbb, cb[4], b_mbc], writes=[b_mhalo])
                            else:
                                S.op("act", lambda e: e.copy(out=mhalo[:, c, :], in_=pb[:, 509:512]), reads=[pbb, b_mbc], writes=[b_mhalo])
                        for tp in (2, 1, 0):
                            sh = 3 - tp
                            for k, c in enumerate(grp):
                                pb, pbb = pbs[k]
                                S.op("dve", lambda e: e.scalar_tensor_tensor(out=acc[k][:, sh:512], in0=pb[:, 0:512 - sh], scalar=wsel[:, c, tp:tp + 1], in1=acc[k][:, sh:512],
                                                                            op0=ALU.mult, op1=ALU.add), reads=[pbb, b_acc[k], CONST], writes=[b_acc[k]])
                        for k, c in enumerate(grp):
                            if not OPT_SILU:
                                S.op("act", lambda e: e.activation(out=et[k][:], in_=acc[k][:], func=AF.Exp, scale=-1.0), reads=[b_acc[k]], writes=[b_et[k]])
                                sk = (128.0 ** 0.5) if c < 4 else 1.0
                                S.op("dve", lambda e: e.tensor_scalar(out=et[k][:], in0=et[k][:], scalar1=1.0, scalar2=sk, op0=ALU.add, op1=ALU.mult), reads=[b_et[k]], writes=[b_et[k]])
                                S.op("dve", lambda e: e.reciprocal(out=et[k][:], in_=et[k][:]), reads=[b_et[k]], writes=[b_et[k]])
                                S.op("dve", lambda e: e.tensor_tensor(out=mqT[:, c, :], in0=acc[k][:], in1=et[k][:], op=ALU.mult), reads=[b_acc[k], b_et[k]], writes=[b_mqT])
                            elif c < 4:
                                S.op("act", lambda e: e.activation(out=et[k][:], in_=acc[k][:], func=AF.Silu), reads=[b_acc[k]], writes=[b_et[k]])
                                S.op("dve", lambda e: e.tensor_scalar(out=mqT[:, c, :], in0=et[k][:], scalar1=128.0 ** -0.5, scalar2=None, op0=ALU.mult),
                                     reads=[b_et[k]], writes=[b_mqT])
                            else:
                                S.op("act", lambda e: e.activation(out=mqT[:, c, :], in_=acc[k][:], func=AF.Silu), reads=[b_acc[k]], writes=[b_mqT])
                    sq = [sbt(st, "sq%d" % i, [128, 512]) for i in range(4)]
                    qb = [sbt(st, "qb%d" % i, [128, 512], BF16) for i in range(4)]
                    rs = [sbt(st, "rs%d" % i, [128, 16]) for i in range(4)]
                    KTb = [sbt(st, "KTb%d" % i, [128, 4, 128], BF16) for i in range(4)]
                    VAb = [sbt(st, "VAb%d" % i, [128, 4, 130], BF16) for i in range(4)]
                    b_t = [S.buf("tmj%d" % i) for i in range(4)]
                    b_KTb = [S.buf("KTb%d" % i) for i in range(4)]
                    b_VAb = [S.buf("VAb%d" % i) for i in range(4)]
                    for i in range(4):
                        if flagged:
                            S.op("dve", lambda e: e.tensor_copy(out=VAb[i][:, :, 128:130], in_=flag[:, 0:1].unsqueeze(1).to_broadcast([128, 4, 2])),
                                 reads=[cb[4]], writes=[b_VAb[i]])
                        else:
                            S.op("dve", lambda e: e.memset(VAb[i][:, :, 128:130], 1.0), writes=[b_VAb[i]])
                    def tm_block(j):
                        blk = u * 4 + j
                        i = j
                        tsl = slice(j * 128, (j + 1) * 128)
                        bsel = [0]

                        def bk():
                            bi = 2 * j + (bsel[0] % 2)
                            bsel[0] += 1
                            return banks[bi], bbufs[bi]

                        def proj(nm):
                            wt, wbf = W[nm]
                            n = PIECES[nm][1] // 8
                            wv = wt[:].rearrange("p (c n) -> p c n", c=8)
                            pb, pbb = bk()
                            for dc in range(8):
                                S.op("pe", lambda e: e.matmul(pb[:, 0:n], lhsT=hT[:, dc, tsl], rhs=wv[:, dc, :], start=(dc == 0), stop=(dc == 7)),
                                     reads=[wbf, b_hT, b_hT2], writes=[pbb])
                            return pb, pbb

                        def evac_scaled(out_ap, in_ap, rd, wr):
                            if flagged:
                                S.op("act", lambda e: e.activation(out=out_ap, in_=in_ap, func=AF.Copy, scale=flag[:, 0:1]), reads=rd + [cb[4]], writes=wr)
                            else:
                                S.op("act", lambda e: e.copy(out=out_ap, in_=in_ap), reads=rd, writes=wr)

                        pb, pbb = proj("mv")
                        yield
                        evac_scaled(mVA[:, j, :, 0:128], pb[:].rearrange("p (h d) -> p h d", h=4), [pbb], [b_mVA])
                        pb, pbb = proj("gif")
                        yield
                        S.op("dve", lambda e: e.tensor_tensor(out=gif[:, j, :], in0=pb[:, 0:8], in1=gifb[:], op=ALU.add), reads=[pbb, cb[9]], writes=[b_gif])
                        pb, pbb = proj("av")
                        yield
                        evac_scaled(VAb[i][:, :, 0:128], pb[:].rearrange("p (h d) -> p h d", h=4), [pbb], [b_VAb[i]])
                        S.dma("pool", va_d.rearrange("h p (b c) -> p h b c", c=130)[:, :, blk, :], VAb[i][:], reads=[b_VAb[i]], writes=[b_vad])
                        if full:
                            pb, pbb = proj("mo")
                            yield
                            S.op("act", lambda e: e.activation(out=sigmo[:, j, :], in_=pb[:], func=AF.Exp, scale=-1.0), reads=[pbb], writes=[b_sigj[j]])
                            yield
                            S.op("dve", lambda e: e.tensor_scalar(out=sigmo[:, j, :], in0=sigmo[:, j, :], scalar1=1.0, scalar2=None, op0=ALU.add), reads=[b_sigj[j]], writes=[b_sigj[j]])
                            S.op("dve", lambda e: e.reciprocal(out=sigmo[:, j, :], in_=sigmo[:, j, :]), reads=[b_sigj[j]], writes=[b_sigj[j], b_sig])
                        for nm in (["aq", "ak"] if full else ["ak"]):
                            pb, pbb = proj(nm)
                            yield
                            S.op("act", lambda e: e.activation(out=sq[i][:], in_=pb[:], func=AF.Square, scale=0.125), reads=[pbb], writes=[b_t[i]])
                            yield
                            S.op("dve", lambda e: e.tensor_reduce(out=rs[i][:, 0:8], in_=sq[i][:].rearrange("p (g d) -> p g d", d=64), axis=AX.X, op=ALU.add),
                                 reads=[b_t[i]], writes=[b_t[i]])
                            yield
                            S.op("act", lambda e: e.activation(out=rs[i][:, 8:16], in_=rs[i][:, 0:8], func=AF.Ln, bias=epst[:, 0:1], scale=1.0), reads=[b_t[i], CONST], writes=[b_t[i]])
                            S.op("act", lambda e: e.activation(out=rs[i][:, 8:16], in_=rs[i][:, 8:16], func=AF.Exp, scale=-0.5), reads=[b_t[i]], writes=[b_t[i]])
                            yield
                            S.op("dve", lambda e: e.tensor_tensor(out=qb[i][:].rearrange("p (g d) -> p g d", d=64), in0=pb[:].rearrange("p (g d) -> p g d", d=64),
                                                                 in1=rs[i][:, 8:16].unsqueeze(2).to_broadcast([128, 8, 64]), op=ALU.mult), reads=[pbb, b_t[i]], writes=[b_t[i]])
                            yield
                            tb, tbb = bk()
                            tv = bfview(tb)[:, 0:512].rearrange("p (h t) -> p h t", h=4)
                            for h in range(4):
                                S.op("pe", lambda e: e.transpose(out=tv[:, h, :], in_=qb[i][:, h * 128:(h + 1) * 128], identity=identb[:]), reads=[b_t[i], cb[0]], writes=[tbb])
                            yield
                            if nm == "aq":
                                S.op("act", lambda e: e.activation(out=Qbd[0:64, :, j, 0:128], in_=tv[0:64, :, :], func=AF.Copy, scale=gqfm[0:64, 0:1]), reads=[tbb, CONST] + cb, writes=[b_Qbd])
                                S.op("act", lambda e: e.activation(out=Qbd[64:128, :, j, 128:256], in_=tv[64:128, :, :], func=AF.Copy, scale=gqfm[64:128, 0:1]), reads=[tbb, CONST] + cb, writes=[b_Qbd])
                            else:
                                S.op("act", lambda e: e.activation(out=KTb[i][:], in_=tv, func=AF.Copy, scale=gkfm[:, 0:1]), reads=[tbb] + cb, writes=[b_KTb[i]])
                                S.dma("pool", kt_d.rearrange("h p k -> p h k")[:, :, blk * 128:(blk + 1) * 128], KTb[i][:], reads=[b_KTb[i]], writes=[b_ktd])
                            yield

                    b_sigj = [S.buf("sigj%d" % j_) for j_ in range(4)]
                    tgens = [tm_block(j_) for j_ in range(4)]
                    alive = list(tgens)
                    while alive:
                        nxt = []
                        for g_ in alive:
                            try:
                                next(g_)
                                nxt.append(g_)
                            except StopIteration:
                                pass
                        alive = nxt
                    S.barrier()

                def mlstm_block(j, st, bankfn, cache):
                    want_out = full and (u != NCTX or j == 3)
                    def sbt_c(st_, name, shape, dt=F32):
                        if name not in cache:
                            cache[name] = sbt(st_, name, shape, dt)
                        return cache[name]

                    def buf_c(name):
                        k_ = "B_" + name
                        if k_ not in cache:
                            cache[k_] = S.buf(name)
                        return cache[k_]
                    tsl = slice(j * 128, (j + 1) * 128)
                    lf = sbt_c(st, "lf%d" % (j % 2), [128, 4])
                    e1 = sbt_c(st, "e1%d" % (j % 2), [128, 4])
                    e2 = sbt_c(st, "e2%d" % (j % 2), [128, 4])
                    wsx = sbt_c(st, "wsx%d" % (j % 2), [128, 4])
                    RL = sbt_c(st, "RL%d" % (j % 2), [128, 4, 128])
                    Et = sbt_c(st, "Et%d" % (j % 2), [128, 512])
                    Kw = sbt_c(st, "Kw%d" % (j % 2), [128, 4, 128], BF16)
                    bm = buf_c("ml%d" % (j % 2))
                    b_e1 = buf_c("e1%d" % (j % 2))
                    b_ws = buf_c("wsx%d" % (j % 2))
                    b_RL = buf_c("RL%d" % (j % 2))
                    b_Et = buf_c("Et%d" % (j % 2))
                    b_Kw = buf_c("Kw%d" % (j % 2))
                    S.op("act", lambda e: e.activation(out=lf[:], in_=gif[:, j, 4:8], func=AF.Exp, scale=-1.0), reads=[b_gif], writes=[bm])
                    yield
                    S.op("act", lambda e: e.activation(out=lf[:], in_=lf[:], func=AF.Ln, bias=onet[:, 0:1], scale=1.0), reads=[bm, CONST], writes=[bm])
                    yield
                    S.op("dve", lambda e: e.tensor_scalar(out=lf[:], in0=lf[:], scalar1=-1.0, scalar2=None, op0=ALU.mult), reads=[bm], writes=[bm])
                    yield
                    p1, p1b = bankfn()
                    S.op("pe", lambda e: e.matmul(p1[:, 0:4], lhsT=umask[:], rhs=lf[:], start=True, stop=True), reads=[bm, cb[2]], writes=[p1b])
                    S.op("dve", lambda e: e.tensor_tensor(out=RL[:], in0=umask[:].unsqueeze(1).to_broadcast([128, 4, 128]),
                                                         in1=lf[:].unsqueeze(2).to_broadcast([128, 4, 128]), op=ALU.mult), reads=[bm, cb[2]], writes=[b_RL])
                    yield
                    S.op("dve", lambda e: e.tensor_tensor(out=e1[:], in0=gif[:, j, 0:4], in1=p1[:, 0:4], op=ALU.subtract), reads=[b_gif, p1b], writes=[b_e1])
                    yield
                    RLf = RL[:].rearrange("p h t -> p (h t)")
                    pBc, pBcb = bankfn()
                    S.op("pe", lambda e: e.matmul(pBc[:], lhsT=ones_f[:], rhs=RLf, start=True, stop=True), reads=[b_RL, CONST], writes=[pBcb])
                    yield
                    blast = pBc[:].rearrange("p (h t) -> p h t", h=4)[:, :, 127]
                    S.op("dve", lambda e: e.tensor_tensor(out=e2[:], in0=e1[:], in1=blast, op=ALU.add), reads=[b_e1, pBcb], writes=[b_ws])
                    yield
                    S.op("act", lambda e: e.activation(out=Et[:], in_=pBc[:], func=AF.Exp), reads=[pBcb], writes=[b_Et])
                    S.op("act", lambda e: e.activation(out=wsx[:], in_=e2[:], func=AF.Exp), reads=[b_ws], writes=[b_ws])
                    yield
                    tb, tbb = bankfn()
                    tv = bfview(tb)[:, 0:512].rearrange("p (h t) -> p h t", h=4)
                    for h in range(4):
                        S.op("pe", lambda e: e.transpose(out=tv[:, h, :], in_=mqT[:, 4 + h, tsl], identity=identb[:]), reads=[b_mqT, cb[0]], writes=[tbb])
                    yield
                    S.op("dve", lambda e: e.tensor_tensor(out=Kw[:], in0=tv, in1=wsx[:].unsqueeze(2).to_broadcast([128, 4, 128]), op=ALU.mult),
                         reads=[tbb, b_ws], writes=[b_Kw])
                    yield
                    if want_out:
                        DT = sbt_c(st, "DT%d" % (j % 2), [128, 4, 128])
                        PT = sbt_c(st, "PT%d" % (j % 2), [128, 4, 128], BF16)
                        qpT = sbt_c(st, "qpT%d" % (j % 2), [128, 4, 128], BF16)
                        numS = sbt_c(st, "numS%d" % (j % 2), [128, 4, 130])
                        rden = sbt_c(st, "rden%d" % (j % 2), [128, 4])
                        hmr = sbt_c(st, "hmr%d" % (j % 2), [128, 4, 128])
                        hsq = sbt_c(st, "hsq%d" % (j % 2), [128, 4, 128])
                        hss = sbt_c(st, "hss%d" % (j % 2), [128, 8])
                        gs = sbt_c(st, "gs%d" % (j % 2), [128, 512])
                        hmf = sbt_c(st, "hmf%d" % (j % 2), [128, 4, 128], BF16)
                        b_o = buf_c("mo%d" % (j % 2))
                        b_gs = buf_c("gs%d" % (j % 2))
                        b_num = buf_c("numS%d" % (j % 2))
                        b_DT = buf_c("DT%d" % (j % 2))
                        b_PT = buf_c("PT%d" % (j % 2))
                        b_qp = buf_c("qpT%d" % (j % 2))
                        pBm, pBmb = bankfn()
                        S.op("pe", lambda e: e.matmul(pBm[:], lhsT=ones_f[:], rhs=RLf, start=True, stop=False), reads=[b_RL, CONST], writes=[pBmb])
                        S.op("pe", lambda e: e.matmul(pBm[:], lhsT=identf[:], rhs=negm4[:], start=False, stop=True), reads=[cb[1], cb[3]], writes=[pBmb])
                        S.op("dve", lambda e: e.tensor_tensor(out=qpT[:], in0=mqT[:, 0:4, tsl], in1=Et[:].rearrange("p (h t) -> p h t", h=4), op=ALU.mult),
                             reads=[b_mqT, b_Et], writes=[b_qp])
                        S.op("dve", lambda e: e.tensor_tensor(out=gs[:], in0=mng[:], in1=sigmo[:, j, :], op=ALU.mult), reads=[b_sig] + cb, writes=[b_gs])
                        yield
                        for h in range(4):
                            S.op("act", lambda e: e.activation(out=DT[:, h, :], in_=pBm[:, h * 128:(h + 1) * 128], func=AF.Exp, bias=e1[:, h:h + 1], scale=1.0),
                                 reads=[pBmb, b_e1], writes=[b_DT])
                        yield
                        pA, pAb = bankfn()
                        for h in range(4):
                            S.op("pe", lambda e: e.matmul(pA[:, h * 128:(h + 1) * 128], lhsT=mqT[:, 4 + h, tsl], rhs=mqT[:, h, tsl], start=True, stop=True),
                                 reads=[b_mqT], writes=[pAb])
                        yield
                        S.op("dve", lambda e: e.tensor_tensor(out=PT[:], in0=pA[:].rearrange("p (h t) -> p h t", h=4), in1=DT[:], op=ALU.mult),
                             reads=[pAb, b_DT], writes=[b_PT])
                        yield
                    yield "AB"
                    if want_out:
                        for hp in range(2):
                            pn, pnb = bankfn()
                            for hh in range(2):
                                h = 2 * hp + hh
                                o = hh * 130
                                S.op("pe", lambda e: e.matmul(pn[:, o:o + 130], lhsT=PT[:, h, :], rhs=mVA[:, j, h, :], start=True, stop=False), reads=[b_PT, b_mVA], writes=[pnb])
                                S.op("pe", lambda e: e.matmul(pn[:, o:o + 130], lhsT=qpT[:, h, :], rhs=Sstb[:, h, :], start=False, stop=True), reads=[b_qp, b_Sb], writes=[pnb])
                            yield
                            S.op("dve", lambda e: e.tensor_copy(out=numS[:, 2 * hp:2 * hp + 2, :], in_=pn[:, 0:260].rearrange("p (h c) -> p h c", h=2)), reads=[pnb], writes=[b_num])
                            yield
                    Ev = Et[:].rearrange("p (h t) -> p h t", h=4)
                    for hp in range(2):
                        pc_, pcb = bankfn()
                        for hh in range(2):
                            h = 2 * hp + hh
                            o = hh * 130
                            S.op("pe", lambda e: e.matmul(pc_[:, o:o + 130], lhsT=Kw[:, h, :], rhs=mVA[:, j, h, :], start=True, stop=True), reads=[b_Kw, b_mVA], writes=[pcb])
                        yield
                        for hh in range(2):
                            h = 2 * hp + hh
                            o = hh * 130
                            S.op("dve", lambda e: e.scalar_tensor_tensor(out=Sst[:, h, :], in0=Sst[:, h, :], scalar=Ev[:, h, 127:128], in1=pc_[:, o:o + 130],
                                                                        op0=ALU.mult, op1=ALU.add), reads=[b_S, b_Et, pcb], writes=[b_S])
                        yield
                    S.op("dve", lambda e: e.tensor_copy(out=Sstb[:], in_=Sst[:]), reads=[b_S], writes=[b_Sb])
                    yield "S"
                    if want_out:
                        S.op("act", lambda e: e.activation(out=rden[:], in_=numS[:, :, 128], func=AF.Abs), reads=[b_num], writes=[b_o])
                        yield
                        S.op("dve", lambda e: e.tensor_scalar(out=rden[:], in0=rden[:], scalar1=1.0, scalar2=None, op0=ALU.max), reads=[b_o], writes=[b_o])
                        S.op("dve", lambda e: e.reciprocal(out=rden[:], in_=rden[:]), reads=[b_o], writes=[b_o])
                        S.op("dve", lambda e: e.tensor_tensor(out=hmr[:], in0=numS[:, :, 0:128], in1=rden[:].unsqueeze(2).to_broadcast([128, 4, 128]), op=ALU.mult),
                             reads=[b_num, b_o], writes=[b_o])
                        yield
                        S.op("act", lambda e: e.activation(out=hsq[:], in_=hmr[:], func=AF.Square, scale=1.0 / math.sqrt(128.0)), reads=[b_o], writes=[b_o])
                        yield
                        S.op("dve", lambda e: e.tensor_reduce(out=hss[:, 0:4], in_=hsq[:], axis=AX.X, op=ALU.add), reads=[b_o], writes=[b_o])
                        yield
                        S.op("act", lambda e: e.activation(out=hss[:, 4:8], in_=hss[:, 0:4], func=AF.Ln, bias=epst[:, 0:1], scale=1.0), reads=[b_o, CONST], writes=[b_o])
                        S.op("act", lambda e: e.activation(out=hss[:, 4:8], in_=hss[:, 4:8], func=AF.Exp, scale=-0.5), reads=[b_o], writes=[b_o])
                        yield
                        for h in range(4):
                            S.op("dve", lambda e: e.scalar_tensor_tensor(out=hmf[:, h, :], in0=hmr[:, h, :], scalar=hss[:, 4 + h:5 + h], in1=gs[:, h * 128:(h + 1) * 128],
                                                                        op0=ALU.mult, op1=ALU.mult), reads=[b_o, b_gs], writes=[b_o])
                        yield
                        tb2, tb2b = bankfn()
                        tv2 = bfview(tb2)[:, 0:512].rearrange("p (h t) -> p h t", h=4)
                        for h in range(4):
                            S.op("pe", lambda e: e.transpose(out=tv2[:, h, :], in_=hmf[:, h, :], identity=identb[:]), reads=[b_o, cb[0]], writes=[tb2b])
                        yield
                        S.op("dve", lambda e: e.tensor_copy(out=hmT[:, :, tsl], in_=tv2), reads=[tb2b], writes=[b_hmT])
                        yield

                def mlstm_driver(st, bankfn):
                    cache = {}
                    for j in range(4):
                        for r in mlstm_block(j, st, bankfn, cache):
                            yield

                if not full:
                    with ExitStack() as st:
                        for _ in mlstm_driver(st, nbank):
                            pass
                        S.barrier()

                if full:
                    with ExitStack() as st:
                        NKC = 8
                        kch = [sbt(st, "kch%d" % i, [128, NKC * 128], BF16) for i in range(3)]
                        vch = [sbt(st, "vch%d" % i, [128, NKC, 130], BF16) for i in range(3)]
                        b_kch = [S.buf("kch%d" % i) for i in range(3)]
                        b_vch = [S.buf("vch%d" % i) for i in range(3)]
                        pex = [sbt(st, "pex%d" % i, [128, 512], BF16) for i in range(6)]
                        b_pex = [S.buf("pex%d" % i) for i in range(6)]
                        rr = sbt(st, "rr", [128, 8])
                        har = sbt(st, "har", [128, 4, 128])
                        hsq = sbt(st, "ahsq", [128, 4, 128])
                        hss = sbt(st, "ahss", [128, 8])
                        haf = sbt(st, "haf", [128, 4, 128], BF16)
                        b_a = S.buf("attn_o")
                        nkb_tot = u * 4 + 4
                        chunk_list = [(s0, min(NKC, nkb_tot - s0)) for s0 in range(0, nkb_tot, NKC)]
                        ring = 0
                        LCH = len(chunk_list)
                        prefetch((0, 1), ["bm", "gm0"])
                        mdrv = mlstm_driver(st, lambda: (banks[7], bbufs[7]))
                        n_items_est = 4 * sum((2 if kb_ <= u * 4 - 2 else 0) + sum(1 for p2 in range(2) for jq in (2 * p2, 2 * p2 + 1) if kb_ > u * 4 + 2 * p2 - 2 and kb_ <= u * 4 + jq)
                                              for kb_ in range(nkb_tot))
                        MSTEP = max(1, n_items_est // 150)
                        gcount = [0]
                        loaded = set()

                        def ensure_chunk(g):
                            if g >= 4 * LCH or g in loaded:
                                return
                            loaded.add(g)
                            h_, ci_ = g // LCH, g % LCH
                            s0, nk = chunk_list[ci_]
                            ri = g % 3
                            S.dma("sp", kch[ri][:, 0:nk * 128], kt_d[h_, :, s0 * 128:(s0 + nk) * 128], reads=[b_ktd], writes=[b_kch[ri]])
                            S.dma("sp", vch[ri][:, 0:nk, :], va_d[h_, :, s0 * 130:(s0 + nk) * 130].rearrange("p (b c) -> p b c", c=130), reads=[b_vad], writes=[b_vch[ri]])

                        accS = [sbt(st, "accS%d" % i, [128, 4, 2, 130]) for i in range(2)]
                        b_accS = [S.buf("accS%d" % i) for i in range(2)]
                        rr2 = [sbt(st, "rr2_%d" % i, [128, 8]) for i in range(2)]
                        har2 = [sbt(st, "har2_%d" % i, [128, 4, 128]) for i in range(2)]
                        hsq2 = [sbt(st, "hsq2_%d" % i, [128, 4, 128]) for i in range(2)]
                        hss2 = [sbt(st, "hss2_%d" % i, [128, 8]) for i in range(2)]
                        haf2 = [sbt(st, "haf2_%d" % i, [128, 4, 128], BF16) for i in range(2)]
                        b_ep = [S.buf("epi%d" % i) for i in range(2)]

                        def attn_epi(h, accs):
                            p = h % 2
                            aS, rr, har, hsq, hss, haf, b_a = accS[p], rr2[p], har2[p], hsq2[p], hss2[p], haf2[p], b_ep[p]
                            for jq in range(4):
                                ab, abb = accs[jq]
                                S.op("dve", lambda e: e.tensor_copy(out=aS[:, jq, :, :], in_=ab[:, 0:260].rearrange("p (m c) -> p m c", m=2)), reads=[abb], writes=[b_accS[p]])
                            yield
                            for jq in range(4):
                                av = aS[:, jq, :, :]
                                S.op("dve", lambda e: e.tensor_scalar(out=rr[:, 2 * jq:2 * jq + 2], in0=av[:, :, 128], scalar1=1e-30, scalar2=None, op0=ALU.max), reads=[b_accS[p]], writes=[b_a])
                                S.op("dve", lambda e: e.reciprocal(out=rr[:, 2 * jq:2 * jq + 2], in_=rr[:, 2 * jq:2 * jq + 2]), reads=[b_a], writes=[b_a])
                                S.op("dve", lambda e: e.tensor_tensor(out=rr[:, 2 * jq + 1:2 * jq + 2], in0=rr[:, 2 * jq + 1:2 * jq + 2], in1=neglam[:], op=ALU.mult),
                                     reads=[b_a, CONST], writes=[b_a])
                                S.op("dve", lambda e: e.tensor_scalar(out=har[:, jq, :], in0=av[:, 0, 0:128], scalar1=rr[:, 2 * jq:2 * jq + 1], scalar2=None, op0=ALU.mult),
                                     reads=[b_accS[p], b_a], writes=[b_a])
                                S.op("dve", lambda e: e.scalar_tensor_tensor(out=har[:, jq, :], in0=av[:, 1, 0:128], scalar=rr[:, 2 * jq + 1:2 * jq + 2], in1=har[:, jq, :],
                                                                            op0=ALU.mult, op1=ALU.add), reads=[b_accS[p], b_a], writes=[b_a])
                                yield
                            S.op("act", lambda e: e.activation(out=hsq[:], in_=har[:], func=AF.Square, scale=1.0 / math.sqrt(128.0)), reads=[b_a], writes=[b_a])
                            yield
                            S.op("dve", lambda e: e.tensor_reduce(out=hss[:, 0:4], in_=hsq[:], axis=AX.X, op=ALU.add), reads=[b_a], writes=[b_a])
                            yield
                            S.op("act", lambda e: e.activation(out=hss[:, 4:8], in_=hss[:, 0:4], func=AF.Ln, bias=epst[:, 0:1], scale=1.0), reads=[b_a, CONST], writes=[b_a])
                            S.op("act", lambda e: e.activation(out=hss[:, 4:8], in_=hss[:, 4:8], func=AF.Exp, scale=-0.5), reads=[b_a], writes=[b_a])
                            yield
                            for jq in range(4):
                                S.op("dve", lambda e: e.scalar_tensor_tensor(out=haf[:, jq, :], in0=har[:, jq, :], scalar=hss[:, 4 + jq:5 + jq], in1=ang[:, h * 128:(h + 1) * 128],
                                                                            op0=ALU.mult, op1=ALU.mult), reads=[b_a, CONST] + cb, writes=[b_a])
                            yield
                            tb, tbb = banks[4 + h % 3], bbufs[4 + h % 3]
                            tv = bfview(tb)[:, 0:512].rearrange("p (j t) -> p j t", j=4)
                            for jq in range(4):
                                S.op("pe", lambda e: e.transpose(out=tv[:, jq, :], in_=haf[:, jq, :], identity=identb[:]), reads=[b_a, cb[0]], writes=[tbb])
                            S.op("dve", lambda e: e.tensor_copy(out=haT[:, h, :], in_=tv.rearrange("p j t -> p (j t)")), reads=[tbb], writes=[b_haT])
                            yield

                        epi = [iter(())]
                        for h in range(4):
                            accs = [(banks[jq], bbufs[jq]) for jq in range(4)]
                            for jq in range(4):
                                S.op("dve", lambda e: e.memset(accs[jq][0][:, 0:260], 0.0), writes=[accs[jq][1]])
                            items = []
                            for ci, (s0, nk) in enumerate(chunk_list):
                                g = h * LCH + ci
                                ri = g % 3
                                for kk in range(nk):
                                    kb = s0 + kk
                                    for p2 in range(2):
                                        j0 = 2 * p2
                                        if u == NCTX:
                                            if p2 == 1 and kb <= u * 4 + 3:
                                                items.append((ri, kk, kb, (3,), g))
                                            continue
                                        if kb <= u * 4 + j0 - 2:
                                            items.append((ri, kk, kb, (j0, j0 + 1), g))
                                        else:
                                            for jq in (j0, j0 + 1):
                                                if kb <= u * 4 + jq:
                                                    items.append((ri, kk, kb, (jq,), g))

                            LAG = 3
                            NS = 3
                            NPX = len(pex)

                            def emit_scores(it, idx):
                                ri, kk, kb, jqs, ci = it
                                ensure_chunk(ci)
                                ensure_chunk(ci + 1)
                                nq = len(jqs)
                                sl = idx % NS
                                sp_, spb = banks[4 + sl], bbufs[4 + sl]
                                sv = sp_[:, 0:256 * nq]
                                px = idx % NPX
                                delta = u * 4 + jqs[0] - kb
                                near = (nq == 1) and delta <= 1
                                rhs = Qbd[:, h, jqs[0]:jqs[0] + nq, :].rearrange("p j c -> p (j c)")
                                S.op("pe", lambda e: e.matmul(sv, lhsT=kch[ri][:, kk * 128:(kk + 1) * 128], rhs=rhs, start=True, stop=not near),
                                     reads=[b_kch[ri], b_Qbd], writes=[spb])
                                if near:
                                    S.op("pe", lambda e: e.matmul(sv, lhsT=identb[:], rhs=biasb[:, h, delta, :], start=False, stop=True), reads=[CONST, cb[0]], writes=[spb])
                                    S.op("act", lambda e: e.activation(out=pex[px][:, 0:256 * nq], in_=sv, func=AF.Exp), reads=[spb], writes=[b_pex[px]])
                                else:
                                    S.op("act", lambda e: e.activation(out=pex[px][:, 0:256 * nq], in_=sv, func=AF.Exp, bias=farb[:, h:h + 1], scale=1.0),
                                         reads=[spb, cb[14]], writes=[b_pex[px]])

                            def emit_pv(it, idx):
                                ri, kk, kb, jqs, ci = it
                                px = idx % NPX
                                for a_, jq in enumerate(jqs):
                                    ab, abb = accs[jq]
                                    for m in range(2):
                                        c0 = a_ * 256 + m * 128
                                        S.op("pe", lambda e: e.matmul(ab[:, m * 130:(m + 1) * 130], lhsT=pex[px][:, c0:c0 + 128], rhs=vch[ri][:, kk, :],
                                                                     start=False, stop=False, skip_group_check=True), reads=[b_pex[px], b_vch[ri]], writes=[abb])

                            for idx in range(len(items) + LAG):
                                if idx < len(items):
                                    emit_scores(items[idx], idx)
                                if idx >= LAG:
                                    emit_pv(items[idx - LAG], idx - LAG)
                                gcount[0] += 1
                                if gcount[0] % MSTEP == 0:
                                    next(mdrv, None)
                                if idx % 6 == 5:
                                    next(epi[0], None)
                            for _ in epi[0]:
                                pass
                            epi[0] = attn_epi(h, accs)
                            next(epi[0], None)
                        for _ in epi[0]:
                            pass
                        for _ in mdrv:
                            pass
                        S.barrier()

                    h2T = hT
                    b_h2T = b_hT
                    with ExitStack() as st:
                        W = {}
                        for nm in ("bm", "gm0", "ba", "ga0", "gm1", "ga1", "out0", "out1"):
                            W[nm] = wload(st, "w_" + nm, nm)
                        prefetch((2, 3), ["up0", "up1"])
                        yT = sbt(st, "yT", [128, 8, 512], BF16)
                        b_yT = S.buf("yT")
                        sg = [[sbt(st, "sg%d_%d" % (i, a), [128, 512]) for a in range(2)] for i in range(2)]
                        ty = [[sbt(st, "ty%d_%d" % (i, a), [128, 512]) for a in range(2)] for i in range(2)]
                        b_sg = [S.buf("sg%d" % i) for i in range(2)]
                        for c in range(8):
                            i = c % 2
                            res = []
                            for a, (bw, gw, srcT, b_src) in enumerate((("bm", "gm", hmT, b_hmT), ("ba", "ga", haT, b_haT))):
                                wt, wbf = W[bw]
                                wv = wt[:].rearrange("p (k n) -> p k n", k=4)
                                py, pyb = nbank()
                                for k in range(4):
                                    S.op("pe", lambda e: e.matmul(py[:], lhsT=wv[:, k, c * 128:(c + 1) * 128], rhs=srcT[:, k, :], start=(k == 0), stop=(k == 3)),
                                         reads=[wbf, b_src], writes=[pyb])
                                gt_, gbf = W[gw + str(c // 4)]
                                gv = gt_[:].rearrange("p (c n) -> p c n", c=8)
                                pg, pgb = nbank()
                                for dc in range(8):
                                    S.op("pe", lambda e: e.matmul(pg[:], lhsT=gv[:, dc, (c % 4) * 128:(c % 4 + 1) * 128], rhs=hT[:, dc, :], start=(dc == 0), stop=(dc == 7)),
                                         reads=[gbf, b_hT, b_hT2], writes=[pgb])
                                S.op("act", lambda e: e.activation(out=sg[i][a][:], in_=pg[:], func=AF.Sigmoid), reads=[pgb], writes=[b_sg[i]])
                                S.op("dve", lambda e: e.tensor_tensor(out=ty[i][a][:], in0=py[:], in1=sg[i][a][:], op=ALU.mult), reads=[pyb, b_sg[i]], writes=[b_sg[i]])
                            S.op("dve", lambda e: e.tensor_tensor(out=yT[:, c, :], in0=ty[i][0][:], in1=ty[i][1][:], op=ALU.add), reads=[b_sg[i]], writes=[b_yT])
                        tx = [sbt(st, "tx%d" % i, [128, 512]) for i in range(2)]
                        b_tx = [S.buf("tx%d" % i) for i in range(2)]
                        for j in range(4):
                            tsl = slice(j * 128, (j + 1) * 128)
                            for n in range(2):
                                wt, wbf = W["out%d" % n]
                                wv = wt[:].rearrange("p (c n) -> p c n", c=8)
                                po, pob = nbank()
                                for c in range(8):
                                    S.op("pe", lambda e: e.matmul(po[:], lhsT=yT[:, c, tsl], rhs=wv[:, c, :], start=(c == 0), stop=(c == 7)), reads=[wbf, b_yT], writes=[pob])
                                i = (j * 2 + n) % 2
                                S.op("dve", lambda e: e.tensor_tensor(out=tx[i][:], in0=po[:], in1=gate1[:, n * 512:(n + 1) * 512], op=ALU.mult), reads=[pob, CONST], writes=[b_tx[i]])
                                S.op("dve", lambda e: e.tensor_tensor(out=xs[:, j, n * 512:(n + 1) * 512], in0=xs[:, j, n * 512:(n + 1) * 512], in1=tx[i][:], op=ALU.add),
                                     reads=[b_tx[i], b_xs[j]], writes=[b_xs[j]])
                        run_norms(st, mult2, shift2, h2T, (b_hT, b_hT2), "n2")
                        S.barrier()

                    sa.close()
                    with ExitStack() as st:
                        actT = sbt(st, "actT", [128, NF, 512], BF16)
                        b_actT = S.buf("actT")
                        wu = [None, None, None]
                        uu = [sbt(st, "fu%d" % i, [128, 512]) for i in range(8)]
                        b_uu = [S.buf("fu%d" % i) for i in range(8)]
                        fe = [sbt(st, "fe%d" % i, [128, 512]) for i in range(2)]
                        b_fe = [S.buf("fe%d" % i) for i in range(2)]
                        wus = [pf[2], pf[3], sbt(st, "wup2", [128, 4096], BF16)]
                        b_wus = [b_pf[2], b_pf[3], S.buf("wup2")]

                        def ldup(jj):
                            if ("up%d" % jj) in prefetched:
                                prefetched.pop("up%d" % jj)
                                return
                            off, n = PIECES["up%d" % jj]
                            S.dma("sp", wus[jj % 3][:], wb_d[:, off:off + n], reads=[b_wb], writes=[b_wus[jj % 3]])

                        ldup(0)
                        ldup(1)
                        wd = sbt(st, "w_down", [128, NF * 1024], BF16)
                        wdb = S.buf("w_down")
                        wdv = wd[:].rearrange("p (f n) -> p f n", f=NF)
                        if not OPT_BATCH:
                            S.op("dve", lambda e: e.memset(fbc[:], 0.0), writes=[b_fbc])
                        if OPT_BATCH:
                            S.op("dve", lambda e: e.tensor_tensor(out=fbc[:, :, 1], in0=fhalo[:, :, 1], in1=fcw[:, :, 0], op=ALU.mult), reads=[b_fhalo] + cb, writes=[b_fbc])
                            S.op("dve", lambda e: e.tensor_tensor(out=fbc[:, :, 0], in0=fhalo[:, :, 1], in1=fcw[:, :, 1], op=ALU.mult), reads=[b_fhalo] + cb, writes=[b_fbc])
                            S.op("dve", lambda e: e.tensor_tensor(out=ctmp[:], in0=fhalo[:, :, 0], in1=fcw[:, :, 0], op=ALU.mult), reads=[b_fhalo] + cb, writes=[b_fbc])
                            S.op("dve", lambda e: e.tensor_tensor(out=fbc[:, :, 0], in0=fbc[:, :, 0], in1=ctmp[:], op=ALU.add), reads=[b_fbc], writes=[b_fbc])
                            S.op("dve", lambda e: e.tensor_tensor(out=fbc[:], in0=fbc[:], in1=fcb[:].unsqueeze(2).to_broadcast([128, 44, 2]), op=ALU.add), reads=[b_fbc] + cb, writes=[b_fbc])

                        def ffn_piece(jj):
                            if jj + 2 < 11:
                                ldup(jj + 2)
                            if jj == 2 and u != NCTX:
                                off_d, n_d = PIECES["down"]
                                S.dma("sp", wd[:], wb_d[:, off_d:off_d + n_d], reads=[b_wb], writes=[wdb])
                            if jj == 4 and u + 1 < NU:
                                prefetch((0, 1), ["mq", "mk"])
                            wv = wus[jj % 3][:].rearrange("p (c n) -> p c n", c=8)
                            wbf = b_wus[jj % 3]
                            wsel = fcwf if flagged else fcw
                            pbs = []
                            for k in range(4):
                                q = jj * 4 + k
                                pb, pbb = nbank()
                                pbs.append((pb, pbb))
                                for dc in range(8):
                                    S.op("pe", lambda e: e.matmul(pb[:], lhsT=wv[:, dc, k * 128:(k + 1) * 128], rhs=h2T[:, dc, :], start=(dc == 0), stop=(dc == 7)),
                                         reads=[wbf, b_hT, b_hT2], writes=[pbb])
                                kk_ = (jj % 2) * 4 + k
                                m0 = 2 if OPT_PARTMAIN else 0
                                S.op("act", lambda e: e.activation(out=uu[kk_][:, m0:512], in_=pb[:, m0:512], func=AF.Identity, scale=wsel[:, q, 2:3], bias=fcb[:, q:q + 1]),
                                     reads=[pbb, CONST] + cb, writes=[b_uu[kk_]])
                                for col in (range(2) if OPT_TINY else ()):
                                    S.op("act", lambda e: e.activation(out=uu[kk_][:, col:col + 1], in_=pb[:, col:col + 1], func=AF.Identity, scale=wsel[:, q, 2:3],
                                                                      bias=fbc[:, q, col:col + 1]), reads=[pbb, CONST, b_fbc] + cb, writes=[b_uu[kk_]])
                                if flagged:
                                    S.op("act", lambda e: e.activation(out=fhalo[:, q, :], in_=pb[:, 510:512], func=AF.Copy, scale=flag[:, 0:1]), reads=[pbb, cb[4], b_fbc], writes=[b_fhalo])
                                else:
                                    S.op("act", lambda e: e.copy(out=fhalo[:, q, :], in_=pb[:, 510:512]), reads=[pbb, b_fbc], writes=[b_fhalo])
                            for tp in (1, 0):
                                sh = 2 - tp
                                for k in range(4):
                                    q = jj * 4 + k
                                    kk_ = (jj % 2) * 4 + k
                                    pb, pbb = pbs[k]
                                    S.op("dve", lambda e: e.scalar_tensor_tensor(out=uu[kk_][:, sh:512], in0=pb[:, 0:512 - sh], scalar=wsel[:, q, tp:tp + 1], in1=uu[kk_][:, sh:512],
                                                                                op0=ALU.mult, op1=ALU.add), reads=[pbb, b_uu[kk_], CONST], writes=[b_uu[kk_]])
                            yield
                            for i in (range(2) if u != NCTX else ()):
                                f = 2 * jj + i
                                uv = uu[(jj % 2) * 4 + i]
                                ug = uu[(jj % 2) * 4 + 2 + i]
                                b_uv = b_uu[(jj % 2) * 4 + i]
                                b_ug = b_uu[(jj % 2) * 4 + 2 + i]
                                S.op("act", lambda e: e.activation(out=fe[i][:], in_=ug[:], func=AF.Silu), reads=[b_ug], writes=[b_fe[i]])
                                S.op("dve", lambda e: e.tensor_tensor(out=actT[:, f, :], in0=uv[:], in1=fe[i][:], op=ALU.mult), reads=[b_uv, b_fe[i]], writes=[b_actT])
                        fgens = [ffn_piece(jj) for jj in range(11)]
                        for step in range(12):
                            if step < 11:
                                next(fgens[step], None)
                            if step >= 1:
                                next(fgens[step - 1], None)
                        tx = fe
                        b_tx = b_fe
                        for j in (range(4) if u != NCTX else ()):
                            tsl = slice(j * 128, (j + 1) * 128)
                            blk = u * 4 + j
                            for n in range(2):
                                po, pob = nbank()
                                for f in range(NF):
                                    S.op("pe", lambda e: e.matmul(po[:], lhsT=actT[:, f, tsl], rhs=wdv[:, f, n * 512:(n + 1) * 512], start=(f == 0), stop=(f == NF - 1)),
                                         reads=[wdb, b_actT], writes=[pob])
                                i = (j * 2 + n) % 2
                                S.op("dve", lambda e: e.tensor_tensor(out=tx[i][:], in0=po[:], in1=gate2[:, n * 512:(n + 1) * 512], op=ALU.mult), reads=[pob, CONST], writes=[b_tx[i]])
                                S.op("dve", lambda e: e.tensor_tensor(out=xs[:, j, n * 512:(n + 1) * 512], in0=xs[:, j, n * 512:(n + 1) * 512], in1=tx[i][:], op=ALU.add),
                                     reads=[b_tx[i], b_xs[j]], writes=[b_xs[j]])
                            if own:
                                ob = blk - NFLAG * 4
                                S.dma("pool", out_d[ob * 128:(ob + 1) * 128, :], xs[:, j, :], reads=[b_xs[j]], writes=[b_out])
                        S.barrier()
        S.barrier()
    return nc


def _t5_bucket(n):
    n = np.maximum(n, 0)
    max_exact = 16
    nf = np.maximum(n, 1).astype(np.float32)
    large = max_exact + (np.log(nf / np.float32(max_exact)) / np.float32(math.log(128 / max_exact)) * np.float32(32 - max_exact)).astype(np.int32)
    large = np.minimum(large, 31)
    return np.where(n < max_exact, n, large)


def _fm(v, nch):
    return np.ascontiguousarray(np.asarray(v, np.float32).reshape(nch, 128).T)


def _piece_fm(w):
    n = w.shape[1]
    return w.reshape(8, 128, n).transpose(1, 0, 2).reshape(128, 8 * n)


def prepare_inputs(NCTX, NFULL, inputs):
    f32 = np.float32
    g = {k: np.asarray(v) for k, v in inputs.items()}
    NU = NCTX + NFULL
    half_tok = (NU // 2) * 512
    x = g["x"].astype(f32, copy=False)
    B = x.shape[0]
    assert x.shape[1] == 2 * half_tok
    w_in = g["w_in"][0]
    cols = {}
    o = 0
    for nm, n in (("mqk", 1024), ("mv", 512), ("mo", 512), ("mi", 4), ("mf", 4), ("aq", 512), ("ak", 512), ("av", 512), ("gm", 1024), ("ga", 1024)):
        cols[nm] = w_in[:, o:o + n]
        o += n

    def perm_qk(w):
        return w.reshape(1024, 2, 4, 64).transpose(0, 2, 1, 3).reshape(1024, 512)

    wall = np.zeros((128, WPAD), f32)

    def put(nm, arr):
        off, n = PIECES[nm]
        assert arr.shape == (128, n), (nm, arr.shape, n)
        wall[:, off:off + n] = arr

    put("mq", _piece_fm(cols["mqk"][:, 0:512]))
    put("mk", _piece_fm(cols["mqk"][:, 512:1024]))
    put("mv", _piece_fm(cols["mv"]))
    put("mo", _piece_fm(cols["mo"]))
    put("aq", _piece_fm(perm_qk(cols["aq"])))
    put("ak", _piece_fm(perm_qk(cols["ak"])))
    put("av", _piece_fm(cols["av"]))
    put("gm0", _piece_fm(cols["gm"][:, 0:512]))
    put("gm1", _piece_fm(cols["gm"][:, 512:1024]))
    put("ga0", _piece_fm(cols["ga"][:, 0:512]))
    put("ga1", _piece_fm(cols["ga"][:, 512:1024]))
    put("gif", _piece_fm(np.concatenate([cols["mi"], cols["mf"]], axis=1)))
    put("bm", g["w_branch_m"][0].reshape(4, 128, 1024).transpose(1, 0, 2).reshape(128, 4096))
    put("ba", g["w_branch_a"][0].reshape(4, 128, 1024).transpose(1, 0, 2).reshape(128, 4096))
    put("out0", _piece_fm(g["w_out"][0][:, 0:512]))
    put("out1", _piece_fm(g["w_out"][0][:, 512:1024]))
    w_up = g["w_up"][0]
    fcw_full = g["ffn_conv_w"][0]
    fcb_full = g["ffn_conv_b"][0]
    chunk_cols = []
    for jj in range(11):
        cc = [np.arange((2 * jj + i) * 128, (2 * jj + i + 1) * 128) for i in range(2)]
        cc += [DFF + np.arange((2 * jj + i) * 128, (2 * jj + i + 1) * 128) for i in range(2)]
        idx = np.concatenate(cc)
        chunk_cols.append(idx)
        put("up%d" % jj, _piece_fm(w_up[:, idx]))
    allidx = np.concatenate(chunk_cols)
    fcw = fcw_full[:, allidx].reshape(3, 44, 128).transpose(2, 1, 0).reshape(128, 44 * 3)
    fcb = fcb_full[allidx].reshape(44, 128).T
    put("down", g["w_down"][0].reshape(NF, 128, 1024).transpose(1, 0, 2).reshape(128, NF * 1024))

    w_ada = g["w_ada"][0]
    wada = np.stack([_piece_fm(w_ada[:, p * 512:(p + 1) * 512]) for p in range(12)]).astype(f32)
    bada = g["b_ada"][0].astype(f32)
    mcw = g["m_conv_w"][0].reshape(4, 8, 128).transpose(2, 1, 0).reshape(128, 32)
    mcb = _fm(g["m_conv_b"][0], 8)
    gifb = np.concatenate([g["m_igate_b"][0], g["m_fgate_b"][0]]).astype(f32)
    gq = np.tile(g["a_qnorm_g"][0], 8).astype(f32)
    gk = np.tile(g["a_knorm_g"][0], 8).astype(f32)
    rel = g["rel_bias"].astype(f32)
    kk = np.arange(128)[:, None]
    qq = np.arange(128)[None, :]
    biasg = np.zeros((128, 4, 2, 2, 128), f32)
    maskn = np.zeros((128, 4, 2, 2, 128), f32)
    for dl in range(2):
        dist = qq - kk + 128 * dl
        bidx = _t5_bucket(dist)
        for h in range(4):
            t = rel[bidx, h]
            biasg[:, h, dl, 0, :] = t
            biasg[:, h, dl, 1, :] = t
        if dl == 0:
            mk = np.where(dist < 0, NEG, 0.0).astype(f32)
            maskn[:, :, 0, :, :] = mk[:, None, None, :]
    farb = rel[31, :].astype(f32)
    umask = (kk <= qq).astype(f32)
    negm4 = np.tile(np.where(kk <= qq, 0.0, NEG).astype(f32), (1, 4))
    import ml_dtypes
    common = dict(
        wada=wada, bada=bada, badafm=_fm(bada, 48), g1fm=_fm(g["norm1_g"][0], 8), g2fm=_fm(g["norm2_g"][0], 8),
        wall=wall, mcw=np.ascontiguousarray(mcw, f32), mcb=mcb, fcw=np.ascontiguousarray(fcw, f32), fcb=np.ascontiguousarray(fcb, f32),
        gifb=gifb, mng=g["m_norm_g"][0].astype(f32), ang=g["a_norm_g"][0].astype(f32), gq=gq, gk=gk,
        gqfm=np.tile(g["a_qnorm_g"][0], 2).reshape(128, 1).astype(f32), gkfm=np.tile(g["a_knorm_g"][0], 2).reshape(128, 1).astype(f32),
        alam=g["a_lambda"][0].reshape(256).astype(f32), biasg=biasg.reshape(128, -1), maskn=maskn.reshape(128, -1), farb=farb,
        identb=np.eye(128).astype(ml_dtypes.bfloat16), identf=np.eye(128, dtype=f32), umask=umask, negm4=negm4,
    )
    in_maps = []
    for b in range(B):
        cfm = _fm(g["c"][b], 8)
        for hf in range(2):
            if hf == 0:
                xl = np.concatenate([np.zeros((half_tok, D), f32), x[b, 0:half_tok]], axis=0)
                fl = np.zeros((128, 1), f32)
            else:
                xl = x[b]
                fl = np.ones((128, 1), f32)
            m = dict(common)
            m["x"] = np.ascontiguousarray(xl)
            m["cfm"] = cfm
            m["flag"] = fl
            in_maps.append(m)
    return in_maps


_NC_CACHE = {}


def run(NCTX, NFULL, inputs):
    key = (NCTX, NFULL)
    if key not in _NC_CACHE:
        _NC_CACHE[key] = build_program(NCTX, NFULL)
    nc = _NC_CACHE[key]
    in_maps = prepare_inputs(NCTX, NFULL, inputs)
    res = run_bass_kernel_spmd(nc, in_maps, core_ids=list(range(len(in_maps))))
    B = len(in_maps) // 2
    half_tok = ((NCTX + NFULL) // 2) * 512
    out = np.empty((B, 2 * half_tok, D), np.float32)
    for b in range(B):
        for hf in range(2):
            out[b, hf * half_tok:(hf + 1) * half_tok] = res.results[b * 2 + hf]["out"]
    return out


def kernel(**inputs):
    return run(7, 9, inputs)
```

```python
import math
from contextlib import ExitStack

import numpy as np

import concourse.bass as bass
import concourse.mybir as mybir
from concourse.bass_utils import run_bass_kernel_spmd

F32 = mybir.dt.float32
BF16 = mybir.dt.bfloat16
AF = mybir.ActivationFunctionType
ALU = mybir.AluOpType
AX = mybir.AxisListType

D = 1024
DC = 8
DFF = 2816
NF = 22
EPS = 1e-6
LAM_INIT = 0.8 - 0.6 * math.exp(-0.3 * 0)
NEG = -30000.0
OPT_SELF_WAR = True
OPT_XN_ACT = True
OPT_SILU = True
OPT_GFOLD = True
OPT_TINY = True
OPT_BATCH = True
OPT_PARTMAIN = True

PIECES = {}
_off = 0
for _nm, _n in (("mq", 4096), ("mk", 4096), ("mv", 4096), ("mo", 4096), ("aq", 4096), ("ak", 4096),
                ("av", 4096), ("gm0", 4096), ("gm1", 4096), ("ga0", 4096), ("ga1", 4096), ("gif", 64),
                ("bm", 4096), ("ba", 4096), ("out0", 4096), ("out1", 4096)):
    PIECES[_nm] = (_off, _n)
    _off += _n
for _j in range(11):
    PIECES["up%d" % _j] = (_off, 4096)
    _off += 4096
PIECES["down"] = (_off, NF * 1024)
_off += NF * 1024
WTOT = _off
WPAD = ((WTOT + 4095) // 4096) * 4096


class Buf:
    __slots__ = ("name", "w", "rs", "sem", "semv", "grp", "psum", "persist")

    def __init__(self, name, grp=None):
        self.psum = False
        self.persist = False
        self.name = name
        self.w = None
        self.rs = []
        self.sem = None
        self.semv = 0
        self.grp = grp


class Sched:
    def __init__(self, nc, stack):
        self.nc = nc
        self.stack = stack
        self.engs = {}
        for nm, h in (("pe", nc.tensor), ("act", nc.scalar), ("dve", nc.vector), ("pool", nc.gpsimd), ("sp", nc.sync)):
            sem = stack.enter_context(nc.semaphore("s_" + nm))
            self.engs[nm] = dict(h=h, sem=sem, cnt=0, seen={})
        self.groups = {}
        self.dma_ev = {}
        self.free_sems = {}
        self.live = []
        self.nsem = 0

    def buf(self, name, grp=None):
        return Buf(name, grp)

    def _need(self, e, deps):
        E = self.engs[e]
        best = {}
        for d in deps:
            if d is None:
                continue
            sem, val, en = d
            if en == e and e == "pe":
                continue
            k = id(sem)
            if k not in best or best[k][1] < val:
                best[k] = (sem, val)
        for k, (sem, val) in best.items():
            if E["seen"].get(k, 0) >= val:
                continue
            E["h"].wait_ge(sem, val)
            E["seen"][k] = val

    def op(self, e, fn, reads=(), writes=()):
        E = self.engs[e]
        deps = []
        for b in reads:
            deps.append(b.w)
            if b.psum:
                for r in b.rs:
                    if r[2] != e:
                        deps.append(r)
        for b in writes:
            if b.w is not None and b.w[2] != e:
                deps.append(b.w)
            for r in b.rs:
                if OPT_SELF_WAR or r[2] != e:
                    deps.append(r)
        self._need(e, deps)
        ins = fn(E["h"])
        E["cnt"] += 1
        ins.then_inc(E["sem"], 1)
        ev = (E["sem"], E["cnt"], e)
        for b in reads:
            b.rs.append(ev)
        for b in writes:
            b.w = ev
            b.rs = []
        return ins

    def dma(self, q, out, in_, reads=(), writes=()):
        E = self.engs[q]
        deps = []
        for b in reads:
            deps.append(b.w)
        for b in writes:
            if b.grp in ("wbw", "kvw", "outw"):
                continue
            deps.append(b.w)
            deps.extend(b.rs)
        self._need(q, deps)
        ins = E["h"].dma_start(out=out, in_=in_)
        cands = list(writes) + list(reads)
        tgt = ([b for b in cands if b.grp is None] + cands)[0]
        if tgt.grp is not None:
            gk = tgt.grp
            if gk not in self.groups:
                self.groups[gk] = [self.stack.enter_context(self.nc.semaphore("g_" + gk)), 0]
            g = self.groups[gk]
            g[1] += 16
            sem, val = g[0], g[1]
        else:
            if tgt.sem is None:
                tgt.sem = {}
                self.live.append(tgt)
            if q not in tgt.sem:
                fs = self.free_sems.setdefault(q, [])
                if not fs:
                    self.nsem += 1
                    fs.append([self.stack.enter_context(self.nc.semaphore("dp%d" % self.nsem)), 0])
                tgt.sem[q] = fs.pop()
            ent = tgt.sem[q]
            ent[1] += 16
            sem, val = ent[0], ent[1]
        ins.then_inc(sem, 16)
        self.dma_ev[id(sem)] = (sem, val)
        ev = (sem, val, "dma")
        for b in reads:
            b.rs.append(ev)
        for b in writes:
            b.w = ev
            b.rs = []
        return ins

    def barrier(self, engines=("pe", "act", "dve", "pool", "sp")):
        deps = [(E["sem"], E["cnt"], "x") for E in self.engs.values() if E["cnt"] > 0]
        deps += [(s, v, "dma") for (s, v) in self.dma_ev.values()]
        for e in engines:
            self._need(e, deps)
        self.dma_ev = {}
        keep = []
        for b in self.live:
            if b.persist:
                keep.append(b)
                continue
            for qq, ent in b.sem.items():
                self.free_sems[qq].append(ent)
            b.sem = None
        self.live = keep

    def seal(self, bufs, grp):
        g = self.groups[grp]
        for b in bufs:
            b.w = (g[0], g[1], "dma")


def build_program(NCTX, NFULL, debug=False):
    NU = NCTX + NFULL
    NFLAG = NU // 2
    NBLK = NU * 4
    NTOK = NU * 512
    NOWN = (NFULL - 1) * 512
    assert NU == 2 * (NFULL - 1)

    nc = bass.Bass("TRN2", target_bir_lowering=False)

    def din(name, shape, dt=F32):
        return nc.dram_tensor(name, list(shape), dt, kind="ExternalInput").ap()

    x_d = din("x", [NTOK, D])
    cfm_d = din("cfm", [128, 8])
    wada_d = din("wada", [12, 128, 8 * 512])
    bada_d = din("bada", [6144])
    badafm_d = din("badafm", [128, 48])
    g1fm_d = din("g1fm", [128, 8])
    g2fm_d = din("g2fm", [128, 8])
    wall_d = din("wall", [128, WPAD])
    mcw_d = din("mcw", [128, 8 * 4])
    mcb_d = din("mcb", [128, 8])
    fcw_d = din("fcw", [128, 44 * 3])
    fcb_d = din("fcb", [128, 44])
    gifb_d = din("gifb", [8])
    mng_d = din("mng", [512])
    ang_d = din("ang", [512])
    gq_d = din("gq", [512])
    gk_d = din("gk", [512])
    gqfm_d = din("gqfm", [128, 1])
    gkfm_d = din("gkfm", [128, 1])
    alam_d = din("alam", [256])
    biasg_d = din("biasg", [128, 4 * 2 * 2 * 128])
    maskn_d = din("maskn", [128, 4 * 2 * 2 * 128])
    farb_d = din("farb", [4])
    flag_d = din("flag", [128, 1])
    identb_d = din("identb", [128, 128], BF16)
    identf_d = din("identf", [128, 128])
    umask_d = din("umask", [128, 128])
    negm4_d = din("negm4", [128, 512])
    out_d = nc.dram_tensor("out", [NOWN, D], F32, kind="ExternalOutput").ap()
    wb_d = nc.dram_tensor("wb_scr", [128, WPAD], BF16, kind="Internal").ap()
    kt_d = nc.dram_tensor("kt_scr", [4, 128, NBLK * 128], BF16, kind="Internal").ap()
    va_d = nc.dram_tensor("va_scr", [4, 128, NBLK * 130], BF16, kind="Internal").ap()

    with ExitStack() as top:
        S = Sched(nc, top)

        uid = [0]

        def sbt(st, name, shape, dt=F32):
            uid[0] += 1
            return st.enter_context(nc.sbuf_tensor("s%d_%s" % (uid[0], name), list(shape), dt))

        banks = [top.enter_context(nc.psum_tensor("bank%d" % i, [128, 512], F32)) for i in range(8)]
        bbufs = [S.buf("bank%d" % i) for i in range(8)]
        for b_ in bbufs:
            b_.psum = True
        bank_rr = [0]

        def nbank(lo=0, hi=8):
            i = lo + (bank_rr[0] % (hi - lo))
            bank_rr[0] += 1
            return banks[i], bbufs[i]

        def bfview(bank):
            return bank[:].bitcast(BF16)

        identb = sbt(top, "identb", [128, 128], BF16)
        identf = sbt(top, "identf", [128, 128])
        umask = sbt(top, "umask", [128, 128])
        negm4 = sbt(top, "negm4", [128, 512])
        ones_f = sbt(top, "ones_f", [128, 128])
        flag = sbt(top, "flag", [128, 1])
        epst = sbt(top, "epst", [128, 1])
        onet = sbt(top, "onet", [128, 1])
        mult1 = sbt(top, "mult1", [128, 8])
        shift1 = sbt(top, "shift1", [128, 8])
        mult2 = sbt(top, "mult2", [128, 8])
        shift2 = sbt(top, "shift2", [128, 8])
        gate1 = sbt(top, "gate1", [128, D])
        gate2 = sbt(top, "gate2", [128, D])
        mcw = sbt(top, "mcw", [128, 8, 4])
        mcb = sbt(top, "mcb", [128, 8])
        mcwf = sbt(top, "mcwf", [128, 8, 4])
        fcwf = sbt(top, "fcwf", [128, 44, 3])
        mcnb = sbt(top, "mcnb", [128, 8])
        fcw = sbt(top, "fcw", [128, 44, 3])
        fcb = sbt(top, "fcb", [128, 44])
        fcnb = sbt(top, "fcnb", [128, 44])
        gifb = sbt(top, "gifb", [128, 8])
        mng = sbt(top, "mng", [128, 512])
        ang = sbt(top, "ang", [128, 512])
        gq = sbt(top, "gq", [128, 512])
        gk = sbt(top, "gk", [128, 512])
        gqfm = sbt(top, "gqfm", [128, 1])
        gkfm = sbt(top, "gkfm", [128, 1])
        mbc = sbt(top, "mbc", [128, 8, 3])
        fbc = sbt(top, "fbc", [128, 44, 2])
        ctmp = sbt(top, "ctmp", [128, 44])
        b_mbc = S.buf("mbc")
        b_fbc = S.buf("fbc")
        biasb = sbt(top, "biasb", [128, 4, 2, 256], BF16)
        farb = sbt(top, "farb", [128, 4])
        neglam = sbt(top, "neglam", [128, 1])
        Sst = sbt(top, "Sst", [128, 4, 130])
        Sstb = sbt(top, "Sstb", [128, 4, 130], BF16)
        mhalo = sbt(top, "mhalo", [128, 8, 3])
        fhalo = sbt(top, "fhalo", [128, 44, 2])
        CONST = S.buf("const")
        b_S = S.buf("Sst")
        b_Sb = S.buf("Sstb")
        b_mhalo = S.buf("mhalo")
        b_fhalo = S.buf("fhalo")
        b_wb = S.buf("wb_scr", grp="wbw")
        b_ktd = S.buf("kt_scr", grp="kvw")
        b_vad = S.buf("va_scr", grp="kvw")
        b_out = S.buf("outd", grp="outw")

        def ld_const(t, src, grp="cst"):
            b = S.buf("c_" + t.name, grp=grp)
            S.dma("sp", t[:], src, writes=[b])
            return b

        cb = []
        cb.append(ld_const(identb, identb_d[:, :]))
        cb.append(ld_const(identf, identf_d[:, :]))
        cb.append(ld_const(umask, umask_d[:, :]))
        cb.append(ld_const(negm4, negm4_d[:, :]))
        cb.append(ld_const(flag, flag_d[:, :]))
        cb.append(ld_const(mcw, mcw_d.rearrange("p (c k) -> p c k", k=4)))
        cb.append(ld_const(mcb, mcb_d[:, :]))
        cb.append(ld_const(fcw, fcw_d.rearrange("p (c k) -> p c k", k=3)))
        cb.append(ld_const(fcb, fcb_d[:, :]))
        cb.append(ld_const(gifb, gifb_d.partition_broadcast(128)))
        cb.append(ld_const(mng, mng_d.partition_broadcast(128)))
        cb.append(ld_const(ang, ang_d.partition_broadcast(128)))
        cb.append(ld_const(gq, gq_d.partition_broadcast(128)))
        cb.append(ld_const(gk, gk_d.partition_broadcast(128)))
        cb.append(ld_const(farb, farb_d.partition_broadcast(128)))
        cb.append(ld_const(gqfm, gqfm_d[:, :]))
        cb.append(ld_const(gkfm, gkfm_d[:, :]))
        S.seal(cb, "cst")
        S.op("dve", lambda e: e.memset(ones_f[:], 1.0), writes=[CONST])
        S.op("dve", lambda e: e.memset(epst[:], EPS), writes=[CONST])
        S.op("dve", lambda e: e.memset(onet[:], 1.0), writes=[CONST])
        S.op("dve", lambda e: e.memset(Sst[:], 0.0), writes=[b_S])
        S.op("dve", lambda e: e.memset(Sstb[:], 0.0), writes=[b_Sb])
        S.op("dve", lambda e: e.memset(mhalo[:], 0.0), writes=[b_mhalo])
        S.op("dve", lambda e: e.memset(fhalo[:], 0.0), writes=[b_fhalo])
        S.op("dve", lambda e: e.tensor_scalar(out=mcnb[:], in0=mcb[:], scalar1=-1.0, scalar2=None, op0=ALU.mult), reads=cb, writes=[CONST])
        S.op("dve", lambda e: e.tensor_scalar(out=fcnb[:], in0=fcb[:], scalar1=-1.0, scalar2=None, op0=ALU.mult), reads=cb, writes=[CONST])
        S.op("dve", lambda e: e.tensor_scalar(out=gqfm[:], in0=gqfm[:], scalar1=0.125, scalar2=None, op0=ALU.mult), reads=cb, writes=[CONST])
        S.op("dve", lambda e: e.tensor_scalar(out=mcwf[:], in0=mcw[:], scalar1=flag[:, 0:1], scalar2=None, op0=ALU.mult), reads=cb, writes=[CONST])
        S.op("dve", lambda e: e.tensor_scalar(out=fcwf[:], in0=fcw[:], scalar1=flag[:, 0:1], scalar2=None, op0=ALU.mult), reads=cb, writes=[CONST])
        S.op("dve", lambda e: e.tensor_scalar(out=ang[:], in0=ang[:], scalar1=1.0 - LAM_INIT, scalar2=None, op0=ALU.mult), reads=cb, writes=[CONST])

        with ExitStack() as st:
            cfm = sbt(st, "cfm", [128, 8])
            sc = sbt(st, "sc", [128, 8])
            sct = sbt(st, "sct", [128, 8])
            scbc = sbt(st, "scbc", [128, 8, 128])
            badafm = sbt(st, "badafm", [128, 48])
            g1fm = sbt(st, "g1fm", [128, 8])
            g2fm = sbt(st, "g2fm", [128, 8])
            modfm = sbt(st, "modfm", [128, 48])
            alam = sbt(st, "alam", [128, 256])
            lt = sbt(st, "lt", [128, 128])
            ls = sbt(st, "ls", [128, 2])
            biasg = sbt(st, "biasg", [128, 2048])
            maskn = sbt(st, "maskn", [128, 2048])
            wst = [sbt(st, "wst%d" % i, [128, 4096]) for i in range(2)]
            wbo = [sbt(st, "wbo%d" % i, [128, 4096], BF16) for i in range(2)]
            bbt = [sbt(st, "bbt%d" % i, [128, 512]) for i in range(2)]
            b_wst = [S.buf("wst%d" % i) for i in range(2)]
            b_wbo = [S.buf("wbo%d" % i) for i in range(2)]
            b_bbt = [S.buf("bbt%d" % i) for i in range(2)]
            L = S.buf("prel")
            b_l = [ld_const(cfm, cfm_d[:, :], "cst2"), ld_const(badafm, badafm_d[:, :], "cst2"), ld_const(g1fm, g1fm_d[:, :], "cst2"),
                   ld_const(g2fm, g2fm_d[:, :], "cst2"), ld_const(alam, alam_d.partition_broadcast(128), "cst2"),
                   ld_const(biasg, biasg_d[:, :], "cst2"), ld_const(maskn, maskn_d[:, :], "cst2")]
            S.seal(b_l, "cst2")
            S.op("act", lambda e: e.activation(out=sct[:], in_=cfm[:], func=AF.Exp, scale=-1.0), reads=b_l, writes=[L])
            S.op("dve", lambda e: e.tensor_scalar(out=sct[:], in0=sct[:], scalar1=1.0, scalar2=None, op0=ALU.add), reads=[L], writes=[L])
            S.op("dve", lambda e: e.reciprocal(out=sct[:], in_=sct[:]), reads=[L], writes=[L])
            S.op("dve", lambda e: e.tensor_tensor(out=sc[:], in0=cfm[:], in1=sct[:], op=ALU.mult), reads=[L], writes=[L])
            S.op("dve", lambda e: e.tensor_copy(out=scbc[:], in_=sc[:].unsqueeze(2).to_broadcast([128, 8, 128])), reads=[L], writes=[L])
            S.op("dve", lambda e: e.tensor_tensor(out=lt[:, 0:64], in0=alam[:, 0:64], in1=alam[:, 64:128], op=ALU.mult), reads=b_l, writes=[L])
            S.op("dve", lambda e: e.tensor_tensor(out=lt[:, 64:128], in0=alam[:, 128:192], in1=alam[:, 192:256], op=ALU.mult), reads=[L], writes=[L])
            S.op("dve", lambda e: e.tensor_reduce(out=ls[:], in_=lt[:].rearrange("p (a b) -> p a b", a=2), axis=AX.X, op=ALU.add), reads=[L], writes=[L])
            S.op("act", lambda e: e.activation(out=ls[:], in_=ls[:], func=AF.Exp), reads=[L], writes=[L])
            S.op("dve", lambda e: e.tensor_tensor(out=neglam[:], in0=ls[:, 1:2], in1=ls[:, 0:1], op=ALU.subtract), reads=[L], writes=[CONST])
            S.op("dve", lambda e: e.tensor_scalar(out=neglam[:], in0=neglam[:], scalar1=-LAM_INIT, scalar2=None, op0=ALU.add), reads=[CONST], writes=[CONST])
            S.op("dve", lambda e: e.tensor_tensor(out=biasb[:].rearrange("p a b c -> p (a b c)"), in0=biasg[:], in1=maskn[:], op=ALU.add), reads=b_l, writes=[CONST])

            fm_ps, fm_b = banks[7], bbufs[7]
            fm_cols = {0: 0, 1: 4, 2: 8, 3: 12, 6: 24, 7: 28, 8: 32, 9: 36}
            for pc in range(12):
                i = pc % 2
                S.dma("sp", wst[i][:], wada_d[pc, :, :], writes=[b_wst[i]])
                wv = wst[i][:].rearrange("p (c n) -> p c n", c=8)
                if pc in (4, 5, 10, 11):
                    S.dma("sp", bbt[i][:], bada_d[pc * 512:(pc + 1) * 512].partition_broadcast(128), writes=[b_bbt[i]])
                    pb, pbb = nbank(0, 6)
                    for dc in range(8):
                        S.op("pe", lambda e: e.matmul(pb[:], lhsT=scbc[:, dc, :], rhs=wv[:, dc, :], start=(dc == 0), stop=(dc == 7)),
                             reads=[L, b_wst[i]], writes=[pbb])
                    gt = gate1 if pc < 6 else gate2
                    off = (pc % 2) * 512
                    S.op("dve", lambda e: e.tensor_tensor(out=gt[:, off:off + 512], in0=pb[:], in1=bbt[i][:], op=ALU.add),
                         reads=[pbb, b_bbt[i]], writes=[CONST])
                else:
                    for k in range(4):
                        col = fm_cols[pc] + k
                        for dc in range(8):
                            S.op("pe", lambda e: e.matmul(fm_ps[:, col:col + 1], lhsT=wv[:, dc, k * 128:(k + 1) * 128], rhs=sc[:, dc:dc + 1],
                                                         start=(dc == 0), stop=(dc == 7)), reads=[L, b_wst[i]], writes=[fm_b])
            S.op("dve", lambda e: e.tensor_tensor(out=modfm[:, 0:16], in0=fm_ps[:, 0:16], in1=badafm[:, 0:16], op=ALU.add), reads=[fm_b] + b_l, writes=[L])
            S.op("dve", lambda e: e.tensor_tensor(out=modfm[:, 24:40], in0=fm_ps[:, 24:40], in1=badafm[:, 24:40], op=ALU.add), reads=[fm_b] + b_l, writes=[L])
            S.op("dve", lambda e: e.scalar_tensor_tensor(out=mult1[:], in0=modfm[:, 8:16], scalar=1.0, in1=g1fm[:], op0=ALU.add, op1=ALU.mult), reads=[L], writes=[CONST])
            S.op("dve", lambda e: e.tensor_copy(out=shift1[:], in_=modfm[:, 0:8]), reads=[L], writes=[CONST])
            S.op("dve", lambda e: e.scalar_tensor_tensor(out=mult2[:], in0=modfm[:, 32:40], scalar=1.0, in1=g2fm[:], op0=ALU.add, op1=ALU.mult), reads=[L], writes=[CONST])
            S.op("dve", lambda e: e.tensor_copy(out=shift2[:], in_=modfm[:, 24:32]), reads=[L], writes=[CONST])

            cast_eng = ["dve", "act"]
            for ch in range(WPAD // 4096):
                i = ch % 2
                S.dma("sp", wst[i][:], wall_d[:, ch * 4096:(ch + 1) * 4096], writes=[b_wst[i]])
                ce = cast_eng[ch % 2]
                if ce == "act":
                    S.op("act", lambda e: e.copy(out=wbo[i][:], in_=wst[i][:]), reads=[b_wst[i]], writes=[b_wbo[i]])
                else:
                    S.op(ce, lambda e: e.tensor_copy(out=wbo[i][:], in_=wst[i][:]), reads=[b_wst[i]], writes=[b_wbo[i]])
                S.dma("pool", wb_d[:, ch * 4096:(ch + 1) * 4096], wbo[i][:], reads=[b_wbo[i]], writes=[b_wb])
            S.barrier()

        njunk = sbt(top, "njunk", [128, D], BF16)
        ncache = {}
        pf = [sbt(top, "pf%d" % i, [128, 4096], BF16) for i in range(4)]
        b_pf = [S.buf("pf%d" % i) for i in range(4)]
        for b_ in b_pf:
            b_.persist = True
        prefetched = {}

        def prefetch(slots, pieces):
            for sl, piece in zip(slots, pieces):
                off, n = PIECES[piece]
                S.dma("sp", pf[sl][:, 0:n], wb_d[:, off:off + n], reads=[b_wb], writes=[b_pf[sl]])
                prefetched[piece] = (pf[sl], b_pf[sl])

        def wload(st, name, piece, shape3=None):
            if piece in prefetched:
                return prefetched.pop(piece)
            off, n = PIECES[piece]
            t = sbt(st, name, [128, n], BF16)
            b = S.buf(name)
            S.dma("sp", t[:], wb_d[:, off:off + n], reads=[b_wb], writes=[b])
            return t, b

        for u in range(NU):
            full = u >= NCTX
            flagged = u < NFLAG
            own = u >= NFLAG
            with ExitStack() as su:
                xs = sbt(su, "xs", [128, 4, D])
                hT = sbt(su, "hT", [128, 8, 512], BF16)
                sa = su.enter_context(ExitStack())
                mqT = sbt(sa, "mqT", [128, 8, 512], BF16)
                mVA = sbt(sa, "mVA", [128, 4, 4, 130], BF16)
                sigmo = sbt(sa, "sigmo", [128, 4, 512])
                gif = sbt(sa, "gif", [128, 4, 8])
                Qbd = sbt(sa, "Qbd", [128, 4, 4, 256], BF16)
                hmT = sbt(sa, "hmT", [128, 4, 512], BF16)
                haT = sbt(sa, "haT", [128, 4, 512], BF16)
                b_xs = [S.buf("xs%d" % j) for j in range(4)]
                b_hT = S.buf("hT")
                b_hT2 = S.buf("hT2")
                b_mqT = S.buf("mqT")
                b_mVA = S.buf("mVA")
                b_sig = S.buf("sigmo")
                b_gif = S.buf("gif")
                b_Qbd = S.buf("Qbd")
                b_hmT = S.buf("hmT")
                b_haT = S.buf("haT")
                if full:
                    S.op("dve", lambda e: e.memset(Qbd[:], 0.0), writes=[b_Qbd])

                def norm_block(st, j, mult, shift, hdst, b_hdst, tagp):
                    junk = njunk
                    if not hasattr(st, "ncache"):
                        st.ncache = {}
                    ck = (tagp, j % 2)
                    if ck not in st.ncache:
                        st.ncache[ck] = (sbt(st, tagp + "xn%d" % (j % 2), [128, D], BF16), S.buf("xn"))
                    xn, b_xn = st.ncache[ck]
                    ss = sbt(st, tagp + "ss%d" % j, [128, 2])
                    bl = S.buf("nb")
                    S.op("act", lambda e: e.activation(out=xn[:], in_=xs[:, j, :], func=AF.Square, scale=1.0 / 32.0, accum_out=ss[:, 0:1]),
                         reads=[b_xs[j]], writes=[bl, b_xn])
                    yield
                    S.op("act", lambda e: e.activation(out=ss[:, 1:2], in_=ss[:, 0:1], func=AF.Ln, bias=epst[:, 0:1], scale=1.0), reads=[bl, CONST], writes=[bl])
                    S.op("act", lambda e: e.activation(out=ss[:, 1:2], in_=ss[:, 1:2], func=AF.Exp, scale=-0.5), reads=[bl], writes=[bl])
                    yield
                    S.op("dve", lambda e: e.tensor_scalar(out=xn[:], in0=xs[:, j, :], scalar1=ss[:, 1:2], scalar2=None, op0=ALU.mult),
                         reads=[bl, b_xs[j]], writes=[b_xn])
                    yield
                    pvs = []
                    for half in range(2):
                        pb, pbb = nbank()
                        pv = bfview(pb)[:, 0:512].rearrange("p (c t) -> p c t", c=4)
                        pvs.append((pv, pbb))
                        for cc in range(4):
                            c = half * 4 + cc
                            S.op("pe", lambda e: e.transpose(out=pv[:, cc, :], in_=xn[:, c * 128:(c + 1) * 128], identity=identb[:]),
                                 reads=[b_xn, cb[0]], writes=[pbb])
                    yield
                    for half in range(2):
                        pv, pbb = pvs[half]
                        for cc in range(4):
                            c = half * 4 + cc
                            if half == 0:
                                S.op("act", lambda e: e.activation(out=hdst[:, c, j * 128:(j + 1) * 128], in_=pv[:, cc, :], func=AF.Identity,
                                                                  scale=mult[:, c:c + 1], bias=shift[:, c:c + 1]), reads=[pbb, CONST], writes=[b_hdst[0]])
                            else:
                                S.op("dve", lambda e: e.tensor_scalar(out=hdst[:, c, j * 128:(j + 1) * 128], in0=pv[:, cc, :], scalar1=mult[:, c:c + 1],
                                                                     scalar2=shift[:, c:c + 1], op0=ALU.mult, op1=ALU.add), reads=[pbb, CONST], writes=[b_hdst[1]])
                    yield

                def run_norms(st, mult, shift, hdst, b_hdst, tagp):
                    for pair_ in ((0, 1), (2, 3)):
                        alive_ = [norm_block(st, j_, mult, shift, hdst, b_hdst, tagp) for j_ in pair_]
                        while alive_:
                            nxt_ = []
                            for g_ in alive_:
                                try:
                                    next(g_)
                                    nxt_.append(g_)
                                except StopIteration:
                                    pass
                            alive_ = nxt_

                with ExitStack() as st:
                    names = (["mq", "mk", "mv", "gif", "av", "mo", "aq", "ak"] if full else ["mk", "mv", "gif", "av", "ak"])
                    W = {}
                    for j in range(4):
                        blk = u * 4 + j
                        S.dma("sp", xs[:, j, :], x_d[blk * 128:(blk + 1) * 128, :], writes=[b_xs[j]])
                    for nm in names:
                        W[nm] = wload(st, "w_" + nm, nm)
                    if not full and u + 1 < NU:
                        nxt_full = (u + 1) >= NCTX
                        slots = (2, 3) if (nxt_full or (u + 1) % 2 == 1) else (0, 1)
                        if nxt_full:
                            slots = (2, 3)
                        prefetch(slots, ["mq", "mk"] if nxt_full else ["mk", "mv"])
                    run_norms(st, mult1, shift1, hT, (b_hT, b_hT2), "n1")
                    if flagged:
                        S.op("dve", lambda e: e.tensor_copy(out=mVA[:, :, :, 128:130], in_=flag[:, 0:1].unsqueeze(1).unsqueeze(1).to_broadcast([128, 4, 4, 2])),
                             reads=[cb[4]], writes=[b_mVA])
                    else:
                        S.op("dve", lambda e: e.memset(mVA[:, :, :, 128:130], 1.0), writes=[b_mVA])
                    acc = [sbt(st, "acc%d" % i, [128, 512]) for i in range(4)]
                    et = [sbt(st, "et%d" % i, [128, 512]) for i in range(4)]
                    b_acc = [S.buf("acc%d" % i) for i in range(4)]
                    b_et = [S.buf("et%d" % i) for i in range(4)]
                    hv = mhalo
                    if not OPT_BATCH:
                        S.op("dve", lambda e: e.memset(mbc[:], 0.0), writes=[b_mbc])
                    if OPT_BATCH:
                        S.op("dve", lambda e: e.tensor_tensor(out=mbc[:, :, 2], in0=hv[:, :, 2], in1=mcw[:, :, 0], op=ALU.mult), reads=[b_mhalo] + cb, writes=[b_mbc])
                        S.op("dve", lambda e: e.tensor_tensor(out=mbc[:, :, 1], in0=hv[:, :, 2], in1=mcw[:, :, 1], op=ALU.mult), reads=[b_mhalo] + cb, writes=[b_mbc])
                        S.op("dve", lambda e: e.tensor_tensor(out=mbc[:, :, 0], in0=hv[:, :, 2], in1=mcw[:, :, 2], op=ALU.mult), reads=[b_mhalo] + cb, writes=[b_mbc])
                        S.op("dve", lambda e: e.tensor_tensor(out=ctmp[:, 0:8], in0=hv[:, :, 1], in1=mcw[:, :, 0], op=ALU.mult), reads=[b_mhalo] + cb, writes=[b_mbc])
                        S.op("dve", lambda e: e.tensor_tensor(out=mbc[:, :, 1], in0=mbc[:, :, 1], in1=ctmp[:, 0:8], op=ALU.add), reads=[b_mbc], writes=[b_mbc])
                        S.op("dve", lambda e: e.tensor_tensor(out=ctmp[:, 8:16], in0=hv[:, :, 1], in1=mcw[:, :, 1], op=ALU.mult), reads=[b_mhalo] + cb, writes=[b_mbc])
                        S.op("dve", lambda e: e.tensor_tensor(out=mbc[:, :, 0], in0=mbc[:, :, 0], in1=ctmp[:, 8:16], op=ALU.add), reads=[b_mbc], writes=[b_mbc])
                        S.op("dve", lambda e: e.tensor_tensor(out=ctmp[:, 16:24], in0=hv[:, :, 0], in1=mcw[:, :, 0], op=ALU.mult), reads=[b_mhalo] + cb, writes=[b_mbc])
                        S.op("dve", lambda e: e.tensor_tensor(out=mbc[:, :, 0], in0=mbc[:, :, 0], in1=ctmp[:, 16:24], op=ALU.add), reads=[b_mbc], writes=[b_mbc])
                        S.op("dve", lambda e: e.tensor_tensor(out=mbc[:], in0=mbc[:], in1=mcb[:].unsqueeze(2).to_broadcast([128, 8, 3]), op=ALU.add), reads=[b_mbc] + cb, writes=[b_mbc])
                    groups = ([[0, 1, 2, 3], [4, 5, 6, 7]] if full else [[4, 5, 6, 7]])
                    wsel = mcwf if flagged else mcw
                    for grp in groups:
                        pbs = []
                        for k, c in enumerate(grp):
                            wt, wbf = W["mq" if c < 4 else "mk"]
                            wv = wt[:].rearrange("p (c n) -> p c n", c=8)
                            pb, pbb = nbank()
                            pbs.append((pb, pbb))
                            for dc in range(8):
                                S.op("pe", lambda e: e.matmul(pb[:], lhsT=wv[:, dc, k * 128:(k + 1) * 128], rhs=hT[:, dc, :], start=(dc == 0), stop=(dc == 7)),
                                     reads=[wbf, b_hT, b_hT2], writes=[pbb])
                            m0 = 3 if OPT_PARTMAIN else 0
                            S.op("act", lambda e: e.activation(out=acc[k][:, m0:512], in_=pb[:, m0:512], func=AF.Identity, scale=wsel[:, c, 3:4], bias=mcb[:, c:c + 1]),
                                 reads=[pbb, CONST] + cb, writes=[b_acc[k]])
                            for col in (range(3) if OPT_TINY else ()):
                                S.op("act", lambda e: e.activation(out=acc[k][:, col:col + 1], in_=pb[:, col:col + 1], func=AF.Identity, scale=wsel[:, c, 3:4],
                                                                  bias=mbc[:, c, col:col + 1]), reads=[pbb, CONST, b_mbc] + cb, writes=[b_acc[k]])
                            if flagged:
                                S.op("act", lambda e: e.activation(out=mhalo[:, c, :], in_=pb[:, 509:512], func=AF.Copy, scale=flag[:, 0:1]), reads=[pbb, cb[4], b_mbc], writes=[b_mhalo])
                            else:
                                S.op("act", lambda e: e.copy(out=mhalo[:, c, :], in_=pb[:, 509:512]), reads=[pbb, b_mbc], writes=[b_mhalo])
                        for tp in (2, 1, 0):
                            sh = 3 - tp
                            for k, c in enumerate(grp):
                                pb, pbb = pbs[k]
                                S.op("dve", lambda e: e.scalar_tensor_tensor(out=acc[k][:, sh:512], in0=pb[:, 0:512 - sh], scalar=wsel[:, c, tp:tp + 1], in1=acc[k][:, sh:512],
                                                                            op0=ALU.mult, op1=ALU.add), reads=[pbb, b_acc[k], CONST], writes=[b_acc[k]])
                        for k, c in enumerate(grp):
                            if not OPT_SILU:
                                S.op("act", lambda e: e.activation(out=et[k][:], in_=acc[k][:], func=AF.Exp, scale=-1.0), reads=[b_acc[k]], writes=[b_et[k]])
                                sk = (128.0 ** 0.5) if c < 4 else 1.0
                                S.op("dve", lambda e: e.tensor_scalar(out=et[k][:], in0=et[k][:], scalar1=1.0, scalar2=sk, op0=ALU.add, op1=ALU.mult), reads=[b_et[k]], writes=[b_et[k]])
                                S.op("dve", lambda e: e.reciprocal(out=et[k][:], in_=et[k][:]), reads=[b_et[k]], writes=[b_et[k]])
                                S.op("dve", lambda e: e.tensor_tensor(out=mqT[:, c, :], in0=acc[k][:], in1=et[k][:], op=ALU.mult), reads=[b_acc[k], b_et[k]], writes=[b_mqT])
                            elif c < 4:
                                S.op("act", lambda e: e.activation(out=et[k][:], in_=acc[k][:], func=AF.Silu), reads=[b_acc[k]], writes=[b_et[k]])
                                S.op("dve", lambda e: e.tensor_scalar(out=mqT[:, c, :], in0=et[k][:], scalar1=128.0 ** -0.5, scalar2=None, op0=ALU.mult),
                                     reads=[b_et[k]], writes=[b_mqT])
                            else:
                                S.op("act", lambda e: e.activation(out=mqT[:, c, :], in_=acc[k][:], func=AF.Silu), reads=[b_acc[k]], writes=[b_mqT])
                    sq = [sbt(st, "sq%d" % i, [128, 512]) for i in range(4)]
                    qb = [sbt(st, "qb%d" % i, [128, 512], BF16) for i in range(4)]
                    rs = [sbt(st, "rs%d" % i, [128, 16]) for i in range(4)]
                    KTb = [sbt(st, "KTb%d" % i, [128, 4, 128], BF16) for i in range(4)]
                    VAb = [sbt(st, "VAb%d" % i, [128, 4, 130], BF16) for i in range(4)]
                    b_t = [S.buf("tmj%d" % i) for i in range(4)]
                    b_KTb = [S.buf("KTb%d" % i) for i in range(4)]
                    b_VAb = [S.buf("VAb%d" % i) for i in range(4)]
                    for i in range(4):
                        if flagged:
                            S.op("dve", lambda e: e.tensor_copy(out=VAb[i][:, :, 128:130], in_=flag[:, 0:1].unsqueeze(1).to_broadcast([128, 4, 2])),
                                 reads=[cb[4]], writes=[b_VAb[i]])
                        else:
                            S.op("dve", lambda e: e.memset(VAb[i][:, :, 128:130], 1.0), writes=[b_VAb[i]])
                    def tm_block(j):
                        blk = u * 4 + j
                        i = j
                        tsl = slice(j * 128, (j + 1) * 128)
                        bsel = [0]

                        def bk():
                            bi = 2 * j + (bsel[0] % 2)
                            bsel[0] += 1
                            return banks[bi], bbufs[bi]

                        def proj(nm):
                            wt, wbf = W[nm]
                            n = PIECES[nm][1] // 8
                            wv = wt[:].rearrange("p (c n) -> p c n", c=8)
                            pb, pbb = bk()
                            for dc in range(8):
                                S.op("pe", lambda e: e.matmul(pb[:, 0:n], lhsT=hT[:, dc, tsl], rhs=wv[:, dc, :], start=(dc == 0), stop=(dc == 7)),
                                     reads=[wbf, b_hT, b_hT2], writes=[pbb])
                            return pb, pbb

                        def evac_scaled(out_ap, in_ap, rd, wr):
                            if flagged:
                                S.op("act", lambda e: e.activation(out=out_ap, in_=in_ap, func=AF.Copy, scale=flag[:, 0:1]), reads=rd + [cb[4]], writes=wr)
                            else:
                                S.op("act", lambda e: e.copy(out=out_ap, in_=in_ap), reads=rd, writes=wr)

                        pb, pbb = proj("mv")
                        yield
                        evac_scaled(mVA[:, j, :, 0:128], pb[:].rearrange("p (h d) -> p h d", h=4), [pbb], [b_mVA])
                        pb, pbb = proj("gif")
                        yield
                        S.op("dve", lambda e: e.tensor_tensor(out=gif[:, j, :], in0=pb[:, 0:8], in1=gifb[:], op=ALU.add), reads=[pbb, cb[9]], writes=[b_gif])
                        pb, pbb = proj("av")
                        yield
                        evac_scaled(VAb[i][:, :, 0:128], pb[:].rearrange("p (h d) -> p h d", h=4), [pbb], [b_VAb[i]])
                        S.dma("pool", va_d.rearrange("h p (b c) -> p h b c", c=130)[:, :, blk, :], VAb[i][:], reads=[b_VAb[i]], writes=[b_vad])
                        if full:
                            pb, pbb = proj("mo")
                            yield
                            S.op("act", lambda e: e.activation(out=sigmo[:, j, :], in_=pb[:], func=AF.Exp, scale=-1.0), reads=[pbb], writes=[b_sigj[j]])
                            yield
                            S.op("dve", lambda e: e.tensor_scalar(out=sigmo[:, j, :], in0=sigmo[:, j, :], scalar1=1.0, scalar2=None, op0=ALU.add), reads=[b_sigj[j]], writes=[b_sigj[j]])
                            S.op("dve", lambda e: e.reciprocal(out=sigmo[:, j, :], in_=sigmo[:, j, :]), reads=[b_sigj[j]], writes=[b_sigj[j], b_sig])
                        for nm in (["aq", "ak"] if full else ["ak"]):
                            pb, pbb = proj(nm)
                            yield
                            S.op("act", lambda e: e.activation(out=sq[i][:], in_=pb[:], func=AF.Square, scale=0.125), reads=[pbb], writes=[b_t[i]])
                            yield
                            S.op("dve", lambda e: e.tensor_reduce(out=rs[i][:, 0:8], in_=sq[i][:].rearrange("p (g d) -> p g d", d=64), axis=AX.X, op=ALU.add),
                                 reads=[b_t[i]], writes=[b_t[i]])
                            yield
                            S.op("act", lambda e: e.activation(out=rs[i][:, 8:16], in_=rs[i][:, 0:8], func=AF.Ln, bias=epst[:, 0:1], scale=1.0), reads=[b_t[i], CONST], writes=[b_t[i]])
                            S.op("act", lambda e: e.activation(out=rs[i][:, 8:16], in_=rs[i][:, 8:16], func=AF.Exp, scale=-0.5), reads=[b_t[i]], writes=[b_t[i]])
                            yield
                            S.op("dve", lambda e: e.tensor_tensor(out=qb[i][:].rearrange("p (g d) -> p g d", d=64), in0=pb[:].rearrange("p (g d) -> p g d", d=64),
                                                                 in1=rs[i][:, 8:16].unsqueeze(2).to_broadcast([128, 8, 64]), op=ALU.mult), reads=[pbb, b_t[i]], writes=[b_t[i]])
                            yield
                            tb, tbb = bk()
                            tv = bfview(tb)[:, 0:512].rearrange("p (h t) -> p h t", h=4)
                            for h in range(4):
                                S.op("pe", lambda e: e.transpose(out=tv[:, h, :], in_=qb[i][:, h * 128:(h + 1) * 128], identity=identb[:]), reads=[b_t[i], cb[0]], writes=[tbb])
                            yield
                            if nm == "aq":
                                S.op("act", lambda e: e.activation(out=Qbd[0:64, :, j, 0:128], in_=tv[0:64, :, :], func=AF.Copy, scale=gqfm[0:64, 0:1]), reads=[tbb, CONST] + cb, writes=[b_Qbd])
                                S.op("act", lambda e: e.activation(out=Qbd[64:128, :, j, 128:256], in_=tv[64:128, :, :], func=AF.Copy, scale=gqfm[64:128, 0:1]), reads=[tbb, CONST] + cb, writes=[b_Qbd])
                            else:
                                S.op("act", lambda e: e.activation(out=KTb[i][:], in_=tv, func=AF.Copy, scale=gkfm[:, 0:1]), reads=[tbb] + cb, writes=[b_KTb[i]])
                                S.dma("pool", kt_d.rearrange("h p k -> p h k")[:, :, blk * 128:(blk + 1) * 128], KTb[i][:], reads=[b_KTb[i]], writes=[b_ktd])
                            yield

                    b_sigj = [S.buf("sigj%d" % j_) for j_ in range(4)]
                    tgens = [tm_block(j_) for j_ in range(4)]
                    alive = list(tgens)
                    while alive:
                        nxt = []
                        for g_ in alive:
                            try:
                                next(g_)
                                nxt.append(g_)
                            except StopIteration:
                                pass
                        alive = nxt
                    S.barrier()

                def mlstm_block(j, st, bankfn, cache):
                    def sbt_c(st_, name, shape, dt=F32):
                        if name not in cache:
                            cache[name] = sbt(st_, name, shape, dt)
                        return cache[name]

                    def buf_c(name):
                        k_ = "B_" + name
                        if k_ not in cache:
                            cache[k_] = S.buf(name)
                        return cache[k_]
                    tsl = slice(j * 128, (j + 1) * 128)
                    lf = sbt_c(st, "lf%d" % (j % 2), [128, 4])
                    e1 = sbt_c(st, "e1%d" % (j % 2), [128, 4])
                    e2 = sbt_c(st, "e2%d" % (j % 2), [128, 4])
                    wsx = sbt_c(st, "wsx%d" % (j % 2), [128, 4])
                    RL = sbt_c(st, "RL%d" % (j % 2), [128, 4, 128])
                    Et = sbt_c(st, "Et%d" % (j % 2), [128, 512])
                    Kw = sbt_c(st, "Kw%d" % (j % 2), [128, 4, 128], BF16)
                    bm = buf_c("ml%d" % (j % 2))
                    b_e1 = buf_c("e1%d" % (j % 2))
                    b_ws = buf_c("wsx%d" % (j % 2))
                    b_RL = buf_c("RL%d" % (j % 2))
                    b_Et = buf_c("Et%d" % (j % 2))
                    b_Kw = buf_c("Kw%d" % (j % 2))
                    S.op("act", lambda e: e.activation(out=lf[:], in_=gif[:, j, 4:8], func=AF.Exp, scale=-1.0), reads=[b_gif], writes=[bm])
                    yield
                    S.op("act", lambda e: e.activation(out=lf[:], in_=lf[:], func=AF.Ln, bias=onet[:, 0:1], scale=1.0), reads=[bm, CONST], writes=[bm])
                    yield
                    S.op("dve", lambda e: e.tensor_scalar(out=lf[:], in0=lf[:], scalar1=-1.0, scalar2=None, op0=ALU.mult), reads=[bm], writes=[bm])
                    yield
                    p1, p1b = bankfn()
                    S.op("pe", lambda e: e.matmul(p1[:, 0:4], lhsT=umask[:], rhs=lf[:], start=True, stop=True), reads=[bm, cb[2]], writes=[p1b])
                    S.op("dve", lambda e: e.tensor_tensor(out=RL[:], in0=umask[:].unsqueeze(1).to_broadcast([128, 4, 128]),
                                                         in1=lf[:].unsqueeze(2).to_broadcast([128, 4, 128]), op=ALU.mult), reads=[bm, cb[2]], writes=[b_RL])
                    yield
                    S.op("dve", lambda e: e.tensor_tensor(out=e1[:], in0=gif[:, j, 0:4], in1=p1[:, 0:4], op=ALU.subtract), reads=[b_gif, p1b], writes=[b_e1])
                    yield
                    RLf = RL[:].rearrange("p h t -> p (h t)")
                    pBc, pBcb = bankfn()
                    S.op("pe", lambda e: e.matmul(pBc[:], lhsT=ones_f[:], rhs=RLf, start=True, stop=True), reads=[b_RL, CONST], writes=[pBcb])
                    yield
                    blast = pBc[:].rearrange("p (h t) -> p h t", h=4)[:, :, 127]
                    S.op("dve", lambda e: e.tensor_tensor(out=e2[:], in0=e1[:], in1=blast, op=ALU.add), reads=[b_e1, pBcb], writes=[b_ws])
                    yield
                    S.op("act", lambda e: e.activation(out=Et[:], in_=pBc[:], func=AF.Exp), reads=[pBcb], writes=[b_Et])
                    S.op("act", lambda e: e.activation(out=wsx[:], in_=e2[:], func=AF.Exp), reads=[b_ws], writes=[b_ws])
                    yield
                    tb, tbb = bankfn()
                    tv = bfview(tb)[:, 0:512].rearrange("p (h t) -> p h t", h=4)
                    for h in range(4):
                        S.op("pe", lambda e: e.transpose(out=tv[:, h, :], in_=mqT[:, 4 + h, tsl], identity=identb[:]), reads=[b_mqT, cb[0]], writes=[tbb])
                    yield
                    S.op("dve", lambda e: e.tensor_tensor(out=Kw[:], in0=tv, in1=wsx[:].unsqueeze(2).to_broadcast([128, 4, 128]), op=ALU.mult),
                         reads=[tbb, b_ws], writes=[b_Kw])
                    yield
                    if full:
                        DT = sbt_c(st, "DT%d" % (j % 2), [128, 4, 128])
                        PT = sbt_c(st, "PT%d" % (j % 2), [128, 4, 128], BF16)
                        qpT = sbt_c(st, "qpT%d" % (j % 2), [128, 4, 128], BF16)
                        numS = sbt_c(st, "numS%d" % (j % 2), [128, 4, 130])
                        rden = sbt_c(st, "rden%d" % (j % 2), [128, 4])
                        hmr = sbt_c(st, "hmr%d" % (j % 2), [128, 4, 128])
                        hsq = sbt_c(st, "hsq%d" % (j % 2), [128, 4, 128])
                        hss = sbt_c(st, "hss%d" % (j % 2), [128, 8])
                        gs = sbt_c(st, "gs%d" % (j % 2), [128, 512])
                        hmf = sbt_c(st, "hmf%d" % (j % 2), [128, 4, 128], BF16)
                        b_o = buf_c("mo%d" % (j % 2))
                        b_gs = buf_c("gs%d" % (j % 2))
                        b_num = buf_c("numS%d" % (j % 2))
                        b_DT = buf_c("DT%d" % (j % 2))
                        b_PT = buf_c("PT%d" % (j % 2))
                        b_qp = buf_c("qpT%d" % (j % 2))
                        pBm, pBmb = bankfn()
                        S.op("pe", lambda e: e.matmul(pBm[:], lhsT=ones_f[:], rhs=RLf, start=True, stop=False), reads=[b_RL, CONST], writes=[pBmb])
                        S.op("pe", lambda e: e.matmul(pBm[:], lhsT=identf[:], rhs=negm4[:], start=False, stop=True), reads=[cb[1], cb[3]], writes=[pBmb])
                        S.op("dve", lambda e: e.tensor_tensor(out=qpT[:], in0=mqT[:, 0:4, tsl], in1=Et[:].rearrange("p (h t) -> p h t", h=4), op=ALU.mult),
                             reads=[b_mqT, b_Et], writes=[b_qp])
                        S.op("dve", lambda e: e.tensor_tensor(out=gs[:], in0=mng[:], in1=sigmo[:, j, :], op=ALU.mult), reads=[b_sig] + cb, writes=[b_gs])
                        yield
                        for h in range(4):
                            S.op("act", lambda e: e.activation(out=DT[:, h, :], in_=pBm[:, h * 128:(h + 1) * 128], func=AF.Exp, bias=e1[:, h:h + 1], scale=1.0),
                                 reads=[pBmb, b_e1], writes=[b_DT])
                        yield
                        pA, pAb = bankfn()
                        for h in range(4):
                            S.op("pe", lambda e: e.matmul(pA[:, h * 128:(h + 1) * 128], lhsT=mqT[:, 4 + h, tsl], rhs=mqT[:, h, tsl], start=True, stop=True),
                                 reads=[b_mqT], writes=[pAb])
                        yield
                        S.op("dve", lambda e: e.tensor_tensor(out=PT[:], in0=pA[:].rearrange("p (h t) -> p h t", h=4), in1=DT[:], op=ALU.mult),
                             reads=[pAb, b_DT], writes=[b_PT])
                        yield
                    yield "AB"
                    if full:
                        for hp in range(2):
                            pn, pnb = bankfn()
                            for hh in range(2):
                                h = 2 * hp + hh
                                o = hh * 130
                                S.op("pe", lambda e: e.matmul(pn[:, o:o + 130], lhsT=PT[:, h, :], rhs=mVA[:, j, h, :], start=True, stop=False), reads=[b_PT, b_mVA], writes=[pnb])
                                S.op("pe", lambda e: e.matmul(pn[:, o:o + 130], lhsT=qpT[:, h, :], rhs=Sstb[:, h, :], start=False, stop=True), reads=[b_qp, b_Sb], writes=[pnb])
                            yield
                            S.op("dve", lambda e: e.tensor_copy(out=numS[:, 2 * hp:2 * hp + 2, :], in_=pn[:, 0:260].rearrange("p (h c) -> p h c", h=2)), reads=[pnb], writes=[b_num])
                            yield
                    Ev = Et[:].rearrange("p (h t) -> p h t", h=4)
                    for hp in range(2):
                        pc_, pcb = bankfn()
                        for hh in range(2):
                            h = 2 * hp + hh
                            o = hh * 130
                            S.op("pe", lambda e: e.matmul(pc_[:, o:o + 130], lhsT=Kw[:, h, :], rhs=mVA[:, j, h, :], start=True, stop=True), reads=[b_Kw, b_mVA], writes=[pcb])
                        yield
                        for hh in range(2):
                            h = 2 * hp + hh
                            o = hh * 130
                            S.op("dve", lambda e: e.scalar_tensor_tensor(out=Sst[:, h, :], in0=Sst[:, h, :], scalar=Ev[:, h, 127:128], in1=pc_[:, o:o + 130],
                                                                        op0=ALU.mult, op1=ALU.add), reads=[b_S, b_Et, pcb], writes=[b_S])
                        yield
                    S.op("dve", lambda e: e.tensor_copy(out=Sstb[:], in_=Sst[:]), reads=[b_S], writes=[b_Sb])
                    yield "S"
                    if full:
                        S.op("act", lambda e: e.activation(out=rden[:], in_=numS[:, :, 128], func=AF.Abs), reads=[b_num], writes=[b_o])
                        yield
                        S.op("dve", lambda e: e.tensor_scalar(out=rden[:], in0=rden[:], scalar1=1.0, scalar2=None, op0=ALU.max), reads=[b_o], writes=[b_o])
                        S.op("dve", lambda e: e.reciprocal(out=rden[:], in_=rden[:]), reads=[b_o], writes=[b_o])
                        S.op("dve", lambda e: e.tensor_tensor(out=hmr[:], in0=numS[:, :, 0:128], in1=rden[:].unsqueeze(2).to_broadcast([128, 4, 128]), op=ALU.mult),
                             reads=[b_num, b_o], writes=[b_o])
                        yield
                        S.op("act", lambda e: e.activation(out=hsq[:], in_=hmr[:], func=AF.Square, scale=1.0 / math.sqrt(128.0)), reads=[b_o], writes=[b_o])
                        yield
                        S.op("dve", lambda e: e.tensor_reduce(out=hss[:, 0:4], in_=hsq[:], axis=AX.X, op=ALU.add), reads=[b_o], writes=[b_o])
                        yield
                        S.op("act", lambda e: e.activation(out=hss[:, 4:8], in_=hss[:, 0:4], func=AF.Ln, bias=epst[:, 0:1], scale=1.0), reads=[b_o, CONST], writes=[b_o])
                        S.op("act", lambda e: e.activation(out=hss[:, 4:8], in_=hss[:, 4:8], func=AF.Exp, scale=-0.5), reads=[b_o], writes=[b_o])
                        yield
                        for h in range(4):
                            S.op("dve", lambda e: e.scalar_tensor_tensor(out=hmf[:, h, :], in0=hmr[:, h, :], scalar=hss[:, 4 + h:5 + h], in1=gs[:, h * 128:(h + 1) * 128],
                                                                        op0=ALU.mult, op1=ALU.mult), reads=[b_o, b_gs], writes=[b_o])
                        yield
                        tb2, tb2b = bankfn()
                        tv2 = bfview(tb2)[:, 0:512].rearrange("p (h t) -> p h t", h=4)
                        for h in range(4):
                            S.op("pe", lambda e: e.transpose(out=tv2[:, h, :], in_=hmf[:, h, :], identity=identb[:]), reads=[b_o, cb[0]], writes=[tb2b])
                        yield
                        S.op("dve", lambda e: e.tensor_copy(out=hmT[:, :, tsl], in_=tv2), reads=[tb2b], writes=[b_hmT])
                        yield

                def mlstm_driver(st, bankfn):
                    cache = {}
                    for j in range(4):
                        for r in mlstm_block(j, st, bankfn, cache):
                            yield

                if not full:
                    with ExitStack() as st:
                        for _ in mlstm_driver(st, nbank):
                            pass
                        S.barrier()

                if full:
                    with ExitStack() as st:
                        NKC = 8
                        kch = [sbt(st, "kch%d" % i, [128, NKC * 128], BF16) for i in range(3)]
                        vch = [sbt(st, "vch%d" % i, [128, NKC, 130], BF16) for i in range(3)]
                        b_kch = [S.buf("kch%d" % i) for i in range(3)]
                        b_vch = [S.buf("vch%d" % i) for i in range(3)]
                        pex = [sbt(st, "pex%d" % i, [128, 512], BF16) for i in range(6)]
                        b_pex = [S.buf("pex%d" % i) for i in range(6)]
                        rr = sbt(st, "rr", [128, 8])
                        har = sbt(st, "har", [128, 4, 128])
                        hsq = sbt(st, "ahsq", [128, 4, 128])
                        hss = sbt(st, "ahss", [128, 8])
                        haf = sbt(st, "haf", [128, 4, 128], BF16)
                        b_a = S.buf("attn_o")
                        nkb_tot = u * 4 + 4
                        chunk_list = [(s0, min(NKC, nkb_tot - s0)) for s0 in range(0, nkb_tot, NKC)]
                        ring = 0
                        LCH = len(chunk_list)
                        prefetch((0, 1), ["bm", "gm0"])
                        mdrv = mlstm_driver(st, lambda: (banks[7], bbufs[7]))
                        n_items_est = 4 * sum((2 if kb_ <= u * 4 - 2 else 0) + sum(1 for p2 in range(2) for jq in (2 * p2, 2 * p2 + 1) if kb_ > u * 4 + 2 * p2 - 2 and kb_ <= u * 4 + jq)
                                              for kb_ in range(nkb_tot))
                        MSTEP = max(1, n_items_est // 150)
                        gcount = [0]
                        loaded = set()

                        def ensure_chunk(g):
                            if g >= 4 * LCH or g in loaded:
                                return
                            loaded.add(g)
                            h_, ci_ = g // LCH, g % LCH
                            s0, nk = chunk_list[ci_]
                            ri = g % 3
                            S.dma("sp", kch[ri][:, 0:nk * 128], kt_d[h_, :, s0 * 128:(s0 + nk) * 128], reads=[b_ktd], writes=[b_kch[ri]])
                            S.dma("sp", vch[ri][:, 0:nk, :], va_d[h_, :, s0 * 130:(s0 + nk) * 130].rearrange("p (b c) -> p b c", c=130), reads=[b_vad], writes=[b_vch[ri]])

                        accS = [sbt(st, "accS%d" % i, [128, 4, 2, 130]) for i in range(2)]
                        b_accS = [S.buf("accS%d" % i) for i in range(2)]
                        rr2 = [sbt(st, "rr2_%d" % i, [128, 8]) for i in range(2)]
                        har2 = [sbt(st, "har2_%d" % i, [128, 4, 128]) for i in range(2)]
                        hsq2 = [sbt(st, "hsq2_%d" % i, [128, 4, 128]) for i in range(2)]
                        hss2 = [sbt(st, "hss2_%d" % i, [128, 8]) for i in range(2)]
                        haf2 = [sbt(st, "haf2_%d" % i, [128, 4, 128], BF16) for i in range(2)]
                        b_ep = [S.buf("epi%d" % i) for i in range(2)]

                        def attn_epi(h, accs):
                            p = h % 2
                            aS, rr, har, hsq, hss, haf, b_a = accS[p], rr2[p], har2[p], hsq2[p], hss2[p], haf2[p], b_ep[p]
                            for jq in range(4):
                                ab, abb = accs[jq]
                                S.op("dve", lambda e: e.tensor_copy(out=aS[:, jq, :, :], in_=ab[:, 0:260].rearrange("p (m c) -> p m c", m=2)), reads=[abb], writes=[b_accS[p]])
                            yield
                            for jq in range(4):
                                av = aS[:, jq, :, :]
                                S.op("dve", lambda e: e.tensor_scalar(out=rr[:, 2 * jq:2 * jq + 2], in0=av[:, :, 128], scalar1=1e-30, scalar2=None, op0=ALU.max), reads=[b_accS[p]], writes=[b_a])
                                S.op("dve", lambda e: e.reciprocal(out=rr[:, 2 * jq:2 * jq + 2], in_=rr[:, 2 * jq:2 * jq + 2]), reads=[b_a], writes=[b_a])
                                S.op("dve", lambda e: e.tensor_tensor(out=rr[:, 2 * jq + 1:2 * jq + 2], in0=rr[:, 2 * jq + 1:2 * jq + 2], in1=neglam[:], op=ALU.mult),
                                     reads=[b_a, CONST], writes=[b_a])
                                S.op("dve", lambda e: e.tensor_scalar(out=har[:, jq, :], in0=av[:, 0, 0:128], scalar1=rr[:, 2 * jq:2 * jq + 1], scalar2=None, op0=ALU.mult),
                                     reads=[b_accS[p], b_a], writes=[b_a])
                                S.op("dve", lambda e: e.scalar_tensor_tensor(out=har[:, jq, :], in0=av[:, 1, 0:128], scalar=rr[:, 2 * jq + 1:2 * jq + 2], in1=har[:, jq, :],
                                                                            op0=ALU.mult, op1=ALU.add), reads=[b_accS[p], b_a], writes=[b_a])
                                yield
                            S.op("act", lambda e: e.activation(out=hsq[:], in_=har[:], func=AF.Square, scale=1.0 / math.sqrt(128.0)), reads=[b_a], writes=[b_a])
                            yield
                            S.op("dve", lambda e: e.tensor_reduce(out=hss[:, 0:4], in_=hsq[:], axis=AX.X, op=ALU.add), reads=[b_a], writes=[b_a])
                            yield
                            S.op("act", lambda e: e.activation(out=hss[:, 4:8], in_=hss[:, 0:4], func=AF.Ln, bias=epst[:, 0:1], scale=1.0), reads=[b_a, CONST], writes=[b_a])
                            S.op("act", lambda e: e.activation(out=hss[:, 4:8], in_=hss[:, 4:8], func=AF.Exp, scale=-0.5), reads=[b_a], writes=[b_a])
                            yield
                            for jq in range(4):
                                S.op("dve", lambda e: e.scalar_tensor_tensor(out=haf[:, jq, :], in0=har[:, jq, :], scalar=hss[:, 4 + jq:5 + jq], in1=ang[:, h * 128:(h + 1) * 128],
                                                                            op0=ALU.mult, op1=ALU.mult), reads=[b_a, CONST] + cb, writes=[b_a])
                            yield
                            tb, tbb = banks[4 + h % 3], bbufs[4 + h % 3]
                            tv = bfview(tb)[:, 0:512].rearrange("p (j t) -> p j t", j=4)
                            for jq in range(4):
                                S.op("pe", lambda e: e.transpose(out=tv[:, jq, :], in_=haf[:, jq, :], identity=identb[:]), reads=[b_a, cb[0]], writes=[tbb])
                            S.op("dve", lambda e: e.tensor_copy(out=haT[:, h, :], in_=tv.rearrange("p j t -> p (j t)")), reads=[tbb], writes=[b_haT])
                            yield

                        epi = [iter(())]
                        for h in range(4):
                            accs = [(banks[jq], bbufs[jq]) for jq in range(4)]
                            for jq in range(4):
                                S.op("dve", lambda e: e.memset(accs[jq][0][:, 0:260], 0.0), writes=[accs[jq][1]])
                            items = []
                            for ci, (s0, nk) in enumerate(chunk_list):
                                g = h * LCH + ci
                                ri = g % 3
                                for kk in range(nk):
                                    kb = s0 + kk
                                    for p2 in range(2):
                                        j0 = 2 * p2
                                        if u == NCTX:
                                            if p2 == 1 and kb <= u * 4 + 3:
                                                items.append((ri, kk, kb, (3,), g))
                                            continue
                                        if kb <= u * 4 + j0 - 2:
                                            items.append((ri, kk, kb, (j0, j0 + 1), g))
                                        else:
                                            for jq in (j0, j0 + 1):
                                                if kb <= u * 4 + jq:
                                                    items.append((ri, kk, kb, (jq,), g))

                            LAG = 3
                            NS = 3
                            NPX = len(pex)

                            def emit_scores(it, idx):
                                ri, kk, kb, jqs, ci = it
                                ensure_chunk(ci)
                                ensure_chunk(ci + 1)
                                nq = len(jqs)
                                sl = idx % NS
                                sp_, spb = banks[4 + sl], bbufs[4 + sl]
                                sv = sp_[:, 0:256 * nq]
                                px = idx % NPX
                                delta = u * 4 + jqs[0] - kb
                                near = (nq == 1) and delta <= 1
                                rhs = Qbd[:, h, jqs[0]:jqs[0] + nq, :].rearrange("p j c -> p (j c)")
                                S.op("pe", lambda e: e.matmul(sv, lhsT=kch[ri][:, kk * 128:(kk + 1) * 128], rhs=rhs, start=True, stop=not near),
                                     reads=[b_kch[ri], b_Qbd], writes=[spb])
                                if near:
                                    S.op("pe", lambda e: e.matmul(sv, lhsT=identb[:], rhs=biasb[:, h, delta, :], start=False, stop=True), reads=[CONST, cb[0]], writes=[spb])
                                    S.op("act", lambda e: e.activation(out=pex[px][:, 0:256 * nq], in_=sv, func=AF.Exp), reads=[spb], writes=[b_pex[px]])
                                else:
                                    S.op("act", lambda e: e.activation(out=pex[px][:, 0:256 * nq], in_=sv, func=AF.Exp, bias=farb[:, h:h + 1], scale=1.0),
                                         reads=[spb, cb[14]], writes=[b_pex[px]])

                            def emit_pv(it, idx):
                                ri, kk, kb, jqs, ci = it
                                px = idx % NPX
                                for a_, jq in enumerate(jqs):
                                    ab, abb = accs[jq]
                                    for m in range(2):
                                        c0 = a_ * 256 + m * 128
                                        S.op("pe", lambda e: e.matmul(ab[:, m * 130:(m + 1) * 130], lhsT=pex[px][:, c0:c0 + 128], rhs=vch[ri][:, kk, :],
                                                                     start=False, stop=False, skip_group_check=True), reads=[b_pex[px], b_vch[ri]], writes=[abb])

                            for idx in range(len(items) + LAG):
                                if idx < len(items):
                                    emit_scores(items[idx], idx)
                                if idx >= LAG:
                                    emit_pv(items[idx - LAG], idx - LAG)
                                gcount[0] += 1
                                if gcount[0] % MSTEP == 0:
                                    next(mdrv, None)
                                if idx % 6 == 5:
                                    next(epi[0], None)
                            for _ in epi[0]:
                                pass
                            epi[0] = attn_epi(h, accs)
                            next(epi[0], None)
                        for _ in epi[0]:
                            pass
                        for _ in mdrv:
                            pass
                        S.barrier()

                    h2T = hT
                    b_h2T = b_hT
                    with ExitStack() as st:
                        W = {}
                        for nm in ("bm", "gm0", "ba", "ga0", "gm1", "ga1", "out0", "out1"):
                            W[nm] = wload(st, "w_" + nm, nm)
                        prefetch((2, 3), ["up0", "up1"])
                        yT = sbt(st, "yT", [128, 8, 512], BF16)
                        b_yT = S.buf("yT")
                        sg = [[sbt(st, "sg%d_%d" % (i, a), [128, 512]) for a in range(2)] for i in range(2)]
                        ty = [[sbt(st, "ty%d_%d" % (i, a), [128, 512]) for a in range(2)] for i in range(2)]
                        b_sg = [S.buf("sg%d" % i) for i in range(2)]
                        for c in range(8):
                            i = c % 2
                            res = []
                            for a, (bw, gw, srcT, b_src) in enumerate((("bm", "gm", hmT, b_hmT), ("ba", "ga", haT, b_haT))):
                                wt, wbf = W[bw]
                                wv = wt[:].rearrange("p (k n) -> p k n", k=4)
                                py, pyb = nbank()
                                for k in range(4):
                                    S.op("pe", lambda e: e.matmul(py[:], lhsT=wv[:, k, c * 128:(c + 1) * 128], rhs=srcT[:, k, :], start=(k == 0), stop=(k == 3)),
                                         reads=[wbf, b_src], writes=[pyb])
                                gt_, gbf = W[gw + str(c // 4)]
                                gv = gt_[:].rearrange("p (c n) -> p c n", c=8)
                                pg, pgb = nbank()
                                for dc in range(8):
                                    S.op("pe", lambda e: e.matmul(pg[:], lhsT=gv[:, dc, (c % 4) * 128:(c % 4 + 1) * 128], rhs=hT[:, dc, :], start=(dc == 0), stop=(dc == 7)),
                                         reads=[gbf, b_hT, b_hT2], writes=[pgb])
                                S.op("act", lambda e: e.activation(out=sg[i][a][:], in_=pg[:], func=AF.Sigmoid), reads=[pgb], writes=[b_sg[i]])
                                S.op("dve", lambda e: e.tensor_tensor(out=ty[i][a][:], in0=py[:], in1=sg[i][a][:], op=ALU.mult), reads=[pyb, b_sg[i]], writes=[b_sg[i]])
                            S.op("dve", lambda e: e.tensor_tensor(out=yT[:, c, :], in0=ty[i][0][:], in1=ty[i][1][:], op=ALU.add), reads=[b_sg[i]], writes=[b_yT])
                        tx = [sbt(st, "tx%d" % i, [128, 512]) for i in range(2)]
                        b_tx = [S.buf("tx%d" % i) for i in range(2)]
                        for j in range(4):
                            tsl = slice(j * 128, (j + 1) * 128)
                            for n in range(2):
                                wt, wbf = W["out%d" % n]
                                wv = wt[:].rearrange("p (c n) -> p c n", c=8)
                                po, pob = nbank()
                                for c in range(8):
                                    S.op("pe", lambda e: e.matmul(po[:], lhsT=yT[:, c, tsl], rhs=wv[:, c, :], start=(c == 0), stop=(c == 7)), reads=[wbf, b_yT], writes=[pob])
                                i = (j * 2 + n) % 2
                                S.op("dve", lambda e: e.tensor_tensor(out=tx[i][:], in0=po[:], in1=gate1[:, n * 512:(n + 1) * 512], op=ALU.mult), reads=[pob, CONST], writes=[b_tx[i]])
                                S.op("dve", lambda e: e.tensor_tensor(out=xs[:, j, n * 512:(n + 1) * 512], in0=xs[:, j, n * 512:(n + 1) * 512], in1=tx[i][:], op=ALU.add),
                                     reads=[b_tx[i], b_xs[j]], writes=[b_xs[j]])
                        run_norms(st, mult2, shift2, h2T, (b_hT, b_hT2), "n2")
                        S.barrier()

                    sa.close()
                    with ExitStack() as st:
                        actT = sbt(st, "actT", [128, NF, 512], BF16)
                        b_actT = S.buf("actT")
                        wu = [None, None, None]
                        uu = [sbt(st, "fu%d" % i, [128, 512]) for i in range(8)]
                        b_uu = [S.buf("fu%d" % i) for i in range(8)]
                        fe = [sbt(st, "fe%d" % i, [128, 512]) for i in range(2)]
                        b_fe = [S.buf("fe%d" % i) for i in range(2)]
                        wus = [pf[2], pf[3], sbt(st, "wup2", [128, 4096], BF16)]
                        b_wus = [b_pf[2], b_pf[3], S.buf("wup2")]

                        def ldup(jj):
                            if ("up%d" % jj) in prefetched:
                                prefetched.pop("up%d" % jj)
                                return
                            off, n = PIECES["up%d" % jj]
                            S.dma("sp", wus[jj % 3][:], wb_d[:, off:off + n], reads=[b_wb], writes=[b_wus[jj % 3]])

                        ldup(0)
                        ldup(1)
                        wd = sbt(st, "w_down", [128, NF * 1024], BF16)
                        wdb = S.buf("w_down")
                        wdv = wd[:].rearrange("p (f n) -> p f n", f=NF)
                        if not OPT_BATCH:
                            S.op("dve", lambda e: e.memset(fbc[:], 0.0), writes=[b_fbc])
                        if OPT_BATCH:
                            S.op("dve", lambda e: e.tensor_tensor(out=fbc[:, :, 1], in0=fhalo[:, :, 1], in1=fcw[:, :, 0], op=ALU.mult), reads=[b_fhalo] + cb, writes=[b_fbc])
                            S.op("dve", lambda e: e.tensor_tensor(out=fbc[:, :, 0], in0=fhalo[:, :, 1], in1=fcw[:, :, 1], op=ALU.mult), reads=[b_fhalo] + cb, writes=[b_fbc])
                            S.op("dve", lambda e: e.tensor_tensor(out=ctmp[:], in0=fhalo[:, :, 0], in1=fcw[:, :, 0], op=ALU.mult), reads=[b_fhalo] + cb, writes=[b_fbc])
                            S.op("dve", lambda e: e.tensor_tensor(out=fbc[:, :, 0], in0=fbc[:, :, 0], in1=ctmp[:], op=ALU.add), reads=[b_fbc], writes=[b_fbc])
                            S.op("dve", lambda e: e.tensor_tensor(out=fbc[:], in0=fbc[:], in1=fcb[:].unsqueeze(2).to_broadcast([128, 44, 2]), op=ALU.add), reads=[b_fbc] + cb, writes=[b_fbc])

                        def ffn_piece(jj):
                            if jj + 2 < 11:
                                ldup(jj + 2)
                            if jj == 2 and u != NCTX:
                                off_d, n_d = PIECES["down"]
                                S.dma("sp", wd[:], wb_d[:, off_d:off_d + n_d], reads=[b_wb], writes=[wdb])
                            if jj == 4 and u + 1 < NU:
                                prefetch((0, 1), ["mq", "mk"])
                            wv = wus[jj % 3][:].rearrange("p (c n) -> p c n", c=8)
                            wbf = b_wus[jj % 3]
                            wsel = fcwf if flagged else fcw
                            pbs = []
                            for k in range(4):
                                q = jj * 4 + k
                                pb, pbb = nbank()
                                pbs.append((pb, pbb))
                                halo_only = (u == NCTX)
                                c0_ = 384 if halo_only else 0
                                for dc in range(8):
                                    S.op("pe", lambda e: e.matmul(pb[:, c0_:512], lhsT=wv[:, dc, k * 128:(k + 1) * 128], rhs=h2T[:, dc, c0_:512], start=(dc == 0), stop=(dc == 7)),
                                         reads=[wbf, b_hT, b_hT2], writes=[pbb])
                                kk_ = (jj % 2) * 4 + k
                                if halo_only:
                                    S.op("act", lambda e: e.activation(out=fhalo[:, q, :], in_=pb[:, 510:512], func=AF.Copy, scale=flag[:, 0:1]), reads=[pbb, cb[4], b_fbc], writes=[b_fhalo])
                                    continue
                                m0 = 2 if OPT_PARTMAIN else 0
                                S.op("act", lambda e: e.activation(out=uu[kk_][:, m0:512], in_=pb[:, m0:512], func=AF.Identity, scale=wsel[:, q, 2:3], bias=fcb[:, q:q + 1]),
                                     reads=[pbb, CONST] + cb, writes=[b_uu[kk_]])
                                for col in (range(2) if OPT_TINY else ()):
                                    S.op("act", lambda e: e.activation(out=uu[kk_][:, col:col + 1], in_=pb[:, col:col + 1], func=AF.Identity, scale=wsel[:, q, 2:3],
                                                                      bias=fbc[:, q, col:col + 1]), reads=[pbb, CONST, b_fbc] + cb, writes=[b_uu[kk_]])
                                if flagged:
                                    S.op("act", lambda e: e.activation(out=fhalo[:, q, :], in_=pb[:, 510:512], func=AF.Copy, scale=flag[:, 0:1]), reads=[pbb, cb[4], b_fbc], writes=[b_fhalo])
                                else:
                                    S.op("act", lambda e: e.copy(out=fhalo[:, q, :], in_=pb[:, 510:512]), reads=[pbb, b_fbc], writes=[b_fhalo])
                            for tp in ((1, 0) if u != NCTX else ()):
                                sh = 2 - tp
                                for k in range(4):
                                    q = jj * 4 + k
                                    kk_ = (jj % 2) * 4 + k
                                    pb, pbb = pbs[k]
                                    S.op("dve", lambda e: e.scalar_tensor_tensor(out=uu[kk_][:, sh:512], in0=pb[:, 0:512 - sh], scalar=wsel[:, q, tp:tp + 1], in1=uu[kk_][:, sh:512],
                                                                                op0=ALU.mult, op1=ALU.add), reads=[pbb, b_uu[kk_], CONST], writes=[b_uu[kk_]])
                            yield
                            for i in (range(2) if u != NCTX else ()):
                                f = 2 * jj + i
                                uv = uu[(jj % 2) * 4 + i]
                                ug = uu[(jj % 2) * 4 + 2 + i]
                                b_uv = b_uu[(jj % 2) * 4 + i]
                                b_ug = b_uu[(jj % 2) * 4 + 2 + i]
                                S.op("act", lambda e: e.activation(out=fe[i][:], in_=ug[:], func=AF.Silu), reads=[b_ug], writes=[b_fe[i]])
                                S.op("dve", lambda e: e.tensor_tensor(out=actT[:, f, :], in0=uv[:], in1=fe[i][:], op=ALU.mult), reads=[b_uv, b_fe[i]], writes=[b_actT])
                        fgens = [ffn_piece(jj) for jj in range(11)]
                        for step in range(12):
                            if step < 11:
                                next(fgens[step], None)
                            if step >= 1:
                                next(fgens[step - 1], None)
                        tx = fe
                        b_tx = b_fe
                        for j in (range(4) if u != NCTX else ()):
                            tsl = slice(j * 128, (j + 1) * 128)
                            blk = u * 4 + j
                            for n in range(2):
                                po, pob = nbank()
                                for f in range(NF):
                                    S.op("pe", lambda e: e.matmul(po[:], lhsT=actT[:, f, tsl], rhs=wdv[:, f, n * 512:(n + 1) * 512], start=(f == 0), stop=(f == NF - 1)),
                                         reads=[wdb, b_actT], writes=[pob])
                                i = (j * 2 + n) % 2
                                S.op("dve", lambda e: e.tensor_tensor(out=tx[i][:], in0=po[:], in1=gate2[:, n * 512:(n + 1) * 512], op=ALU.mult), reads=[pob, CONST], writes=[b_tx[i]])
                                S.op("dve", lambda e: e.tensor_tensor(out=xs[:, j, n * 512:(n + 1) * 512], in0=xs[:, j, n * 512:(n + 1) * 512], in1=tx[i][:], op=ALU.add),
                                     reads=[b_tx[i], b_xs[j]], writes=[b_xs[j]])
                            if own:
                                ob = blk - NFLAG * 4
                                S.dma("pool", out_d[ob * 128:(ob + 1) * 128, :], xs[:, j, :], reads=[b_xs[j]], writes=[b_out])
                        S.barrier()
        S.barrier()
    return nc


def _t5_bucket(n):
    n = np.maximum(n, 0)
    max_exact = 16
    nf = np.maximum(n, 1).astype(np.float32)
    large = max_exact + (np.log(nf / np.float32(max_exact)) / np.float32(math.log(128 / max_exact)) * np.float32(32 - max_exact)).astype(np.int32)
    large = np.minimum(large, 31)
    return np.where(n < max_exact, n, large)


def _fm(v, nch):
    return np.ascontiguousarray(np.asarray(v, np.float32).reshape(nch, 128).T)


def _piece_fm(w):
    n = w.shape[1]
    return w.reshape(8, 128, n).transpose(1, 0, 2).reshape(128, 8 * n)


def prepare_inputs(NCTX, NFULL, inputs):
    f32 = np.float32
    g = {k: np.asarray(v) for k, v in inputs.items()}
    NU = NCTX + NFULL
    half_tok = (NU // 2) * 512
    x = g["x"].astype(f32, copy=False)
    B = x.shape[0]
    assert x.shape[1] == 2 * half_tok
    w_in = g["w_in"][0]
    cols = {}
    o = 0
    for nm, n in (("mqk", 1024), ("mv", 512), ("mo", 512), ("mi", 4), ("mf", 4), ("aq", 512), ("ak", 512), ("av", 512), ("gm", 1024), ("ga", 1024)):
        cols[nm] = w_in[:, o:o + n]
        o += n

    def perm_qk(w):
        return w.reshape(1024, 2, 4, 64).transpose(0, 2, 1, 3).reshape(1024, 512)

    wall = np.zeros((128, WPAD), f32)

    def put(nm, arr):
        off, n = PIECES[nm]
        assert arr.shape == (128, n), (nm, arr.shape, n)
        wall[:, off:off + n] = arr

    put("mq", _piece_fm(cols["mqk"][:, 0:512]))
    put("mk", _piece_fm(cols["mqk"][:, 512:1024]))
    put("mv", _piece_fm(cols["mv"]))
    put("mo", _piece_fm(cols["mo"]))
    put("aq", _piece_fm(perm_qk(cols["aq"])))
    put("ak", _piece_fm(perm_qk(cols["ak"])))
    put("av", _piece_fm(cols["av"]))
    put("gm0", _piece_fm(cols["gm"][:, 0:512]))
    put("gm1", _piece_fm(cols["gm"][:, 512:1024]))
    put("ga0", _piece_fm(cols["ga"][:, 0:512]))
    put("ga1", _piece_fm(cols["ga"][:, 512:1024]))
    put("gif", _piece_fm(np.concatenate([cols["mi"], cols["mf"]], axis=1)))
    put("bm", g["w_branch_m"][0].reshape(4, 128, 1024).transpose(1, 0, 2).reshape(128, 4096))
    put("ba", g["w_branch_a"][0].reshape(4, 128, 1024).transpose(1, 0, 2).reshape(128, 4096))
    put("out0", _piece_fm(g["w_out"][0][:, 0:512]))
    put("out1", _piece_fm(g["w_out"][0][:, 512:1024]))
    w_up = g["w_up"][0]
    fcw_full = g["ffn_conv_w"][0]
    fcb_full = g["ffn_conv_b"][0]
    chunk_cols = []
    for jj in range(11):
        cc = [np.arange((2 * jj + i) * 128, (2 * jj + i + 1) * 128) for i in range(2)]
        cc += [DFF + np.arange((2 * jj + i) * 128, (2 * jj + i + 1) * 128) for i in range(2)]
        idx = np.concatenate(cc)
        chunk_cols.append(idx)
        put("up%d" % jj, _piece_fm(w_up[:, idx]))
    allidx = np.concatenate(chunk_cols)
    fcw = fcw_full[:, allidx].reshape(3, 44, 128).transpose(2, 1, 0).reshape(128, 44 * 3)
    fcb = fcb_full[allidx].reshape(44, 128).T
    put("down", g["w_down"][0].reshape(NF, 128, 1024).transpose(1, 0, 2).reshape(128, NF * 1024))

    w_ada = g["w_ada"][0]
    wada = np.stack([_piece_fm(w_ada[:, p * 512:(p + 1) * 512]) for p in range(12)]).astype(f32)
    bada = g["b_ada"][0].astype(f32)
    mcw = g["m_conv_w"][0].reshape(4, 8, 128).transpose(2, 1, 0).reshape(128, 32)
    mcb = _fm(g["m_conv_b"][0], 8)
    gifb = np.concatenate([g["m_igate_b"][0], g["m_fgate_b"][0]]).astype(f32)
    gq = np.tile(g["a_qnorm_g"][0], 8).astype(f32)
    gk = np.tile(g["a_knorm_g"][0], 8).astype(f32)
    rel = g["rel_bias"].astype(f32)
    kk = np.arange(128)[:, None]
    qq = np.arange(128)[None, :]
    biasg = np.zeros((128, 4, 2, 2, 128), f32)
    maskn = np.zeros((128, 4, 2, 2, 128), f32)
    for dl in range(2):
        dist = qq - kk + 128 * dl
        bidx = _t5_bucket(dist)
        for h in range(4):
            t = rel[bidx, h]
            biasg[:, h, dl, 0, :] = t
            biasg[:, h, dl, 1, :] = t
        if dl == 0:
            mk = np.where(dist < 0, NEG, 0.0).astype(f32)
            maskn[:, :, 0, :, :] = mk[:, None, None, :]
    farb = rel[31, :].astype(f32)
    umask = (kk <= qq).astype(f32)
    negm4 = np.tile(np.where(kk <= qq, 0.0, NEG).astype(f32), (1, 4))
    import ml_dtypes
    common = dict(
        wada=wada, bada=bada, badafm=_fm(bada, 48), g1fm=_fm(g["norm1_g"][0], 8), g2fm=_fm(g["norm2_g"][0], 8),
        wall=wall, mcw=np.ascontiguousarray(mcw, f32), mcb=mcb, fcw=np.ascontiguousarray(fcw, f32), fcb=np.ascontiguousarray(fcb, f32),
        gifb=gifb, mng=g["m_norm_g"][0].astype(f32), ang=g["a_norm_g"][0].astype(f32), gq=gq, gk=gk,
        gqfm=np.tile(g["a_qnorm_g"][0], 2).reshape(128, 1).astype(f32), gkfm=np.tile(g["a_knorm_g"][0], 2).reshape(128, 1).astype(f32),
        alam=g["a_lambda"][0].reshape(256).astype(f32), biasg=biasg.reshape(128, -1), maskn=maskn.reshape(128, -1), farb=farb,
        identb=np.eye(128).astype(ml_dtypes.bfloat16), identf=np.eye(128, dtype=f32), umask=umask, negm4=negm4,
    )
    in_maps = []
    for b in range(B):
        cfm = _fm(g["c"][b], 8)
        for hf in range(2):
            if hf == 0:
                xl = np.concatenate([np.zeros((half_tok, D), f32), x[b, 0:half_tok]], axis=0)
                fl = np.zeros((128, 1), f32)
            else:
                xl = x[b]
                fl = np.ones((128, 1), f32)
            m = dict(common)
            m["x"] = np.ascontiguousarray(xl)
            m["cfm"] = cfm
            m["flag"] = fl
            in_maps.append(m)
    return in_maps


_NC_CACHE = {}


def run(NCTX, NFULL, inputs):
    key = (NCTX, NFULL)
    if key not in _NC_CACHE:
        _NC_CACHE[key] = build_program(NCTX, NFULL)
    nc = _NC_CACHE[key]
    in_maps = prepare_inputs(NCTX, NFULL, inputs)
    res = run_bass_kernel_spmd(nc, in_maps, core_ids=list(range(len(in_maps))))
    B = len(in_maps) // 2
    half_tok = ((NCTX + NFULL) // 2) * 512
    out = np.empty((B, 2 * half_tok, D), np.float32)
    for b in range(B):
        for hf in range(2):
            out[b, hf * half_tok:(hf + 1) * half_tok] = res.results[b * 2 + hf]["out"]
    return out


def kernel(**inputs):
    return run(7, 9, inputs)
```
